# Optimizing a Trainium2 kernel written in Bass

```python
import math
import jax, jax.numpy as jnp
from jax import lax
import numpy as np

D_MODEL = 1024
BATCH = 1
SEQ = 16384
DEPTH = 1
DEC_BATCH = 32
DEC_SEQ = 32
PAST_LEN = 1024

CHUNK = 64
Q_BLOCK = 128
SSM_WIDTH = D_MODEL // 2
SSM_GROUP = 16
SSM_GROUPS = SSM_WIDTH // SSM_GROUP
SSM_STATE = 64
DT_MIN = 1e-3
DT_MAX = 1e-1
MLA_HEADS = 4
QK_NOPE = 128
QK_ROPE = 64
V_HEAD = 128
MLA_WIDTH = MLA_HEADS * V_HEAD
Q_LORA = 384
KV_LORA = 256
ROPE_THETA = 10000.0
MLA_SCALE = (QK_NOPE + QK_ROPE) ** -0.5
IN_WIDTH = SSM_WIDTH + Q_LORA + KV_LORA + QK_ROPE
MIX_WIDTH = SSM_WIDTH + MLA_WIDTH
N_MEM = 256
X_HEADS = 4
X_HEAD_DIM = D_MODEL // X_HEADS
D_FF = 4 * D_MODEL
ALPHA = (2 * DEPTH) ** 0.25
BETA = (8 * DEPTH) ** -0.25
EPS = 1e-5
NEG_INF = -1e30

kernel_name = "hymba_s5_mla_deepnorm_stream_step"


def _layer_norm(x, g, b):
    xf = x.astype(jnp.float32)
    mu = jnp.mean(xf, -1, keepdims=True)
    var = jnp.mean(jnp.square(xf - mu), -1, keepdims=True)
    return ((xf - mu) * lax.rsqrt(var + EPS) * g.astype(jnp.float32) + b.astype(jnp.float32)).astype(x.dtype)


def _rms_norm(x, g):
    xf = x.astype(jnp.float32)
    return (xf * lax.rsqrt(jnp.mean(xf * xf, -1, keepdims=True) + EPS) * g.astype(jnp.float32)).astype(x.dtype)


def _rope(x, pos):
    half = QK_ROPE // 2
    inv = ROPE_THETA ** (-jnp.arange(half, dtype=jnp.float32) / half)
    ang = pos.astype(jnp.float32)[:, None] * inv[None, :]
    cos = jnp.cos(ang)[None, :, None, :]
    sin = jnp.sin(ang)[None, :, None, :]
    xf = x.astype(jnp.float32)
    x1, x2 = xf[..., :half], xf[..., half:]
    return jnp.concatenate([x1 * cos - x2 * sin, x2 * cos + x1 * sin], -1).astype(x.dtype)


def _chunk_attend(q, k, v, q_pos, k_pos):
    s = jnp.einsum('bqhd,bkhd->bhqk', q, k).astype(jnp.float32) * MLA_SCALE
    mask = (k_pos // CHUNK)[None, :] <= (q_pos // CHUNK)[:, None]
    s = jnp.where(mask[None, None], s, NEG_INF)
    p = jax.nn.softmax(s, axis=-1).astype(v.dtype)
    return jnp.einsum('bhqk,bkhd->bqhd', p, v)


def _blocked_attend(q, k, v, q_pos, k_pos):
    b, s, hh, dk = q.shape
    nblk = s // Q_BLOCK
    qb = jnp.moveaxis(q.reshape(b, nblk, Q_BLOCK, hh, dk), 1, 0)
    pb = q_pos.reshape(nblk, Q_BLOCK)
    ob = lax.map(lambda qp: _chunk_attend(qp[0], k, v, qp[1], k_pos), (qb, pb))
    return jnp.moveaxis(ob, 0, 1).reshape(b, s, hh, v.shape[-1])


def _s5_scan(u, h0_re, h0_im, a_re, a_im, b_re, b_im, c_re, c_im, d_skip, log_dt):
    nb, s = u.shape[:2]
    f32 = jnp.float32
    ug = u.astype(f32).reshape(nb, s, SSM_GROUPS, SSM_GROUP)
    ar, ai = a_re.astype(f32), a_im.astype(f32)
    dt = jnp.exp(log_dt.astype(f32))[:, None]
    mag = jnp.exp(ar * dt)
    lb_re, lb_im = mag * jnp.cos(ai * dt), mag * jnp.sin(ai * dt)
    nr, ni = lb_re - 1.0, lb_im
    den = ar * ar + ai * ai
    f_re, f_im = (nr * ar + ni * ai) / den, (ni * ar - nr * ai) / den
    br, bi = b_re.astype(f32), b_im.astype(f32)
    bb_re = f_re[..., None] * br - f_im[..., None] * bi
    bb_im = f_re[..., None] * bi + f_im[..., None] * br
    bu_re = jnp.einsum('gph,bsgh->bsgp', bb_re, ug)
    bu_im = jnp.einsum('gph,bsgh->bsgp', bb_im, ug)
    h0r, h0i = h0_re.astype(f32), h0_im.astype(f32)
    bu_re = bu_re.at[:, 0].add(lb_re * h0r - lb_im * h0i)
    bu_im = bu_im.at[:, 0].add(lb_re * h0i + lb_im * h0r)
    a_seq_re = jnp.broadcast_to(lb_re, bu_re.shape)
    a_seq_im = jnp.broadcast_to(lb_im, bu_im.shape)

    def combine(e1, e2):
        a1r, a1i, b1r, b1i = e1
        a2r, a2i, b2r, b2i = e2
        return (a1r * a2r - a1i * a2i, a1r * a2i + a1i * a2r,
                a2r * b1r - a2i * b1i + b2r, a2r * b1i + a2i * b1r + b2i)

    _, _, xr, xi = lax.associative_scan(combine, (a_seq_re, a_seq_im, bu_re, bu_im), axis=1)
    y = (jnp.einsum('ghp,bsgp->bsgh', c_re.astype(f32), xr)
         - jnp.einsum('ghp,bsgp->bsgh', c_im.astype(f32), xi)
         + d_skip.astype(f32) * ug)
    return y.reshape(nb, s, SSM_WIDTH).astype(u.dtype), xr[:, -1], xi[:, -1]


def _token_mixer(h, pos, past_ckv, past_kpe, ssm0_re, ssm0_im, block_sweep,
                 w_in, g_q, w_q_up, g_kv, w_kv_up, a_re, a_im, b_re, b_im, c_re, c_im,
                 d_skip, log_dt, w_glu, g_out_ssm, g_out_mla, w_o):
    nb, s = h.shape[:2]
    proj = jnp.einsum('bsd,de->bse', h, w_in)
    u, c_q, c_kv, k_pe = jnp.split(
        proj, [SSM_WIDTH, SSM_WIDTH + Q_LORA, SSM_WIDTH + Q_LORA + KV_LORA], axis=-1)
    y_ssm, ssm_re, ssm_im = _s5_scan(u, ssm0_re, ssm0_im, a_re, a_im, b_re, b_im,
                                     c_re, c_im, d_skip, log_dt)
    g = jnp.einsum('bsc,ce->bse', jax.nn.gelu(y_ssm), w_glu)
    o_ssm = g[..., :SSM_WIDTH] * jax.nn.sigmoid(g[..., SSM_WIDTH:])
    q = jnp.einsum('bsr,rhe->bshe', _rms_norm(c_q, g_q), w_q_up)
    q = jnp.concatenate([q[..., :QK_NOPE], _rope(q[..., QK_NOPE:], pos)], -1)
    ckv_new = _rms_norm(c_kv, g_kv)
    kpe_new = _rope(k_pe[:, :, None, :], pos)[:, :, 0, :]
    if past_ckv is None:
        ckv_all, kpe_all, k_pos = ckv_new, kpe_new, pos
    else:
        past_len = past_ckv.shape[1]
        ckv_all = jnp.concatenate([past_ckv.astype(ckv_new.dtype), ckv_new], 1)
        kpe_all = jnp.concatenate([past_kpe.astype(kpe_new.dtype), kpe_new], 1)
        k_pos = jnp.concatenate([jnp.arange(past_len, dtype=jnp.int32), pos])
    kv = jnp.einsum('bsr,rhe->bshe', ckv_all, w_kv_up)
    k = jnp.concatenate(
        [kv[..., :QK_NOPE], jnp.broadcast_to(kpe_all[:, :, None, :], kv.shape[:3] + (QK_ROPE,))], -1)
    v = kv[..., QK_NOPE:]
    if block_sweep:
        attn = _blocked_attend(q, k, v, pos, k_pos)
    else:
        attn = _chunk_attend(q, k, v, pos, k_pos)
    o_mla = attn.reshape(nb, s, MLA_WIDTH)
    mixed = jnp.concatenate([_rms_norm(o_ssm, g_out_ssm), _rms_norm(o_mla, g_out_mla)], -1)
    return jnp.einsum('bsc,cd->bsd', mixed, w_o), ckv_new, kpe_new, ssm_re, ssm_im


def _mem_kv(mem, w_xk, w_xv):
    return (jnp.einsum('bmd,dhe->bmhe', mem, w_xk), jnp.einsum('bmd,dhe->bmhe', mem, w_xv))


def _mem_attend(h, mem_k, mem_v, w_xq, w_xo):
    q = jnp.einsum('bsd,dhe->bshe', h, w_xq)
    s = jnp.einsum('bshe,bmhe->bhsm', q, mem_k.astype(q.dtype)).astype(jnp.float32) * (X_HEAD_DIM ** -0.5)
    p = jax.nn.softmax(s, axis=-1).astype(q.dtype)
    o = jnp.einsum('bhsm,bmhe->bshe', p, mem_v.astype(q.dtype))
    return jnp.einsum('bshe,hed->bsd', o, w_xo)


def _sq_relu_mlp(h, w_ff1, w_ff2):
    z = jax.nn.relu(jnp.einsum('bsd,df->bsf', h, w_ff1))
    return jnp.einsum('bsf,fd->bsd', z * z, w_ff2)


def _layer(h, pos, past_ckv, past_kpe, ssm0_re, ssm0_im, mem_k, mem_v, block_sweep,
           w_in, g_q, w_q_up, g_kv, w_kv_up, a_re, a_im, b_re, b_im, c_re, c_im, d_skip,
           log_dt, w_glu, g_out_ssm, g_out_mla, w_o, w_xq, w_xo, w_ff1, w_ff2, ln_g, ln_b):
    a, ckv, kpe, sr, si = _token_mixer(
        h, pos, past_ckv, past_kpe, ssm0_re, ssm0_im, block_sweep,
        w_in, g_q, w_q_up, g_kv, w_kv_up, a_re, a_im, b_re, b_im, c_re, c_im,
        d_skip, log_dt, w_glu, g_out_ssm, g_out_mla, w_o)
    h = _layer_norm(ALPHA * h + a, ln_g[0], ln_b[0])
    h = _layer_norm(ALPHA * h + _mem_attend(h, mem_k, mem_v, w_xq, w_xo), ln_g[1], ln_b[1])
    h = _layer_norm(ALPHA * h + _sq_relu_mlp(h, w_ff1, w_ff2), ln_g[2], ln_b[2])
    return h, ckv, kpe, sr, si


def setup_inputs(seed: int = 0) -> dict:
    key = jax.random.key(seed)
    ks = iter(jax.random.split(key, 40))
    f32 = jnp.float32

    def nrm(shape, scale):
        return jax.random.normal(next(ks), shape, f32) * scale

    L, G, P, H = DEPTH, SSM_GROUPS, SSM_STATE, SSM_GROUP
    n = jnp.arange(P, dtype=f32)
    log_dt = math.log(DT_MIN) + jax.random.uniform(next(ks), (L, G), f32) * (math.log(DT_MAX) - math.log(DT_MIN))
    return {
        "x_prompt": nrm((BATCH, SEQ, D_MODEL), 1.0),
        "x_sample": nrm((DEC_BATCH, DEC_SEQ, D_MODEL), 1.0),
        "mem_prompt": nrm((BATCH, N_MEM, D_MODEL), 1.0),
        "cache_mla_ckv": nrm((L, DEC_BATCH, PAST_LEN, KV_LORA), 1.0),
        "cache_mla_kpe": nrm((L, DEC_BATCH, PAST_LEN, QK_ROPE), 1.0),
        "state_ssm_re": nrm((L, DEC_BATCH, G, P), 0.1),
        "state_ssm_im": nrm((L, DEC_BATCH, G, P), 0.1),
        "cache_mem_k": nrm((L, DEC_BATCH, N_MEM, X_HEADS, X_HEAD_DIM), 1.0),
        "cache_mem_v": nrm((L, DEC_BATCH, N_MEM, X_HEADS, X_HEAD_DIM), 1.0),
        "w_in": nrm((L, D_MODEL, IN_WIDTH), D_MODEL ** -0.5),
        "g_q": 1.0 + nrm((L, Q_LORA), 0.01),
        "w_q_up": nrm((L, Q_LORA, MLA_HEADS, QK_NOPE + QK_ROPE), Q_LORA ** -0.5),
        "g_kv": 1.0 + nrm((L, KV_LORA), 0.01),
        "w_kv_up": nrm((L, KV_LORA, MLA_HEADS, QK_NOPE + V_HEAD), KV_LORA ** -0.5),
        "a_re": -0.5 + nrm((L, G, P), 0.01),
        "a_im": math.pi * n + nrm((L, G, P), 0.01),
        "b_re": nrm((L, G, P, H), (2 * H) ** -0.5),
        "b_im": nrm((L, G, P, H), (2 * H) ** -0.5),
        "c_re": nrm((L, G, H, P), (2 * P) ** -0.5),
        "c_im": nrm((L, G, H, P), (2 * P) ** -0.5),
        "d_skip": nrm((L, G, H), 1.0),
        "log_dt": log_dt,
        "w_glu": nrm((L, SSM_WIDTH, 2 * SSM_WIDTH), SSM_WIDTH ** -0.5),
        "g_out_ssm": 1.0 + nrm((L, SSM_WIDTH), 0.01),
        "g_out_mla": 1.0 + nrm((L, MLA_WIDTH), 0.01),
        "w_o": nrm((L, MIX_WIDTH, D_MODEL), MIX_WIDTH ** -0.5 * BETA),
        "w_xq": nrm((L, D_MODEL, X_HEADS, X_HEAD_DIM), D_MODEL ** -0.5),
        "w_xk": nrm((L, D_MODEL, X_HEADS, X_HEAD_DIM), D_MODEL ** -0.5),
        "w_xv": nrm((L, D_MODEL, X_HEADS, X_HEAD_DIM), D_MODEL ** -0.5),
        "w_xo": nrm((L, X_HEADS, X_HEAD_DIM, D_MODEL), D_MODEL ** -0.5 * BETA),
        "w_ff1": nrm((L, D_MODEL, D_FF), D_MODEL ** -0.5),
        "w_ff2": nrm((L, D_FF, D_MODEL), D_FF ** -0.5 * BETA),
        "ln_g": 1.0 + nrm((L, 3, D_MODEL), 0.01),
        "ln_b": nrm((L, 3, D_MODEL), 0.01),
    }


def reference(x_prompt, x_sample, mem_prompt, cache_mla_ckv, cache_mla_kpe, state_ssm_re,
              state_ssm_im, cache_mem_k, cache_mem_v, w_in, g_q, w_q_up, g_kv, w_kv_up,
              a_re, a_im, b_re, b_im, c_re, c_im, d_skip, log_dt, w_glu, g_out_ssm,
              g_out_mla, w_o, w_xq, w_xk, w_xv, w_xo, w_ff1, w_ff2, ln_g, ln_b):
    nbp, sp = x_prompt.shape[:2]
    sd = x_sample.shape[1]
    past_len = cache_mla_ckv.shape[2]
    pos_p = jnp.arange(sp, dtype=jnp.int32)
    pos_s = past_len + jnp.arange(sd, dtype=jnp.int32)
    zero_state = jnp.zeros((nbp, SSM_GROUPS, SSM_STATE), jnp.float32)

    hp, hs = x_prompt, x_sample
    ckv_p, kpe_p, sre_p, sim_p, mk_p, mv_p = [], [], [], [], [], []
    ckv_s, kpe_s, sre_s, sim_s = [], [], [], []
    for l in range(DEPTH):
        lw = (w_in[l], g_q[l], w_q_up[l], g_kv[l], w_kv_up[l], a_re[l], a_im[l], b_re[l],
              b_im[l], c_re[l], c_im[l], d_skip[l], log_dt[l], w_glu[l], g_out_ssm[l],
              g_out_mla[l], w_o[l], w_xq[l], w_xo[l], w_ff1[l], w_ff2[l], ln_g[l], ln_b[l])
        mk, mv = _mem_kv(mem_prompt, w_xk[l], w_xv[l])
        hp, c1, k1, r1, i1 = _layer(hp, pos_p, None, None, zero_state, zero_state,
                                    mk, mv, True, *lw)
        ckv_p.append(c1); kpe_p.append(k1); sre_p.append(r1); sim_p.append(i1)
        mk_p.append(mk); mv_p.append(mv)
        hs, c2, k2, r2, i2 = _layer(hs, pos_s, cache_mla_ckv[l], cache_mla_kpe[l],
                                    state_ssm_re[l], state_ssm_im[l],
                                    cache_mem_k[l], cache_mem_v[l], False, *lw)
        ckv_s.append(c2); kpe_s.append(k2); sre_s.append(r2); sim_s.append(i2)

    return (hp, hs,
            jnp.stack(ckv_p), jnp.stack(kpe_p), jnp.stack(sre_p), jnp.stack(sim_p),
            jnp.stack(mk_p), jnp.stack(mv_p),
            jnp.stack(ckv_s), jnp.stack(kpe_s), jnp.stack(sre_s), jnp.stack(sim_s))
```

```python
import math
from contextlib import ExitStack

import numpy as np
import concourse.bass as bass
import concourse.mybir as mybir
from concourse.bass_utils import run_bass_kernel_spmd

F32 = mybir.dt.float32
BF16 = mybir.dt.bfloat16
I32 = mybir.dt.int32
AF = mybir.ActivationFunctionType
ALU = mybir.AluOpType
AX = mybir.AxisListType

NCORES = 8
D = 1024
NTP = 128
NOWN = 17
EPS = 1e-5
ALPHA = 2.0 ** 0.25
MLA_SCALE = 192.0 ** -0.5
X_SCALE = 256.0 ** -0.5
TWO_PI = 2.0 * math.pi
NEG = -1e30
NRING = 24
COMPUTE = ("pe", "act", "dve", "pool")


class Buf:
    __slots__ = ("name", "lw", "rd")

    def __init__(self, name=""):
        self.name = name
        self.lw = None
        self.rd = []


def _flat(bs):
    out = []
    for b in bs:
        if isinstance(b, (tuple, list)):
            out.extend(b)
        else:
            out.append(b)
    return out


class Op:
    __slots__ = ("eng", "fn", "deps", "isdma", "flag", "cnt", "ring", "n")

    def __init__(self, eng, fn, isdma):
        self.eng = eng
        self.fn = fn
        self.isdma = isdma
        self.deps = set()
        self.flag = False
        self.cnt = 0
        self.ring = None
        self.n = 0


class Prog:
    def __init__(self, nc):
        self.nc = nc
        self.ops = {e: [] for e in ("pe", "act", "dve", "pool", "sp")}
        self.ndma = {e: 0 for e in self.ops}
        self.allops = []
        self.floor = []
        self.dmas_since = []

    def _add(self, eng, fn, reads, writes, isdma):
        op = Op(eng, fn, isdma)
        reads = _flat(reads)
        writes = _flat(writes)
        for f in self.floor:
            op.deps.add(f)
        for b in reads:
            if b.lw is not None:
                op.deps.add(b.lw)
        for b in writes:
            if b.lw is not None:
                op.deps.add(b.lw)
            for r in b.rd:
                op.deps.add(r)
        for b in reads:
            b.rd.append(op)
        for b in writes:
            b.lw = op
            b.rd = []
        op.deps.discard(op)
        if isdma:
            op.n = self.ndma[eng]
            self.ndma[eng] += 1
            self.dmas_since.append(op)
        self.ops[eng].append(op)
        self.allops.append(op)
        return op

    def op(self, eng, fn, reads=(), writes=()):
        return self._add(eng, fn, reads, writes, False)

    def dma(self, eng, fn, reads=(), writes=()):
        return self._add(eng, fn, reads, writes, True)

    def barrier(self, fn):
        op = Op("pool", fn, False)
        for f in self.floor:
            op.deps.add(f)
        for e in COMPUTE:
            for o in reversed(self.ops[e]):
                if not o.isdma:
                    op.deps.add(o)
                    break
        for o in self.dmas_since:
            op.deps.add(o)
        self.dmas_since = []
        self.ops["pool"].append(op)
        self.allops.append(op)
        self.floor = [op]

    def emit(self, stack):
        nc = self.nc
        for op in self.allops:
            for d in op.deps:
                if d.eng == "pe" and op.eng == "pe" and not d.isdma:
                    continue
                d.flag = True
        sems = {e: stack.enter_context(nc.semaphore("s_" + e)) for e in COMPUTE}
        rings = {}
        for e in self.ops:
            if self.ndma[e]:
                rings[e] = [stack.enter_context(nc.semaphore("r_%s_%d" % (e, i))) for i in range(NRING)]
        for e in self.ops:
            c = 0
            for op in self.ops[e]:
                if op.isdma:
                    op.ring = rings[e][op.n % NRING]
                    op.cnt = 16 * (op.n // NRING + 1)
                elif op.flag:
                    c += 1
                    op.cnt = c
        block = stack.enter_context(nc.Block())

        def run(e, h):
            waited = {}

            def wait(sem, val):
                k = id(sem)
                if waited.get(k, 0) >= val:
                    return
                waited[k] = val
                h.wait_ge(sem, val)

            for op in self.ops[e]:
                for d in op.deps:
                    if d.isdma:
                        wait(d.ring, d.cnt)
                    else:
                        if d.eng == "pe" and e == "pe":
                            continue
                        wait(sems[d.eng], d.cnt)
                if op.isdma and op.n >= NRING:
                    wait(op.ring, op.cnt - 16)
                ins = op.fn(h)
                if op.isdma:
                    ins.then_inc(op.ring, 16)
                elif op.flag:
                    ins.then_inc(sems[e], 1)
            if e in rings:
                n = self.ndma[e]
                for i in range(min(n, NRING)):
                    last = ((n - 1 - i) // NRING) * NRING + i
                    wait(rings[e][i], 16 * (last // NRING + 1))

        @block.tensor
        def _(h):
            run("pe", h)

        @block.scalar
        def _(h):
            run("act", h)

        @block.vector
        def _(h):
            run("dve", h)

        @block.gpsimd
        def _(h):
            run("pool", h)

        @block.sync
        def _(h):
            run("sp", h)


class V:
    __slots__ = ("ap", "b")

    def __init__(self, ap, b):
        self.ap = ap
        self.b = b

    def __getitem__(self, idx):
        return V(self.ap[idx], self.b)

    def re(self, pat, **kw):
        return V(self.ap.rearrange(pat, **kw), self.b)

    def cast(self, dt):
        return V(self.ap.bitcast(dt), self.b)

    def bc(self, shape):
        return V(self.ap.broadcast_to(shape), self.b)

    def un(self, ax):
        return V(self.ap.unsqueeze(ax), self.b)


def build_nc():
    nc = bass.Bass("TRN2", target_bir_lowering=False)
    P = Prog(nc)
    dram = {}

    def din(name, shape, dt=F32):
        t = nc.dram_tensor(name, list(shape), dt, kind="ExternalInput")
        dram[name] = V(t.ap(), Buf(name))
        return dram[name]

    def dout(name, shape):
        t = nc.dram_tensor(name, list(shape), F32, kind="ExternalOutput")
        dram[name] = V(t.ap(), Buf(name))
        return dram[name]

    def dscr(name, shape, dt=F32):
        t = nc.dram_tensor(name, list(shape), dt)
        return V(t.ap(), Buf(name))

    xp = din("xp", [NTP * 128, D])
    xo = din("xo", [NOWN * 128, D])
    mem = din("mem", [256, D])
    ident_d = din("ident", [128, 128])
    w_in_d = din("w_in", [D, 1216]); w_q_d = din("w_q", [384, 768]); w_kv_d = din("w_kv", [256, 1024])
    w_glu_d = din("w_glu", [512, 1024]); w_o_d = din("w_o", [D, D]); w_xq_d = din("w_xq", [D, D])
    w_xk_d = din("w_xk", [D, D]); w_xv_d = din("w_xv", [D, D]); w_xo_d = din("w_xo", [D, D])
    w_ff1_d = din("w_ff1", [D, 4096]); w_ff2_d = din("w_ff2", [4096, D])
    gq_d = din("gq", [128, 384]); gkv_d = din("gkv", [128, 256]); gos_d = din("gos", [128, 512]); gom_d = din("gom", [128, 512])
    lng_d = din("lng", [128, 3 * D]); lnb_d = din("lnb", [128, 3 * D])
    ropec_p = din("ropec_p", [NTP * 128, 64]); ropes_p = din("ropes_p", [NTP * 128, 64])
    ropec_o = din("ropec_o", [NOWN * 128, 64]); ropes_o = din("ropes_o", [NOWN * 128, 64])
    maskd_d = din("maskd", [128, 1024]); onehot_d = din("onehot", [128, 8])
    ar_d = din("ar", [128, 16]); ai_d = din("ai", [128, 16]); ldt_d = din("ldt", [128, 16])
    bre_d = din("bre", [128, 256]); bim_d = din("bim", [128, 256]); jj_d = din("jj", [128, 128]); jj2_d = din("jj2", [128, 128])
    cre_d = din("cre", [128, 512]); cimn_d = din("cimn", [128, 512]); dblk_d = din("dblk", [128, 512])
    s0_d = din("s0", [128, 128])
    cckv_d = din("cckv", [4 * 1024, 256]); ckpe_d = din("ckpe", [4 * 1024, 64])
    cmk_d = din("cmk", [4 * 256, D]); cmv_d = din("cmv", [4 * 256, D])

    y_o = dout("y_o", [NOWN * 128, D])
    ckv_o = dout("ckv_o", [NTP * 128, 256]); kpe_o = dout("kpe_o", [NTP * 128, 64])
    ssmp_o = dout("ssmp_o", [128, 32])
    memk_o = dout("memk_o", [256, D]); memv_o = dout("memv_o", [256, D])
    ckvs_o = dout("ckvs_o", [128, 256]); kpes_o = dout("kpes_o", [128, 64]); ssms_o = dout("ssms_o", [128, 128])

    h1_d = dscr("h1_d", [NOWN * 128, D]); h2_d = dscr("h2_d", [NOWN * 128, D])
    tab_d = {n_: dscr("tab_" + n_, [128, 2048]) for n_ in ("S_t", "C_t", "R_t")}
    tab_d["WBre"] = dscr("tab_WBre", [128, 2048], BF16); tab_d["WBim"] = dscr("tab_WBim", [128, 2048], BF16)

    with ExitStack() as st:
        ARENA_N = 52000
        arena = st.enter_context(nc.sbuf_tensor("arena", [128, ARENA_N], F32))
        bump = [0]

        def alloc(shape, dt=F32, name=""):
            n = 1
            for s_ in shape[1:]:
                n *= s_
            words = (n * (2 if dt == BF16 else 4) + 3) // 4
            words = (words + 7) // 8 * 8
            off = bump[0]
            bump[0] += words
            assert bump[0] <= ARENA_N, ("SBUF arena overflow", name, bump[0])
            ap = arena[:, off:off + words]
            if dt != F32:
                ap = ap.bitcast(dt)
            ap = ap[:, 0:n]
            if len(shape) > 2:
                names = " ".join("d%d" % i for i in range(len(shape) - 1))
                kw = {"d%d" % i: shape[i + 1] for i in range(len(shape) - 1)}
                ap = ap.rearrange("p (%s) -> p %s" % (names, names), **kw)
            return V(ap, Buf(name))

        banks = []
        dbl = []
        for d_ in range(4):
            t = st.enter_context(nc.psum_tensor("dbank%d" % d_, [128, 1024], F32))
            b0 = Buf("bank%d" % (2 * d_)); b1 = Buf("bank%d" % (2 * d_ + 1))
            banks.append(V(t[:, 0:512], b0)); banks.append(V(t[:, 512:1024], b1))
            dbl.append(V(t[:], (b0, b1)))
        SC = dbl[3]
        SCs = [dbl[3], dbl[0]]

        def mm(out, lhsT, rhs, start=True, stop=True):
            P.op("pe", lambda e: e.matmul(out.ap, lhsT=lhsT.ap, rhs=rhs.ap, start=start, stop=stop),
                 reads=[lhsT.b, rhs.b], writes=[out.b])

        def tr(out, in_, idt):
            P.op("pe", lambda e: e.transpose(out=out.ap, in_=in_.ap, identity=idt.ap), reads=[in_.b, idt.b], writes=[out.b])

        def act(out, in_, func, bias=None, scale=None, accum=None):
            kw = {}
            rd = [in_.b]
            wr = [out.b]
            if bias is not None:
                if isinstance(bias, V):
                    kw["bias"] = bias.ap; rd.append(bias.b)
                else:
                    kw["bias"] = bias
            if scale is not None:
                if isinstance(scale, V):
                    kw["scale"] = scale.ap; rd.append(scale.b)
                else:
                    kw["scale"] = scale
            if accum is not None:
                kw["accum_out"] = accum.ap; wr.append(accum.b)
            P.op("act", lambda e: e.activation(out=out.ap, in_=in_.ap, func=func, **kw), reads=rd, writes=wr)

        def acopy(out, in_):
            P.op("act", lambda e: e.copy(out=out.ap, in_=in_.ap), reads=[in_.b], writes=[out.b])

        def vcopy(out, in_, eng="dve"):
            P.op(eng, lambda e: e.tensor_copy(out=out.ap, in_=in_.ap), reads=[in_.b], writes=[out.b])

        def tt(out, in0, in1, op, eng="dve"):
            P.op(eng, lambda e: e.tensor_tensor(out=out.ap, in0=in0.ap, in1=in1.ap, op=op), reads=[in0.b, in1.b], writes=[out.b])

        def ts(out, in0, s1, s2, op0, op1=None, eng="dve"):
            rd = [in0.b]
            a1 = s1
            a2 = s2
            if isinstance(s1, V):
                a1 = s1.ap; rd.append(s1.b)
            if isinstance(s2, V):
                a2 = s2.ap; rd.append(s2.b)
            if op1 is None:
                P.op(eng, lambda e: e.tensor_scalar(out=out.ap, in0=in0.ap, scalar1=a1, scalar2=None, op0=op0), reads=rd, writes=[out.b])
            else:
                P.op(eng, lambda e: e.tensor_scalar(out=out.ap, in0=in0.ap, scalar1=a1, scalar2=a2, op0=op0, op1=op1), reads=rd, writes=[out.b])

        def stt(out, in0, scalar, in1, op0, op1):
            rd = [in0.b, in1.b]
            a = scalar
            if isinstance(scalar, V):
                a = scalar.ap; rd.append(scalar.b)
            P.op("dve", lambda e: e.scalar_tensor_tensor(out=out.ap, in0=in0.ap, scalar=a, in1=in1.ap, op0=op0, op1=op1), reads=rd, writes=[out.b])

        def red(out, in_, op, axis=AX.X):
            P.op("dve", lambda e: e.tensor_reduce(out=out.ap, in_=in_.ap, axis=axis, op=op), reads=[in_.b], writes=[out.b])

        def recip(out, in_):
            P.op("dve", lambda e: e.reciprocal(out=out.ap, in_=in_.ap), reads=[in_.b], writes=[out.b])

        def scan(out, d0, d1, init):
            P.op("dve", lambda e: e.tensor_tensor_scan(out=out.ap, data0=d0.ap, data1=d1.ap, initial=init.ap, op0=ALU.mult, op1=ALU.add),
                 reads=[d0.b, d1.b, init.b], writes=[out.b])

        def mset(out, val, eng="pool"):
            P.op(eng, lambda e: e.memset(out.ap, val), writes=[out.b])

        def ld(out, in_, eng="sp"):
            P.dma(eng, lambda e: e.dma_start(out=out.ap, in_=in_.ap), reads=[in_.b], writes=[out.b])

        def ldw(out, in_):
            P.dma("pool", lambda e: e.dma_start(out=out.ap, in_=in_.ap), reads=[in_.b], writes=[out.b])

        ident = alloc([128, 128], F32, "ident"); identb = alloc([128, 128], BF16, "identb")
        bar_scr = alloc([128, 8], F32, "barscr")
        ld(ident, ident_d); ldw(identb, ident_d)
        epsc = alloc([128, 1], F32, "epsc"); mset(epsc, EPS)

        def barrier():
            P.barrier(lambda e: e.memset(bar_scr.ap, 0.0))

        def load_xT(src_rows, xt, xT, bank):
            ld(xt, src_rows)
            for hb in range(2):
                for k in range(4):
                    kk = hb * 4 + k
                    tr(bank[:, k * 128:(k + 1) * 128], xt[:, kk * 128:(kk + 1) * 128], ident)
                src = bank.re("p (k n) -> p k n", k=4)
                if hb == 0:
                    acopy(xT[:, 0:4, :], src)
                else:
                    vcopy(xT[:, 4:8, :], src)

        def transpose_to(dst, src, ncol, bank, eng="act", rows=128):
            bb = bank.cast(BF16)
            for k in range(ncol):
                tr(bb[:, k * 128:k * 128 + rows], src[:, k * 128:(k + 1) * 128], identb[0:rows, 0:rows])
            s_ = bb[:, 0:ncol * 128].re("p (k n) -> p k n", k=ncol)[:, :, 0:rows]
            if eng == "act":
                acopy(dst, s_)
            else:
                vcopy(dst, s_)

        def rmsnorm(out, src, n, gtile, sq, ss):
            act(sq, src, AF.Square, accum=ss)
            act(ss, ss, AF.Ln, scale=1.0 / n, bias=epsc)
            act(ss, ss, AF.Exp, scale=-0.5)
            stt(out, src, ss, gtile, ALU.mult, ALU.mult)

        def layernorm(out, r, g, b, stats, mv):
            for c in range(2):
                P.op("dve", lambda e, c=c: e.bn_stats(out=stats.ap[:, c * 6:(c + 1) * 6], in_=r.ap[:, c * 512:(c + 1) * 512]), reads=[r.b], writes=[stats.b])
            P.op("dve", lambda e: e.bn_aggr(out=mv.ap[:, 0:2], in_=stats.ap[:, 0:12]), reads=[stats.b], writes=[mv.b])
            act(mv[:, 1:2], mv[:, 1:2], AF.Ln, bias=epsc)
            act(mv[:, 1:2], mv[:, 1:2], AF.Exp, scale=-0.5)
            ts(out, r, mv[:, 0:1], mv[:, 1:2], ALU.subtract, ALU.mult)
            tt(out, out, g, ALU.mult, eng="pool")
            tt(out, out, b, ALU.add, eng="pool")

        def rope(out, src, cc, ss_, tmp, nh, eng="dve"):
            ccb = cc.un(1).bc([128, nh, 64]); ssb = ss_.un(1).bc([128, nh, 64])
            tt(out, src, ccb, ALU.mult, eng=eng)
            tt(tmp[:, :, 0:32], src[:, :, 32:64], ssb[:, :, 0:32], ALU.mult, eng=eng)
            tt(tmp[:, :, 32:64], src[:, :, 0:32], ssb[:, :, 32:64], ALU.mult, eng=eng)
            tt(out, out, tmp, ALU.add, eng="pool")

        mkT = alloc([128, 4, 2, 256], BF16, "mkT")
        mvb = alloc([128, 2, D], BF16, "mvb")
        mark_after_mem = bump[0]
        btab = Buf("tab")
        W_in_u = alloc([128, 8, 512], BF16, "W_in_u"); W_in_kv = alloc([128, 8, 320], BF16, "W_in_kv")
        LrT = alloc([128, 16, 128], BF16, "LrT"); LiT = alloc([128, 16, 128], BF16, "LiT")
        BRb = alloc([128, 16, 32], F32, "BRb"); BIb = alloc([128, 16, 32], F32, "BIb")
        lam128 = alloc([128, 2, 16], F32, "lam128")
        Xpp = alloc([128, 2, 32], F32, "Xpp")
        Xown = alloc([128, 16, 32], F32, "Xown")
        ohot = alloc([128, 8], F32, "ohot")
        Oacc = alloc([128, 16, 4, 128], F32, "Oacc")
        m_run = alloc([128, 64], F32, "m_run"); l_run = alloc([128, 64], F32, "l_run")
        Osam = alloc([128, 512], F32, "Osam")
        ssmx = {}
        mark_after_mixer_state = bump[0]
        QTn = alloc([128, NOWN, 4, 128], BF16, "QTn"); QTr = alloc([128, NOWN, 4, 128], BF16, "QTr")
        mark_after_q = bump[0]

        w_in_v = w_in_d.re("(kt p) n -> p kt n", p=128)
        ldw(W_in_u, w_in_v[:, :, 0:512]); ldw(W_in_kv, w_in_v[:, :, 896:1216])
        ld(ohot, onehot_d)

        S_t = alloc([128, 16, 128], F32, "S_t"); C_t = alloc([128, 16, 128], F32, "C_t"); R_t = alloc([128, 16, 128], F32, "R_t")
        for v_ in (S_t, C_t, R_t):
            v_.b = btab
        WBre = alloc([128, 16, 128], BF16, "WBre"); WBim = alloc([128, 16, 128], BF16, "WBim")

        p0 = bump[0]
        ar = alloc([128, 16]); ai = alloc([128, 16]); ldt = alloc([128, 16]); jj = alloc([128, 128])
        bre = alloc([128, 16, 16]); bim = alloc([128, 16, 16])
        bsm = Buf("ssm_small")
        for v_ in (ar, ai, ldt, jj, bre, bim):
            v_.b = bsm
        ld(ar, ar_d); ld(ai, ai_d); ld(ldt, ldt_d); ld(jj, jj_d)
        ld(bre.re("p a b -> p (a b)"), bre_d); ld(bim.re("p a b -> p (a b)"), bim_d)
        dt_ = alloc([128, 16]); th = alloc([128, 16]); rr = alloc([128, 16])
        sm = [alloc([128, 16]) for _ in range(8)]
        for v_ in [dt_, th, rr] + sm:
            v_.b = bsm
        act(dt_, ldt, AF.Exp)
        tt(th, ai, dt_, ALU.mult)
        tt(rr, ar, dt_, ALU.mult)
        act(rr, rr, AF.Exp)
        A_t = alloc([128, 16, 128]); T1 = alloc([128, 2048]); TI = alloc([128, 2048], I32)
        for v_ in (A_t, T1, TI):
            v_.b = btab
        jjb = jj.un(1).bc([128, 16, 128])
        tt(A_t, jjb, th.un(2).bc([128, 16, 128]), ALU.mult)
        tt(R_t, jjb, rr.un(2).bc([128, 16, 128]), ALU.max)
        tt(R_t, R_t, rr.un(2).bc([128, 16, 128]), ALU.min)
        Af = A_t.re("p a b -> p (a b)")
        TIf = TI.cast(F32)

        def sin_of(out, shift):
            ts(T1, Af, shift, 1.0 / TWO_PI, ALU.add, ALU.mult)
            vcopy(TI, T1)
            vcopy(T1, TI)
            stt(T1, T1, -TWO_PI, Af, ALU.mult, ALU.add)
            if shift != 0.0:
                ts(T1, T1, shift, None, ALU.add)
            ts(TIf, T1, math.pi, -TWO_PI, ALU.is_gt, ALU.mult)
            tt(T1, T1, TIf, ALU.add)
            ts(TIf, T1, -math.pi, TWO_PI, ALU.is_lt, ALU.mult)
            tt(T1, T1, TIf, ALU.add)
            ts(T1, T1, math.pi, -math.pi, ALU.min, ALU.max)
            act(out, T1, AF.Sin)

        sin_of(S_t.re("p a b -> p (a b)"), 0.0)
        sin_of(C_t.re("p a b -> p (a b)"), math.pi / 2)
        lbr, lbi, den, fre, fim, t0_, t1_, t2_ = sm
        cos1 = C_t[:, :, 0]; sin1 = S_t[:, :, 0]
        tt(lbr, rr, cos1, ALU.mult)
        ts(lbr, lbr, -1.0, None, ALU.add)
        tt(lbi, rr, sin1, ALU.mult)
        tt(den, ar, ar, ALU.mult)
        tt(t0_, ai, ai, ALU.mult)
        tt(den, den, t0_, ALU.add)
        recip(den, den)
        tt(t0_, lbr, ar, ALU.mult); tt(t1_, lbi, ai, ALU.mult); tt(fre, t0_, t1_, ALU.add); tt(fre, fre, den, ALU.mult)
        tt(t0_, lbi, ar, ALU.mult); tt(t1_, lbr, ai, ALU.mult); tt(fim, t0_, t1_, ALU.subtract); tt(fim, fim, den, ALU.mult)
        Mre = alloc([128, 16, 128]); Mim = alloc([128, 16, 128]); tb = alloc([128, 16, 16]); tb2 = alloc([128, 16, 16])
        Mreb = alloc([128, 16, 128], BF16); Mimb = alloc([128, 16, 128], BF16)
        bM = Buf("M")
        for v_ in (Mre, Mim, tb, tb2, Mreb, Mimb):
            v_.b = bM
        freb = fre.un(2).bc([128, 16, 16]); fimb = fim.un(2).bc([128, 16, 16])
        mset(Mre, 0.0); mset(Mim, 0.0)
        tt(tb, bre, freb, ALU.mult); tt(tb2, bim, fimb, ALU.mult)
        for lo, col in ((0, 0), (64, 16)):
            for j4 in range(4):
                tt(Mre[lo:lo + 64, j4::4, 32 * j4 + col:32 * j4 + col + 16], tb[lo:lo + 64, j4::4, :], tb2[lo:lo + 64, j4::4, :], ALU.subtract)
        tt(tb, bim, freb, ALU.mult); tt(tb2, bre, fimb, ALU.mult)
        for lo, col in ((0, 0), (64, 16)):
            for j4 in range(4):
                tt(Mim[lo:lo + 64, j4::4, 32 * j4 + col:32 * j4 + col + 16], tb[lo:lo + 64, j4::4, :], tb2[lo:lo + 64, j4::4, :], ALU.add)
        vcopy(Mreb, Mre); vcopy(Mimb, Mim)
        for M_, WB_ in ((Mreb, WBre), (Mimb, WBim)):
            for hf in range(2):
                bb = banks[0].cast(BF16)
                for q in range(8):
                    tr(bb[:, q * 128:(q + 1) * 128], M_[:, 8 * hf + q, :], identb)
                vcopy(WB_[:, 8 * hf:8 * hf + 8, :].re("p a b -> p (a b)"), bb)

        mset(BRb, 0.0); mset(BIb, 0.0)
        tt(tb, bre, freb, ALU.mult); tt(tb2, bim, fimb, ALU.mult)
        for lo, col in ((0, 0), (64, 16)):
            tt(BRb[lo:lo + 64, :, col:col + 16], tb[lo:lo + 64], tb2[lo:lo + 64], ALU.subtract)
        tt(tb, bim, freb, ALU.mult); tt(tb2, bre, fimb, ALU.mult)
        for lo, col in ((0, 0), (64, 16)):
            tt(BIb[lo:lo + 64, :, col:col + 16], tb[lo:lo + 64], tb2[lo:lo + 64], ALU.add)
        lnr = alloc([128, 16]); lnr.b = bsm
        tt(lnr, ar, dt_, ALU.mult)
        act(t2_, lnr, AF.Exp, scale=128.0)
        tt(lam128[:, 0, :], t2_, C_t[:, :, 127], ALU.mult)
        tt(lam128[:, 1, :], t2_, S_t[:, :, 127], ALU.mult)
        jj2 = alloc([128, 128]); jj2.b = bsm
        ld(jj2, jj2_d)
        C2 = Mre; S2 = Mim; Mag = T1.re("p (a b) -> p a b", a=16)
        jj2b = jj2.un(1).bc([128, 16, 128])
        tt(A_t, jj2b, th.un(2).bc([128, 16, 128]), ALU.mult)
        sin_of(S2.re("p a b -> p (a b)"), 0.0)
        sin_of(C2.re("p a b -> p (a b)"), math.pi / 2)
        tt(Mag, jj2b, lnr.un(2).bc([128, 16, 128]), ALU.mult)
        act(Mag, Mag, AF.Exp)
        Lb = [Mreb, Mimb]
        tt(Lb[0], Mag, C2, ALU.mult); tt(Lb[1], Mag, S2, ALU.mult)
        for M_, LT_ in ((Lb[0], LrT), (Lb[1], LiT)):
            for hf in range(2):
                bb = banks[0].cast(BF16)
                for q in range(8):
                    tr(bb[:, q * 128:(q + 1) * 128], M_[:, 8 * hf + q, :], identb)
                vcopy(LT_[:, 8 * hf:8 * hf + 8, :].re("p a b -> p (a b)"), bb)

        for nm_, v_ in (("S_t", S_t), ("C_t", C_t), ("R_t", R_t)):
            ld(tab_d[nm_], v_.re("p a b -> p (a b)"))
        ld(tab_d["WBre"], WBre.re("p a b -> p (a b)")); ld(tab_d["WBim"], WBim.re("p a b -> p (a b)"))

        barrier()
        bump[0] = mark_after_q
        W_xk = alloc([128, 8, D], BF16, "W_xk"); W_xv = alloc([128, 8, D], BF16, "W_xv")
        ldw(W_xk, w_xk_d.re("(kt p) n -> p kt n", p=128)); ldw(W_xv, w_xv_d.re("(kt p) n -> p kt n", p=128))
        xt0 = alloc([128, D], F32, "xt0"); memT = alloc([128, 8, 256], BF16, "memT"); xT0 = alloc([128, 8, 128], BF16, "xT0")
        mkf = alloc([128, D], F32, "mkf")
        for mt in range(2):
            load_xT(mem[mt * 128:(mt + 1) * 128, :], xt0, xT0, banks[0])
            vcopy(memT[:, :, mt * 128:(mt + 1) * 128], xT0, eng="pool")
            for W_, o_d, keep in ((W_xk, memk_o, False), (W_xv, memv_o, True)):
                for nb in range(2):
                    for k in range(8):
                        mm(banks[1 + nb], xT0[:, k, :], W_[:, k, nb * 512:(nb + 1) * 512], start=(k == 0), stop=(k == 7))
                    acopy(mkf[:, nb * 512:(nb + 1) * 512], banks[1 + nb])
                ld(o_d[mt * 128:(mt + 1) * 128, :], mkf)
                if keep:
                    vcopy(mvb[:, mt, :], mkf)
        for h in range(4):
            for et in range(2):
                c0 = h * 256 + et * 128
                for k in range(8):
                    mm(banks[3][:, 0:256], W_xk[:, k, c0:c0 + 128], memT[:, k, :], start=(k == 0), stop=(k == 7))
                acopy(mkT[:, h, et, :], banks[3][:, 0:256])

        barrier()
        bump[0] = mark_after_q

        W_q = alloc([128, 3, 768], BF16, "W_q")
        ldw(W_q, w_q_d.re("(kt p) n -> p kt n", p=128))
        W_in_q = alloc([128, 8, 384], BF16, "W_in_q")
        ldw(W_in_q, w_in_v[:, :, 512:896])
        gq = alloc([128, 384], F32, "gq"); ld(gq, gq_d)
        NB = 2
        xt = [alloc([128, D], F32, "xt%d" % i) for i in range(NB)]
        xT = [alloc([128, 8, 128], BF16, "xT%d" % i) for i in range(NB)]
        rc = [alloc([128, 64], F32, "rc%d" % i) for i in range(NB)]
        rs = [alloc([128, 64], F32, "rs%d" % i) for i in range(NB)]
        sq = alloc([128, 512], F32, "sq"); ss1 = [alloc([128, 1], F32) for _ in range(NB)]
        cqn = [alloc([128, 384], BF16) for _ in range(NB)]
        cqT = [alloc([128, 3, 128], BF16) for _ in range(NB)]
        qf = [alloc([128, 4, 192], F32) for _ in range(NB)]
        qr = [alloc([128, 4, 64], F32) for _ in range(NB)]
        qtmp = [alloc([128, 4, 64], F32) for _ in range(NB)]
        qb = [alloc([128, 4, 192], BF16) for _ in range(NB)]
        for i in range(NOWN):
            b = i % NB
            load_xT(xo[i * 128:(i + 1) * 128, :], xt[b], xT[b], banks[0])
            ld(rc[b], ropec_o[i * 128:(i + 1) * 128, :]); ld(rs[b], ropes_o[i * 128:(i + 1) * 128, :])
            for k in range(8):
                mm(banks[1][:, 0:384], xT[b][:, k, :], W_in_q[:, k, :], start=(k == 0), stop=(k == 7))
            rmsnorm(cqn[b], banks[1][:, 0:384], 384, gq, sq[:, 0:384], ss1[b])
            transpose_to(cqT[b], cqn[b], 3, banks[2], eng="act")
            for nb, (c0, c1) in enumerate(((0, 512), (512, 768))):
                for k in range(3):
                    mm(SC[:, nb * 512:nb * 512 + (c1 - c0)], cqT[b][:, k, :], W_q[:, k, c0:c1], start=(k == 0), stop=(k == 2))
            acopy(qf[b].re("p a b -> p (a b)"), SC[:, 0:768])
            rope(qr[b], qf[b][:, :, 128:192], rc[b], rs[b], qtmp[b], 4)
            P.op("act", lambda e, b=b: e.mul(out=qb[b].ap[:, :, 0:128], in_=qf[b].ap[:, :, 0:128], mul=MLA_SCALE), reads=[qf[b].b], writes=[qb[b].b])
            P.op("act", lambda e, b=b: e.mul(out=qb[b].ap[:, :, 128:192], in_=qr[b].ap, mul=MLA_SCALE), reads=[qr[b].b], writes=[qb[b].b])
            bb = banks[3].cast(BF16)
            for h in range(4):
                tr(bb[:, h * 128:(h + 1) * 128], qb[b][:, h, 0:128], identb)
                tr(bb[0:64, 512 + h * 128:512 + (h + 1) * 128], qb[b][:, h, 128:192], identb)
            vcopy(QTn[:, i, :, :].re("p a b -> p (a b)"), bb[:, 0:512])
            vcopy(QTr[0:64, i, :, :].re("p a b -> p (a b)"), bb[0:64, 512:1024])

        barrier()
        bump[0] = mark_after_q

        def ssm_rounds(uT_, pbs, init_fn, nseg, ybank=None, last_out=None):
            L = 128 // nseg
            S_t = ssmx["S_t"]; C_t = ssmx["C_t"]; R_t = ssmx["R_t"]; WBre = ssmx["WBre"]; WBim = ssmx["WBim"]
            Cre = ssmx["Cre"]; Cimn = ssmx["Cimn"]; Dblk = ssmx["Dblk"]
            xs4 = ssmx["xs4"]; zl_all = ssmx["zl_all"]
            def do_round(r):
                m4 = ssmx["m4"][r % 2]; wri = ssmx["wri"][r % 2]; zri = ssmx["zri"][r % 2]
                pb = pbs[r % 2]
                for j in range(2):
                    gp = 2 * r + j
                    mm(pb[:, j * 128:(j + 1) * 128], WBre[:, gp, :], uT_[:, gp // 4, :])
                    mm(pb[:, 256 + j * 128:256 + (j + 1) * 128], WBim[:, gp, :], uT_[:, gp // 4, :])
                if nseg == 1:
                    Cq = C_t[:, 2 * r:2 * r + 2, :].re("p a b -> p (a b)"); Sq = S_t[:, 2 * r:2 * r + 2, :].re("p a b -> p (a b)")
                    pre = pb[:, 0:256]; pim = pb[:, 256:512]
                    mk = lambda v_: v_
                else:
                    Cq = C_t[:, 2 * r:2 * r + 2, 0:L].un(2).bc([128, 2, nseg, L]); Sq = S_t[:, 2 * r:2 * r + 2, 0:L].un(2).bc([128, 2, nseg, L])
                    pre = pb[:, 0:256].re("p (a s l) -> p a s l", a=2, s=nseg); pim = pb[:, 256:512].re("p (a s l) -> p a s l", a=2, s=nseg)
                    mk = lambda v_: v_.re("p (a s l) -> p a s l", a=2, s=nseg)
                tt(mk(m4[0]), pre, Cq, ALU.mult); tt(mk(m4[1]), pim, Sq, ALU.mult)
                tt(mk(m4[2]), pim, Cq, ALU.mult); tt(mk(m4[3]), pre, Sq, ALU.mult)
                tt(wri[0], m4[0], m4[1], ALU.add, eng="pool")
                tt(wri[1], m4[2], m4[3], ALU.subtract, eng="pool")

            def do_back(r):
                m4 = ssmx["m4"][r % 2]; wri = ssmx["wri"][r % 2]; zri = ssmx["zri"][r % 2]
                if nseg == 1:
                    Cq = C_t[:, 2 * r:2 * r + 2, :].re("p a b -> p (a b)"); Sq = S_t[:, 2 * r:2 * r + 2, :].re("p a b -> p (a b)")
                else:
                    Cq = C_t[:, 2 * r:2 * r + 2, 0:L].un(2).bc([128, 2, nseg, L]); Sq = S_t[:, 2 * r:2 * r + 2, 0:L].un(2).bc([128, 2, nseg, L])
                for j in range(2):
                    gp = 2 * r + j
                    for s_ in range(nseg):
                        for part in range(2):
                            scan(zri[part][:, j, s_ * L:(s_ + 1) * L], R_t[:, gp, 0:L], wri[part][:, j * 128 + s_ * L:j * 128 + (s_ + 1) * L], init_fn(gp, s_, part))
                if last_out is not None:
                    for part in range(2):
                        src = zri[part].re("p a (s l) -> p a s l", s=nseg)[:, :, :, L - 1]
                        vcopy(zl_all[part][:, 0:nseg, 2 * r:2 * r + 2].re("p s a -> p a s"), src, eng="pool")
                if ybank is not None:
                    dmd = ssmx["dmd"][r % 2]; xrb = ssmx["xrb"]; xib = ssmx["xib"]
                    if nseg == 1:
                        Cd, Sd = Cq, Sq
                        zr_ = zri[0].re("p a b -> p (a b)"); zi_ = zri[1].re("p a b -> p (a b)")
                        dk = lambda v_: v_
                    else:
                        Cd, Sd = Cq, Sq
                        zr_ = zri[0].re("p a (s l) -> p a s l", s=nseg); zi_ = zri[1].re("p a (s l) -> p a s l", s=nseg)
                        dk = lambda v_: v_.re("p (a s l) -> p a s l", a=2, s=nseg)
                    tt(dk(dmd[0]), zr_, Cd, ALU.mult); tt(dk(dmd[1]), zi_, Sd, ALU.mult, eng="pool")
                    tt(dk(dmd[2]), zr_, Sd, ALU.mult); tt(dk(dmd[3]), zi_, Cd, ALU.mult, eng="pool")
                    tt(xrb[r % 2].re("p a b -> p (a b)"), dmd[0], dmd[1], ALU.subtract, eng="pool")
                    tt(xib[r % 2].re("p a b -> p (a b)"), dmd[2], dmd[3], ALU.add, eng="pool")
                    for j in range(2):
                        gp = 2 * r + j
                        o_ = ybank[:, 32 * gp:32 * gp + 32]
                        mm(o_, xrb[r % 2][:, j, :], Cre[:, gp, :], start=True, stop=False)
                        mm(o_, xib[r % 2][:, j, :], Cimn[:, gp, :], start=False, stop=False)
                        mm(o_, uT_[:, gp // 4, :], Dblk[:, gp, :], start=False, stop=True)
            def do_tail():
                if last_out is None:
                    return
                if True:
                    cL = C_t[:, :, L - 1]; sL = S_t[:, :, L - 1]
                    for s_ in range(nseg):
                        o_re, o_im = last_out(s_)
                        zr_l = zl_all[0][:, s_, :]; zi_l = zl_all[1][:, s_, :]
                        tt(xs4[0], zr_l, cL, ALU.mult); tt(xs4[1], zi_l, sL, ALU.mult)
                        tt(xs4[2], zr_l, sL, ALU.mult); tt(xs4[3], zi_l, cL, ALU.mult)
                        tt(o_re, xs4[0], xs4[1], ALU.subtract)
                        tt(o_im, xs4[2], xs4[3], ALU.add)


            def step(k_):
                def f():
                    if k_ == 0:
                        do_round(0)
                    if k_ + 1 < 8:
                        do_round(k_ + 1)
                    do_back(k_)
                return f
            return [step(k_) for k_ in range(8)] + [do_tail]

        def ssm_tile(uT_, pbs, init_fn, nseg, ybank=None, last_out=None):
            for f_ in ssm_rounds(uT_, pbs, init_fn, nseg, ybank, last_out):
                f_()

        W_kv = alloc([128, 2, 4, 256], BF16, "W_kv")
        ldw(W_kv.re("p k h e -> p k (h e)"), w_kv_d.re("(kt p) n -> p kt n", p=128))
        gkv = alloc([128, 256], F32, "gkv"); ld(gkv, gkv_d)
        maskd = alloc([128, 1024], F32, "maskd"); ld(maskd, maskd_d)
        xt = [alloc([128, D], F32, "hxt%d" % i) for i in range(NB)]
        xT = [alloc([128, 8, 128], BF16, "hxT%d" % i) for i in range(NB)]
        utok = [alloc([128, 512], BF16, "hut%d" % i) for i in range(NB)]
        kvf = [alloc([128, 320], F32, "kvf%d" % i) for i in range(NB)]
        prs = [alloc([128, 16, 32], F32, "prs%d" % i) for i in range(NB)]
        pis = [alloc([128, 16, 32], F32, "pis%d" % i) for i in range(NB)]
        rc = [alloc([128, 64], F32) for _ in range(NB)]; rs = [alloc([128, 64], F32) for _ in range(NB)]
        ckvf = [alloc([128, 256], F32) for _ in range(NB)]; kpef = [alloc([128, 64], F32) for _ in range(NB)]
        tmp64 = [alloc([128, 64], F32) for _ in range(NB)]
        sq = alloc([128, 256], F32, "hsq"); ss1 = [alloc([128, 1], F32) for _ in range(NB)]
        ckvb = [alloc([128, 256], BF16) for _ in range(NB)]; kpeb = [alloc([128, 128], BF16) for _ in range(NB)]
        ckvT = [alloc([128, 2, 128], BF16) for _ in range(NB)]
        KT = [alloc([128, 4, 1024], BF16, "KT%d" % i) for i in range(2)]
        KR = [alloc([128, 1024], BF16, "KR%d" % i) for i in range(2)]
        Vb = [alloc([128, 8, 512], BF16, "Vb%d" % i) for i in range(2)]
        Pb = [alloc([128, 1024], BF16, "Pb%d" % i) for i in range(2)]
        PT = [alloc([128, 8, 128], BF16, "PT%d" % i) for i in range(2)]
        NS4 = 4
        mx = [alloc([128, 1], F32) for _ in range(NS4)]; mnew = [alloc([128, 1], F32) for _ in range(NS4)]
        negm = [alloc([128, 1], F32) for _ in range(NS4)]; alp = [alloc([128, 1], F32) for _ in range(NS4)]
        rsum = [alloc([128, 1], F32) for _ in range(NS4)]
        dm_ = [alloc([128, 1], F32) for _ in range(NS4)]
        hm1 = [alloc([128, 16, 32], F32, "hm1_%d" % i) for i in range(NB)]; hm2 = [alloc([128, 16, 32], F32, "hm2_%d" % i) for i in range(NB)]
        sre = [alloc([128, 32], F32) for _ in range(2)]
        xs8 = [alloc([128, 16], F32) for _ in range(4)]

        mset(Xpp, 0.0); mset(Xown, 0.0)
        mset(m_run, NEG); mset(l_run, 0.0); mset(Oacc, 0.0)
        for kp in range(2):
            mset(kpeb[kp], 0.0)

        def hist_stages(t, kbuf, j, kb):
            b = t % NB
            bA = banks[0]; bB = banks[1]
            xc = Xpp[:, t % 2, :]; xn = Xpp[:, (t + 1) % 2, :]

            def sA():
                ld(xt[b], xp[t * 128:(t + 1) * 128, :])
                ld(rc[b], ropec_p[t * 128:(t + 1) * 128, :]); ld(rs[b], ropes_p[t * 128:(t + 1) * 128, :])

            def sB():
                for k in range(4):
                    tr(bA[:, k * 128:(k + 1) * 128], xt[b][:, k * 128:(k + 1) * 128], ident)
                for k in range(4):
                    tr(bB[:, k * 128:(k + 1) * 128], xt[b][:, (4 + k) * 128:(5 + k) * 128], ident)

            def sC():
                acopy(xT[b][:, 0:4, :], bA.re("p (k n) -> p k n", k=4))
                vcopy(xT[b][:, 4:8, :], bB.re("p (k n) -> p k n", k=4))

            def sD():
                for k in range(8):
                    mm(bA, xT[b][:, k, :], W_in_u[:, k, :], start=(k == 0), stop=(k == 7))
                for k in range(8):
                    mm(bB[:, 0:320], xT[b][:, k, :], W_in_kv[:, k, :], start=(k == 0), stop=(k == 7))

            def sE():
                acopy(utok[b], bA)
                acopy(kvf[b], bB[:, 0:320])

            def sF():
                act(sq, kvf[b][:, 0:256], AF.Square, accum=ss1[b])
                act(ss1[b], ss1[b], AF.Ln, scale=1.0 / 256, bias=epsc)
                act(ss1[b], ss1[b], AF.Exp, scale=-0.5)
                for gp in range(16):
                    mm(bA[:, 32 * gp:32 * gp + 32], LrT[:, gp, :], utok[b][:, 32 * gp:32 * gp + 32])
                for gp in range(16):
                    mm(bB[:, 32 * gp:32 * gp + 32], LiT[:, gp, :], utok[b][:, 32 * gp:32 * gp + 32])

            def sG():
                acopy(prs[b].re("p a b -> p (a b)"), bA)
                acopy(pis[b].re("p a b -> p (a b)"), bB)
                tt(ckvf[b], kvf[b][:, 0:256], gkv, ALU.mult, eng="pool")
                ts(ckvf[b], ckvf[b], ss1[b], 1.0, ALU.mult, ALU.mult, eng="pool")
                rope(kpef[b].re("p (a b) -> p a b", a=1), kvf[b][:, 256:320].re("p (a b) -> p a b", a=1), rc[b], rs[b],
                     tmp64[b].re("p (a b) -> p a b", a=1), 1, eng="pool")
                vcopy(ckvb[b], ckvf[b], eng="pool")
                vcopy(kpeb[b][:, 0:64], kpef[b], eng="pool")

            def sH():
                bb = bA.cast(BF16)
                for kc in range(2):
                    tr(bb[:, kc * 128:(kc + 1) * 128], ckvb[b][:, kc * 128:(kc + 1) * 128], identb)
                tr(bb[:, 256:384], kpeb[b], identb)
                tt(hm1[b], prs[b], BRb, ALU.mult, eng="pool"); tt(hm2[b], pis[b], BIb, ALU.mult, eng="pool")
                tt(hm1[b], hm1[b], hm2[b], ALU.subtract, eng="pool")
                tt(hm2[b], pis[b], BRb, ALU.mult, eng="pool"); tt(prs[b], prs[b], BIb, ALU.mult, eng="pool")
                tt(hm2[b], hm2[b], prs[b], ALU.add, eng="pool")

            def sI():
                bb = bA.cast(BF16)
                acopy(ckvT[b].re("p a b -> p (a b)"), bb[:, 0:256])
                acopy(KR[kbuf][0:64, j * 128:(j + 1) * 128], bb[0:64, 256:384])

            def sJ():
                for h in range(4):
                    for kc in range(2):
                        mm(bB[:, h * 128:(h + 1) * 128], W_kv[:, kc, h, 0:128], ckvT[b][:, kc, :], start=(kc == 0), stop=(kc == 1))
                for kc in range(2):
                    mm(bA, ckvT[b][:, kc, :], W_kv[:, kc, :, 128:256], start=(kc == 0), stop=(kc == 1))
                lr_ = lam128[:, 0, :]; li_ = lam128[:, 1, :]
                ld(ckv_o[t * 128:(t + 1) * 128, :], ckvf[b])
                ld(kpe_o[t * 128:(t + 1) * 128, :], kpef[b])
                red(sre[t % 2][:, 0:16], hm1[b], ALU.add)
                red(sre[t % 2][:, 16:32], hm2[b], ALU.add)
                stt(Xown[:, kb, :], xc, ohot[:, j:j + 1], Xown[:, kb, :], ALU.mult, ALU.add)
                tt(xs8[0], xc[:, 0:16], lr_, ALU.mult, eng="pool"); tt(xs8[1], xc[:, 16:32], li_, ALU.mult, eng="pool")
                tt(xs8[2], xc[:, 16:32], lr_, ALU.mult, eng="pool"); tt(xs8[3], xc[:, 0:16], li_, ALU.mult, eng="pool")
                tt(xs8[0], xs8[0], xs8[1], ALU.subtract, eng="pool"); tt(xs8[2], xs8[2], xs8[3], ALU.add, eng="pool")
                tt(xn[:, 0:16], xs8[0], sre[t % 2][:, 0:16], ALU.add, eng="pool")
                tt(xn[:, 16:32], xs8[2], sre[t % 2][:, 16:32], ALU.add, eng="pool")

            def sK():
                acopy(KT[kbuf][:, :, j * 128:(j + 1) * 128], bB.re("p (h n) -> p h n", h=4))
                vcopy(Vb[kbuf][:, j, :], bA)

            def pair(p_, c_):
                def f():
                    p_(); c_()
                return f
            return [sA, pair(sB, sC), pair(sD, sE), pair(sF, sG), pair(sH, sI), pair(sJ, sK)]

        def hist_items(kb):
            items = []
            for jp in range(4):
                sa = hist_stages(kb * 8 + 2 * jp, kb % 2, 2 * jp, kb)
                sb_ = hist_stages(kb * 8 + 2 * jp + 1, kb % 2, 2 * jp + 1, kb)
                for a_, b_ in zip(sa, sb_):
                    items.append(a_); items.append(b_)
            return items

        SCp = [dbl[3], dbl[2]]
        ptb = [banks[2], banks[3]]

        def att_A(n, i, h, kbuf):
            sc = SCp[n % 2]
            for c2 in range(2):
                mm(sc[:, c2 * 512:(c2 + 1) * 512], QTn[:, i, h, :], KT[kbuf][:, h, c2 * 512:(c2 + 1) * 512], start=True, stop=False)
                mm(sc[:, c2 * 512:(c2 + 1) * 512], QTr[0:64, i, h, :], KR[kbuf][0:64, c2 * 512:(c2 + 1) * 512], start=False, stop=True)

        def att_B(n, i, h, kbuf, diag):
            u2 = n % 2; u4 = n % NS4
            sc = SCp[u2]
            col = i * 4 + h
            if diag:
                tt(sc, sc, maskd, ALU.add)
            red(mx[u4], sc, ALU.max)
            tt(mnew[u4], mx[u4], m_run[:, col:col + 1], ALU.max)
            tt(dm_[u4], m_run[:, col:col + 1], mnew[u4], ALU.subtract)
            vcopy(m_run[:, col:col + 1], mnew[u4])
            ts(negm[u4], mnew[u4], -1.0, None, ALU.mult)
            act(alp[u4], dm_[u4], AF.Exp)
            act(Pb[u2], sc, AF.Exp, bias=negm[u4], accum=rsum[u4])

        def att_CD(n, i, h, kbuf):
            u2 = n % 2
            pt_b = ptb[u2].cast(BF16)
            for kt in range(8):
                tr(pt_b[:, kt * 128:(kt + 1) * 128], Pb[u2][:, kt * 128:(kt + 1) * 128], identb)
            if n % 3 == 0:
                vcopy(PT[u2].re("p a b -> p (a b)"), pt_b)
            else:
                acopy(PT[u2].re("p a b -> p (a b)"), pt_b)

        def att_EF(n, i, h, kbuf):
            u2 = n % 2; u4 = n % NS4
            ov = ptb[u2][:, 0:128]
            for kt in range(8):
                mm(ov, PT[u2][:, kt, :], Vb[kbuf][:, kt, h * 128:(h + 1) * 128], start=(kt == 0), stop=(kt == 7))
            stt(Oacc[:, i, h, :], Oacc[:, i, h, :], alp[u4], ov, ALU.mult, ALU.add)
            col = i * 4 + h
            stt(l_run[:, col:col + 1], l_run[:, col:col + 1], alp[u4], rsum[u4], ALU.mult, ALU.add)

        for it in hist_items(0):
            it()
        nun = 0
        for kb in range(16):
            kbuf = kb % 2
            units = [(i, h) for i in range(kb, 16) for h in range(4)]
            U = len(units)
            items = hist_items(kb + 1) if kb + 1 < 16 else []
            per = (len(items) + U - 1) // U if items else 0
            ip = 0
            for q_ in range(U + 3):
                if q_ < U:
                    att_A(nun + q_, units[q_][0], units[q_][1], kbuf)
                if 0 <= q_ - 1 < U:
                    att_B(nun + q_ - 1, units[q_ - 1][0], units[q_ - 1][1], kbuf, units[q_ - 1][0] == kb)
                if 0 <= q_ - 2 < U:
                    att_CD(nun + q_ - 2, units[q_ - 2][0], units[q_ - 2][1], kbuf)
                if 0 <= q_ - 3 < U:
                    att_EF(nun + q_ - 3, units[q_ - 3][0], units[q_ - 3][1], kbuf)
                for _ in range(per):
                    if ip < len(items):
                        items[ip](); ip += 1
            while ip < len(items):
                items[ip](); ip += 1
            nun += U

        ld(ssmp_o, Xpp[:, NTP % 2, :])

        barrier()
        bump[0] = mark_after_q

        W_kv = alloc([128, 2, 4, 256], BF16, "W_kv2")
        ldw(W_kv.re("p k h e -> p k (h e)"), w_kv_d.re("(kt p) n -> p kt n", p=128))
        gkv = alloc([128, 256], F32, "gkv2"); ld(gkv, gkv_d)
        xt_s = alloc([128, D], F32, "sxt"); xT_s = alloc([128, 8, 128], BF16, "sxT")
        rc_s = alloc([128, 64], F32); rs_s = alloc([128, 64], F32)
        ckvf_s = alloc([128, 256], F32); kpef_s = alloc([128, 64], F32); tmp64_s = alloc([128, 64], F32)
        sq = alloc([128, 512], F32, "ssq"); ss_s = alloc([128, 1], F32)
        ckvb_s = alloc([128, 256], BF16); kpeb_s = alloc([128, 128], BF16)
        ckvTn = alloc([128, 2, 128], BF16, "ckvTn"); kpeTn = alloc([128, 128], BF16, "kpeTn")
        KTn = alloc([128, 4, 128], BF16, "KTn")
        cat = [alloc([128, 256], F32, "cat%d" % i) for i in range(2)]
        catb = [alloc([128, 256], BF16) for _ in range(2)]
        cpt = [alloc([128, 64], F32, "cpt%d" % i) for i in range(2)]
        cptb = [alloc([128, 128], BF16) for _ in range(2)]
        ckvTc = alloc([128, 2, 1024], BF16, "ckvTc"); kpeTc = alloc([128, 1024], BF16, "kpeTc")
        KTc = alloc([128, 4, 1024], BF16, "KTc"); Vc = alloc([128, 8, 512], BF16, "Vc"); Vn = alloc([128, 512], BF16, "Vn")
        scs = alloc([128, 1056], F32, "scs")
        Ps = alloc([128, 1152], BF16, "Ps"); PTs = alloc([128, 9, 32], BF16, "PTs")
        mxs = alloc([128, 1], F32); negs = alloc([128, 1], F32); sums = alloc([128, 1], F32)
        osb = alloc([128, 512], F32, "osb")
        isam = 16
        load_xT(xo[isam * 128:(isam + 1) * 128, :], xt_s, xT_s, banks[0])
        ld(rc_s, ropec_o[isam * 128:(isam + 1) * 128, :]); ld(rs_s, ropes_o[isam * 128:(isam + 1) * 128, :])
        for k in range(8):
            mm(banks[2][:, 0:320], xT_s[:, k, :], W_in_kv[:, k, :], start=(k == 0), stop=(k == 7))
        rmsnorm(ckvf_s, banks[2][:, 0:256], 256, gkv, sq[:, 0:256], ss_s)
        ld(ckvs_o, ckvf_s)
        rope(kpef_s.re("p (a b) -> p a b", a=1), banks[2][:, 256:320].re("p (a b) -> p a b", a=1), rc_s, rs_s, tmp64_s.re("p (a b) -> p a b", a=1), 1)
        ld(kpes_o, kpef_s)
        vcopy(ckvb_s, ckvf_s)
        mset(kpeb_s, 0.0)
        vcopy(kpeb_s[:, 0:64], kpef_s)
        bb = banks[3].cast(BF16)
        for kc in range(2):
            tr(bb[:, kc * 128:(kc + 1) * 128], ckvb_s[:, kc * 128:(kc + 1) * 128], identb)
        tr(bb[:, 256:384], kpeb_s, identb)
        acopy(ckvTn.re("p a b -> p (a b)"), bb[:, 0:256])
        acopy(kpeTn[0:64, :], bb[0:64, 256:384])
        for h in range(4):
            for kc in range(2):
                mm(banks[3][:, h * 128:(h + 1) * 128], W_kv[:, kc, h, 0:128], ckvTn[:, kc, :], start=(kc == 0), stop=(kc == 1))
        acopy(KTn.re("p a b -> p (a b)"), banks[3])
        for kp in range(2):
            mset(cptb[kp], 0.0)
        for s_ in range(4):
            for kt in range(8):
                b = kt % 2
                r0 = s_ * 1024 + kt * 128
                ld(cat[b], cckv_d[r0:r0 + 128, :]); ld(cpt[b], ckpe_d[r0:r0 + 128, :])
                vcopy(catb[b], cat[b], eng="pool"); vcopy(cptb[b][:, 0:64], cpt[b], eng="pool")
                bb = banks[1].cast(BF16)
                for kc in range(2):
                    tr(bb[:, kc * 128:(kc + 1) * 128], catb[b][:, kc * 128:(kc + 1) * 128], identb)
                tr(bb[:, 256:384], cptb[b], identb)
                acopy(ckvTc[:, :, kt * 128:(kt + 1) * 128], bb[:, 0:256].re("p (a b) -> p a b", a=2))
                acopy(kpeTc[0:64, kt * 128:(kt + 1) * 128], bb[0:64, 256:384])
            for h in range(4):
                for n in range(2):
                    for kc in range(2):
                        mm(banks[2], W_kv[:, kc, h, 0:128], ckvTc[:, kc, n * 512:(n + 1) * 512], start=(kc == 0), stop=(kc == 1))
                    acopy(KTc[:, h, n * 512:(n + 1) * 512], banks[2])
            for kt in range(8):
                for kc in range(2):
                    mm(banks[3], ckvTc[:, kc, kt * 128:(kt + 1) * 128], W_kv[:, kc, :, 128:256], start=(kc == 0), stop=(kc == 1))
                vcopy(Vc[:, kt, :], banks[3])
            for kc in range(2):
                mm(banks[3][0:32, :], ckvTn[:, kc, s_ * 32:(s_ + 1) * 32], W_kv[:, kc, :, 128:256], start=(kc == 0), stop=(kc == 1))
            vcopy(Vn[0:32, :], banks[3][0:32, :])
            qs = slice(s_ * 32, (s_ + 1) * 32)
            for h in range(4):
                for n in range(2):
                    mm(SC[0:32, n * 512:(n + 1) * 512], QTn[:, isam, h, qs], KTc[:, h, n * 512:(n + 1) * 512], start=True, stop=False)
                    mm(SC[0:32, n * 512:(n + 1) * 512], QTr[0:64, isam, h, qs], kpeTc[0:64, n * 512:(n + 1) * 512], start=False, stop=True)
                mm(banks[4][0:32, 0:32], QTn[:, isam, h, qs], KTn[:, h, qs], start=True, stop=False)
                mm(banks[4][0:32, 0:32], QTr[0:64, isam, h, qs], kpeTn[0:64, qs], start=False, stop=True)
                acopy(scs[0:32, 0:1024], SC[0:32, :])
                acopy(scs[0:32, 1024:1056], banks[4][0:32, 0:32])
                red(mxs[0:32, :], scs[0:32, :], ALU.max)
                ts(negs[0:32, :], mxs[0:32, :], -1.0, None, ALU.mult)
                mset(Ps[0:32, 1024:1152], 0.0)
                act(Ps[0:32, 0:1056], scs[0:32, :], AF.Exp, bias=negs[0:32, :], accum=sums[0:32, :])
                pt_b = banks[5].cast(BF16)
                for kt in range(9):
                    tr(pt_b[:, kt * 32:(kt + 1) * 32], Ps[0:32, kt * 128:(kt + 1) * 128], identb[0:32, 0:32])
                vcopy(PTs.re("p a b -> p (a b)"), pt_b[:, 0:288])
                ov = banks[4][0:32, 128:256]
                for kt in range(8):
                    mm(ov, PTs[:, kt, :], Vc[:, kt, h * 128:(h + 1) * 128], start=(kt == 0), stop=False)
                mm(ov, PTs[0:32, 8, :], Vn[0:32, h * 128:(h + 1) * 128], start=False, stop=True)
                recip(sums[0:32, :], sums[0:32, :])
                ts(osb[0:32, h * 128:(h + 1) * 128], ov, sums[0:32, :], None, ALU.mult)
            ld(Osam[s_ * 32:(s_ + 1) * 32, :], osb[0:32, :])

        barrier()
        bump[0] = mark_after_mixer_state

        W_glu = alloc([128, 4, D], BF16, "W_glu"); W_o = alloc([128, 8, D], BF16, "W_o")
        ldw(W_glu, w_glu_d.re("(kt p) n -> p kt n", p=128)); ldw(W_o, w_o_d.re("(kt p) n -> p kt n", p=128))
        gos = alloc([128, 512], F32, "gos"); gom = alloc([128, 512], F32, "gom"); ld(gos, gos_d); ld(gom, gom_d)
        lng = alloc([128, D], F32, "lng"); lnb = alloc([128, D], F32, "lnb")
        ld(lng, lng_d[:, 0:D]); ld(lnb, lnb_d[:, 0:D])
        btab2 = Buf("tab2")
        for nm_ in ("S_t", "C_t", "R_t"):
            v_ = alloc([128, 16, 128], F32, nm_ + "2"); v_.b = btab2
            ld(v_.re("p a b -> p (a b)"), tab_d[nm_]); ssmx[nm_] = v_
        for nm_ in ("WBre", "WBim"):
            v_ = alloc([128, 16, 128], BF16, nm_ + "2")
            ld(v_.re("p a b -> p (a b)"), tab_d[nm_]); ssmx[nm_] = v_
        for nm_, src_ in (("Cre", cre_d), ("Cimn", cimn_d), ("Dblk", dblk_d)):
            v_ = alloc([128, 16, 32], BF16, nm_)
            ldw(v_.re("p a b -> p (a b)"), src_); ssmx[nm_] = v_
        ssmx["m4"] = [[alloc([128, 256], F32) for _ in range(4)] for _ in range(2)]
        ssmx["wri"] = [[alloc([128, 256], F32) for _ in range(2)] for _ in range(2)]
        ssmx["zri"] = [[alloc([128, 2, 128], F32) for _ in range(2)] for _ in range(2)]
        ssmx["xs4"] = [alloc([128, 16], F32) for _ in range(4)]
        ssmx["zl_all"] = [alloc([128, 4, 16], F32) for _ in range(2)]
        ssmx["dmd"] = [[alloc([128, 256], F32) for _ in range(4)]] * 2
        ssmx["xrb"] = [alloc([128, 2, 128], BF16) for _ in range(2)]
        ssmx["xib"] = [alloc([128, 2, 128], BF16) for _ in range(2)]
        S0 = alloc([128, 2, 4, 16], F32, "S0"); ld(S0.re("p a b c -> p (a b c)"), s0_d)
        Sfin = alloc([128, 2, 4, 16], F32, "Sfin")
        xt = [alloc([128, D], F32, "axt%d" % i) for i in range(NB)]
        xT = [alloc([128, 8, 128], BF16, "axT%d" % i) for i in range(NB)]
        uT = [alloc([128, 4, 128], BF16, "auT%d" % i) for i in range(NB)]
        ysq = alloc([128, 512], F32, "ysq"); yt = alloc([128, 512], F32, "yt"); ysg = alloc([128, 512], F32, "ysg")
        glb = alloc([128, 512], BF16, "glb"); gT = alloc([128, 4, 128], BF16, "gT")
        sg2 = ysq; osf = yt
        mixb = alloc([128, D], BF16, "mixb"); mixT = alloc([128, 8, 128], BF16, "mixT")
        rl = alloc([128, 4], F32, "rl"); omf = alloc([128, 4, 128], F32, "omf")
        ss_a = alloc([128, 1], F32); sq = ysg
        rres = alloc([128, D], F32, "rres"); hout = alloc([128, D], F32, "hout")
        stats = alloc([128, 12], F32); mvv = alloc([128, 2], F32)

        def pre_stages(i):
            b = i % NB
            yb = banks[3] if i % 2 == 0 else banks[0]

            def p0():
                load_xT(xo[i * 128:(i + 1) * 128, :], xt[b], xT[b], banks[1])
                for kt in range(4):
                    for k in range(8):
                        mm(banks[1][:, kt * 128:(kt + 1) * 128], W_in_u[:, k, kt * 128:(kt + 1) * 128], xT[b][:, k, :], start=(k == 0), stop=(k == 7))
                acopy(uT[b].re("p a b -> p (a b)"), banks[1])

            if i < 16:
                rr_ = ssm_rounds(uT[b], (banks[4], banks[5]), lambda gp, s_, part, i=i: Xown[:, i, part * 16 + gp:part * 16 + gp + 1], 1, ybank=yb)
            else:
                rr_ = ssm_rounds(uT[b], (banks[4], banks[5]), lambda gp, s_, part: S0[:, part, s_, gp:gp + 1], 4, ybank=yb,
                                 last_out=lambda s_: (Sfin[:, 0, s_, :], Sfin[:, 1, s_, :]))
                rr_.append(lambda: ld(ssms_o, Sfin.re("p a b c -> p (a b c)")))
            return [p0] + rr_

        def post_stages(i):
            b = i % NB
            yb = banks[3] if i % 2 == 0 else banks[0]
            xt_ = xt[b]

            def q0():
                act(ysq, yb, AF.Square)
                ts(yt, ysq, 0.044715, 1.0, ALU.mult, ALU.add)
                tt(yt, yt, yb, ALU.mult)

            def q1():
                act(ysg, yt, AF.Sigmoid, scale=1.5957691216057308)
                tt(glb, ysg, yb, ALU.mult)

            def q2():
                transpose_to(gT, glb, 4, banks[2], eng="act")

            def q3():
                for nb in range(2):
                    for k in range(4):
                        mm(SC[:, nb * 512:(nb + 1) * 512], gT[:, k, :], W_glu[:, k, nb * 512:(nb + 1) * 512], start=(k == 0), stop=(k == 3))

            def q4():
                act(sg2, SC[:, 512:1024], AF.Sigmoid)
                tt(osf, sg2, SC[:, 0:512], ALU.mult)

            def q5():
                rmsnorm(mixb[:, 0:512], osf, 512, gos, sq, ss_a)

            def q6():
                if i < 16:
                    recip(rl, l_run[:, i * 4:(i + 1) * 4])
                    tt(omf, Oacc[:, i, :, :], rl.un(2).bc([128, 4, 128]), ALU.mult)
                    rmsnorm(mixb[:, 512:1024], omf.re("p a b -> p (a b)"), 512, gom, sq, ss_a)
                else:
                    rmsnorm(mixb[:, 512:1024], Osam, 512, gom, sq, ss_a)

            def q7():
                for half in range(2):
                    transpose_to(mixT[:, half * 4:(half + 1) * 4, :], mixb[:, half * 512:(half + 1) * 512], 4, banks[2], eng=("act" if half == 0 else "dve"))

            def q8():
                for nb in range(2):
                    for k in range(8):
                        mm(SC[:, nb * 512:(nb + 1) * 512], mixT[:, k, :], W_o[:, k, nb * 512:(nb + 1) * 512], start=(k == 0), stop=(k == 7))

            def q9():
                stt(rres, xt_, ALPHA, SC, ALU.mult, ALU.add)
                layernorm(hout, rres, lng, lnb, stats, mvv)
                ld(h1_d[i * 128:(i + 1) * 128, :], hout)

            return [q0, q1, q2, q3, q4, q5, q6, q7, q8, q9]

        prev_post = []
        for i in range(NOWN + 1):
            pre = pre_stages(i) if i < NOWN else []
            n_ = max(len(pre), len(prev_post))
            for k_ in range(n_):
                if k_ < len(pre):
                    pre[k_]()
                if k_ < len(prev_post):
                    prev_post[k_]()
            prev_post = post_stages(i) if i < NOWN else []

        barrier()
        bump[0] = mark_after_mem

        W1 = alloc([128, 8, 4096], BF16, "W1")
        mark_after_w1 = bump[0]
        W_xq = alloc([128, 8, D], BF16, "W_xq"); W_xo = alloc([128, 8, D], BF16, "W_xo")
        ldw(W_xq, w_xq_d.re("(kt p) n -> p kt n", p=128)); ldw(W_xo, w_xo_d.re("(kt p) n -> p kt n", p=128))
        for c in range(2):
            ldw(W1[:, :, c * 2048:(c + 1) * 2048], w_ff1_d.re("(kt p) n -> p kt n", p=128)[:, :, c * 2048:(c + 1) * 2048])
        lng = alloc([128, D], F32, "lng2"); lnb = alloc([128, D], F32, "lnb2")
        ld(lng, lng_d[:, D:2 * D]); ld(lnb, lnb_d[:, D:2 * D])
        hin = [alloc([128, D], F32, "bh%d" % i) for i in range(NB)]
        hb2 = [alloc([128, D], BF16, "bhb%d" % i) for i in range(2)]; hT2 = [alloc([128, 8, 128], BF16, "bhT%d" % i) for i in range(2)]
        qxb2 = [alloc([128, D], BF16, "qxb%d" % i) for i in range(2)]; qxT2 = [alloc([128, 8, 128], BF16, "qxT%d" % i) for i in range(2)]
        mx42 = [alloc([128, 4], F32) for _ in range(2)]; neg42 = [alloc([128, 4], F32) for _ in range(2)]; sum42 = [alloc([128, 4], F32) for _ in range(2)]
        Px2 = [alloc([128, 4, 256], BF16, "Px%d" % i) for i in range(2)]; PxT2 = [alloc([128, 8, 128], BF16, "PxT%d" % i) for i in range(2)]
        oxb2 = [alloc([128, D], BF16, "oxb%d" % i) for i in range(2)]; oxT2 = [alloc([128, 8, 128], BF16, "oxT%d" % i) for i in range(2)]
        rres2 = [alloc([128, D], F32, "brres%d" % i) for i in range(2)]; hout2 = [alloc([128, D], F32, "bhout%d" % i) for i in range(2)]
        stats2 = [alloc([128, 12], F32) for _ in range(2)]; mvv2 = [alloc([128, 2], F32) for _ in range(2)]
        hb_ = hb2[0]; hT = hT2[0]; qxb = qxb2[0]; qxT = qxT2[0]; mx4 = mx42[0]; neg4 = neg42[0]; sum4 = sum42[0]
        Px = Px2[0]; PxT = PxT2[0]; oxb = oxb2[0]; oxT = oxT2[0]; rres = rres2[0]; hout = hout2[0]; stats = stats2[0]; mvv = mvv2[0]
        cmk = [alloc([128, D], F32, "cmk%d" % i) for i in range(2)]
        cmkb = [alloc([128, D], BF16, "cmkb%d" % i) for i in range(1)]
        mkTs = alloc([128, 4, 2, 256], BF16, "mkTs"); mvs = alloc([128, 2, D], BF16, "mvs")
        scx = alloc([128, 8], F32, "scx"); oxs = alloc([128, D], BF16, "oxs")

        def xa_stages(i, p):
            A_ = banks[p]; C_ = dbl[1 + p]
            Ab = A_.cast(BF16)

            def tr8(src):
                for k in range(8):
                    tr(Ab[:, k * 128:(k + 1) * 128], src[:, k * 128:(k + 1) * 128], identb)

            def ev8(dst):
                acopy(dst[:, 0:4, :], Ab[:, 0:512].re("p (k n) -> p k n", k=4))
                vcopy(dst[:, 4:8, :], Ab[:, 512:1024].re("p (k n) -> p k n", k=4))

            def t0():
                ld(hin[p], h1_d[i * 128:(i + 1) * 128, :])
                vcopy(hb2[p], hin[p], eng="pool")

            def t3():
                for nb in range(2):
                    for k in range(8):
                        mm(C_[:, nb * 512:(nb + 1) * 512], hT2[p][:, k, :], W_xq[:, k, nb * 512:(nb + 1) * 512], start=(k == 0), stop=(k == 7))

            def t4():
                for nb in range(2):
                    P.op("act", lambda e, nb=nb: e.mul(out=qxb2[p].ap[:, nb * 512:(nb + 1) * 512], in_=C_.ap[:, nb * 512:(nb + 1) * 512], mul=X_SCALE),
                         reads=[C_.b], writes=[qxb2[p].b])

            def t7():
                for h in range(4):
                    for et in range(2):
                        mm(C_[:, h * 256:(h + 1) * 256], qxT2[p][:, h * 2 + et, :], mkT[:, h, et, :], start=(et == 0), stop=(et == 1))

            def t8():
                red(mx42[p], C_.re("p (h m) -> p h m", h=4), ALU.max)
                ts(neg42[p], mx42[p], -1.0, None, ALU.mult)
                for h in range(4):
                    act(Px2[p][:, h, :], C_[:, h * 256:(h + 1) * 256], AF.Exp, bias=neg42[p][:, h:h + 1], accum=sum42[p][:, h:h + 1])

            def t11():
                for h in range(4):
                    for mt in range(2):
                        mm(C_[:, h * 256:(h + 1) * 256], PxT2[p][:, h * 2 + mt, :], mvb[:, mt, h * 256:(h + 1) * 256], start=(mt == 0), stop=(mt == 1))

            def t12():
                recip(sum42[p], sum42[p])
                tt(oxb2[p].re("p (h e) -> p h e", h=4), C_.re("p (h e) -> p h e", h=4), sum42[p].un(2).bc([128, 4, 256]), ALU.mult)

            def t15():
                for nb in range(2):
                    for k in range(8):
                        mm(C_[:, nb * 512:(nb + 1) * 512], oxT2[p][:, k, :], W_xo[:, k, nb * 512:(nb + 1) * 512], start=(k == 0), stop=(k == 7))

            def t16():
                stt(rres2[p], hin[p], ALPHA, C_, ALU.mult, ALU.add)
                layernorm(hout2[p], rres2[p], lng, lnb, stats2[p], mvv2[p])
                ld(h2_d[i * 128:(i + 1) * 128, :], hout2[p])

            return [t0, lambda: tr8(hb2[p]), lambda: ev8(hT2[p]), t3, t4, lambda: tr8(qxb2[p]), lambda: ev8(qxT2[p]), t7, t8,
                    lambda: tr8(Px2[p].re("p a b -> p (a b)")), lambda: ev8(PxT2[p]), t11, t12, lambda: tr8(oxb2[p]), lambda: ev8(oxT2[p]), t15, t16]

        for ip in range(8):
            sa = xa_stages(2 * ip, 0); sb_ = xa_stages(2 * ip + 1, 1)
            for a_, b_ in zip(sa, sb_):
                a_(); b_()

        for i in range(16, NOWN):
            b = i % NB
            ld(hin[b], h1_d[i * 128:(i + 1) * 128, :])
            vcopy(hb_, hin[b], eng="pool")
            for half in range(2):
                transpose_to(hT[:, half * 4:(half + 1) * 4, :], hb_[:, half * 512:(half + 1) * 512], 4, banks[0], eng=("act" if half == 0 else "dve"))
            for nb in range(2):
                for k in range(8):
                    mm(banks[1 + nb], hT[:, k, :], W_xq[:, k, nb * 512:(nb + 1) * 512], start=(k == 0), stop=(k == 7))
                P.op("act", lambda e, nb=nb: e.mul(out=qxb.ap[:, nb * 512:(nb + 1) * 512], in_=banks[1 + nb].ap, mul=X_SCALE), reads=[banks[1 + nb].b], writes=[qxb.b])
            for half in range(2):
                transpose_to(qxT[:, half * 4:(half + 1) * 4, :], qxb[:, half * 512:(half + 1) * 512], 4, banks[3], eng=("act" if half == 0 else "dve"))
            if i < 16:
                for h in range(4):
                    for et in range(2):
                        mm(SC[:, h * 256:(h + 1) * 256], qxT[:, h * 2 + et, :], mkT[:, h, et, :], start=(et == 0), stop=(et == 1))
                red(mx4, SC.re("p (h m) -> p h m", h=4), ALU.max)
                ts(neg4, mx4, -1.0, None, ALU.mult)
                for h in range(4):
                    act(Px[:, h, :], SC[:, h * 256:(h + 1) * 256], AF.Exp, bias=neg4[:, h:h + 1], accum=sum4[:, h:h + 1])
                for half in range(2):
                    transpose_to(PxT[:, half * 4:(half + 1) * 4, :], Px.re("p a b -> p (a b)")[:, half * 512:(half + 1) * 512], 4, banks[4], eng=("act" if half == 0 else "dve"))
                for h in range(4):
                    for mt in range(2):
                        mm(SC[:, h * 256:(h + 1) * 256], PxT[:, h * 2 + mt, :], mvb[:, mt, h * 256:(h + 1) * 256], start=(mt == 0), stop=(mt == 1))
                recip(sum4, sum4)
                tt(oxb.re("p (h e) -> p h e", h=4), SC.re("p (h e) -> p h e", h=4), sum4.un(2).bc([128, 4, 256]), ALU.mult)
            else:
                for s_ in range(4):
                    qs = slice(s_ * 32, (s_ + 1) * 32)
                    for mt in range(2):
                        r0 = s_ * 256 + mt * 128
                        ld(cmk[0], cmk_d[r0:r0 + 128, :]); ld(cmk[1], cmv_d[r0:r0 + 128, :])
                        vcopy(cmkb[0], cmk[0], eng="pool")
                        vcopy(mvs[:, mt, :], cmk[1], eng="pool")
                        for half in range(2):
                            bb = banks[0].cast(BF16)
                            for k in range(4):
                                kk = half * 4 + k
                                tr(bb[:, k * 128:(k + 1) * 128], cmkb[0][:, kk * 128:(kk + 1) * 128], identb)
                            acopy(mkTs[:, half * 2:half * 2 + 2, :, mt * 128:(mt + 1) * 128].re("p h e m -> p (h e) m"), bb[:, 0:512].re("p (k m) -> p k m", k=4))
                    for h in range(4):
                        for et in range(2):
                            mm(SC[0:32, h * 256:(h + 1) * 256], qxT[:, h * 2 + et, qs], mkTs[:, h, et, :], start=(et == 0), stop=(et == 1))
                    red(mx4[0:32, :], SC[0:32, :].re("p (h m) -> p h m", h=4), ALU.max)
                    ts(neg4[0:32, :], mx4[0:32, :], -1.0, None, ALU.mult)
                    for h in range(4):
                        act(Px[0:32, h, :], SC[0:32, h * 256:(h + 1) * 256], AF.Exp, bias=neg4[0:32, h:h + 1], accum=sum4[0:32, h:h + 1])
                    bb = banks[4].cast(BF16)
                    for k in range(8):
                        tr(bb[:, k * 32:(k + 1) * 32], Px.re("p a b -> p (a b)")[0:32, k * 128:(k + 1) * 128], identb[0:32, 0:32])
                    vcopy(PxT[:, :, 0:32], bb[:, 0:256].re("p (k q) -> p k q", k=8))
                    for h in range(4):
                        for mt in range(2):
                            mm(SC[0:32, h * 256:(h + 1) * 256], PxT[:, h * 2 + mt, 0:32], mvs[:, mt, h * 256:(h + 1) * 256], start=(mt == 0), stop=(mt == 1))
                    recip(sum4[0:32, :], sum4[0:32, :])
                    tt(oxs[0:32, :].re("p (h e) -> p h e", h=4), SC[0:32, :].re("p (h e) -> p h e", h=4), sum4[0:32, :].un(2).bc([32, 4, 256]), ALU.mult)
                    ld(oxb[s_ * 32:(s_ + 1) * 32, :], oxs[0:32, :])
            for half in range(2):
                transpose_to(oxT[:, half * 4:(half + 1) * 4, :], oxb[:, half * 512:(half + 1) * 512], 4, banks[5], eng=("act" if half == 0 else "dve"))
            for nb in range(2):
                for k in range(8):
                    mm(banks[1 + nb], oxT[:, k, :], W_xo[:, k, nb * 512:(nb + 1) * 512], start=(k == 0), stop=(k == 7))
            for nb in range(2):
                stt(rres[:, nb * 512:(nb + 1) * 512], hin[b][:, nb * 512:(nb + 1) * 512], ALPHA, banks[1 + nb], ALU.mult, ALU.add)
            layernorm(hout, rres, lng, lnb, stats, mvv)
            ld(h2_d[i * 128:(i + 1) * 128, :], hout)

        barrier()
        bump[0] = mark_after_w1

        W2 = alloc([128, 32, D], BF16, "W2")
        for c in range(4):
            ldw(W2[:, c * 8:(c + 1) * 8, :], w_ff2_d.re("(kt p) n -> p kt n", p=128)[:, c * 8:(c + 1) * 8, :])
        lng = alloc([128, D], F32, "lng3"); lnb = alloc([128, D], F32, "lnb3")
        ld(lng, lng_d[:, 2 * D:3 * D]); ld(lnb, lnb_d[:, 2 * D:3 * D])
        hin = alloc([128, D], F32, "ch")
        hb_ = alloc([128, D], BF16, "chb"); hT = alloc([128, 8, 512], BF16, "chT")
        zr = [alloc([128, 512], F32, "zr%d" % i) for i in range(2)]; zT = alloc([128, 32, 512], BF16, "zT")
        rres = alloc([128, D], F32, "crres"); hout = alloc([128, D], F32, "chout")
        stats = alloc([128, 12], F32); mvv = alloc([128, 2], F32)
        groups = [list(range(g_ * 4, g_ * 4 + 4)) for g_ in range(4)] + [[16]]
        for grp in groups:
            nt = len(grp)
            for q_, i in enumerate(grp):
                ld(hin, h2_d[i * 128:(i + 1) * 128, :])
                vcopy(hb_, hin, eng="pool")
                for half in range(2):
                    transpose_to(hT[:, half * 4:(half + 1) * 4, q_ * 128:(q_ + 1) * 128], hb_[:, half * 512:(half + 1) * 512], 4, banks[0], eng=("act" if half == 0 else "dve"))
            W_ = nt * 128
            for f in range(32):
                bk = banks[1 + f % 2]
                for k in range(8):
                    mm(bk[:, 0:W_], W1[:, k, f * 128:(f + 1) * 128], hT[:, k, 0:W_], start=(k == 0), stop=(k == 7))
                act(zr[f % 2][:, 0:W_], bk[:, 0:W_], AF.Relu)
                tt(zT[:, f, 0:W_], zr[f % 2][:, 0:W_], zr[f % 2][:, 0:W_], ALU.mult)
            for q_, i in enumerate(grp):
                for nb in range(2):
                    for f in range(32):
                        mm(SC[:, nb * 512:(nb + 1) * 512], zT[:, f, q_ * 128:(q_ + 1) * 128], W2[:, f, nb * 512:(nb + 1) * 512], start=(f == 0), stop=(f == 31))
                ld(hin, h2_d[i * 128:(i + 1) * 128, :])
                stt(rres, hin, ALPHA, SC, ALU.mult, ALU.add)
                layernorm(hout, rres, lng, lnb, stats, mvv)
                ld(y_o[i * 128:(i + 1) * 128, :], hout)

        P.emit(st)
    return nc


def _lay_gp(a):
    sh = a.shape[2:]
    n = len(sh)
    return np.ascontiguousarray(a.reshape(16, 2, 64, *sh).transpose(1, 2, 0, *range(3, 3 + n)).reshape(128, 16, *sh))


def _bc(v, n=128):
    return np.ascontiguousarray(np.broadcast_to(np.asarray(v, np.float32).reshape(1, -1), (n, np.asarray(v).size)))


def _rope_tables(pos):
    inv = (10000.0 ** (-np.arange(32, dtype=np.float32) / 32)).astype(np.float32)
    ang = pos.astype(np.float32)[:, None] * inv[None, :]
    c = np.cos(ang).astype(np.float32)
    s = np.sin(ang).astype(np.float32)
    return np.concatenate([c, c], 1), np.concatenate([-s, s], 1)


_NC_CACHE = {}


def kernel(x_prompt, x_sample, mem_prompt, cache_mla_ckv, cache_mla_kpe, state_ssm_re, state_ssm_im,
           cache_mem_k, cache_mem_v, w_in, g_q, w_q_up, g_kv, w_kv_up, a_re, a_im, b_re, b_im, c_re, c_im,
           d_skip, log_dt, w_glu, g_out_ssm, g_out_mla, w_o, w_xq, w_xk, w_xv, w_xo, w_ff1, w_ff2, ln_g, ln_b):
    f = lambda a: np.ascontiguousarray(np.asarray(a, dtype=np.float32))
    x_prompt = f(x_prompt); x_sample = f(x_sample)
    xp = x_prompt[0]
    cre = np.zeros((128, 16, 32), np.float32); cimn = np.zeros((128, 16, 32), np.float32)
    cr = f(c_re)[0].reshape(16, 2, 16, 64); ci = f(c_im)[0].reshape(16, 2, 16, 64)
    for g2 in range(2):
        cre[g2 * 64:(g2 + 1) * 64, :, g2 * 16:(g2 + 1) * 16] = cr[:, g2].transpose(2, 0, 1)
        cimn[g2 * 64:(g2 + 1) * 64, :, g2 * 16:(g2 + 1) * 16] = -ci[:, g2].transpose(2, 0, 1)
    dblk = np.zeros((128, 16, 32), np.float32)
    dd = f(d_skip)[0].reshape(512)
    for gp in range(16):
        for c in range(32):
            ch = gp * 32 + c
            dblk[ch % 128, gp, c] = dd[ch]
    pos_p = np.arange(16384)
    rcp, rsp = _rope_tables(pos_p)
    common = {
        "xp": xp, "mem": f(mem_prompt)[0], "ident": np.eye(128, dtype=np.float32),
        "w_in": f(w_in)[0], "w_q": f(w_q_up)[0].reshape(384, 768), "w_kv": f(w_kv_up)[0].reshape(256, 1024),
        "w_glu": f(w_glu)[0], "w_o": f(w_o)[0], "w_xq": f(w_xq)[0].reshape(D, D), "w_xk": f(w_xk)[0].reshape(D, D),
        "w_xv": f(w_xv)[0].reshape(D, D), "w_xo": f(w_xo)[0].reshape(D, D), "w_ff1": f(w_ff1)[0], "w_ff2": f(w_ff2)[0],
        "gq": _bc(f(g_q)[0]), "gkv": _bc(f(g_kv)[0]), "gos": _bc(f(g_out_ssm)[0]), "gom": _bc(f(g_out_mla)[0]),
        "lng": _bc(f(ln_g)[0].reshape(-1)), "lnb": _bc(f(ln_b)[0].reshape(-1)),
        "ropec_p": rcp, "ropes_p": rsp,
        "ar": _lay_gp(f(a_re)[0]), "ai": _lay_gp(f(a_im)[0]),
        "ldt": _lay_gp(np.ascontiguousarray(np.broadcast_to(f(log_dt)[0][:, None], (32, 64)))),
        "bre": _lay_gp(f(b_re)[0]).reshape(128, 256), "bim": _lay_gp(f(b_im)[0]).reshape(128, 256),
        "jj": _bc(np.arange(1, 129, dtype=np.float32)), "jj2": _bc(127.0 - np.arange(128, dtype=np.float32)),
        "cre": cre.reshape(128, 512), "cimn": cimn.reshape(128, 512), "dblk": dblk.reshape(128, 512),
    }
    qi = np.arange(128)[:, None]
    in_maps = []
    for c in range(NCORES):
        tiles = [8 * i + c for i in range(16)]
        xo = np.concatenate([xp[t * 128:(t + 1) * 128] for t in tiles] + [x_sample[4 * c:4 * c + 4].reshape(128, D)], 0)
        pos_o = np.concatenate([np.arange(t * 128, (t + 1) * 128) for t in tiles] + [np.tile(1024 + np.arange(32), 4)])
        rco, rso = _rope_tables(pos_o)
        kj = np.arange(1024)[None, :]
        vis = ((kj // 128) < c) | (((kj // 128) == c) & (((kj % 128) // 64) <= (qi // 64)))
        maskd = np.where(vis, 0.0, NEG).astype(np.float32)
        onehot = np.zeros((128, 8), np.float32); onehot[:, c] = 1.0
        s0 = np.stack([_lay_gp(f(state_ssm_re)[0, 4 * c + s]) for s in range(4)], 1)
        s0i = np.stack([_lay_gp(f(state_ssm_im)[0, 4 * c + s]) for s in range(4)], 1)
        m = dict(common)
        m.update({
            "xo": np.ascontiguousarray(xo), "ropec_o": rco, "ropes_o": rso, "maskd": maskd, "onehot": onehot,
            "s0": np.ascontiguousarray(np.stack([s0, s0i], 1).reshape(128, 128)),
            "cckv": f(cache_mla_ckv)[0, 4 * c:4 * c + 4].reshape(4096, 256),
            "ckpe": f(cache_mla_kpe)[0, 4 * c:4 * c + 4].reshape(4096, 64),
            "cmk": f(cache_mem_k)[0, 4 * c:4 * c + 4].reshape(1024, D),
            "cmv": f(cache_mem_v)[0, 4 * c:4 * c + 4].reshape(1024, D),
        })
        in_maps.append(m)
    if "nc" not in _NC_CACHE:
        _NC_CACHE["nc"] = build_nc()
    res = run_bass_kernel_spmd(_NC_CACHE["nc"], in_maps, core_ids=list(range(NCORES)))
    R = res.results
    y_p = np.zeros((1, 16384, D), np.float32); y_s = np.zeros((32, 32, D), np.float32)
    ckv_s = np.zeros((1, 32, 32, 256), np.float32); kpe_s = np.zeros((1, 32, 32, 64), np.float32)
    sre_s = np.zeros((1, 32, 32, 64), np.float32); sim_s = np.zeros((1, 32, 32, 64), np.float32)

    def unlay(a):
        return a.reshape(2, 64, 16).transpose(2, 0, 1).reshape(32, 64)

    for c in range(NCORES):
        yo = R[c]["y_o"]
        for i in range(16):
            t = 8 * i + c
            y_p[0, t * 128:(t + 1) * 128] = yo[i * 128:(i + 1) * 128]
        y_s[4 * c:4 * c + 4] = yo[16 * 128:].reshape(4, 32, D)
        ckv_s[0, 4 * c:4 * c + 4] = R[c]["ckvs_o"].reshape(4, 32, 256)
        kpe_s[0, 4 * c:4 * c + 4] = R[c]["kpes_o"].reshape(4, 32, 64)
        sf = R[c]["ssms_o"].reshape(128, 2, 4, 16)
        for s in range(4):
            sre_s[0, 4 * c + s] = unlay(sf[:, 0, s, :])
            sim_s[0, 4 * c + s] = unlay(sf[:, 1, s, :])
    r0 = R[0]
    ckv_p = r0["ckv_o"].reshape(1, 1, 16384, 256); kpe_p = r0["kpe_o"].reshape(1, 1, 16384, 64)
    sp = r0["ssmp_o"]
    sre_p = unlay(sp[:, 0:16]).reshape(1, 1, 32, 64); sim_p = unlay(sp[:, 16:32]).reshape(1, 1, 32, 64)
    mk_p = r0["memk_o"].reshape(1, 1, 256, 4, 256); mv_p = r0["memv_o"].reshape(1, 1, 256, 4, 256)
    return (y_p, y_s, ckv_p, kpe_p, sre_p, sim_p, mk_p, mv_p, ckv_s, kpe_s, sre_s, sim_s)
```

```python
import math
from contextlib import ExitStack

import numpy as np
import concourse.bass as bass
import concourse.mybir as mybir
from concourse.bass_utils import run_bass_kernel_spmd

F32 = mybir.dt.float32
BF16 = mybir.dt.bfloat16
I32 = mybir.dt.int32
AF = mybir.ActivationFunctionType
ALU = mybir.AluOpType
AX = mybir.AxisListType

NCORES = 8
D = 1024
NTP = 128
NOWN = 17
EPS = 1e-5
ALPHA = 2.0 ** 0.25
MLA_SCALE = 192.0 ** -0.5
X_SCALE = 256.0 ** -0.5
TWO_PI = 2.0 * math.pi
NEG = -1e30
NRING = 24
COMPUTE = ("pe", "act", "dve", "pool")


class Buf:
    __slots__ = ("name", "lw", "rd")

    def __init__(self, name=""):
        self.name = name
        self.lw = None
        self.rd = []


def _flat(bs):
    out = []
    for b in bs:
        if isinstance(b, (tuple, list)):
            out.extend(b)
        else:
            out.append(b)
    return out


class Op:
    __slots__ = ("eng", "fn", "deps", "isdma", "flag", "cnt", "ring", "n")

    def __init__(self, eng, fn, isdma):
        self.eng = eng
        self.fn = fn
        self.isdma = isdma
        self.deps = set()
        self.flag = False
        self.cnt = 0
        self.ring = None
        self.n = 0


class Prog:
    def __init__(self, nc):
        self.nc = nc
        self.ops = {e: [] for e in ("pe", "act", "dve", "pool", "sp")}
        self.ndma = {e: 0 for e in self.ops}
        self.allops = []
        self.floor = []
        self.dmas_since = []

    def _add(self, eng, fn, reads, writes, isdma):
        op = Op(eng, fn, isdma)
        reads = _flat(reads)
        writes = _flat(writes)
        for f in self.floor:
            op.deps.add(f)
        for b in reads:
            if b.lw is not None:
                op.deps.add(b.lw)
        for b in writes:
            if b.lw is not None:
                op.deps.add(b.lw)
            for r in b.rd:
                op.deps.add(r)
        for b in reads:
            b.rd.append(op)
        for b in writes:
            b.lw = op
            b.rd = []
        op.deps.discard(op)
        if isdma:
            op.n = self.ndma[eng]
            self.ndma[eng] += 1
            self.dmas_since.append(op)
        self.ops[eng].append(op)
        self.allops.append(op)
        return op

    def op(self, eng, fn, reads=(), writes=()):
        return self._add(eng, fn, reads, writes, False)

    def dma(self, eng, fn, reads=(), writes=()):
        return self._add(eng, fn, reads, writes, True)

    def barrier(self, fn):
        op = Op("pool", fn, False)
        for f in self.floor:
            op.deps.add(f)
        for e in COMPUTE:
            for o in reversed(self.ops[e]):
                if not o.isdma:
                    op.deps.add(o)
                    break
        for o in self.dmas_since:
            op.deps.add(o)
        self.dmas_since = []
        self.ops["pool"].append(op)
        self.allops.append(op)
        self.floor = [op]

    def emit(self, stack):
        nc = self.nc
        for op in self.allops:
            for d in op.deps:
                if d.eng == "pe" and op.eng == "pe" and not d.isdma:
                    continue
                d.flag = True
        sems = {e: stack.enter_context(nc.semaphore("s_" + e)) for e in COMPUTE}
        rings = {}
        for e in self.ops:
            if self.ndma[e]:
                rings[e] = [stack.enter_context(nc.semaphore("r_%s_%d" % (e, i))) for i in range(NRING)]
        for e in self.ops:
            c = 0
            for op in self.ops[e]:
                if op.isdma:
                    op.ring = rings[e][op.n % NRING]
                    op.cnt = 16 * (op.n // NRING + 1)
                elif op.flag:
                    c += 1
                    op.cnt = c
        block = stack.enter_context(nc.Block())

        def run(e, h):
            waited = {}

            def wait(sem, val):
                k = id(sem)
                if waited.get(k, 0) >= val:
                    return
                waited[k] = val
                h.wait_ge(sem, val)

            for op in self.ops[e]:
                for d in op.deps:
                    if d.isdma:
                        wait(d.ring, d.cnt)
                    else:
                        if d.eng == "pe" and e == "pe":
                            continue
                        wait(sems[d.eng], d.cnt)
                if op.isdma and op.n >= NRING:
                    wait(op.ring, op.cnt - 16)
                ins = op.fn(h)
                if op.isdma:
                    ins.then_inc(op.ring, 16)
                elif op.flag:
                    ins.then_inc(sems[e], 1)
            if e in rings:
                n = self.ndma[e]
                for i in range(min(n, NRING)):
                    last = ((n - 1 - i) // NRING) * NRING + i
                    wait(rings[e][i], 16 * (last // NRING + 1))

        @block.tensor
        def _(h):
            run("pe", h)

        @block.scalar
        def _(h):
            run("act", h)

        @block.vector
        def _(h):
            run("dve", h)

        @block.gpsimd
        def _(h):
            run("pool", h)

        @block.sync
        def _(h):
            run("sp", h)


class V:
    __slots__ = ("ap", "b")

    def __init__(self, ap, b):
        self.ap = ap
        self.b = b

    def __getitem__(self, idx):
        return V(self.ap[idx], self.b)

    def re(self, pat, **kw):
        return V(self.ap.rearrange(pat, **kw), self.b)

    def cast(self, dt):
        return V(self.ap.bitcast(dt), self.b)

    def bc(self, shape):
        return V(self.ap.broadcast_to(shape), self.b)

    def un(self, ax):
        return V(self.ap.unsqueeze(ax), self.b)


def build_nc():
    nc = bass.Bass("TRN2", target_bir_lowering=False)
    P = Prog(nc)
    dram = {}

    def din(name, shape, dt=F32):
        t = nc.dram_tensor(name, list(shape), dt, kind="ExternalInput")
        dram[name] = V(t.ap(), Buf(name))
        return dram[name]

    def dout(name, shape):
        t = nc.dram_tensor(name, list(shape), F32, kind="ExternalOutput")
        dram[name] = V(t.ap(), Buf(name))
        return dram[name]

    def dscr(name, shape, dt=F32):
        t = nc.dram_tensor(name, list(shape), dt)
        return V(t.ap(), Buf(name))

    xp = din("xp", [NTP * 128, D])
    xo = din("xo", [NOWN * 128, D])
    mem = din("mem", [256, D])
    ident_d = din("ident", [128, 128])
    w_in_d = din("w_in", [D, 1216]); w_q_d = din("w_q", [384, 768]); w_kv_d = din("w_kv", [256, 1024])
    w_glu_d = din("w_glu", [512, 1024]); w_o_d = din("w_o", [D, D]); w_xq_d = din("w_xq", [D, D])
    w_xk_d = din("w_xk", [D, D]); w_xv_d = din("w_xv", [D, D]); w_xo_d = din("w_xo", [D, D])
    w_ff1_d = din("w_ff1", [D, 4096]); w_ff2_d = din("w_ff2", [4096, D])
    gq_d = din("gq", [128, 384]); gkv_d = din("gkv", [128, 256]); gos_d = din("gos", [128, 512]); gom_d = din("gom", [128, 512])
    lng_d = din("lng", [128, 3 * D]); lnb_d = din("lnb", [128, 3 * D])
    ropec_p = din("ropec_p", [NTP * 128, 64]); ropes_p = din("ropes_p", [NTP * 128, 64])
    ropec_o = din("ropec_o", [NOWN * 128, 64]); ropes_o = din("ropes_o", [NOWN * 128, 64])
    maskd_d = din("maskd", [128, 1024]); onehot_d = din("onehot", [128, 8])
    ar_d = din("ar", [128, 16]); ai_d = din("ai", [128, 16]); ldt_d = din("ldt", [128, 16])
    bre_d = din("bre", [128, 256]); bim_d = din("bim", [128, 256]); jj_d = din("jj", [128, 128]); jj2_d = din("jj2", [128, 128])
    cre_d = din("cre", [128, 512]); cimn_d = din("cimn", [128, 512]); dblk_d = din("dblk", [128, 512])
    s0_d = din("s0", [128, 128])
    cckv_d = din("cckv", [4 * 1024, 256]); ckpe_d = din("ckpe", [4 * 1024, 64])
    cmk_d = din("cmk", [4 * 256, D]); cmv_d = din("cmv", [4 * 256, D])

    y_o = dout("y_o", [NOWN * 128, D])
    ckv_o = dout("ckv_o", [NTP * 128, 256]); kpe_o = dout("kpe_o", [NTP * 128, 64])
    ssmp_o = dout("ssmp_o", [128, 32])
    memk_o = dout("memk_o", [256, D]); memv_o = dout("memv_o", [256, D])
    ckvs_o = dout("ckvs_o", [128, 256]); kpes_o = dout("kpes_o", [128, 64]); ssms_o = dout("ssms_o", [128, 128])

    h1_d = dscr("h1_d", [NOWN * 128, D]); h2_d = dscr("h2_d", [NOWN * 128, D])
    tab_d = {n_: dscr("tab_" + n_, [128, 2048]) for n_ in ("S_t", "C_t", "R_t")}
    tab_d["WBre"] = dscr("tab_WBre", [128, 2048], BF16); tab_d["WBim"] = dscr("tab_WBim", [128, 2048], BF16)

    with ExitStack() as st:
        ARENA_N = 52000
        arena = st.enter_context(nc.sbuf_tensor("arena", [128, ARENA_N], F32))
        bump = [0]

        def alloc(shape, dt=F32, name=""):
            n = 1
            for s_ in shape[1:]:
                n *= s_
            words = (n * (2 if dt == BF16 else 4) + 3) // 4
            words = (words + 7) // 8 * 8
            off = bump[0]
            bump[0] += words
            assert bump[0] <= ARENA_N, ("SBUF arena overflow", name, bump[0])
            ap = arena[:, off:off + words]
            if dt != F32:
                ap = ap.bitcast(dt)
            ap = ap[:, 0:n]
            if len(shape) > 2:
                names = " ".join("d%d" % i for i in range(len(shape) - 1))
                kw = {"d%d" % i: shape[i + 1] for i in range(len(shape) - 1)}
                ap = ap.rearrange("p (%s) -> p %s" % (names, names), **kw)
            return V(ap, Buf(name))

        banks = []
        dbl = []
        for d_ in range(4):
            t = st.enter_context(nc.psum_tensor("dbank%d" % d_, [128, 1024], F32))
            b0 = Buf("bank%d" % (2 * d_)); b1 = Buf("bank%d" % (2 * d_ + 1))
            banks.append(V(t[:, 0:512], b0)); banks.append(V(t[:, 512:1024], b1))
            dbl.append(V(t[:], (b0, b1)))
        SC = dbl[3]
        SCs = [dbl[3], dbl[0]]

        def mm(out, lhsT, rhs, start=True, stop=True):
            P.op("pe", lambda e: e.matmul(out.ap, lhsT=lhsT.ap, rhs=rhs.ap, start=start, stop=stop),
                 reads=[lhsT.b, rhs.b], writes=[out.b])

        def tr(out, in_, idt):
            P.op("pe", lambda e: e.transpose(out=out.ap, in_=in_.ap, identity=idt.ap), reads=[in_.b, idt.b], writes=[out.b])

        def act(out, in_, func, bias=None, scale=None, accum=None):
            kw = {}
            rd = [in_.b]
            wr = [out.b]
            if bias is not None:
                if isinstance(bias, V):
                    kw["bias"] = bias.ap; rd.append(bias.b)
                else:
                    kw["bias"] = bias
            if scale is not None:
                if isinstance(scale, V):
                    kw["scale"] = scale.ap; rd.append(scale.b)
                else:
                    kw["scale"] = scale
            if accum is not None:
                kw["accum_out"] = accum.ap; wr.append(accum.b)
            P.op("act", lambda e: e.activation(out=out.ap, in_=in_.ap, func=func, **kw), reads=rd, writes=wr)

        def acopy(out, in_):
            P.op("act", lambda e: e.copy(out=out.ap, in_=in_.ap), reads=[in_.b], writes=[out.b])

        def vcopy(out, in_, eng="dve"):
            P.op(eng, lambda e: e.tensor_copy(out=out.ap, in_=in_.ap), reads=[in_.b], writes=[out.b])

        def tt(out, in0, in1, op, eng="dve"):
            P.op(eng, lambda e: e.tensor_tensor(out=out.ap, in0=in0.ap, in1=in1.ap, op=op), reads=[in0.b, in1.b], writes=[out.b])

        def ts(out, in0, s1, s2, op0, op1=None, eng="dve"):
            rd = [in0.b]
            a1 = s1
            a2 = s2
            if isinstance(s1, V):
                a1 = s1.ap; rd.append(s1.b)
            if isinstance(s2, V):
                a2 = s2.ap; rd.append(s2.b)
            if op1 is None:
                P.op(eng, lambda e: e.tensor_scalar(out=out.ap, in0=in0.ap, scalar1=a1, scalar2=None, op0=op0), reads=rd, writes=[out.b])
            else:
                P.op(eng, lambda e: e.tensor_scalar(out=out.ap, in0=in0.ap, scalar1=a1, scalar2=a2, op0=op0, op1=op1), reads=rd, writes=[out.b])

        def stt(out, in0, scalar, in1, op0, op1):
            rd = [in0.b, in1.b]
            a = scalar
            if isinstance(scalar, V):
                a = scalar.ap; rd.append(scalar.b)
            P.op("dve", lambda e: e.scalar_tensor_tensor(out=out.ap, in0=in0.ap, scalar=a, in1=in1.ap, op0=op0, op1=op1), reads=rd, writes=[out.b])

        def red(out, in_, op, axis=AX.X):
            P.op("dve", lambda e: e.tensor_reduce(out=out.ap, in_=in_.ap, axis=axis, op=op), reads=[in_.b], writes=[out.b])

        def recip(out, in_):
            P.op("dve", lambda e: e.reciprocal(out=out.ap, in_=in_.ap), reads=[in_.b], writes=[out.b])

        def scan(out, d0, d1, init):
            P.op("dve", lambda e: e.tensor_tensor_scan(out=out.ap, data0=d0.ap, data1=d1.ap, initial=init.ap, op0=ALU.mult, op1=ALU.add),
                 reads=[d0.b, d1.b, init.b], writes=[out.b])

        def mset(out, val, eng="pool"):
            P.op(eng, lambda e: e.memset(out.ap, val), writes=[out.b])

        def ld(out, in_, eng="sp"):
            P.dma(eng, lambda e: e.dma_start(out=out.ap, in_=in_.ap), reads=[in_.b], writes=[out.b])

        def ldw(out, in_):
            P.dma("pool", lambda e: e.dma_start(out=out.ap, in_=in_.ap), reads=[in_.b], writes=[out.b])

        ident = alloc([128, 128], F32, "ident"); identb = alloc([128, 128], BF16, "identb")
        bar_scr = alloc([128, 8], F32, "barscr")
        ld(ident, ident_d); ldw(identb, ident_d)
        epsc = alloc([128, 1], F32, "epsc"); mset(epsc, EPS)

        def barrier():
            P.barrier(lambda e: e.memset(bar_scr.ap, 0.0))

        def load_xT(src_rows, xt, xT, bank):
            ld(xt, src_rows)
            for hb in range(2):
                for k in range(4):
                    kk = hb * 4 + k
                    tr(bank[:, k * 128:(k + 1) * 128], xt[:, kk * 128:(kk + 1) * 128], ident)
                src = bank.re("p (k n) -> p k n", k=4)
                if hb == 0:
                    acopy(xT[:, 0:4, :], src)
                else:
                    vcopy(xT[:, 4:8, :], src)

        def transpose_to(dst, src, ncol, bank, eng="act", rows=128):
            bb = bank.cast(BF16)
            for k in range(ncol):
                tr(bb[:, k * 128:k * 128 + rows], src[:, k * 128:(k + 1) * 128], identb[0:rows, 0:rows])
            s_ = bb[:, 0:ncol * 128].re("p (k n) -> p k n", k=ncol)[:, :, 0:rows]
            if eng == "act":
                acopy(dst, s_)
            else:
                vcopy(dst, s_)

        def rmsnorm(out, src, n, gtile, sq, ss):
            act(sq, src, AF.Square, accum=ss)
            act(ss, ss, AF.Ln, scale=1.0 / n, bias=epsc)
            act(ss, ss, AF.Exp, scale=-0.5)
            stt(out, src, ss, gtile, ALU.mult, ALU.mult)

        def layernorm(out, r, g, b, stats, mv):
            for c in range(2):
                P.op("dve", lambda e, c=c: e.bn_stats(out=stats.ap[:, c * 6:(c + 1) * 6], in_=r.ap[:, c * 512:(c + 1) * 512]), reads=[r.b], writes=[stats.b])
            P.op("dve", lambda e: e.bn_aggr(out=mv.ap[:, 0:2], in_=stats.ap[:, 0:12]), reads=[stats.b], writes=[mv.b])
            act(mv[:, 1:2], mv[:, 1:2], AF.Ln, bias=epsc)
            act(mv[:, 1:2], mv[:, 1:2], AF.Exp, scale=-0.5)
            ts(out, r, mv[:, 0:1], mv[:, 1:2], ALU.subtract, ALU.mult)
            tt(out, out, g, ALU.mult, eng="pool")
            tt(out, out, b, ALU.add, eng="pool")

        def rope(out, src, cc, ss_, tmp, nh, eng="dve"):
            ccb = cc.un(1).bc([128, nh, 64]); ssb = ss_.un(1).bc([128, nh, 64])
            tt(out, src, ccb, ALU.mult, eng=eng)
            tt(tmp[:, :, 0:32], src[:, :, 32:64], ssb[:, :, 0:32], ALU.mult, eng=eng)
            tt(tmp[:, :, 32:64], src[:, :, 0:32], ssb[:, :, 32:64], ALU.mult, eng=eng)
            tt(out, out, tmp, ALU.add, eng="pool")

        mkT = alloc([128, 4, 2, 256], BF16, "mkT")
        mvb = alloc([128, 2, D], BF16, "mvb")
        mark_after_mem = bump[0]
        btab = Buf("tab")
        W_in_u = alloc([128, 8, 512], BF16, "W_in_u"); W_in_kv = alloc([128, 8, 320], BF16, "W_in_kv")
        LrT = alloc([128, 16, 128], BF16, "LrT"); LiT = alloc([128, 16, 128], BF16, "LiT")
        BRb = alloc([128, 16, 32], F32, "BRb"); BIb = alloc([128, 16, 32], F32, "BIb")
        lam128 = alloc([128, 2, 16], F32, "lam128")
        Xpp = alloc([128, 2, 32], F32, "Xpp")
        Xown = alloc([128, 16, 32], F32, "Xown")
        ohot = alloc([128, 8], F32, "ohot")
        Oacc = alloc([128, 16, 4, 128], F32, "Oacc")
        m_run = alloc([128, 64], F32, "m_run"); l_run = alloc([128, 64], F32, "l_run")
        Osam = alloc([128, 512], F32, "Osam")
        ssmx = {}
        mark_after_mixer_state = bump[0]
        QTn = alloc([128, NOWN, 4, 128], BF16, "QTn"); QTr = alloc([128, NOWN, 4, 128], BF16, "QTr")
        mark_after_q = bump[0]

        w_in_v = w_in_d.re("(kt p) n -> p kt n", p=128)
        ldw(W_in_u, w_in_v[:, :, 0:512]); ldw(W_in_kv, w_in_v[:, :, 896:1216])
        ld(ohot, onehot_d)

        S_t = alloc([128, 16, 128], F32, "S_t"); C_t = alloc([128, 16, 128], F32, "C_t"); R_t = alloc([128, 16, 128], F32, "R_t")
        for v_ in (S_t, C_t, R_t):
            v_.b = btab
        WBre = alloc([128, 16, 128], BF16, "WBre"); WBim = alloc([128, 16, 128], BF16, "WBim")

        p0 = bump[0]
        ar = alloc([128, 16]); ai = alloc([128, 16]); ldt = alloc([128, 16]); jj = alloc([128, 128])
        bre = alloc([128, 16, 16]); bim = alloc([128, 16, 16])
        bsm = Buf("ssm_small")
        for v_ in (ar, ai, ldt, jj, bre, bim):
            v_.b = bsm
        ld(ar, ar_d); ld(ai, ai_d); ld(ldt, ldt_d); ld(jj, jj_d)
        ld(bre.re("p a b -> p (a b)"), bre_d); ld(bim.re("p a b -> p (a b)"), bim_d)
        dt_ = alloc([128, 16]); th = alloc([128, 16]); rr = alloc([128, 16])
        sm = [alloc([128, 16]) for _ in range(8)]
        for v_ in [dt_, th, rr] + sm:
            v_.b = bsm
        act(dt_, ldt, AF.Exp)
        tt(th, ai, dt_, ALU.mult)
        tt(rr, ar, dt_, ALU.mult)
        act(rr, rr, AF.Exp)
        A_t = alloc([128, 16, 128]); T1 = alloc([128, 2048]); TI = alloc([128, 2048], I32)
        for v_ in (A_t, T1, TI):
            v_.b = btab
        jjb = jj.un(1).bc([128, 16, 128])
        tt(A_t, jjb, th.un(2).bc([128, 16, 128]), ALU.mult)
        tt(R_t, jjb, rr.un(2).bc([128, 16, 128]), ALU.max)
        tt(R_t, R_t, rr.un(2).bc([128, 16, 128]), ALU.min)
        Af = A_t.re("p a b -> p (a b)")
        TIf = TI.cast(F32)

        def sin_of(out, shift):
            ts(T1, Af, shift, 1.0 / TWO_PI, ALU.add, ALU.mult)
            vcopy(TI, T1)
            vcopy(T1, TI)
            stt(T1, T1, -TWO_PI, Af, ALU.mult, ALU.add)
            if shift != 0.0:
                ts(T1, T1, shift, None, ALU.add)
            ts(TIf, T1, math.pi, -TWO_PI, ALU.is_gt, ALU.mult)
            tt(T1, T1, TIf, ALU.add)
            ts(TIf, T1, -math.pi, TWO_PI, ALU.is_lt, ALU.mult)
            tt(T1, T1, TIf, ALU.add)
            ts(T1, T1, math.pi, -math.pi, ALU.min, ALU.max)
            act(out, T1, AF.Sin)

        sin_of(S_t.re("p a b -> p (a b)"), 0.0)
        sin_of(C_t.re("p a b -> p (a b)"), math.pi / 2)
        lbr, lbi, den, fre, fim, t0_, t1_, t2_ = sm
        cos1 = C_t[:, :, 0]; sin1 = S_t[:, :, 0]
        tt(lbr, rr, cos1, ALU.mult)
        ts(lbr, lbr, -1.0, None, ALU.add)
        tt(lbi, rr, sin1, ALU.mult)
        tt(den, ar, ar, ALU.mult)
        tt(t0_, ai, ai, ALU.mult)
        tt(den, den, t0_, ALU.add)
        recip(den, den)
        tt(t0_, lbr, ar, ALU.mult); tt(t1_, lbi, ai, ALU.mult); tt(fre, t0_, t1_, ALU.add); tt(fre, fre, den, ALU.mult)
        tt(t0_, lbi, ar, ALU.mult); tt(t1_, lbr, ai, ALU.mult); tt(fim, t0_, t1_, ALU.subtract); tt(fim, fim, den, ALU.mult)
        Mre = alloc([128, 16, 128]); Mim = alloc([128, 16, 128]); tb = alloc([128, 16, 16]); tb2 = alloc([128, 16, 16])
        Mreb = alloc([128, 16, 128], BF16); Mimb = alloc([128, 16, 128], BF16)
        bM = Buf("M")
        for v_ in (Mre, Mim, tb, tb2, Mreb, Mimb):
            v_.b = bM
        freb = fre.un(2).bc([128, 16, 16]); fimb = fim.un(2).bc([128, 16, 16])
        mset(Mre, 0.0); mset(Mim, 0.0)
        tt(tb, bre, freb, ALU.mult); tt(tb2, bim, fimb, ALU.mult)
        for lo, col in ((0, 0), (64, 16)):
            for j4 in range(4):
                tt(Mre[lo:lo + 64, j4::4, 32 * j4 + col:32 * j4 + col + 16], tb[lo:lo + 64, j4::4, :], tb2[lo:lo + 64, j4::4, :], ALU.subtract)
        tt(tb, bim, freb, ALU.mult); tt(tb2, bre, fimb, ALU.mult)
        for lo, col in ((0, 0), (64, 16)):
            for j4 in range(4):
                tt(Mim[lo:lo + 64, j4::4, 32 * j4 + col:32 * j4 + col + 16], tb[lo:lo + 64, j4::4, :], tb2[lo:lo + 64, j4::4, :], ALU.add)
        vcopy(Mreb, Mre); vcopy(Mimb, Mim)
        for M_, WB_ in ((Mreb, WBre), (Mimb, WBim)):
            for hf in range(2):
                bb = banks[0].cast(BF16)
                for q in range(8):
                    tr(bb[:, q * 128:(q + 1) * 128], M_[:, 8 * hf + q, :], identb)
                vcopy(WB_[:, 8 * hf:8 * hf + 8, :].re("p a b -> p (a b)"), bb)

        mset(BRb, 0.0); mset(BIb, 0.0)
        tt(tb, bre, freb, ALU.mult); tt(tb2, bim, fimb, ALU.mult)
        for lo, col in ((0, 0), (64, 16)):
            tt(BRb[lo:lo + 64, :, col:col + 16], tb[lo:lo + 64], tb2[lo:lo + 64], ALU.subtract)
        tt(tb, bim, freb, ALU.mult); tt(tb2, bre, fimb, ALU.mult)
        for lo, col in ((0, 0), (64, 16)):
            tt(BIb[lo:lo + 64, :, col:col + 16], tb[lo:lo + 64], tb2[lo:lo + 64], ALU.add)
        lnr = alloc([128, 16]); lnr.b = bsm
        tt(lnr, ar, dt_, ALU.mult)
        act(t2_, lnr, AF.Exp, scale=128.0)
        tt(lam128[:, 0, :], t2_, C_t[:, :, 127], ALU.mult)
        tt(lam128[:, 1, :], t2_, S_t[:, :, 127], ALU.mult)
        jj2 = alloc([128, 128]); jj2.b = bsm
        ld(jj2, jj2_d)
        C2 = Mre; S2 = Mim; Mag = T1.re("p (a b) -> p a b", a=16)
        jj2b = jj2.un(1).bc([128, 16, 128])
        tt(A_t, jj2b, th.un(2).bc([128, 16, 128]), ALU.mult)
        sin_of(S2.re("p a b -> p (a b)"), 0.0)
        sin_of(C2.re("p a b -> p (a b)"), math.pi / 2)
        tt(Mag, jj2b, lnr.un(2).bc([128, 16, 128]), ALU.mult)
        act(Mag, Mag, AF.Exp)
        Lb = [Mreb, Mimb]
        tt(Lb[0], Mag, C2, ALU.mult); tt(Lb[1], Mag, S2, ALU.mult)
        for M_, LT_ in ((Lb[0], LrT), (Lb[1], LiT)):
            for hf in range(2):
                bb = banks[0].cast(BF16)
                for q in range(8):
                    tr(bb[:, q * 128:(q + 1) * 128], M_[:, 8 * hf + q, :], identb)
                vcopy(LT_[:, 8 * hf:8 * hf + 8, :].re("p a b -> p (a b)"), bb)

        for nm_, v_ in (("S_t", S_t), ("C_t", C_t), ("R_t", R_t)):
            ld(tab_d[nm_], v_.re("p a b -> p (a b)"))
        ld(tab_d["WBre"], WBre.re("p a b -> p (a b)")); ld(tab_d["WBim"], WBim.re("p a b -> p (a b)"))

        barrier()
        bump[0] = mark_after_q
        W_xk = alloc([128, 8, D], BF16, "W_xk"); W_xv = alloc([128, 8, D], BF16, "W_xv")
        ldw(W_xk, w_xk_d.re("(kt p) n -> p kt n", p=128)); ldw(W_xv, w_xv_d.re("(kt p) n -> p kt n", p=128))
        xt0 = alloc([128, D], F32, "xt0"); memT = alloc([128, 8, 256], BF16, "memT"); xT0 = alloc([128, 8, 128], BF16, "xT0")
        mkf = alloc([128, D], F32, "mkf")
        for mt in range(2):
            load_xT(mem[mt * 128:(mt + 1) * 128, :], xt0, xT0, banks[0])
            vcopy(memT[:, :, mt * 128:(mt + 1) * 128], xT0, eng="pool")
            for W_, o_d, keep in ((W_xk, memk_o, False), (W_xv, memv_o, True)):
                for nb in range(2):
                    for k in range(8):
                        mm(banks[1 + nb], xT0[:, k, :], W_[:, k, nb * 512:(nb + 1) * 512], start=(k == 0), stop=(k == 7))
                    acopy(mkf[:, nb * 512:(nb + 1) * 512], banks[1 + nb])
                ld(o_d[mt * 128:(mt + 1) * 128, :], mkf)
                if keep:
                    vcopy(mvb[:, mt, :], mkf)
        for h in range(4):
            for et in range(2):
                c0 = h * 256 + et * 128
                for k in range(8):
                    mm(banks[3][:, 0:256], W_xk[:, k, c0:c0 + 128], memT[:, k, :], start=(k == 0), stop=(k == 7))
                acopy(mkT[:, h, et, :], banks[3][:, 0:256])

        barrier()
        bump[0] = mark_after_q

        W_q = alloc([128, 3, 768], BF16, "W_q")
        ldw(W_q, w_q_d.re("(kt p) n -> p kt n", p=128))
        W_in_q = alloc([128, 8, 384], BF16, "W_in_q")
        ldw(W_in_q, w_in_v[:, :, 512:896])
        gq = alloc([128, 384], F32, "gq"); ld(gq, gq_d)
        NB = 2
        xt = [alloc([128, D], F32, "xt%d" % i) for i in range(NB)]
        xT = [alloc([128, 8, 128], BF16, "xT%d" % i) for i in range(NB)]
        rc = [alloc([128, 64], F32, "rc%d" % i) for i in range(NB)]
        rs = [alloc([128, 64], F32, "rs%d" % i) for i in range(NB)]
        sq = alloc([128, 512], F32, "sq"); ss1 = [alloc([128, 1], F32) for _ in range(NB)]
        cqn = [alloc([128, 384], BF16) for _ in range(NB)]
        cqT = [alloc([128, 3, 128], BF16) for _ in range(NB)]
        qf = [alloc([128, 4, 192], F32) for _ in range(NB)]
        qr = [alloc([128, 4, 64], F32) for _ in range(NB)]
        qtmp = [alloc([128, 4, 64], F32) for _ in range(NB)]
        qb = [alloc([128, 4, 192], BF16) for _ in range(NB)]
        for i in range(NOWN):
            b = i % NB
            load_xT(xo[i * 128:(i + 1) * 128, :], xt[b], xT[b], banks[0])
            ld(rc[b], ropec_o[i * 128:(i + 1) * 128, :]); ld(rs[b], ropes_o[i * 128:(i + 1) * 128, :])
            for k in range(8):
                mm(banks[1][:, 0:384], xT[b][:, k, :], W_in_q[:, k, :], start=(k == 0), stop=(k == 7))
            rmsnorm(cqn[b], banks[1][:, 0:384], 384, gq, sq[:, 0:384], ss1[b])
            transpose_to(cqT[b], cqn[b], 3, banks[2], eng="act")
            for nb, (c0, c1) in enumerate(((0, 512), (512, 768))):
                for k in range(3):
                    mm(SC[:, nb * 512:nb * 512 + (c1 - c0)], cqT[b][:, k, :], W_q[:, k, c0:c1], start=(k == 0), stop=(k == 2))
            acopy(qf[b].re("p a b -> p (a b)"), SC[:, 0:768])
            rope(qr[b], qf[b][:, :, 128:192], rc[b], rs[b], qtmp[b], 4)
            P.op("act", lambda e, b=b: e.mul(out=qb[b].ap[:, :, 0:128], in_=qf[b].ap[:, :, 0:128], mul=MLA_SCALE), reads=[qf[b].b], writes=[qb[b].b])
            P.op("act", lambda e, b=b: e.mul(out=qb[b].ap[:, :, 128:192], in_=qr[b].ap, mul=MLA_SCALE), reads=[qr[b].b], writes=[qb[b].b])
            bb = banks[3].cast(BF16)
            for h in range(4):
                tr(bb[:, h * 128:(h + 1) * 128], qb[b][:, h, 0:128], identb)
                tr(bb[0:64, 512 + h * 128:512 + (h + 1) * 128], qb[b][:, h, 128:192], identb)
            vcopy(QTn[:, i, :, :].re("p a b -> p (a b)"), bb[:, 0:512])
            vcopy(QTr[0:64, i, :, :].re("p a b -> p (a b)"), bb[0:64, 512:1024])

        barrier()
        bump[0] = mark_after_q

        def ssm_rounds(uT_, pbs, init_fn, nseg, ybank=None, last_out=None):
            L = 128 // nseg
            S_t = ssmx["S_t"]; C_t = ssmx["C_t"]; R_t = ssmx["R_t"]; WBre = ssmx["WBre"]; WBim = ssmx["WBim"]
            Cre = ssmx["Cre"]; Cimn = ssmx["Cimn"]; Dblk = ssmx["Dblk"]
            xs4 = ssmx["xs4"]; zl_all = ssmx["zl_all"]
            def do_round(r):
                m4 = ssmx["m4"][r % 2]; wri = ssmx["wri"][r % 2]; zri = ssmx["zri"][r % 2]
                pb = pbs[r % 2]
                for j in range(2):
                    gp = 2 * r + j
                    mm(pb[:, j * 128:(j + 1) * 128], WBre[:, gp, :], uT_[:, gp // 4, :])
                    mm(pb[:, 256 + j * 128:256 + (j + 1) * 128], WBim[:, gp, :], uT_[:, gp // 4, :])
                if nseg == 1:
                    Cq = C_t[:, 2 * r:2 * r + 2, :].re("p a b -> p (a b)"); Sq = S_t[:, 2 * r:2 * r + 2, :].re("p a b -> p (a b)")
                    pre = pb[:, 0:256]; pim = pb[:, 256:512]
                    mk = lambda v_: v_
                else:
                    Cq = C_t[:, 2 * r:2 * r + 2, 0:L].un(2).bc([128, 2, nseg, L]); Sq = S_t[:, 2 * r:2 * r + 2, 0:L].un(2).bc([128, 2, nseg, L])
                    pre = pb[:, 0:256].re("p (a s l) -> p a s l", a=2, s=nseg); pim = pb[:, 256:512].re("p (a s l) -> p a s l", a=2, s=nseg)
                    mk = lambda v_: v_.re("p (a s l) -> p a s l", a=2, s=nseg)
                tt(mk(m4[0]), pre, Cq, ALU.mult); tt(mk(m4[1]), pim, Sq, ALU.mult)
                tt(mk(m4[2]), pim, Cq, ALU.mult); tt(mk(m4[3]), pre, Sq, ALU.mult)
                tt(wri[0], m4[0], m4[1], ALU.add, eng="pool")
                tt(wri[1], m4[2], m4[3], ALU.subtract, eng="pool")

            def do_back(r):
                m4 = ssmx["m4"][r % 2]; wri = ssmx["wri"][r % 2]; zri = ssmx["zri"][r % 2]
                if nseg == 1:
                    Cq = C_t[:, 2 * r:2 * r + 2, :].re("p a b -> p (a b)"); Sq = S_t[:, 2 * r:2 * r + 2, :].re("p a b -> p (a b)")
                else:
                    Cq = C_t[:, 2 * r:2 * r + 2, 0:L].un(2).bc([128, 2, nseg, L]); Sq = S_t[:, 2 * r:2 * r + 2, 0:L].un(2).bc([128, 2, nseg, L])
                for j in range(2):
                    gp = 2 * r + j
                    for s_ in range(nseg):
                        for part in range(2):
                            scan(zri[part][:, j, s_ * L:(s_ + 1) * L], R_t[:, gp, 0:L], wri[part][:, j * 128 + s_ * L:j * 128 + (s_ + 1) * L], init_fn(gp, s_, part))
                if last_out is not None:
                    for part in range(2):
                        src = zri[part].re("p a (s l) -> p a s l", s=nseg)[:, :, :, L - 1]
                        vcopy(zl_all[part][:, 0:nseg, 2 * r:2 * r + 2].re("p s a -> p a s"), src, eng="pool")
                if ybank is not None:
                    dmd = ssmx["dmd"][r % 2]; xrb = ssmx["xrb"]; xib = ssmx["xib"]
                    if nseg == 1:
                        Cd, Sd = Cq, Sq
                        zr_ = zri[0].re("p a b -> p (a b)"); zi_ = zri[1].re("p a b -> p (a b)")
                        dk = lambda v_: v_
                    else:
                        Cd, Sd = Cq, Sq
                        zr_ = zri[0].re("p a (s l) -> p a s l", s=nseg); zi_ = zri[1].re("p a (s l) -> p a s l", s=nseg)
                        dk = lambda v_: v_.re("p (a s l) -> p a s l", a=2, s=nseg)
                    tt(dk(dmd[0]), zr_, Cd, ALU.mult); tt(dk(dmd[1]), zi_, Sd, ALU.mult, eng="pool")
                    tt(dk(dmd[2]), zr_, Sd, ALU.mult); tt(dk(dmd[3]), zi_, Cd, ALU.mult, eng="pool")
                    tt(xrb[r % 2].re("p a b -> p (a b)"), dmd[0], dmd[1], ALU.subtract, eng="pool")
                    tt(xib[r % 2].re("p a b -> p (a b)"), dmd[2], dmd[3], ALU.add, eng="pool")
                    for j in range(2):
                        gp = 2 * r + j
                        o_ = ybank[:, 32 * gp:32 * gp + 32]
                        mm(o_, xrb[r % 2][:, j, :], Cre[:, gp, :], start=True, stop=False)
                        mm(o_, xib[r % 2][:, j, :], Cimn[:, gp, :], start=False, stop=False)
                        mm(o_, uT_[:, gp // 4, :], Dblk[:, gp, :], start=False, stop=True)
            def do_tail():
                if last_out is None:
                    return
                if True:
                    cL = C_t[:, :, L - 1]; sL = S_t[:, :, L - 1]
                    for s_ in range(nseg):
                        o_re, o_im = last_out(s_)
                        zr_l = zl_all[0][:, s_, :]; zi_l = zl_all[1][:, s_, :]
                        tt(xs4[0], zr_l, cL, ALU.mult); tt(xs4[1], zi_l, sL, ALU.mult)
                        tt(xs4[2], zr_l, sL, ALU.mult); tt(xs4[3], zi_l, cL, ALU.mult)
                        tt(o_re, xs4[0], xs4[1], ALU.subtract)
                        tt(o_im, xs4[2], xs4[3], ALU.add)


            def step(k_):
                def f():
                    if k_ == 0:
                        do_round(0)
                    if k_ + 1 < 8:
                        do_round(k_ + 1)
                    do_back(k_)
                return f
            return [step(k_) for k_ in range(8)] + [do_tail]

        def ssm_tile(uT_, pbs, init_fn, nseg, ybank=None, last_out=None):
            for f_ in ssm_rounds(uT_, pbs, init_fn, nseg, ybank, last_out):
                f_()

        W_kv = alloc([128, 2, 4, 256], BF16, "W_kv")
        ldw(W_kv.re("p k h e -> p k (h e)"), w_kv_d.re("(kt p) n -> p kt n", p=128))
        gkv = alloc([128, 256], F32, "gkv"); ld(gkv, gkv_d)
        maskd = alloc([128, 1024], F32, "maskd"); ld(maskd, maskd_d)
        xt = [alloc([128, D], F32, "hxt%d" % i) for i in range(NB)]
        xT = [alloc([128, 8, 128], BF16, "hxT%d" % i) for i in range(NB)]
        utok = [alloc([128, 512], BF16, "hut%d" % i) for i in range(NB)]
        kvf = [alloc([128, 320], F32, "kvf%d" % i) for i in range(NB)]
        prs = [alloc([128, 16, 32], F32, "prs%d" % i) for i in range(NB)]
        pis = [alloc([128, 16, 32], F32, "pis%d" % i) for i in range(NB)]
        rc = [alloc([128, 64], F32) for _ in range(NB)]; rs = [alloc([128, 64], F32) for _ in range(NB)]
        ckvf = [alloc([128, 256], F32) for _ in range(NB)]; kpef = [alloc([128, 64], F32) for _ in range(NB)]
        tmp64 = [alloc([128, 64], F32) for _ in range(NB)]
        sq = alloc([128, 256], F32, "hsq"); ss1 = [alloc([128, 1], F32) for _ in range(NB)]
        ckvb = [alloc([128, 256], BF16) for _ in range(NB)]; kpeb = [alloc([128, 128], BF16) for _ in range(NB)]
        ckvT = [alloc([128, 2, 128], BF16) for _ in range(NB)]
        KT = [alloc([128, 4, 1024], BF16, "KT%d" % i) for i in range(2)]
        KR = [alloc([128, 1024], BF16, "KR%d" % i) for i in range(2)]
        Vb = [alloc([128, 8, 512], BF16, "Vb%d" % i) for i in range(2)]
        Pb = [alloc([128, 1024], BF16, "Pb%d" % i) for i in range(2)]
        PT = [alloc([128, 8, 128], BF16, "PT%d" % i) for i in range(2)]
        NS4 = 4
        mx = [alloc([128, 1], F32) for _ in range(NS4)]; mnew = [alloc([128, 1], F32) for _ in range(NS4)]
        negm = [alloc([128, 1], F32) for _ in range(NS4)]; alp = [alloc([128, 1], F32) for _ in range(NS4)]
        rsum = [alloc([128, 1], F32) for _ in range(NS4)]
        dm_ = [alloc([128, 1], F32) for _ in range(NS4)]
        hm1 = [alloc([128, 16, 32], F32, "hm1_%d" % i) for i in range(NB)]; hm2 = [alloc([128, 16, 32], F32, "hm2_%d" % i) for i in range(NB)]
        sre = [alloc([128, 32], F32) for _ in range(2)]
        xs8 = [alloc([128, 16], F32) for _ in range(4)]

        mset(Xpp, 0.0); mset(Xown, 0.0)
        mset(m_run, NEG); mset(l_run, 0.0); mset(Oacc, 0.0)
        for kp in range(2):
            mset(kpeb[kp], 0.0)

        def hist_stages(t, kbuf, j, kb):
            b = t % NB
            bA = banks[0]; bB = banks[1]
            xc = Xpp[:, t % 2, :]; xn = Xpp[:, (t + 1) % 2, :]

            def sA():
                ld(xt[b], xp[t * 128:(t + 1) * 128, :])
                ld(rc[b], ropec_p[t * 128:(t + 1) * 128, :]); ld(rs[b], ropes_p[t * 128:(t + 1) * 128, :])

            def sB():
                for k in range(4):
                    tr(bA[:, k * 128:(k + 1) * 128], xt[b][:, k * 128:(k + 1) * 128], ident)
                for k in range(4):
                    tr(bB[:, k * 128:(k + 1) * 128], xt[b][:, (4 + k) * 128:(5 + k) * 128], ident)

            def sC():
                acopy(xT[b][:, 0:4, :], bA.re("p (k n) -> p k n", k=4))
                vcopy(xT[b][:, 4:8, :], bB.re("p (k n) -> p k n", k=4))

            def sD():
                for k in range(8):
                    mm(bA, xT[b][:, k, :], W_in_u[:, k, :], start=(k == 0), stop=(k == 7))
                for k in range(8):
                    mm(bB[:, 0:320], xT[b][:, k, :], W_in_kv[:, k, :], start=(k == 0), stop=(k == 7))

            def sE():
                acopy(utok[b], bA)
                acopy(kvf[b], bB[:, 0:320])

            def sF():
                act(sq, kvf[b][:, 0:256], AF.Square, accum=ss1[b])
                act(ss1[b], ss1[b], AF.Ln, scale=1.0 / 256, bias=epsc)
                act(ss1[b], ss1[b], AF.Exp, scale=-0.5)
                for gp in range(16):
                    mm(bA[:, 32 * gp:32 * gp + 32], LrT[:, gp, :], utok[b][:, 32 * gp:32 * gp + 32])
                for gp in range(16):
                    mm(bB[:, 32 * gp:32 * gp + 32], LiT[:, gp, :], utok[b][:, 32 * gp:32 * gp + 32])

            def sG():
                acopy(prs[b].re("p a b -> p (a b)"), bA)
                acopy(pis[b].re("p a b -> p (a b)"), bB)
                tt(ckvf[b], kvf[b][:, 0:256], gkv, ALU.mult, eng="pool")
                ts(ckvf[b], ckvf[b], ss1[b], 1.0, ALU.mult, ALU.mult, eng="pool")
                rope(kpef[b].re("p (a b) -> p a b", a=1), kvf[b][:, 256:320].re("p (a b) -> p a b", a=1), rc[b], rs[b],
                     tmp64[b].re("p (a b) -> p a b", a=1), 1, eng="pool")
                vcopy(ckvb[b], ckvf[b], eng="pool")
                vcopy(kpeb[b][:, 0:64], kpef[b], eng="pool")

            def sH():
                bb = bA.cast(BF16)
                for kc in range(2):
                    tr(bb[:, kc * 128:(kc + 1) * 128], ckvb[b][:, kc * 128:(kc + 1) * 128], identb)
                tr(bb[:, 256:384], kpeb[b], identb)
                tt(hm1[b], prs[b], BRb, ALU.mult, eng="pool"); tt(hm2[b], pis[b], BIb, ALU.mult, eng="pool")
                tt(hm1[b], hm1[b], hm2[b], ALU.subtract, eng="pool")
                tt(hm2[b], pis[b], BRb, ALU.mult, eng="pool"); tt(prs[b], prs[b], BIb, ALU.mult, eng="pool")
                tt(hm2[b], hm2[b], prs[b], ALU.add, eng="pool")

            def sI():
                bb = bA.cast(BF16)
                acopy(ckvT[b].re("p a b -> p (a b)"), bb[:, 0:256])
                acopy(KR[kbuf][0:64, j * 128:(j + 1) * 128], bb[0:64, 256:384])

            def sJ():
                for h in range(4):
                    for kc in range(2):
                        mm(bB[:, h * 128:(h + 1) * 128], W_kv[:, kc, h, 0:128], ckvT[b][:, kc, :], start=(kc == 0), stop=(kc == 1))
                for kc in range(2):
                    mm(bA, ckvT[b][:, kc, :], W_kv[:, kc, :, 128:256], start=(kc == 0), stop=(kc == 1))
                lr_ = lam128[:, 0, :]; li_ = lam128[:, 1, :]
                ld(ckv_o[t * 128:(t + 1) * 128, :], ckvf[b])
                ld(kpe_o[t * 128:(t + 1) * 128, :], kpef[b])
                red(sre[t % 2][:, 0:16], hm1[b], ALU.add)
                red(sre[t % 2][:, 16:32], hm2[b], ALU.add)
                stt(Xown[:, kb, :], xc, ohot[:, j:j + 1], Xown[:, kb, :], ALU.mult, ALU.add)
                tt(xs8[0], xc[:, 0:16], lr_, ALU.mult, eng="pool"); tt(xs8[1], xc[:, 16:32], li_, ALU.mult, eng="pool")
                tt(xs8[2], xc[:, 16:32], lr_, ALU.mult, eng="pool"); tt(xs8[3], xc[:, 0:16], li_, ALU.mult, eng="pool")
                tt(xs8[0], xs8[0], xs8[1], ALU.subtract, eng="pool"); tt(xs8[2], xs8[2], xs8[3], ALU.add, eng="pool")
                tt(xn[:, 0:16], xs8[0], sre[t % 2][:, 0:16], ALU.add, eng="pool")
                tt(xn[:, 16:32], xs8[2], sre[t % 2][:, 16:32], ALU.add, eng="pool")

            def sK():
                acopy(KT[kbuf][:, :, j * 128:(j + 1) * 128], bB.re("p (h n) -> p h n", h=4))
                vcopy(Vb[kbuf][:, j, :], bA)

            def pair(p_, c_):
                def f():
                    p_(); c_()
                return f
            return [sA, pair(sB, sC), pair(sD, sE), pair(sF, sG), pair(sH, sI), pair(sJ, sK)]

        def hist_items(kb):
            items = []
            for jp in range(4):
                sa = hist_stages(kb * 8 + 2 * jp, kb % 2, 2 * jp, kb)
                sb_ = hist_stages(kb * 8 + 2 * jp + 1, kb % 2, 2 * jp + 1, kb)
                for a_, b_ in zip(sa, sb_):
                    items.append(a_); items.append(b_)
            return items

        SCp = [dbl[3], dbl[2]]
        ptb = [banks[2], banks[3]]

        def att_A(n, i, h, kbuf):
            sc = SCp[n % 2]
            for c2 in range(2):
                mm(sc[:, c2 * 512:(c2 + 1) * 512], QTn[:, i, h, :], KT[kbuf][:, h, c2 * 512:(c2 + 1) * 512], start=True, stop=False)
                mm(sc[:, c2 * 512:(c2 + 1) * 512], QTr[0:64, i, h, :], KR[kbuf][0:64, c2 * 512:(c2 + 1) * 512], start=False, stop=True)

        def att_B(n, i, h, kbuf, diag):
            u2 = n % 2; u4 = n % NS4
            sc = SCp[u2]
            col = i * 4 + h
            if diag:
                tt(sc, sc, maskd, ALU.add)
            red(mx[u4], sc, ALU.max)
            tt(mnew[u4], mx[u4], m_run[:, col:col + 1], ALU.max)
            tt(dm_[u4], m_run[:, col:col + 1], mnew[u4], ALU.subtract)
            vcopy(m_run[:, col:col + 1], mnew[u4])
            ts(negm[u4], mnew[u4], -1.0, None, ALU.mult)
            act(alp[u4], dm_[u4], AF.Exp)
            act(Pb[u2], sc, AF.Exp, bias=negm[u4], accum=rsum[u4])

        def att_CD(n, i, h, kbuf):
            u2 = n % 2
            pt_b = ptb[u2].cast(BF16)
            for kt in range(8):
                tr(pt_b[:, kt * 128:(kt + 1) * 128], Pb[u2][:, kt * 128:(kt + 1) * 128], identb)
            if n % 3 == 0:
                vcopy(PT[u2].re("p a b -> p (a b)"), pt_b)
            else:
                acopy(PT[u2].re("p a b -> p (a b)"), pt_b)

        def att_EF(n, i, h, kbuf):
            u2 = n % 2; u4 = n % NS4
            ov = ptb[u2][:, 0:128]
            for kt in range(8):
                mm(ov, PT[u2][:, kt, :], Vb[kbuf][:, kt, h * 128:(h + 1) * 128], start=(kt == 0), stop=(kt == 7))
            stt(Oacc[:, i, h, :], Oacc[:, i, h, :], alp[u4], ov, ALU.mult, ALU.add)
            col = i * 4 + h
            stt(l_run[:, col:col + 1], l_run[:, col:col + 1], alp[u4], rsum[u4], ALU.mult, ALU.add)

        for it in hist_items(0):
            it()
        nun = 0
        for kb in range(16):
            kbuf = kb % 2
            units = [(i, h) for i in range(kb, 16) for h in range(4)]
            U = len(units)
            items = hist_items(kb + 1) if kb + 1 < 16 else []
            per = (len(items) + U - 1) // U if items else 0
            ip = 0
            for q_ in range(U + 3):
                if q_ < U:
                    att_A(nun + q_, units[q_][0], units[q_][1], kbuf)
                if 0 <= q_ - 1 < U:
                    att_B(nun + q_ - 1, units[q_ - 1][0], units[q_ - 1][1], kbuf, units[q_ - 1][0] == kb)
                if 0 <= q_ - 2 < U:
                    att_CD(nun + q_ - 2, units[q_ - 2][0], units[q_ - 2][1], kbuf)
                if 0 <= q_ - 3 < U:
                    att_EF(nun + q_ - 3, units[q_ - 3][0], units[q_ - 3][1], kbuf)
                for _ in range(per):
                    if ip < len(items):
                        items[ip](); ip += 1
            while ip < len(items):
                items[ip](); ip += 1
            nun += U

        ld(ssmp_o, Xpp[:, NTP % 2, :])

        barrier()
        bump[0] = mark_after_q

        W_kv = alloc([128, 2, 4, 256], BF16, "W_kv2")
        ldw(W_kv.re("p k h e -> p k (h e)"), w_kv_d.re("(kt p) n -> p kt n", p=128))
        gkv = alloc([128, 256], F32, "gkv2"); ld(gkv, gkv_d)
        xt_s = alloc([128, D], F32, "sxt"); xT_s = alloc([128, 8, 128], BF16, "sxT")
        rc_s = alloc([128, 64], F32); rs_s = alloc([128, 64], F32)
        ckvf_s = alloc([128, 256], F32); kpef_s = alloc([128, 64], F32); tmp64_s = alloc([128, 64], F32)
        sq = alloc([128, 512], F32, "ssq"); ss_s = alloc([128, 1], F32)
        ckvb_s = alloc([128, 256], BF16); kpeb_s = alloc([128, 128], BF16)
        ckvTn = alloc([128, 2, 128], BF16, "ckvTn"); kpeTn = alloc([128, 128], BF16, "kpeTn")
        KTn = alloc([128, 4, 128], BF16, "KTn")
        cat = [alloc([128, 256], F32, "cat%d" % i) for i in range(2)]
        catb = [alloc([128, 256], BF16) for _ in range(2)]
        cpt = [alloc([128, 64], F32, "cpt%d" % i) for i in range(2)]
        cptb = [alloc([128, 128], BF16) for _ in range(2)]
        ckvTc = alloc([128, 2, 1024], BF16, "ckvTc"); kpeTc = alloc([128, 1024], BF16, "kpeTc")
        KTc = alloc([128, 4, 1024], BF16, "KTc"); Vc = alloc([128, 8, 512], BF16, "Vc"); Vn = alloc([128, 512], BF16, "Vn")
        scs = alloc([128, 1056], F32, "scs")
        Ps = alloc([128, 1152], BF16, "Ps"); PTs = alloc([128, 9, 32], BF16, "PTs")
        mxs = alloc([128, 1], F32); negs = alloc([128, 1], F32); sums = alloc([128, 1], F32)
        osb = alloc([128, 512], F32, "osb")
        isam = 16
        load_xT(xo[isam * 128:(isam + 1) * 128, :], xt_s, xT_s, banks[0])
        ld(rc_s, ropec_o[isam * 128:(isam + 1) * 128, :]); ld(rs_s, ropes_o[isam * 128:(isam + 1) * 128, :])
        for k in range(8):
            mm(banks[2][:, 0:320], xT_s[:, k, :], W_in_kv[:, k, :], start=(k == 0), stop=(k == 7))
        rmsnorm(ckvf_s, banks[2][:, 0:256], 256, gkv, sq[:, 0:256], ss_s)
        ld(ckvs_o, ckvf_s)
        rope(kpef_s.re("p (a b) -> p a b", a=1), banks[2][:, 256:320].re("p (a b) -> p a b", a=1), rc_s, rs_s, tmp64_s.re("p (a b) -> p a b", a=1), 1)
        ld(kpes_o, kpef_s)
        vcopy(ckvb_s, ckvf_s)
        mset(kpeb_s, 0.0)
        vcopy(kpeb_s[:, 0:64], kpef_s)
        bb = banks[3].cast(BF16)
        for kc in range(2):
            tr(bb[:, kc * 128:(kc + 1) * 128], ckvb_s[:, kc * 128:(kc + 1) * 128], identb)
        tr(bb[:, 256:384], kpeb_s, identb)
        acopy(ckvTn.re("p a b -> p (a b)"), bb[:, 0:256])
        acopy(kpeTn[0:64, :], bb[0:64, 256:384])
        for h in range(4):
            for kc in range(2):
                mm(banks[3][:, h * 128:(h + 1) * 128], W_kv[:, kc, h, 0:128], ckvTn[:, kc, :], start=(kc == 0), stop=(kc == 1))
        acopy(KTn.re("p a b -> p (a b)"), banks[3])
        for kp in range(2):
            mset(cptb[kp], 0.0)
        for s_ in range(4):
            for kt in range(8):
                b = kt % 2
                r0 = s_ * 1024 + kt * 128
                ld(cat[b], cckv_d[r0:r0 + 128, :]); ld(cpt[b], ckpe_d[r0:r0 + 128, :])
                vcopy(catb[b], cat[b], eng="pool"); vcopy(cptb[b][:, 0:64], cpt[b], eng="pool")
                bb = banks[1].cast(BF16)
                for kc in range(2):
                    tr(bb[:, kc * 128:(kc + 1) * 128], catb[b][:, kc * 128:(kc + 1) * 128], identb)
                tr(bb[:, 256:384], cptb[b], identb)
                acopy(ckvTc[:, :, kt * 128:(kt + 1) * 128], bb[:, 0:256].re("p (a b) -> p a b", a=2))
                acopy(kpeTc[0:64, kt * 128:(kt + 1) * 128], bb[0:64, 256:384])
            for h in range(4):
                for n in range(2):
                    for kc in range(2):
                        mm(banks[2], W_kv[:, kc, h, 0:128], ckvTc[:, kc, n * 512:(n + 1) * 512], start=(kc == 0), stop=(kc == 1))
                    acopy(KTc[:, h, n * 512:(n + 1) * 512], banks[2])
            for kt in range(8):
                for kc in range(2):
                    mm(banks[3], ckvTc[:, kc, kt * 128:(kt + 1) * 128], W_kv[:, kc, :, 128:256], start=(kc == 0), stop=(kc == 1))
                vcopy(Vc[:, kt, :], banks[3])
            for kc in range(2):
                mm(banks[3][0:32, :], ckvTn[:, kc, s_ * 32:(s_ + 1) * 32], W_kv[:, kc, :, 128:256], start=(kc == 0), stop=(kc == 1))
            vcopy(Vn[0:32, :], banks[3][0:32, :])
            qs = slice(s_ * 32, (s_ + 1) * 32)
            for h in range(4):
                for n in range(2):
                    mm(SC[0:32, n * 512:(n + 1) * 512], QTn[:, isam, h, qs], KTc[:, h, n * 512:(n + 1) * 512], start=True, stop=False)
                    mm(SC[0:32, n * 512:(n + 1) * 512], QTr[0:64, isam, h, qs], kpeTc[0:64, n * 512:(n + 1) * 512], start=False, stop=True)
                mm(banks[4][0:32, 0:32], QTn[:, isam, h, qs], KTn[:, h, qs], start=True, stop=False)
                mm(banks[4][0:32, 0:32], QTr[0:64, isam, h, qs], kpeTn[0:64, qs], start=False, stop=True)
                acopy(scs[0:32, 0:1024], SC[0:32, :])
                acopy(scs[0:32, 1024:1056], banks[4][0:32, 0:32])
                red(mxs[0:32, :], scs[0:32, :], ALU.max)
                ts(negs[0:32, :], mxs[0:32, :], -1.0, None, ALU.mult)
                mset(Ps[0:32, 1024:1152], 0.0)
                act(Ps[0:32, 0:1056], scs[0:32, :], AF.Exp, bias=negs[0:32, :], accum=sums[0:32, :])
                pt_b = banks[5].cast(BF16)
                for kt in range(9):
                    tr(pt_b[:, kt * 32:(kt + 1) * 32], Ps[0:32, kt * 128:(kt + 1) * 128], identb[0:32, 0:32])
                vcopy(PTs.re("p a b -> p (a b)"), pt_b[:, 0:288])
                ov = banks[4][0:32, 128:256]
                for kt in range(8):
                    mm(ov, PTs[:, kt, :], Vc[:, kt, h * 128:(h + 1) * 128], start=(kt == 0), stop=False)
                mm(ov, PTs[0:32, 8, :], Vn[0:32, h * 128:(h + 1) * 128], start=False, stop=True)
                recip(sums[0:32, :], sums[0:32, :])
                ts(osb[0:32, h * 128:(h + 1) * 128], ov, sums[0:32, :], None, ALU.mult)
            ld(Osam[s_ * 32:(s_ + 1) * 32, :], osb[0:32, :])

        barrier()
        bump[0] = mark_after_mixer_state

        W_glu = alloc([128, 4, D], BF16, "W_glu"); W_o = alloc([128, 8, D], BF16, "W_o")
        ldw(W_glu, w_glu_d.re("(kt p) n -> p kt n", p=128)); ldw(W_o, w_o_d.re("(kt p) n -> p kt n", p=128))
        gos = alloc([128, 512], F32, "gos"); gom = alloc([128, 512], F32, "gom"); ld(gos, gos_d); ld(gom, gom_d)
        lng = alloc([128, D], F32, "lng"); lnb = alloc([128, D], F32, "lnb")
        ld(lng, lng_d[:, 0:D]); ld(lnb, lnb_d[:, 0:D])
        btab2 = Buf("tab2")
        for nm_ in ("S_t", "C_t", "R_t"):
            v_ = alloc([128, 16, 128], F32, nm_ + "2"); v_.b = btab2
            ld(v_.re("p a b -> p (a b)"), tab_d[nm_]); ssmx[nm_] = v_
        for nm_ in ("WBre", "WBim"):
            v_ = alloc([128, 16, 128], BF16, nm_ + "2")
            ld(v_.re("p a b -> p (a b)"), tab_d[nm_]); ssmx[nm_] = v_
        for nm_, src_ in (("Cre", cre_d), ("Cimn", cimn_d), ("Dblk", dblk_d)):
            v_ = alloc([128, 16, 32], BF16, nm_)
            ldw(v_.re("p a b -> p (a b)"), src_); ssmx[nm_] = v_
        ssmx["m4"] = [[alloc([128, 256], F32) for _ in range(4)] for _ in range(2)]
        ssmx["wri"] = [[alloc([128, 256], F32) for _ in range(2)] for _ in range(2)]
        ssmx["zri"] = [[alloc([128, 2, 128], F32) for _ in range(2)] for _ in range(2)]
        ssmx["xs4"] = [alloc([128, 16], F32) for _ in range(4)]
        ssmx["zl_all"] = [alloc([128, 4, 16], F32) for _ in range(2)]
        ssmx["dmd"] = [[alloc([128, 256], F32) for _ in range(4)]] * 2
        ssmx["xrb"] = [alloc([128, 2, 128], BF16) for _ in range(2)]
        ssmx["xib"] = [alloc([128, 2, 128], BF16) for _ in range(2)]
        S0 = alloc([128, 2, 4, 16], F32, "S0"); ld(S0.re("p a b c -> p (a b c)"), s0_d)
        Sfin = alloc([128, 2, 4, 16], F32, "Sfin")
        xt = [alloc([128, D], F32, "axt%d" % i) for i in range(NB)]
        xT = [alloc([128, 8, 128], BF16, "axT%d" % i) for i in range(NB)]
        uT = [alloc([128, 4, 128], BF16, "auT%d" % i) for i in range(NB)]
        ysq = alloc([128, 512], F32, "ysq"); yt = alloc([128, 512], F32, "yt"); ysg = alloc([128, 512], F32, "ysg")
        glb = alloc([128, 512], BF16, "glb"); gT = alloc([128, 4, 128], BF16, "gT")
        sg2 = ysq; osf = yt
        mixb = alloc([128, D], BF16, "mixb"); mixT = alloc([128, 8, 128], BF16, "mixT")
        rl = alloc([128, 4], F32, "rl"); omf = alloc([128, 4, 128], F32, "omf")
        ss_a = alloc([128, 1], F32); sq = ysg
        rres = alloc([128, D], F32, "rres"); hout = alloc([128, D], F32, "hout")
        stats = alloc([128, 12], F32); mvv = alloc([128, 2], F32)

        def pre_stages(i):
            b = i % NB
            yb = banks[3] if i % 2 == 0 else banks[0]

            def p0():
                load_xT(xo[i * 128:(i + 1) * 128, :], xt[b], xT[b], banks[1])
                for kt in range(4):
                    for k in range(8):
                        mm(banks[1][:, kt * 128:(kt + 1) * 128], W_in_u[:, k, kt * 128:(kt + 1) * 128], xT[b][:, k, :], start=(k == 0), stop=(k == 7))
                acopy(uT[b].re("p a b -> p (a b)"), banks[1])

            if i < 16:
                rr_ = ssm_rounds(uT[b], (banks[4], banks[5]), lambda gp, s_, part, i=i: Xown[:, i, part * 16 + gp:part * 16 + gp + 1], 1, ybank=yb)
            else:
                rr_ = ssm_rounds(uT[b], (banks[4], banks[5]), lambda gp, s_, part: S0[:, part, s_, gp:gp + 1], 4, ybank=yb,
                                 last_out=lambda s_: (Sfin[:, 0, s_, :], Sfin[:, 1, s_, :]))
                rr_.append(lambda: ld(ssms_o, Sfin.re("p a b c -> p (a b c)")))
            return [p0] + rr_

        def post_stages(i):
            b = i % NB
            yb = banks[3] if i % 2 == 0 else banks[0]
            xt_ = xt[b]

            def q0():
                act(ysq, yb, AF.Square)
                ts(yt, ysq, 0.044715, 1.0, ALU.mult, ALU.add)
                tt(yt, yt, yb, ALU.mult)

            def q1():
                act(ysg, yt, AF.Sigmoid, scale=1.5957691216057308)
                tt(glb, ysg, yb, ALU.mult)

            def q2():
                transpose_to(gT, glb, 4, banks[2], eng="act")

            def q3():
                for nb in range(2):
                    for k in range(4):
                        mm(SC[:, nb * 512:(nb + 1) * 512], gT[:, k, :], W_glu[:, k, nb * 512:(nb + 1) * 512], start=(k == 0), stop=(k == 3))

            def q4():
                act(sg2, SC[:, 512:1024], AF.Sigmoid)
                tt(osf, sg2, SC[:, 0:512], ALU.mult)

            def q5():
                rmsnorm(mixb[:, 0:512], osf, 512, gos, sq, ss_a)

            def q6():
                if i < 16:
                    recip(rl, l_run[:, i * 4:(i + 1) * 4])
                    tt(omf, Oacc[:, i, :, :], rl.un(2).bc([128, 4, 128]), ALU.mult)
                    rmsnorm(mixb[:, 512:1024], omf.re("p a b -> p (a b)"), 512, gom, sq, ss_a)
                else:
                    rmsnorm(mixb[:, 512:1024], Osam, 512, gom, sq, ss_a)

            def q7():
                for half in range(2):
                    transpose_to(mixT[:, half * 4:(half + 1) * 4, :], mixb[:, half * 512:(half + 1) * 512], 4, banks[2], eng=("act" if half == 0 else "dve"))

            def q8():
                for nb in range(2):
                    for k in range(8):
                        mm(SC[:, nb * 512:(nb + 1) * 512], mixT[:, k, :], W_o[:, k, nb * 512:(nb + 1) * 512], start=(k == 0), stop=(k == 7))

            def q9():
                stt(rres, xt_, ALPHA, SC, ALU.mult, ALU.add)
                layernorm(hout, rres, lng, lnb, stats, mvv)
                ld(h1_d[i * 128:(i + 1) * 128, :], hout)

            return [q0, q1, q2, q3, q4, q5, q6, q7, q8, q9]

        prev_post = []
        for i in range(NOWN + 1):
            pre = pre_stages(i) if i < NOWN else []
            n_ = max(len(pre), len(prev_post))
            for k_ in range(n_):
                if k_ < len(pre):
                    pre[k_]()
                if k_ < len(prev_post):
                    prev_post[k_]()
            prev_post = post_stages(i) if i < NOWN else []

        barrier()
        bump[0] = mark_after_mem

        W_xq = alloc([128, 8, D], BF16, "W_xq"); W_xo = alloc([128, 8, D], BF16, "W_xo")
        ldw(W_xq, w_xq_d.re("(kt p) n -> p kt n", p=128)); ldw(W_xo, w_xo_d.re("(kt p) n -> p kt n", p=128))
        lng = alloc([128, D], F32, "lng2"); lnb = alloc([128, D], F32, "lnb2")
        ld(lng, lng_d[:, D:2 * D]); ld(lnb, lnb_d[:, D:2 * D])
        hin = [alloc([128, D], F32, "bh%d" % i) for i in range(NB)]
        hb2 = [alloc([128, D], BF16, "bhb%d" % i) for i in range(2)]; hT2 = [alloc([128, 8, 128], BF16, "bhT%d" % i) for i in range(2)]
        qxb2 = [alloc([128, D], BF16, "qxb%d" % i) for i in range(2)]; qxT2 = [alloc([128, 8, 128], BF16, "qxT%d" % i) for i in range(2)]
        mx42 = [alloc([128, 4], F32) for _ in range(2)]; neg42 = [alloc([128, 4], F32) for _ in range(2)]; sum42 = [alloc([128, 4], F32) for _ in range(2)]
        Px2 = [alloc([128, 4, 256], BF16, "Px%d" % i) for i in range(2)]; PxT2 = [alloc([128, 8, 128], BF16, "PxT%d" % i) for i in range(2)]
        oxb2 = [alloc([128, D], BF16, "oxb%d" % i) for i in range(2)]; oxT2 = [alloc([128, 8, 128], BF16, "oxT%d" % i) for i in range(2)]
        rres2 = [alloc([128, D], F32, "brres%d" % i) for i in range(2)]; hout2 = [alloc([128, D], F32, "bhout%d" % i) for i in range(2)]
        stats2 = [alloc([128, 12], F32) for _ in range(2)]; mvv2 = [alloc([128, 2], F32) for _ in range(2)]
        hb_ = hb2[0]; hT = hT2[0]; qxb = qxb2[0]; qxT = qxT2[0]; mx4 = mx42[0]; neg4 = neg42[0]; sum4 = sum42[0]
        Px = Px2[0]; PxT = PxT2[0]; oxb = oxb2[0]; oxT = oxT2[0]; rres = rres2[0]; hout = hout2[0]; stats = stats2[0]; mvv = mvv2[0]
        cmk = [alloc([128, D], F32, "cmk%d" % i) for i in range(2)]
        cmkb = [alloc([128, D], BF16, "cmkb%d" % i) for i in range(1)]
        mkTs4 = [alloc([128, 4, 2, 256], BF16, "mkTs%d" % i) for i in range(4)]; mvs4 = [alloc([128, 2, D], BF16, "mvs%d" % i) for i in range(4)]
        scx = alloc([128, 8], F32, "scx"); oxs = alloc([128, D], BF16, "oxs")

        def xa_stages(i, p):
            A_ = banks[p]; C_ = dbl[1 + p]
            Ab = A_.cast(BF16)

            def tr8(src):
                for k in range(8):
                    tr(Ab[:, k * 128:(k + 1) * 128], src[:, k * 128:(k + 1) * 128], identb)

            def ev8(dst):
                acopy(dst[:, 0:4, :], Ab[:, 0:512].re("p (k n) -> p k n", k=4))
                vcopy(dst[:, 4:8, :], Ab[:, 512:1024].re("p (k n) -> p k n", k=4))

            def t0():
                ld(hin[p], h1_d[i * 128:(i + 1) * 128, :])
                vcopy(hb2[p], hin[p], eng="pool")

            def t3():
                for nb in range(2):
                    for k in range(8):
                        mm(C_[:, nb * 512:(nb + 1) * 512], hT2[p][:, k, :], W_xq[:, k, nb * 512:(nb + 1) * 512], start=(k == 0), stop=(k == 7))

            def t4():
                for nb in range(2):
                    P.op("act", lambda e, nb=nb: e.mul(out=qxb2[p].ap[:, nb * 512:(nb + 1) * 512], in_=C_.ap[:, nb * 512:(nb + 1) * 512], mul=X_SCALE),
                         reads=[C_.b], writes=[qxb2[p].b])

            def t7():
                for h in range(4):
                    for et in range(2):
                        mm(C_[:, h * 256:(h + 1) * 256], qxT2[p][:, h * 2 + et, :], mkT[:, h, et, :], start=(et == 0), stop=(et == 1))

            def t8():
                red(mx42[p], C_.re("p (h m) -> p h m", h=4), ALU.max)
                ts(neg42[p], mx42[p], -1.0, None, ALU.mult)
                for h in range(4):
                    act(Px2[p][:, h, :], C_[:, h * 256:(h + 1) * 256], AF.Exp, bias=neg42[p][:, h:h + 1], accum=sum42[p][:, h:h + 1])

            def t11():
                for h in range(4):
                    for mt in range(2):
                        mm(C_[:, h * 256:(h + 1) * 256], PxT2[p][:, h * 2 + mt, :], mvb[:, mt, h * 256:(h + 1) * 256], start=(mt == 0), stop=(mt == 1))

            def t12():
                recip(sum42[p], sum42[p])
                tt(oxb2[p].re("p (h e) -> p h e", h=4), C_.re("p (h e) -> p h e", h=4), sum42[p].un(2).bc([128, 4, 256]), ALU.mult)

            def t15():
                for nb in range(2):
                    for k in range(8):
                        mm(C_[:, nb * 512:(nb + 1) * 512], oxT2[p][:, k, :], W_xo[:, k, nb * 512:(nb + 1) * 512], start=(k == 0), stop=(k == 7))

            def t16():
                stt(rres2[p], hin[p], ALPHA, C_, ALU.mult, ALU.add)
                layernorm(hout2[p], rres2[p], lng, lnb, stats2[p], mvv2[p])
                ld(h2_d[i * 128:(i + 1) * 128, :], hout2[p])

            return [t0, lambda: tr8(hb2[p]), lambda: ev8(hT2[p]), t3, t4, lambda: tr8(qxb2[p]), lambda: ev8(qxT2[p]), t7, t8,
                    lambda: tr8(Px2[p].re("p a b -> p (a b)")), lambda: ev8(PxT2[p]), t11, t12, lambda: tr8(oxb2[p]), lambda: ev8(oxT2[p]), t15, t16]

        def prep_items():
            items = []
            for s_ in range(4):
                for mt in range(2):
                    r0 = s_ * 256 + mt * 128

                    def l0(r0=r0):
                        ld(cmk[0], cmk_d[r0:r0 + 128, :]); ld(cmk[1], cmv_d[r0:r0 + 128, :])

                    def l1(s_=s_, mt=mt):
                        vcopy(cmkb[0], cmk[0], eng="pool")
                        vcopy(mvs4[s_][:, mt, :], cmk[1], eng="pool")

                    def l2(s_=s_, mt=mt):
                        for half in range(2):
                            bb = banks[6 + half].cast(BF16)
                            for k in range(4):
                                kk = half * 4 + k
                                tr(bb[:, k * 128:(k + 1) * 128], cmkb[0][:, kk * 128:(kk + 1) * 128], identb)

                    def l3(s_=s_, mt=mt):
                        for half in range(2):
                            bb = banks[6 + half].cast(BF16)
                            acopy(mkTs4[s_][:, half * 2:half * 2 + 2, :, mt * 128:(mt + 1) * 128].re("p h e m -> p (h e) m"), bb[:, 0:512].re("p (k m) -> p k m", k=4))

                    items += [l0, l1, l2, l3]
            return items

        pitems = prep_items()
        pi_ = 0
        slot = 0
        for ip in range(8):
            sa = xa_stages(2 * ip, 0); sb_ = xa_stages(2 * ip + 1, 1)
            for a_, b_ in zip(sa, sb_):
                a_(); b_()
                slot += 1
                if slot % 4 == 0 and pi_ < len(pitems):
                    pitems[pi_](); pi_ += 1
        while pi_ < len(pitems):
            pitems[pi_](); pi_ += 1

        for i in range(16, NOWN):
            b = i % NB
            ld(hin[b], h1_d[i * 128:(i + 1) * 128, :])
            vcopy(hb_, hin[b], eng="pool")
            for half in range(2):
                transpose_to(hT[:, half * 4:(half + 1) * 4, :], hb_[:, half * 512:(half + 1) * 512], 4, banks[0], eng=("act" if half == 0 else "dve"))
            for nb in range(2):
                for k in range(8):
                    mm(banks[1 + nb], hT[:, k, :], W_xq[:, k, nb * 512:(nb + 1) * 512], start=(k == 0), stop=(k == 7))
                P.op("act", lambda e, nb=nb: e.mul(out=qxb.ap[:, nb * 512:(nb + 1) * 512], in_=banks[1 + nb].ap, mul=X_SCALE), reads=[banks[1 + nb].b], writes=[qxb.b])
            for half in range(2):
                transpose_to(qxT[:, half * 4:(half + 1) * 4, :], qxb[:, half * 512:(half + 1) * 512], 4, banks[3], eng=("act" if half == 0 else "dve"))
            if i < 16:
                for h in range(4):
                    for et in range(2):
                        mm(SC[:, h * 256:(h + 1) * 256], qxT[:, h * 2 + et, :], mkT[:, h, et, :], start=(et == 0), stop=(et == 1))
                red(mx4, SC.re("p (h m) -> p h m", h=4), ALU.max)
                ts(neg4, mx4, -1.0, None, ALU.mult)
                for h in range(4):
                    act(Px[:, h, :], SC[:, h * 256:(h + 1) * 256], AF.Exp, bias=neg4[:, h:h + 1], accum=sum4[:, h:h + 1])
                for half in range(2):
                    transpose_to(PxT[:, half * 4:(half + 1) * 4, :], Px.re("p a b -> p (a b)")[:, half * 512:(half + 1) * 512], 4, banks[4], eng=("act" if half == 0 else "dve"))
                for h in range(4):
                    for mt in range(2):
                        mm(SC[:, h * 256:(h + 1) * 256], PxT[:, h * 2 + mt, :], mvb[:, mt, h * 256:(h + 1) * 256], start=(mt == 0), stop=(mt == 1))
                recip(sum4, sum4)
                tt(oxb.re("p (h e) -> p h e", h=4), SC.re("p (h e) -> p h e", h=4), sum4.un(2).bc([128, 4, 256]), ALU.mult)
            else:
                for s_ in range(4):
                    qs = slice(s_ * 32, (s_ + 1) * 32)
                    mkTs = mkTs4[s_]; mvs = mvs4[s_]
                    for h in range(4):
                        for et in range(2):
                            mm(SC[0:32, h * 256:(h + 1) * 256], qxT[:, h * 2 + et, qs], mkTs[:, h, et, :], start=(et == 0), stop=(et == 1))
                    red(mx4[0:32, :], SC[0:32, :].re("p (h m) -> p h m", h=4), ALU.max)
                    ts(neg4[0:32, :], mx4[0:32, :], -1.0, None, ALU.mult)
                    for h in range(4):
                        act(Px[0:32, h, :], SC[0:32, h * 256:(h + 1) * 256], AF.Exp, bias=neg4[0:32, h:h + 1], accum=sum4[0:32, h:h + 1])
                    bb = banks[4].cast(BF16)
                    for k in range(8):
                        tr(bb[:, k * 32:(k + 1) * 32], Px.re("p a b -> p (a b)")[0:32, k * 128:(k + 1) * 128], identb[0:32, 0:32])
                    vcopy(PxT[:, :, 0:32], bb[:, 0:256].re("p (k q) -> p k q", k=8))
                    for h in range(4):
                        for mt in range(2):
                            mm(SC[0:32, h * 256:(h + 1) * 256], PxT[:, h * 2 + mt, 0:32], mvs[:, mt, h * 256:(h + 1) * 256], start=(mt == 0), stop=(mt == 1))
                    recip(sum4[0:32, :], sum4[0:32, :])
                    tt(oxs[0:32, :].re("p (h e) -> p h e", h=4), SC[0:32, :].re("p (h e) -> p h e", h=4), sum4[0:32, :].un(2).bc([32, 4, 256]), ALU.mult)
                    ld(oxb[s_ * 32:(s_ + 1) * 32, :], oxs[0:32, :])
            for half in range(2):
                transpose_to(oxT[:, half * 4:(half + 1) * 4, :], oxb[:, half * 512:(half + 1) * 512], 4, banks[5], eng=("act" if half == 0 else "dve"))
            for nb in range(2):
                for k in range(8):
                    mm(banks[1 + nb], oxT[:, k, :], W_xo[:, k, nb * 512:(nb + 1) * 512], start=(k == 0), stop=(k == 7))
            for nb in range(2):
                stt(rres[:, nb * 512:(nb + 1) * 512], hin[b][:, nb * 512:(nb + 1) * 512], ALPHA, banks[1 + nb], ALU.mult, ALU.add)
            layernorm(hout, rres, lng, lnb, stats, mvv)
            ld(h2_d[i * 128:(i + 1) * 128, :], hout)

        barrier()
        bump[0] = mark_after_mem

        W1 = alloc([128, 8, 4096], BF16, "W1"); W2 = alloc([128, 32, D], BF16, "W2")
        for c in range(2):
            ldw(W1[:, :, c * 2048:(c + 1) * 2048], w_ff1_d.re("(kt p) n -> p kt n", p=128)[:, :, c * 2048:(c + 1) * 2048])
        for c in range(4):
            ldw(W2[:, c * 8:(c + 1) * 8, :], w_ff2_d.re("(kt p) n -> p kt n", p=128)[:, c * 8:(c + 1) * 8, :])
        lng = alloc([128, D], F32, "lng3"); lnb = alloc([128, D], F32, "lnb3")
        ld(lng, lng_d[:, 2 * D:3 * D]); ld(lnb, lnb_d[:, 2 * D:3 * D])
        hin = alloc([128, D], F32, "ch")
        hb_ = alloc([128, D], BF16, "chb"); hT = alloc([128, 8, 512], BF16, "chT")
        zr = [alloc([128, 512], F32, "zr%d" % i) for i in range(2)]; zT = alloc([128, 32, 512], BF16, "zT")
        rres = alloc([128, D], F32, "crres"); hout = alloc([128, D], F32, "chout")
        stats = alloc([128, 12], F32); mvv = alloc([128, 2], F32)
        groups = [list(range(g_ * 4, g_ * 4 + 4)) for g_ in range(4)] + [[16]]
        for grp in groups:
            nt = len(grp)
            for q_, i in enumerate(grp):
                ld(hin, h2_d[i * 128:(i + 1) * 128, :])
                vcopy(hb_, hin, eng="pool")
                for half in range(2):
                    transpose_to(hT[:, half * 4:(half + 1) * 4, q_ * 128:(q_ + 1) * 128], hb_[:, half * 512:(half + 1) * 512], 4, banks[0], eng=("act" if half == 0 else "dve"))
            W_ = nt * 128
            for f in range(32):
                bk = banks[1 + f % 2]
                for k in range(8):
                    mm(bk[:, 0:W_], W1[:, k, f * 128:(f + 1) * 128], hT[:, k, 0:W_], start=(k == 0), stop=(k == 7))
                act(zr[f % 2][:, 0:W_], bk[:, 0:W_], AF.Relu)
                tt(zT[:, f, 0:W_], zr[f % 2][:, 0:W_], zr[f % 2][:, 0:W_], ALU.mult)
            for q_, i in enumerate(grp):
                for nb in range(2):
                    for f in range(32):
                        mm(SC[:, nb * 512:(nb + 1) * 512], zT[:, f, q_ * 128:(q_ + 1) * 128], W2[:, f, nb * 512:(nb + 1) * 512], start=(f == 0), stop=(f == 31))
                ld(hin, h2_d[i * 128:(i + 1) * 128, :])
                stt(rres, hin, ALPHA, SC, ALU.mult, ALU.add)
                layernorm(hout, rres, lng, lnb, stats, mvv)
                ld(y_o[i * 128:(i + 1) * 128, :], hout)

        P.emit(st)
    return nc


def _lay_gp(a):
    sh = a.shape[2:]
    n = len(sh)
    return np.ascontiguousarray(a.reshape(16, 2, 64, *sh).transpose(1, 2, 0, *range(3, 3 + n)).reshape(128, 16, *sh))


def _bc(v, n=128):
    return np.ascontiguousarray(np.broadcast_to(np.asarray(v, np.float32).reshape(1, -1), (n, np.asarray(v).size)))


def _rope_tables(pos):
    inv = (10000.0 ** (-np.arange(32, dtype=np.float32) / 32)).astype(np.float32)
    ang = pos.astype(np.float32)[:, None] * inv[None, :]
    c = np.cos(ang).astype(np.float32)
    s = np.sin(ang).astype(np.float32)
    return np.concatenate([c, c], 1), np.concatenate([-s, s], 1)


_NC_CACHE = {}


def kernel(x_prompt, x_sample, mem_prompt, cache_mla_ckv, cache_mla_kpe, state_ssm_re, state_ssm_im,
           cache_mem_k, cache_mem_v, w_in, g_q, w_q_up, g_kv, w_kv_up, a_re, a_im, b_re, b_im, c_re, c_im,
           d_skip, log_dt, w_glu, g_out_ssm, g_out_mla, w_o, w_xq, w_xk, w_xv, w_xo, w_ff1, w_ff2, ln_g, ln_b):
    f = lambda a: np.ascontiguousarray(np.asarray(a, dtype=np.float32))
    x_prompt = f(x_prompt); x_sample = f(x_sample)
    xp = x_prompt[0]
    cre = np.zeros((128, 16, 32), np.float32); cimn = np.zeros((128, 16, 32), np.float32)
    cr = f(c_re)[0].reshape(16, 2, 16, 64); ci = f(c_im)[0].reshape(16, 2, 16, 64)
    for g2 in range(2):
        cre[g2 * 64:(g2 + 1) * 64, :, g2 * 16:(g2 + 1) * 16] = cr[:, g2].transpose(2, 0, 1)
        cimn[g2 * 64:(g2 + 1) * 64, :, g2 * 16:(g2 + 1) * 16] = -ci[:, g2].transpose(2, 0, 1)
    dblk = np.zeros((128, 16, 32), np.float32)
    dd = f(d_skip)[0].reshape(512)
    for gp in range(16):
        for c in range(32):
            ch = gp * 32 + c
            dblk[ch % 128, gp, c] = dd[ch]
    pos_p = np.arange(16384)
    rcp, rsp = _rope_tables(pos_p)
    common = {
        "xp": xp, "mem": f(mem_prompt)[0], "ident": np.eye(128, dtype=np.float32),
        "w_in": f(w_in)[0], "w_q": f(w_q_up)[0].reshape(384, 768), "w_kv": f(w_kv_up)[0].reshape(256, 1024),
        "w_glu": f(w_glu)[0], "w_o": f(w_o)[0], "w_xq": f(w_xq)[0].reshape(D, D), "w_xk": f(w_xk)[0].reshape(D, D),
        "w_xv": f(w_xv)[0].reshape(D, D), "w_xo": f(w_xo)[0].reshape(D, D), "w_ff1": f(w_ff1)[0], "w_ff2": f(w_ff2)[0],
        "gq": _bc(f(g_q)[0]), "gkv": _bc(f(g_kv)[0]), "gos": _bc(f(g_out_ssm)[0]), "gom": _bc(f(g_out_mla)[0]),
        "lng": _bc(f(ln_g)[0].reshape(-1)), "lnb": _bc(f(ln_b)[0].reshape(-1)),
        "ropec_p": rcp, "ropes_p": rsp,
        "ar": _lay_gp(f(a_re)[0]), "ai": _lay_gp(f(a_im)[0]),
        "ldt": _lay_gp(np.ascontiguousarray(np.broadcast_to(f(log_dt)[0][:, None], (32, 64)))),
        "bre": _lay_gp(f(b_re)[0]).reshape(128, 256), "bim": _lay_gp(f(b_im)[0]).reshape(128, 256),
        "jj": _bc(np.arange(1, 129, dtype=np.float32)), "jj2": _bc(127.0 - np.arange(128, dtype=np.float32)),
        "cre": cre.reshape(128, 512), "cimn": cimn.reshape(128, 512), "dblk": dblk.reshape(128, 512),
    }
    qi = np.arange(128)[:, None]
    in_maps = []
    for c in range(NCORES):
        tiles = [8 * i + c for i in range(16)]
        xo = np.concatenate([xp[t * 128:(t + 1) * 128] for t in tiles] + [x_sample[4 * c:4 * c + 4].reshape(128, D)], 0)
        pos_o = np.concatenate([np.arange(t * 128, (t + 1) * 128) for t in tiles] + [np.tile(1024 + np.arange(32), 4)])
        rco, rso = _rope_tables(pos_o)
        kj = np.arange(1024)[None, :]
        vis = ((kj // 128) < c) | (((kj // 128) == c) & (((kj % 128) // 64) <= (qi // 64)))
        maskd = np.where(vis, 0.0, NEG).astype(np.float32)
        onehot = np.zeros((128, 8), np.float32); onehot[:, c] = 1.0
        s0 = np.stack([_lay_gp(f(state_ssm_re)[0, 4 * c + s]) for s in range(4)], 1)
        s0i = np.stack([_lay_gp(f(state_ssm_im)[0, 4 * c + s]) for s in range(4)], 1)
        m = dict(common)
        m.update({
            "xo": np.ascontiguousarray(xo), "ropec_o": rco, "ropes_o": rso, "maskd": maskd, "onehot": onehot,
            "s0": np.ascontiguousarray(np.stack([s0, s0i], 1).reshape(128, 128)),
            "cckv": f(cache_mla_ckv)[0, 4 * c:4 * c + 4].reshape(4096, 256),
            "ckpe": f(cache_mla_kpe)[0, 4 * c:4 * c + 4].reshape(4096, 64),
            "cmk": f(cache_mem_k)[0, 4 * c:4 * c + 4].reshape(1024, D),
            "cmv": f(cache_mem_v)[0, 4 * c:4 * c + 4].reshape(1024, D),
        })
        in_maps.append(m)
    if "nc" not in _NC_CACHE:
        _NC_CACHE["nc"] = build_nc()
    res = run_bass_kernel_spmd(_NC_CACHE["nc"], in_maps, core_ids=list(range(NCORES)))
    R = res.results
    y_p = np.zeros((1, 16384, D), np.float32); y_s = np.zeros((32, 32, D), np.float32)
    ckv_s = np.zeros((1, 32, 32, 256), np.float32); kpe_s = np.zeros((1, 32, 32, 64), np.float32)
    sre_s = np.zeros((1, 32, 32, 64), np.float32); sim_s = np.zeros((1, 32, 32, 64), np.float32)

    def unlay(a):
        return a.reshape(2, 64, 16).transpose(2, 0, 1).reshape(32, 64)

    for c in range(NCORES):
        yo = R[c]["y_o"]
        for i in range(16):
            t = 8 * i + c
            y_p[0, t * 128:(t + 1) * 128] = yo[i * 128:(i + 1) * 128]
        y_s[4 * c:4 * c + 4] = yo[16 * 128:].reshape(4, 32, D)
        ckv_s[0, 4 * c:4 * c + 4] = R[c]["ckvs_o"].reshape(4, 32, 256)
        kpe_s[0, 4 * c:4 * c + 4] = R[c]["kpes_o"].reshape(4, 32, 64)
        sf = R[c]["ssms_o"].reshape(128, 2, 4, 16)
        for s in range(4):
            sre_s[0, 4 * c + s] = unlay(sf[:, 0, s, :])
            sim_s[0, 4 * c + s] = unlay(sf[:, 1, s, :])
    r0 = R[0]
    ckv_p = r0["ckv_o"].reshape(1, 1, 16384, 256); kpe_p = r0["kpe_o"].reshape(1, 1, 16384, 64)
    sp = r0["ssmp_o"]
    sre_p = unlay(sp[:, 0:16]).reshape(1, 1, 32, 64); sim_p = unlay(sp[:, 16:32]).reshape(1, 1, 32, 64)
    mk_p = r0["memk_o"].reshape(1, 1, 256, 4, 256); mv_p = r0["memv_o"].reshape(1, 1, 256, 4, 256)
    return (y_p, y_s, ckv_p, kpe_p, sre_p, sim_p, mk_p, mv_p, ckv_s, kpe_s, sre_s, sim_s)
```

```python
import math
from contextlib import ExitStack

import numpy as np
import concourse.bass as bass
import concourse.mybir as mybir
from concourse.bass_utils import run_bass_kernel_spmd

F32 = mybir.dt.float32
BF16 = mybir.dt.bfloat16
I32 = mybir.dt.int32
AF = mybir.ActivationFunctionType
ALU = mybir.AluOpType
AX = mybir.AxisListType

NCORES = 8
D = 1024
NTP = 128
NOWN = 17
EPS = 1e-5
ALPHA = 2.0 ** 0.25
MLA_SCALE = 192.0 ** -0.5
X_SCALE = 256.0 ** -0.5
TWO_PI = 2.0 * math.pi
NEG = -1e30
NRING = 24
COMPUTE = ("pe", "act", "dve", "pool")


class Buf:
    __slots__ = ("name", "lw", "rd")

    def __init__(self, name=""):
        self.name = name
        self.lw = None
        self.rd = []


def _flat(bs):
    out = []
    for b in bs:
        if isinstance(b, (tuple, list)):
            out.extend(b)
        else:
            out.append(b)
    return out


class Op:
    __slots__ = ("eng", "fn", "deps", "isdma", "flag", "cnt", "ring", "n")

    def __init__(self, eng, fn, isdma):
        self.eng = eng
        self.fn = fn
        self.isdma = isdma
        self.deps = set()
        self.flag = False
        self.cnt = 0
        self.ring = None
        self.n = 0


class Prog:
    def __init__(self, nc):
        self.nc = nc
        self.ops = {e: [] for e in ("pe", "act", "dve", "pool", "sp")}
        self.ndma = {e: 0 for e in self.ops}
        self.allops = []
        self.floor = []
        self.dmas_since = []

    def _add(self, eng, fn, reads, writes, isdma):
        op = Op(eng, fn, isdma)
        reads = _flat(reads)
        writes = _flat(writes)
        for f in self.floor:
            op.deps.add(f)
        for b in reads:
            if b.lw is not None:
                op.deps.add(b.lw)
        for b in writes:
            if b.lw is not None:
                op.deps.add(b.lw)
            for r in b.rd:
                op.deps.add(r)
        for b in reads:
            b.rd.append(op)
        for b in writes:
            b.lw = op
            b.rd = []
        op.deps.discard(op)
        if isdma:
            op.n = self.ndma[eng]
            self.ndma[eng] += 1
            self.dmas_since.append(op)
        self.ops[eng].append(op)
        self.allops.append(op)
        return op

    def op(self, eng, fn, reads=(), writes=()):
        return self._add(eng, fn, reads, writes, False)

    def dma(self, eng, fn, reads=(), writes=()):
        return self._add(eng, fn, reads, writes, True)

    def barrier(self, fn):
        op = Op("pool", fn, False)
        for f in self.floor:
            op.deps.add(f)
        for e in COMPUTE:
            for o in reversed(self.ops[e]):
                if not o.isdma:
                    op.deps.add(o)
                    break
        for o in self.dmas_since:
            op.deps.add(o)
        self.dmas_since = []
        self.ops["pool"].append(op)
        self.allops.append(op)
        self.floor = [op]

    def emit(self, stack):
        nc = self.nc
        for op in self.allops:
            for d in op.deps:
                if d.eng == "pe" and op.eng == "pe" and not d.isdma:
                    continue
                d.flag = True
        sems = {e: stack.enter_context(nc.semaphore("s_" + e)) for e in COMPUTE}
        rings = {}
        for e in self.ops:
            if self.ndma[e]:
                rings[e] = [stack.enter_context(nc.semaphore("r_%s_%d" % (e, i))) for i in range(NRING)]
        for e in self.ops:
            c = 0
            for op in self.ops[e]:
                if op.isdma:
                    op.ring = rings[e][op.n % NRING]
                    op.cnt = 16 * (op.n // NRING + 1)
                elif op.flag:
                    c += 1
                    op.cnt = c
        block = stack.enter_context(nc.Block())

        def run(e, h):
            waited = {}

            def wait(sem, val):
                k = id(sem)
                if waited.get(k, 0) >= val:
                    return
                waited[k] = val
                h.wait_ge(sem, val)

            for op in self.ops[e]:
                for d in op.deps:
                    if d.isdma:
                        wait(d.ring, d.cnt)
                    else:
                        if d.eng == "pe" and e == "pe":
                            continue
                        wait(sems[d.eng], d.cnt)
                if op.isdma and op.n >= NRING:
                    wait(op.ring, op.cnt - 16)
                ins = op.fn(h)
                if op.isdma:
                    ins.then_inc(op.ring, 16)
                elif op.flag:
                    ins.then_inc(sems[e], 1)
            if e in rings:
                n = self.ndma[e]
                for i in range(min(n, NRING)):
                    last = ((n - 1 - i) // NRING) * NRING + i
                    wait(rings[e][i], 16 * (last // NRING + 1))

        @block.tensor
        def _(h):
            run("pe", h)

        @block.scalar
        def _(h):
            run("act", h)

        @block.vector
        def _(h):
            run("dve", h)

        @block.gpsimd
        def _(h):
            run("pool", h)

        @block.sync
        def _(h):
            run("sp", h)


class V:
    __slots__ = ("ap", "b")

    def __init__(self, ap, b):
        self.ap = ap
        self.b = b

    def __getitem__(self, idx):
        return V(self.ap[idx], self.b)

    def re(self, pat, **kw):
        return V(self.ap.rearrange(pat, **kw), self.b)

    def cast(self, dt):
        return V(self.ap.bitcast(dt), self.b)

    def bc(self, shape):
        return V(self.ap.broadcast_to(shape), self.b)

    def un(self, ax):
        return V(self.ap.unsqueeze(ax), self.b)


def build_nc():
    nc = bass.Bass("TRN2", target_bir_lowering=False)
    P = Prog(nc)
    dram = {}

    def din(name, shape, dt=F32):
        t = nc.dram_tensor(name, list(shape), dt, kind="ExternalInput")
        dram[name] = V(t.ap(), Buf(name))
        return dram[name]

    def dout(name, shape):
        t = nc.dram_tensor(name, list(shape), F32, kind="ExternalOutput")
        dram[name] = V(t.ap(), Buf(name))
        return dram[name]

    def dscr(name, shape, dt=F32):
        t = nc.dram_tensor(name, list(shape), dt)
        return V(t.ap(), Buf(name))

    xp = din("xp", [NTP * 128, D])
    xo = din("xo", [NOWN * 128, D])
    mem = din("mem", [256, D])
    ident_d = din("ident", [128, 128])
    w_in_d = din("w_in", [D, 1216]); w_q_d = din("w_q", [384, 768]); w_kv_d = din("w_kv", [256, 1024])
    w_glu_d = din("w_glu", [512, 1024]); w_o_d = din("w_o", [D, D]); w_xq_d = din("w_xq", [D, D])
    w_xk_d = din("w_xk", [D, D]); w_xv_d = din("w_xv", [D, D]); w_xo_d = din("w_xo", [D, D])
    w_ff1_d = din("w_ff1", [D, 4096]); w_ff2_d = din("w_ff2", [4096, D])
    gq_d = din("gq", [128, 384]); gkv_d = din("gkv", [128, 256]); gos_d = din("gos", [128, 512]); gom_d = din("gom", [128, 512])
    lng_d = din("lng", [128, 3 * D]); lnb_d = din("lnb", [128, 3 * D])
    ropec_p = din("ropec_p", [NTP * 128, 64]); ropes_p = din("ropes_p", [NTP * 128, 64])
    ropec_o = din("ropec_o", [NOWN * 128, 64]); ropes_o = din("ropes_o", [NOWN * 128, 64])
    maskd_d = din("maskd", [128, 1024]); onehot_d = din("onehot", [128, 8])
    ar_d = din("ar", [128, 16]); ai_d = din("ai", [128, 16]); ldt_d = din("ldt", [128, 16])
    bre_d = din("bre", [128, 256]); bim_d = din("bim", [128, 256]); jj_d = din("jj", [128, 128]); jj2_d = din("jj2", [128, 128])
    cre_d = din("cre", [128, 512]); cimn_d = din("cimn", [128, 512]); dblk_d = din("dblk", [128, 512])
    s0_d = din("s0", [128, 128])
    cckv_d = din("cckv", [4 * 1024, 256]); ckpe_d = din("ckpe", [4 * 1024, 64])
    cmk_d = din("cmk", [4 * 256, D]); cmv_d = din("cmv", [4 * 256, D])

    y_o = dout("y_o", [NOWN * 128, D])
    ckv_o = dout("ckv_o", [NTP * 128, 256]); kpe_o = dout("kpe_o", [NTP * 128, 64])
    ssmp_o = dout("ssmp_o", [128, 32])
    memk_o = dout("memk_o", [256, D]); memv_o = dout("memv_o", [256, D])
    ckvs_o = dout("ckvs_o", [128, 256]); kpes_o = dout("kpes_o", [128, 64]); ssms_o = dout("ssms_o", [128, 128])

    h1_d = dscr("h1_d", [NOWN * 128, D]); h2_d = dscr("h2_d", [NOWN * 128, D])
    tab_d = {n_: dscr("tab_" + n_, [128, 2048]) for n_ in ("S_t", "C_t", "R_t")}
    tab_d["WBre"] = dscr("tab_WBre", [128, 2048], BF16); tab_d["WBim"] = dscr("tab_WBim", [128, 2048], BF16)

    with ExitStack() as st:
        ARENA_N = 52000
        arena = st.enter_context(nc.sbuf_tensor("arena", [128, ARENA_N], F32))
        bump = [0]

        def alloc(shape, dt=F32, name=""):
            n = 1
            for s_ in shape[1:]:
                n *= s_
            words = (n * (2 if dt == BF16 else 4) + 3) // 4
            words = (words + 7) // 8 * 8
            off = bump[0]
            bump[0] += words
            assert bump[0] <= ARENA_N, ("SBUF arena overflow", name, bump[0])
            ap = arena[:, off:off + words]
            if dt != F32:
                ap = ap.bitcast(dt)
            ap = ap[:, 0:n]
            if len(shape) > 2:
                names = " ".join("d%d" % i for i in range(len(shape) - 1))
                kw = {"d%d" % i: shape[i + 1] for i in range(len(shape) - 1)}
                ap = ap.rearrange("p (%s) -> p %s" % (names, names), **kw)
            return V(ap, Buf(name))

        banks = []
        dbl = []
        for d_ in range(4):
            t = st.enter_context(nc.psum_tensor("dbank%d" % d_, [128, 1024], F32))
            b0 = Buf("bank%d" % (2 * d_)); b1 = Buf("bank%d" % (2 * d_ + 1))
            banks.append(V(t[:, 0:512], b0)); banks.append(V(t[:, 512:1024], b1))
            dbl.append(V(t[:], (b0, b1)))
        SC = dbl[3]
        SCs = [dbl[3], dbl[0]]

        def mm(out, lhsT, rhs, start=True, stop=True):
            P.op("pe", lambda e: e.matmul(out.ap, lhsT=lhsT.ap, rhs=rhs.ap, start=start, stop=stop),
                 reads=[lhsT.b, rhs.b], writes=[out.b])

        def tr(out, in_, idt):
            P.op("pe", lambda e: e.transpose(out=out.ap, in_=in_.ap, identity=idt.ap), reads=[in_.b, idt.b], writes=[out.b])

        def act(out, in_, func, bias=None, scale=None, accum=None):
            kw = {}
            rd = [in_.b]
            wr = [out.b]
            if bias is not None:
                if isinstance(bias, V):
                    kw["bias"] = bias.ap; rd.append(bias.b)
                else:
                    kw["bias"] = bias
            if scale is not None:
                if isinstance(scale, V):
                    kw["scale"] = scale.ap; rd.append(scale.b)
                else:
                    kw["scale"] = scale
            if accum is not None:
                kw["accum_out"] = accum.ap; wr.append(accum.b)
            P.op("act", lambda e: e.activation(out=out.ap, in_=in_.ap, func=func, **kw), reads=rd, writes=wr)

        def acopy(out, in_):
            P.op("act", lambda e: e.copy(out=out.ap, in_=in_.ap), reads=[in_.b], writes=[out.b])

        def vcopy(out, in_, eng="dve"):
            P.op(eng, lambda e: e.tensor_copy(out=out.ap, in_=in_.ap), reads=[in_.b], writes=[out.b])

        def tt(out, in0, in1, op, eng="dve"):
            P.op(eng, lambda e: e.tensor_tensor(out=out.ap, in0=in0.ap, in1=in1.ap, op=op), reads=[in0.b, in1.b], writes=[out.b])

        def ts(out, in0, s1, s2, op0, op1=None, eng="dve"):
            rd = [in0.b]
            a1 = s1
            a2 = s2
            if isinstance(s1, V):
                a1 = s1.ap; rd.append(s1.b)
            if isinstance(s2, V):
                a2 = s2.ap; rd.append(s2.b)
            if op1 is None:
                P.op(eng, lambda e: e.tensor_scalar(out=out.ap, in0=in0.ap, scalar1=a1, scalar2=None, op0=op0), reads=rd, writes=[out.b])
            else:
                P.op(eng, lambda e: e.tensor_scalar(out=out.ap, in0=in0.ap, scalar1=a1, scalar2=a2, op0=op0, op1=op1), reads=rd, writes=[out.b])

        def stt(out, in0, scalar, in1, op0, op1):
            rd = [in0.b, in1.b]
            a = scalar
            if isinstance(scalar, V):
                a = scalar.ap; rd.append(scalar.b)
            P.op("dve", lambda e: e.scalar_tensor_tensor(out=out.ap, in0=in0.ap, scalar=a, in1=in1.ap, op0=op0, op1=op1), reads=rd, writes=[out.b])

        def red(out, in_, op, axis=AX.X):
            P.op("dve", lambda e: e.tensor_reduce(out=out.ap, in_=in_.ap, axis=axis, op=op), reads=[in_.b], writes=[out.b])

        def recip(out, in_):
            P.op("dve", lambda e: e.reciprocal(out=out.ap, in_=in_.ap), reads=[in_.b], writes=[out.b])

        def scan(out, d0, d1, init):
            P.op("dve", lambda e: e.tensor_tensor_scan(out=out.ap, data0=d0.ap, data1=d1.ap, initial=init.ap, op0=ALU.mult, op1=ALU.add),
                 reads=[d0.b, d1.b, init.b], writes=[out.b])

        def mset(out, val, eng="pool"):
            P.op(eng, lambda e: e.memset(out.ap, val), writes=[out.b])

        def ld(out, in_, eng="sp"):
            P.dma(eng, lambda e: e.dma_start(out=out.ap, in_=in_.ap), reads=[in_.b], writes=[out.b])

        def ldw(out, in_):
            P.dma("pool", lambda e: e.dma_start(out=out.ap, in_=in_.ap), reads=[in_.b], writes=[out.b])

        ident = alloc([128, 128], F32, "ident"); identb = alloc([128, 128], BF16, "identb")
        bar_scr = alloc([128, 8], F32, "barscr")
        ld(ident, ident_d); ldw(identb, ident_d)
        epsc = alloc([128, 1], F32, "epsc"); mset(epsc, EPS)

        def barrier():
            P.barrier(lambda e: e.memset(bar_scr.ap, 0.0))

        def load_xT(src_rows, xt, xT, bank):
            ld(xt, src_rows)
            for hb in range(2):
                for k in range(4):
                    kk = hb * 4 + k
                    tr(bank[:, k * 128:(k + 1) * 128], xt[:, kk * 128:(kk + 1) * 128], ident)
                src = bank.re("p (k n) -> p k n", k=4)
                if hb == 0:
                    acopy(xT[:, 0:4, :], src)
                else:
                    vcopy(xT[:, 4:8, :], src)

        def transpose_to(dst, src, ncol, bank, eng="act", rows=128):
            bb = bank.cast(BF16)
            for k in range(ncol):
                tr(bb[:, k * 128:k * 128 + rows], src[:, k * 128:(k + 1) * 128], identb[0:rows, 0:rows])
            s_ = bb[:, 0:ncol * 128].re("p (k n) -> p k n", k=ncol)[:, :, 0:rows]
            if eng == "act":
                acopy(dst, s_)
            else:
                vcopy(dst, s_)

        def rmsnorm(out, src, n, gtile, sq, ss):
            act(sq, src, AF.Square, accum=ss)
            act(ss, ss, AF.Ln, scale=1.0 / n, bias=epsc)
            act(ss, ss, AF.Exp, scale=-0.5)
            stt(out, src, ss, gtile, ALU.mult, ALU.mult)

        def layernorm(out, r, g, b, stats, mv):
            for c in range(2):
                P.op("dve", lambda e, c=c: e.bn_stats(out=stats.ap[:, c * 6:(c + 1) * 6], in_=r.ap[:, c * 512:(c + 1) * 512]), reads=[r.b], writes=[stats.b])
            P.op("dve", lambda e: e.bn_aggr(out=mv.ap[:, 0:2], in_=stats.ap[:, 0:12]), reads=[stats.b], writes=[mv.b])
            act(mv[:, 1:2], mv[:, 1:2], AF.Ln, bias=epsc)
            act(mv[:, 1:2], mv[:, 1:2], AF.Exp, scale=-0.5)
            ts(out, r, mv[:, 0:1], mv[:, 1:2], ALU.subtract, ALU.mult)
            tt(out, out, g, ALU.mult, eng="pool")
            tt(out, out, b, ALU.add, eng="pool")

        def rope(out, src, cc, ss_, tmp, nh, eng="dve"):
            ccb = cc.un(1).bc([128, nh, 64]); ssb = ss_.un(1).bc([128, nh, 64])
            tt(out, src, ccb, ALU.mult, eng=eng)
            tt(tmp[:, :, 0:32], src[:, :, 32:64], ssb[:, :, 0:32], ALU.mult, eng=eng)
            tt(tmp[:, :, 32:64], src[:, :, 0:32], ssb[:, :, 32:64], ALU.mult, eng=eng)
            tt(out, out, tmp, ALU.add, eng="pool")

        mkT = alloc([128, 4, 2, 256], BF16, "mkT")
        mvb = alloc([128, 2, D], BF16, "mvb")
        mark_after_mem = bump[0]
        btab = Buf("tab")
        W_in_u = alloc([128, 8, 512], BF16, "W_in_u"); W_in_kv = alloc([128, 8, 320], BF16, "W_in_kv")
        LrT = alloc([128, 16, 128], BF16, "LrT"); LiT = alloc([128, 16, 128], BF16, "LiT")
        BRb = alloc([128, 16, 32], F32, "BRb"); BIb = alloc([128, 16, 32], F32, "BIb")
        lam128 = alloc([128, 2, 16], F32, "lam128")
        Xpp = alloc([128, 2, 32], F32, "Xpp")
        Xown = alloc([128, 16, 32], F32, "Xown")
        ohot = alloc([128, 8], F32, "ohot")
        Oacc = alloc([128, 16, 4, 128], F32, "Oacc")
        m_run = alloc([128, 64], F32, "m_run"); l_run = alloc([128, 64], F32, "l_run")
        Osam = alloc([128, 512], F32, "Osam")
        ssmx = {}
        mark_after_mixer_state = bump[0]
        QTn = alloc([128, NOWN, 4, 128], BF16, "QTn"); QTr = alloc([128, NOWN, 4, 128], BF16, "QTr")
        mark_after_q = bump[0]

        w_in_v = w_in_d.re("(kt p) n -> p kt n", p=128)
        ldw(W_in_u, w_in_v[:, :, 0:512]); ldw(W_in_kv, w_in_v[:, :, 896:1216])
        ld(ohot, onehot_d)

        S_t = alloc([128, 16, 128], F32, "S_t"); C_t = alloc([128, 16, 128], F32, "C_t"); R_t = alloc([128, 16, 128], F32, "R_t")
        for v_ in (S_t, C_t, R_t):
            v_.b = btab
        WBre = alloc([128, 16, 128], BF16, "WBre"); WBim = alloc([128, 16, 128], BF16, "WBim")

        p0 = bump[0]
        ar = alloc([128, 16]); ai = alloc([128, 16]); ldt = alloc([128, 16]); jj = alloc([128, 128])
        bre = alloc([128, 16, 16]); bim = alloc([128, 16, 16])
        bsm = Buf("ssm_small")
        for v_ in (ar, ai, ldt, jj, bre, bim):
            v_.b = bsm
        ld(ar, ar_d); ld(ai, ai_d); ld(ldt, ldt_d); ld(jj, jj_d)
        ld(bre.re("p a b -> p (a b)"), bre_d); ld(bim.re("p a b -> p (a b)"), bim_d)
        dt_ = alloc([128, 16]); th = alloc([128, 16]); rr = alloc([128, 16])
        sm = [alloc([128, 16]) for _ in range(8)]
        for v_ in [dt_, th, rr] + sm:
            v_.b = bsm
        act(dt_, ldt, AF.Exp)
        tt(th, ai, dt_, ALU.mult)
        tt(rr, ar, dt_, ALU.mult)
        act(rr, rr, AF.Exp)
        A_t = alloc([128, 16, 128]); T1 = alloc([128, 2048]); TI = alloc([128, 2048], I32)
        for v_ in (A_t, T1, TI):
            v_.b = btab
        jjb = jj.un(1).bc([128, 16, 128])
        tt(A_t, jjb, th.un(2).bc([128, 16, 128]), ALU.mult)
        tt(R_t, jjb, rr.un(2).bc([128, 16, 128]), ALU.max)
        tt(R_t, R_t, rr.un(2).bc([128, 16, 128]), ALU.min)
        Af = A_t.re("p a b -> p (a b)")
        TIf = TI.cast(F32)

        def sin_of(out, shift):
            ts(T1, Af, shift, 1.0 / TWO_PI, ALU.add, ALU.mult)
            vcopy(TI, T1)
            vcopy(T1, TI)
            stt(T1, T1, -TWO_PI, Af, ALU.mult, ALU.add)
            if shift != 0.0:
                ts(T1, T1, shift, None, ALU.add)
            ts(TIf, T1, math.pi, -TWO_PI, ALU.is_gt, ALU.mult)
            tt(T1, T1, TIf, ALU.add)
            ts(TIf, T1, -math.pi, TWO_PI, ALU.is_lt, ALU.mult)
            tt(T1, T1, TIf, ALU.add)
            ts(T1, T1, math.pi, -math.pi, ALU.min, ALU.max)
            act(out, T1, AF.Sin)

        sin_of(S_t.re("p a b -> p (a b)"), 0.0)
        sin_of(C_t.re("p a b -> p (a b)"), math.pi / 2)
        lbr, lbi, den, fre, fim, t0_, t1_, t2_ = sm
        cos1 = C_t[:, :, 0]; sin1 = S_t[:, :, 0]
        tt(lbr, rr, cos1, ALU.mult)
        ts(lbr, lbr, -1.0, None, ALU.add)
        tt(lbi, rr, sin1, ALU.mult)
        tt(den, ar, ar, ALU.mult)
        tt(t0_, ai, ai, ALU.mult)
        tt(den, den, t0_, ALU.add)
        recip(den, den)
        tt(t0_, lbr, ar, ALU.mult); tt(t1_, lbi, ai, ALU.mult); tt(fre, t0_, t1_, ALU.add); tt(fre, fre, den, ALU.mult)
        tt(t0_, lbi, ar, ALU.mult); tt(t1_, lbr, ai, ALU.mult); tt(fim, t0_, t1_, ALU.subtract); tt(fim, fim, den, ALU.mult)
        Mre = alloc([128, 16, 128]); Mim = alloc([128, 16, 128]); tb = alloc([128, 16, 16]); tb2 = alloc([128, 16, 16])
        Mreb = alloc([128, 16, 128], BF16); Mimb = alloc([128, 16, 128], BF16)
        bM = Buf("M")
        for v_ in (Mre, Mim, tb, tb2, Mreb, Mimb):
            v_.b = bM
        freb = fre.un(2).bc([128, 16, 16]); fimb = fim.un(2).bc([128, 16, 16])
        mset(Mre, 0.0); mset(Mim, 0.0)
        tt(tb, bre, freb, ALU.mult); tt(tb2, bim, fimb, ALU.mult)
        for lo, col in ((0, 0), (64, 16)):
            for j4 in range(4):
                tt(Mre[lo:lo + 64, j4::4, 32 * j4 + col:32 * j4 + col + 16], tb[lo:lo + 64, j4::4, :], tb2[lo:lo + 64, j4::4, :], ALU.subtract)
        tt(tb, bim, freb, ALU.mult); tt(tb2, bre, fimb, ALU.mult)
        for lo, col in ((0, 0), (64, 16)):
            for j4 in range(4):
                tt(Mim[lo:lo + 64, j4::4, 32 * j4 + col:32 * j4 + col + 16], tb[lo:lo + 64, j4::4, :], tb2[lo:lo + 64, j4::4, :], ALU.add)
        vcopy(Mreb, Mre); vcopy(Mimb, Mim)
        for M_, WB_ in ((Mreb, WBre), (Mimb, WBim)):
            for hf in range(2):
                bb = banks[0].cast(BF16)
                for q in range(8):
                    tr(bb[:, q * 128:(q + 1) * 128], M_[:, 8 * hf + q, :], identb)
                vcopy(WB_[:, 8 * hf:8 * hf + 8, :].re("p a b -> p (a b)"), bb)

        mset(BRb, 0.0); mset(BIb, 0.0)
        tt(tb, bre, freb, ALU.mult); tt(tb2, bim, fimb, ALU.mult)
        for lo, col in ((0, 0), (64, 16)):
            tt(BRb[lo:lo + 64, :, col:col + 16], tb[lo:lo + 64], tb2[lo:lo + 64], ALU.subtract)
        tt(tb, bim, freb, ALU.mult); tt(tb2, bre, fimb, ALU.mult)
        for lo, col in ((0, 0), (64, 16)):
            tt(BIb[lo:lo + 64, :, col:col + 16], tb[lo:lo + 64], tb2[lo:lo + 64], ALU.add)
        lnr = alloc([128, 16]); lnr.b = bsm
        tt(lnr, ar, dt_, ALU.mult)
        act(t2_, lnr, AF.Exp, scale=128.0)
        tt(lam128[:, 0, :], t2_, C_t[:, :, 127], ALU.mult)
        tt(lam128[:, 1, :], t2_, S_t[:, :, 127], ALU.mult)
        jj2 = alloc([128, 128]); jj2.b = bsm
        ld(jj2, jj2_d)
        C2 = Mre; S2 = Mim; Mag = T1.re("p (a b) -> p a b", a=16)
        jj2b = jj2.un(1).bc([128, 16, 128])
        tt(A_t, jj2b, th.un(2).bc([128, 16, 128]), ALU.mult)
        sin_of(S2.re("p a b -> p (a b)"), 0.0)
        sin_of(C2.re("p a b -> p (a b)"), math.pi / 2)
        tt(Mag, jj2b, lnr.un(2).bc([128, 16, 128]), ALU.mult)
        act(Mag, Mag, AF.Exp)
        Lb = [Mreb, Mimb]
        tt(Lb[0], Mag, C2, ALU.mult); tt(Lb[1], Mag, S2, ALU.mult)
        for M_, LT_ in ((Lb[0], LrT), (Lb[1], LiT)):
            for hf in range(2):
                bb = banks[0].cast(BF16)
                for q in range(8):
                    tr(bb[:, q * 128:(q + 1) * 128], M_[:, 8 * hf + q, :], identb)
                vcopy(LT_[:, 8 * hf:8 * hf + 8, :].re("p a b -> p (a b)"), bb)

        for nm_, v_ in (("S_t", S_t), ("C_t", C_t), ("R_t", R_t)):
            ld(tab_d[nm_], v_.re("p a b -> p (a b)"))
        ld(tab_d["WBre"], WBre.re("p a b -> p (a b)")); ld(tab_d["WBim"], WBim.re("p a b -> p (a b)"))

        barrier()
        bump[0] = mark_after_q
        W_xk = alloc([128, 8, D], BF16, "W_xk"); W_xv = alloc([128, 8, D], BF16, "W_xv")
        ldw(W_xk, w_xk_d.re("(kt p) n -> p kt n", p=128)); ldw(W_xv, w_xv_d.re("(kt p) n -> p kt n", p=128))
        xt0 = alloc([128, D], F32, "xt0"); memT = alloc([128, 8, 256], BF16, "memT"); xT0 = alloc([128, 8, 128], BF16, "xT0")
        mkf = alloc([128, D], F32, "mkf")
        for mt in range(2):
            load_xT(mem[mt * 128:(mt + 1) * 128, :], xt0, xT0, banks[0])
            vcopy(memT[:, :, mt * 128:(mt + 1) * 128], xT0, eng="pool")
            for W_, o_d, keep in ((W_xk, memk_o, False), (W_xv, memv_o, True)):
                for nb in range(2):
                    for k in range(8):
                        mm(banks[1 + nb], xT0[:, k, :], W_[:, k, nb * 512:(nb + 1) * 512], start=(k == 0), stop=(k == 7))
                    acopy(mkf[:, nb * 512:(nb + 1) * 512], banks[1 + nb])
                ld(o_d[mt * 128:(mt + 1) * 128, :], mkf)
                if keep:
                    vcopy(mvb[:, mt, :], mkf)
        for h in range(4):
            for et in range(2):
                c0 = h * 256 + et * 128
                for k in range(8):
                    mm(banks[3][:, 0:256], W_xk[:, k, c0:c0 + 128], memT[:, k, :], start=(k == 0), stop=(k == 7))
                acopy(mkT[:, h, et, :], banks[3][:, 0:256])

        barrier()
        bump[0] = mark_after_q

        W_q = alloc([128, 3, 768], BF16, "W_q")
        ldw(W_q, w_q_d.re("(kt p) n -> p kt n", p=128))
        W_in_q = alloc([128, 8, 384], BF16, "W_in_q")
        ldw(W_in_q, w_in_v[:, :, 512:896])
        gq = alloc([128, 384], F32, "gq"); ld(gq, gq_d)
        NB = 2
        xt = [alloc([128, D], F32, "xt%d" % i) for i in range(NB)]
        xT = [alloc([128, 8, 128], BF16, "xT%d" % i) for i in range(NB)]
        rc = [alloc([128, 64], F32, "rc%d" % i) for i in range(NB)]
        rs = [alloc([128, 64], F32, "rs%d" % i) for i in range(NB)]
        sq = alloc([128, 512], F32, "sq"); ss1 = [alloc([128, 1], F32) for _ in range(NB)]
        cqn = [alloc([128, 384], BF16) for _ in range(NB)]
        cqT = [alloc([128, 3, 128], BF16) for _ in range(NB)]
        qf = [alloc([128, 4, 192], F32) for _ in range(NB)]
        qr = [alloc([128, 4, 64], F32) for _ in range(NB)]
        qtmp = [alloc([128, 4, 64], F32) for _ in range(NB)]
        qb = [alloc([128, 4, 192], BF16) for _ in range(NB)]
        for i in range(NOWN):
            b = i % NB
            load_xT(xo[i * 128:(i + 1) * 128, :], xt[b], xT[b], banks[0])
            ld(rc[b], ropec_o[i * 128:(i + 1) * 128, :]); ld(rs[b], ropes_o[i * 128:(i + 1) * 128, :])
            for k in range(8):
                mm(banks[1][:, 0:384], xT[b][:, k, :], W_in_q[:, k, :], start=(k == 0), stop=(k == 7))
            rmsnorm(cqn[b], banks[1][:, 0:384], 384, gq, sq[:, 0:384], ss1[b])
            transpose_to(cqT[b], cqn[b], 3, banks[2], eng="act")
            for nb, (c0, c1) in enumerate(((0, 512), (512, 768))):
                for k in range(3):
                    mm(SC[:, nb * 512:nb * 512 + (c1 - c0)], cqT[b][:, k, :], W_q[:, k, c0:c1], start=(k == 0), stop=(k == 2))
            acopy(qf[b].re("p a b -> p (a b)"), SC[:, 0:768])
            rope(qr[b], qf[b][:, :, 128:192], rc[b], rs[b], qtmp[b], 4)
            P.op("act", lambda e, b=b: e.mul(out=qb[b].ap[:, :, 0:128], in_=qf[b].ap[:, :, 0:128], mul=MLA_SCALE), reads=[qf[b].b], writes=[qb[b].b])
            P.op("act", lambda e, b=b: e.mul(out=qb[b].ap[:, :, 128:192], in_=qr[b].ap, mul=MLA_SCALE), reads=[qr[b].b], writes=[qb[b].b])
            bb = banks[3].cast(BF16)
            for h in range(4):
                tr(bb[:, h * 128:(h + 1) * 128], qb[b][:, h, 0:128], identb)
                tr(bb[0:64, 512 + h * 128:512 + (h + 1) * 128], qb[b][:, h, 128:192], identb)
            vcopy(QTn[:, i, :, :].re("p a b -> p (a b)"), bb[:, 0:512])
            vcopy(QTr[0:64, i, :, :].re("p a b -> p (a b)"), bb[0:64, 512:1024])

        barrier()
        bump[0] = mark_after_q

        def ssm_rounds(uT_, pbs, init_fn, nseg, ybank=None, last_out=None):
            L = 128 // nseg
            S_t = ssmx["S_t"]; C_t = ssmx["C_t"]; R_t = ssmx["R_t"]; WBre = ssmx["WBre"]; WBim = ssmx["WBim"]
            Cre = ssmx["Cre"]; Cimn = ssmx["Cimn"]; Dblk = ssmx["Dblk"]
            xs4 = ssmx["xs4"]; zl_all = ssmx["zl_all"]
            def do_round(r):
                m4 = ssmx["m4"][r % 2]; wri = ssmx["wri"][r % 2]; zri = ssmx["zri"][r % 2]
                pb = pbs[r % 2]
                for j in range(2):
                    gp = 2 * r + j
                    mm(pb[:, j * 128:(j + 1) * 128], WBre[:, gp, :], uT_[:, gp // 4, :])
                    mm(pb[:, 256 + j * 128:256 + (j + 1) * 128], WBim[:, gp, :], uT_[:, gp // 4, :])
                if nseg == 1:
                    Cq = C_t[:, 2 * r:2 * r + 2, :].re("p a b -> p (a b)"); Sq = S_t[:, 2 * r:2 * r + 2, :].re("p a b -> p (a b)")
                    pre = pb[:, 0:256]; pim = pb[:, 256:512]
                    mk = lambda v_: v_
                else:
                    Cq = C_t[:, 2 * r:2 * r + 2, 0:L].un(2).bc([128, 2, nseg, L]); Sq = S_t[:, 2 * r:2 * r + 2, 0:L].un(2).bc([128, 2, nseg, L])
                    pre = pb[:, 0:256].re("p (a s l) -> p a s l", a=2, s=nseg); pim = pb[:, 256:512].re("p (a s l) -> p a s l", a=2, s=nseg)
                    mk = lambda v_: v_.re("p (a s l) -> p a s l", a=2, s=nseg)
                tt(mk(m4[0]), pre, Cq, ALU.mult); tt(mk(m4[1]), pim, Sq, ALU.mult)
                tt(mk(m4[2]), pim, Cq, ALU.mult); tt(mk(m4[3]), pre, Sq, ALU.mult)
                tt(wri[0], m4[0], m4[1], ALU.add, eng="pool")
                tt(wri[1], m4[2], m4[3], ALU.subtract, eng="pool")

            def do_back(r):
                m4 = ssmx["m4"][r % 2]; wri = ssmx["wri"][r % 2]; zri = ssmx["zri"][r % 2]
                if nseg == 1:
                    Cq = C_t[:, 2 * r:2 * r + 2, :].re("p a b -> p (a b)"); Sq = S_t[:, 2 * r:2 * r + 2, :].re("p a b -> p (a b)")
                else:
                    Cq = C_t[:, 2 * r:2 * r + 2, 0:L].un(2).bc([128, 2, nseg, L]); Sq = S_t[:, 2 * r:2 * r + 2, 0:L].un(2).bc([128, 2, nseg, L])
                for j in range(2):
                    gp = 2 * r + j
                    for s_ in range(nseg):
                        for part in range(2):
                            scan(zri[part][:, j, s_ * L:(s_ + 1) * L], R_t[:, gp, 0:L], wri[part][:, j * 128 + s_ * L:j * 128 + (s_ + 1) * L], init_fn(gp, s_, part))
                if last_out is not None:
                    for part in range(2):
                        src = zri[part].re("p a (s l) -> p a s l", s=nseg)[:, :, :, L - 1]
                        vcopy(zl_all[part][:, 0:nseg, 2 * r:2 * r + 2].re("p s a -> p a s"), src, eng="pool")
                if ybank is not None:
                    dmd = ssmx["dmd"][r % 2]; xrb = ssmx["xrb"]; xib = ssmx["xib"]
                    if nseg == 1:
                        Cd, Sd = Cq, Sq
                        zr_ = zri[0].re("p a b -> p (a b)"); zi_ = zri[1].re("p a b -> p (a b)")
                        dk = lambda v_: v_
                    else:
                        Cd, Sd = Cq, Sq
                        zr_ = zri[0].re("p a (s l) -> p a s l", s=nseg); zi_ = zri[1].re("p a (s l) -> p a s l", s=nseg)
                        dk = lambda v_: v_.re("p (a s l) -> p a s l", a=2, s=nseg)
                    tt(dk(dmd[0]), zr_, Cd, ALU.mult); tt(dk(dmd[1]), zi_, Sd, ALU.mult, eng="pool")
                    tt(dk(dmd[2]), zr_, Sd, ALU.mult); tt(dk(dmd[3]), zi_, Cd, ALU.mult, eng="pool")
                    tt(xrb[r % 2].re("p a b -> p (a b)"), dmd[0], dmd[1], ALU.subtract, eng="pool")
                    tt(xib[r % 2].re("p a b -> p (a b)"), dmd[2], dmd[3], ALU.add, eng="pool")

            def do_cproj(r):
                if ybank is None:
                    return
                xrb = ssmx["xrb"]; xib = ssmx["xib"]
                for j in range(2):
                    gp = 2 * r + j
                    o_ = ybank[:, 32 * gp:32 * gp + 32]
                    mm(o_, xrb[r % 2][:, j, :], Cre[:, gp, :], start=True, stop=False)
                    mm(o_, xib[r % 2][:, j, :], Cimn[:, gp, :], start=False, stop=False)
                    mm(o_, uT_[:, gp // 4, :], Dblk[:, gp, :], start=False, stop=True)
            def do_tail():
                do_cproj(7)
                if last_out is None:
                    return
                if True:
                    cL = C_t[:, :, L - 1]; sL = S_t[:, :, L - 1]
                    for s_ in range(nseg):
                        o_re, o_im = last_out(s_)
                        zr_l = zl_all[0][:, s_, :]; zi_l = zl_all[1][:, s_, :]
                        tt(xs4[0], zr_l, cL, ALU.mult); tt(xs4[1], zi_l, sL, ALU.mult)
                        tt(xs4[2], zr_l, sL, ALU.mult); tt(xs4[3], zi_l, cL, ALU.mult)
                        tt(o_re, xs4[0], xs4[1], ALU.subtract)
                        tt(o_im, xs4[2], xs4[3], ALU.add)


            def step(k_):
                def f():
                    if k_ == 0:
                        do_round(0)
                    if k_ + 1 < 8:
                        do_round(k_ + 1)
                    do_back(k_)
                    if k_ >= 1:
                        do_cproj(k_ - 1)
                return f
            return [step(k_) for k_ in range(8)] + [do_tail]

        def ssm_tile(uT_, pbs, init_fn, nseg, ybank=None, last_out=None):
            for f_ in ssm_rounds(uT_, pbs, init_fn, nseg, ybank, last_out):
                f_()

        W_kv = alloc([128, 2, 4, 256], BF16, "W_kv")
        ldw(W_kv.re("p k h e -> p k (h e)"), w_kv_d.re("(kt p) n -> p kt n", p=128))
        gkv = alloc([128, 256], F32, "gkv"); ld(gkv, gkv_d)
        maskd = alloc([128, 1024], F32, "maskd"); ld(maskd, maskd_d)
        xt = [alloc([128, D], F32, "hxt%d" % i) for i in range(NB)]
        xT = [alloc([128, 8, 128], BF16, "hxT%d" % i) for i in range(NB)]
        utok = [alloc([128, 512], BF16, "hut%d" % i) for i in range(NB)]
        kvf = [alloc([128, 320], F32, "kvf%d" % i) for i in range(NB)]
        prs = [alloc([128, 16, 32], F32, "prs%d" % i) for i in range(NB)]
        pis = [alloc([128, 16, 32], F32, "pis%d" % i) for i in range(NB)]
        rc = [alloc([128, 64], F32) for _ in range(NB)]; rs = [alloc([128, 64], F32) for _ in range(NB)]
        ckvf = [alloc([128, 256], F32) for _ in range(NB)]; kpef = [alloc([128, 64], F32) for _ in range(NB)]
        tmp64 = [alloc([128, 64], F32) for _ in range(NB)]
        sq = alloc([128, 256], F32, "hsq"); ss1 = [alloc([128, 1], F32) for _ in range(NB)]
        ckvb = [alloc([128, 256], BF16) for _ in range(NB)]; kpeb = [alloc([128, 128], BF16) for _ in range(NB)]
        ckvT = [alloc([128, 2, 128], BF16) for _ in range(NB)]
        KT = [alloc([128, 4, 1024], BF16, "KT%d" % i) for i in range(2)]
        KR = [alloc([128, 1024], BF16, "KR%d" % i) for i in range(2)]
        Vb = [alloc([128, 8, 512], BF16, "Vb%d" % i) for i in range(2)]
        Pb = [alloc([128, 1024], BF16, "Pb%d" % i) for i in range(2)]
        PT = [alloc([128, 8, 128], BF16, "PT%d" % i) for i in range(2)]
        NS4 = 4
        mx = [alloc([128, 1], F32) for _ in range(NS4)]; mnew = [alloc([128, 1], F32) for _ in range(NS4)]
        negm = [alloc([128, 1], F32) for _ in range(NS4)]; alp = [alloc([128, 1], F32) for _ in range(NS4)]
        rsum = [alloc([128, 1], F32) for _ in range(NS4)]
        dm_ = [alloc([128, 1], F32) for _ in range(NS4)]
        hm1 = [alloc([128, 16, 32], F32, "hm1_%d" % i) for i in range(NB)]; hm2 = [alloc([128, 16, 32], F32, "hm2_%d" % i) for i in range(NB)]
        sre = [alloc([128, 32], F32) for _ in range(2)]
        xs8 = [alloc([128, 16], F32) for _ in range(4)]

        mset(Xpp, 0.0); mset(Xown, 0.0)
        mset(m_run, NEG); mset(l_run, 0.0); mset(Oacc, 0.0)
        for kp in range(2):
            mset(kpeb[kp], 0.0)

        def hist_stages(t, kbuf, j, kb):
            b = t % NB
            bA = banks[0]; bB = banks[1]
            xc = Xpp[:, t % 2, :]; xn = Xpp[:, (t + 1) % 2, :]

            def sA():
                ld(xt[b], xp[t * 128:(t + 1) * 128, :])
                ld(rc[b], ropec_p[t * 128:(t + 1) * 128, :]); ld(rs[b], ropes_p[t * 128:(t + 1) * 128, :])

            def sB():
                for k in range(4):
                    tr(bA[:, k * 128:(k + 1) * 128], xt[b][:, k * 128:(k + 1) * 128], ident)
                for k in range(4):
                    tr(bB[:, k * 128:(k + 1) * 128], xt[b][:, (4 + k) * 128:(5 + k) * 128], ident)

            def sC():
                acopy(xT[b][:, 0:4, :], bA.re("p (k n) -> p k n", k=4))
                vcopy(xT[b][:, 4:8, :], bB.re("p (k n) -> p k n", k=4))

            def sD():
                for k in range(8):
                    mm(bA, xT[b][:, k, :], W_in_u[:, k, :], start=(k == 0), stop=(k == 7))
                for k in range(8):
                    mm(bB[:, 0:320], xT[b][:, k, :], W_in_kv[:, k, :], start=(k == 0), stop=(k == 7))

            def sE():
                acopy(utok[b], bA)
                acopy(kvf[b], bB[:, 0:320])

            def sF():
                act(sq, kvf[b][:, 0:256], AF.Square, accum=ss1[b])
                act(ss1[b], ss1[b], AF.Ln, scale=1.0 / 256, bias=epsc)
                act(ss1[b], ss1[b], AF.Exp, scale=-0.5)
                for gp in range(16):
                    mm(bA[:, 32 * gp:32 * gp + 32], LrT[:, gp, :], utok[b][:, 32 * gp:32 * gp + 32])
                for gp in range(16):
                    mm(bB[:, 32 * gp:32 * gp + 32], LiT[:, gp, :], utok[b][:, 32 * gp:32 * gp + 32])

            def sG():
                acopy(prs[b].re("p a b -> p (a b)"), bA)
                acopy(pis[b].re("p a b -> p (a b)"), bB)
                tt(ckvf[b], kvf[b][:, 0:256], gkv, ALU.mult, eng="pool")
                ts(ckvf[b], ckvf[b], ss1[b], 1.0, ALU.mult, ALU.mult, eng="pool")
                rope(kpef[b].re("p (a b) -> p a b", a=1), kvf[b][:, 256:320].re("p (a b) -> p a b", a=1), rc[b], rs[b],
                     tmp64[b].re("p (a b) -> p a b", a=1), 1, eng="pool")
                vcopy(ckvb[b], ckvf[b], eng="pool")
                vcopy(kpeb[b][:, 0:64], kpef[b], eng="pool")

            def sH():
                bb = bA.cast(BF16)
                for kc in range(2):
                    tr(bb[:, kc * 128:(kc + 1) * 128], ckvb[b][:, kc * 128:(kc + 1) * 128], identb)
                tr(bb[:, 256:384], kpeb[b], identb)
                tt(hm1[b], prs[b], BRb, ALU.mult, eng="pool"); tt(hm2[b], pis[b], BIb, ALU.mult, eng="pool")
                tt(hm1[b], hm1[b], hm2[b], ALU.subtract, eng="pool")
                tt(hm2[b], pis[b], BRb, ALU.mult, eng="pool"); tt(prs[b], prs[b], BIb, ALU.mult, eng="pool")
                tt(hm2[b], hm2[b], prs[b], ALU.add, eng="pool")

            def sI():
                bb = bA.cast(BF16)
                acopy(ckvT[b].re("p a b -> p (a b)"), bb[:, 0:256])
                acopy(KR[kbuf][0:64, j * 128:(j + 1) * 128], bb[0:64, 256:384])

            def sJ():
                for h in range(4):
                    for kc in range(2):
                        mm(bB[:, h * 128:(h + 1) * 128], W_kv[:, kc, h, 0:128], ckvT[b][:, kc, :], start=(kc == 0), stop=(kc == 1))
                for kc in range(2):
                    mm(bA, ckvT[b][:, kc, :], W_kv[:, kc, :, 128:256], start=(kc == 0), stop=(kc == 1))
                lr_ = lam128[:, 0, :]; li_ = lam128[:, 1, :]
                ld(ckv_o[t * 128:(t + 1) * 128, :], ckvf[b])
                ld(kpe_o[t * 128:(t + 1) * 128, :], kpef[b])
                red(sre[t % 2][:, 0:16], hm1[b], ALU.add)
                red(sre[t % 2][:, 16:32], hm2[b], ALU.add)
                stt(Xown[:, kb, :], xc, ohot[:, j:j + 1], Xown[:, kb, :], ALU.mult, ALU.add)
                tt(xs8[0], xc[:, 0:16], lr_, ALU.mult, eng="pool"); tt(xs8[1], xc[:, 16:32], li_, ALU.mult, eng="pool")
                tt(xs8[2], xc[:, 16:32], lr_, ALU.mult, eng="pool"); tt(xs8[3], xc[:, 0:16], li_, ALU.mult, eng="pool")
                tt(xs8[0], xs8[0], xs8[1], ALU.subtract, eng="pool"); tt(xs8[2], xs8[2], xs8[3], ALU.add, eng="pool")
                tt(xn[:, 0:16], xs8[0], sre[t % 2][:, 0:16], ALU.add, eng="pool")
                tt(xn[:, 16:32], xs8[2], sre[t % 2][:, 16:32], ALU.add, eng="pool")

            def sK():
                acopy(KT[kbuf][:, :, j * 128:(j + 1) * 128], bB.re("p (h n) -> p h n", h=4))
                vcopy(Vb[kbuf][:, j, :], bA)

            def pair(p_, c_):
                def f():
                    p_(); c_()
                return f
            return [sA, pair(sB, sC), pair(sD, sE), pair(sF, sG), pair(sH, sI), pair(sJ, sK)]

        def hist_items(kb):
            items = []
            for jp in range(4):
                sa = hist_stages(kb * 8 + 2 * jp, kb % 2, 2 * jp, kb)
                sb_ = hist_stages(kb * 8 + 2 * jp + 1, kb % 2, 2 * jp + 1, kb)
                for a_, b_ in zip(sa, sb_):
                    items.append(a_); items.append(b_)
            return items

        SCp = [dbl[3], dbl[2]]
        ptb = [banks[2], banks[3]]

        def att_A(n, i, h, kbuf):
            sc = SCp[n % 2]
            for c2 in range(2):
                mm(sc[:, c2 * 512:(c2 + 1) * 512], QTn[:, i, h, :], KT[kbuf][:, h, c2 * 512:(c2 + 1) * 512], start=True, stop=False)
                mm(sc[:, c2 * 512:(c2 + 1) * 512], QTr[0:64, i, h, :], KR[kbuf][0:64, c2 * 512:(c2 + 1) * 512], start=False, stop=True)

        def att_B(n, i, h, kbuf, diag):
            u2 = n % 2; u4 = n % NS4
            sc = SCp[u2]
            col = i * 4 + h
            if diag:
                tt(sc, sc, maskd, ALU.add)
            red(mx[u4], sc, ALU.max)
            tt(mnew[u4], mx[u4], m_run[:, col:col + 1], ALU.max)
            tt(dm_[u4], m_run[:, col:col + 1], mnew[u4], ALU.subtract)
            vcopy(m_run[:, col:col + 1], mnew[u4])
            ts(negm[u4], mnew[u4], -1.0, None, ALU.mult)
            act(alp[u4], dm_[u4], AF.Exp)
            act(Pb[u2], sc, AF.Exp, bias=negm[u4], accum=rsum[u4])

        def att_CD(n, i, h, kbuf):
            u2 = n % 2
            pt_b = ptb[u2].cast(BF16)
            for kt in range(8):
                tr(pt_b[:, kt * 128:(kt + 1) * 128], Pb[u2][:, kt * 128:(kt + 1) * 128], identb)
            if n % 3 == 0:
                vcopy(PT[u2].re("p a b -> p (a b)"), pt_b)
            else:
                acopy(PT[u2].re("p a b -> p (a b)"), pt_b)

        def att_EF(n, i, h, kbuf):
            u2 = n % 2; u4 = n % NS4
            ov = ptb[u2][:, 0:128]
            for kt in range(8):
                mm(ov, PT[u2][:, kt, :], Vb[kbuf][:, kt, h * 128:(h + 1) * 128], start=(kt == 0), stop=(kt == 7))
            stt(Oacc[:, i, h, :], Oacc[:, i, h, :], alp[u4], ov, ALU.mult, ALU.add)
            col = i * 4 + h
            stt(l_run[:, col:col + 1], l_run[:, col:col + 1], alp[u4], rsum[u4], ALU.mult, ALU.add)

        for it in hist_items(0):
            it()
        nun = 0
        for kb in range(16):
            kbuf = kb % 2
            units = [(i, h) for i in range(kb, 16) for h in range(4)]
            U = len(units)
            items = hist_items(kb + 1) if kb + 1 < 16 else []
            per = (len(items) + U - 1) // U if items else 0
            ip = 0
            for q_ in range(U + 3):
                if q_ < U:
                    att_A(nun + q_, units[q_][0], units[q_][1], kbuf)
                if 0 <= q_ - 1 < U:
                    att_B(nun + q_ - 1, units[q_ - 1][0], units[q_ - 1][1], kbuf, units[q_ - 1][0] == kb)
                if 0 <= q_ - 2 < U:
                    att_CD(nun + q_ - 2, units[q_ - 2][0], units[q_ - 2][1], kbuf)
                if 0 <= q_ - 3 < U:
                    att_EF(nun + q_ - 3, units[q_ - 3][0], units[q_ - 3][1], kbuf)
                for _ in range(per):
                    if ip < len(items):
                        items[ip](); ip += 1
            while ip < len(items):
                items[ip](); ip += 1
            nun += U

        ld(ssmp_o, Xpp[:, NTP % 2, :])

        barrier()
        bump[0] = mark_after_q

        W_kv = alloc([128, 2, 4, 256], BF16, "W_kv2")
        ldw(W_kv.re("p k h e -> p k (h e)"), w_kv_d.re("(kt p) n -> p kt n", p=128))
        gkv = alloc([128, 256], F32, "gkv2"); ld(gkv, gkv_d)
        xt_s = alloc([128, D], F32, "sxt"); xT_s = alloc([128, 8, 128], BF16, "sxT")
        rc_s = alloc([128, 64], F32); rs_s = alloc([128, 64], F32)
        ckvf_s = alloc([128, 256], F32); kpef_s = alloc([128, 64], F32); tmp64_s = alloc([128, 64], F32)
        sq = alloc([128, 512], F32, "ssq"); ss_s = alloc([128, 1], F32)
        ckvb_s = alloc([128, 256], BF16); kpeb_s = alloc([128, 128], BF16)
        ckvTn = alloc([128, 2, 128], BF16, "ckvTn"); kpeTn = alloc([128, 128], BF16, "kpeTn")
        KTn = alloc([128, 4, 128], BF16, "KTn")
        cat = [alloc([128, 256], F32, "cat%d" % i) for i in range(2)]
        catb = [alloc([128, 256], BF16) for _ in range(2)]
        cpt = [alloc([128, 64], F32, "cpt%d" % i) for i in range(2)]
        cptb = [alloc([128, 128], BF16) for _ in range(2)]
        ckvTc = alloc([128, 2, 1024], BF16, "ckvTc"); kpeTc = alloc([128, 1024], BF16, "kpeTc")
        KTc = alloc([128, 4, 1024], BF16, "KTc"); Vc = alloc([128, 8, 512], BF16, "Vc"); Vn = alloc([128, 512], BF16, "Vn")
        scs = alloc([128, 1056], F32, "scs")
        Ps = alloc([128, 1152], BF16, "Ps"); PTs = alloc([128, 9, 32], BF16, "PTs")
        mxs = alloc([128, 1], F32); negs = alloc([128, 1], F32); sums = alloc([128, 1], F32)
        osb = alloc([128, 512], F32, "osb")
        isam = 16
        load_xT(xo[isam * 128:(isam + 1) * 128, :], xt_s, xT_s, banks[0])
        ld(rc_s, ropec_o[isam * 128:(isam + 1) * 128, :]); ld(rs_s, ropes_o[isam * 128:(isam + 1) * 128, :])
        for k in range(8):
            mm(banks[2][:, 0:320], xT_s[:, k, :], W_in_kv[:, k, :], start=(k == 0), stop=(k == 7))
        rmsnorm(ckvf_s, banks[2][:, 0:256], 256, gkv, sq[:, 0:256], ss_s)
        ld(ckvs_o, ckvf_s)
        rope(kpef_s.re("p (a b) -> p a b", a=1), banks[2][:, 256:320].re("p (a b) -> p a b", a=1), rc_s, rs_s, tmp64_s.re("p (a b) -> p a b", a=1), 1)
        ld(kpes_o, kpef_s)
        vcopy(ckvb_s, ckvf_s)
        mset(kpeb_s, 0.0)
        vcopy(kpeb_s[:, 0:64], kpef_s)
        bb = banks[3].cast(BF16)
        for kc in range(2):
            tr(bb[:, kc * 128:(kc + 1) * 128], ckvb_s[:, kc * 128:(kc + 1) * 128], identb)
        tr(bb[:, 256:384], kpeb_s, identb)
        acopy(ckvTn.re("p a b -> p (a b)"), bb[:, 0:256])
        acopy(kpeTn[0:64, :], bb[0:64, 256:384])
        for h in range(4):
            for kc in range(2):
                mm(banks[3][:, h * 128:(h + 1) * 128], W_kv[:, kc, h, 0:128], ckvTn[:, kc, :], start=(kc == 0), stop=(kc == 1))
        acopy(KTn.re("p a b -> p (a b)"), banks[3])
        for kp in range(2):
            mset(cptb[kp], 0.0)
        for s_ in range(4):
            for kt in range(8):
                b = kt % 2
                r0 = s_ * 1024 + kt * 128
                ld(cat[b], cckv_d[r0:r0 + 128, :]); ld(cpt[b], ckpe_d[r0:r0 + 128, :])
                vcopy(catb[b], cat[b], eng="pool"); vcopy(cptb[b][:, 0:64], cpt[b], eng="pool")
                bb = banks[1].cast(BF16)
                for kc in range(2):
                    tr(bb[:, kc * 128:(kc + 1) * 128], catb[b][:, kc * 128:(kc + 1) * 128], identb)
                tr(bb[:, 256:384], cptb[b], identb)
                acopy(ckvTc[:, :, kt * 128:(kt + 1) * 128], bb[:, 0:256].re("p (a b) -> p a b", a=2))
                acopy(kpeTc[0:64, kt * 128:(kt + 1) * 128], bb[0:64, 256:384])
            for h in range(4):
                for n in range(2):
                    for kc in range(2):
                        mm(banks[2], W_kv[:, kc, h, 0:128], ckvTc[:, kc, n * 512:(n + 1) * 512], start=(kc == 0), stop=(kc == 1))
                    acopy(KTc[:, h, n * 512:(n + 1) * 512], banks[2])
            for kt in range(8):
                for kc in range(2):
                    mm(banks[3], ckvTc[:, kc, kt * 128:(kt + 1) * 128], W_kv[:, kc, :, 128:256], start=(kc == 0), stop=(kc == 1))
                vcopy(Vc[:, kt, :], banks[3])
            for kc in range(2):
                mm(banks[3][0:32, :], ckvTn[:, kc, s_ * 32:(s_ + 1) * 32], W_kv[:, kc, :, 128:256], start=(kc == 0), stop=(kc == 1))
            vcopy(Vn[0:32, :], banks[3][0:32, :])
            qs = slice(s_ * 32, (s_ + 1) * 32)
            for h in range(4):
                for n in range(2):
                    mm(SC[0:32, n * 512:(n + 1) * 512], QTn[:, isam, h, qs], KTc[:, h, n * 512:(n + 1) * 512], start=True, stop=False)
                    mm(SC[0:32, n * 512:(n + 1) * 512], QTr[0:64, isam, h, qs], kpeTc[0:64, n * 512:(n + 1) * 512], start=False, stop=True)
                mm(banks[4][0:32, 0:32], QTn[:, isam, h, qs], KTn[:, h, qs], start=True, stop=False)
                mm(banks[4][0:32, 0:32], QTr[0:64, isam, h, qs], kpeTn[0:64, qs], start=False, stop=True)
                acopy(scs[0:32, 0:1024], SC[0:32, :])
                acopy(scs[0:32, 1024:1056], banks[4][0:32, 0:32])
                red(mxs[0:32, :], scs[0:32, :], ALU.max)
                ts(negs[0:32, :], mxs[0:32, :], -1.0, None, ALU.mult)
                mset(Ps[0:32, 1024:1152], 0.0)
                act(Ps[0:32, 0:1056], scs[0:32, :], AF.Exp, bias=negs[0:32, :], accum=sums[0:32, :])
                pt_b = banks[5].cast(BF16)
                for kt in range(9):
                    tr(pt_b[:, kt * 32:(kt + 1) * 32], Ps[0:32, kt * 128:(kt + 1) * 128], identb[0:32, 0:32])
                vcopy(PTs.re("p a b -> p (a b)"), pt_b[:, 0:288])
                ov = banks[4][0:32, 128:256]
                for kt in range(8):
                    mm(ov, PTs[:, kt, :], Vc[:, kt, h * 128:(h + 1) * 128], start=(kt == 0), stop=False)
                mm(ov, PTs[0:32, 8, :], Vn[0:32, h * 128:(h + 1) * 128], start=False, stop=True)
                recip(sums[0:32, :], sums[0:32, :])
                ts(osb[0:32, h * 128:(h + 1) * 128], ov, sums[0:32, :], None, ALU.mult)
            ld(Osam[s_ * 32:(s_ + 1) * 32, :], osb[0:32, :])

        barrier()
        bump[0] = mark_after_mixer_state

        W_glu = alloc([128, 4, D], BF16, "W_glu"); W_o = alloc([128, 8, D], BF16, "W_o")
        ldw(W_glu, w_glu_d.re("(kt p) n -> p kt n", p=128)); ldw(W_o, w_o_d.re("(kt p) n -> p kt n", p=128))
        gos = alloc([128, 512], F32, "gos"); gom = alloc([128, 512], F32, "gom"); ld(gos, gos_d); ld(gom, gom_d)
        lng = alloc([128, D], F32, "lng"); lnb = alloc([128, D], F32, "lnb")
        ld(lng, lng_d[:, 0:D]); ld(lnb, lnb_d[:, 0:D])
        btab2 = Buf("tab2")
        for nm_ in ("S_t", "C_t", "R_t"):
            v_ = alloc([128, 16, 128], F32, nm_ + "2"); v_.b = btab2
            ld(v_.re("p a b -> p (a b)"), tab_d[nm_]); ssmx[nm_] = v_
        for nm_ in ("WBre", "WBim"):
            v_ = alloc([128, 16, 128], BF16, nm_ + "2")
            ld(v_.re("p a b -> p (a b)"), tab_d[nm_]); ssmx[nm_] = v_
        for nm_, src_ in (("Cre", cre_d), ("Cimn", cimn_d), ("Dblk", dblk_d)):
            v_ = alloc([128, 16, 32], BF16, nm_)
            ldw(v_.re("p a b -> p (a b)"), src_); ssmx[nm_] = v_
        ssmx["m4"] = [[alloc([128, 256], F32) for _ in range(4)] for _ in range(2)]
        ssmx["wri"] = [[alloc([128, 256], F32) for _ in range(2)] for _ in range(2)]
        ssmx["zri"] = [[alloc([128, 2, 128], F32) for _ in range(2)] for _ in range(2)]
        ssmx["xs4"] = [alloc([128, 16], F32) for _ in range(4)]
        ssmx["zl_all"] = [alloc([128, 4, 16], F32) for _ in range(2)]
        ssmx["dmd"] = [[alloc([128, 256], F32) for _ in range(4)]] * 2
        ssmx["xrb"] = [alloc([128, 2, 128], BF16) for _ in range(2)]
        ssmx["xib"] = [alloc([128, 2, 128], BF16) for _ in range(2)]
        S0 = alloc([128, 2, 4, 16], F32, "S0"); ld(S0.re("p a b c -> p (a b c)"), s0_d)
        Sfin = alloc([128, 2, 4, 16], F32, "Sfin")
        xt = [alloc([128, D], F32, "axt%d" % i) for i in range(NB)]
        xT = [alloc([128, 8, 128], BF16, "axT%d" % i) for i in range(NB)]
        uT = [alloc([128, 4, 128], BF16, "auT%d" % i) for i in range(NB)]
        ysq = alloc([128, 512], F32, "ysq"); yt = alloc([128, 512], F32, "yt"); ysg = alloc([128, 512], F32, "ysg")
        glb = alloc([128, 512], BF16, "glb"); gT = alloc([128, 4, 128], BF16, "gT")
        sg2 = ysq; osf = yt
        mixb = alloc([128, D], BF16, "mixb"); mixT = alloc([128, 8, 128], BF16, "mixT")
        rl = alloc([128, 4], F32, "rl"); omf = alloc([128, 4, 128], F32, "omf")
        ss_a = alloc([128, 1], F32); sq = ysg
        rres = alloc([128, D], F32, "rres"); hout = alloc([128, D], F32, "hout")
        stats = alloc([128, 12], F32); mvv = alloc([128, 2], F32)

        def pre_stages(i):
            b = i % NB
            yb = banks[3] if i % 2 == 0 else banks[0]

            def p0():
                load_xT(xo[i * 128:(i + 1) * 128, :], xt[b], xT[b], banks[1])
                for kt in range(4):
                    for k in range(8):
                        mm(banks[1][:, kt * 128:(kt + 1) * 128], W_in_u[:, k, kt * 128:(kt + 1) * 128], xT[b][:, k, :], start=(k == 0), stop=(k == 7))
                acopy(uT[b].re("p a b -> p (a b)"), banks[1])

            if i < 16:
                rr_ = ssm_rounds(uT[b], (banks[4], banks[5]), lambda gp, s_, part, i=i: Xown[:, i, part * 16 + gp:part * 16 + gp + 1], 1, ybank=yb)
            else:
                rr_ = ssm_rounds(uT[b], (banks[4], banks[5]), lambda gp, s_, part: S0[:, part, s_, gp:gp + 1], 4, ybank=yb,
                                 last_out=lambda s_: (Sfin[:, 0, s_, :], Sfin[:, 1, s_, :]))
                rr_.append(lambda: ld(ssms_o, Sfin.re("p a b c -> p (a b c)")))
            return [p0] + rr_

        def post_stages(i):
            b = i % NB
            yb = banks[3] if i % 2 == 0 else banks[0]
            xt_ = xt[b]

            def q0():
                act(ysq, yb, AF.Square)
                ts(yt, ysq, 0.044715, 1.0, ALU.mult, ALU.add)
                tt(yt, yt, yb, ALU.mult)

            def q1():
                act(ysg, yt, AF.Sigmoid, scale=1.5957691216057308)
                tt(glb, ysg, yb, ALU.mult)

            def q2():
                transpose_to(gT, glb, 4, banks[2], eng="act")

            def q3():
                for nb in range(2):
                    for k in range(4):
                        mm(SC[:, nb * 512:(nb + 1) * 512], gT[:, k, :], W_glu[:, k, nb * 512:(nb + 1) * 512], start=(k == 0), stop=(k == 3))

            def q4():
                act(sg2, SC[:, 512:1024], AF.Sigmoid)
                tt(osf, sg2, SC[:, 0:512], ALU.mult)

            def q5():
                rmsnorm(mixb[:, 0:512], osf, 512, gos, sq, ss_a)

            def q6():
                if i < 16:
                    recip(rl, l_run[:, i * 4:(i + 1) * 4])
                    tt(omf, Oacc[:, i, :, :], rl.un(2).bc([128, 4, 128]), ALU.mult)
                    rmsnorm(mixb[:, 512:1024], omf.re("p a b -> p (a b)"), 512, gom, sq, ss_a)
                else:
                    rmsnorm(mixb[:, 512:1024], Osam, 512, gom, sq, ss_a)

            def q7():
                for half in range(2):
                    transpose_to(mixT[:, half * 4:(half + 1) * 4, :], mixb[:, half * 512:(half + 1) * 512], 4, banks[2], eng=("act" if half == 0 else "dve"))

            def q8():
                for nb in range(2):
                    for k in range(8):
                        mm(SC[:, nb * 512:(nb + 1) * 512], mixT[:, k, :], W_o[:, k, nb * 512:(nb + 1) * 512], start=(k == 0), stop=(k == 7))

            def q9():
                stt(rres, xt_, ALPHA, SC, ALU.mult, ALU.add)
                layernorm(hout, rres, lng, lnb, stats, mvv)
                ld(h1_d[i * 128:(i + 1) * 128, :], hout)

            return [q0, q1, q2, q3, q4, q5, q6, q7, q8, q9]

        prev_post = []
        for i in range(NOWN + 1):
            pre = pre_stages(i) if i < NOWN else []
            n_ = max(len(pre), len(prev_post))
            for k_ in range(n_):
                if k_ < len(pre):
                    pre[k_]()
                if k_ < len(prev_post):
                    prev_post[k_]()
            prev_post = post_stages(i) if i < NOWN else []

        barrier()
        bump[0] = mark_after_mem

        W_xq = alloc([128, 8, D], BF16, "W_xq"); W_xo = alloc([128, 8, D], BF16, "W_xo")
        ldw(W_xq, w_xq_d.re("(kt p) n -> p kt n", p=128)); ldw(W_xo, w_xo_d.re("(kt p) n -> p kt n", p=128))
        lng = alloc([128, D], F32, "lng2"); lnb = alloc([128, D], F32, "lnb2")
        ld(lng, lng_d[:, D:2 * D]); ld(lnb, lnb_d[:, D:2 * D])
        hin = [alloc([128, D], F32, "bh%d" % i) for i in range(NB)]
        hb2 = [alloc([128, D], BF16, "bhb%d" % i) for i in range(2)]; hT2 = [alloc([128, 8, 128], BF16, "bhT%d" % i) for i in range(2)]
        qxb2 = [alloc([128, D], BF16, "qxb%d" % i) for i in range(2)]; qxT2 = [alloc([128, 8, 128], BF16, "qxT%d" % i) for i in range(2)]
        mx42 = [alloc([128, 4], F32) for _ in range(2)]; neg42 = [alloc([128, 4], F32) for _ in range(2)]; sum42 = [alloc([128, 4], F32) for _ in range(2)]
        Px2 = [alloc([128, 4, 256], BF16, "Px%d" % i) for i in range(2)]; PxT2 = [alloc([128, 8, 128], BF16, "PxT%d" % i) for i in range(2)]
        oxb2 = [alloc([128, D], BF16, "oxb%d" % i) for i in range(2)]; oxT2 = [alloc([128, 8, 128], BF16, "oxT%d" % i) for i in range(2)]
        rres2 = [alloc([128, D], F32, "brres%d" % i) for i in range(2)]; hout2 = [alloc([128, D], F32, "bhout%d" % i) for i in range(2)]
        stats2 = [alloc([128, 12], F32) for _ in range(2)]; mvv2 = [alloc([128, 2], F32) for _ in range(2)]
        hb_ = hb2[0]; hT = hT2[0]; qxb = qxb2[0]; qxT = qxT2[0]; mx4 = mx42[0]; neg4 = neg42[0]; sum4 = sum42[0]
        Px = Px2[0]; PxT = PxT2[0]; oxb = oxb2[0]; oxT = oxT2[0]; rres = rres2[0]; hout = hout2[0]; stats = stats2[0]; mvv = mvv2[0]
        cmk = [alloc([128, D], F32, "cmk%d" % i) for i in range(2)]
        cmkb = [alloc([128, D], BF16, "cmkb%d" % i) for i in range(1)]
        mkTs4 = [alloc([128, 4, 2, 256], BF16, "mkTs%d" % i) for i in range(4)]; mvs4 = [alloc([128, 2, D], BF16, "mvs%d" % i) for i in range(4)]
        scx = alloc([128, 8], F32, "scx"); oxs = alloc([128, D], BF16, "oxs")

        def xa_stages(i, p):
            A_ = banks[p]; C_ = dbl[1 + p]
            Ab = A_.cast(BF16)

            def tr8(src):
                for k in range(8):
                    tr(Ab[:, k * 128:(k + 1) * 128], src[:, k * 128:(k + 1) * 128], identb)

            def ev8(dst):
                acopy(dst[:, 0:4, :], Ab[:, 0:512].re("p (k n) -> p k n", k=4))
                vcopy(dst[:, 4:8, :], Ab[:, 512:1024].re("p (k n) -> p k n", k=4))

            def t0():
                ld(hin[p], h1_d[i * 128:(i + 1) * 128, :])
                vcopy(hb2[p], hin[p], eng="pool")

            def t3():
                for nb in range(2):
                    for k in range(8):
                        mm(C_[:, nb * 512:(nb + 1) * 512], hT2[p][:, k, :], W_xq[:, k, nb * 512:(nb + 1) * 512], start=(k == 0), stop=(k == 7))

            def t4():
                for nb in range(2):
                    P.op("act", lambda e, nb=nb: e.mul(out=qxb2[p].ap[:, nb * 512:(nb + 1) * 512], in_=C_.ap[:, nb * 512:(nb + 1) * 512], mul=X_SCALE),
                         reads=[C_.b], writes=[qxb2[p].b])

            def t7():
                for h in range(4):
                    for et in range(2):
                        mm(C_[:, h * 256:(h + 1) * 256], qxT2[p][:, h * 2 + et, :], mkT[:, h, et, :], start=(et == 0), stop=(et == 1))

            def t8():
                red(mx42[p], C_.re("p (h m) -> p h m", h=4), ALU.max)
                ts(neg42[p], mx42[p], -1.0, None, ALU.mult)
                for h in range(4):
                    act(Px2[p][:, h, :], C_[:, h * 256:(h + 1) * 256], AF.Exp, bias=neg42[p][:, h:h + 1], accum=sum42[p][:, h:h + 1])

            def t11():
                for h in range(4):
                    for mt in range(2):
                        mm(C_[:, h * 256:(h + 1) * 256], PxT2[p][:, h * 2 + mt, :], mvb[:, mt, h * 256:(h + 1) * 256], start=(mt == 0), stop=(mt == 1))

            def t12():
                recip(sum42[p], sum42[p])
                tt(oxb2[p].re("p (h e) -> p h e", h=4), C_.re("p (h e) -> p h e", h=4), sum42[p].un(2).bc([128, 4, 256]), ALU.mult)

            def t15():
                for nb in range(2):
                    for k in range(8):
                        mm(C_[:, nb * 512:(nb + 1) * 512], oxT2[p][:, k, :], W_xo[:, k, nb * 512:(nb + 1) * 512], start=(k == 0), stop=(k == 7))

            def t16():
                stt(rres2[p], hin[p], ALPHA, C_, ALU.mult, ALU.add)
                layernorm(hout2[p], rres2[p], lng, lnb, stats2[p], mvv2[p])
                ld(h2_d[i * 128:(i + 1) * 128, :], hout2[p])

            return [t0, lambda: tr8(hb2[p]), lambda: ev8(hT2[p]), t3, t4, lambda: tr8(qxb2[p]), lambda: ev8(qxT2[p]), t7, t8,
                    lambda: tr8(Px2[p].re("p a b -> p (a b)")), lambda: ev8(PxT2[p]), t11, t12, lambda: tr8(oxb2[p]), lambda: ev8(oxT2[p]), t15, t16]

        def prep_items():
            items = []
            for s_ in range(4):
                for mt in range(2):
                    r0 = s_ * 256 + mt * 128

                    def l0(r0=r0):
                        ld(cmk[0], cmk_d[r0:r0 + 128, :]); ld(cmk[1], cmv_d[r0:r0 + 128, :])

                    def l1(s_=s_, mt=mt):
                        vcopy(cmkb[0], cmk[0], eng="pool")
                        vcopy(mvs4[s_][:, mt, :], cmk[1], eng="pool")

                    def l2(s_=s_, mt=mt):
                        for half in range(2):
                            bb = banks[6 + half].cast(BF16)
                            for k in range(4):
                                kk = half * 4 + k
                                tr(bb[:, k * 128:(k + 1) * 128], cmkb[0][:, kk * 128:(kk + 1) * 128], identb)

                    def l3(s_=s_, mt=mt):
                        for half in range(2):
                            bb = banks[6 + half].cast(BF16)
                            acopy(mkTs4[s_][:, half * 2:half * 2 + 2, :, mt * 128:(mt + 1) * 128].re("p h e m -> p (h e) m"), bb[:, 0:512].re("p (k m) -> p k m", k=4))

                    items += [l0, l1, l2, l3]
            return items

        pitems = prep_items()
        pi_ = 0
        slot = 0
        for ip in range(8):
            sa = xa_stages(2 * ip, 0); sb_ = xa_stages(2 * ip + 1, 1)
            for a_, b_ in zip(sa, sb_):
                a_(); b_()
                slot += 1
                if slot % 4 == 0 and pi_ < len(pitems):
                    pitems[pi_](); pi_ += 1
        while pi_ < len(pitems):
            pitems[pi_](); pi_ += 1

        for i in range(16, NOWN):
            b = i % NB
            ld(hin[b], h1_d[i * 128:(i + 1) * 128, :])
            vcopy(hb_, hin[b], eng="pool")
            for half in range(2):
                transpose_to(hT[:, half * 4:(half + 1) * 4, :], hb_[:, half * 512:(half + 1) * 512], 4, banks[0], eng=("act" if half == 0 else "dve"))
            for nb in range(2):
                for k in range(8):
                    mm(banks[1 + nb], hT[:, k, :], W_xq[:, k, nb * 512:(nb + 1) * 512], start=(k == 0), stop=(k == 7))
                P.op("act", lambda e, nb=nb: e.mul(out=qxb.ap[:, nb * 512:(nb + 1) * 512], in_=banks[1 + nb].ap, mul=X_SCALE), reads=[banks[1 + nb].b], writes=[qxb.b])
            for half in range(2):
                transpose_to(qxT[:, half * 4:(half + 1) * 4, :], qxb[:, half * 512:(half + 1) * 512], 4, banks[3], eng=("act" if half == 0 else "dve"))
            if i < 16:
                for h in range(4):
                    for et in range(2):
                        mm(SC[:, h * 256:(h + 1) * 256], qxT[:, h * 2 + et, :], mkT[:, h, et, :], start=(et == 0), stop=(et == 1))
                red(mx4, SC.re("p (h m) -> p h m", h=4), ALU.max)
                ts(neg4, mx4, -1.0, None, ALU.mult)
                for h in range(4):
                    act(Px[:, h, :], SC[:, h * 256:(h + 1) * 256], AF.Exp, bias=neg4[:, h:h + 1], accum=sum4[:, h:h + 1])
                for half in range(2):
                    transpose_to(PxT[:, half * 4:(half + 1) * 4, :], Px.re("p a b -> p (a b)")[:, half * 512:(half + 1) * 512], 4, banks[4], eng=("act" if half == 0 else "dve"))
                for h in range(4):
                    for mt in range(2):
                        mm(SC[:, h * 256:(h + 1) * 256], PxT[:, h * 2 + mt, :], mvb[:, mt, h * 256:(h + 1) * 256], start=(mt == 0), stop=(mt == 1))
                recip(sum4, sum4)
                tt(oxb.re("p (h e) -> p h e", h=4), SC.re("p (h e) -> p h e", h=4), sum4.un(2).bc([128, 4, 256]), ALU.mult)
            else:
                for s_ in range(4):
                    qs = slice(s_ * 32, (s_ + 1) * 32)
                    mkTs = mkTs4[s_]; mvs = mvs4[s_]
                    for h in range(4):
                        for et in range(2):
                            mm(SC[0:32, h * 256:(h + 1) * 256], qxT[:, h * 2 + et, qs], mkTs[:, h, et, :], start=(et == 0), stop=(et == 1))
                    red(mx4[0:32, :], SC[0:32, :].re("p (h m) -> p h m", h=4), ALU.max)
                    ts(neg4[0:32, :], mx4[0:32, :], -1.0, None, ALU.mult)
                    for h in range(4):
                        act(Px[0:32, h, :], SC[0:32, h * 256:(h + 1) * 256], AF.Exp, bias=neg4[0:32, h:h + 1], accum=sum4[0:32, h:h + 1])
                    bb = banks[4].cast(BF16)
                    for k in range(8):
                        tr(bb[:, k * 32:(k + 1) * 32], Px.re("p a b -> p (a b)")[0:32, k * 128:(k + 1) * 128], identb[0:32, 0:32])
                    vcopy(PxT[:, :, 0:32], bb[:, 0:256].re("p (k q) -> p k q", k=8))
                    for h in range(4):
                        for mt in range(2):
                            mm(SC[0:32, h * 256:(h + 1) * 256], PxT[:, h * 2 + mt, 0:32], mvs[:, mt, h * 256:(h + 1) * 256], start=(mt == 0), stop=(mt == 1))
                    recip(sum4[0:32, :], sum4[0:32, :])
                    tt(oxs[0:32, :].re("p (h e) -> p h e", h=4), SC[0:32, :].re("p (h e) -> p h e", h=4), sum4[0:32, :].un(2).bc([32, 4, 256]), ALU.mult)
                    ld(oxb[s_ * 32:(s_ + 1) * 32, :], oxs[0:32, :])
            for half in range(2):
                transpose_to(oxT[:, half * 4:(half + 1) * 4, :], oxb[:, half * 512:(half + 1) * 512], 4, banks[5], eng=("act" if half == 0 else "dve"))
            for nb in range(2):
                for k in range(8):
                    mm(banks[1 + nb], oxT[:, k, :], W_xo[:, k, nb * 512:(nb + 1) * 512], start=(k == 0), stop=(k == 7))
            for nb in range(2):
                stt(rres[:, nb * 512:(nb + 1) * 512], hin[b][:, nb * 512:(nb + 1) * 512], ALPHA, banks[1 + nb], ALU.mult, ALU.add)
            layernorm(hout, rres, lng, lnb, stats, mvv)
            ld(h2_d[i * 128:(i + 1) * 128, :], hout)

        barrier()
        bump[0] = mark_after_mem

        W1 = alloc([128, 8, 4096], BF16, "W1"); W2 = alloc([128, 32, D], BF16, "W2")
        for c in range(2):
            ldw(W1[:, :, c * 2048:(c + 1) * 2048], w_ff1_d.re("(kt p) n -> p kt n", p=128)[:, :, c * 2048:(c + 1) * 2048])
        for c in range(4):
            ldw(W2[:, c * 8:(c + 1) * 8, :], w_ff2_d.re("(kt p) n -> p kt n", p=128)[:, c * 8:(c + 1) * 8, :])
        lng = alloc([128, D], F32, "lng3"); lnb = alloc([128, D], F32, "lnb3")
        ld(lng, lng_d[:, 2 * D:3 * D]); ld(lnb, lnb_d[:, 2 * D:3 * D])
        hin = alloc([128, D], F32, "ch")
        hb_ = alloc([128, D], BF16, "chb"); hT = alloc([128, 8, 512], BF16, "chT")
        zr = [alloc([128, 512], F32, "zr%d" % i) for i in range(2)]; zT = alloc([128, 32, 512], BF16, "zT")
        rres = alloc([128, D], F32, "crres"); hout = alloc([128, D], F32, "chout")
        stats = alloc([128, 12], F32); mvv = alloc([128, 2], F32)
        groups = [list(range(g_ * 4, g_ * 4 + 4)) for g_ in range(4)] + [[16]]
        for grp in groups:
            nt = len(grp)
            for q_, i in enumerate(grp):
                ld(hin, h2_d[i * 128:(i + 1) * 128, :])
                vcopy(hb_, hin, eng="pool")
                for half in range(2):
                    transpose_to(hT[:, half * 4:(half + 1) * 4, q_ * 128:(q_ + 1) * 128], hb_[:, half * 512:(half + 1) * 512], 4, banks[0], eng=("act" if half == 0 else "dve"))
            W_ = nt * 128
            for f in range(32):
                bk = banks[1 + f % 2]
                for k in range(8):
                    mm(bk[:, 0:W_], W1[:, k, f * 128:(f + 1) * 128], hT[:, k, 0:W_], start=(k == 0), stop=(k == 7))
                act(zr[f % 2][:, 0:W_], bk[:, 0:W_], AF.Relu)
                tt(zT[:, f, 0:W_], zr[f % 2][:, 0:W_], zr[f % 2][:, 0:W_], ALU.mult)
            for q_, i in enumerate(grp):
                for nb in range(2):
                    for f in range(32):
                        mm(SC[:, nb * 512:(nb + 1) * 512], zT[:, f, q_ * 128:(q_ + 1) * 128], W2[:, f, nb * 512:(nb + 1) * 512], start=(f == 0), stop=(f == 31))
                ld(hin, h2_d[i * 128:(i + 1) * 128, :])
                stt(rres, hin, ALPHA, SC, ALU.mult, ALU.add)
                layernorm(hout, rres, lng, lnb, stats, mvv)
                ld(y_o[i * 128:(i + 1) * 128, :], hout)

        P.emit(st)
    return nc


def _lay_gp(a):
    sh = a.shape[2:]
    n = len(sh)
    return np.ascontiguousarray(a.reshape(16, 2, 64, *sh).transpose(1, 2, 0, *range(3, 3 + n)).reshape(128, 16, *sh))


def _bc(v, n=128):
    return np.ascontiguousarray(np.broadcast_to(np.asarray(v, np.float32).reshape(1, -1), (n, np.asarray(v).size)))


def _rope_tables(pos):
    inv = (10000.0 ** (-np.arange(32, dtype=np.float32) / 32)).astype(np.float32)
    ang = pos.astype(np.float32)[:, None] * inv[None, :]
    c = np.cos(ang).astype(np.float32)
    s = np.sin(ang).astype(np.float32)
    return np.concatenate([c, c], 1), np.concatenate([-s, s], 1)


_NC_CACHE = {}


def kernel(x_prompt, x_sample, mem_prompt, cache_mla_ckv, cache_mla_kpe, state_ssm_re, state_ssm_im,
           cache_mem_k, cache_mem_v, w_in, g_q, w_q_up, g_kv, w_kv_up, a_re, a_im, b_re, b_im, c_re, c_im,
           d_skip, log_dt, w_glu, g_out_ssm, g_out_mla, w_o, w_xq, w_xk, w_xv, w_xo, w_ff1, w_ff2, ln_g, ln_b):
    f = lambda a: np.ascontiguousarray(np.asarray(a, dtype=np.float32))
    x_prompt = f(x_prompt); x_sample = f(x_sample)
    xp = x_prompt[0]
    cre = np.zeros((128, 16, 32), np.float32); cimn = np.zeros((128, 16, 32), np.float32)
    cr = f(c_re)[0].reshape(16, 2, 16, 64); ci = f(c_im)[0].reshape(16, 2, 16, 64)
    for g2 in range(2):
        cre[g2 * 64:(g2 + 1) * 64, :, g2 * 16:(g2 + 1) * 16] = cr[:, g2].transpose(2, 0, 1)
        cimn[g2 * 64:(g2 + 1) * 64, :, g2 * 16:(g2 + 1) * 16] = -ci[:, g2].transpose(2, 0, 1)
    dblk = np.zeros((128, 16, 32), np.float32)
    dd = f(d_skip)[0].reshape(512)
    for gp in range(16):
        for c in range(32):
            ch = gp * 32 + c
            dblk[ch % 128, gp, c] = dd[ch]
    pos_p = np.arange(16384)
    rcp, rsp = _rope_tables(pos_p)
    common = {
        "xp": xp, "mem": f(mem_prompt)[0], "ident": np.eye(128, dtype=np.float32),
        "w_in": f(w_in)[0], "w_q": f(w_q_up)[0].reshape(384, 768), "w_kv": f(w_kv_up)[0].reshape(256, 1024),
        "w_glu": f(w_glu)[0], "w_o": f(w_o)[0], "w_xq": f(w_xq)[0].reshape(D, D), "w_xk": f(w_xk)[0].reshape(D, D),
        "w_xv": f(w_xv)[0].reshape(D, D), "w_xo": f(w_xo)[0].reshape(D, D), "w_ff1": f(w_ff1)[0], "w_ff2": f(w_ff2)[0],
        "gq": _bc(f(g_q)[0]), "gkv": _bc(f(g_kv)[0]), "gos": _bc(f(g_out_ssm)[0]), "gom": _bc(f(g_out_mla)[0]),
        "lng": _bc(f(ln_g)[0].reshape(-1)), "lnb": _bc(f(ln_b)[0].reshape(-1)),
        "ropec_p": rcp, "ropes_p": rsp,
        "ar": _lay_gp(f(a_re)[0]), "ai": _lay_gp(f(a_im)[0]),
        "ldt": _lay_gp(np.ascontiguousarray(np.broadcast_to(f(log_dt)[0][:, None], (32, 64)))),
        "bre": _lay_gp(f(b_re)[0]).reshape(128, 256), "bim": _lay_gp(f(b_im)[0]).reshape(128, 256),
        "jj": _bc(np.arange(1, 129, dtype=np.float32)), "jj2": _bc(127.0 - np.arange(128, dtype=np.float32)),
        "cre": cre.reshape(128, 512), "cimn": cimn.reshape(128, 512), "dblk": dblk.reshape(128, 512),
    }
    qi = np.arange(128)[:, None]
    in_maps = []
    for c in range(NCORES):
        tiles = [8 * i + c for i in range(16)]
        xo = np.concatenate([xp[t * 128:(t + 1) * 128] for t in tiles] + [x_sample[4 * c:4 * c + 4].reshape(128, D)], 0)
        pos_o = np.concatenate([np.arange(t * 128, (t + 1) * 128) for t in tiles] + [np.tile(1024 + np.arange(32), 4)])
        rco, rso = _rope_tables(pos_o)
        kj = np.arange(1024)[None, :]
        vis = ((kj // 128) < c) | (((kj // 128) == c) & (((kj % 128) // 64) <= (qi // 64)))
        maskd = np.where(vis, 0.0, NEG).astype(np.float32)
        onehot = np.zeros((128, 8), np.float32); onehot[:, c] = 1.0
        s0 = np.stack([_lay_gp(f(state_ssm_re)[0, 4 * c + s]) for s in range(4)], 1)
        s0i = np.stack([_lay_gp(f(state_ssm_im)[0, 4 * c + s]) for s in range(4)], 1)
        m = dict(common)
        m.update({
            "xo": np.ascontiguousarray(xo), "ropec_o": rco, "ropes_o": rso, "maskd": maskd, "onehot": onehot,
            "s0": np.ascontiguousarray(np.stack([s0, s0i], 1).reshape(128, 128)),
            "cckv": f(cache_mla_ckv)[0, 4 * c:4 * c + 4].reshape(4096, 256),
            "ckpe": f(cache_mla_kpe)[0, 4 * c:4 * c + 4].reshape(4096, 64),
            "cmk": f(cache_mem_k)[0, 4 * c:4 * c + 4].reshape(1024, D),
            "cmv": f(cache_mem_v)[0, 4 * c:4 * c + 4].reshape(1024, D),
        })
        in_maps.append(m)
    if "nc" not in _NC_CACHE:
        _NC_CACHE["nc"] = build_nc()
    res = run_bass_kernel_spmd(_NC_CACHE["nc"], in_maps, core_ids=list(range(NCORES)))
    R = res.results
    y_p = np.zeros((1, 16384, D), np.float32); y_s = np.zeros((32, 32, D), np.float32)
    ckv_s = np.zeros((1, 32, 32, 256), np.float32); kpe_s = np.zeros((1, 32, 32, 64), np.float32)
    sre_s = np.zeros((1, 32, 32, 64), np.float32); sim_s = np.zeros((1, 32, 32, 64), np.float32)

    def unlay(a):
        return a.reshape(2, 64, 16).transpose(2, 0, 1).reshape(32, 64)

    for c in range(NCORES):
        yo = R[c]["y_o"]
        for i in range(16):
            t = 8 * i + c
            y_p[0, t * 128:(t + 1) * 128] = yo[i * 128:(i + 1) * 128]
        y_s[4 * c:4 * c + 4] = yo[16 * 128:].reshape(4, 32, D)
        ckv_s[0, 4 * c:4 * c + 4] = R[c]["ckvs_o"].reshape(4, 32, 256)
        kpe_s[0, 4 * c:4 * c + 4] = R[c]["kpes_o"].reshape(4, 32, 64)
        sf = R[c]["ssms_o"].reshape(128, 2, 4, 16)
        for s in range(4):
            sre_s[0, 4 * c + s] = unlay(sf[:, 0, s, :])
            sim_s[0, 4 * c + s] = unlay(sf[:, 1, s, :])
    r0 = R[0]
    ckv_p = r0["ckv_o"].reshape(1, 1, 16384, 256); kpe_p = r0["kpe_o"].reshape(1, 1, 16384, 64)
    sp = r0["ssmp_o"]
    sre_p = unlay(sp[:, 0:16]).reshape(1, 1, 32, 64); sim_p = unlay(sp[:, 16:32]).reshape(1, 1, 32, 64)
    mk_p = r0["memk_o"].reshape(1, 1, 256, 4, 256); mv_p = r0["memv_o"].reshape(1, 1, 256, 4, 256)
    return (y_p, y_s, ckv_p, kpe_p, sre_p, sim_p, mk_p, mv_p, ckv_s, kpe_s, sre_s, sim_s)
```

```python
import math
from contextlib import ExitStack

import numpy as np
import concourse.bass as bass
import concourse.mybir as mybir
from concourse.bass_utils import run_bass_kernel_spmd

F32 = mybir.dt.float32
BF16 = mybir.dt.bfloat16
I32 = mybir.dt.int32
AF = mybir.ActivationFunctionType
ALU = mybir.AluOpType
AX = mybir.AxisListType

NCORES = 8
D = 1024
NTP = 128
NOWN = 17
EPS = 1e-5
ALPHA = 2.0 ** 0.25
MLA_SCALE = 192.0 ** -0.5
X_SCALE = 256.0 ** -0.5
TWO_PI = 2.0 * math.pi
NEG = -1e30
NRING = 24
COMPUTE = ("pe", "act", "dve", "pool")


class Buf:
    __slots__ = ("name", "lw", "rd")

    def __init__(self, name=""):
        self.name = name
        self.lw = None
        self.rd = []


def _flat(bs):
    out = []
    for b in bs:
        if isinstance(b, (tuple, list)):
            out.extend(b)
        else:
            out.append(b)
    return out


class Op:
    __slots__ = ("eng", "fn", "deps", "isdma", "flag", "cnt", "ring", "n")

    def __init__(self, eng, fn, isdma):
        self.eng = eng
        self.fn = fn
        self.isdma = isdma
        self.deps = set()
        self.flag = False
        self.cnt = 0
        self.ring = None
        self.n = 0


class Prog:
    def __init__(self, nc):
        self.nc = nc
        self.ops = {e: [] for e in ("pe", "act", "dve", "pool", "sp")}
        self.ndma = {e: 0 for e in self.ops}
        self.allops = []
        self.floor = []
        self.dmas_since = []

    def _add(self, eng, fn, reads, writes, isdma):
        op = Op(eng, fn, isdma)
        reads = _flat(reads)
        writes = _flat(writes)
        for f in self.floor:
            op.deps.add(f)
        for b in reads:
            if b.lw is not None:
                op.deps.add(b.lw)
        for b in writes:
            if b.lw is not None:
                op.deps.add(b.lw)
            for r in b.rd:
                op.deps.add(r)
        for b in reads:
            b.rd.append(op)
        for b in writes:
            b.lw = op
            b.rd = []
        op.deps.discard(op)
        if isdma:
            op.n = self.ndma[eng]
            self.ndma[eng] += 1
            self.dmas_since.append(op)
        self.ops[eng].append(op)
        self.allops.append(op)
        return op

    def op(self, eng, fn, reads=(), writes=()):
        return self._add(eng, fn, reads, writes, False)

    def dma(self, eng, fn, reads=(), writes=()):
        return self._add(eng, fn, reads, writes, True)

    def barrier(self, fn):
        op = Op("pool", fn, False)
        for f in self.floor:
            op.deps.add(f)
        for e in COMPUTE:
            for o in reversed(self.ops[e]):
                if not o.isdma:
                    op.deps.add(o)
                    break
        for o in self.dmas_since:
            op.deps.add(o)
        self.dmas_since = []
        self.ops["pool"].append(op)
        self.allops.append(op)
        self.floor = [op]

    def emit(self, stack):
        nc = self.nc
        for op in self.allops:
            for d in op.deps:
                if d.eng == "pe" and op.eng == "pe" and not d.isdma:
                    continue
                d.flag = True
        sems = {e: stack.enter_context(nc.semaphore("s_" + e)) for e in COMPUTE}
        rings = {}
        for e in self.ops:
            if self.ndma[e]:
                rings[e] = [stack.enter_context(nc.semaphore("r_%s_%d" % (e, i))) for i in range(NRING)]
        for e in self.ops:
            c = 0
            for op in self.ops[e]:
                if op.isdma:
                    op.ring = rings[e][op.n % NRING]
                    op.cnt = 16 * (op.n // NRING + 1)
                elif op.flag:
                    c += 1
                    op.cnt = c
        block = stack.enter_context(nc.Block())

        def run(e, h):
            waited = {}

            def wait(sem, val):
                k = id(sem)
                if waited.get(k, 0) >= val:
                    return
                waited[k] = val
                h.wait_ge(sem, val)

            for op in self.ops[e]:
                for d in op.deps:
                    if d.isdma:
                        wait(d.ring, d.cnt)
                    else:
                        if d.eng == "pe" and e == "pe":
                            continue
                        wait(sems[d.eng], d.cnt)
                if op.isdma and op.n >= NRING:
                    wait(op.ring, op.cnt - 16)
                ins = op.fn(h)
                if op.isdma:
                    ins.then_inc(op.ring, 16)
                elif op.flag:
                    ins.then_inc(sems[e], 1)
            if e in rings:
                n = self.ndma[e]
                for i in range(min(n, NRING)):
                    last = ((n - 1 - i) // NRING) * NRING + i
                    wait(rings[e][i], 16 * (last // NRING + 1))

        @block.tensor
        def _(h):
            run("pe", h)

        @block.scalar
        def _(h):
            run("act", h)

        @block.vector
        def _(h):
            run("dve", h)

        @block.gpsimd
        def _(h):
            run("pool", h)

        @block.sync
        def _(h):
            run("sp", h)


class V:
    __slots__ = ("ap", "b")

    def __init__(self, ap, b):
        self.ap = ap
        self.b = b

    def __getitem__(self, idx):
        return V(self.ap[idx], self.b)

    def re(self, pat, **kw):
        return V(self.ap.rearrange(pat, **kw), self.b)

    def cast(self, dt):
        return V(self.ap.bitcast(dt), self.b)

    def bc(self, shape):
        return V(self.ap.broadcast_to(shape), self.b)

    def un(self, ax):
        return V(self.ap.unsqueeze(ax), self.b)


def build_nc():
    nc = bass.Bass("TRN2", target_bir_lowering=False)
    P = Prog(nc)
    dram = {}

    def din(name, shape, dt=F32):
        t = nc.dram_tensor(name, list(shape), dt, kind="ExternalInput")
        dram[name] = V(t.ap(), Buf(name))
        return dram[name]

    def dout(name, shape):
        t = nc.dram_tensor(name, list(shape), F32, kind="ExternalOutput")
        dram[name] = V(t.ap(), Buf(name))
        return dram[name]

    def dscr(name, shape, dt=F32):
        t = nc.dram_tensor(name, list(shape), dt)
        return V(t.ap(), Buf(name))

    xp = din("xp", [NTP * 128, D])
    xo = din("xo", [NOWN * 128, D])
    mem = din("mem", [256, D])
    ident_d = din("ident", [128, 128])
    w_in_d = din("w_in", [D, 1216]); w_q_d = din("w_q", [384, 768]); w_kv_d = din("w_kv", [256, 1024])
    w_glu_d = din("w_glu", [512, 1024]); w_o_d = din("w_o", [D, D]); w_xq_d = din("w_xq", [D, D])
    w_xk_d = din("w_xk", [D, D]); w_xv_d = din("w_xv", [D, D]); w_xo_d = din("w_xo", [D, D])
    w_ff1_d = din("w_ff1", [D, 4096]); w_ff2_d = din("w_ff2", [4096, D])
    gq_d = din("gq", [128, 384]); gkv_d = din("gkv", [128, 256]); gos_d = din("gos", [128, 512]); gom_d = din("gom", [128, 512])
    lng_d = din("lng", [128, 3 * D]); lnb_d = din("lnb", [128, 3 * D])
    ropec_p = din("ropec_p", [NTP * 128, 64]); ropes_p = din("ropes_p", [NTP * 128, 64])
    ropec_o = din("ropec_o", [NOWN * 128, 64]); ropes_o = din("ropes_o", [NOWN * 128, 64])
    maskd_d = din("maskd", [128, 1024]); onehot_d = din("onehot", [128, 8])
    ar_d = din("ar", [128, 16]); ai_d = din("ai", [128, 16]); ldt_d = din("ldt", [128, 16])
    bre_d = din("bre", [128, 256]); bim_d = din("bim", [128, 256]); jj_d = din("jj", [128, 128]); jj2_d = din("jj2", [128, 128])
    cre_d = din("cre", [128, 512]); cimn_d = din("cimn", [128, 512]); dblk_d = din("dblk", [128, 512])
    s0_d = din("s0", [128, 128])
    cckv_d = din("cckv", [4 * 1024, 256]); ckpe_d = din("ckpe", [4 * 1024, 64])
    cmk_d = din("cmk", [4 * 256, D]); cmv_d = din("cmv", [4 * 256, D])

    y_o = dout("y_o", [NOWN * 128, D])
    ckv_o = dout("ckv_o", [NTP * 128, 256]); kpe_o = dout("kpe_o", [NTP * 128, 64])
    ssmp_o = dout("ssmp_o", [128, 32])
    memk_o = dout("memk_o", [256, D]); memv_o = dout("memv_o", [256, D])
    ckvs_o = dout("ckvs_o", [128, 256]); kpes_o = dout("kpes_o", [128, 64]); ssms_o = dout("ssms_o", [128, 128])

    h1_d = dscr("h1_d", [NOWN * 128, D]); h2_d = dscr("h2_d", [NOWN * 128, D])
    tab_d = {n_: dscr("tab_" + n_, [128, 2048]) for n_ in ("S_t", "C_t", "R_t")}
    tab_d["WBre"] = dscr("tab_WBre", [128, 2048], BF16); tab_d["WBim"] = dscr("tab_WBim", [128, 2048], BF16)

    with ExitStack() as st:
        ARENA_N = 52000
        arena = st.enter_context(nc.sbuf_tensor("arena", [128, ARENA_N], F32))
        bump = [0]

        def alloc(shape, dt=F32, name=""):
            n = 1
            for s_ in shape[1:]:
                n *= s_
            words = (n * (2 if dt == BF16 else 4) + 3) // 4
            words = (words + 7) // 8 * 8
            off = bump[0]
            bump[0] += words
            assert bump[0] <= ARENA_N, ("SBUF arena overflow", name, bump[0])
            ap = arena[:, off:off + words]
            if dt != F32:
                ap = ap.bitcast(dt)
            ap = ap[:, 0:n]
            if len(shape) > 2:
                names = " ".join("d%d" % i for i in range(len(shape) - 1))
                kw = {"d%d" % i: shape[i + 1] for i in range(len(shape) - 1)}
                ap = ap.rearrange("p (%s) -> p %s" % (names, names), **kw)
            return V(ap, Buf(name))

        banks = []
        dbl = []
        for d_ in range(4):
            t = st.enter_context(nc.psum_tensor("dbank%d" % d_, [128, 1024], F32))
            b0 = Buf("bank%d" % (2 * d_)); b1 = Buf("bank%d" % (2 * d_ + 1))
            banks.append(V(t[:, 0:512], b0)); banks.append(V(t[:, 512:1024], b1))
            dbl.append(V(t[:], (b0, b1)))
        SC = dbl[3]
        SCs = [dbl[3], dbl[0]]

        def mm(out, lhsT, rhs, start=True, stop=True):
            P.op("pe", lambda e: e.matmul(out.ap, lhsT=lhsT.ap, rhs=rhs.ap, start=start, stop=stop),
                 reads=[lhsT.b, rhs.b], writes=[out.b])

        def tr(out, in_, idt):
            P.op("pe", lambda e: e.transpose(out=out.ap, in_=in_.ap, identity=idt.ap), reads=[in_.b, idt.b], writes=[out.b])

        def act(out, in_, func, bias=None, scale=None, accum=None):
            kw = {}
            rd = [in_.b]
            wr = [out.b]
            if bias is not None:
                if isinstance(bias, V):
                    kw["bias"] = bias.ap; rd.append(bias.b)
                else:
                    kw["bias"] = bias
            if scale is not None:
                if isinstance(scale, V):
                    kw["scale"] = scale.ap; rd.append(scale.b)
                else:
                    kw["scale"] = scale
            if accum is not None:
                kw["accum_out"] = accum.ap; wr.append(accum.b)
            P.op("act", lambda e: e.activation(out=out.ap, in_=in_.ap, func=func, **kw), reads=rd, writes=wr)

        def acopy(out, in_):
            P.op("act", lambda e: e.copy(out=out.ap, in_=in_.ap), reads=[in_.b], writes=[out.b])

        def vcopy(out, in_, eng="dve"):
            P.op(eng, lambda e: e.tensor_copy(out=out.ap, in_=in_.ap), reads=[in_.b], writes=[out.b])

        def tt(out, in0, in1, op, eng="dve"):
            P.op(eng, lambda e: e.tensor_tensor(out=out.ap, in0=in0.ap, in1=in1.ap, op=op), reads=[in0.b, in1.b], writes=[out.b])

        def ts(out, in0, s1, s2, op0, op1=None, eng="dve"):
            rd = [in0.b]
            a1 = s1
            a2 = s2
            if isinstance(s1, V):
                a1 = s1.ap; rd.append(s1.b)
            if isinstance(s2, V):
                a2 = s2.ap; rd.append(s2.b)
            if op1 is None:
                P.op(eng, lambda e: e.tensor_scalar(out=out.ap, in0=in0.ap, scalar1=a1, scalar2=None, op0=op0), reads=rd, writes=[out.b])
            else:
                P.op(eng, lambda e: e.tensor_scalar(out=out.ap, in0=in0.ap, scalar1=a1, scalar2=a2, op0=op0, op1=op1), reads=rd, writes=[out.b])

        def stt(out, in0, scalar, in1, op0, op1):
            rd = [in0.b, in1.b]
            a = scalar
            if isinstance(scalar, V):
                a = scalar.ap; rd.append(scalar.b)
            P.op("dve", lambda e: e.scalar_tensor_tensor(out=out.ap, in0=in0.ap, scalar=a, in1=in1.ap, op0=op0, op1=op1), reads=rd, writes=[out.b])

        def red(out, in_, op, axis=AX.X):
            P.op("dve", lambda e: e.tensor_reduce(out=out.ap, in_=in_.ap, axis=axis, op=op), reads=[in_.b], writes=[out.b])

        def recip(out, in_):
            P.op("dve", lambda e: e.reciprocal(out=out.ap, in_=in_.ap), reads=[in_.b], writes=[out.b])

        def scan(out, d0, d1, init):
            P.op("dve", lambda e: e.tensor_tensor_scan(out=out.ap, data0=d0.ap, data1=d1.ap, initial=init.ap, op0=ALU.mult, op1=ALU.add),
                 reads=[d0.b, d1.b, init.b], writes=[out.b])

        def mset(out, val, eng="pool"):
            P.op(eng, lambda e: e.memset(out.ap, val), writes=[out.b])

        def ld(out, in_, eng="sp"):
            P.dma(eng, lambda e: e.dma_start(out=out.ap, in_=in_.ap), reads=[in_.b], writes=[out.b])

        def ldw(out, in_):
            P.dma("pool", lambda e: e.dma_start(out=out.ap, in_=in_.ap), reads=[in_.b], writes=[out.b])

        ident = alloc([128, 128], F32, "ident"); identb = alloc([128, 128], BF16, "identb")
        bar_scr = alloc([128, 8], F32, "barscr")
        ld(ident, ident_d); ldw(identb, ident_d)
        epsc = alloc([128, 1], F32, "epsc"); mset(epsc, EPS)

        def barrier():
            P.barrier(lambda e: e.memset(bar_scr.ap, 0.0))

        def load_xT(src_rows, xt, xT, bank):
            ld(xt, src_rows)
            for hb in range(2):
                for k in range(4):
                    kk = hb * 4 + k
                    tr(bank[:, k * 128:(k + 1) * 128], xt[:, kk * 128:(kk + 1) * 128], ident)
                src = bank.re("p (k n) -> p k n", k=4)
                if hb == 0:
                    acopy(xT[:, 0:4, :], src)
                else:
                    vcopy(xT[:, 4:8, :], src)

        def transpose_to(dst, src, ncol, bank, eng="act", rows=128):
            bb = bank.cast(BF16)
            for k in range(ncol):
                tr(bb[:, k * 128:k * 128 + rows], src[:, k * 128:(k + 1) * 128], identb[0:rows, 0:rows])
            s_ = bb[:, 0:ncol * 128].re("p (k n) -> p k n", k=ncol)[:, :, 0:rows]
            if eng == "act":
                acopy(dst, s_)
            else:
                vcopy(dst, s_)

        def rmsnorm(out, src, n, gtile, sq, ss):
            act(sq, src, AF.Square, accum=ss)
            act(ss, ss, AF.Ln, scale=1.0 / n, bias=epsc)
            act(ss, ss, AF.Exp, scale=-0.5)
            stt(out, src, ss, gtile, ALU.mult, ALU.mult)

        def layernorm(out, r, g, b, stats, mv):
            for c in range(2):
                P.op("dve", lambda e, c=c: e.bn_stats(out=stats.ap[:, c * 6:(c + 1) * 6], in_=r.ap[:, c * 512:(c + 1) * 512]), reads=[r.b], writes=[stats.b])
            P.op("dve", lambda e: e.bn_aggr(out=mv.ap[:, 0:2], in_=stats.ap[:, 0:12]), reads=[stats.b], writes=[mv.b])
            act(mv[:, 1:2], mv[:, 1:2], AF.Ln, bias=epsc)
            act(mv[:, 1:2], mv[:, 1:2], AF.Exp, scale=-0.5)
            ts(out, r, mv[:, 0:1], mv[:, 1:2], ALU.subtract, ALU.mult)
            tt(out, out, g, ALU.mult, eng="pool")
            tt(out, out, b, ALU.add, eng="pool")

        def rope(out, src, cc, ss_, tmp, nh, eng="dve"):
            ccb = cc.un(1).bc([128, nh, 64]); ssb = ss_.un(1).bc([128, nh, 64])
            tt(out, src, ccb, ALU.mult, eng=eng)
            tt(tmp[:, :, 0:32], src[:, :, 32:64], ssb[:, :, 0:32], ALU.mult, eng=eng)
            tt(tmp[:, :, 32:64], src[:, :, 0:32], ssb[:, :, 32:64], ALU.mult, eng=eng)
            tt(out, out, tmp, ALU.add, eng="pool")

        mkT = alloc([128, 4, 2, 256], BF16, "mkT")
        mvb = alloc([128, 2, D], BF16, "mvb")
        mark_after_mem = bump[0]
        btab = Buf("tab")
        W_in_u = alloc([128, 8, 512], BF16, "W_in_u"); W_in_kv = alloc([128, 8, 320], BF16, "W_in_kv")
        LrT = alloc([128, 16, 128], BF16, "LrT"); LiT = alloc([128, 16, 128], BF16, "LiT")
        BRb = alloc([128, 16, 32], F32, "BRb"); BIb = alloc([128, 16, 32], F32, "BIb")
        lam128 = alloc([128, 2, 16], F32, "lam128")
        Xpp = alloc([128, 2, 32], F32, "Xpp")
        Xown = alloc([128, 16, 32], F32, "Xown")
        ohot = alloc([128, 8], F32, "ohot")
        Oacc = alloc([128, 16, 4, 128], F32, "Oacc")
        m_run = alloc([128, 64], F32, "m_run"); l_run = alloc([128, 64], F32, "l_run")
        Osam = alloc([128, 512], F32, "Osam")
        ssmx = {}
        mark_after_mixer_state = bump[0]
        QTn = alloc([128, NOWN, 4, 128], BF16, "QTn"); QTr = alloc([128, NOWN, 4, 128], BF16, "QTr")
        mark_after_q = bump[0]

        w_in_v = w_in_d.re("(kt p) n -> p kt n", p=128)
        ldw(W_in_u, w_in_v[:, :, 0:512]); ldw(W_in_kv, w_in_v[:, :, 896:1216])
        ld(ohot, onehot_d)

        S_t = alloc([128, 16, 128], F32, "S_t"); C_t = alloc([128, 16, 128], F32, "C_t"); R_t = alloc([128, 16, 128], F32, "R_t")
        for v_ in (S_t, C_t, R_t):
            v_.b = btab
        WBre = alloc([128, 16, 128], BF16, "WBre"); WBim = alloc([128, 16, 128], BF16, "WBim")

        p0 = bump[0]
        ar = alloc([128, 16]); ai = alloc([128, 16]); ldt = alloc([128, 16]); jj = alloc([128, 128])
        bre = alloc([128, 16, 16]); bim = alloc([128, 16, 16])
        bsm = Buf("ssm_small")
        for v_ in (ar, ai, ldt, jj, bre, bim):
            v_.b = bsm
        ld(ar, ar_d); ld(ai, ai_d); ld(ldt, ldt_d); ld(jj, jj_d)
        ld(bre.re("p a b -> p (a b)"), bre_d); ld(bim.re("p a b -> p (a b)"), bim_d)
        dt_ = alloc([128, 16]); th = alloc([128, 16]); rr = alloc([128, 16])
        sm = [alloc([128, 16]) for _ in range(8)]
        for v_ in [dt_, th, rr] + sm:
            v_.b = bsm
        act(dt_, ldt, AF.Exp)
        tt(th, ai, dt_, ALU.mult)
        tt(rr, ar, dt_, ALU.mult)
        act(rr, rr, AF.Exp)
        A_t = alloc([128, 16, 128]); T1 = alloc([128, 2048]); TI = alloc([128, 2048], I32)
        for v_ in (A_t, T1, TI):
            v_.b = btab
        jjb = jj.un(1).bc([128, 16, 128])
        tt(A_t, jjb, th.un(2).bc([128, 16, 128]), ALU.mult)
        tt(R_t, jjb, rr.un(2).bc([128, 16, 128]), ALU.max)
        tt(R_t, R_t, rr.un(2).bc([128, 16, 128]), ALU.min)
        Af = A_t.re("p a b -> p (a b)")
        TIf = TI.cast(F32)

        def sin_of(out, shift):
            ts(T1, Af, shift, 1.0 / TWO_PI, ALU.add, ALU.mult)
            vcopy(TI, T1)
            vcopy(T1, TI)
            stt(T1, T1, -TWO_PI, Af, ALU.mult, ALU.add)
            if shift != 0.0:
                ts(T1, T1, shift, None, ALU.add)
            ts(TIf, T1, math.pi, -TWO_PI, ALU.is_gt, ALU.mult)
            tt(T1, T1, TIf, ALU.add)
            ts(TIf, T1, -math.pi, TWO_PI, ALU.is_lt, ALU.mult)
            tt(T1, T1, TIf, ALU.add)
            ts(T1, T1, math.pi, -math.pi, ALU.min, ALU.max)
            act(out, T1, AF.Sin)

        sin_of(S_t.re("p a b -> p (a b)"), 0.0)
        sin_of(C_t.re("p a b -> p (a b)"), math.pi / 2)
        lbr, lbi, den, fre, fim, t0_, t1_, t2_ = sm
        cos1 = C_t[:, :, 0]; sin1 = S_t[:, :, 0]
        tt(lbr, rr, cos1, ALU.mult)
        ts(lbr, lbr, -1.0, None, ALU.add)
        tt(lbi, rr, sin1, ALU.mult)
        tt(den, ar, ar, ALU.mult)
        tt(t0_, ai, ai, ALU.mult)
        tt(den, den, t0_, ALU.add)
        recip(den, den)
        tt(t0_, lbr, ar, ALU.mult); tt(t1_, lbi, ai, ALU.mult); tt(fre, t0_, t1_, ALU.add); tt(fre, fre, den, ALU.mult)
        tt(t0_, lbi, ar, ALU.mult); tt(t1_, lbr, ai, ALU.mult); tt(fim, t0_, t1_, ALU.subtract); tt(fim, fim, den, ALU.mult)
        Mre = alloc([128, 16, 128]); Mim = alloc([128, 16, 128]); tb = alloc([128, 16, 16]); tb2 = alloc([128, 16, 16])
        Mreb = alloc([128, 16, 128], BF16); Mimb = alloc([128, 16, 128], BF16)
        bM = Buf("M")
        for v_ in (Mre, Mim, tb, tb2, Mreb, Mimb):
            v_.b = bM
        freb = fre.un(2).bc([128, 16, 16]); fimb = fim.un(2).bc([128, 16, 16])
        mset(Mre, 0.0); mset(Mim, 0.0)
        tt(tb, bre, freb, ALU.mult); tt(tb2, bim, fimb, ALU.mult)
        for lo, col in ((0, 0), (64, 16)):
            for j4 in range(4):
                tt(Mre[lo:lo + 64, j4::4, 32 * j4 + col:32 * j4 + col + 16], tb[lo:lo + 64, j4::4, :], tb2[lo:lo + 64, j4::4, :], ALU.subtract)
        tt(tb, bim, freb, ALU.mult); tt(tb2, bre, fimb, ALU.mult)
        for lo, col in ((0, 0), (64, 16)):
            for j4 in range(4):
                tt(Mim[lo:lo + 64, j4::4, 32 * j4 + col:32 * j4 + col + 16], tb[lo:lo + 64, j4::4, :], tb2[lo:lo + 64, j4::4, :], ALU.add)
        vcopy(Mreb, Mre); vcopy(Mimb, Mim)
        for M_, WB_ in ((Mreb, WBre), (Mimb, WBim)):
            for hf in range(2):
                bb = banks[0].cast(BF16)
                for q in range(8):
                    tr(bb[:, q * 128:(q + 1) * 128], M_[:, 8 * hf + q, :], identb)
                vcopy(WB_[:, 8 * hf:8 * hf + 8, :].re("p a b -> p (a b)"), bb)

        mset(BRb, 0.0); mset(BIb, 0.0)
        tt(tb, bre, freb, ALU.mult); tt(tb2, bim, fimb, ALU.mult)
        for lo, col in ((0, 0), (64, 16)):
            tt(BRb[lo:lo + 64, :, col:col + 16], tb[lo:lo + 64], tb2[lo:lo + 64], ALU.subtract)
        tt(tb, bim, freb, ALU.mult); tt(tb2, bre, fimb, ALU.mult)
        for lo, col in ((0, 0), (64, 16)):
            tt(BIb[lo:lo + 64, :, col:col + 16], tb[lo:lo + 64], tb2[lo:lo + 64], ALU.add)
        lnr = alloc([128, 16]); lnr.b = bsm
        tt(lnr, ar, dt_, ALU.mult)
        act(t2_, lnr, AF.Exp, scale=128.0)
        tt(lam128[:, 0, :], t2_, C_t[:, :, 127], ALU.mult)
        tt(lam128[:, 1, :], t2_, S_t[:, :, 127], ALU.mult)
        jj2 = alloc([128, 128]); jj2.b = bsm
        ld(jj2, jj2_d)
        C2 = Mre; S2 = Mim; Mag = T1.re("p (a b) -> p a b", a=16)
        jj2b = jj2.un(1).bc([128, 16, 128])
        tt(A_t, jj2b, th.un(2).bc([128, 16, 128]), ALU.mult)
        sin_of(S2.re("p a b -> p (a b)"), 0.0)
        sin_of(C2.re("p a b -> p (a b)"), math.pi / 2)
        tt(Mag, jj2b, lnr.un(2).bc([128, 16, 128]), ALU.mult)
        act(Mag, Mag, AF.Exp)
        Lb = [Mreb, Mimb]
        tt(Lb[0], Mag, C2, ALU.mult); tt(Lb[1], Mag, S2, ALU.mult)
        for M_, LT_ in ((Lb[0], LrT), (Lb[1], LiT)):
            for hf in range(2):
                bb = banks[0].cast(BF16)
                for q in range(8):
                    tr(bb[:, q * 128:(q + 1) * 128], M_[:, 8 * hf + q, :], identb)
                vcopy(LT_[:, 8 * hf:8 * hf + 8, :].re("p a b -> p (a b)"), bb)

        for nm_, v_ in (("S_t", S_t), ("C_t", C_t), ("R_t", R_t)):
            ld(tab_d[nm_], v_.re("p a b -> p (a b)"))
        ld(tab_d["WBre"], WBre.re("p a b -> p (a b)")); ld(tab_d["WBim"], WBim.re("p a b -> p (a b)"))

        barrier()
        bump[0] = mark_after_q
        W_xk = alloc([128, 8, D], BF16, "W_xk"); W_xv = alloc([128, 8, D], BF16, "W_xv")
        ldw(W_xk, w_xk_d.re("(kt p) n -> p kt n", p=128)); ldw(W_xv, w_xv_d.re("(kt p) n -> p kt n", p=128))
        xt0 = alloc([128, D], F32, "xt0"); memT = alloc([128, 8, 256], BF16, "memT"); xT0 = alloc([128, 8, 128], BF16, "xT0")
        mkf = alloc([128, D], F32, "mkf")
        for mt in range(2):
            load_xT(mem[mt * 128:(mt + 1) * 128, :], xt0, xT0, banks[0])
            vcopy(memT[:, :, mt * 128:(mt + 1) * 128], xT0, eng="pool")
            for W_, o_d, keep in ((W_xk, memk_o, False), (W_xv, memv_o, True)):
                for nb in range(2):
                    for k in range(8):
                        mm(banks[1 + nb], xT0[:, k, :], W_[:, k, nb * 512:(nb + 1) * 512], start=(k == 0), stop=(k == 7))
                    acopy(mkf[:, nb * 512:(nb + 1) * 512], banks[1 + nb])
                ld(o_d[mt * 128:(mt + 1) * 128, :], mkf)
                if keep:
                    vcopy(mvb[:, mt, :], mkf)
        for h in range(4):
            for et in range(2):
                c0 = h * 256 + et * 128
                for k in range(8):
                    mm(banks[3][:, 0:256], W_xk[:, k, c0:c0 + 128], memT[:, k, :], start=(k == 0), stop=(k == 7))
                acopy(mkT[:, h, et, :], banks[3][:, 0:256])

        barrier()
        bump[0] = mark_after_q

        W_q = alloc([128, 3, 768], BF16, "W_q")
        ldw(W_q, w_q_d.re("(kt p) n -> p kt n", p=128))
        W_in_q = alloc([128, 8, 384], BF16, "W_in_q")
        ldw(W_in_q, w_in_v[:, :, 512:896])
        gq = alloc([128, 384], F32, "gq"); ld(gq, gq_d)
        NB = 2
        xt = [alloc([128, D], F32, "xt%d" % i) for i in range(NB)]
        xT = [alloc([128, 8, 128], BF16, "xT%d" % i) for i in range(NB)]
        rc = [alloc([128, 64], F32, "rc%d" % i) for i in range(NB)]
        rs = [alloc([128, 64], F32, "rs%d" % i) for i in range(NB)]
        sq = alloc([128, 512], F32, "sq"); ss1 = [alloc([128, 1], F32) for _ in range(NB)]
        cqn = [alloc([128, 384], BF16) for _ in range(NB)]
        cqT = [alloc([128, 3, 128], BF16) for _ in range(NB)]
        qf = [alloc([128, 4, 192], F32) for _ in range(NB)]
        qr = [alloc([128, 4, 64], F32) for _ in range(NB)]
        qtmp = [alloc([128, 4, 64], F32) for _ in range(NB)]
        qb = [alloc([128, 4, 192], BF16) for _ in range(NB)]
        for i in range(NOWN):
            b = i % NB
            load_xT(xo[i * 128:(i + 1) * 128, :], xt[b], xT[b], banks[0])
            ld(rc[b], ropec_o[i * 128:(i + 1) * 128, :]); ld(rs[b], ropes_o[i * 128:(i + 1) * 128, :])
            for k in range(8):
                mm(banks[1][:, 0:384], xT[b][:, k, :], W_in_q[:, k, :], start=(k == 0), stop=(k == 7))
            rmsnorm(cqn[b], banks[1][:, 0:384], 384, gq, sq[:, 0:384], ss1[b])
            transpose_to(cqT[b], cqn[b], 3, banks[2], eng="act")
            for nb, (c0, c1) in enumerate(((0, 512), (512, 768))):
                for k in range(3):
                    mm(SC[:, nb * 512:nb * 512 + (c1 - c0)], cqT[b][:, k, :], W_q[:, k, c0:c1], start=(k == 0), stop=(k == 2))
            acopy(qf[b].re("p a b -> p (a b)"), SC[:, 0:768])
            rope(qr[b], qf[b][:, :, 128:192], rc[b], rs[b], qtmp[b], 4)
            P.op("act", lambda e, b=b: e.mul(out=qb[b].ap[:, :, 0:128], in_=qf[b].ap[:, :, 0:128], mul=MLA_SCALE), reads=[qf[b].b], writes=[qb[b].b])
            P.op("act", lambda e, b=b: e.mul(out=qb[b].ap[:, :, 128:192], in_=qr[b].ap, mul=MLA_SCALE), reads=[qr[b].b], writes=[qb[b].b])
            bb = banks[3].cast(BF16)
            for h in range(4):
                tr(bb[:, h * 128:(h + 1) * 128], qb[b][:, h, 0:128], identb)
                tr(bb[0:64, 512 + h * 128:512 + (h + 1) * 128], qb[b][:, h, 128:192], identb)
            vcopy(QTn[:, i, :, :].re("p a b -> p (a b)"), bb[:, 0:512])
            vcopy(QTr[0:64, i, :, :].re("p a b -> p (a b)"), bb[0:64, 512:1024])

        barrier()
        bump[0] = mark_after_q

        def ssm_rounds(uT_, pbs, init_fn, nseg, ybank=None, last_out=None):
            L = 128 // nseg
            S_t = ssmx["S_t"]; C_t = ssmx["C_t"]; R_t = ssmx["R_t"]; WBre = ssmx["WBre"]; WBim = ssmx["WBim"]
            Cre = ssmx["Cre"]; Cimn = ssmx["Cimn"]; Dblk = ssmx["Dblk"]
            xs4 = ssmx["xs4"]; zl_all = ssmx["zl_all"]
            def do_round(r):
                m4 = ssmx["m4"][r % 2]; wri = ssmx["wri"][r % 2]; zri = ssmx["zri"][r % 2]
                pb = pbs[r % 2]
                for j in range(2):
                    gp = 2 * r + j
                    mm(pb[:, j * 128:(j + 1) * 128], WBre[:, gp, :], uT_[:, gp // 4, :])
                    mm(pb[:, 256 + j * 128:256 + (j + 1) * 128], WBim[:, gp, :], uT_[:, gp // 4, :])
                if nseg == 1:
                    Cq = C_t[:, 2 * r:2 * r + 2, :].re("p a b -> p (a b)"); Sq = S_t[:, 2 * r:2 * r + 2, :].re("p a b -> p (a b)")
                    pre = pb[:, 0:256]; pim = pb[:, 256:512]
                    mk = lambda v_: v_
                else:
                    Cq = C_t[:, 2 * r:2 * r + 2, 0:L].un(2).bc([128, 2, nseg, L]); Sq = S_t[:, 2 * r:2 * r + 2, 0:L].un(2).bc([128, 2, nseg, L])
                    pre = pb[:, 0:256].re("p (a s l) -> p a s l", a=2, s=nseg); pim = pb[:, 256:512].re("p (a s l) -> p a s l", a=2, s=nseg)
                    mk = lambda v_: v_.re("p (a s l) -> p a s l", a=2, s=nseg)
                tt(mk(m4[0]), pre, Cq, ALU.mult); tt(mk(m4[1]), pim, Sq, ALU.mult)
                tt(mk(m4[2]), pim, Cq, ALU.mult); tt(mk(m4[3]), pre, Sq, ALU.mult)
                tt(wri[0], m4[0], m4[1], ALU.add, eng="pool")
                tt(wri[1], m4[2], m4[3], ALU.subtract, eng="pool")

            def do_back(r):
                m4 = ssmx["m4"][r % 2]; wri = ssmx["wri"][r % 2]; zri = ssmx["zri"][r % 2]
                if nseg == 1:
                    Cq = C_t[:, 2 * r:2 * r + 2, :].re("p a b -> p (a b)"); Sq = S_t[:, 2 * r:2 * r + 2, :].re("p a b -> p (a b)")
                else:
                    Cq = C_t[:, 2 * r:2 * r + 2, 0:L].un(2).bc([128, 2, nseg, L]); Sq = S_t[:, 2 * r:2 * r + 2, 0:L].un(2).bc([128, 2, nseg, L])
                for j in range(2):
                    gp = 2 * r + j
                    for s_ in range(nseg):
                        for part in range(2):
                            scan(zri[part][:, j, s_ * L:(s_ + 1) * L], R_t[:, gp, 0:L], wri[part][:, j * 128 + s_ * L:j * 128 + (s_ + 1) * L], init_fn(gp, s_, part))
                if last_out is not None:
                    for part in range(2):
                        src = zri[part].re("p a (s l) -> p a s l", s=nseg)[:, :, :, L - 1]
                        vcopy(zl_all[part][:, 0:nseg, 2 * r:2 * r + 2].re("p s a -> p a s"), src, eng="pool")
                if ybank is not None:
                    dmd = ssmx["dmd"][r % 2]; xrb = ssmx["xrb"]; xib = ssmx["xib"]
                    if nseg == 1:
                        Cd, Sd = Cq, Sq
                        zr_ = zri[0].re("p a b -> p (a b)"); zi_ = zri[1].re("p a b -> p (a b)")
                        dk = lambda v_: v_
                    else:
                        Cd, Sd = Cq, Sq
                        zr_ = zri[0].re("p a (s l) -> p a s l", s=nseg); zi_ = zri[1].re("p a (s l) -> p a s l", s=nseg)
                        dk = lambda v_: v_.re("p (a s l) -> p a s l", a=2, s=nseg)
                    tt(dk(dmd[0]), zr_, Cd, ALU.mult); tt(dk(dmd[1]), zi_, Sd, ALU.mult, eng="pool")
                    tt(dk(dmd[2]), zr_, Sd, ALU.mult); tt(dk(dmd[3]), zi_, Cd, ALU.mult, eng="pool")
                    tt(xrb[r % 2].re("p a b -> p (a b)"), dmd[0], dmd[1], ALU.subtract, eng="pool")
                    tt(xib[r % 2].re("p a b -> p (a b)"), dmd[2], dmd[3], ALU.add, eng="pool")

            def do_cproj(r):
                if ybank is None:
                    return
                xrb = ssmx["xrb"]; xib = ssmx["xib"]
                for j in range(2):
                    gp = 2 * r + j
                    o_ = ybank[:, 32 * gp:32 * gp + 32]
                    mm(o_, xrb[r % 2][:, j, :], Cre[:, gp, :], start=True, stop=False)
                    mm(o_, xib[r % 2][:, j, :], Cimn[:, gp, :], start=False, stop=False)
                    mm(o_, uT_[:, gp // 4, :], Dblk[:, gp, :], start=False, stop=True)
            def do_tail():
                do_cproj(7)
                if last_out is None:
                    return
                if True:
                    cL = C_t[:, :, L - 1]; sL = S_t[:, :, L - 1]
                    for s_ in range(nseg):
                        o_re, o_im = last_out(s_)
                        zr_l = zl_all[0][:, s_, :]; zi_l = zl_all[1][:, s_, :]
                        tt(xs4[0], zr_l, cL, ALU.mult); tt(xs4[1], zi_l, sL, ALU.mult)
                        tt(xs4[2], zr_l, sL, ALU.mult); tt(xs4[3], zi_l, cL, ALU.mult)
                        tt(o_re, xs4[0], xs4[1], ALU.subtract)
                        tt(o_im, xs4[2], xs4[3], ALU.add)


            def step(k_):
                def f():
                    if k_ == 0:
                        do_round(0)
                    if k_ + 1 < 8:
                        do_round(k_ + 1)
                    do_back(k_)
                    if k_ >= 1:
                        do_cproj(k_ - 1)
                return f
            return [step(k_) for k_ in range(8)] + [do_tail]

        def ssm_tile(uT_, pbs, init_fn, nseg, ybank=None, last_out=None):
            for f_ in ssm_rounds(uT_, pbs, init_fn, nseg, ybank, last_out):
                f_()

        W_kv = alloc([128, 2, 4, 256], BF16, "W_kv")
        ldw(W_kv.re("p k h e -> p k (h e)"), w_kv_d.re("(kt p) n -> p kt n", p=128))
        gkv = alloc([128, 256], F32, "gkv"); ld(gkv, gkv_d)
        maskd = alloc([128, 1024], F32, "maskd"); ld(maskd, maskd_d)
        xt = [alloc([128, D], F32, "hxt%d" % i) for i in range(NB)]
        xT = [alloc([128, 8, 128], BF16, "hxT%d" % i) for i in range(NB)]
        utok = [alloc([128, 512], BF16, "hut%d" % i) for i in range(NB)]
        kvf = [alloc([128, 320], F32, "kvf%d" % i) for i in range(NB)]
        prs = [alloc([128, 16, 32], F32, "prs%d" % i) for i in range(NB)]
        pis = [alloc([128, 16, 32], F32, "pis%d" % i) for i in range(NB)]
        rc = [alloc([128, 64], F32) for _ in range(NB)]; rs = [alloc([128, 64], F32) for _ in range(NB)]
        ckvf = [alloc([128, 256], F32) for _ in range(NB)]; kpef = [alloc([128, 64], F32) for _ in range(NB)]
        tmp64 = [alloc([128, 64], F32) for _ in range(NB)]
        sq = alloc([128, 256], F32, "hsq"); ss1 = [alloc([128, 1], F32) for _ in range(NB)]
        ckvb = [alloc([128, 256], BF16) for _ in range(NB)]; kpeb = [alloc([128, 128], BF16) for _ in range(NB)]
        ckvT = [alloc([128, 2, 128], BF16) for _ in range(NB)]
        KT = [alloc([128, 4, 1024], BF16, "KT%d" % i) for i in range(2)]
        KR = [alloc([128, 1024], BF16, "KR%d" % i) for i in range(2)]
        Vb = [alloc([128, 8, 512], BF16, "Vb%d" % i) for i in range(2)]
        Pb = [alloc([128, 1024], BF16, "Pb%d" % i) for i in range(2)]
        PT = [alloc([128, 8, 128], BF16, "PT%d" % i) for i in range(2)]
        NS4 = 4
        mx = [alloc([128, 1], F32) for _ in range(NS4)]; mnew = [alloc([128, 1], F32) for _ in range(NS4)]
        negm = [alloc([128, 1], F32) for _ in range(NS4)]; alp = [alloc([128, 1], F32) for _ in range(NS4)]
        rsum = [alloc([128, 1], F32) for _ in range(NS4)]
        dm_ = [alloc([128, 1], F32) for _ in range(NS4)]
        hm1 = [alloc([128, 16, 32], F32, "hm1_%d" % i) for i in range(NB)]; hm2 = [alloc([128, 16, 32], F32, "hm2_%d" % i) for i in range(NB)]
        sre = [alloc([128, 32], F32) for _ in range(2)]
        xs8 = [alloc([128, 16], F32) for _ in range(4)]

        mset(Xpp, 0.0); mset(Xown, 0.0)
        mset(m_run, NEG); mset(l_run, 0.0); mset(Oacc, 0.0)
        for kp in range(2):
            mset(kpeb[kp], 0.0)

        def hist_stages(t, kbuf, j, kb):
            b = t % NB
            bA = banks[0]; bB = banks[1]
            xc = Xpp[:, t % 2, :]; xn = Xpp[:, (t + 1) % 2, :]

            def sA():
                ld(xt[b], xp[t * 128:(t + 1) * 128, :])
                ld(rc[b], ropec_p[t * 128:(t + 1) * 128, :]); ld(rs[b], ropes_p[t * 128:(t + 1) * 128, :])

            def sB():
                for k in range(4):
                    tr(bA[:, k * 128:(k + 1) * 128], xt[b][:, k * 128:(k + 1) * 128], ident)
                for k in range(4):
                    tr(bB[:, k * 128:(k + 1) * 128], xt[b][:, (4 + k) * 128:(5 + k) * 128], ident)

            def sC():
                acopy(xT[b][:, 0:4, :], bA.re("p (k n) -> p k n", k=4))
                vcopy(xT[b][:, 4:8, :], bB.re("p (k n) -> p k n", k=4))

            def sD():
                for k in range(8):
                    mm(bA, xT[b][:, k, :], W_in_u[:, k, :], start=(k == 0), stop=(k == 7))
                for k in range(8):
                    mm(bB[:, 0:320], xT[b][:, k, :], W_in_kv[:, k, :], start=(k == 0), stop=(k == 7))

            def sE():
                acopy(utok[b], bA)
                acopy(kvf[b], bB[:, 0:320])

            def sF():
                act(sq, kvf[b][:, 0:256], AF.Square, accum=ss1[b])
                act(ss1[b], ss1[b], AF.Ln, scale=1.0 / 256, bias=epsc)
                act(ss1[b], ss1[b], AF.Exp, scale=-0.5)
                for gp in range(16):
                    mm(bA[:, 32 * gp:32 * gp + 32], LrT[:, gp, :], utok[b][:, 32 * gp:32 * gp + 32])
                for gp in range(16):
                    mm(bB[:, 32 * gp:32 * gp + 32], LiT[:, gp, :], utok[b][:, 32 * gp:32 * gp + 32])

            late = kb >= 8

            def sG():
                if late:
                    pr3 = bA.re("p (a b) -> p a b", a=16); pi3 = bB.re("p (a b) -> p a b", a=16)
                    tt(hm1[b], pr3, BRb, ALU.mult); tt(hm2[b], pi3, BIb, ALU.mult)
                    tt(prs[b], pi3, BRb, ALU.mult); tt(pis[b], pr3, BIb, ALU.mult)
                else:
                    acopy(prs[b].re("p a b -> p (a b)"), bA)
                    acopy(pis[b].re("p a b -> p (a b)"), bB)
                tt(ckvf[b], kvf[b][:, 0:256], gkv, ALU.mult, eng="pool")
                ts(ckvf[b], ckvf[b], ss1[b], 1.0, ALU.mult, ALU.mult, eng="pool")
                rope(kpef[b].re("p (a b) -> p a b", a=1), kvf[b][:, 256:320].re("p (a b) -> p a b", a=1), rc[b], rs[b],
                     tmp64[b].re("p (a b) -> p a b", a=1), 1, eng="pool")
                vcopy(ckvb[b], ckvf[b], eng="pool")
                vcopy(kpeb[b][:, 0:64], kpef[b], eng="pool")

            def sH():
                bb = bA.cast(BF16)
                for kc in range(2):
                    tr(bb[:, kc * 128:(kc + 1) * 128], ckvb[b][:, kc * 128:(kc + 1) * 128], identb)
                tr(bb[:, 256:384], kpeb[b], identb)
                if late:
                    tt(hm1[b], hm1[b], hm2[b], ALU.subtract, eng="pool")
                    tt(hm2[b], prs[b], pis[b], ALU.add, eng="pool")
                else:
                    tt(hm1[b], prs[b], BRb, ALU.mult, eng="pool"); tt(hm2[b], pis[b], BIb, ALU.mult, eng="pool")
                    tt(hm1[b], hm1[b], hm2[b], ALU.subtract, eng="pool")
                    tt(hm2[b], pis[b], BRb, ALU.mult, eng="pool"); tt(prs[b], prs[b], BIb, ALU.mult, eng="pool")
                    tt(hm2[b], hm2[b], prs[b], ALU.add, eng="pool")

            def sI():
                bb = bA.cast(BF16)
                acopy(ckvT[b].re("p a b -> p (a b)"), bb[:, 0:256])
                acopy(KR[kbuf][0:64, j * 128:(j + 1) * 128], bb[0:64, 256:384])

            def sJ():
                for h in range(4):
                    for kc in range(2):
                        mm(bB[:, h * 128:(h + 1) * 128], W_kv[:, kc, h, 0:128], ckvT[b][:, kc, :], start=(kc == 0), stop=(kc == 1))
                for kc in range(2):
                    mm(bA, ckvT[b][:, kc, :], W_kv[:, kc, :, 128:256], start=(kc == 0), stop=(kc == 1))
                lr_ = lam128[:, 0, :]; li_ = lam128[:, 1, :]
                ld(ckv_o[t * 128:(t + 1) * 128, :], ckvf[b])
                ld(kpe_o[t * 128:(t + 1) * 128, :], kpef[b])
                red(sre[t % 2][:, 0:16], hm1[b], ALU.add)
                red(sre[t % 2][:, 16:32], hm2[b], ALU.add)
                stt(Xown[:, kb, :], xc, ohot[:, j:j + 1], Xown[:, kb, :], ALU.mult, ALU.add)
                tt(xs8[0], xc[:, 0:16], lr_, ALU.mult, eng="pool"); tt(xs8[1], xc[:, 16:32], li_, ALU.mult, eng="pool")
                tt(xs8[2], xc[:, 16:32], lr_, ALU.mult, eng="pool"); tt(xs8[3], xc[:, 0:16], li_, ALU.mult, eng="pool")
                tt(xs8[0], xs8[0], xs8[1], ALU.subtract, eng="pool"); tt(xs8[2], xs8[2], xs8[3], ALU.add, eng="pool")
                tt(xn[:, 0:16], xs8[0], sre[t % 2][:, 0:16], ALU.add, eng="pool")
                tt(xn[:, 16:32], xs8[2], sre[t % 2][:, 16:32], ALU.add, eng="pool")

            def sK():
                acopy(KT[kbuf][:, :, j * 128:(j + 1) * 128], bB.re("p (h n) -> p h n", h=4))
                vcopy(Vb[kbuf][:, j, :], bA)

            def pair(p_, c_):
                def f():
                    p_(); c_()
                return f
            return [sA, pair(sB, sC), pair(sD, sE), pair(sF, sG), pair(sH, sI), pair(sJ, sK)]

        def hist_items(kb):
            items = []
            for jp in range(4):
                sa = hist_stages(kb * 8 + 2 * jp, kb % 2, 2 * jp, kb)
                sb_ = hist_stages(kb * 8 + 2 * jp + 1, kb % 2, 2 * jp + 1, kb)
                for a_, b_ in zip(sa, sb_):
                    items.append(a_); items.append(b_)
            return items

        SCp = [dbl[3], dbl[2]]
        ptb = [banks[2], banks[3]]

        def att_A(n, i, h, kbuf):
            sc = SCp[n % 2]
            for c2 in range(2):
                mm(sc[:, c2 * 512:(c2 + 1) * 512], QTn[:, i, h, :], KT[kbuf][:, h, c2 * 512:(c2 + 1) * 512], start=True, stop=False)
                mm(sc[:, c2 * 512:(c2 + 1) * 512], QTr[0:64, i, h, :], KR[kbuf][0:64, c2 * 512:(c2 + 1) * 512], start=False, stop=True)

        def att_B(n, i, h, kbuf, diag):
            u2 = n % 2; u4 = n % NS4
            sc = SCp[u2]
            col = i * 4 + h
            if diag:
                tt(sc, sc, maskd, ALU.add)
            red(mx[u4], sc, ALU.max)
            tt(mnew[u4], mx[u4], m_run[:, col:col + 1], ALU.max)
            tt(dm_[u4], m_run[:, col:col + 1], mnew[u4], ALU.subtract)
            vcopy(m_run[:, col:col + 1], mnew[u4])
            ts(negm[u4], mnew[u4], -1.0, None, ALU.mult)
            act(alp[u4], dm_[u4], AF.Exp)
            act(Pb[u2], sc, AF.Exp, bias=negm[u4], accum=rsum[u4])

        def att_CD(n, i, h, kbuf):
            u2 = n % 2
            pt_b = ptb[u2].cast(BF16)
            for kt in range(8):
                tr(pt_b[:, kt * 128:(kt + 1) * 128], Pb[u2][:, kt * 128:(kt + 1) * 128], identb)
            if n % 3 == 0:
                vcopy(PT[u2].re("p a b -> p (a b)"), pt_b)
            else:
                acopy(PT[u2].re("p a b -> p (a b)"), pt_b)

        def att_EF(n, i, h, kbuf):
            u2 = n % 2; u4 = n % NS4
            ov = ptb[u2][:, 0:128]
            for kt in range(8):
                mm(ov, PT[u2][:, kt, :], Vb[kbuf][:, kt, h * 128:(h + 1) * 128], start=(kt == 0), stop=(kt == 7))
            stt(Oacc[:, i, h, :], Oacc[:, i, h, :], alp[u4], ov, ALU.mult, ALU.add)
            col = i * 4 + h
            stt(l_run[:, col:col + 1], l_run[:, col:col + 1], alp[u4], rsum[u4], ALU.mult, ALU.add)

        for it in hist_items(0):
            it()
        nun = 0
        for kb in range(16):
            kbuf = kb % 2
            units = [(i, h) for i in range(kb, 16) for h in range(4)]
            U = len(units)
            items = hist_items(kb + 1) if kb + 1 < 16 else []
            per = (len(items) + U - 1) // U if items else 0
            ip = 0
            for q_ in range(U + 3):
                if q_ < U:
                    att_A(nun + q_, units[q_][0], units[q_][1], kbuf)
                if 0 <= q_ - 1 < U:
                    att_B(nun + q_ - 1, units[q_ - 1][0], units[q_ - 1][1], kbuf, units[q_ - 1][0] == kb)
                if 0 <= q_ - 2 < U:
                    att_CD(nun + q_ - 2, units[q_ - 2][0], units[q_ - 2][1], kbuf)
                if 0 <= q_ - 3 < U:
                    att_EF(nun + q_ - 3, units[q_ - 3][0], units[q_ - 3][1], kbuf)
                for _ in range(per):
                    if ip < len(items):
                        items[ip](); ip += 1
            while ip < len(items):
                items[ip](); ip += 1
            nun += U

        ld(ssmp_o, Xpp[:, NTP % 2, :])

        barrier()
        bump[0] = mark_after_q

        W_kv = alloc([128, 2, 4, 256], BF16, "W_kv2")
        ldw(W_kv.re("p k h e -> p k (h e)"), w_kv_d.re("(kt p) n -> p kt n", p=128))
        gkv = alloc([128, 256], F32, "gkv2"); ld(gkv, gkv_d)
        xt_s = alloc([128, D], F32, "sxt"); xT_s = alloc([128, 8, 128], BF16, "sxT")
        rc_s = alloc([128, 64], F32); rs_s = alloc([128, 64], F32)
        ckvf_s = alloc([128, 256], F32); kpef_s = alloc([128, 64], F32); tmp64_s = alloc([128, 64], F32)
        sq = alloc([128, 512], F32, "ssq"); ss_s = alloc([128, 1], F32)
        ckvb_s = alloc([128, 256], BF16); kpeb_s = alloc([128, 128], BF16)
        ckvTn = alloc([128, 2, 128], BF16, "ckvTn"); kpeTn = alloc([128, 128], BF16, "kpeTn")
        KTn = alloc([128, 4, 128], BF16, "KTn")
        cat = [alloc([128, 256], F32, "cat%d" % i) for i in range(2)]
        catb = [alloc([128, 256], BF16) for _ in range(2)]
        cpt = [alloc([128, 64], F32, "cpt%d" % i) for i in range(2)]
        cptb = [alloc([128, 128], BF16) for _ in range(2)]
        ckvTc = alloc([128, 2, 1024], BF16, "ckvTc"); kpeTc = alloc([128, 1024], BF16, "kpeTc")
        KTc = alloc([128, 4, 1024], BF16, "KTc"); Vc = alloc([128, 8, 512], BF16, "Vc"); Vn = alloc([128, 512], BF16, "Vn")
        scs = alloc([128, 1056], F32, "scs")
        Ps = alloc([128, 1152], BF16, "Ps"); PTs = alloc([128, 9, 32], BF16, "PTs")
        mxs = alloc([128, 1], F32); negs = alloc([128, 1], F32); sums = alloc([128, 1], F32)
        osb = alloc([128, 512], F32, "osb")
        isam = 16
        load_xT(xo[isam * 128:(isam + 1) * 128, :], xt_s, xT_s, banks[0])
        ld(rc_s, ropec_o[isam * 128:(isam + 1) * 128, :]); ld(rs_s, ropes_o[isam * 128:(isam + 1) * 128, :])
        for k in range(8):
            mm(banks[2][:, 0:320], xT_s[:, k, :], W_in_kv[:, k, :], start=(k == 0), stop=(k == 7))
        rmsnorm(ckvf_s, banks[2][:, 0:256], 256, gkv, sq[:, 0:256], ss_s)
        ld(ckvs_o, ckvf_s)
        rope(kpef_s.re("p (a b) -> p a b", a=1), banks[2][:, 256:320].re("p (a b) -> p a b", a=1), rc_s, rs_s, tmp64_s.re("p (a b) -> p a b", a=1), 1)
        ld(kpes_o, kpef_s)
        vcopy(ckvb_s, ckvf_s)
        mset(kpeb_s, 0.0)
        vcopy(kpeb_s[:, 0:64], kpef_s)
        bb = banks[3].cast(BF16)
        for kc in range(2):
            tr(bb[:, kc * 128:(kc + 1) * 128], ckvb_s[:, kc * 128:(kc + 1) * 128], identb)
        tr(bb[:, 256:384], kpeb_s, identb)
        acopy(ckvTn.re("p a b -> p (a b)"), bb[:, 0:256])
        acopy(kpeTn[0:64, :], bb[0:64, 256:384])
        for h in range(4):
            for kc in range(2):
                mm(banks[3][:, h * 128:(h + 1) * 128], W_kv[:, kc, h, 0:128], ckvTn[:, kc, :], start=(kc == 0), stop=(kc == 1))
        acopy(KTn.re("p a b -> p (a b)"), banks[3])
        for kp in range(2):
            mset(cptb[kp], 0.0)
        for s_ in range(4):
            for kt in range(8):
                b = kt % 2
                r0 = s_ * 1024 + kt * 128
                ld(cat[b], cckv_d[r0:r0 + 128, :]); ld(cpt[b], ckpe_d[r0:r0 + 128, :])
                vcopy(catb[b], cat[b], eng="pool"); vcopy(cptb[b][:, 0:64], cpt[b], eng="pool")
                bb = banks[1].cast(BF16)
                for kc in range(2):
                    tr(bb[:, kc * 128:(kc + 1) * 128], catb[b][:, kc * 128:(kc + 1) * 128], identb)
                tr(bb[:, 256:384], cptb[b], identb)
                acopy(ckvTc[:, :, kt * 128:(kt + 1) * 128], bb[:, 0:256].re("p (a b) -> p a b", a=2))
                acopy(kpeTc[0:64, kt * 128:(kt + 1) * 128], bb[0:64, 256:384])
            for h in range(4):
                for n in range(2):
                    for kc in range(2):
                        mm(banks[2], W_kv[:, kc, h, 0:128], ckvTc[:, kc, n * 512:(n + 1) * 512], start=(kc == 0), stop=(kc == 1))
                    acopy(KTc[:, h, n * 512:(n + 1) * 512], banks[2])
            for kt in range(8):
                for kc in range(2):
                    mm(banks[3], ckvTc[:, kc, kt * 128:(kt + 1) * 128], W_kv[:, kc, :, 128:256], start=(kc == 0), stop=(kc == 1))
                vcopy(Vc[:, kt, :], banks[3])
            for kc in range(2):
                mm(banks[3][0:32, :], ckvTn[:, kc, s_ * 32:(s_ + 1) * 32], W_kv[:, kc, :, 128:256], start=(kc == 0), stop=(kc == 1))
            vcopy(Vn[0:32, :], banks[3][0:32, :])
            qs = slice(s_ * 32, (s_ + 1) * 32)
            for h in range(4):
                for n in range(2):
                    mm(SC[0:32, n * 512:(n + 1) * 512], QTn[:, isam, h, qs], KTc[:, h, n * 512:(n + 1) * 512], start=True, stop=False)
                    mm(SC[0:32, n * 512:(n + 1) * 512], QTr[0:64, isam, h, qs], kpeTc[0:64, n * 512:(n + 1) * 512], start=False, stop=True)
                mm(banks[4][0:32, 0:32], QTn[:, isam, h, qs], KTn[:, h, qs], start=True, stop=False)
                mm(banks[4][0:32, 0:32], QTr[0:64, isam, h, qs], kpeTn[0:64, qs], start=False, stop=True)
                acopy(scs[0:32, 0:1024], SC[0:32, :])
                acopy(scs[0:32, 1024:1056], banks[4][0:32, 0:32])
                red(mxs[0:32, :], scs[0:32, :], ALU.max)
                ts(negs[0:32, :], mxs[0:32, :], -1.0, None, ALU.mult)
                mset(Ps[0:32, 1024:1152], 0.0)
                act(Ps[0:32, 0:1056], scs[0:32, :], AF.Exp, bias=negs[0:32, :], accum=sums[0:32, :])
                pt_b = banks[5].cast(BF16)
                for kt in range(9):
                    tr(pt_b[:, kt * 32:(kt + 1) * 32], Ps[0:32, kt * 128:(kt + 1) * 128], identb[0:32, 0:32])
                vcopy(PTs.re("p a b -> p (a b)"), pt_b[:, 0:288])
                ov = banks[4][0:32, 128:256]
                for kt in range(8):
                    mm(ov, PTs[:, kt, :], Vc[:, kt, h * 128:(h + 1) * 128], start=(kt == 0), stop=False)
                mm(ov, PTs[0:32, 8, :], Vn[0:32, h * 128:(h + 1) * 128], start=False, stop=True)
                recip(sums[0:32, :], sums[0:32, :])
                ts(osb[0:32, h * 128:(h + 1) * 128], ov, sums[0:32, :], None, ALU.mult)
            ld(Osam[s_ * 32:(s_ + 1) * 32, :], osb[0:32, :])

        barrier()
        bump[0] = mark_after_mixer_state

        W_glu = alloc([128, 4, D], BF16, "W_glu"); W_o = alloc([128, 8, D], BF16, "W_o")
        ldw(W_glu, w_glu_d.re("(kt p) n -> p kt n", p=128)); ldw(W_o, w_o_d.re("(kt p) n -> p kt n", p=128))
        gos = alloc([128, 512], F32, "gos"); gom = alloc([128, 512], F32, "gom"); ld(gos, gos_d); ld(gom, gom_d)
        lng = alloc([128, D], F32, "lng"); lnb = alloc([128, D], F32, "lnb")
        ld(lng, lng_d[:, 0:D]); ld(lnb, lnb_d[:, 0:D])
        btab2 = Buf("tab2")
        for nm_ in ("S_t", "C_t", "R_t"):
            v_ = alloc([128, 16, 128], F32, nm_ + "2"); v_.b = btab2
            ld(v_.re("p a b -> p (a b)"), tab_d[nm_]); ssmx[nm_] = v_
        for nm_ in ("WBre", "WBim"):
            v_ = alloc([128, 16, 128], BF16, nm_ + "2")
            ld(v_.re("p a b -> p (a b)"), tab_d[nm_]); ssmx[nm_] = v_
        for nm_, src_ in (("Cre", cre_d), ("Cimn", cimn_d), ("Dblk", dblk_d)):
            v_ = alloc([128, 16, 32], BF16, nm_)
            ldw(v_.re("p a b -> p (a b)"), src_); ssmx[nm_] = v_
        ssmx["m4"] = [[alloc([128, 256], F32) for _ in range(4)] for _ in range(2)]
        ssmx["wri"] = [[alloc([128, 256], F32) for _ in range(2)] for _ in range(2)]
        ssmx["zri"] = [[alloc([128, 2, 128], F32) for _ in range(2)] for _ in range(2)]
        ssmx["xs4"] = [alloc([128, 16], F32) for _ in range(4)]
        ssmx["zl_all"] = [alloc([128, 4, 16], F32) for _ in range(2)]
        ssmx["dmd"] = [[alloc([128, 256], F32) for _ in range(4)]] * 2
        ssmx["xrb"] = [alloc([128, 2, 128], BF16) for _ in range(2)]
        ssmx["xib"] = [alloc([128, 2, 128], BF16) for _ in range(2)]
        S0 = alloc([128, 2, 4, 16], F32, "S0"); ld(S0.re("p a b c -> p (a b c)"), s0_d)
        Sfin = alloc([128, 2, 4, 16], F32, "Sfin")
        xt = [alloc([128, D], F32, "axt%d" % i) for i in range(NB)]
        xT = [alloc([128, 8, 128], BF16, "axT%d" % i) for i in range(NB)]
        uT = [alloc([128, 4, 128], BF16, "auT%d" % i) for i in range(NB)]
        ysq = alloc([128, 512], F32, "ysq"); yt = alloc([128, 512], F32, "yt"); ysg = alloc([128, 512], F32, "ysg")
        glb = alloc([128, 512], BF16, "glb"); gT = alloc([128, 4, 128], BF16, "gT")
        sg2 = ysq; osf = yt
        mixb = alloc([128, D], BF16, "mixb"); mixT = alloc([128, 8, 128], BF16, "mixT")
        rl = alloc([128, 4], F32, "rl"); omf = alloc([128, 4, 128], F32, "omf")
        ss_a = alloc([128, 1], F32); sq = ysg
        rres = alloc([128, D], F32, "rres"); hout = alloc([128, D], F32, "hout")
        stats = alloc([128, 12], F32); mvv = alloc([128, 2], F32)

        def pre_stages(i):
            b = i % NB
            yb = banks[3] if i % 2 == 0 else banks[0]

            def p0():
                load_xT(xo[i * 128:(i + 1) * 128, :], xt[b], xT[b], banks[1])
                for kt in range(4):
                    for k in range(8):
                        mm(banks[1][:, kt * 128:(kt + 1) * 128], W_in_u[:, k, kt * 128:(kt + 1) * 128], xT[b][:, k, :], start=(k == 0), stop=(k == 7))
                acopy(uT[b].re("p a b -> p (a b)"), banks[1])

            if i < 16:
                rr_ = ssm_rounds(uT[b], (banks[4], banks[5]), lambda gp, s_, part, i=i: Xown[:, i, part * 16 + gp:part * 16 + gp + 1], 1, ybank=yb)
            else:
                rr_ = ssm_rounds(uT[b], (banks[4], banks[5]), lambda gp, s_, part: S0[:, part, s_, gp:gp + 1], 4, ybank=yb,
                                 last_out=lambda s_: (Sfin[:, 0, s_, :], Sfin[:, 1, s_, :]))
                rr_.append(lambda: ld(ssms_o, Sfin.re("p a b c -> p (a b c)")))
            return [p0] + rr_

        def post_stages(i):
            b = i % NB
            yb = banks[3] if i % 2 == 0 else banks[0]
            xt_ = xt[b]

            def q0():
                act(ysq, yb, AF.Square)
                ts(yt, ysq, 0.044715, 1.0, ALU.mult, ALU.add)
                tt(yt, yt, yb, ALU.mult)

            def q1():
                act(ysg, yt, AF.Sigmoid, scale=1.5957691216057308)
                tt(glb, ysg, yb, ALU.mult)

            def q2():
                transpose_to(gT, glb, 4, banks[2], eng="act")

            def q3():
                for nb in range(2):
                    for k in range(4):
                        mm(SC[:, nb * 512:(nb + 1) * 512], gT[:, k, :], W_glu[:, k, nb * 512:(nb + 1) * 512], start=(k == 0), stop=(k == 3))

            def q4():
                act(sg2, SC[:, 512:1024], AF.Sigmoid)
                tt(osf, sg2, SC[:, 0:512], ALU.mult)

            def q5():
                rmsnorm(mixb[:, 0:512], osf, 512, gos, sq, ss_a)

            def q6():
                if i < 16:
                    recip(rl, l_run[:, i * 4:(i + 1) * 4])
                    tt(omf, Oacc[:, i, :, :], rl.un(2).bc([128, 4, 128]), ALU.mult)
                    rmsnorm(mixb[:, 512:1024], omf.re("p a b -> p (a b)"), 512, gom, sq, ss_a)
                else:
                    rmsnorm(mixb[:, 512:1024], Osam, 512, gom, sq, ss_a)

            def q7():
                for half in range(2):
                    transpose_to(mixT[:, half * 4:(half + 1) * 4, :], mixb[:, half * 512:(half + 1) * 512], 4, banks[2], eng=("act" if half == 0 else "dve"))

            def q8():
                for nb in range(2):
                    for k in range(8):
                        mm(SC[:, nb * 512:(nb + 1) * 512], mixT[:, k, :], W_o[:, k, nb * 512:(nb + 1) * 512], start=(k == 0), stop=(k == 7))

            def q9():
                stt(rres, xt_, ALPHA, SC, ALU.mult, ALU.add)
                layernorm(hout, rres, lng, lnb, stats, mvv)
                ld(h1_d[i * 128:(i + 1) * 128, :], hout)

            return [q0, q1, q2, q3, q4, q5, q6, q7, q8, q9]

        prev_post = []
        for i in range(NOWN + 1):
            pre = pre_stages(i) if i < NOWN else []
            n_ = max(len(pre), len(prev_post))
            for k_ in range(n_):
                if k_ < len(pre):
                    pre[k_]()
                if k_ < len(prev_post):
                    prev_post[k_]()
            prev_post = post_stages(i) if i < NOWN else []

        barrier()
        bump[0] = mark_after_mem

        W_xq = alloc([128, 8, D], BF16, "W_xq"); W_xo = alloc([128, 8, D], BF16, "W_xo")
        ldw(W_xq, w_xq_d.re("(kt p) n -> p kt n", p=128)); ldw(W_xo, w_xo_d.re("(kt p) n -> p kt n", p=128))
        lng = alloc([128, D], F32, "lng2"); lnb = alloc([128, D], F32, "lnb2")
        ld(lng, lng_d[:, D:2 * D]); ld(lnb, lnb_d[:, D:2 * D])
        hin = [alloc([128, D], F32, "bh%d" % i) for i in range(NB)]
        hb2 = [alloc([128, D], BF16, "bhb%d" % i) for i in range(2)]; hT2 = [alloc([128, 8, 128], BF16, "bhT%d" % i) for i in range(2)]
        qxb2 = [alloc([128, D], BF16, "qxb%d" % i) for i in range(2)]; qxT2 = [alloc([128, 8, 128], BF16, "qxT%d" % i) for i in range(2)]
        mx42 = [alloc([128, 4], F32) for _ in range(2)]; neg42 = [alloc([128, 4], F32) for _ in range(2)]; sum42 = [alloc([128, 4], F32) for _ in range(2)]
        Px2 = [alloc([128, 4, 256], BF16, "Px%d" % i) for i in range(2)]; PxT2 = [alloc([128, 8, 128], BF16, "PxT%d" % i) for i in range(2)]
        oxb2 = [alloc([128, D], BF16, "oxb%d" % i) for i in range(2)]; oxT2 = [alloc([128, 8, 128], BF16, "oxT%d" % i) for i in range(2)]
        rres2 = [alloc([128, D], F32, "brres%d" % i) for i in range(2)]; hout2 = [alloc([128, D], F32, "bhout%d" % i) for i in range(2)]
        stats2 = [alloc([128, 12], F32) for _ in range(2)]; mvv2 = [alloc([128, 2], F32) for _ in range(2)]
        hb_ = hb2[0]; hT = hT2[0]; qxb = qxb2[0]; qxT = qxT2[0]; mx4 = mx42[0]; neg4 = neg42[0]; sum4 = sum42[0]
        Px = Px2[0]; PxT = PxT2[0]; oxb = oxb2[0]; oxT = oxT2[0]; rres = rres2[0]; hout = hout2[0]; stats = stats2[0]; mvv = mvv2[0]
        cmk = [alloc([128, D], F32, "cmk%d" % i) for i in range(2)]
        cmkb = [alloc([128, D], BF16, "cmkb%d" % i) for i in range(1)]
        mkTs4 = [alloc([128, 4, 2, 256], BF16, "mkTs%d" % i) for i in range(4)]; mvs4 = [alloc([128, 2, D], BF16, "mvs%d" % i) for i in range(4)]
        scx = alloc([128, 8], F32, "scx"); oxs = alloc([128, D], BF16, "oxs")

        def xa_stages(i, p):
            A_ = banks[p]; C_ = dbl[1 + p]
            Ab = A_.cast(BF16)

            def tr8(src):
                for k in range(8):
                    tr(Ab[:, k * 128:(k + 1) * 128], src[:, k * 128:(k + 1) * 128], identb)

            def ev8(dst):
                acopy(dst[:, 0:4, :], Ab[:, 0:512].re("p (k n) -> p k n", k=4))
                vcopy(dst[:, 4:8, :], Ab[:, 512:1024].re("p (k n) -> p k n", k=4))

            def t0():
                ld(hin[p], h1_d[i * 128:(i + 1) * 128, :])
                vcopy(hb2[p], hin[p], eng="pool")

            def t3():
                for nb in range(2):
                    for k in range(8):
                        mm(C_[:, nb * 512:(nb + 1) * 512], hT2[p][:, k, :], W_xq[:, k, nb * 512:(nb + 1) * 512], start=(k == 0), stop=(k == 7))

            def t4():
                for nb in range(2):
                    P.op("act", lambda e, nb=nb: e.mul(out=qxb2[p].ap[:, nb * 512:(nb + 1) * 512], in_=C_.ap[:, nb * 512:(nb + 1) * 512], mul=X_SCALE),
                         reads=[C_.b], writes=[qxb2[p].b])

            def t7():
                for h in range(4):
                    for et in range(2):
                        mm(C_[:, h * 256:(h + 1) * 256], qxT2[p][:, h * 2 + et, :], mkT[:, h, et, :], start=(et == 0), stop=(et == 1))

            def t8():
                red(mx42[p], C_.re("p (h m) -> p h m", h=4), ALU.max)
                ts(neg42[p], mx42[p], -1.0, None, ALU.mult)
                for h in range(4):
                    act(Px2[p][:, h, :], C_[:, h * 256:(h + 1) * 256], AF.Exp, bias=neg42[p][:, h:h + 1], accum=sum42[p][:, h:h + 1])

            def t11():
                for h in range(4):
                    for mt in range(2):
                        mm(C_[:, h * 256:(h + 1) * 256], PxT2[p][:, h * 2 + mt, :], mvb[:, mt, h * 256:(h + 1) * 256], start=(mt == 0), stop=(mt == 1))

            def t12():
                recip(sum42[p], sum42[p])
                tt(oxb2[p].re("p (h e) -> p h e", h=4), C_.re("p (h e) -> p h e", h=4), sum42[p].un(2).bc([128, 4, 256]), ALU.mult)

            def t15():
                for nb in range(2):
                    for k in range(8):
                        mm(C_[:, nb * 512:(nb + 1) * 512], oxT2[p][:, k, :], W_xo[:, k, nb * 512:(nb + 1) * 512], start=(k == 0), stop=(k == 7))

            def t16():
                stt(rres2[p], hin[p], ALPHA, C_, ALU.mult, ALU.add)
                layernorm(hout2[p], rres2[p], lng, lnb, stats2[p], mvv2[p])
                ld(h2_d[i * 128:(i + 1) * 128, :], hout2[p])

            return [t0, lambda: tr8(hb2[p]), lambda: ev8(hT2[p]), t3, t4, lambda: tr8(qxb2[p]), lambda: ev8(qxT2[p]), t7, t8,
                    lambda: tr8(Px2[p].re("p a b -> p (a b)")), lambda: ev8(PxT2[p]), t11, t12, lambda: tr8(oxb2[p]), lambda: ev8(oxT2[p]), t15, t16]

        def prep_items():
            items = []
            for s_ in range(4):
                for mt in range(2):
                    r0 = s_ * 256 + mt * 128

                    def l0(r0=r0):
                        ld(cmk[0], cmk_d[r0:r0 + 128, :]); ld(cmk[1], cmv_d[r0:r0 + 128, :])

                    def l1(s_=s_, mt=mt):
                        vcopy(cmkb[0], cmk[0], eng="pool")
                        vcopy(mvs4[s_][:, mt, :], cmk[1], eng="pool")

                    def l2(s_=s_, mt=mt):
                        for half in range(2):
                            bb = banks[6 + half].cast(BF16)
                            for k in range(4):
                                kk = half * 4 + k
                                tr(bb[:, k * 128:(k + 1) * 128], cmkb[0][:, kk * 128:(kk + 1) * 128], identb)

                    def l3(s_=s_, mt=mt):
                        for half in range(2):
                            bb = banks[6 + half].cast(BF16)
                            acopy(mkTs4[s_][:, half * 2:half * 2 + 2, :, mt * 128:(mt + 1) * 128].re("p h e m -> p (h e) m"), bb[:, 0:512].re("p (k m) -> p k m", k=4))

                    items += [l0, l1, l2, l3]
            return items

        pitems = prep_items()
        pi_ = 0
        slot = 0
        for ip in range(8):
            sa = xa_stages(2 * ip, 0); sb_ = xa_stages(2 * ip + 1, 1)
            for a_, b_ in zip(sa, sb_):
                a_(); b_()
                slot += 1
                if slot % 4 == 0 and pi_ < len(pitems):
                    pitems[pi_](); pi_ += 1
        while pi_ < len(pitems):
            pitems[pi_](); pi_ += 1

        for i in range(16, NOWN):
            b = i % NB
            ld(hin[b], h1_d[i * 128:(i + 1) * 128, :])
            vcopy(hb_, hin[b], eng="pool")
            for half in range(2):
                transpose_to(hT[:, half * 4:(half + 1) * 4, :], hb_[:, half * 512:(half + 1) * 512], 4, banks[0], eng=("act" if half == 0 else "dve"))
            for nb in range(2):
                for k in range(8):
                    mm(banks[1 + nb], hT[:, k, :], W_xq[:, k, nb * 512:(nb + 1) * 512], start=(k == 0), stop=(k == 7))
                P.op("act", lambda e, nb=nb: e.mul(out=qxb.ap[:, nb * 512:(nb + 1) * 512], in_=banks[1 + nb].ap, mul=X_SCALE), reads=[banks[1 + nb].b], writes=[qxb.b])
            for half in range(2):
                transpose_to(qxT[:, half * 4:(half + 1) * 4, :], qxb[:, half * 512:(half + 1) * 512], 4, banks[3], eng=("act" if half == 0 else "dve"))
            if i < 16:
                for h in range(4):
                    for et in range(2):
                        mm(SC[:, h * 256:(h + 1) * 256], qxT[:, h * 2 + et, :], mkT[:, h, et, :], start=(et == 0), stop=(et == 1))
                red(mx4, SC.re("p (h m) -> p h m", h=4), ALU.max)
                ts(neg4, mx4, -1.0, None, ALU.mult)
                for h in range(4):
                    act(Px[:, h, :], SC[:, h * 256:(h + 1) * 256], AF.Exp, bias=neg4[:, h:h + 1], accum=sum4[:, h:h + 1])
                for half in range(2):
                    transpose_to(PxT[:, half * 4:(half + 1) * 4, :], Px.re("p a b -> p (a b)")[:, half * 512:(half + 1) * 512], 4, banks[4], eng=("act" if half == 0 else "dve"))
                for h in range(4):
                    for mt in range(2):
                        mm(SC[:, h * 256:(h + 1) * 256], PxT[:, h * 2 + mt, :], mvb[:, mt, h * 256:(h + 1) * 256], start=(mt == 0), stop=(mt == 1))
                recip(sum4, sum4)
                tt(oxb.re("p (h e) -> p h e", h=4), SC.re("p (h e) -> p h e", h=4), sum4.un(2).bc([128, 4, 256]), ALU.mult)
            else:
                for s_ in range(4):
                    qs = slice(s_ * 32, (s_ + 1) * 32)
                    mkTs = mkTs4[s_]; mvs = mvs4[s_]
                    for h in range(4):
                        for et in range(2):
                            mm(SC[0:32, h * 256:(h + 1) * 256], qxT[:, h * 2 + et, qs], mkTs[:, h, et, :], start=(et == 0), stop=(et == 1))
                    red(mx4[0:32, :], SC[0:32, :].re("p (h m) -> p h m", h=4), ALU.max)
                    ts(neg4[0:32, :], mx4[0:32, :], -1.0, None, ALU.mult)
                    for h in range(4):
                        act(Px[0:32, h, :], SC[0:32, h * 256:(h + 1) * 256], AF.Exp, bias=neg4[0:32, h:h + 1], accum=sum4[0:32, h:h + 1])
                    bb = banks[4].cast(BF16)
                    for k in range(8):
                        tr(bb[:, k * 32:(k + 1) * 32], Px.re("p a b -> p (a b)")[0:32, k * 128:(k + 1) * 128], identb[0:32, 0:32])
                    vcopy(PxT[:, :, 0:32], bb[:, 0:256].re("p (k q) -> p k q", k=8))
                    for h in range(4):
                        for mt in range(2):
                            mm(SC[0:32, h * 256:(h + 1) * 256], PxT[:, h * 2 + mt, 0:32], mvs[:, mt, h * 256:(h + 1) * 256], start=(mt == 0), stop=(mt == 1))
                    recip(sum4[0:32, :], sum4[0:32, :])
                    tt(oxs[0:32, :].re("p (h e) -> p h e", h=4), SC[0:32, :].re("p (h e) -> p h e", h=4), sum4[0:32, :].un(2).bc([32, 4, 256]), ALU.mult)
                    ld(oxb[s_ * 32:(s_ + 1) * 32, :], oxs[0:32, :])
            for half in range(2):
                transpose_to(oxT[:, half * 4:(half + 1) * 4, :], oxb[:, half * 512:(half + 1) * 512], 4, banks[5], eng=("act" if half == 0 else "dve"))
            for nb in range(2):
                for k in range(8):
                    mm(banks[1 + nb], oxT[:, k, :], W_xo[:, k, nb * 512:(nb + 1) * 512], start=(k == 0), stop=(k == 7))
            for nb in range(2):
                stt(rres[:, nb * 512:(nb + 1) * 512], hin[b][:, nb * 512:(nb + 1) * 512], ALPHA, banks[1 + nb], ALU.mult, ALU.add)
            layernorm(hout, rres, lng, lnb, stats, mvv)
            ld(h2_d[i * 128:(i + 1) * 128, :], hout)

        barrier()
        bump[0] = mark_after_mem

        W1 = alloc([128, 8, 4096], BF16, "W1"); W2 = alloc([128, 32, D], BF16, "W2")
        for c in range(2):
            ldw(W1[:, :, c * 2048:(c + 1) * 2048], w_ff1_d.re("(kt p) n -> p kt n", p=128)[:, :, c * 2048:(c + 1) * 2048])
        for c in range(4):
            ldw(W2[:, c * 8:(c + 1) * 8, :], w_ff2_d.re("(kt p) n -> p kt n", p=128)[:, c * 8:(c + 1) * 8, :])
        lng = alloc([128, D], F32, "lng3"); lnb = alloc([128, D], F32, "lnb3")
        ld(lng, lng_d[:, 2 * D:3 * D]); ld(lnb, lnb_d[:, 2 * D:3 * D])
        hin = alloc([128, D], F32, "ch")
        hb_ = alloc([128, D], BF16, "chb"); hT = alloc([128, 8, 512], BF16, "chT")
        zr = [alloc([128, 512], F32, "zr%d" % i) for i in range(2)]; zT = alloc([128, 32, 512], BF16, "zT")
        rres = alloc([128, D], F32, "crres"); hout = alloc([128, D], F32, "chout")
        stats = alloc([128, 12], F32); mvv = alloc([128, 2], F32)
        groups = [list(range(g_ * 4, g_ * 4 + 4)) for g_ in range(4)] + [[16]]
        for grp in groups:
            nt = len(grp)
            for q_, i in enumerate(grp):
                ld(hin, h2_d[i * 128:(i + 1) * 128, :])
                vcopy(hb_, hin, eng="pool")
                for half in range(2):
                    transpose_to(hT[:, half * 4:(half + 1) * 4, q_ * 128:(q_ + 1) * 128], hb_[:, half * 512:(half + 1) * 512], 4, banks[0], eng=("act" if half == 0 else "dve"))
            W_ = nt * 128
            for f in range(32):
                bk = banks[1 + f % 2]
                for k in range(8):
                    mm(bk[:, 0:W_], W1[:, k, f * 128:(f + 1) * 128], hT[:, k, 0:W_], start=(k == 0), stop=(k == 7))
                act(zr[f % 2][:, 0:W_], bk[:, 0:W_], AF.Relu)
                tt(zT[:, f, 0:W_], zr[f % 2][:, 0:W_], zr[f % 2][:, 0:W_], ALU.mult)
            for q_, i in enumerate(grp):
                for nb in range(2):
                    for f in range(32):
                        mm(SC[:, nb * 512:(nb + 1) * 512], zT[:, f, q_ * 128:(q_ + 1) * 128], W2[:, f, nb * 512:(nb + 1) * 512], start=(f == 0), stop=(f == 31))
                ld(hin, h2_d[i * 128:(i + 1) * 128, :])
                stt(rres, hin, ALPHA, SC, ALU.mult, ALU.add)
                layernorm(hout, rres, lng, lnb, stats, mvv)
                ld(y_o[i * 128:(i + 1) * 128, :], hout)

        P.emit(st)
    return nc


def _lay_gp(a):
    sh = a.shape[2:]
    n = len(sh)
    return np.ascontiguousarray(a.reshape(16, 2, 64, *sh).transpose(1, 2, 0, *range(3, 3 + n)).reshape(128, 16, *sh))


def _bc(v, n=128):
    return np.ascontiguousarray(np.broadcast_to(np.asarray(v, np.float32).reshape(1, -1), (n, np.asarray(v).size)))


def _rope_tables(pos):
    inv = (10000.0 ** (-np.arange(32, dtype=np.float32) / 32)).astype(np.float32)
    ang = pos.astype(np.float32)[:, None] * inv[None, :]
    c = np.cos(ang).astype(np.float32)
    s = np.sin(ang).astype(np.float32)
    return np.concatenate([c, c], 1), np.concatenate([-s, s], 1)


_NC_CACHE = {}


def kernel(x_prompt, x_sample, mem_prompt, cache_mla_ckv, cache_mla_kpe, state_ssm_re, state_ssm_im,
           cache_mem_k, cache_mem_v, w_in, g_q, w_q_up, g_kv, w_kv_up, a_re, a_im, b_re, b_im, c_re, c_im,
           d_skip, log_dt, w_glu, g_out_ssm, g_out_mla, w_o, w_xq, w_xk, w_xv, w_xo, w_ff1, w_ff2, ln_g, ln_b):
    f = lambda a: np.ascontiguousarray(np.asarray(a, dtype=np.float32))
    x_prompt = f(x_prompt); x_sample = f(x_sample)
    xp = x_prompt[0]
    cre = np.zeros((128, 16, 32), np.float32); cimn = np.zeros((128, 16, 32), np.float32)
    cr = f(c_re)[0].reshape(16, 2, 16, 64); ci = f(c_im)[0].reshape(16, 2, 16, 64)
    for g2 in range(2):
        cre[g2 * 64:(g2 + 1) * 64, :, g2 * 16:(g2 + 1) * 16] = cr[:, g2].transpose(2, 0, 1)
        cimn[g2 * 64:(g2 + 1) * 64, :, g2 * 16:(g2 + 1) * 16] = -ci[:, g2].transpose(2, 0, 1)
    dblk = np.zeros((128, 16, 32), np.float32)
    dd = f(d_skip)[0].reshape(512)
    for gp in range(16):
        for c in range(32):
            ch = gp * 32 + c
            dblk[ch % 128, gp, c] = dd[ch]
    pos_p = np.arange(16384)
    rcp, rsp = _rope_tables(pos_p)
    common = {
        "xp": xp, "mem": f(mem_prompt)[0], "ident": np.eye(128, dtype=np.float32),
        "w_in": f(w_in)[0], "w_q": f(w_q_up)[0].reshape(384, 768), "w_kv": f(w_kv_up)[0].reshape(256, 1024),
        "w_glu": f(w_glu)[0], "w_o": f(w_o)[0], "w_xq": f(w_xq)[0].reshape(D, D), "w_xk": f(w_xk)[0].reshape(D, D),
        "w_xv": f(w_xv)[0].reshape(D, D), "w_xo": f(w_xo)[0].reshape(D, D), "w_ff1": f(w_ff1)[0], "w_ff2": f(w_ff2)[0],
        "gq": _bc(f(g_q)[0]), "gkv": _bc(f(g_kv)[0]), "gos": _bc(f(g_out_ssm)[0]), "gom": _bc(f(g_out_mla)[0]),
        "lng": _bc(f(ln_g)[0].reshape(-1)), "lnb": _bc(f(ln_b)[0].reshape(-1)),
        "ropec_p": rcp, "ropes_p": rsp,
        "ar": _lay_gp(f(a_re)[0]), "ai": _lay_gp(f(a_im)[0]),
        "ldt": _lay_gp(np.ascontiguousarray(np.broadcast_to(f(log_dt)[0][:, None], (32, 64)))),
        "bre": _lay_gp(f(b_re)[0]).reshape(128, 256), "bim": _lay_gp(f(b_im)[0]).reshape(128, 256),
        "jj": _bc(np.arange(1, 129, dtype=np.float32)), "jj2": _bc(127.0 - np.arange(128, dtype=np.float32)),
        "cre": cre.reshape(128, 512), "cimn": cimn.reshape(128, 512), "dblk": dblk.reshape(128, 512),
    }
    qi = np.arange(128)[:, None]
    in_maps = []
    for c in range(NCORES):
        tiles = [8 * i + c for i in range(16)]
        xo = np.concatenate([xp[t * 128:(t + 1) * 128] for t in tiles] + [x_sample[4 * c:4 * c + 4].reshape(128, D)], 0)
        pos_o = np.concatenate([np.arange(t * 128, (t + 1) * 128) for t in tiles] + [np.tile(1024 + np.arange(32), 4)])
        rco, rso = _rope_tables(pos_o)
        kj = np.arange(1024)[None, :]
        vis = ((kj // 128) < c) | (((kj // 128) == c) & (((kj % 128) // 64) <= (qi // 64)))
        maskd = np.where(vis, 0.0, NEG).astype(np.float32)
        onehot = np.zeros((128, 8), np.float32); onehot[:, c] = 1.0
        s0 = np.stack([_lay_gp(f(state_ssm_re)[0, 4 * c + s]) for s in range(4)], 1)
        s0i = np.stack([_lay_gp(f(state_ssm_im)[0, 4 * c + s]) for s in range(4)], 1)
        m = dict(common)
        m.update({
            "xo": np.ascontiguousarray(xo), "ropec_o": rco, "ropes_o": rso, "maskd": maskd, "onehot": onehot,
            "s0": np.ascontiguousarray(np.stack([s0, s0i], 1).reshape(128, 128)),
            "cckv": f(cache_mla_ckv)[0, 4 * c:4 * c + 4].reshape(4096, 256),
            "ckpe": f(cache_mla_kpe)[0, 4 * c:4 * c + 4].reshape(4096, 64),
            "cmk": f(cache_mem_k)[0, 4 * c:4 * c + 4].reshape(1024, D),
            "cmv": f(cache_mem_v)[0, 4 * c:4 * c + 4].reshape(1024, D),
        })
        in_maps.append(m)
    if "nc" not in _NC_CACHE:
        _NC_CACHE["nc"] = build_nc()
    res = run_bass_kernel_spmd(_NC_CACHE["nc"], in_maps, core_ids=list(range(NCORES)))
    R = res.results
    y_p = np.zeros((1, 16384, D), np.float32); y_s = np.zeros((32, 32, D), np.float32)
    ckv_s = np.zeros((1, 32, 32, 256), np.float32); kpe_s = np.zeros((1, 32, 32, 64), np.float32)
    sre_s = np.zeros((1, 32, 32, 64), np.float32); sim_s = np.zeros((1, 32, 32, 64), np.float32)

    def unlay(a):
        return a.reshape(2, 64, 16).transpose(2, 0, 1).reshape(32, 64)

    for c in range(NCORES):
        yo = R[c]["y_o"]
        for i in range(16):
            t = 8 * i + c
            y_p[0, t * 128:(t + 1) * 128] = yo[i * 128:(i + 1) * 128]
        y_s[4 * c:4 * c + 4] = yo[16 * 128:].reshape(4, 32, D)
        ckv_s[0, 4 * c:4 * c + 4] = R[c]["ckvs_o"].reshape(4, 32, 256)
        kpe_s[0, 4 * c:4 * c + 4] = R[c]["kpes_o"].reshape(4, 32, 64)
        sf = R[c]["ssms_o"].reshape(128, 2, 4, 16)
        for s in range(4):
            sre_s[0, 4 * c + s] = unlay(sf[:, 0, s, :])
            sim_s[0, 4 * c + s] = unlay(sf[:, 1, s, :])
    r0 = R[0]
    ckv_p = r0["ckv_o"].reshape(1, 1, 16384, 256); kpe_p = r0["kpe_o"].reshape(1, 1, 16384, 64)
    sp = r0["ssmp_o"]
    sre_p = unlay(sp[:, 0:16]).reshape(1, 1, 32, 64); sim_p = unlay(sp[:, 16:32]).reshape(1, 1, 32, 64)
    mk_p = r0["memk_o"].reshape(1, 1, 256, 4, 256); mv_p = r0["memv_o"].reshape(1, 1, 256, 4, 256)
    return (y_p, y_s, ckv_p, kpe_p, sre_p, sim_p, mk_p, mv_p, ckv_s, kpe_s, sre_s, sim_s)
```

```python
import math
from contextlib import ExitStack

import numpy as np
import concourse.bass as bass
import concourse.mybir as mybir
from concourse.bass_utils import run_bass_kernel_spmd

F32 = mybir.dt.float32
BF16 = mybir.dt.bfloat16
I32 = mybir.dt.int32
AF = mybir.ActivationFunctionType
ALU = mybir.AluOpType
AX = mybir.AxisListType

NCORES = 8
D = 1024
NTP = 128
NOWN = 17
EPS = 1e-5
ALPHA = 2.0 ** 0.25
MLA_SCALE = 192.0 ** -0.5
X_SCALE = 256.0 ** -0.5
TWO_PI = 2.0 * math.pi
NEG = -1e30
NRING = 24
COMPUTE = ("pe", "act", "dve", "pool")


class Buf:
    __slots__ = ("name", "lw", "rd")

    def __init__(self, name=""):
        self.name = name
        self.lw = None
        self.rd = []


def _flat(bs):
    out = []
    for b in bs:
        if isinstance(b, (tuple, list)):
            out.extend(b)
        else:
            out.append(b)
    return out


class Op:
    __slots__ = ("eng", "fn", "deps", "isdma", "flag", "cnt", "ring", "n")

    def __init__(self, eng, fn, isdma):
        self.eng = eng
        self.fn = fn
        self.isdma = isdma
        self.deps = set()
        self.flag = False
        self.cnt = 0
        self.ring = None
        self.n = 0


class Prog:
    def __init__(self, nc):
        self.nc = nc
        self.ops = {e: [] for e in ("pe", "act", "dve", "pool", "sp")}
        self.ndma = {e: 0 for e in self.ops}
        self.allops = []
        self.floor = []
        self.dmas_since = []

    def _add(self, eng, fn, reads, writes, isdma):
        op = Op(eng, fn, isdma)
        reads = _flat(reads)
        writes = _flat(writes)
        for f in self.floor:
            op.deps.add(f)
        for b in reads:
            if b.lw is not None:
                op.deps.add(b.lw)
        for b in writes:
            if b.lw is not None:
                op.deps.add(b.lw)
            for r in b.rd:
                op.deps.add(r)
        for b in reads:
            b.rd.append(op)
        for b in writes:
            b.lw = op
            b.rd = []
        op.deps.discard(op)
        if isdma:
            op.n = self.ndma[eng]
            self.ndma[eng] += 1
            self.dmas_since.append(op)
        self.ops[eng].append(op)
        self.allops.append(op)
        return op

    def op(self, eng, fn, reads=(), writes=()):
        return self._add(eng, fn, reads, writes, False)

    def dma(self, eng, fn, reads=(), writes=()):
        return self._add(eng, fn, reads, writes, True)

    def barrier(self, fn):
        op = Op("pool", fn, False)
        for f in self.floor:
            op.deps.add(f)
        for e in COMPUTE:
            for o in reversed(self.ops[e]):
                if not o.isdma:
                    op.deps.add(o)
                    break
        for o in self.dmas_since:
            op.deps.add(o)
        self.dmas_since = []
        self.ops["pool"].append(op)
        self.allops.append(op)
        self.floor = [op]

    def emit(self, stack):
        nc = self.nc
        for op in self.allops:
            for d in op.deps:
                if d.eng == "pe" and op.eng == "pe" and not d.isdma:
                    continue
                d.flag = True
        sems = {e: stack.enter_context(nc.semaphore("s_" + e)) for e in COMPUTE}
        rings = {}
        for e in self.ops:
            if self.ndma[e]:
                rings[e] = [stack.enter_context(nc.semaphore("r_%s_%d" % (e, i))) for i in range(NRING)]
        for e in self.ops:
            c = 0
            for op in self.ops[e]:
                if op.isdma:
                    op.ring = rings[e][op.n % NRING]
                    op.cnt = 16 * (op.n // NRING + 1)
                elif op.flag:
                    c += 1
                    op.cnt = c
        block = stack.enter_context(nc.Block())

        def run(e, h):
            waited = {}

            def wait(sem, val):
                k = id(sem)
                if waited.get(k, 0) >= val:
                    return
                waited[k] = val
                h.wait_ge(sem, val)

            for op in self.ops[e]:
                for d in op.deps:
                    if d.isdma:
                        wait(d.ring, d.cnt)
                    else:
                        if d.eng == "pe" and e == "pe":
                            continue
                        wait(sems[d.eng], d.cnt)
                if op.isdma and op.n >= NRING:
                    wait(op.ring, op.cnt - 16)
                ins = op.fn(h)
                if op.isdma:
                    ins.then_inc(op.ring, 16)
                elif op.flag:
                    ins.then_inc(sems[e], 1)
            if e in rings:
                n = self.ndma[e]
                for i in range(min(n, NRING)):
                    last = ((n - 1 - i) // NRING) * NRING + i
                    wait(rings[e][i], 16 * (last // NRING + 1))

        @block.tensor
        def _(h):
            run("pe", h)

        @block.scalar
        def _(h):
            run("act", h)

        @block.vector
        def _(h):
            run("dve", h)

        @block.gpsimd
        def _(h):
            run("pool", h)

        @block.sync
        def _(h):
            run("sp", h)


class V:
    __slots__ = ("ap", "b")

    def __init__(self, ap, b):
        self.ap = ap
        self.b = b

    def __getitem__(self, idx):
        return V(self.ap[idx], self.b)

    def re(self, pat, **kw):
        return V(self.ap.rearrange(pat, **kw), self.b)

    def cast(self, dt):
        return V(self.ap.bitcast(dt), self.b)

    def bc(self, shape):
        return V(self.ap.broadcast_to(shape), self.b)

    def un(self, ax):
        return V(self.ap.unsqueeze(ax), self.b)


def build_nc():
    nc = bass.Bass("TRN2", target_bir_lowering=False)
    P = Prog(nc)
    dram = {}

    def din(name, shape, dt=F32):
        t = nc.dram_tensor(name, list(shape), dt, kind="ExternalInput")
        dram[name] = V(t.ap(), Buf(name))
        return dram[name]

    def dout(name, shape):
        t = nc.dram_tensor(name, list(shape), F32, kind="ExternalOutput")
        dram[name] = V(t.ap(), Buf(name))
        return dram[name]

    def dscr(name, shape, dt=F32):
        t = nc.dram_tensor(name, list(shape), dt)
        return V(t.ap(), Buf(name))

    xp = din("xp", [NTP * 128, D])
    xo = din("xo", [NOWN * 128, D])
    mem = din("mem", [256, D])
    ident_d = din("ident", [128, 128])
    w_in_d = din("w_in", [D, 1216]); w_q_d = din("w_q", [384, 768]); w_kv_d = din("w_kv", [256, 1024])
    w_glu_d = din("w_glu", [512, 1024]); w_o_d = din("w_o", [D, D]); w_xq_d = din("w_xq", [D, D])
    w_xk_d = din("w_xk", [D, D]); w_xv_d = din("w_xv", [D, D]); w_xo_d = din("w_xo", [D, D])
    w_ff1_d = din("w_ff1", [D, 4096]); w_ff2_d = din("w_ff2", [4096, D])
    gq_d = din("gq", [128, 384]); gkv_d = din("gkv", [128, 256]); gos_d = din("gos", [128, 512]); gom_d = din("gom", [128, 512])
    lng_d = din("lng", [128, 3 * D]); lnb_d = din("lnb", [128, 3 * D])
    ropec_p = din("ropec_p", [NTP * 128, 64]); ropes_p = din("ropes_p", [NTP * 128, 64])
    ropec_o = din("ropec_o", [NOWN * 128, 64]); ropes_o = din("ropes_o", [NOWN * 128, 64])
    maskd_d = din("maskd", [128, 1024]); onehot_d = din("onehot", [128, 8])
    ar_d = din("ar", [128, 16]); ai_d = din("ai", [128, 16]); ldt_d = din("ldt", [128, 16])
    bre_d = din("bre", [128, 256]); bim_d = din("bim", [128, 256]); jj_d = din("jj", [128, 128]); jj2_d = din("jj2", [128, 128])
    cre_d = din("cre", [128, 512]); cimn_d = din("cimn", [128, 512]); dblk_d = din("dblk", [128, 512])
    s0_d = din("s0", [128, 128])
    cckv_d = din("cckv", [4 * 1024, 256]); ckpe_d = din("ckpe", [4 * 1024, 64])
    cmk_d = din("cmk", [4 * 256, D]); cmv_d = din("cmv", [4 * 256, D])

    y_o = dout("y_o", [NOWN * 128, D])
    ckv_o = dout("ckv_o", [NTP * 128, 256]); kpe_o = dout("kpe_o", [NTP * 128, 64])
    ssmp_o = dout("ssmp_o", [128, 32])
    memk_o = dout("memk_o", [256, D]); memv_o = dout("memv_o", [256, D])
    ckvs_o = dout("ckvs_o", [128, 256]); kpes_o = dout("kpes_o", [128, 64]); ssms_o = dout("ssms_o", [128, 128])

    h1_d = dscr("h1_d", [NOWN * 128, D]); h2_d = dscr("h2_d", [NOWN * 128, D])
    tab_d = {n_: dscr("tab_" + n_, [128, 2048]) for n_ in ("S_t", "C_t", "R_t")}
    tab_d["WBre"] = dscr("tab_WBre", [128, 2048], BF16); tab_d["WBim"] = dscr("tab_WBim", [128, 2048], BF16)

    with ExitStack() as st:
        ARENA_N = 52000
        arena = st.enter_context(nc.sbuf_tensor("arena", [128, ARENA_N], F32))
        bump = [0]

        def alloc(shape, dt=F32, name=""):
            n = 1
            for s_ in shape[1:]:
                n *= s_
            words = (n * (2 if dt == BF16 else 4) + 3) // 4
            words = (words + 7) // 8 * 8
            off = bump[0]
            bump[0] += words
            assert bump[0] <= ARENA_N, ("SBUF arena overflow", name, bump[0])
            ap = arena[:, off:off + words]
            if dt != F32:
                ap = ap.bitcast(dt)
            ap = ap[:, 0:n]
            if len(shape) > 2:
                names = " ".join("d%d" % i for i in range(len(shape) - 1))
                kw = {"d%d" % i: shape[i + 1] for i in range(len(shape) - 1)}
                ap = ap.rearrange("p (%s) -> p %s" % (names, names), **kw)
            return V(ap, Buf(name))

        banks = []
        dbl = []
        for d_ in range(4):
            t = st.enter_context(nc.psum_tensor("dbank%d" % d_, [128, 1024], F32))
            b0 = Buf("bank%d" % (2 * d_)); b1 = Buf("bank%d" % (2 * d_ + 1))
            banks.append(V(t[:, 0:512], b0)); banks.append(V(t[:, 512:1024], b1))
            dbl.append(V(t[:], (b0, b1)))
        SC = dbl[3]
        SCs = [dbl[3], dbl[0]]

        def mm(out, lhsT, rhs, start=True, stop=True):
            P.op("pe", lambda e: e.matmul(out.ap, lhsT=lhsT.ap, rhs=rhs.ap, start=start, stop=stop),
                 reads=[lhsT.b, rhs.b], writes=[out.b])

        def tr(out, in_, idt):
            P.op("pe", lambda e: e.transpose(out=out.ap, in_=in_.ap, identity=idt.ap), reads=[in_.b, idt.b], writes=[out.b])

        def act(out, in_, func, bias=None, scale=None, accum=None):
            kw = {}
            rd = [in_.b]
            wr = [out.b]
            if bias is not None:
                if isinstance(bias, V):
                    kw["bias"] = bias.ap; rd.append(bias.b)
                else:
                    kw["bias"] = bias
            if scale is not None:
                if isinstance(scale, V):
                    kw["scale"] = scale.ap; rd.append(scale.b)
                else:
                    kw["scale"] = scale
            if accum is not None:
                kw["accum_out"] = accum.ap; wr.append(accum.b)
            P.op("act", lambda e: e.activation(out=out.ap, in_=in_.ap, func=func, **kw), reads=rd, writes=wr)

        def acopy(out, in_):
            P.op("act", lambda e: e.copy(out=out.ap, in_=in_.ap), reads=[in_.b], writes=[out.b])

        def vcopy(out, in_, eng="dve"):
            P.op(eng, lambda e: e.tensor_copy(out=out.ap, in_=in_.ap), reads=[in_.b], writes=[out.b])

        def tt(out, in0, in1, op, eng="dve"):
            P.op(eng, lambda e: e.tensor_tensor(out=out.ap, in0=in0.ap, in1=in1.ap, op=op), reads=[in0.b, in1.b], writes=[out.b])

        def ts(out, in0, s1, s2, op0, op1=None, eng="dve"):
            rd = [in0.b]
            a1 = s1
            a2 = s2
            if isinstance(s1, V):
                a1 = s1.ap; rd.append(s1.b)
            if isinstance(s2, V):
                a2 = s2.ap; rd.append(s2.b)
            if op1 is None:
                P.op(eng, lambda e: e.tensor_scalar(out=out.ap, in0=in0.ap, scalar1=a1, scalar2=None, op0=op0), reads=rd, writes=[out.b])
            else:
                P.op(eng, lambda e: e.tensor_scalar(out=out.ap, in0=in0.ap, scalar1=a1, scalar2=a2, op0=op0, op1=op1), reads=rd, writes=[out.b])

        def stt(out, in0, scalar, in1, op0, op1):
            rd = [in0.b, in1.b]
            a = scalar
            if isinstance(scalar, V):
                a = scalar.ap; rd.append(scalar.b)
            P.op("dve", lambda e: e.scalar_tensor_tensor(out=out.ap, in0=in0.ap, scalar=a, in1=in1.ap, op0=op0, op1=op1), reads=rd, writes=[out.b])

        def red(out, in_, op, axis=AX.X):
            P.op("dve", lambda e: e.tensor_reduce(out=out.ap, in_=in_.ap, axis=axis, op=op), reads=[in_.b], writes=[out.b])

        def recip(out, in_):
            P.op("dve", lambda e: e.reciprocal(out=out.ap, in_=in_.ap), reads=[in_.b], writes=[out.b])

        def scan(out, d0, d1, init):
            P.op("dve", lambda e: e.tensor_tensor_scan(out=out.ap, data0=d0.ap, data1=d1.ap, initial=init.ap, op0=ALU.mult, op1=ALU.add),
                 reads=[d0.b, d1.b, init.b], writes=[out.b])

        def mset(out, val, eng="pool"):
            P.op(eng, lambda e: e.memset(out.ap, val), writes=[out.b])

        def ld(out, in_, eng="sp"):
            P.dma(eng, lambda e: e.dma_start(out=out.ap, in_=in_.ap), reads=[in_.b], writes=[out.b])

        def ldw(out, in_):
            P.dma("pool", lambda e: e.dma_start(out=out.ap, in_=in_.ap), reads=[in_.b], writes=[out.b])

        ident = alloc([128, 128], F32, "ident"); identb = alloc([128, 128], BF16, "identb")
        bar_scr = alloc([128, 8], F32, "barscr")
        ld(ident, ident_d); ldw(identb, ident_d)
        epsc = alloc([128, 1], F32, "epsc"); mset(epsc, EPS)

        def barrier():
            P.barrier(lambda e: e.memset(bar_scr.ap, 0.0))

        def load_xT(src_rows, xt, xT, bank):
            ld(xt, src_rows)
            for hb in range(2):
                for k in range(4):
                    kk = hb * 4 + k
                    tr(bank[:, k * 128:(k + 1) * 128], xt[:, kk * 128:(kk + 1) * 128], ident)
                src = bank.re("p (k n) -> p k n", k=4)
                if hb == 0:
                    acopy(xT[:, 0:4, :], src)
                else:
                    vcopy(xT[:, 4:8, :], src)

        def transpose_to(dst, src, ncol, bank, eng="act", rows=128):
            bb = bank.cast(BF16)
            for k in range(ncol):
                tr(bb[:, k * 128:k * 128 + rows], src[:, k * 128:(k + 1) * 128], identb[0:rows, 0:rows])
            s_ = bb[:, 0:ncol * 128].re("p (k n) -> p k n", k=ncol)[:, :, 0:rows]
            if eng == "act":
                acopy(dst, s_)
            else:
                vcopy(dst, s_)

        def rmsnorm(out, src, n, gtile, sq, ss):
            act(sq, src, AF.Square, accum=ss)
            act(ss, ss, AF.Ln, scale=1.0 / n, bias=epsc)
            act(ss, ss, AF.Exp, scale=-0.5)
            stt(out, src, ss, gtile, ALU.mult, ALU.mult)

        def layernorm(out, r, g, b, stats, mv):
            for c in range(2):
                P.op("dve", lambda e, c=c: e.bn_stats(out=stats.ap[:, c * 6:(c + 1) * 6], in_=r.ap[:, c * 512:(c + 1) * 512]), reads=[r.b], writes=[stats.b])
            P.op("dve", lambda e: e.bn_aggr(out=mv.ap[:, 0:2], in_=stats.ap[:, 0:12]), reads=[stats.b], writes=[mv.b])
            act(mv[:, 1:2], mv[:, 1:2], AF.Ln, bias=epsc)
            act(mv[:, 1:2], mv[:, 1:2], AF.Exp, scale=-0.5)
            ts(out, r, mv[:, 0:1], mv[:, 1:2], ALU.subtract, ALU.mult)
            tt(out, out, g, ALU.mult, eng="pool")
            tt(out, out, b, ALU.add, eng="pool")

        def rope(out, src, cc, ss_, tmp, nh, eng="dve"):
            ccb = cc.un(1).bc([128, nh, 64]); ssb = ss_.un(1).bc([128, nh, 64])
            tt(out, src, ccb, ALU.mult, eng=eng)
            tt(tmp[:, :, 0:32], src[:, :, 32:64], ssb[:, :, 0:32], ALU.mult, eng=eng)
            tt(tmp[:, :, 32:64], src[:, :, 0:32], ssb[:, :, 32:64], ALU.mult, eng=eng)
            tt(out, out, tmp, ALU.add, eng="pool")

        mkT = alloc([128, 4, 2, 256], BF16, "mkT")
        mvb = alloc([128, 2, D], BF16, "mvb")
        mark_after_mem = bump[0]
        btab = Buf("tab")
        W_in_u = alloc([128, 8, 512], BF16, "W_in_u"); W_in_kv = alloc([128, 8, 320], BF16, "W_in_kv")
        LrT = alloc([128, 16, 128], BF16, "LrT"); LiT = alloc([128, 16, 128], BF16, "LiT")
        BRb = alloc([128, 16, 32], F32, "BRb"); BIb = alloc([128, 16, 32], F32, "BIb")
        lam128 = alloc([128, 2, 16], F32, "lam128")
        Xpp = alloc([128, 2, 32], F32, "Xpp")
        Xown = alloc([128, 16, 32], F32, "Xown")
        ohot = alloc([128, 8], F32, "ohot")
        Oacc = alloc([128, 16, 4, 128], F32, "Oacc")
        m_run = alloc([128, 64], F32, "m_run"); l_run = alloc([128, 64], F32, "l_run")
        Osam = alloc([128, 512], F32, "Osam")
        ssmx = {}
        mark_after_mixer_state = bump[0]
        QTn = alloc([128, NOWN, 4, 128], BF16, "QTn"); QTr = alloc([128, NOWN, 4, 128], BF16, "QTr")
        mark_after_q = bump[0]

        w_in_v = w_in_d.re("(kt p) n -> p kt n", p=128)
        ldw(W_in_u, w_in_v[:, :, 0:512]); ldw(W_in_kv, w_in_v[:, :, 896:1216])
        ld(ohot, onehot_d)

        S_t = alloc([128, 16, 128], F32, "S_t"); C_t = alloc([128, 16, 128], F32, "C_t"); R_t = alloc([128, 16, 128], F32, "R_t")
        for v_ in (S_t, C_t, R_t):
            v_.b = btab
        WBre = alloc([128, 16, 128], BF16, "WBre"); WBim = alloc([128, 16, 128], BF16, "WBim")

        p0 = bump[0]
        ar = alloc([128, 16]); ai = alloc([128, 16]); ldt = alloc([128, 16]); jj = alloc([128, 128])
        bre = alloc([128, 16, 16]); bim = alloc([128, 16, 16])
        bsm = Buf("ssm_small")
        for v_ in (ar, ai, ldt, jj, bre, bim):
            v_.b = bsm
        ld(ar, ar_d); ld(ai, ai_d); ld(ldt, ldt_d); ld(jj, jj_d)
        ld(bre.re("p a b -> p (a b)"), bre_d); ld(bim.re("p a b -> p (a b)"), bim_d)
        dt_ = alloc([128, 16]); th = alloc([128, 16]); rr = alloc([128, 16])
        sm = [alloc([128, 16]) for _ in range(8)]
        for v_ in [dt_, th, rr] + sm:
            v_.b = bsm
        act(dt_, ldt, AF.Exp)
        tt(th, ai, dt_, ALU.mult)
        tt(rr, ar, dt_, ALU.mult)
        act(rr, rr, AF.Exp)
        A_t = alloc([128, 16, 128]); T1 = alloc([128, 2048]); TI = alloc([128, 2048], I32)
        for v_ in (A_t, T1, TI):
            v_.b = btab
        jjb = jj.un(1).bc([128, 16, 128])
        tt(A_t, jjb, th.un(2).bc([128, 16, 128]), ALU.mult)
        tt(R_t, jjb, rr.un(2).bc([128, 16, 128]), ALU.max)
        tt(R_t, R_t, rr.un(2).bc([128, 16, 128]), ALU.min)
        Af = A_t.re("p a b -> p (a b)")
        TIf = TI.cast(F32)

        def sin_of(out, shift):
            ts(T1, Af, shift, 1.0 / TWO_PI, ALU.add, ALU.mult)
            vcopy(TI, T1)
            vcopy(T1, TI)
            stt(T1, T1, -TWO_PI, Af, ALU.mult, ALU.add)
            if shift != 0.0:
                ts(T1, T1, shift, None, ALU.add)
            ts(TIf, T1, math.pi, -TWO_PI, ALU.is_gt, ALU.mult)
            tt(T1, T1, TIf, ALU.add)
            ts(TIf, T1, -math.pi, TWO_PI, ALU.is_lt, ALU.mult)
            tt(T1, T1, TIf, ALU.add)
            ts(T1, T1, math.pi, -math.pi, ALU.min, ALU.max)
            act(out, T1, AF.Sin)

        sin_of(S_t.re("p a b -> p (a b)"), 0.0)
        sin_of(C_t.re("p a b -> p (a b)"), math.pi / 2)
        lbr, lbi, den, fre, fim, t0_, t1_, t2_ = sm
        cos1 = C_t[:, :, 0]; sin1 = S_t[:, :, 0]
        tt(lbr, rr, cos1, ALU.mult)
        ts(lbr, lbr, -1.0, None, ALU.add)
        tt(lbi, rr, sin1, ALU.mult)
        tt(den, ar, ar, ALU.mult)
        tt(t0_, ai, ai, ALU.mult)
        tt(den, den, t0_, ALU.add)
        recip(den, den)
        tt(t0_, lbr, ar, ALU.mult); tt(t1_, lbi, ai, ALU.mult); tt(fre, t0_, t1_, ALU.add); tt(fre, fre, den, ALU.mult)
        tt(t0_, lbi, ar, ALU.mult); tt(t1_, lbr, ai, ALU.mult); tt(fim, t0_, t1_, ALU.subtract); tt(fim, fim, den, ALU.mult)
        Mre = alloc([128, 16, 128]); Mim = alloc([128, 16, 128]); tb = alloc([128, 16, 16]); tb2 = alloc([128, 16, 16])
        Mreb = alloc([128, 16, 128], BF16); Mimb = alloc([128, 16, 128], BF16)
        bM = Buf("M")
        for v_ in (Mre, Mim, tb, tb2, Mreb, Mimb):
            v_.b = bM
        freb = fre.un(2).bc([128, 16, 16]); fimb = fim.un(2).bc([128, 16, 16])
        mset(Mre, 0.0); mset(Mim, 0.0)
        tt(tb, bre, freb, ALU.mult); tt(tb2, bim, fimb, ALU.mult)
        for lo, col in ((0, 0), (64, 16)):
            for j4 in range(4):
                tt(Mre[lo:lo + 64, j4::4, 32 * j4 + col:32 * j4 + col + 16], tb[lo:lo + 64, j4::4, :], tb2[lo:lo + 64, j4::4, :], ALU.subtract)
        tt(tb, bim, freb, ALU.mult); tt(tb2, bre, fimb, ALU.mult)
        for lo, col in ((0, 0), (64, 16)):
            for j4 in range(4):
                tt(Mim[lo:lo + 64, j4::4, 32 * j4 + col:32 * j4 + col + 16], tb[lo:lo + 64, j4::4, :], tb2[lo:lo + 64, j4::4, :], ALU.add)
        vcopy(Mreb, Mre); vcopy(Mimb, Mim)
        for M_, WB_ in ((Mreb, WBre), (Mimb, WBim)):
            for hf in range(2):
                bb = banks[0].cast(BF16)
                for q in range(8):
                    tr(bb[:, q * 128:(q + 1) * 128], M_[:, 8 * hf + q, :], identb)
                vcopy(WB_[:, 8 * hf:8 * hf + 8, :].re("p a b -> p (a b)"), bb)

        mset(BRb, 0.0); mset(BIb, 0.0)
        tt(tb, bre, freb, ALU.mult); tt(tb2, bim, fimb, ALU.mult)
        for lo, col in ((0, 0), (64, 16)):
            tt(BRb[lo:lo + 64, :, col:col + 16], tb[lo:lo + 64], tb2[lo:lo + 64], ALU.subtract)
        tt(tb, bim, freb, ALU.mult); tt(tb2, bre, fimb, ALU.mult)
        for lo, col in ((0, 0), (64, 16)):
            tt(BIb[lo:lo + 64, :, col:col + 16], tb[lo:lo + 64], tb2[lo:lo + 64], ALU.add)
        lnr = alloc([128, 16]); lnr.b = bsm
        tt(lnr, ar, dt_, ALU.mult)
        act(t2_, lnr, AF.Exp, scale=128.0)
        tt(lam128[:, 0, :], t2_, C_t[:, :, 127], ALU.mult)
        tt(lam128[:, 1, :], t2_, S_t[:, :, 127], ALU.mult)
        jj2 = alloc([128, 128]); jj2.b = bsm
        ld(jj2, jj2_d)
        C2 = Mre; S2 = Mim; Mag = T1.re("p (a b) -> p a b", a=16)
        jj2b = jj2.un(1).bc([128, 16, 128])
        tt(A_t, jj2b, th.un(2).bc([128, 16, 128]), ALU.mult)
        sin_of(S2.re("p a b -> p (a b)"), 0.0)
        sin_of(C2.re("p a b -> p (a b)"), math.pi / 2)
        tt(Mag, jj2b, lnr.un(2).bc([128, 16, 128]), ALU.mult)
        act(Mag, Mag, AF.Exp)
        Lb = [Mreb, Mimb]
        tt(Lb[0], Mag, C2, ALU.mult); tt(Lb[1], Mag, S2, ALU.mult)
        for M_, LT_ in ((Lb[0], LrT), (Lb[1], LiT)):
            for hf in range(2):
                bb = banks[0].cast(BF16)
                for q in range(8):
                    tr(bb[:, q * 128:(q + 1) * 128], M_[:, 8 * hf + q, :], identb)
                vcopy(LT_[:, 8 * hf:8 * hf + 8, :].re("p a b -> p (a b)"), bb)

        for nm_, v_ in (("S_t", S_t), ("C_t", C_t), ("R_t", R_t)):
            ld(tab_d[nm_], v_.re("p a b -> p (a b)"))
        ld(tab_d["WBre"], WBre.re("p a b -> p (a b)")); ld(tab_d["WBim"], WBim.re("p a b -> p (a b)"))

        barrier()
        bump[0] = mark_after_q
        W_xk = alloc([128, 8, D], BF16, "W_xk"); W_xv = alloc([128, 8, D], BF16, "W_xv")
        ldw(W_xk, w_xk_d.re("(kt p) n -> p kt n", p=128)); ldw(W_xv, w_xv_d.re("(kt p) n -> p kt n", p=128))
        xt0 = alloc([128, D], F32, "xt0"); memT = alloc([128, 8, 256], BF16, "memT"); xT0 = alloc([128, 8, 128], BF16, "xT0")
        mkf = alloc([128, D], F32, "mkf")
        for mt in range(2):
            load_xT(mem[mt * 128:(mt + 1) * 128, :], xt0, xT0, banks[0])
            vcopy(memT[:, :, mt * 128:(mt + 1) * 128], xT0, eng="pool")
            for W_, o_d, keep in ((W_xk, memk_o, False), (W_xv, memv_o, True)):
                for nb in range(2):
                    for k in range(8):
                        mm(banks[1 + nb], xT0[:, k, :], W_[:, k, nb * 512:(nb + 1) * 512], start=(k == 0), stop=(k == 7))
                    acopy(mkf[:, nb * 512:(nb + 1) * 512], banks[1 + nb])
                ld(o_d[mt * 128:(mt + 1) * 128, :], mkf)
                if keep:
                    vcopy(mvb[:, mt, :], mkf)
        for h in range(4):
            for et in range(2):
                c0 = h * 256 + et * 128
                for k in range(8):
                    mm(banks[3][:, 0:256], W_xk[:, k, c0:c0 + 128], memT[:, k, :], start=(k == 0), stop=(k == 7))
                acopy(mkT[:, h, et, :], banks[3][:, 0:256])

        barrier()
        bump[0] = mark_after_q

        W_q = alloc([128, 3, 768], BF16, "W_q")
        ldw(W_q, w_q_d.re("(kt p) n -> p kt n", p=128))
        W_in_q = alloc([128, 8, 384], BF16, "W_in_q")
        ldw(W_in_q, w_in_v[:, :, 512:896])
        gq = alloc([128, 384], F32, "gq"); ld(gq, gq_d)
        NB = 2
        xt = [alloc([128, D], F32, "xt%d" % i) for i in range(NB)]
        xT = [alloc([128, 8, 128], BF16, "xT%d" % i) for i in range(NB)]
        rc = [alloc([128, 64], F32, "rc%d" % i) for i in range(NB)]
        rs = [alloc([128, 64], F32, "rs%d" % i) for i in range(NB)]
        sq = alloc([128, 512], F32, "sq"); ss1 = [alloc([128, 1], F32) for _ in range(NB)]
        cqn = [alloc([128, 384], BF16) for _ in range(NB)]
        cqT = [alloc([128, 3, 128], BF16) for _ in range(NB)]
        qf = [alloc([128, 4, 192], F32) for _ in range(NB)]
        qr = [alloc([128, 4, 64], F32) for _ in range(NB)]
        qtmp = [alloc([128, 4, 64], F32) for _ in range(NB)]
        qb = [alloc([128, 4, 192], BF16) for _ in range(NB)]
        for i in range(NOWN):
            b = i % NB
            load_xT(xo[i * 128:(i + 1) * 128, :], xt[b], xT[b], banks[0])
            ld(rc[b], ropec_o[i * 128:(i + 1) * 128, :]); ld(rs[b], ropes_o[i * 128:(i + 1) * 128, :])
            for k in range(8):
                mm(banks[1][:, 0:384], xT[b][:, k, :], W_in_q[:, k, :], start=(k == 0), stop=(k == 7))
            rmsnorm(cqn[b], banks[1][:, 0:384], 384, gq, sq[:, 0:384], ss1[b])
            transpose_to(cqT[b], cqn[b], 3, banks[2], eng="act")
            for nb, (c0, c1) in enumerate(((0, 512), (512, 768))):
                for k in range(3):
                    mm(SC[:, nb * 512:nb * 512 + (c1 - c0)], cqT[b][:, k, :], W_q[:, k, c0:c1], start=(k == 0), stop=(k == 2))
            acopy(qf[b].re("p a b -> p (a b)"), SC[:, 0:768])
            rope(qr[b], qf[b][:, :, 128:192], rc[b], rs[b], qtmp[b], 4)
            P.op("act", lambda e, b=b: e.mul(out=qb[b].ap[:, :, 0:128], in_=qf[b].ap[:, :, 0:128], mul=MLA_SCALE), reads=[qf[b].b], writes=[qb[b].b])
            P.op("act", lambda e, b=b: e.mul(out=qb[b].ap[:, :, 128:192], in_=qr[b].ap, mul=MLA_SCALE), reads=[qr[b].b], writes=[qb[b].b])
            bb = banks[3].cast(BF16)
            for h in range(4):
                tr(bb[:, h * 128:(h + 1) * 128], qb[b][:, h, 0:128], identb)
                tr(bb[0:64, 512 + h * 128:512 + (h + 1) * 128], qb[b][:, h, 128:192], identb)
            vcopy(QTn[:, i, :, :].re("p a b -> p (a b)"), bb[:, 0:512])
            vcopy(QTr[0:64, i, :, :].re("p a b -> p (a b)"), bb[0:64, 512:1024])

        barrier()
        bump[0] = mark_after_q

        def ssm_rounds(uT_, pbs, init_fn, nseg, ybank=None, last_out=None):
            L = 128 // nseg
            S_t = ssmx["S_t"]; C_t = ssmx["C_t"]; R_t = ssmx["R_t"]; WBre = ssmx["WBre"]; WBim = ssmx["WBim"]
            Cre = ssmx["Cre"]; Cimn = ssmx["Cimn"]; Dblk = ssmx["Dblk"]
            xs4 = ssmx["xs4"]; zl_all = ssmx["zl_all"]
            def do_round(r):
                m4 = ssmx["m4"][r % 2]; wri = ssmx["wri"][r % 2]; zri = ssmx["zri"][r % 2]
                pb = pbs[r % 2]
                for j in range(2):
                    gp = 2 * r + j
                    mm(pb[:, j * 128:(j + 1) * 128], WBre[:, gp, :], uT_[:, gp // 4, :])
                    mm(pb[:, 256 + j * 128:256 + (j + 1) * 128], WBim[:, gp, :], uT_[:, gp // 4, :])
                if nseg == 1:
                    Cq = C_t[:, 2 * r:2 * r + 2, :].re("p a b -> p (a b)"); Sq = S_t[:, 2 * r:2 * r + 2, :].re("p a b -> p (a b)")
                    pre = pb[:, 0:256]; pim = pb[:, 256:512]
                    mk = lambda v_: v_
                else:
                    Cq = C_t[:, 2 * r:2 * r + 2, 0:L].un(2).bc([128, 2, nseg, L]); Sq = S_t[:, 2 * r:2 * r + 2, 0:L].un(2).bc([128, 2, nseg, L])
                    pre = pb[:, 0:256].re("p (a s l) -> p a s l", a=2, s=nseg); pim = pb[:, 256:512].re("p (a s l) -> p a s l", a=2, s=nseg)
                    mk = lambda v_: v_.re("p (a s l) -> p a s l", a=2, s=nseg)
                tt(mk(m4[0]), pre, Cq, ALU.mult); tt(mk(m4[1]), pim, Sq, ALU.mult)
                tt(mk(m4[2]), pim, Cq, ALU.mult); tt(mk(m4[3]), pre, Sq, ALU.mult)
                tt(wri[0], m4[0], m4[1], ALU.add, eng="pool")
                tt(wri[1], m4[2], m4[3], ALU.subtract, eng="pool")

            def do_back(r):
                m4 = ssmx["m4"][r % 2]; wri = ssmx["wri"][r % 2]; zri = ssmx["zri"][r % 2]
                if nseg == 1:
                    Cq = C_t[:, 2 * r:2 * r + 2, :].re("p a b -> p (a b)"); Sq = S_t[:, 2 * r:2 * r + 2, :].re("p a b -> p (a b)")
                else:
                    Cq = C_t[:, 2 * r:2 * r + 2, 0:L].un(2).bc([128, 2, nseg, L]); Sq = S_t[:, 2 * r:2 * r + 2, 0:L].un(2).bc([128, 2, nseg, L])
                for j in range(2):
                    gp = 2 * r + j
                    for s_ in range(nseg):
                        for part in range(2):
                            scan(zri[part][:, j, s_ * L:(s_ + 1) * L], R_t[:, gp, 0:L], wri[part][:, j * 128 + s_ * L:j * 128 + (s_ + 1) * L], init_fn(gp, s_, part))
                if last_out is not None:
                    for part in range(2):
                        src = zri[part].re("p a (s l) -> p a s l", s=nseg)[:, :, :, L - 1]
                        vcopy(zl_all[part][:, 0:nseg, 2 * r:2 * r + 2].re("p s a -> p a s"), src, eng="pool")
                if ybank is not None:
                    dmd = ssmx["dmd"][r % 2]; xrb = ssmx["xrb"]; xib = ssmx["xib"]
                    if nseg == 1:
                        Cd, Sd = Cq, Sq
                        zr_ = zri[0].re("p a b -> p (a b)"); zi_ = zri[1].re("p a b -> p (a b)")
                        dk = lambda v_: v_
                    else:
                        Cd, Sd = Cq, Sq
                        zr_ = zri[0].re("p a (s l) -> p a s l", s=nseg); zi_ = zri[1].re("p a (s l) -> p a s l", s=nseg)
                        dk = lambda v_: v_.re("p (a s l) -> p a s l", a=2, s=nseg)
                    tt(dk(dmd[0]), zr_, Cd, ALU.mult); tt(dk(dmd[1]), zi_, Sd, ALU.mult, eng="pool")
                    tt(dk(dmd[2]), zr_, Sd, ALU.mult); tt(dk(dmd[3]), zi_, Cd, ALU.mult, eng="pool")
                    tt(xrb[r % 2].re("p a b -> p (a b)"), dmd[0], dmd[1], ALU.subtract, eng="pool")
                    tt(xib[r % 2].re("p a b -> p (a b)"), dmd[2], dmd[3], ALU.add, eng="pool")

            def do_cproj(r):
                if ybank is None:
                    return
                xrb = ssmx["xrb"]; xib = ssmx["xib"]
                for j in range(2):
                    gp = 2 * r + j
                    o_ = ybank[:, 32 * gp:32 * gp + 32]
                    mm(o_, xrb[r % 2][:, j, :], Cre[:, gp, :], start=True, stop=False)
                    mm(o_, xib[r % 2][:, j, :], Cimn[:, gp, :], start=False, stop=False)
                    mm(o_, uT_[:, gp // 4, :], Dblk[:, gp, :], start=False, stop=True)
            def do_tail():
                do_cproj(7)
                if last_out is None:
                    return
                if True:
                    cL = C_t[:, :, L - 1]; sL = S_t[:, :, L - 1]
                    for s_ in range(nseg):
                        o_re, o_im = last_out(s_)
                        zr_l = zl_all[0][:, s_, :]; zi_l = zl_all[1][:, s_, :]
                        tt(xs4[0], zr_l, cL, ALU.mult); tt(xs4[1], zi_l, sL, ALU.mult)
                        tt(xs4[2], zr_l, sL, ALU.mult); tt(xs4[3], zi_l, cL, ALU.mult)
                        tt(o_re, xs4[0], xs4[1], ALU.subtract)
                        tt(o_im, xs4[2], xs4[3], ALU.add)


            def step(k_):
                def f():
                    if k_ == 0:
                        do_round(0)
                    if k_ + 1 < 8:
                        do_round(k_ + 1)
                    do_back(k_)
                    if k_ >= 1:
                        do_cproj(k_ - 1)
                return f
            return [step(k_) for k_ in range(8)] + [do_tail]

        def ssm_tile(uT_, pbs, init_fn, nseg, ybank=None, last_out=None):
            for f_ in ssm_rounds(uT_, pbs, init_fn, nseg, ybank, last_out):
                f_()

        W_kv = alloc([128, 2, 4, 256], BF16, "W_kv")
        ldw(W_kv.re("p k h e -> p k (h e)"), w_kv_d.re("(kt p) n -> p kt n", p=128))
        gkv = alloc([128, 256], F32, "gkv"); ld(gkv, gkv_d)
        maskd = alloc([128, 1024], F32, "maskd"); ld(maskd, maskd_d)
        xt = [alloc([128, D], F32, "hxt%d" % i) for i in range(NB)]
        xT = [alloc([128, 8, 128], BF16, "hxT%d" % i) for i in range(NB)]
        utok = [alloc([128, 512], BF16, "hut%d" % i) for i in range(NB)]
        kvf = [alloc([128, 320], F32, "kvf%d" % i) for i in range(NB)]
        prs = [alloc([128, 16, 32], F32, "prs%d" % i) for i in range(NB)]
        pis = [alloc([128, 16, 32], F32, "pis%d" % i) for i in range(NB)]
        rc = [alloc([128, 64], F32) for _ in range(NB)]; rs = [alloc([128, 64], F32) for _ in range(NB)]
        ckvf = [alloc([128, 256], F32) for _ in range(NB)]; kpef = [alloc([128, 64], F32) for _ in range(NB)]
        tmp64 = [alloc([128, 64], F32) for _ in range(NB)]
        sq = alloc([128, 256], F32, "hsq"); ss1 = [alloc([128, 1], F32) for _ in range(NB)]
        ckvb = [alloc([128, 256], BF16) for _ in range(NB)]; kpeb = [alloc([128, 128], BF16) for _ in range(NB)]
        ckvT = [alloc([128, 2, 128], BF16) for _ in range(NB)]
        KT = [alloc([128, 4, 1024], BF16, "KT%d" % i) for i in range(2)]
        KR = [alloc([128, 1024], BF16, "KR%d" % i) for i in range(2)]
        Vb = [alloc([128, 8, 512], BF16, "Vb%d" % i) for i in range(2)]
        Pb = [alloc([128, 1024], BF16, "Pb%d" % i) for i in range(2)]
        PT = [alloc([128, 8, 128], BF16, "PT%d" % i) for i in range(2)]
        NS4 = 4
        mx = [alloc([128, 1], F32) for _ in range(NS4)]; mnew = [alloc([128, 1], F32) for _ in range(NS4)]
        negm = [alloc([128, 1], F32) for _ in range(NS4)]; alp = [alloc([128, 1], F32) for _ in range(NS4)]
        rsum = [alloc([128, 1], F32) for _ in range(NS4)]
        dm_ = [alloc([128, 1], F32) for _ in range(NS4)]
        hm1 = [alloc([128, 16, 32], F32, "hm1_%d" % i) for i in range(NB)]; hm2 = [alloc([128, 16, 32], F32, "hm2_%d" % i) for i in range(NB)]
        sre = [alloc([128, 32], F32) for _ in range(2)]
        xs8 = [alloc([128, 16], F32) for _ in range(4)]

        mset(Xpp, 0.0); mset(Xown, 0.0)
        mset(m_run, NEG); mset(l_run, 0.0); mset(Oacc, 0.0)
        for kp in range(2):
            mset(kpeb[kp], 0.0)

        def hist_stages(t, kbuf, j, kb):
            b = t % NB
            bA = banks[0]; bB = banks[1]
            xc = Xpp[:, t % 2, :]; xn = Xpp[:, (t + 1) % 2, :]

            def sA():
                ld(xt[b], xp[t * 128:(t + 1) * 128, :])
                ld(rc[b], ropec_p[t * 128:(t + 1) * 128, :]); ld(rs[b], ropes_p[t * 128:(t + 1) * 128, :])

            def sB():
                for k in range(4):
                    tr(bA[:, k * 128:(k + 1) * 128], xt[b][:, k * 128:(k + 1) * 128], ident)
                for k in range(4):
                    tr(bB[:, k * 128:(k + 1) * 128], xt[b][:, (4 + k) * 128:(5 + k) * 128], ident)

            def sC():
                acopy(xT[b][:, 0:4, :], bA.re("p (k n) -> p k n", k=4))
                vcopy(xT[b][:, 4:8, :], bB.re("p (k n) -> p k n", k=4))

            def sD():
                for k in range(8):
                    mm(bA, xT[b][:, k, :], W_in_u[:, k, :], start=(k == 0), stop=(k == 7))
                for k in range(8):
                    mm(bB[:, 0:320], xT[b][:, k, :], W_in_kv[:, k, :], start=(k == 0), stop=(k == 7))

            def sE():
                acopy(utok[b], bA)
                acopy(kvf[b], bB[:, 0:320])

            def sF():
                act(sq, kvf[b][:, 0:256], AF.Square, accum=ss1[b])
                act(ss1[b], ss1[b], AF.Ln, scale=1.0 / 256, bias=epsc)
                act(ss1[b], ss1[b], AF.Exp, scale=-0.5)
                for gp in range(16):
                    mm(bA[:, 32 * gp:32 * gp + 32], LrT[:, gp, :], utok[b][:, 32 * gp:32 * gp + 32])
                for gp in range(16):
                    mm(bB[:, 32 * gp:32 * gp + 32], LiT[:, gp, :], utok[b][:, 32 * gp:32 * gp + 32])

            late = kb >= 4

            def sG():
                if late:
                    pr3 = bA.re("p (a b) -> p a b", a=16); pi3 = bB.re("p (a b) -> p a b", a=16)
                    tt(hm1[b], pr3, BRb, ALU.mult); tt(hm2[b], pi3, BIb, ALU.mult)
                    tt(prs[b], pi3, BRb, ALU.mult); tt(pis[b], pr3, BIb, ALU.mult)
                else:
                    acopy(prs[b].re("p a b -> p (a b)"), bA)
                    acopy(pis[b].re("p a b -> p (a b)"), bB)
                tt(ckvf[b], kvf[b][:, 0:256], gkv, ALU.mult, eng="pool")
                ts(ckvf[b], ckvf[b], ss1[b], 1.0, ALU.mult, ALU.mult, eng="pool")
                rope(kpef[b].re("p (a b) -> p a b", a=1), kvf[b][:, 256:320].re("p (a b) -> p a b", a=1), rc[b], rs[b],
                     tmp64[b].re("p (a b) -> p a b", a=1), 1, eng="pool")
                vcopy(ckvb[b], ckvf[b], eng="pool")
                vcopy(kpeb[b][:, 0:64], kpef[b], eng="pool")

            def sH():
                bb = bA.cast(BF16)
                for kc in range(2):
                    tr(bb[:, kc * 128:(kc + 1) * 128], ckvb[b][:, kc * 128:(kc + 1) * 128], identb)
                tr(bb[:, 256:384], kpeb[b], identb)
                if late:
                    tt(hm1[b], hm1[b], hm2[b], ALU.subtract, eng="pool")
                    tt(hm2[b], prs[b], pis[b], ALU.add, eng="pool")
                else:
                    tt(hm1[b], prs[b], BRb, ALU.mult, eng="pool"); tt(hm2[b], pis[b], BIb, ALU.mult, eng="pool")
                    tt(hm1[b], hm1[b], hm2[b], ALU.subtract, eng="pool")
                    tt(hm2[b], pis[b], BRb, ALU.mult, eng="pool"); tt(prs[b], prs[b], BIb, ALU.mult, eng="pool")
                    tt(hm2[b], hm2[b], prs[b], ALU.add, eng="pool")

            def sI():
                bb = bA.cast(BF16)
                acopy(ckvT[b].re("p a b -> p (a b)"), bb[:, 0:256])
                acopy(KR[kbuf][0:64, j * 128:(j + 1) * 128], bb[0:64, 256:384])

            def sJ():
                for h in range(4):
                    for kc in range(2):
                        mm(bB[:, h * 128:(h + 1) * 128], W_kv[:, kc, h, 0:128], ckvT[b][:, kc, :], start=(kc == 0), stop=(kc == 1))
                for kc in range(2):
                    mm(bA, ckvT[b][:, kc, :], W_kv[:, kc, :, 128:256], start=(kc == 0), stop=(kc == 1))
                lr_ = lam128[:, 0, :]; li_ = lam128[:, 1, :]
                ld(ckv_o[t * 128:(t + 1) * 128, :], ckvf[b])
                ld(kpe_o[t * 128:(t + 1) * 128, :], kpef[b])
                red(sre[t % 2][:, 0:16], hm1[b], ALU.add)
                red(sre[t % 2][:, 16:32], hm2[b], ALU.add)
                stt(Xown[:, kb, :], xc, ohot[:, j:j + 1], Xown[:, kb, :], ALU.mult, ALU.add)
                tt(xs8[0], xc[:, 0:16], lr_, ALU.mult, eng="pool"); tt(xs8[1], xc[:, 16:32], li_, ALU.mult, eng="pool")
                tt(xs8[2], xc[:, 16:32], lr_, ALU.mult, eng="pool"); tt(xs8[3], xc[:, 0:16], li_, ALU.mult, eng="pool")
                tt(xs8[0], xs8[0], xs8[1], ALU.subtract, eng="pool"); tt(xs8[2], xs8[2], xs8[3], ALU.add, eng="pool")
                tt(xn[:, 0:16], xs8[0], sre[t % 2][:, 0:16], ALU.add, eng="pool")
                tt(xn[:, 16:32], xs8[2], sre[t % 2][:, 16:32], ALU.add, eng="pool")

            def sK():
                acopy(KT[kbuf][:, :, j * 128:(j + 1) * 128], bB.re("p (h n) -> p h n", h=4))
                vcopy(Vb[kbuf][:, j, :], bA)

            def pair(p_, c_):
                def f():
                    p_(); c_()
                return f
            return [sA, pair(sB, sC), pair(sD, sE), pair(sF, sG), pair(sH, sI), pair(sJ, sK)]

        def hist_items(kb):
            items = []
            for jp in range(4):
                sa = hist_stages(kb * 8 + 2 * jp, kb % 2, 2 * jp, kb)
                sb_ = hist_stages(kb * 8 + 2 * jp + 1, kb % 2, 2 * jp + 1, kb)
                for a_, b_ in zip(sa, sb_):
                    items.append(a_); items.append(b_)
            return items

        SCp = [dbl[3], dbl[2]]
        ptb = [banks[2], banks[3]]

        def att_A(n, i, h, kbuf):
            sc = SCp[n % 2]
            for c2 in range(2):
                mm(sc[:, c2 * 512:(c2 + 1) * 512], QTn[:, i, h, :], KT[kbuf][:, h, c2 * 512:(c2 + 1) * 512], start=True, stop=False)
                mm(sc[:, c2 * 512:(c2 + 1) * 512], QTr[0:64, i, h, :], KR[kbuf][0:64, c2 * 512:(c2 + 1) * 512], start=False, stop=True)

        def att_B(n, i, h, kbuf, diag):
            u2 = n % 2; u4 = n % NS4
            sc = SCp[u2]
            col = i * 4 + h
            if diag:
                tt(sc, sc, maskd, ALU.add)
            red(mx[u4], sc, ALU.max)
            tt(mnew[u4], mx[u4], m_run[:, col:col + 1], ALU.max)
            tt(dm_[u4], m_run[:, col:col + 1], mnew[u4], ALU.subtract)
            vcopy(m_run[:, col:col + 1], mnew[u4])
            ts(negm[u4], mnew[u4], -1.0, None, ALU.mult)
            act(alp[u4], dm_[u4], AF.Exp)
            act(Pb[u2], sc, AF.Exp, bias=negm[u4], accum=rsum[u4])

        def att_CD(n, i, h, kbuf):
            u2 = n % 2
            pt_b = ptb[u2].cast(BF16)
            for kt in range(8):
                tr(pt_b[:, kt * 128:(kt + 1) * 128], Pb[u2][:, kt * 128:(kt + 1) * 128], identb)
            if n % 3 == 0:
                vcopy(PT[u2].re("p a b -> p (a b)"), pt_b)
            else:
                acopy(PT[u2].re("p a b -> p (a b)"), pt_b)

        def att_EF(n, i, h, kbuf):
            u2 = n % 2; u4 = n % NS4
            ov = ptb[u2][:, 0:128]
            for kt in range(8):
                mm(ov, PT[u2][:, kt, :], Vb[kbuf][:, kt, h * 128:(h + 1) * 128], start=(kt == 0), stop=(kt == 7))
            stt(Oacc[:, i, h, :], Oacc[:, i, h, :], alp[u4], ov, ALU.mult, ALU.add)
            col = i * 4 + h
            stt(l_run[:, col:col + 1], l_run[:, col:col + 1], alp[u4], rsum[u4], ALU.mult, ALU.add)

        for it in hist_items(0):
            it()
        nun = 0
        for kb in range(16):
            kbuf = kb % 2
            units = [(i, h) for i in range(kb, 16) for h in range(4)]
            U = len(units)
            items = hist_items(kb + 1) if kb + 1 < 16 else []
            per = (len(items) + U - 1) // U if items else 0
            ip = 0
            for q_ in range(U + 3):
                if q_ < U:
                    att_A(nun + q_, units[q_][0], units[q_][1], kbuf)
                if 0 <= q_ - 1 < U:
                    att_B(nun + q_ - 1, units[q_ - 1][0], units[q_ - 1][1], kbuf, units[q_ - 1][0] == kb)
                if 0 <= q_ - 2 < U:
                    att_CD(nun + q_ - 2, units[q_ - 2][0], units[q_ - 2][1], kbuf)
                if 0 <= q_ - 3 < U:
                    att_EF(nun + q_ - 3, units[q_ - 3][0], units[q_ - 3][1], kbuf)
                for _ in range(per):
                    if ip < len(items):
                        items[ip](); ip += 1
            while ip < len(items):
                items[ip](); ip += 1
            nun += U

        ld(ssmp_o, Xpp[:, NTP % 2, :])

        barrier()
        bump[0] = mark_after_q

        W_kv = alloc([128, 2, 4, 256], BF16, "W_kv2")
        ldw(W_kv.re("p k h e -> p k (h e)"), w_kv_d.re("(kt p) n -> p kt n", p=128))
        gkv = alloc([128, 256], F32, "gkv2"); ld(gkv, gkv_d)
        xt_s = alloc([128, D], F32, "sxt"); xT_s = alloc([128, 8, 128], BF16, "sxT")
        rc_s = alloc([128, 64], F32); rs_s = alloc([128, 64], F32)
        ckvf_s = alloc([128, 256], F32); kpef_s = alloc([128, 64], F32); tmp64_s = alloc([128, 64], F32)
        sq = alloc([128, 512], F32, "ssq"); ss_s = alloc([128, 1], F32)
        ckvb_s = alloc([128, 256], BF16); kpeb_s = alloc([128, 128], BF16)
        ckvTn = alloc([128, 2, 128], BF16, "ckvTn"); kpeTn = alloc([128, 128], BF16, "kpeTn")
        KTn = alloc([128, 4, 128], BF16, "KTn")
        cat = [alloc([128, 256], F32, "cat%d" % i) for i in range(2)]
        catb = [alloc([128, 256], BF16) for _ in range(2)]
        cpt = [alloc([128, 64], F32, "cpt%d" % i) for i in range(2)]
        cptb = [alloc([128, 128], BF16) for _ in range(2)]
        ckvTc = alloc([128, 2, 1024], BF16, "ckvTc"); kpeTc = alloc([128, 1024], BF16, "kpeTc")
        KTc = alloc([128, 4, 1024], BF16, "KTc"); Vc = alloc([128, 8, 512], BF16, "Vc"); Vn = alloc([128, 512], BF16, "Vn")
        scs = alloc([128, 1056], F32, "scs")
        Ps = alloc([128, 1152], BF16, "Ps"); PTs = alloc([128, 9, 32], BF16, "PTs")
        mxs = alloc([128, 1], F32); negs = alloc([128, 1], F32); sums = alloc([128, 1], F32)
        osb = alloc([128, 512], F32, "osb")
        isam = 16
        load_xT(xo[isam * 128:(isam + 1) * 128, :], xt_s, xT_s, banks[0])
        ld(rc_s, ropec_o[isam * 128:(isam + 1) * 128, :]); ld(rs_s, ropes_o[isam * 128:(isam + 1) * 128, :])
        for k in range(8):
            mm(banks[2][:, 0:320], xT_s[:, k, :], W_in_kv[:, k, :], start=(k == 0), stop=(k == 7))
        rmsnorm(ckvf_s, banks[2][:, 0:256], 256, gkv, sq[:, 0:256], ss_s)
        ld(ckvs_o, ckvf_s)
        rope(kpef_s.re("p (a b) -> p a b", a=1), banks[2][:, 256:320].re("p (a b) -> p a b", a=1), rc_s, rs_s, tmp64_s.re("p (a b) -> p a b", a=1), 1)
        ld(kpes_o, kpef_s)
        vcopy(ckvb_s, ckvf_s)
        mset(kpeb_s, 0.0)
        vcopy(kpeb_s[:, 0:64], kpef_s)
        bb = banks[3].cast(BF16)
        for kc in range(2):
            tr(bb[:, kc * 128:(kc + 1) * 128], ckvb_s[:, kc * 128:(kc + 1) * 128], identb)
        tr(bb[:, 256:384], kpeb_s, identb)
        acopy(ckvTn.re("p a b -> p (a b)"), bb[:, 0:256])
        acopy(kpeTn[0:64, :], bb[0:64, 256:384])
        for h in range(4):
            for kc in range(2):
                mm(banks[3][:, h * 128:(h + 1) * 128], W_kv[:, kc, h, 0:128], ckvTn[:, kc, :], start=(kc == 0), stop=(kc == 1))
        acopy(KTn.re("p a b -> p (a b)"), banks[3])
        for kp in range(2):
            mset(cptb[kp], 0.0)
        for s_ in range(4):
            for kt in range(8):
                b = kt % 2
                r0 = s_ * 1024 + kt * 128
                ld(cat[b], cckv_d[r0:r0 + 128, :]); ld(cpt[b], ckpe_d[r0:r0 + 128, :])
                vcopy(catb[b], cat[b], eng="pool"); vcopy(cptb[b][:, 0:64], cpt[b], eng="pool")
                bb = banks[1].cast(BF16)
                for kc in range(2):
                    tr(bb[:, kc * 128:(kc + 1) * 128], catb[b][:, kc * 128:(kc + 1) * 128], identb)
                tr(bb[:, 256:384], cptb[b], identb)
                acopy(ckvTc[:, :, kt * 128:(kt + 1) * 128], bb[:, 0:256].re("p (a b) -> p a b", a=2))
                acopy(kpeTc[0:64, kt * 128:(kt + 1) * 128], bb[0:64, 256:384])
            for h in range(4):
                for n in range(2):
                    for kc in range(2):
                        mm(banks[2], W_kv[:, kc, h, 0:128], ckvTc[:, kc, n * 512:(n + 1) * 512], start=(kc == 0), stop=(kc == 1))
                    acopy(KTc[:, h, n * 512:(n + 1) * 512], banks[2])
            for kt in range(8):
                for kc in range(2):
                    mm(banks[3], ckvTc[:, kc, kt * 128:(kt + 1) * 128], W_kv[:, kc, :, 128:256], start=(kc == 0), stop=(kc == 1))
                vcopy(Vc[:, kt, :], banks[3])
            for kc in range(2):
                mm(banks[3][0:32, :], ckvTn[:, kc, s_ * 32:(s_ + 1) * 32], W_kv[:, kc, :, 128:256], start=(kc == 0), stop=(kc == 1))
            vcopy(Vn[0:32, :], banks[3][0:32, :])
            qs = slice(s_ * 32, (s_ + 1) * 32)
            for h in range(4):
                for n in range(2):
                    mm(SC[0:32, n * 512:(n + 1) * 512], QTn[:, isam, h, qs], KTc[:, h, n * 512:(n + 1) * 512], start=True, stop=False)
                    mm(SC[0:32, n * 512:(n + 1) * 512], QTr[0:64, isam, h, qs], kpeTc[0:64, n * 512:(n + 1) * 512], start=False, stop=True)
                mm(banks[4][0:32, 0:32], QTn[:, isam, h, qs], KTn[:, h, qs], start=True, stop=False)
                mm(banks[4][0:32, 0:32], QTr[0:64, isam, h, qs], kpeTn[0:64, qs], start=False, stop=True)
                acopy(scs[0:32, 0:1024], SC[0:32, :])
                acopy(scs[0:32, 1024:1056], banks[4][0:32, 0:32])
                red(mxs[0:32, :], scs[0:32, :], ALU.max)
                ts(negs[0:32, :], mxs[0:32, :], -1.0, None, ALU.mult)
                mset(Ps[0:32, 1024:1152], 0.0)
                act(Ps[0:32, 0:1056], scs[0:32, :], AF.Exp, bias=negs[0:32, :], accum=sums[0:32, :])
                pt_b = banks[5].cast(BF16)
                for kt in range(9):
                    tr(pt_b[:, kt * 32:(kt + 1) * 32], Ps[0:32, kt * 128:(kt + 1) * 128], identb[0:32, 0:32])
                vcopy(PTs.re("p a b -> p (a b)"), pt_b[:, 0:288])
                ov = banks[4][0:32, 128:256]
                for kt in range(8):
                    mm(ov, PTs[:, kt, :], Vc[:, kt, h * 128:(h + 1) * 128], start=(kt == 0), stop=False)
                mm(ov, PTs[0:32, 8, :], Vn[0:32, h * 128:(h + 1) * 128], start=False, stop=True)
                recip(sums[0:32, :], sums[0:32, :])
                ts(osb[0:32, h * 128:(h + 1) * 128], ov, sums[0:32, :], None, ALU.mult)
            ld(Osam[s_ * 32:(s_ + 1) * 32, :], osb[0:32, :])

        barrier()
        bump[0] = mark_after_mixer_state

        W_glu = alloc([128, 4, D], BF16, "W_glu"); W_o = alloc([128, 8, D], BF16, "W_o")
        ldw(W_glu, w_glu_d.re("(kt p) n -> p kt n", p=128)); ldw(W_o, w_o_d.re("(kt p) n -> p kt n", p=128))
        gos = alloc([128, 512], F32, "gos"); gom = alloc([128, 512], F32, "gom"); ld(gos, gos_d); ld(gom, gom_d)
        lng = alloc([128, D], F32, "lng"); lnb = alloc([128, D], F32, "lnb")
        ld(lng, lng_d[:, 0:D]); ld(lnb, lnb_d[:, 0:D])
        btab2 = Buf("tab2")
        for nm_ in ("S_t", "C_t", "R_t"):
            v_ = alloc([128, 16, 128], F32, nm_ + "2"); v_.b = btab2
            ld(v_.re("p a b -> p (a b)"), tab_d[nm_]); ssmx[nm_] = v_
        for nm_ in ("WBre", "WBim"):
            v_ = alloc([128, 16, 128], BF16, nm_ + "2")
            ld(v_.re("p a b -> p (a b)"), tab_d[nm_]); ssmx[nm_] = v_
        for nm_, src_ in (("Cre", cre_d), ("Cimn", cimn_d), ("Dblk", dblk_d)):
            v_ = alloc([128, 16, 32], BF16, nm_)
            ldw(v_.re("p a b -> p (a b)"), src_); ssmx[nm_] = v_
        ssmx["m4"] = [[alloc([128, 256], F32) for _ in range(4)] for _ in range(2)]
        ssmx["wri"] = [[alloc([128, 256], F32) for _ in range(2)] for _ in range(2)]
        ssmx["zri"] = [[alloc([128, 2, 128], F32) for _ in range(2)] for _ in range(2)]
        ssmx["xs4"] = [alloc([128, 16], F32) for _ in range(4)]
        ssmx["zl_all"] = [alloc([128, 4, 16], F32) for _ in range(2)]
        ssmx["dmd"] = [[alloc([128, 256], F32) for _ in range(4)]] * 2
        ssmx["xrb"] = [alloc([128, 2, 128], BF16) for _ in range(2)]
        ssmx["xib"] = [alloc([128, 2, 128], BF16) for _ in range(2)]
        S0 = alloc([128, 2, 4, 16], F32, "S0"); ld(S0.re("p a b c -> p (a b c)"), s0_d)
        Sfin = alloc([128, 2, 4, 16], F32, "Sfin")
        xt = [alloc([128, D], F32, "axt%d" % i) for i in range(NB)]
        xT = [alloc([128, 8, 128], BF16, "axT%d" % i) for i in range(NB)]
        uT = [alloc([128, 4, 128], BF16, "auT%d" % i) for i in range(NB)]
        ysq = alloc([128, 512], F32, "ysq"); yt = alloc([128, 512], F32, "yt"); ysg = alloc([128, 512], F32, "ysg")
        glb = alloc([128, 512], BF16, "glb"); gT = alloc([128, 4, 128], BF16, "gT")
        sg2 = ysq; osf = yt
        mixb = alloc([128, D], BF16, "mixb"); mixT = alloc([128, 8, 128], BF16, "mixT")
        rl = alloc([128, 4], F32, "rl"); omf = alloc([128, 4, 128], F32, "omf")
        ss_a = alloc([128, 1], F32); sq = ysg
        rres = alloc([128, D], F32, "rres"); hout = alloc([128, D], F32, "hout")
        stats = alloc([128, 12], F32); mvv = alloc([128, 2], F32)

        def pre_stages(i):
            b = i % NB
            yb = banks[3] if i % 2 == 0 else banks[0]

            def p0():
                load_xT(xo[i * 128:(i + 1) * 128, :], xt[b], xT[b], banks[1])
                for kt in range(4):
                    for k in range(8):
                        mm(banks[1][:, kt * 128:(kt + 1) * 128], W_in_u[:, k, kt * 128:(kt + 1) * 128], xT[b][:, k, :], start=(k == 0), stop=(k == 7))
                acopy(uT[b].re("p a b -> p (a b)"), banks[1])

            if i < 16:
                rr_ = ssm_rounds(uT[b], (banks[4], banks[5]), lambda gp, s_, part, i=i: Xown[:, i, part * 16 + gp:part * 16 + gp + 1], 1, ybank=yb)
            else:
                rr_ = ssm_rounds(uT[b], (banks[4], banks[5]), lambda gp, s_, part: S0[:, part, s_, gp:gp + 1], 4, ybank=yb,
                                 last_out=lambda s_: (Sfin[:, 0, s_, :], Sfin[:, 1, s_, :]))
                rr_.append(lambda: ld(ssms_o, Sfin.re("p a b c -> p (a b c)")))
            return [p0] + rr_

        def post_stages(i):
            b = i % NB
            yb = banks[3] if i % 2 == 0 else banks[0]
            xt_ = xt[b]

            def q0():
                act(ysq, yb, AF.Square)
                ts(yt, ysq, 0.044715, 1.0, ALU.mult, ALU.add)
                tt(yt, yt, yb, ALU.mult)

            def q1():
                act(ysg, yt, AF.Sigmoid, scale=1.5957691216057308)
                tt(glb, ysg, yb, ALU.mult)

            def q2():
                transpose_to(gT, glb, 4, banks[2], eng="act")

            def q3():
                for nb in range(2):
                    for k in range(4):
                        mm(SC[:, nb * 512:(nb + 1) * 512], gT[:, k, :], W_glu[:, k, nb * 512:(nb + 1) * 512], start=(k == 0), stop=(k == 3))

            def q4():
                act(sg2, SC[:, 512:1024], AF.Sigmoid)
                tt(osf, sg2, SC[:, 0:512], ALU.mult)

            def q5():
                rmsnorm(mixb[:, 0:512], osf, 512, gos, sq, ss_a)

            def q6():
                if i < 16:
                    recip(rl, l_run[:, i * 4:(i + 1) * 4])
                    tt(omf, Oacc[:, i, :, :], rl.un(2).bc([128, 4, 128]), ALU.mult)
                    rmsnorm(mixb[:, 512:1024], omf.re("p a b -> p (a b)"), 512, gom, sq, ss_a)
                else:
                    rmsnorm(mixb[:, 512:1024], Osam, 512, gom, sq, ss_a)

            def q7():
                for half in range(2):
                    transpose_to(mixT[:, half * 4:(half + 1) * 4, :], mixb[:, half * 512:(half + 1) * 512], 4, banks[2], eng=("act" if half == 0 else "dve"))

            def q8():
                for nb in range(2):
                    for k in range(8):
                        mm(SC[:, nb * 512:(nb + 1) * 512], mixT[:, k, :], W_o[:, k, nb * 512:(nb + 1) * 512], start=(k == 0), stop=(k == 7))

            def q9():
                stt(rres, xt_, ALPHA, SC, ALU.mult, ALU.add)
                layernorm(hout, rres, lng, lnb, stats, mvv)
                ld(h1_d[i * 128:(i + 1) * 128, :], hout)

            return [q0, q1, q2, q3, q4, q5, q6, q7, q8, q9]

        prev_post = []
        for i in range(NOWN + 1):
            pre = pre_stages(i) if i < NOWN else []
            n_ = max(len(pre), len(prev_post))
            for k_ in range(n_):
                if k_ < len(pre):
                    pre[k_]()
                if k_ < len(prev_post):
                    prev_post[k_]()
            prev_post = post_stages(i) if i < NOWN else []

        barrier()
        bump[0] = mark_after_mem

        W_xq = alloc([128, 8, D], BF16, "W_xq"); W_xo = alloc([128, 8, D], BF16, "W_xo")
        ldw(W_xq, w_xq_d.re("(kt p) n -> p kt n", p=128)); ldw(W_xo, w_xo_d.re("(kt p) n -> p kt n", p=128))
        lng = alloc([128, D], F32, "lng2"); lnb = alloc([128, D], F32, "lnb2")
        ld(lng, lng_d[:, D:2 * D]); ld(lnb, lnb_d[:, D:2 * D])
        hin = [alloc([128, D], F32, "bh%d" % i) for i in range(NB)]
        hb2 = [alloc([128, D], BF16, "bhb%d" % i) for i in range(2)]; hT2 = [alloc([128, 8, 128], BF16, "bhT%d" % i) for i in range(2)]
        qxb2 = [alloc([128, D], BF16, "qxb%d" % i) for i in range(2)]; qxT2 = [alloc([128, 8, 128], BF16, "qxT%d" % i) for i in range(2)]
        mx42 = [alloc([128, 4], F32) for _ in range(2)]; neg42 = [alloc([128, 4], F32) for _ in range(2)]; sum42 = [alloc([128, 4], F32) for _ in range(2)]
        Px2 = [alloc([128, 4, 256], BF16, "Px%d" % i) for i in range(2)]; PxT2 = [alloc([128, 8, 128], BF16, "PxT%d" % i) for i in range(2)]
        oxb2 = [alloc([128, D], BF16, "oxb%d" % i) for i in range(2)]; oxT2 = [alloc([128, 8, 128], BF16, "oxT%d" % i) for i in range(2)]
        rres2 = [alloc([128, D], F32, "brres%d" % i) for i in range(2)]; hout2 = [alloc([128, D], F32, "bhout%d" % i) for i in range(2)]
        stats2 = [alloc([128, 12], F32) for _ in range(2)]; mvv2 = [alloc([128, 2], F32) for _ in range(2)]
        hb_ = hb2[0]; hT = hT2[0]; qxb = qxb2[0]; qxT = qxT2[0]; mx4 = mx42[0]; neg4 = neg42[0]; sum4 = sum42[0]
        Px = Px2[0]; PxT = PxT2[0]; oxb = oxb2[0]; oxT = oxT2[0]; rres = rres2[0]; hout = hout2[0]; stats = stats2[0]; mvv = mvv2[0]
        cmk = [alloc([128, D], F32, "cmk%d" % i) for i in range(2)]
        cmkb = [alloc([128, D], BF16, "cmkb%d" % i) for i in range(1)]
        mkTs4 = [alloc([128, 4, 2, 256], BF16, "mkTs%d" % i) for i in range(4)]; mvs4 = [alloc([128, 2, D], BF16, "mvs%d" % i) for i in range(4)]
        scx = alloc([128, 8], F32, "scx"); oxs = alloc([128, D], BF16, "oxs")

        def xa_stages(i, p):
            A_ = banks[p]; C_ = dbl[1 + p]
            Ab = A_.cast(BF16)

            def tr8(src):
                for k in range(8):
                    tr(Ab[:, k * 128:(k + 1) * 128], src[:, k * 128:(k + 1) * 128], identb)

            def ev8(dst):
                acopy(dst[:, 0:4, :], Ab[:, 0:512].re("p (k n) -> p k n", k=4))
                vcopy(dst[:, 4:8, :], Ab[:, 512:1024].re("p (k n) -> p k n", k=4))

            def t0():
                ld(hin[p], h1_d[i * 128:(i + 1) * 128, :])
                vcopy(hb2[p], hin[p], eng="pool")

            def t3():
                for nb in range(2):
                    for k in range(8):
                        mm(C_[:, nb * 512:(nb + 1) * 512], hT2[p][:, k, :], W_xq[:, k, nb * 512:(nb + 1) * 512], start=(k == 0), stop=(k == 7))

            def t4():
                for nb in range(2):
                    P.op("act", lambda e, nb=nb: e.mul(out=qxb2[p].ap[:, nb * 512:(nb + 1) * 512], in_=C_.ap[:, nb * 512:(nb + 1) * 512], mul=X_SCALE),
                         reads=[C_.b], writes=[qxb2[p].b])

            def t7():
                for h in range(4):
                    for et in range(2):
                        mm(C_[:, h * 256:(h + 1) * 256], qxT2[p][:, h * 2 + et, :], mkT[:, h, et, :], start=(et == 0), stop=(et == 1))

            def t8():
                red(mx42[p], C_.re("p (h m) -> p h m", h=4), ALU.max)
                ts(neg42[p], mx42[p], -1.0, None, ALU.mult)
                for h in range(4):
                    act(Px2[p][:, h, :], C_[:, h * 256:(h + 1) * 256], AF.Exp, bias=neg42[p][:, h:h + 1], accum=sum42[p][:, h:h + 1])

            def t11():
                for h in range(4):
                    for mt in range(2):
                        mm(C_[:, h * 256:(h + 1) * 256], PxT2[p][:, h * 2 + mt, :], mvb[:, mt, h * 256:(h + 1) * 256], start=(mt == 0), stop=(mt == 1))

            def t12():
                recip(sum42[p], sum42[p])
                tt(oxb2[p].re("p (h e) -> p h e", h=4), C_.re("p (h e) -> p h e", h=4), sum42[p].un(2).bc([128, 4, 256]), ALU.mult)

            def t15():
                for nb in range(2):
                    for k in range(8):
                        mm(C_[:, nb * 512:(nb + 1) * 512], oxT2[p][:, k, :], W_xo[:, k, nb * 512:(nb + 1) * 512], start=(k == 0), stop=(k == 7))

            def t16():
                stt(rres2[p], hin[p], ALPHA, C_, ALU.mult, ALU.add)
                layernorm(hout2[p], rres2[p], lng, lnb, stats2[p], mvv2[p])
                ld(h2_d[i * 128:(i + 1) * 128, :], hout2[p])

            return [t0, lambda: tr8(hb2[p]), lambda: ev8(hT2[p]), t3, t4, lambda: tr8(qxb2[p]), lambda: ev8(qxT2[p]), t7, t8,
                    lambda: tr8(Px2[p].re("p a b -> p (a b)")), lambda: ev8(PxT2[p]), t11, t12, lambda: tr8(oxb2[p]), lambda: ev8(oxT2[p]), t15, t16]

        def prep_items():
            items = []
            for s_ in range(4):
                for mt in range(2):
                    r0 = s_ * 256 + mt * 128

                    def l0(r0=r0):
                        ld(cmk[0], cmk_d[r0:r0 + 128, :]); ld(cmk[1], cmv_d[r0:r0 + 128, :])

                    def l1(s_=s_, mt=mt):
                        vcopy(cmkb[0], cmk[0], eng="pool")
                        vcopy(mvs4[s_][:, mt, :], cmk[1], eng="pool")

                    def l2(s_=s_, mt=mt):
                        for half in range(2):
                            bb = banks[6 + half].cast(BF16)
                            for k in range(4):
                                kk = half * 4 + k
                                tr(bb[:, k * 128:(k + 1) * 128], cmkb[0][:, kk * 128:(kk + 1) * 128], identb)

                    def l3(s_=s_, mt=mt):
                        for half in range(2):
                            bb = banks[6 + half].cast(BF16)
                            acopy(mkTs4[s_][:, half * 2:half * 2 + 2, :, mt * 128:(mt + 1) * 128].re("p h e m -> p (h e) m"), bb[:, 0:512].re("p (k m) -> p k m", k=4))

                    items += [l0, l1, l2, l3]
            return items

        pitems = prep_items()
        pi_ = 0
        slot = 0
        for ip in range(8):
            sa = xa_stages(2 * ip, 0); sb_ = xa_stages(2 * ip + 1, 1)
            for a_, b_ in zip(sa, sb_):
                a_(); b_()
                slot += 1
                if slot % 4 == 0 and pi_ < len(pitems):
                    pitems[pi_](); pi_ += 1
        while pi_ < len(pitems):
            pitems[pi_](); pi_ += 1

        for i in range(16, NOWN):
            b = i % NB
            ld(hin[b], h1_d[i * 128:(i + 1) * 128, :])
            vcopy(hb_, hin[b], eng="pool")
            for half in range(2):
                transpose_to(hT[:, half * 4:(half + 1) * 4, :], hb_[:, half * 512:(half + 1) * 512], 4, banks[0], eng=("act" if half == 0 else "dve"))
            for nb in range(2):
                for k in range(8):
                    mm(banks[1 + nb], hT[:, k, :], W_xq[:, k, nb * 512:(nb + 1) * 512], start=(k == 0), stop=(k == 7))
                P.op("act", lambda e, nb=nb: e.mul(out=qxb.ap[:, nb * 512:(nb + 1) * 512], in_=banks[1 + nb].ap, mul=X_SCALE), reads=[banks[1 + nb].b], writes=[qxb.b])
            for half in range(2):
                transpose_to(qxT[:, half * 4:(half + 1) * 4, :], qxb[:, half * 512:(half + 1) * 512], 4, banks[3], eng=("act" if half == 0 else "dve"))
            if i < 16:
                for h in range(4):
                    for et in range(2):
                        mm(SC[:, h * 256:(h + 1) * 256], qxT[:, h * 2 + et, :], mkT[:, h, et, :], start=(et == 0), stop=(et == 1))
                red(mx4, SC.re("p (h m) -> p h m", h=4), ALU.max)
                ts(neg4, mx4, -1.0, None, ALU.mult)
                for h in range(4):
                    act(Px[:, h, :], SC[:, h * 256:(h + 1) * 256], AF.Exp, bias=neg4[:, h:h + 1], accum=sum4[:, h:h + 1])
                for half in range(2):
                    transpose_to(PxT[:, half * 4:(half + 1) * 4, :], Px.re("p a b -> p (a b)")[:, half * 512:(half + 1) * 512], 4, banks[4], eng=("act" if half == 0 else "dve"))
                for h in range(4):
                    for mt in range(2):
                        mm(SC[:, h * 256:(h + 1) * 256], PxT[:, h * 2 + mt, :], mvb[:, mt, h * 256:(h + 1) * 256], start=(mt == 0), stop=(mt == 1))
                recip(sum4, sum4)
                tt(oxb.re("p (h e) -> p h e", h=4), SC.re("p (h e) -> p h e", h=4), sum4.un(2).bc([128, 4, 256]), ALU.mult)
            else:
                for s_ in range(4):
                    qs = slice(s_ * 32, (s_ + 1) * 32)
                    mkTs = mkTs4[s_]; mvs = mvs4[s_]
                    for h in range(4):
                        for et in range(2):
                            mm(SC[0:32, h * 256:(h + 1) * 256], qxT[:, h * 2 + et, qs], mkTs[:, h, et, :], start=(et == 0), stop=(et == 1))
                    red(mx4[0:32, :], SC[0:32, :].re("p (h m) -> p h m", h=4), ALU.max)
                    ts(neg4[0:32, :], mx4[0:32, :], -1.0, None, ALU.mult)
                    for h in range(4):
                        act(Px[0:32, h, :], SC[0:32, h * 256:(h + 1) * 256], AF.Exp, bias=neg4[0:32, h:h + 1], accum=sum4[0:32, h:h + 1])
                    bb = banks[4].cast(BF16)
                    for k in range(8):
                        tr(bb[:, k * 32:(k + 1) * 32], Px.re("p a b -> p (a b)")[0:32, k * 128:(k + 1) * 128], identb[0:32, 0:32])
                    vcopy(PxT[:, :, 0:32], bb[:, 0:256].re("p (k q) -> p k q", k=8))
                    for h in range(4):
                        for mt in range(2):
                            mm(SC[0:32, h * 256:(h + 1) * 256], PxT[:, h * 2 + mt, 0:32], mvs[:, mt, h * 256:(h + 1) * 256], start=(mt == 0), stop=(mt == 1))
                    recip(sum4[0:32, :], sum4[0:32, :])
                    tt(oxs[0:32, :].re("p (h e) -> p h e", h=4), SC[0:32, :].re("p (h e) -> p h e", h=4), sum4[0:32, :].un(2).bc([32, 4, 256]), ALU.mult)
                    ld(oxb[s_ * 32:(s_ + 1) * 32, :], oxs[0:32, :])
            for half in range(2):
                transpose_to(oxT[:, half * 4:(half + 1) * 4, :], oxb[:, half * 512:(half + 1) * 512], 4, banks[5], eng=("act" if half == 0 else "dve"))
            for nb in range(2):
                for k in range(8):
                    mm(banks[1 + nb], oxT[:, k, :], W_xo[:, k, nb * 512:(nb + 1) * 512], start=(k == 0), stop=(k == 7))
            for nb in range(2):
                stt(rres[:, nb * 512:(nb + 1) * 512], hin[b][:, nb * 512:(nb + 1) * 512], ALPHA, banks[1 + nb], ALU.mult, ALU.add)
            layernorm(hout, rres, lng, lnb, stats, mvv)
            ld(h2_d[i * 128:(i + 1) * 128, :], hout)

        barrier()
        bump[0] = mark_after_mem

        W1 = alloc([128, 8, 4096], BF16, "W1"); W2 = alloc([128, 32, D], BF16, "W2")
        for c in range(2):
            ldw(W1[:, :, c * 2048:(c + 1) * 2048], w_ff1_d.re("(kt p) n -> p kt n", p=128)[:, :, c * 2048:(c + 1) * 2048])
        for c in range(4):
            ldw(W2[:, c * 8:(c + 1) * 8, :], w_ff2_d.re("(kt p) n -> p kt n", p=128)[:, c * 8:(c + 1) * 8, :])
        lng = alloc([128, D], F32, "lng3"); lnb = alloc([128, D], F32, "lnb3")
        ld(lng, lng_d[:, 2 * D:3 * D]); ld(lnb, lnb_d[:, 2 * D:3 * D])
        hin = alloc([128, D], F32, "ch")
        hb_ = alloc([128, D], BF16, "chb"); hT = alloc([128, 8, 512], BF16, "chT")
        zr = [alloc([128, 512], F32, "zr%d" % i) for i in range(2)]; zT = alloc([128, 32, 512], BF16, "zT")
        rres = alloc([128, D], F32, "crres"); hout = alloc([128, D], F32, "chout")
        stats = alloc([128, 12], F32); mvv = alloc([128, 2], F32)
        groups = [list(range(g_ * 4, g_ * 4 + 4)) for g_ in range(4)] + [[16]]
        for grp in groups:
            nt = len(grp)
            for q_, i in enumerate(grp):
                ld(hin, h2_d[i * 128:(i + 1) * 128, :])
                vcopy(hb_, hin, eng="pool")
                for half in range(2):
                    transpose_to(hT[:, half * 4:(half + 1) * 4, q_ * 128:(q_ + 1) * 128], hb_[:, half * 512:(half + 1) * 512], 4, banks[0], eng=("act" if half == 0 else "dve"))
            W_ = nt * 128
            for f in range(32):
                bk = banks[1 + f % 2]
                for k in range(8):
                    mm(bk[:, 0:W_], W1[:, k, f * 128:(f + 1) * 128], hT[:, k, 0:W_], start=(k == 0), stop=(k == 7))
                act(zr[f % 2][:, 0:W_], bk[:, 0:W_], AF.Relu)
                tt(zT[:, f, 0:W_], zr[f % 2][:, 0:W_], zr[f % 2][:, 0:W_], ALU.mult)
            for q_, i in enumerate(grp):
                for nb in range(2):
                    for f in range(32):
                        mm(SC[:, nb * 512:(nb + 1) * 512], zT[:, f, q_ * 128:(q_ + 1) * 128], W2[:, f, nb * 512:(nb + 1) * 512], start=(f == 0), stop=(f == 31))
                ld(hin, h2_d[i * 128:(i + 1) * 128, :])
                stt(rres, hin, ALPHA, SC, ALU.mult, ALU.add)
                layernorm(hout, rres, lng, lnb, stats, mvv)
                ld(y_o[i * 128:(i + 1) * 128, :], hout)

        P.emit(st)
    return nc


def _lay_gp(a):
    sh = a.shape[2:]
    n = len(sh)
    return np.ascontiguousarray(a.reshape(16, 2, 64, *sh).transpose(1, 2, 0, *range(3, 3 + n)).reshape(128, 16, *sh))


def _bc(v, n=128):
    return np.ascontiguousarray(np.broadcast_to(np.asarray(v, np.float32).reshape(1, -1), (n, np.asarray(v).size)))


def _rope_tables(pos):
    inv = (10000.0 ** (-np.arange(32, dtype=np.float32) / 32)).astype(np.float32)
    ang = pos.astype(np.float32)[:, None] * inv[None, :]
    c = np.cos(ang).astype(np.float32)
    s = np.sin(ang).astype(np.float32)
    return np.concatenate([c, c], 1), np.concatenate([-s, s], 1)


_NC_CACHE = {}


def kernel(x_prompt, x_sample, mem_prompt, cache_mla_ckv, cache_mla_kpe, state_ssm_re, state_ssm_im,
           cache_mem_k, cache_mem_v, w_in, g_q, w_q_up, g_kv, w_kv_up, a_re, a_im, b_re, b_im, c_re, c_im,
           d_skip, log_dt, w_glu, g_out_ssm, g_out_mla, w_o, w_xq, w_xk, w_xv, w_xo, w_ff1, w_ff2, ln_g, ln_b):
    f = lambda a: np.ascontiguousarray(np.asarray(a, dtype=np.float32))
    x_prompt = f(x_prompt); x_sample = f(x_sample)
    xp = x_prompt[0]
    cre = np.zeros((128, 16, 32), np.float32); cimn = np.zeros((128, 16, 32), np.float32)
    cr = f(c_re)[0].reshape(16, 2, 16, 64); ci = f(c_im)[0].reshape(16, 2, 16, 64)
    for g2 in range(2):
        cre[g2 * 64:(g2 + 1) * 64, :, g2 * 16:(g2 + 1) * 16] = cr[:, g2].transpose(2, 0, 1)
        cimn[g2 * 64:(g2 + 1) * 64, :, g2 * 16:(g2 + 1) * 16] = -ci[:, g2].transpose(2, 0, 1)
    dblk = np.zeros((128, 16, 32), np.float32)
    dd = f(d_skip)[0].reshape(512)
    for gp in range(16):
        for c in range(32):
            ch = gp * 32 + c
            dblk[ch % 128, gp, c] = dd[ch]
    pos_p = np.arange(16384)
    rcp, rsp = _rope_tables(pos_p)
    common = {
        "xp": xp, "mem": f(mem_prompt)[0], "ident": np.eye(128, dtype=np.float32),
        "w_in": f(w_in)[0], "w_q": f(w_q_up)[0].reshape(384, 768), "w_kv": f(w_kv_up)[0].reshape(256, 1024),
        "w_glu": f(w_glu)[0], "w_o": f(w_o)[0], "w_xq": f(w_xq)[0].reshape(D, D), "w_xk": f(w_xk)[0].reshape(D, D),
        "w_xv": f(w_xv)[0].reshape(D, D), "w_xo": f(w_xo)[0].reshape(D, D), "w_ff1": f(w_ff1)[0], "w_ff2": f(w_ff2)[0],
        "gq": _bc(f(g_q)[0]), "gkv": _bc(f(g_kv)[0]), "gos": _bc(f(g_out_ssm)[0]), "gom": _bc(f(g_out_mla)[0]),
        "lng": _bc(f(ln_g)[0].reshape(-1)), "lnb": _bc(f(ln_b)[0].reshape(-1)),
        "ropec_p": rcp, "ropes_p": rsp,
        "ar": _lay_gp(f(a_re)[0]), "ai": _lay_gp(f(a_im)[0]),
        "ldt": _lay_gp(np.ascontiguousarray(np.broadcast_to(f(log_dt)[0][:, None], (32, 64)))),
        "bre": _lay_gp(f(b_re)[0]).reshape(128, 256), "bim": _lay_gp(f(b_im)[0]).reshape(128, 256),
        "jj": _bc(np.arange(1, 129, dtype=np.float32)), "jj2": _bc(127.0 - np.arange(128, dtype=np.float32)),
        "cre": cre.reshape(128, 512), "cimn": cimn.reshape(128, 512), "dblk": dblk.reshape(128, 512),
    }
    qi = np.arange(128)[:, None]
    in_maps = []
    for c in range(NCORES):
        tiles = [8 * i + c for i in range(16)]
        xo = np.concatenate([xp[t * 128:(t + 1) * 128] for t in tiles] + [x_sample[4 * c:4 * c + 4].reshape(128, D)], 0)
        pos_o = np.concatenate([np.arange(t * 128, (t + 1) * 128) for t in tiles] + [np.tile(1024 + np.arange(32), 4)])
        rco, rso = _rope_tables(pos_o)
        kj = np.arange(1024)[None, :]
        vis = ((kj // 128) < c) | (((kj // 128) == c) & (((kj % 128) // 64) <= (qi // 64)))
        maskd = np.where(vis, 0.0, NEG).astype(np.float32)
        onehot = np.zeros((128, 8), np.float32); onehot[:, c] = 1.0
        s0 = np.stack([_lay_gp(f(state_ssm_re)[0, 4 * c + s]) for s in range(4)], 1)
        s0i = np.stack([_lay_gp(f(state_ssm_im)[0, 4 * c + s]) for s in range(4)], 1)
        m = dict(common)
        m.update({
            "xo": np.ascontiguousarray(xo), "ropec_o": rco, "ropes_o": rso, "maskd": maskd, "onehot": onehot,
            "s0": np.ascontiguousarray(np.stack([s0, s0i], 1).reshape(128, 128)),
            "cckv": f(cache_mla_ckv)[0, 4 * c:4 * c + 4].reshape(4096, 256),
            "ckpe": f(cache_mla_kpe)[0, 4 * c:4 * c + 4].reshape(4096, 64),
            "cmk": f(cache_mem_k)[0, 4 * c:4 * c + 4].reshape(1024, D),
            "cmv": f(cache_mem_v)[0, 4 * c:4 * c + 4].reshape(1024, D),
        })
        in_maps.append(m)
    if "nc" not in _NC_CACHE:
        _NC_CACHE["nc"] = build_nc()
    res = run_bass_kernel_spmd(_NC_CACHE["nc"], in_maps, core_ids=list(range(NCORES)))
    R = res.results
    y_p = np.zeros((1, 16384, D), np.float32); y_s = np.zeros((32, 32, D), np.float32)
    ckv_s = np.zeros((1, 32, 32, 256), np.float32); kpe_s = np.zeros((1, 32, 32, 64), np.float32)
    sre_s = np.zeros((1, 32, 32, 64), np.float32); sim_s = np.zeros((1, 32, 32, 64), np.float32)

    def unlay(a):
        return a.reshape(2, 64, 16).transpose(2, 0, 1).reshape(32, 64)

    for c in range(NCORES):
        yo = R[c]["y_o"]
        for i in range(16):
            t = 8 * i + c
            y_p[0, t * 128:(t + 1) * 128] = yo[i * 128:(i + 1) * 128]
        y_s[4 * c:4 * c + 4] = yo[16 * 128:].reshape(4, 32, D)
        ckv_s[0, 4 * c:4 * c + 4] = R[c]["ckvs_o"].reshape(4, 32, 256)
        kpe_s[0, 4 * c:4 * c + 4] = R[c]["kpes_o"].reshape(4, 32, 64)
        sf = R[c]["ssms_o"].reshape(128, 2, 4, 16)
        for s in range(4):
            sre_s[0, 4 * c + s] = unlay(sf[:, 0, s, :])
            sim_s[0, 4 * c + s] = unlay(sf[:, 1, s, :])
    r0 = R[0]
    ckv_p = r0["ckv_o"].reshape(1, 1, 16384, 256); kpe_p = r0["kpe_o"].reshape(1, 1, 16384, 64)
    sp = r0["ssmp_o"]
    sre_p = unlay(sp[:, 0:16]).reshape(1, 1, 32, 64); sim_p = unlay(sp[:, 16:32]).reshape(1, 1, 32, 64)
    mk_p = r0["memk_o"].reshape(1, 1, 256, 4, 256); mv_p = r0["memv_o"].reshape(1, 1, 256, 4, 256)
    return (y_p, y_s, ckv_p, kpe_p, sre_p, sim_p, mk_p, mv_p, ckv_s, kpe_s, sre_s, sim_s)
```

```python
import math
from contextlib import ExitStack

import numpy as np
import concourse.bass as bass
import concourse.mybir as mybir
from concourse.bass_utils import run_bass_kernel_spmd

F32 = mybir.dt.float32
BF16 = mybir.dt.bfloat16
I32 = mybir.dt.int32
AF = mybir.ActivationFunctionType
ALU = mybir.AluOpType
AX = mybir.AxisListType

NCORES = 8
D = 1024
NTP = 128
NOWN = 17
EPS = 1e-5
ALPHA = 2.0 ** 0.25
MLA_SCALE = 192.0 ** -0.5
X_SCALE = 256.0 ** -0.5
TWO_PI = 2.0 * math.pi
NEG = -1e30
NRING = 24
COMPUTE = ("pe", "act", "dve", "pool")


class Buf:
    __slots__ = ("name", "lw", "rd")

    def __init__(self, name=""):
        self.name = name
        self.lw = None
        self.rd = []


def _flat(bs):
    out = []
    for b in bs:
        if isinstance(b, (tuple, list)):
            out.extend(b)
        else:
            out.append(b)
    return out


class Op:
    __slots__ = ("eng", "fn", "deps", "isdma", "flag", "cnt", "ring", "n")

    def __init__(self, eng, fn, isdma):
        self.eng = eng
        self.fn = fn
        self.isdma = isdma
        self.deps = set()
        self.flag = False
        self.cnt = 0
        self.ring = None
        self.n = 0


class Prog:
    def __init__(self, nc):
        self.nc = nc
        self.ops = {e: [] for e in ("pe", "act", "dve", "pool", "sp")}
        self.ndma = {e: 0 for e in self.ops}
        self.allops = []
        self.floor = []
        self.dmas_since = []

    def _add(self, eng, fn, reads, writes, isdma):
        op = Op(eng, fn, isdma)
        reads = _flat(reads)
        writes = _flat(writes)
        for f in self.floor:
            op.deps.add(f)
        for b in reads:
            if b.lw is not None:
                op.deps.add(b.lw)
        for b in writes:
            if b.lw is not None:
                op.deps.add(b.lw)
            for r in b.rd:
                op.deps.add(r)
        for b in reads:
            b.rd.append(op)
        for b in writes:
            b.lw = op
            b.rd = []
        op.deps.discard(op)
        if isdma:
            op.n = self.ndma[eng]
            self.ndma[eng] += 1
            self.dmas_since.append(op)
        self.ops[eng].append(op)
        self.allops.append(op)
        return op

    def op(self, eng, fn, reads=(), writes=()):
        return self._add(eng, fn, reads, writes, False)

    def dma(self, eng, fn, reads=(), writes=()):
        return self._add(eng, fn, reads, writes, True)

    def barrier(self, fn):
        op = Op("pool", fn, False)
        for f in self.floor:
            op.deps.add(f)
        for e in COMPUTE:
            for o in reversed(self.ops[e]):
                if not o.isdma:
                    op.deps.add(o)
                    break
        for o in self.dmas_since:
            op.deps.add(o)
        self.dmas_since = []
        self.ops["pool"].append(op)
        self.allops.append(op)
        self.floor = [op]

    def emit(self, stack):
        nc = self.nc
        for op in self.allops:
            for d in op.deps:
                if d.eng == "pe" and op.eng == "pe" and not d.isdma:
                    continue
                d.flag = True
        sems = {e: stack.enter_context(nc.semaphore("s_" + e)) for e in COMPUTE}
        rings = {}
        for e in self.ops:
            if self.ndma[e]:
                rings[e] = [stack.enter_context(nc.semaphore("r_%s_%d" % (e, i))) for i in range(NRING)]
        for e in self.ops:
            c = 0
            for op in self.ops[e]:
                if op.isdma:
                    op.ring = rings[e][op.n % NRING]
                    op.cnt = 16 * (op.n // NRING + 1)
                elif op.flag:
                    c += 1
                    op.cnt = c
        block = stack.enter_context(nc.Block())

        def run(e, h):
            waited = {}

            def wait(sem, val):
                k = id(sem)
                if waited.get(k, 0) >= val:
                    return
                waited[k] = val
                h.wait_ge(sem, val)

            for op in self.ops[e]:
                for d in op.deps:
                    if d.isdma:
                        wait(d.ring, d.cnt)
                    else:
                        if d.eng == "pe" and e == "pe":
                            continue
                        wait(sems[d.eng], d.cnt)
                if op.isdma and op.n >= NRING:
                    wait(op.ring, op.cnt - 16)
                ins = op.fn(h)
                if op.isdma:
                    ins.then_inc(op.ring, 16)
                elif op.flag:
                    ins.then_inc(sems[e], 1)
            if e in rings:
                n = self.ndma[e]
                for i in range(min(n, NRING)):
                    last = ((n - 1 - i) // NRING) * NRING + i
                    wait(rings[e][i], 16 * (last // NRING + 1))

        @block.tensor
        def _(h):
            run("pe", h)

        @block.scalar
        def _(h):
            run("act", h)

        @block.vector
        def _(h):
            run("dve", h)

        @block.gpsimd
        def _(h):
            run("pool", h)

        @block.sync
        def _(h):
            run("sp", h)


class V:
    __slots__ = ("ap", "b")

    def __init__(self, ap, b):
        self.ap = ap
        self.b = b

    def __getitem__(self, idx):
        return V(self.ap[idx], self.b)

    def re(self, pat, **kw):
        return V(self.ap.rearrange(pat, **kw), self.b)

    def cast(self, dt):
        return V(self.ap.bitcast(dt), self.b)

    def bc(self, shape):
        return V(self.ap.broadcast_to(shape), self.b)

    def un(self, ax):
        return V(self.ap.unsqueeze(ax), self.b)


def build_nc():
    nc = bass.Bass("TRN2", target_bir_lowering=False)
    P = Prog(nc)
    dram = {}

    def din(name, shape, dt=F32):
        t = nc.dram_tensor(name, list(shape), dt, kind="ExternalInput")
        dram[name] = V(t.ap(), Buf(name))
        return dram[name]

    def dout(name, shape):
        t = nc.dram_tensor(name, list(shape), F32, kind="ExternalOutput")
        dram[name] = V(t.ap(), Buf(name))
        return dram[name]

    def dscr(name, shape, dt=F32):
        t = nc.dram_tensor(name, list(shape), dt)
        return V(t.ap(), Buf(name))

    xp = din("xp", [NTP * 128, D])
    xo = din("xo", [NOWN * 128, D])
    mem = din("mem", [256, D])
    ident_d = din("ident", [128, 128])
    w_in_d = din("w_in", [D, 1216]); w_q_d = din("w_q", [384, 768]); w_kv_d = din("w_kv", [256, 1024])
    w_glu_d = din("w_glu", [512, 1024]); w_o_d = din("w_o", [D, D]); w_xq_d = din("w_xq", [D, D])
    w_xk_d = din("w_xk", [D, D]); w_xv_d = din("w_xv", [D, D]); w_xo_d = din("w_xo", [D, D])
    w_ff1_d = din("w_ff1", [D, 4096]); w_ff2_d = din("w_ff2", [4096, D])
    gq_d = din("gq", [128, 384]); gkv_d = din("gkv", [128, 256]); gos_d = din("gos", [128, 512]); gom_d = din("gom", [128, 512])
    lng_d = din("lng", [128, 3 * D]); lnb_d = din("lnb", [128, 3 * D])
    ropec_p = din("ropec_p", [NTP * 128, 64]); ropes_p = din("ropes_p", [NTP * 128, 64])
    ropec_o = din("ropec_o", [NOWN * 128, 64]); ropes_o = din("ropes_o", [NOWN * 128, 64])
    maskd_d = din("maskd", [128, 1024]); onehot_d = din("onehot", [128, 8])
    ar_d = din("ar", [128, 16]); ai_d = din("ai", [128, 16]); ldt_d = din("ldt", [128, 16])
    bre_d = din("bre", [128, 256]); bim_d = din("bim", [128, 256]); jj_d = din("jj", [128, 128]); jj2_d = din("jj2", [128, 128])
    cre_d = din("cre", [128, 512]); cimn_d = din("cimn", [128, 512]); dblk_d = din("dblk", [128, 512])
    s0_d = din("s0", [128, 128])
    cckv_d = din("cckv", [4 * 1024, 256]); ckpe_d = din("ckpe", [4 * 1024, 64])
    cmk_d = din("cmk", [4 * 256, D]); cmv_d = din("cmv", [4 * 256, D])

    y_o = dout("y_o", [NOWN * 128, D])
    ckv_o = dout("ckv_o", [NTP * 128, 256]); kpe_o = dout("kpe_o", [NTP * 128, 64])
    ssmp_o = dout("ssmp_o", [128, 32])
    memk_o = dout("memk_o", [256, D]); memv_o = dout("memv_o", [256, D])
    ckvs_o = dout("ckvs_o", [128, 256]); kpes_o = dout("kpes_o", [128, 64]); ssms_o = dout("ssms_o", [128, 128])

    h1_d = dscr("h1_d", [NOWN * 128, D]); h2_d = dscr("h2_d", [NOWN * 128, D])
    tab_d = {n_: dscr("tab_" + n_, [128, 2048]) for n_ in ("S_t", "C_t", "R_t")}
    tab_d["WBre"] = dscr("tab_WBre", [128, 2048], BF16); tab_d["WBim"] = dscr("tab_WBim", [128, 2048], BF16)

    with ExitStack() as st:
        ARENA_N = 52000
        arena = st.enter_context(nc.sbuf_tensor("arena", [128, ARENA_N], F32))
        bump = [0]

        def alloc(shape, dt=F32, name=""):
            n = 1
            for s_ in shape[1:]:
                n *= s_
            words = (n * (2 if dt == BF16 else 4) + 3) // 4
            words = (words + 7) // 8 * 8
            off = bump[0]
            bump[0] += words
            assert bump[0] <= ARENA_N, ("SBUF arena overflow", name, bump[0])
            ap = arena[:, off:off + words]
            if dt != F32:
                ap = ap.bitcast(dt)
            ap = ap[:, 0:n]
            if len(shape) > 2:
                names = " ".join("d%d" % i for i in range(len(shape) - 1))
                kw = {"d%d" % i: shape[i + 1] for i in range(len(shape) - 1)}
                ap = ap.rearrange("p (%s) -> p %s" % (names, names), **kw)
            return V(ap, Buf(name))

        banks = []
        dbl = []
        for d_ in range(4):
            t = st.enter_context(nc.psum_tensor("dbank%d" % d_, [128, 1024], F32))
            b0 = Buf("bank%d" % (2 * d_)); b1 = Buf("bank%d" % (2 * d_ + 1))
            banks.append(V(t[:, 0:512], b0)); banks.append(V(t[:, 512:1024], b1))
            dbl.append(V(t[:], (b0, b1)))
        SC = dbl[3]
        SCs = [dbl[3], dbl[0]]

        def mm(out, lhsT, rhs, start=True, stop=True):
            P.op("pe", lambda e: e.matmul(out.ap, lhsT=lhsT.ap, rhs=rhs.ap, start=start, stop=stop),
                 reads=[lhsT.b, rhs.b], writes=[out.b])

        def tr(out, in_, idt):
            P.op("pe", lambda e: e.transpose(out=out.ap, in_=in_.ap, identity=idt.ap), reads=[in_.b, idt.b], writes=[out.b])

        def act(out, in_, func, bias=None, scale=None, accum=None):
            kw = {}
            rd = [in_.b]
            wr = [out.b]
            if bias is not None:
                if isinstance(bias, V):
                    kw["bias"] = bias.ap; rd.append(bias.b)
                else:
                    kw["bias"] = bias
            if scale is not None:
                if isinstance(scale, V):
                    kw["scale"] = scale.ap; rd.append(scale.b)
                else:
                    kw["scale"] = scale
            if accum is not None:
                kw["accum_out"] = accum.ap; wr.append(accum.b)
            P.op("act", lambda e: e.activation(out=out.ap, in_=in_.ap, func=func, **kw), reads=rd, writes=wr)

        def acopy(out, in_):
            P.op("act", lambda e: e.copy(out=out.ap, in_=in_.ap), reads=[in_.b], writes=[out.b])

        def vcopy(out, in_, eng="dve"):
            P.op(eng, lambda e: e.tensor_copy(out=out.ap, in_=in_.ap), reads=[in_.b], writes=[out.b])

        def tt(out, in0, in1, op, eng="dve"):
            P.op(eng, lambda e: e.tensor_tensor(out=out.ap, in0=in0.ap, in1=in1.ap, op=op), reads=[in0.b, in1.b], writes=[out.b])

        def ts(out, in0, s1, s2, op0, op1=None, eng="dve"):
            rd = [in0.b]
            a1 = s1
            a2 = s2
            if isinstance(s1, V):
                a1 = s1.ap; rd.append(s1.b)
            if isinstance(s2, V):
                a2 = s2.ap; rd.append(s2.b)
            if op1 is None:
                P.op(eng, lambda e: e.tensor_scalar(out=out.ap, in0=in0.ap, scalar1=a1, scalar2=None, op0=op0), reads=rd, writes=[out.b])
            else:
                P.op(eng, lambda e: e.tensor_scalar(out=out.ap, in0=in0.ap, scalar1=a1, scalar2=a2, op0=op0, op1=op1), reads=rd, writes=[out.b])

        def stt(out, in0, scalar, in1, op0, op1):
            rd = [in0.b, in1.b]
            a = scalar
            if isinstance(scalar, V):
                a = scalar.ap; rd.append(scalar.b)
            P.op("dve", lambda e: e.scalar_tensor_tensor(out=out.ap, in0=in0.ap, scalar=a, in1=in1.ap, op0=op0, op1=op1), reads=rd, writes=[out.b])

        def red(out, in_, op, axis=AX.X):
            P.op("dve", lambda e: e.tensor_reduce(out=out.ap, in_=in_.ap, axis=axis, op=op), reads=[in_.b], writes=[out.b])

        def recip(out, in_):
            P.op("dve", lambda e: e.reciprocal(out=out.ap, in_=in_.ap), reads=[in_.b], writes=[out.b])

        def scan(out, d0, d1, init):
            P.op("dve", lambda e: e.tensor_tensor_scan(out=out.ap, data0=d0.ap, data1=d1.ap, initial=init.ap, op0=ALU.mult, op1=ALU.add),
                 reads=[d0.b, d1.b, init.b], writes=[out.b])

        def mset(out, val, eng="pool"):
            P.op(eng, lambda e: e.memset(out.ap, val), writes=[out.b])

        def ld(out, in_, eng="sp"):
            P.dma(eng, lambda e: e.dma_start(out=out.ap, in_=in_.ap), reads=[in_.b], writes=[out.b])

        def ldw(out, in_):
            P.dma("pool", lambda e: e.dma_start(out=out.ap, in_=in_.ap), reads=[in_.b], writes=[out.b])

        ident = alloc([128, 128], F32, "ident"); identb = alloc([128, 128], BF16, "identb")
        bar_scr = alloc([128, 8], F32, "barscr")
        ld(ident, ident_d); ldw(identb, ident_d)
        epsc = alloc([128, 1], F32, "epsc"); mset(epsc, EPS)

        def barrier():
            P.barrier(lambda e: e.memset(bar_scr.ap, 0.0))

        def load_xT(src_rows, xt, xT, bank):
            ld(xt, src_rows)
            for hb in range(2):
                for k in range(4):
                    kk = hb * 4 + k
                    tr(bank[:, k * 128:(k + 1) * 128], xt[:, kk * 128:(kk + 1) * 128], ident)
                src = bank.re("p (k n) -> p k n", k=4)
                if hb == 0:
                    acopy(xT[:, 0:4, :], src)
                else:
                    vcopy(xT[:, 4:8, :], src)

        def transpose_to(dst, src, ncol, bank, eng="act", rows=128):
            bb = bank.cast(BF16)
            for k in range(ncol):
                tr(bb[:, k * 128:k * 128 + rows], src[:, k * 128:(k + 1) * 128], identb[0:rows, 0:rows])
            s_ = bb[:, 0:ncol * 128].re("p (k n) -> p k n", k=ncol)[:, :, 0:rows]
            if eng == "act":
                acopy(dst, s_)
            else:
                vcopy(dst, s_)

        def rmsnorm(out, src, n, gtile, sq, ss):
            act(sq, src, AF.Square, accum=ss)
            act(ss, ss, AF.Ln, scale=1.0 / n, bias=epsc)
            act(ss, ss, AF.Exp, scale=-0.5)
            stt(out, src, ss, gtile, ALU.mult, ALU.mult)

        def layernorm(out, r, g, b, stats, mv):
            for c in range(2):
                P.op("dve", lambda e, c=c: e.bn_stats(out=stats.ap[:, c * 6:(c + 1) * 6], in_=r.ap[:, c * 512:(c + 1) * 512]), reads=[r.b], writes=[stats.b])
            P.op("dve", lambda e: e.bn_aggr(out=mv.ap[:, 0:2], in_=stats.ap[:, 0:12]), reads=[stats.b], writes=[mv.b])
            act(mv[:, 1:2], mv[:, 1:2], AF.Ln, bias=epsc)
            act(mv[:, 1:2], mv[:, 1:2], AF.Exp, scale=-0.5)
            ts(out, r, mv[:, 0:1], mv[:, 1:2], ALU.subtract, ALU.mult)
            tt(out, out, g, ALU.mult, eng="pool")
            tt(out, out, b, ALU.add, eng="pool")

        def rope(out, src, cc, ss_, tmp, nh, eng="dve"):
            ccb = cc.un(1).bc([128, nh, 64]); ssb = ss_.un(1).bc([128, nh, 64])
            tt(out, src, ccb, ALU.mult, eng=eng)
            tt(tmp[:, :, 0:32], src[:, :, 32:64], ssb[:, :, 0:32], ALU.mult, eng=eng)
            tt(tmp[:, :, 32:64], src[:, :, 0:32], ssb[:, :, 32:64], ALU.mult, eng=eng)
            tt(out, out, tmp, ALU.add, eng="pool")

        mkT = alloc([128, 4, 2, 256], BF16, "mkT")
        mvb = alloc([128, 2, D], BF16, "mvb")
        mark_after_mem = bump[0]
        btab = Buf("tab")
        W_in_u = alloc([128, 8, 512], BF16, "W_in_u"); W_in_kv = alloc([128, 8, 320], BF16, "W_in_kv")
        LrT = alloc([128, 16, 128], BF16, "LrT"); LiT = alloc([128, 16, 128], BF16, "LiT")
        BRb = alloc([128, 16, 32], F32, "BRb"); BIb = alloc([128, 16, 32], F32, "BIb")
        lam128 = alloc([128, 2, 16], F32, "lam128")
        Xpp = alloc([128, 2, 32], F32, "Xpp")
        Xown = alloc([128, 16, 32], F32, "Xown")
        ohot = alloc([128, 8], F32, "ohot")
        Oacc = alloc([128, 16, 4, 128], F32, "Oacc")
        m_run = alloc([128, 64], F32, "m_run"); l_run = alloc([128, 64], F32, "l_run")
        Osam = alloc([128, 512], F32, "Osam")
        ssmx = {}
        mark_after_mixer_state = bump[0]
        QTn = alloc([128, NOWN, 4, 128], BF16, "QTn"); QTr = alloc([128, NOWN, 4, 128], BF16, "QTr")
        mark_after_q = bump[0]

        w_in_v = w_in_d.re("(kt p) n -> p kt n", p=128)
        ldw(W_in_u, w_in_v[:, :, 0:512]); ldw(W_in_kv, w_in_v[:, :, 896:1216])
        ld(ohot, onehot_d)

        S_t = alloc([128, 16, 128], F32, "S_t"); C_t = alloc([128, 16, 128], F32, "C_t"); R_t = alloc([128, 16, 128], F32, "R_t")
        for v_ in (S_t, C_t, R_t):
            v_.b = btab
        WBre = alloc([128, 16, 128], BF16, "WBre"); WBim = alloc([128, 16, 128], BF16, "WBim")

        p0 = bump[0]
        ar = alloc([128, 16]); ai = alloc([128, 16]); ldt = alloc([128, 16]); jj = alloc([128, 128])
        bre = alloc([128, 16, 16]); bim = alloc([128, 16, 16])
        bsm = Buf("ssm_small")
        for v_ in (ar, ai, ldt, jj, bre, bim):
            v_.b = bsm
        ld(ar, ar_d); ld(ai, ai_d); ld(ldt, ldt_d); ld(jj, jj_d)
        ld(bre.re("p a b -> p (a b)"), bre_d); ld(bim.re("p a b -> p (a b)"), bim_d)
        dt_ = alloc([128, 16]); th = alloc([128, 16]); rr = alloc([128, 16])
        sm = [alloc([128, 16]) for _ in range(8)]
        for v_ in [dt_, th, rr] + sm:
            v_.b = bsm
        act(dt_, ldt, AF.Exp)
        tt(th, ai, dt_, ALU.mult)
        tt(rr, ar, dt_, ALU.mult)
        act(rr, rr, AF.Exp)
        A_t = alloc([128, 16, 128]); T1 = alloc([128, 2048]); TI = alloc([128, 2048], I32)
        for v_ in (A_t, T1, TI):
            v_.b = btab
        jjb = jj.un(1).bc([128, 16, 128])
        tt(A_t, jjb, th.un(2).bc([128, 16, 128]), ALU.mult)
        tt(R_t, jjb, rr.un(2).bc([128, 16, 128]), ALU.max)
        tt(R_t, R_t, rr.un(2).bc([128, 16, 128]), ALU.min)
        Af = A_t.re("p a b -> p (a b)")
        TIf = TI.cast(F32)

        def sin_of(out, shift):
            ts(T1, Af, shift, 1.0 / TWO_PI, ALU.add, ALU.mult)
            vcopy(TI, T1)
            vcopy(T1, TI)
            stt(T1, T1, -TWO_PI, Af, ALU.mult, ALU.add)
            if shift != 0.0:
                ts(T1, T1, shift, None, ALU.add)
            ts(TIf, T1, math.pi, -TWO_PI, ALU.is_gt, ALU.mult)
            tt(T1, T1, TIf, ALU.add)
            ts(TIf, T1, -math.pi, TWO_PI, ALU.is_lt, ALU.mult)
            tt(T1, T1, TIf, ALU.add)
            ts(T1, T1, math.pi, -math.pi, ALU.min, ALU.max)
            act(out, T1, AF.Sin)

        sin_of(S_t.re("p a b -> p (a b)"), 0.0)
        sin_of(C_t.re("p a b -> p (a b)"), math.pi / 2)
        lbr, lbi, den, fre, fim, t0_, t1_, t2_ = sm
        cos1 = C_t[:, :, 0]; sin1 = S_t[:, :, 0]
        tt(lbr, rr, cos1, ALU.mult)
        ts(lbr, lbr, -1.0, None, ALU.add)
        tt(lbi, rr, sin1, ALU.mult)
        tt(den, ar, ar, ALU.mult)
        tt(t0_, ai, ai, ALU.mult)
        tt(den, den, t0_, ALU.add)
        recip(den, den)
        tt(t0_, lbr, ar, ALU.mult); tt(t1_, lbi, ai, ALU.mult); tt(fre, t0_, t1_, ALU.add); tt(fre, fre, den, ALU.mult)
        tt(t0_, lbi, ar, ALU.mult); tt(t1_, lbr, ai, ALU.mult); tt(fim, t0_, t1_, ALU.subtract); tt(fim, fim, den, ALU.mult)
        Mre = alloc([128, 16, 128]); Mim = alloc([128, 16, 128]); tb = alloc([128, 16, 16]); tb2 = alloc([128, 16, 16])
        Mreb = alloc([128, 16, 128], BF16); Mimb = alloc([128, 16, 128], BF16)
        bM = Buf("M")
        for v_ in (Mre, Mim, tb, tb2, Mreb, Mimb):
            v_.b = bM
        freb = fre.un(2).bc([128, 16, 16]); fimb = fim.un(2).bc([128, 16, 16])
        mset(Mre, 0.0); mset(Mim, 0.0)
        tt(tb, bre, freb, ALU.mult); tt(tb2, bim, fimb, ALU.mult)
        for lo, col in ((0, 0), (64, 16)):
            for j4 in range(4):
                tt(Mre[lo:lo + 64, j4::4, 32 * j4 + col:32 * j4 + col + 16], tb[lo:lo + 64, j4::4, :], tb2[lo:lo + 64, j4::4, :], ALU.subtract)
        tt(tb, bim, freb, ALU.mult); tt(tb2, bre, fimb, ALU.mult)
        for lo, col in ((0, 0), (64, 16)):
            for j4 in range(4):
                tt(Mim[lo:lo + 64, j4::4, 32 * j4 + col:32 * j4 + col + 16], tb[lo:lo + 64, j4::4, :], tb2[lo:lo + 64, j4::4, :], ALU.add)
        vcopy(Mreb, Mre); vcopy(Mimb, Mim)
        for M_, WB_ in ((Mreb, WBre), (Mimb, WBim)):
            for hf in range(2):
                bb = banks[0].cast(BF16)
                for q in range(8):
                    tr(bb[:, q * 128:(q + 1) * 128], M_[:, 8 * hf + q, :], identb)
                vcopy(WB_[:, 8 * hf:8 * hf + 8, :].re("p a b -> p (a b)"), bb)

        mset(BRb, 0.0); mset(BIb, 0.0)
        tt(tb, bre, freb, ALU.mult); tt(tb2, bim, fimb, ALU.mult)
        for lo, col in ((0, 0), (64, 16)):
            tt(BRb[lo:lo + 64, :, col:col + 16], tb[lo:lo + 64], tb2[lo:lo + 64], ALU.subtract)
        tt(tb, bim, freb, ALU.mult); tt(tb2, bre, fimb, ALU.mult)
        for lo, col in ((0, 0), (64, 16)):
            tt(BIb[lo:lo + 64, :, col:col + 16], tb[lo:lo + 64], tb2[lo:lo + 64], ALU.add)
        lnr = alloc([128, 16]); lnr.b = bsm
        tt(lnr, ar, dt_, ALU.mult)
        act(t2_, lnr, AF.Exp, scale=128.0)
        tt(lam128[:, 0, :], t2_, C_t[:, :, 127], ALU.mult)
        tt(lam128[:, 1, :], t2_, S_t[:, :, 127], ALU.mult)
        jj2 = alloc([128, 128]); jj2.b = bsm
        ld(jj2, jj2_d)
        C2 = Mre; S2 = Mim; Mag = T1.re("p (a b) -> p a b", a=16)
        jj2b = jj2.un(1).bc([128, 16, 128])
        tt(A_t, jj2b, th.un(2).bc([128, 16, 128]), ALU.mult)
        sin_of(S2.re("p a b -> p (a b)"), 0.0)
        sin_of(C2.re("p a b -> p (a b)"), math.pi / 2)
        tt(Mag, jj2b, lnr.un(2).bc([128, 16, 128]), ALU.mult)
        act(Mag, Mag, AF.Exp)
        Lb = [Mreb, Mimb]
        tt(Lb[0], Mag, C2, ALU.mult); tt(Lb[1], Mag, S2, ALU.mult)
        for M_, LT_ in ((Lb[0], LrT), (Lb[1], LiT)):
            for hf in range(2):
                bb = banks[0].cast(BF16)
                for q in range(8):
                    tr(bb[:, q * 128:(q + 1) * 128], M_[:, 8 * hf + q, :], identb)
                vcopy(LT_[:, 8 * hf:8 * hf + 8, :].re("p a b -> p (a b)"), bb)

        for nm_, v_ in (("S_t", S_t), ("C_t", C_t), ("R_t", R_t)):
            ld(tab_d[nm_], v_.re("p a b -> p (a b)"))
        ld(tab_d["WBre"], WBre.re("p a b -> p (a b)")); ld(tab_d["WBim"], WBim.re("p a b -> p (a b)"))

        barrier()
        bump[0] = mark_after_q
        W_xk = alloc([128, 8, D], BF16, "W_xk"); W_xv = alloc([128, 8, D], BF16, "W_xv")
        ldw(W_xk, w_xk_d.re("(kt p) n -> p kt n", p=128)); ldw(W_xv, w_xv_d.re("(kt p) n -> p kt n", p=128))
        xt0 = alloc([128, D], F32, "xt0"); memT = alloc([128, 8, 256], BF16, "memT"); xT0 = alloc([128, 8, 128], BF16, "xT0")
        mkf = alloc([128, D], F32, "mkf")
        for mt in range(2):
            load_xT(mem[mt * 128:(mt + 1) * 128, :], xt0, xT0, banks[0])
            vcopy(memT[:, :, mt * 128:(mt + 1) * 128], xT0, eng="pool")
            for W_, o_d, keep in ((W_xk, memk_o, False), (W_xv, memv_o, True)):
                for nb in range(2):
                    for k in range(8):
                        mm(banks[1 + nb], xT0[:, k, :], W_[:, k, nb * 512:(nb + 1) * 512], start=(k == 0), stop=(k == 7))
                    acopy(mkf[:, nb * 512:(nb + 1) * 512], banks[1 + nb])
                ld(o_d[mt * 128:(mt + 1) * 128, :], mkf)
                if keep:
                    vcopy(mvb[:, mt, :], mkf)
        for h in range(4):
            for et in range(2):
                c0 = h * 256 + et * 128
                for k in range(8):
                    mm(banks[3][:, 0:256], W_xk[:, k, c0:c0 + 128], memT[:, k, :], start=(k == 0), stop=(k == 7))
                acopy(mkT[:, h, et, :], banks[3][:, 0:256])

        barrier()
        bump[0] = mark_after_q

        W_q = alloc([128, 3, 768], BF16, "W_q")
        ldw(W_q, w_q_d.re("(kt p) n -> p kt n", p=128))
        W_in_q = alloc([128, 8, 384], BF16, "W_in_q")
        ldw(W_in_q, w_in_v[:, :, 512:896])
        gq = alloc([128, 384], F32, "gq"); ld(gq, gq_d)
        NB = 2
        xt = [alloc([128, D], F32, "xt%d" % i) for i in range(NB)]
        xT = [alloc([128, 8, 128], BF16, "xT%d" % i) for i in range(NB)]
        rc = [alloc([128, 64], F32, "rc%d" % i) for i in range(NB)]
        rs = [alloc([128, 64], F32, "rs%d" % i) for i in range(NB)]
        sq = alloc([128, 512], F32, "sq"); ss1 = [alloc([128, 1], F32) for _ in range(NB)]
        cqn = [alloc([128, 384], BF16) for _ in range(NB)]
        cqT = [alloc([128, 3, 128], BF16) for _ in range(NB)]
        qf = [alloc([128, 4, 192], F32) for _ in range(NB)]
        qr = [alloc([128, 4, 64], F32) for _ in range(NB)]
        qtmp = [alloc([128, 4, 64], F32) for _ in range(NB)]
        qb = [alloc([128, 4, 192], BF16) for _ in range(NB)]
        for i in range(NOWN):
            b = i % NB
            load_xT(xo[i * 128:(i + 1) * 128, :], xt[b], xT[b], banks[0])
            ld(rc[b], ropec_o[i * 128:(i + 1) * 128, :]); ld(rs[b], ropes_o[i * 128:(i + 1) * 128, :])
            for k in range(8):
                mm(banks[1][:, 0:384], xT[b][:, k, :], W_in_q[:, k, :], start=(k == 0), stop=(k == 7))
            rmsnorm(cqn[b], banks[1][:, 0:384], 384, gq, sq[:, 0:384], ss1[b])
            transpose_to(cqT[b], cqn[b], 3, banks[2], eng="act")
            for nb, (c0, c1) in enumerate(((0, 512), (512, 768))):
                for k in range(3):
                    mm(SC[:, nb * 512:nb * 512 + (c1 - c0)], cqT[b][:, k, :], W_q[:, k, c0:c1], start=(k == 0), stop=(k == 2))
            acopy(qf[b].re("p a b -> p (a b)"), SC[:, 0:768])
            rope(qr[b], qf[b][:, :, 128:192], rc[b], rs[b], qtmp[b], 4)
            P.op("act", lambda e, b=b: e.mul(out=qb[b].ap[:, :, 0:128], in_=qf[b].ap[:, :, 0:128], mul=MLA_SCALE), reads=[qf[b].b], writes=[qb[b].b])
            P.op("act", lambda e, b=b: e.mul(out=qb[b].ap[:, :, 128:192], in_=qr[b].ap, mul=MLA_SCALE), reads=[qr[b].b], writes=[qb[b].b])
            bb = banks[3].cast(BF16)
            for h in range(4):
                tr(bb[:, h * 128:(h + 1) * 128], qb[b][:, h, 0:128], identb)
                tr(bb[0:64, 512 + h * 128:512 + (h + 1) * 128], qb[b][:, h, 128:192], identb)
            vcopy(QTn[:, i, :, :].re("p a b -> p (a b)"), bb[:, 0:512])
            vcopy(QTr[0:64, i, :, :].re("p a b -> p (a b)"), bb[0:64, 512:1024])

        barrier()
        bump[0] = mark_after_q

        def ssm_rounds(uT_, pbs, init_fn, nseg, ybank=None, last_out=None):
            L = 128 // nseg
            S_t = ssmx["S_t"]; C_t = ssmx["C_t"]; R_t = ssmx["R_t"]; WBre = ssmx["WBre"]; WBim = ssmx["WBim"]
            Cre = ssmx["Cre"]; Cimn = ssmx["Cimn"]; Dblk = ssmx["Dblk"]
            xs4 = ssmx["xs4"]; zl_all = ssmx["zl_all"]
            def do_round(r):
                m4 = ssmx["m4"][r % 2]; wri = ssmx["wri"][r % 2]; zri = ssmx["zri"][r % 2]
                pb = pbs[r % 2]
                for j in range(2):
                    gp = 2 * r + j
                    mm(pb[:, j * 128:(j + 1) * 128], WBre[:, gp, :], uT_[:, gp // 4, :])
                    mm(pb[:, 256 + j * 128:256 + (j + 1) * 128], WBim[:, gp, :], uT_[:, gp // 4, :])
                if nseg == 1:
                    Cq = C_t[:, 2 * r:2 * r + 2, :].re("p a b -> p (a b)"); Sq = S_t[:, 2 * r:2 * r + 2, :].re("p a b -> p (a b)")
                    pre = pb[:, 0:256]; pim = pb[:, 256:512]
                    mk = lambda v_: v_
                else:
                    Cq = C_t[:, 2 * r:2 * r + 2, 0:L].un(2).bc([128, 2, nseg, L]); Sq = S_t[:, 2 * r:2 * r + 2, 0:L].un(2).bc([128, 2, nseg, L])
                    pre = pb[:, 0:256].re("p (a s l) -> p a s l", a=2, s=nseg); pim = pb[:, 256:512].re("p (a s l) -> p a s l", a=2, s=nseg)
                    mk = lambda v_: v_.re("p (a s l) -> p a s l", a=2, s=nseg)
                tt(mk(m4[0]), pre, Cq, ALU.mult); tt(mk(m4[1]), pim, Sq, ALU.mult)
                tt(mk(m4[2]), pim, Cq, ALU.mult); tt(mk(m4[3]), pre, Sq, ALU.mult)
                tt(wri[0], m4[0], m4[1], ALU.add, eng="pool")
                tt(wri[1], m4[2], m4[3], ALU.subtract, eng="pool")

            def do_back(r):
                m4 = ssmx["m4"][r % 2]; wri = ssmx["wri"][r % 2]; zri = ssmx["zri"][r % 2]
                if nseg == 1:
                    Cq = C_t[:, 2 * r:2 * r + 2, :].re("p a b -> p (a b)"); Sq = S_t[:, 2 * r:2 * r + 2, :].re("p a b -> p (a b)")
                else:
                    Cq = C_t[:, 2 * r:2 * r + 2, 0:L].un(2).bc([128, 2, nseg, L]); Sq = S_t[:, 2 * r:2 * r + 2, 0:L].un(2).bc([128, 2, nseg, L])
                for j in range(2):
                    gp = 2 * r + j
                    for s_ in range(nseg):
                        for part in range(2):
                            scan(zri[part][:, j, s_ * L:(s_ + 1) * L], R_t[:, gp, 0:L], wri[part][:, j * 128 + s_ * L:j * 128 + (s_ + 1) * L], init_fn(gp, s_, part))
                if last_out is not None:
                    for part in range(2):
                        src = zri[part].re("p a (s l) -> p a s l", s=nseg)[:, :, :, L - 1]
                        vcopy(zl_all[part][:, 0:nseg, 2 * r:2 * r + 2].re("p s a -> p a s"), src, eng="pool")
                if ybank is not None:
                    dmd = ssmx["dmd"][r % 2]; xrb = ssmx["xrb"]; xib = ssmx["xib"]
                    if nseg == 1:
                        Cd, Sd = Cq, Sq
                        zr_ = zri[0].re("p a b -> p (a b)"); zi_ = zri[1].re("p a b -> p (a b)")
                        dk = lambda v_: v_
                    else:
                        Cd, Sd = Cq, Sq
                        zr_ = zri[0].re("p a (s l) -> p a s l", s=nseg); zi_ = zri[1].re("p a (s l) -> p a s l", s=nseg)
                        dk = lambda v_: v_.re("p (a s l) -> p a s l", a=2, s=nseg)
                    tt(dk(dmd[0]), zr_, Cd, ALU.mult); tt(dk(dmd[1]), zi_, Sd, ALU.mult, eng="pool")
                    tt(dk(dmd[2]), zr_, Sd, ALU.mult); tt(dk(dmd[3]), zi_, Cd, ALU.mult, eng="pool")
                    tt(xrb[r % 2].re("p a b -> p (a b)"), dmd[0], dmd[1], ALU.subtract, eng="pool")
                    tt(xib[r % 2].re("p a b -> p (a b)"), dmd[2], dmd[3], ALU.add, eng="pool")

            def do_cproj(r):
                if ybank is None:
                    return
                xrb = ssmx["xrb"]; xib = ssmx["xib"]
                for j in range(2):
                    gp = 2 * r + j
                    o_ = ybank[:, 32 * gp:32 * gp + 32]
                    mm(o_, xrb[r % 2][:, j, :], Cre[:, gp, :], start=True, stop=False)
                    mm(o_, xib[r % 2][:, j, :], Cimn[:, gp, :], start=False, stop=False)
                    mm(o_, uT_[:, gp // 4, :], Dblk[:, gp, :], start=False, stop=True)
            def do_tail():
                do_cproj(7)
                if last_out is None:
                    return
                if True:
                    cL = C_t[:, :, L - 1]; sL = S_t[:, :, L - 1]
                    for s_ in range(nseg):
                        o_re, o_im = last_out(s_)
                        zr_l = zl_all[0][:, s_, :]; zi_l = zl_all[1][:, s_, :]
                        tt(xs4[0], zr_l, cL, ALU.mult); tt(xs4[1], zi_l, sL, ALU.mult)
                        tt(xs4[2], zr_l, sL, ALU.mult); tt(xs4[3], zi_l, cL, ALU.mult)
                        tt(o_re, xs4[0], xs4[1], ALU.subtract)
                        tt(o_im, xs4[2], xs4[3], ALU.add)


            def step(k_):
                def f():
                    if k_ == 0:
                        do_round(0)
                    if k_ + 1 < 8:
                        do_round(k_ + 1)
                    do_back(k_)
                    if k_ >= 1:
                        do_cproj(k_ - 1)
                return f
            return [step(k_) for k_ in range(8)] + [do_tail]

        def ssm_tile(uT_, pbs, init_fn, nseg, ybank=None, last_out=None):
            for f_ in ssm_rounds(uT_, pbs, init_fn, nseg, ybank, last_out):
                f_()

        W_kv = alloc([128, 2, 4, 256], BF16, "W_kv")
        ldw(W_kv.re("p k h e -> p k (h e)"), w_kv_d.re("(kt p) n -> p kt n", p=128))
        gkv = alloc([128, 256], F32, "gkv"); ld(gkv, gkv_d)
        maskd = alloc([128, 1024], F32, "maskd"); ld(maskd, maskd_d)
        xt = [alloc([128, D], F32, "hxt%d" % i) for i in range(NB)]
        xT = [alloc([128, 8, 128], BF16, "hxT%d" % i) for i in range(NB)]
        utok = [alloc([128, 512], BF16, "hut%d" % i) for i in range(NB)]
        kvf = [alloc([128, 320], F32, "kvf%d" % i) for i in range(NB)]
        prs = [alloc([128, 16, 32], F32, "prs%d" % i) for i in range(NB)]
        pis = [alloc([128, 16, 32], F32, "pis%d" % i) for i in range(NB)]
        rc = [alloc([128, 64], F32) for _ in range(NB)]; rs = [alloc([128, 64], F32) for _ in range(NB)]
        ckvf = [alloc([128, 256], F32) for _ in range(NB)]; kpef = [alloc([128, 64], F32) for _ in range(NB)]
        tmp64 = [alloc([128, 64], F32) for _ in range(NB)]
        sq = alloc([128, 256], F32, "hsq"); ss1 = [alloc([128, 1], F32) for _ in range(NB)]
        ckvb = [alloc([128, 256], BF16) for _ in range(NB)]; kpeb = [alloc([128, 128], BF16) for _ in range(NB)]
        ckvT = [alloc([128, 2, 128], BF16) for _ in range(NB)]
        KT = [alloc([128, 4, 1024], BF16, "KT%d" % i) for i in range(2)]
        KR = [alloc([128, 1024], BF16, "KR%d" % i) for i in range(2)]
        Vb = [alloc([128, 8, 512], BF16, "Vb%d" % i) for i in range(2)]
        Pb = [alloc([128, 1024], BF16, "Pb%d" % i) for i in range(2)]
        PT = [alloc([128, 8, 128], BF16, "PT%d" % i) for i in range(2)]
        NS4 = 4
        mx = [alloc([128, 1], F32) for _ in range(NS4)]; mnew = [alloc([128, 1], F32) for _ in range(NS4)]
        negm = [alloc([128, 1], F32) for _ in range(NS4)]; alp = [alloc([128, 1], F32) for _ in range(NS4)]
        rsum = [alloc([128, 1], F32) for _ in range(NS4)]
        dm_ = [alloc([128, 1], F32) for _ in range(NS4)]
        hm1 = [alloc([128, 16, 32], F32, "hm1_%d" % i) for i in range(NB)]; hm2 = [alloc([128, 16, 32], F32, "hm2_%d" % i) for i in range(NB)]
        sre = [alloc([128, 32], F32) for _ in range(2)]
        xs8 = [alloc([128, 16], F32) for _ in range(4)]

        mset(Xpp, 0.0); mset(Xown, 0.0)
        mset(m_run, NEG); mset(l_run, 0.0); mset(Oacc, 0.0)
        for kp in range(2):
            mset(kpeb[kp], 0.0)

        def hist_stages(t, kbuf, j, kb):
            b = t % NB
            bA = banks[0]; bB = banks[1]
            xc = Xpp[:, t % 2, :]; xn = Xpp[:, (t + 1) % 2, :]

            def sA():
                ld(xt[b], xp[t * 128:(t + 1) * 128, :])
                ld(rc[b], ropec_p[t * 128:(t + 1) * 128, :]); ld(rs[b], ropes_p[t * 128:(t + 1) * 128, :])

            def sB():
                for k in range(4):
                    tr(bA[:, k * 128:(k + 1) * 128], xt[b][:, k * 128:(k + 1) * 128], ident)
                for k in range(4):
                    tr(bB[:, k * 128:(k + 1) * 128], xt[b][:, (4 + k) * 128:(5 + k) * 128], ident)

            def sC():
                acopy(xT[b][:, 0:4, :], bA.re("p (k n) -> p k n", k=4))
                acopy(xT[b][:, 4:8, :], bB.re("p (k n) -> p k n", k=4))

            def sD():
                for k in range(8):
                    mm(bA, xT[b][:, k, :], W_in_u[:, k, :], start=(k == 0), stop=(k == 7))
                for k in range(8):
                    mm(bB[:, 0:320], xT[b][:, k, :], W_in_kv[:, k, :], start=(k == 0), stop=(k == 7))

            def sE():
                acopy(utok[b], bA)
                acopy(kvf[b], bB[:, 0:320])

            def sF():
                act(sq, kvf[b][:, 0:256], AF.Square, accum=ss1[b])
                act(ss1[b], ss1[b], AF.Ln, scale=1.0 / 256, bias=epsc)
                act(ss1[b], ss1[b], AF.Exp, scale=-0.5)
                for gp in range(16):
                    mm(bA[:, 32 * gp:32 * gp + 32], LrT[:, gp, :], utok[b][:, 32 * gp:32 * gp + 32])
                for gp in range(16):
                    mm(bB[:, 32 * gp:32 * gp + 32], LiT[:, gp, :], utok[b][:, 32 * gp:32 * gp + 32])

            late = kb >= 4

            def sG():
                if late:
                    pr3 = bA.re("p (a b) -> p a b", a=16); pi3 = bB.re("p (a b) -> p a b", a=16)
                    tt(hm1[b], pr3, BRb, ALU.mult); tt(hm2[b], pi3, BIb, ALU.mult)
                    tt(prs[b], pi3, BRb, ALU.mult); tt(pis[b], pr3, BIb, ALU.mult)
                else:
                    acopy(prs[b].re("p a b -> p (a b)"), bA)
                    acopy(pis[b].re("p a b -> p (a b)"), bB)
                tt(ckvf[b], kvf[b][:, 0:256], gkv, ALU.mult, eng="pool")
                ts(ckvf[b], ckvf[b], ss1[b], 1.0, ALU.mult, ALU.mult, eng="pool")
                rope(kpef[b].re("p (a b) -> p a b", a=1), kvf[b][:, 256:320].re("p (a b) -> p a b", a=1), rc[b], rs[b],
                     tmp64[b].re("p (a b) -> p a b", a=1), 1, eng="pool")
                vcopy(ckvb[b], ckvf[b], eng="pool")
                vcopy(kpeb[b][:, 0:64], kpef[b], eng="pool")

            def sH():
                bb = bA.cast(BF16)
                for kc in range(2):
                    tr(bb[:, kc * 128:(kc + 1) * 128], ckvb[b][:, kc * 128:(kc + 1) * 128], identb)
                tr(bb[:, 256:384], kpeb[b], identb)
                if late:
                    tt(hm1[b], hm1[b], hm2[b], ALU.subtract, eng="pool")
                    tt(hm2[b], prs[b], pis[b], ALU.add, eng="pool")
                else:
                    tt(hm1[b], prs[b], BRb, ALU.mult, eng="pool"); tt(hm2[b], pis[b], BIb, ALU.mult, eng="pool")
                    tt(hm1[b], hm1[b], hm2[b], ALU.subtract, eng="pool")
                    tt(hm2[b], pis[b], BRb, ALU.mult, eng="pool"); tt(prs[b], prs[b], BIb, ALU.mult, eng="pool")
                    tt(hm2[b], hm2[b], prs[b], ALU.add, eng="pool")

            def sI():
                bb = bA.cast(BF16)
                acopy(ckvT[b].re("p a b -> p (a b)"), bb[:, 0:256])
                acopy(KR[kbuf][0:64, j * 128:(j + 1) * 128], bb[0:64, 256:384])

            def sJ():
                for h in range(4):
                    for kc in range(2):
                        mm(bB[:, h * 128:(h + 1) * 128], W_kv[:, kc, h, 0:128], ckvT[b][:, kc, :], start=(kc == 0), stop=(kc == 1))
                for kc in range(2):
                    mm(bA, ckvT[b][:, kc, :], W_kv[:, kc, :, 128:256], start=(kc == 0), stop=(kc == 1))
                lr_ = lam128[:, 0, :]; li_ = lam128[:, 1, :]
                ld(ckv_o[t * 128:(t + 1) * 128, :], ckvf[b])
                ld(kpe_o[t * 128:(t + 1) * 128, :], kpef[b])
                red(sre[t % 2][:, 0:16], hm1[b], ALU.add)
                red(sre[t % 2][:, 16:32], hm2[b], ALU.add)
                stt(Xown[:, kb, :], xc, ohot[:, j:j + 1], Xown[:, kb, :], ALU.mult, ALU.add)
                tt(xs8[0], xc[:, 0:16], lr_, ALU.mult, eng="pool"); tt(xs8[1], xc[:, 16:32], li_, ALU.mult, eng="pool")
                tt(xs8[2], xc[:, 16:32], lr_, ALU.mult, eng="pool"); tt(xs8[3], xc[:, 0:16], li_, ALU.mult, eng="pool")
                tt(xs8[0], xs8[0], xs8[1], ALU.subtract, eng="pool"); tt(xs8[2], xs8[2], xs8[3], ALU.add, eng="pool")
                tt(xn[:, 0:16], xs8[0], sre[t % 2][:, 0:16], ALU.add, eng="pool")
                tt(xn[:, 16:32], xs8[2], sre[t % 2][:, 16:32], ALU.add, eng="pool")

            def sK():
                acopy(KT[kbuf][:, :, j * 128:(j + 1) * 128], bB.re("p (h n) -> p h n", h=4))
                acopy(Vb[kbuf][:, j, :], bA)

            def pair(p_, c_):
                def f():
                    p_(); c_()
                return f
            return [sA, pair(sB, sC), pair(sD, sE), pair(sF, sG), pair(sH, sI), pair(sJ, sK)]

        def hist_items(kb):
            items = []
            for jp in range(4):
                sa = hist_stages(kb * 8 + 2 * jp, kb % 2, 2 * jp, kb)
                sb_ = hist_stages(kb * 8 + 2 * jp + 1, kb % 2, 2 * jp + 1, kb)
                for a_, b_ in zip(sa, sb_):
                    items.append(a_); items.append(b_)
            return items

        SCp = [dbl[3], dbl[2]]
        ptb = [banks[2], banks[3]]

        def att_A(n, i, h, kbuf):
            sc = SCp[n % 2]
            for c2 in range(2):
                mm(sc[:, c2 * 512:(c2 + 1) * 512], QTn[:, i, h, :], KT[kbuf][:, h, c2 * 512:(c2 + 1) * 512], start=True, stop=False)
                mm(sc[:, c2 * 512:(c2 + 1) * 512], QTr[0:64, i, h, :], KR[kbuf][0:64, c2 * 512:(c2 + 1) * 512], start=False, stop=True)

        def att_B(n, i, h, kbuf, diag):
            u2 = n % 2; u4 = n % NS4
            sc = SCp[u2]
            col = i * 4 + h
            if diag:
                tt(sc, sc, maskd, ALU.add)
            red(mx[u4], sc, ALU.max)
            tt(mnew[u4], mx[u4], m_run[:, col:col + 1], ALU.max)
            tt(dm_[u4], m_run[:, col:col + 1], mnew[u4], ALU.subtract)
            vcopy(m_run[:, col:col + 1], mnew[u4])
            ts(negm[u4], mnew[u4], -1.0, None, ALU.mult)
            act(alp[u4], dm_[u4], AF.Exp)
            act(Pb[u2], sc, AF.Exp, bias=negm[u4], accum=rsum[u4])

        def att_CD(n, i, h, kbuf):
            u2 = n % 2
            pt_b = ptb[u2].cast(BF16)
            for kt in range(8):
                tr(pt_b[:, kt * 128:(kt + 1) * 128], Pb[u2][:, kt * 128:(kt + 1) * 128], identb)
            if n % 3 == 0:
                vcopy(PT[u2].re("p a b -> p (a b)"), pt_b)
            else:
                acopy(PT[u2].re("p a b -> p (a b)"), pt_b)

        def att_EF(n, i, h, kbuf):
            u2 = n % 2; u4 = n % NS4
            ov = ptb[u2][:, 0:128]
            for kt in range(8):
                mm(ov, PT[u2][:, kt, :], Vb[kbuf][:, kt, h * 128:(h + 1) * 128], start=(kt == 0), stop=(kt == 7))
            stt(Oacc[:, i, h, :], Oacc[:, i, h, :], alp[u4], ov, ALU.mult, ALU.add)
            col = i * 4 + h
            stt(l_run[:, col:col + 1], l_run[:, col:col + 1], alp[u4], rsum[u4], ALU.mult, ALU.add)

        for it in hist_items(0):
            it()
        nun = 0
        for kb in range(16):
            kbuf = kb % 2
            units = [(i, h) for i in range(kb, 16) for h in range(4)]
            U = len(units)
            items = hist_items(kb + 1) if kb + 1 < 16 else []
            per = (len(items) + U - 1) // U if items else 0
            ip = 0
            for q_ in range(U + 3):
                if q_ < U:
                    att_A(nun + q_, units[q_][0], units[q_][1], kbuf)
                if 0 <= q_ - 1 < U:
                    att_B(nun + q_ - 1, units[q_ - 1][0], units[q_ - 1][1], kbuf, units[q_ - 1][0] == kb)
                if 0 <= q_ - 2 < U:
                    att_CD(nun + q_ - 2, units[q_ - 2][0], units[q_ - 2][1], kbuf)
                if 0 <= q_ - 3 < U:
                    att_EF(nun + q_ - 3, units[q_ - 3][0], units[q_ - 3][1], kbuf)
                for _ in range(per):
                    if ip < len(items):
                        items[ip](); ip += 1
            while ip < len(items):
                items[ip](); ip += 1
            nun += U

        ld(ssmp_o, Xpp[:, NTP % 2, :])

        barrier()
        bump[0] = mark_after_q

        W_kv = alloc([128, 2, 4, 256], BF16, "W_kv2")
        ldw(W_kv.re("p k h e -> p k (h e)"), w_kv_d.re("(kt p) n -> p kt n", p=128))
        gkv = alloc([128, 256], F32, "gkv2"); ld(gkv, gkv_d)
        xt_s = alloc([128, D], F32, "sxt"); xT_s = alloc([128, 8, 128], BF16, "sxT")
        rc_s = alloc([128, 64], F32); rs_s = alloc([128, 64], F32)
        ckvf_s = alloc([128, 256], F32); kpef_s = alloc([128, 64], F32); tmp64_s = alloc([128, 64], F32)
        sq = alloc([128, 512], F32, "ssq"); ss_s = alloc([128, 1], F32)
        ckvb_s = alloc([128, 256], BF16); kpeb_s = alloc([128, 128], BF16)
        ckvTn = alloc([128, 2, 128], BF16, "ckvTn"); kpeTn = alloc([128, 128], BF16, "kpeTn")
        KTn = alloc([128, 4, 128], BF16, "KTn")
        cat = [alloc([128, 256], F32, "cat%d" % i) for i in range(2)]
        catb = [alloc([128, 256], BF16) for _ in range(2)]
        cpt = [alloc([128, 64], F32, "cpt%d" % i) for i in range(2)]
        cptb = [alloc([128, 128], BF16) for _ in range(2)]
        ckvTc = alloc([128, 2, 1024], BF16, "ckvTc"); kpeTc = alloc([128, 1024], BF16, "kpeTc")
        KTc = alloc([128, 4, 1024], BF16, "KTc"); Vc = alloc([128, 8, 512], BF16, "Vc"); Vn = alloc([128, 512], BF16, "Vn")
        scs = alloc([128, 1056], F32, "scs")
        Ps = alloc([128, 1152], BF16, "Ps"); PTs = alloc([128, 9, 32], BF16, "PTs")
        mxs = alloc([128, 1], F32); negs = alloc([128, 1], F32); sums = alloc([128, 1], F32)
        osb = alloc([128, 512], F32, "osb")
        isam = 16
        load_xT(xo[isam * 128:(isam + 1) * 128, :], xt_s, xT_s, banks[0])
        ld(rc_s, ropec_o[isam * 128:(isam + 1) * 128, :]); ld(rs_s, ropes_o[isam * 128:(isam + 1) * 128, :])
        for k in range(8):
            mm(banks[2][:, 0:320], xT_s[:, k, :], W_in_kv[:, k, :], start=(k == 0), stop=(k == 7))
        rmsnorm(ckvf_s, banks[2][:, 0:256], 256, gkv, sq[:, 0:256], ss_s)
        ld(ckvs_o, ckvf_s)
        rope(kpef_s.re("p (a b) -> p a b", a=1), banks[2][:, 256:320].re("p (a b) -> p a b", a=1), rc_s, rs_s, tmp64_s.re("p (a b) -> p a b", a=1), 1)
        ld(kpes_o, kpef_s)
        vcopy(ckvb_s, ckvf_s)
        mset(kpeb_s, 0.0)
        vcopy(kpeb_s[:, 0:64], kpef_s)
        bb = banks[3].cast(BF16)
        for kc in range(2):
            tr(bb[:, kc * 128:(kc + 1) * 128], ckvb_s[:, kc * 128:(kc + 1) * 128], identb)
        tr(bb[:, 256:384], kpeb_s, identb)
        acopy(ckvTn.re("p a b -> p (a b)"), bb[:, 0:256])
        acopy(kpeTn[0:64, :], bb[0:64, 256:384])
        for h in range(4):
            for kc in range(2):
                mm(banks[3][:, h * 128:(h + 1) * 128], W_kv[:, kc, h, 0:128], ckvTn[:, kc, :], start=(kc == 0), stop=(kc == 1))
        acopy(KTn.re("p a b -> p (a b)"), banks[3])
        for kp in range(2):
            mset(cptb[kp], 0.0)
        for s_ in range(4):
            for kt in range(8):
                b = kt % 2
                r0 = s_ * 1024 + kt * 128
                ld(cat[b], cckv_d[r0:r0 + 128, :]); ld(cpt[b], ckpe_d[r0:r0 + 128, :])
                vcopy(catb[b], cat[b], eng="pool"); vcopy(cptb[b][:, 0:64], cpt[b], eng="pool")
                bb = banks[1].cast(BF16)
                for kc in range(2):
                    tr(bb[:, kc * 128:(kc + 1) * 128], catb[b][:, kc * 128:(kc + 1) * 128], identb)
                tr(bb[:, 256:384], cptb[b], identb)
                acopy(ckvTc[:, :, kt * 128:(kt + 1) * 128], bb[:, 0:256].re("p (a b) -> p a b", a=2))
                acopy(kpeTc[0:64, kt * 128:(kt + 1) * 128], bb[0:64, 256:384])
            for h in range(4):
                for n in range(2):
                    for kc in range(2):
                        mm(banks[2], W_kv[:, kc, h, 0:128], ckvTc[:, kc, n * 512:(n + 1) * 512], start=(kc == 0), stop=(kc == 1))
                    acopy(KTc[:, h, n * 512:(n + 1) * 512], banks[2])
            for kt in range(8):
                for kc in range(2):
                    mm(banks[3], ckvTc[:, kc, kt * 128:(kt + 1) * 128], W_kv[:, kc, :, 128:256], start=(kc == 0), stop=(kc == 1))
                vcopy(Vc[:, kt, :], banks[3])
            for kc in range(2):
                mm(banks[3][0:32, :], ckvTn[:, kc, s_ * 32:(s_ + 1) * 32], W_kv[:, kc, :, 128:256], start=(kc == 0), stop=(kc == 1))
            vcopy(Vn[0:32, :], banks[3][0:32, :])
            qs = slice(s_ * 32, (s_ + 1) * 32)
            for h in range(4):
                for n in range(2):
                    mm(SC[0:32, n * 512:(n + 1) * 512], QTn[:, isam, h, qs], KTc[:, h, n * 512:(n + 1) * 512], start=True, stop=False)
                    mm(SC[0:32, n * 512:(n + 1) * 512], QTr[0:64, isam, h, qs], kpeTc[0:64, n * 512:(n + 1) * 512], start=False, stop=True)
                mm(banks[4][0:32, 0:32], QTn[:, isam, h, qs], KTn[:, h, qs], start=True, stop=False)
                mm(banks[4][0:32, 0:32], QTr[0:64, isam, h, qs], kpeTn[0:64, qs], start=False, stop=True)
                acopy(scs[0:32, 0:1024], SC[0:32, :])
                acopy(scs[0:32, 1024:1056], banks[4][0:32, 0:32])
                red(mxs[0:32, :], scs[0:32, :], ALU.max)
                ts(negs[0:32, :], mxs[0:32, :], -1.0, None, ALU.mult)
                mset(Ps[0:32, 1024:1152], 0.0)
                act(Ps[0:32, 0:1056], scs[0:32, :], AF.Exp, bias=negs[0:32, :], accum=sums[0:32, :])
                pt_b = banks[5].cast(BF16)
                for kt in range(9):
                    tr(pt_b[:, kt * 32:(kt + 1) * 32], Ps[0:32, kt * 128:(kt + 1) * 128], identb[0:32, 0:32])
                vcopy(PTs.re("p a b -> p (a b)"), pt_b[:, 0:288])
                ov = banks[4][0:32, 128:256]
                for kt in range(8):
                    mm(ov, PTs[:, kt, :], Vc[:, kt, h * 128:(h + 1) * 128], start=(kt == 0), stop=False)
                mm(ov, PTs[0:32, 8, :], Vn[0:32, h * 128:(h + 1) * 128], start=False, stop=True)
                recip(sums[0:32, :], sums[0:32, :])
                ts(osb[0:32, h * 128:(h + 1) * 128], ov, sums[0:32, :], None, ALU.mult)
            ld(Osam[s_ * 32:(s_ + 1) * 32, :], osb[0:32, :])

        barrier()
        bump[0] = mark_after_mixer_state

        W_glu = alloc([128, 4, D], BF16, "W_glu"); W_o = alloc([128, 8, D], BF16, "W_o")
        ldw(W_glu, w_glu_d.re("(kt p) n -> p kt n", p=128)); ldw(W_o, w_o_d.re("(kt p) n -> p kt n", p=128))
        gos = alloc([128, 512], F32, "gos"); gom = alloc([128, 512], F32, "gom"); ld(gos, gos_d); ld(gom, gom_d)
        lng = alloc([128, D], F32, "lng"); lnb = alloc([128, D], F32, "lnb")
        ld(lng, lng_d[:, 0:D]); ld(lnb, lnb_d[:, 0:D])
        btab2 = Buf("tab2")
        for nm_ in ("S_t", "C_t", "R_t"):
            v_ = alloc([128, 16, 128], F32, nm_ + "2"); v_.b = btab2
            ld(v_.re("p a b -> p (a b)"), tab_d[nm_]); ssmx[nm_] = v_
        for nm_ in ("WBre", "WBim"):
            v_ = alloc([128, 16, 128], BF16, nm_ + "2")
            ld(v_.re("p a b -> p (a b)"), tab_d[nm_]); ssmx[nm_] = v_
        for nm_, src_ in (("Cre", cre_d), ("Cimn", cimn_d), ("Dblk", dblk_d)):
            v_ = alloc([128, 16, 32], BF16, nm_)
            ldw(v_.re("p a b -> p (a b)"), src_); ssmx[nm_] = v_
        ssmx["m4"] = [[alloc([128, 256], F32) for _ in range(4)] for _ in range(2)]
        ssmx["wri"] = [[alloc([128, 256], F32) for _ in range(2)] for _ in range(2)]
        ssmx["zri"] = [[alloc([128, 2, 128], F32) for _ in range(2)] for _ in range(2)]
        ssmx["xs4"] = [alloc([128, 16], F32) for _ in range(4)]
        ssmx["zl_all"] = [alloc([128, 4, 16], F32) for _ in range(2)]
        ssmx["dmd"] = [[alloc([128, 256], F32) for _ in range(4)]] * 2
        ssmx["xrb"] = [alloc([128, 2, 128], BF16) for _ in range(2)]
        ssmx["xib"] = [alloc([128, 2, 128], BF16) for _ in range(2)]
        S0 = alloc([128, 2, 4, 16], F32, "S0"); ld(S0.re("p a b c -> p (a b c)"), s0_d)
        Sfin = alloc([128, 2, 4, 16], F32, "Sfin")
        xt = [alloc([128, D], F32, "axt%d" % i) for i in range(NB)]
        xT = [alloc([128, 8, 128], BF16, "axT%d" % i) for i in range(NB)]
        uT = [alloc([128, 4, 128], BF16, "auT%d" % i) for i in range(NB)]
        ysq = alloc([128, 512], F32, "ysq"); yt = alloc([128, 512], F32, "yt"); ysg = alloc([128, 512], F32, "ysg")
        glb = alloc([128, 512], BF16, "glb"); gT = alloc([128, 4, 128], BF16, "gT")
        sg2 = ysq; osf = yt
        mixb = alloc([128, D], BF16, "mixb"); mixT = alloc([128, 8, 128], BF16, "mixT")
        rl = alloc([128, 4], F32, "rl"); omf = alloc([128, 4, 128], F32, "omf")
        ss_a = alloc([128, 1], F32); sq = ysg
        rres = alloc([128, D], F32, "rres"); hout = alloc([128, D], F32, "hout")
        stats = alloc([128, 12], F32); mvv = alloc([128, 2], F32)

        def pre_stages(i):
            b = i % NB
            yb = banks[3] if i % 2 == 0 else banks[0]

            def p0():
                load_xT(xo[i * 128:(i + 1) * 128, :], xt[b], xT[b], banks[1])
                for kt in range(4):
                    for k in range(8):
                        mm(banks[1][:, kt * 128:(kt + 1) * 128], W_in_u[:, k, kt * 128:(kt + 1) * 128], xT[b][:, k, :], start=(k == 0), stop=(k == 7))
                acopy(uT[b].re("p a b -> p (a b)"), banks[1])

            if i < 16:
                rr_ = ssm_rounds(uT[b], (banks[4], banks[5]), lambda gp, s_, part, i=i: Xown[:, i, part * 16 + gp:part * 16 + gp + 1], 1, ybank=yb)
            else:
                rr_ = ssm_rounds(uT[b], (banks[4], banks[5]), lambda gp, s_, part: S0[:, part, s_, gp:gp + 1], 4, ybank=yb,
                                 last_out=lambda s_: (Sfin[:, 0, s_, :], Sfin[:, 1, s_, :]))
                rr_.append(lambda: ld(ssms_o, Sfin.re("p a b c -> p (a b c)")))
            return [p0] + rr_

        def post_stages(i):
            b = i % NB
            yb = banks[3] if i % 2 == 0 else banks[0]
            xt_ = xt[b]

            def q0():
                act(ysq, yb, AF.Square)
                ts(yt, ysq, 0.044715, 1.0, ALU.mult, ALU.add)
                tt(yt, yt, yb, ALU.mult)

            def q1():
                act(ysg, yt, AF.Sigmoid, scale=1.5957691216057308)
                tt(glb, ysg, yb, ALU.mult)

            def q2():
                transpose_to(gT, glb, 4, banks[2], eng="act")

            def q3():
                for nb in range(2):
                    for k in range(4):
                        mm(SC[:, nb * 512:(nb + 1) * 512], gT[:, k, :], W_glu[:, k, nb * 512:(nb + 1) * 512], start=(k == 0), stop=(k == 3))

            def q4():
                act(sg2, SC[:, 512:1024], AF.Sigmoid)
                tt(osf, sg2, SC[:, 0:512], ALU.mult)

            def q5():
                rmsnorm(mixb[:, 0:512], osf, 512, gos, sq, ss_a)

            def q6():
                if i < 16:
                    recip(rl, l_run[:, i * 4:(i + 1) * 4])
                    tt(omf, Oacc[:, i, :, :], rl.un(2).bc([128, 4, 128]), ALU.mult)
                    rmsnorm(mixb[:, 512:1024], omf.re("p a b -> p (a b)"), 512, gom, sq, ss_a)
                else:
                    rmsnorm(mixb[:, 512:1024], Osam, 512, gom, sq, ss_a)

            def q7():
                for half in range(2):
                    transpose_to(mixT[:, half * 4:(half + 1) * 4, :], mixb[:, half * 512:(half + 1) * 512], 4, banks[2], eng=("act" if half == 0 else "dve"))

            def q8():
                for nb in range(2):
                    for k in range(8):
                        mm(SC[:, nb * 512:(nb + 1) * 512], mixT[:, k, :], W_o[:, k, nb * 512:(nb + 1) * 512], start=(k == 0), stop=(k == 7))

            def q9():
                stt(rres, xt_, ALPHA, SC, ALU.mult, ALU.add)
                layernorm(hout, rres, lng, lnb, stats, mvv)
                ld(h1_d[i * 128:(i + 1) * 128, :], hout)

            return [q0, q1, q2, q3, q4, q5, q6, q7, q8, q9]

        prev_post = []
        for i in range(NOWN + 1):
            pre = pre_stages(i) if i < NOWN else []
            n_ = max(len(pre), len(prev_post))
            for k_ in range(n_):
                if k_ < len(pre):
                    pre[k_]()
                if k_ < len(prev_post):
                    prev_post[k_]()
            prev_post = post_stages(i) if i < NOWN else []

        barrier()
        bump[0] = mark_after_mem

        W_xq = alloc([128, 8, D], BF16, "W_xq"); W_xo = alloc([128, 8, D], BF16, "W_xo")
        ldw(W_xq, w_xq_d.re("(kt p) n -> p kt n", p=128)); ldw(W_xo, w_xo_d.re("(kt p) n -> p kt n", p=128))
        lng = alloc([128, D], F32, "lng2"); lnb = alloc([128, D], F32, "lnb2")
        ld(lng, lng_d[:, D:2 * D]); ld(lnb, lnb_d[:, D:2 * D])
        hin = [alloc([128, D], F32, "bh%d" % i) for i in range(NB)]
        hb2 = [alloc([128, D], BF16, "bhb%d" % i) for i in range(2)]; hT2 = [alloc([128, 8, 128], BF16, "bhT%d" % i) for i in range(2)]
        qxb2 = [alloc([128, D], BF16, "qxb%d" % i) for i in range(2)]; qxT2 = [alloc([128, 8, 128], BF16, "qxT%d" % i) for i in range(2)]
        mx42 = [alloc([128, 4], F32) for _ in range(2)]; neg42 = [alloc([128, 4], F32) for _ in range(2)]; sum42 = [alloc([128, 4], F32) for _ in range(2)]
        Px2 = [alloc([128, 4, 256], BF16, "Px%d" % i) for i in range(2)]; PxT2 = [alloc([128, 8, 128], BF16, "PxT%d" % i) for i in range(2)]
        oxb2 = [alloc([128, D], BF16, "oxb%d" % i) for i in range(2)]; oxT2 = [alloc([128, 8, 128], BF16, "oxT%d" % i) for i in range(2)]
        rres2 = [alloc([128, D], F32, "brres%d" % i) for i in range(2)]; hout2 = [alloc([128, D], F32, "bhout%d" % i) for i in range(2)]
        stats2 = [alloc([128, 12], F32) for _ in range(2)]; mvv2 = [alloc([128, 2], F32) for _ in range(2)]
        hb_ = hb2[0]; hT = hT2[0]; qxb = qxb2[0]; qxT = qxT2[0]; mx4 = mx42[0]; neg4 = neg42[0]; sum4 = sum42[0]
        Px = Px2[0]; PxT = PxT2[0]; oxb = oxb2[0]; oxT = oxT2[0]; rres = rres2[0]; hout = hout2[0]; stats = stats2[0]; mvv = mvv2[0]
        cmk = [alloc([128, D], F32, "cmk%d" % i) for i in range(2)]
        cmkb = [alloc([128, D], BF16, "cmkb%d" % i) for i in range(1)]
        mkTs4 = [alloc([128, 4, 2, 256], BF16, "mkTs%d" % i) for i in range(4)]; mvs4 = [alloc([128, 2, D], BF16, "mvs%d" % i) for i in range(4)]
        scx = alloc([128, 8], F32, "scx"); oxs = alloc([128, D], BF16, "oxs")

        def xa_stages(i, p):
            A_ = banks[p]; C_ = dbl[1 + p]
            Ab = A_.cast(BF16)

            def tr8(src):
                for k in range(8):
                    tr(Ab[:, k * 128:(k + 1) * 128], src[:, k * 128:(k + 1) * 128], identb)

            def ev8(dst):
                acopy(dst[:, 0:4, :], Ab[:, 0:512].re("p (k n) -> p k n", k=4))
                vcopy(dst[:, 4:8, :], Ab[:, 512:1024].re("p (k n) -> p k n", k=4))

            def t0():
                ld(hin[p], h1_d[i * 128:(i + 1) * 128, :])
                vcopy(hb2[p], hin[p], eng="pool")

            def t3():
                for nb in range(2):
                    for k in range(8):
                        mm(C_[:, nb * 512:(nb + 1) * 512], hT2[p][:, k, :], W_xq[:, k, nb * 512:(nb + 1) * 512], start=(k == 0), stop=(k == 7))

            def t4():
                for nb in range(2):
                    P.op("act", lambda e, nb=nb: e.mul(out=qxb2[p].ap[:, nb * 512:(nb + 1) * 512], in_=C_.ap[:, nb * 512:(nb + 1) * 512], mul=X_SCALE),
                         reads=[C_.b], writes=[qxb2[p].b])

            def t7():
                for h in range(4):
                    for et in range(2):
                        mm(C_[:, h * 256:(h + 1) * 256], qxT2[p][:, h * 2 + et, :], mkT[:, h, et, :], start=(et == 0), stop=(et == 1))

            def t8():
                red(mx42[p], C_.re("p (h m) -> p h m", h=4), ALU.max)
                ts(neg42[p], mx42[p], -1.0, None, ALU.mult)
                for h in range(4):
                    act(Px2[p][:, h, :], C_[:, h * 256:(h + 1) * 256], AF.Exp, bias=neg42[p][:, h:h + 1], accum=sum42[p][:, h:h + 1])

            def t11():
                for h in range(4):
                    for mt in range(2):
                        mm(C_[:, h * 256:(h + 1) * 256], PxT2[p][:, h * 2 + mt, :], mvb[:, mt, h * 256:(h + 1) * 256], start=(mt == 0), stop=(mt == 1))

            def t12():
                recip(sum42[p], sum42[p])
                tt(oxb2[p].re("p (h e) -> p h e", h=4), C_.re("p (h e) -> p h e", h=4), sum42[p].un(2).bc([128, 4, 256]), ALU.mult)

            def t15():
                for nb in range(2):
                    for k in range(8):
                        mm(C_[:, nb * 512:(nb + 1) * 512], oxT2[p][:, k, :], W_xo[:, k, nb * 512:(nb + 1) * 512], start=(k == 0), stop=(k == 7))

            def t16():
                stt(rres2[p], hin[p], ALPHA, C_, ALU.mult, ALU.add)
                layernorm(hout2[p], rres2[p], lng, lnb, stats2[p], mvv2[p])
                ld(h2_d[i * 128:(i + 1) * 128, :], hout2[p])

            return [t0, lambda: tr8(hb2[p]), lambda: ev8(hT2[p]), t3, t4, lambda: tr8(qxb2[p]), lambda: ev8(qxT2[p]), t7, t8,
                    lambda: tr8(Px2[p].re("p a b -> p (a b)")), lambda: ev8(PxT2[p]), t11, t12, lambda: tr8(oxb2[p]), lambda: ev8(oxT2[p]), t15, t16]

        def prep_items():
            items = []
            for s_ in range(4):
                for mt in range(2):
                    r0 = s_ * 256 + mt * 128

                    def l0(r0=r0):
                        ld(cmk[0], cmk_d[r0:r0 + 128, :]); ld(cmk[1], cmv_d[r0:r0 + 128, :])

                    def l1(s_=s_, mt=mt):
                        vcopy(cmkb[0], cmk[0], eng="pool")
                        vcopy(mvs4[s_][:, mt, :], cmk[1], eng="pool")

                    def l2(s_=s_, mt=mt):
                        for half in range(2):
                            bb = banks[6 + half].cast(BF16)
                            for k in range(4):
                                kk = half * 4 + k
                                tr(bb[:, k * 128:(k + 1) * 128], cmkb[0][:, kk * 128:(kk + 1) * 128], identb)

                    def l3(s_=s_, mt=mt):
                        for half in range(2):
                            bb = banks[6 + half].cast(BF16)
                            acopy(mkTs4[s_][:, half * 2:half * 2 + 2, :, mt * 128:(mt + 1) * 128].re("p h e m -> p (h e) m"), bb[:, 0:512].re("p (k m) -> p k m", k=4))

                    items += [l0, l1, l2, l3]
            return items

        pitems = prep_items()
        pi_ = 0
        slot = 0
        for ip in range(8):
            sa = xa_stages(2 * ip, 0); sb_ = xa_stages(2 * ip + 1, 1)
            for a_, b_ in zip(sa, sb_):
                a_(); b_()
                slot += 1
                if slot % 4 == 0 and pi_ < len(pitems):
                    pitems[pi_](); pi_ += 1
        while pi_ < len(pitems):
            pitems[pi_](); pi_ += 1

        for i in range(16, NOWN):
            b = i % NB
            ld(hin[b], h1_d[i * 128:(i + 1) * 128, :])
            vcopy(hb_, hin[b], eng="pool")
            for half in range(2):
                transpose_to(hT[:, half * 4:(half + 1) * 4, :], hb_[:, half * 512:(half + 1) * 512], 4, banks[0], eng=("act" if half == 0 else "dve"))
            for nb in range(2):
                for k in range(8):
                    mm(banks[1 + nb], hT[:, k, :], W_xq[:, k, nb * 512:(nb + 1) * 512], start=(k == 0), stop=(k == 7))
                P.op("act", lambda e, nb=nb: e.mul(out=qxb.ap[:, nb * 512:(nb + 1) * 512], in_=banks[1 + nb].ap, mul=X_SCALE), reads=[banks[1 + nb].b], writes=[qxb.b])
            for half in range(2):
                transpose_to(qxT[:, half * 4:(half + 1) * 4, :], qxb[:, half * 512:(half + 1) * 512], 4, banks[3], eng=("act" if half == 0 else "dve"))
            if i < 16:
                for h in range(4):
                    for et in range(2):
                        mm(SC[:, h * 256:(h + 1) * 256], qxT[:, h * 2 + et, :], mkT[:, h, et, :], start=(et == 0), stop=(et == 1))
                red(mx4, SC.re("p (h m) -> p h m", h=4), ALU.max)
                ts(neg4, mx4, -1.0, None, ALU.mult)
                for h in range(4):
                    act(Px[:, h, :], SC[:, h * 256:(h + 1) * 256], AF.Exp, bias=neg4[:, h:h + 1], accum=sum4[:, h:h + 1])
                for half in range(2):
                    transpose_to(PxT[:, half * 4:(half + 1) * 4, :], Px.re("p a b -> p (a b)")[:, half * 512:(half + 1) * 512], 4, banks[4], eng=("act" if half == 0 else "dve"))
                for h in range(4):
                    for mt in range(2):
                        mm(SC[:, h * 256:(h + 1) * 256], PxT[:, h * 2 + mt, :], mvb[:, mt, h * 256:(h + 1) * 256], start=(mt == 0), stop=(mt == 1))
                recip(sum4, sum4)
                tt(oxb.re("p (h e) -> p h e", h=4), SC.re("p (h e) -> p h e", h=4), sum4.un(2).bc([128, 4, 256]), ALU.mult)
            else:
                for s_ in range(4):
                    qs = slice(s_ * 32, (s_ + 1) * 32)
                    mkTs = mkTs4[s_]; mvs = mvs4[s_]
                    for h in range(4):
                        for et in range(2):
                            mm(SC[0:32, h * 256:(h + 1) * 256], qxT[:, h * 2 + et, qs], mkTs[:, h, et, :], start=(et == 0), stop=(et == 1))
                    red(mx4[0:32, :], SC[0:32, :].re("p (h m) -> p h m", h=4), ALU.max)
                    ts(neg4[0:32, :], mx4[0:32, :], -1.0, None, ALU.mult)
                    for h in range(4):
                        act(Px[0:32, h, :], SC[0:32, h * 256:(h + 1) * 256], AF.Exp, bias=neg4[0:32, h:h + 1], accum=sum4[0:32, h:h + 1])
                    bb = banks[4].cast(BF16)
                    for k in range(8):
                        tr(bb[:, k * 32:(k + 1) * 32], Px.re("p a b -> p (a b)")[0:32, k * 128:(k + 1) * 128], identb[0:32, 0:32])
                    vcopy(PxT[:, :, 0:32], bb[:, 0:256].re("p (k q) -> p k q", k=8))
                    for h in range(4):
                        for mt in range(2):
                            mm(SC[0:32, h * 256:(h + 1) * 256], PxT[:, h * 2 + mt, 0:32], mvs[:, mt, h * 256:(h + 1) * 256], start=(mt == 0), stop=(mt == 1))
                    recip(sum4[0:32, :], sum4[0:32, :])
                    tt(oxs[0:32, :].re("p (h e) -> p h e", h=4), SC[0:32, :].re("p (h e) -> p h e", h=4), sum4[0:32, :].un(2).bc([32, 4, 256]), ALU.mult)
                    ld(oxb[s_ * 32:(s_ + 1) * 32, :], oxs[0:32, :])
            for half in range(2):
                transpose_to(oxT[:, half * 4:(half + 1) * 4, :], oxb[:, half * 512:(half + 1) * 512], 4, banks[5], eng=("act" if half == 0 else "dve"))
            for nb in range(2):
                for k in range(8):
                    mm(banks[1 + nb], oxT[:, k, :], W_xo[:, k, nb * 512:(nb + 1) * 512], start=(k == 0), stop=(k == 7))
            for nb in range(2):
                stt(rres[:, nb * 512:(nb + 1) * 512], hin[b][:, nb * 512:(nb + 1) * 512], ALPHA, banks[1 + nb], ALU.mult, ALU.add)
            layernorm(hout, rres, lng, lnb, stats, mvv)
            ld(h2_d[i * 128:(i + 1) * 128, :], hout)

        barrier()
        bump[0] = mark_after_mem

        W1 = alloc([128, 8, 4096], BF16, "W1"); W2 = alloc([128, 32, D], BF16, "W2")
        for c in range(2):
            ldw(W1[:, :, c * 2048:(c + 1) * 2048], w_ff1_d.re("(kt p) n -> p kt n", p=128)[:, :, c * 2048:(c + 1) * 2048])
        for c in range(4):
            ldw(W2[:, c * 8:(c + 1) * 8, :], w_ff2_d.re("(kt p) n -> p kt n", p=128)[:, c * 8:(c + 1) * 8, :])
        lng = alloc([128, D], F32, "lng3"); lnb = alloc([128, D], F32, "lnb3")
        ld(lng, lng_d[:, 2 * D:3 * D]); ld(lnb, lnb_d[:, 2 * D:3 * D])
        hin = alloc([128, D], F32, "ch")
        hb_ = alloc([128, D], BF16, "chb"); hT = alloc([128, 8, 512], BF16, "chT")
        zr = [alloc([128, 512], F32, "zr%d" % i) for i in range(2)]; zT = alloc([128, 32, 512], BF16, "zT")
        rres = alloc([128, D], F32, "crres"); hout = alloc([128, D], F32, "chout")
        stats = alloc([128, 12], F32); mvv = alloc([128, 2], F32)
        groups = [list(range(g_ * 4, g_ * 4 + 4)) for g_ in range(4)] + [[16]]
        for grp in groups:
            nt = len(grp)
            for q_, i in enumerate(grp):
                ld(hin, h2_d[i * 128:(i + 1) * 128, :])
                vcopy(hb_, hin, eng="pool")
                for half in range(2):
                    transpose_to(hT[:, half * 4:(half + 1) * 4, q_ * 128:(q_ + 1) * 128], hb_[:, half * 512:(half + 1) * 512], 4, banks[0], eng=("act" if half == 0 else "dve"))
            W_ = nt * 128
            for f in range(32):
                bk = banks[1 + f % 2]
                for k in range(8):
                    mm(bk[:, 0:W_], W1[:, k, f * 128:(f + 1) * 128], hT[:, k, 0:W_], start=(k == 0), stop=(k == 7))
                act(zr[f % 2][:, 0:W_], bk[:, 0:W_], AF.Relu)
                tt(zT[:, f, 0:W_], zr[f % 2][:, 0:W_], zr[f % 2][:, 0:W_], ALU.mult)
            for q_, i in enumerate(grp):
                for nb in range(2):
                    for f in range(32):
                        mm(SC[:, nb * 512:(nb + 1) * 512], zT[:, f, q_ * 128:(q_ + 1) * 128], W2[:, f, nb * 512:(nb + 1) * 512], start=(f == 0), stop=(f == 31))
                ld(hin, h2_d[i * 128:(i + 1) * 128, :])
                stt(rres, hin, ALPHA, SC, ALU.mult, ALU.add)
                layernorm(hout, rres, lng, lnb, stats, mvv)
                ld(y_o[i * 128:(i + 1) * 128, :], hout)

        P.emit(st)
    return nc


def _lay_gp(a):
    sh = a.shape[2:]
    n = len(sh)
    return np.ascontiguousarray(a.reshape(16, 2, 64, *sh).transpose(1, 2, 0, *range(3, 3 + n)).reshape(128, 16, *sh))


def _bc(v, n=128):
    return np.ascontiguousarray(np.broadcast_to(np.asarray(v, np.float32).reshape(1, -1), (n, np.asarray(v).size)))


def _rope_tables(pos):
    inv = (10000.0 ** (-np.arange(32, dtype=np.float32) / 32)).astype(np.float32)
    ang = pos.astype(np.float32)[:, None] * inv[None, :]
    c = np.cos(ang).astype(np.float32)
    s = np.sin(ang).astype(np.float32)
    return np.concatenate([c, c], 1), np.concatenate([-s, s], 1)


_NC_CACHE = {}


def kernel(x_prompt, x_sample, mem_prompt, cache_mla_ckv, cache_mla_kpe, state_ssm_re, state_ssm_im,
           cache_mem_k, cache_mem_v, w_in, g_q, w_q_up, g_kv, w_kv_up, a_re, a_im, b_re, b_im, c_re, c_im,
           d_skip, log_dt, w_glu, g_out_ssm, g_out_mla, w_o, w_xq, w_xk, w_xv, w_xo, w_ff1, w_ff2, ln_g, ln_b):
    f = lambda a: np.ascontiguousarray(np.asarray(a, dtype=np.float32))
    x_prompt = f(x_prompt); x_sample = f(x_sample)
    xp = x_prompt[0]
    cre = np.zeros((128, 16, 32), np.float32); cimn = np.zeros((128, 16, 32), np.float32)
    cr = f(c_re)[0].reshape(16, 2, 16, 64); ci = f(c_im)[0].reshape(16, 2, 16, 64)
    for g2 in range(2):
        cre[g2 * 64:(g2 + 1) * 64, :, g2 * 16:(g2 + 1) * 16] = cr[:, g2].transpose(2, 0, 1)
        cimn[g2 * 64:(g2 + 1) * 64, :, g2 * 16:(g2 + 1) * 16] = -ci[:, g2].transpose(2, 0, 1)
    dblk = np.zeros((128, 16, 32), np.float32)
    dd = f(d_skip)[0].reshape(512)
    for gp in range(16):
        for c in range(32):
            ch = gp * 32 + c
            dblk[ch % 128, gp, c] = dd[ch]
    pos_p = np.arange(16384)
    rcp, rsp = _rope_tables(pos_p)
    common = {
        "xp": xp, "mem": f(mem_prompt)[0], "ident": np.eye(128, dtype=np.float32),
        "w_in": f(w_in)[0], "w_q": f(w_q_up)[0].reshape(384, 768), "w_kv": f(w_kv_up)[0].reshape(256, 1024),
        "w_glu": f(w_glu)[0], "w_o": f(w_o)[0], "w_xq": f(w_xq)[0].reshape(D, D), "w_xk": f(w_xk)[0].reshape(D, D),
        "w_xv": f(w_xv)[0].reshape(D, D), "w_xo": f(w_xo)[0].reshape(D, D), "w_ff1": f(w_ff1)[0], "w_ff2": f(w_ff2)[0],
        "gq": _bc(f(g_q)[0]), "gkv": _bc(f(g_kv)[0]), "gos": _bc(f(g_out_ssm)[0]), "gom": _bc(f(g_out_mla)[0]),
        "lng": _bc(f(ln_g)[0].reshape(-1)), "lnb": _bc(f(ln_b)[0].reshape(-1)),
        "ropec_p": rcp, "ropes_p": rsp,
        "ar": _lay_gp(f(a_re)[0]), "ai": _lay_gp(f(a_im)[0]),
        "ldt": _lay_gp(np.ascontiguousarray(np.broadcast_to(f(log_dt)[0][:, None], (32, 64)))),
        "bre": _lay_gp(f(b_re)[0]).reshape(128, 256), "bim": _lay_gp(f(b_im)[0]).reshape(128, 256),
        "jj": _bc(np.arange(1, 129, dtype=np.float32)), "jj2": _bc(127.0 - np.arange(128, dtype=np.float32)),
        "cre": cre.reshape(128, 512), "cimn": cimn.reshape(128, 512), "dblk": dblk.reshape(128, 512),
    }
    qi = np.arange(128)[:, None]
    in_maps = []
    for c in range(NCORES):
        tiles = [8 * i + c for i in range(16)]
        xo = np.concatenate([xp[t * 128:(t + 1) * 128] for t in tiles] + [x_sample[4 * c:4 * c + 4].reshape(128, D)], 0)
        pos_o = np.concatenate([np.arange(t * 128, (t + 1) * 128) for t in tiles] + [np.tile(1024 + np.arange(32), 4)])
        rco, rso = _rope_tables(pos_o)
        kj = np.arange(1024)[None, :]
        vis = ((kj // 128) < c) | (((kj // 128) == c) & (((kj % 128) // 64) <= (qi // 64)))
        maskd = np.where(vis, 0.0, NEG).astype(np.float32)
        onehot = np.zeros((128, 8), np.float32); onehot[:, c] = 1.0
        s0 = np.stack([_lay_gp(f(state_ssm_re)[0, 4 * c + s]) for s in range(4)], 1)
        s0i = np.stack([_lay_gp(f(state_ssm_im)[0, 4 * c + s]) for s in range(4)], 1)
        m = dict(common)
        m.update({
            "xo": np.ascontiguousarray(xo), "ropec_o": rco, "ropes_o": rso, "maskd": maskd, "onehot": onehot,
            "s0": np.ascontiguousarray(np.stack([s0, s0i], 1).reshape(128, 128)),
            "cckv": f(cache_mla_ckv)[0, 4 * c:4 * c + 4].reshape(4096, 256),
            "ckpe": f(cache_mla_kpe)[0, 4 * c:4 * c + 4].reshape(4096, 64),
            "cmk": f(cache_mem_k)[0, 4 * c:4 * c + 4].reshape(1024, D),
            "cmv": f(cache_mem_v)[0, 4 * c:4 * c + 4].reshape(1024, D),
        })
        in_maps.append(m)
    if "nc" not in _NC_CACHE:
        _NC_CACHE["nc"] = build_nc()
    res = run_bass_kernel_spmd(_NC_CACHE["nc"], in_maps, core_ids=list(range(NCORES)))
    R = res.results
    y_p = np.zeros((1, 16384, D), np.float32); y_s = np.zeros((32, 32, D), np.float32)
    ckv_s = np.zeros((1, 32, 32, 256), np.float32); kpe_s = np.zeros((1, 32, 32, 64), np.float32)
    sre_s = np.zeros((1, 32, 32, 64), np.float32); sim_s = np.zeros((1, 32, 32, 64), np.float32)

    def unlay(a):
        return a.reshape(2, 64, 16).transpose(2, 0, 1).reshape(32, 64)

    for c in range(NCORES):
        yo = R[c]["y_o"]
        for i in range(16):
            t = 8 * i + c
            y_p[0, t * 128:(t + 1) * 128] = yo[i * 128:(i + 1) * 128]
        y_s[4 * c:4 * c + 4] = yo[16 * 128:].reshape(4, 32, D)
        ckv_s[0, 4 * c:4 * c + 4] = R[c]["ckvs_o"].reshape(4, 32, 256)
        kpe_s[0, 4 * c:4 * c + 4] = R[c]["kpes_o"].reshape(4, 32, 64)
        sf = R[c]["ssms_o"].reshape(128, 2, 4, 16)
        for s in range(4):
            sre_s[0, 4 * c + s] = unlay(sf[:, 0, s, :])
            sim_s[0, 4 * c + s] = unlay(sf[:, 1, s, :])
    r0 = R[0]
    ckv_p = r0["ckv_o"].reshape(1, 1, 16384, 256); kpe_p = r0["kpe_o"].reshape(1, 1, 16384, 64)
    sp = r0["ssmp_o"]
    sre_p = unlay(sp[:, 0:16]).reshape(1, 1, 32, 64); sim_p = unlay(sp[:, 16:32]).reshape(1, 1, 32, 64)
    mk_p = r0["memk_o"].reshape(1, 1, 256, 4, 256); mv_p = r0["memv_o"].reshape(1, 1, 256, 4, 256)
    return (y_p, y_s, ckv_p, kpe_p, sre_p, sim_p, mk_p, mv_p, ckv_s, kpe_s, sre_s, sim_s)
```

```python
import math
from contextlib import ExitStack

import numpy as np
import concourse.bass as bass
import concourse.mybir as mybir
from concourse.bass_utils import run_bass_kernel_spmd

F32 = mybir.dt.float32
BF16 = mybir.dt.bfloat16
I32 = mybir.dt.int32
AF = mybir.ActivationFunctionType
ALU = mybir.AluOpType
AX = mybir.AxisListType

NCORES = 8
D = 1024
NTP = 128
NOWN = 17
EPS = 1e-5
ALPHA = 2.0 ** 0.25
MLA_SCALE = 192.0 ** -0.5
X_SCALE = 256.0 ** -0.5
TWO_PI = 2.0 * math.pi
NEG = -1e30
NRING = 24
COMPUTE = ("pe", "act", "dve", "pool")


class Buf:
    __slots__ = ("name", "lw", "rd")

    def __init__(self, name=""):
        self.name = name
        self.lw = None
        self.rd = []


def _flat(bs):
    out = []
    for b in bs:
        if isinstance(b, (tuple, list)):
            out.extend(b)
        else:
            out.append(b)
    return out


class Op:
    __slots__ = ("eng", "fn", "deps", "isdma", "flag", "cnt", "ring", "n")

    def __init__(self, eng, fn, isdma):
        self.eng = eng
        self.fn = fn
        self.isdma = isdma
        self.deps = set()
        self.flag = False
        self.cnt = 0
        self.ring = None
        self.n = 0


class Prog:
    def __init__(self, nc):
        self.nc = nc
        self.ops = {e: [] for e in ("pe", "act", "dve", "pool", "sp")}
        self.ndma = {e: 0 for e in self.ops}
        self.allops = []
        self.floor = []
        self.dmas_since = []

    def _add(self, eng, fn, reads, writes, isdma):
        op = Op(eng, fn, isdma)
        reads = _flat(reads)
        writes = _flat(writes)
        for f in self.floor:
            op.deps.add(f)
        for b in reads:
            if b.lw is not None:
                op.deps.add(b.lw)
        for b in writes:
            if b.lw is not None:
                op.deps.add(b.lw)
            for r in b.rd:
                op.deps.add(r)
        for b in reads:
            b.rd.append(op)
        for b in writes:
            b.lw = op
            b.rd = []
        op.deps.discard(op)
        if isdma:
            op.n = self.ndma[eng]
            self.ndma[eng] += 1
            self.dmas_since.append(op)
        self.ops[eng].append(op)
        self.allops.append(op)
        return op

    def op(self, eng, fn, reads=(), writes=()):
        return self._add(eng, fn, reads, writes, False)

    def dma(self, eng, fn, reads=(), writes=()):
        return self._add(eng, fn, reads, writes, True)

    def barrier(self, fn):
        op = Op("pool", fn, False)
        for f in self.floor:
            op.deps.add(f)
        for e in COMPUTE:
            for o in reversed(self.ops[e]):
                if not o.isdma:
                    op.deps.add(o)
                    break
        for o in self.dmas_since:
            op.deps.add(o)
        self.dmas_since = []
        self.ops["pool"].append(op)
        self.allops.append(op)
        self.floor = [op]

    def emit(self, stack):
        nc = self.nc
        for op in self.allops:
            for d in op.deps:
                if d.eng == "pe" and op.eng == "pe" and not d.isdma:
                    continue
                d.flag = True
        sems = {e: stack.enter_context(nc.semaphore("s_" + e)) for e in COMPUTE}
        rings = {}
        for e in self.ops:
            if self.ndma[e]:
                rings[e] = [stack.enter_context(nc.semaphore("r_%s_%d" % (e, i))) for i in range(NRING)]
        for e in self.ops:
            c = 0
            for op in self.ops[e]:
                if op.isdma:
                    op.ring = rings[e][op.n % NRING]
                    op.cnt = 16 * (op.n // NRING + 1)
                elif op.flag:
                    c += 1
                    op.cnt = c
        block = stack.enter_context(nc.Block())

        def run(e, h):
            waited = {}

            def wait(sem, val):
                k = id(sem)
                if waited.get(k, 0) >= val:
                    return
                waited[k] = val
                h.wait_ge(sem, val)

            for op in self.ops[e]:
                for d in op.deps:
                    if d.isdma:
                        wait(d.ring, d.cnt)
                    else:
                        if d.eng == "pe" and e == "pe":
                            continue
                        wait(sems[d.eng], d.cnt)
                if op.isdma and op.n >= NRING:
                    wait(op.ring, op.cnt - 16)
                ins = op.fn(h)
                if op.isdma:
                    ins.then_inc(op.ring, 16)
                elif op.flag:
                    ins.then_inc(sems[e], 1)
            if e in rings:
                n = self.ndma[e]
                for i in range(min(n, NRING)):
                    last = ((n - 1 - i) // NRING) * NRING + i
                    wait(rings[e][i], 16 * (last // NRING + 1))

        @block.tensor
        def _(h):
            run("pe", h)

        @block.scalar
        def _(h):
            run("act", h)

        @block.vector
        def _(h):
            run("dve", h)

        @block.gpsimd
        def _(h):
            run("pool", h)

        @block.sync
        def _(h):
            run("sp", h)


class V:
    __slots__ = ("ap", "b")

    def __init__(self, ap, b):
        self.ap = ap
        self.b = b

    def __getitem__(self, idx):
        return V(self.ap[idx], self.b)

    def re(self, pat, **kw):
        return V(self.ap.rearrange(pat, **kw), self.b)

    def cast(self, dt):
        return V(self.ap.bitcast(dt), self.b)

    def bc(self, shape):
        return V(self.ap.broadcast_to(shape), self.b)

    def un(self, ax):
        return V(self.ap.unsqueeze(ax), self.b)


def build_nc():
    nc = bass.Bass("TRN2", target_bir_lowering=False)
    P = Prog(nc)
    dram = {}

    def din(name, shape, dt=F32):
        t = nc.dram_tensor(name, list(shape), dt, kind="ExternalInput")
        dram[name] = V(t.ap(), Buf(name))
        return dram[name]

    def dout(name, shape):
        t = nc.dram_tensor(name, list(shape), F32, kind="ExternalOutput")
        dram[name] = V(t.ap(), Buf(name))
        return dram[name]

    def dscr(name, shape, dt=F32):
        t = nc.dram_tensor(name, list(shape), dt)
        return V(t.ap(), Buf(name))

    xp = din("xp", [NTP * 128, D])
    xo = din("xo", [NOWN * 128, D])
    mem = din("mem", [256, D])
    ident_d = din("ident", [128, 128])
    w_in_d = din("w_in", [D, 1216]); w_q_d = din("w_q", [384, 768]); w_kv_d = din("w_kv", [256, 1024])
    w_glu_d = din("w_glu", [512, 1024]); w_o_d = din("w_o", [D, D]); w_xq_d = din("w_xq", [D, D])
    w_xk_d = din("w_xk", [D, D]); w_xv_d = din("w_xv", [D, D]); w_xo_d = din("w_xo", [D, D])
    w_ff1_d = din("w_ff1", [D, 4096]); w_ff2_d = din("w_ff2", [4096, D])
    gq_d = din("gq", [128, 384]); gkv_d = din("gkv", [128, 256]); gos_d = din("gos", [128, 512]); gom_d = din("gom", [128, 512])
    lng_d = din("lng", [128, 3 * D]); lnb_d = din("lnb", [128, 3 * D])
    ropec_p = din("ropec_p", [NTP * 128, 64]); ropes_p = din("ropes_p", [NTP * 128, 64])
    ropec_o = din("ropec_o", [NOWN * 128, 64]); ropes_o = din("ropes_o", [NOWN * 128, 64])
    maskd_d = din("maskd", [128, 1024]); onehot_d = din("onehot", [128, 8])
    ar_d = din("ar", [128, 16]); ai_d = din("ai", [128, 16]); ldt_d = din("ldt", [128, 16])
    bre_d = din("bre", [128, 256]); bim_d = din("bim", [128, 256]); jj_d = din("jj", [128, 128]); jj2_d = din("jj2", [128, 128])
    cre_d = din("cre", [128, 512]); cimn_d = din("cimn", [128, 512]); dblk_d = din("dblk", [128, 512])
    s0_d = din("s0", [128, 128])
    cckv_d = din("cckv", [4 * 1024, 256]); ckpe_d = din("ckpe", [4 * 1024, 64])
    cmk_d = din("cmk", [4 * 256, D]); cmv_d = din("cmv", [4 * 256, D])

    y_o = dout("y_o", [NOWN * 128, D])
    ckv_o = dout("ckv_o", [NTP * 128, 256]); kpe_o = dout("kpe_o", [NTP * 128, 64])
    ssmp_o = dout("ssmp_o", [128, 32])
    memk_o = dout("memk_o", [256, D]); memv_o = dout("memv_o", [256, D])
    ckvs_o = dout("ckvs_o", [128, 256]); kpes_o = dout("kpes_o", [128, 64]); ssms_o = dout("ssms_o", [128, 128])

    h1_d = dscr("h1_d", [NOWN * 128, D]); h2_d = dscr("h2_d", [NOWN * 128, D])
    tab_d = {n_: dscr("tab_" + n_, [128, 2048]) for n_ in ("S_t", "C_t", "R_t")}
    tab_d["WBre"] = dscr("tab_WBre", [128, 2048], BF16); tab_d["WBim"] = dscr("tab_WBim", [128, 2048], BF16)

    with ExitStack() as st:
        ARENA_N = 52000
        arena = st.enter_context(nc.sbuf_tensor("arena", [128, ARENA_N], F32))
        bump = [0]

        def alloc(shape, dt=F32, name=""):
            n = 1
            for s_ in shape[1:]:
                n *= s_
            words = (n * (2 if dt == BF16 else 4) + 3) // 4
            words = (words + 7) // 8 * 8
            off = bump[0]
            bump[0] += words
            assert bump[0] <= ARENA_N, ("SBUF arena overflow", name, bump[0])
            ap = arena[:, off:off + words]
            if dt != F32:
                ap = ap.bitcast(dt)
            ap = ap[:, 0:n]
            if len(shape) > 2:
                names = " ".join("d%d" % i for i in range(len(shape) - 1))
                kw = {"d%d" % i: shape[i + 1] for i in range(len(shape) - 1)}
                ap = ap.rearrange("p (%s) -> p %s" % (names, names), **kw)
            return V(ap, Buf(name))

        banks = []
        dbl = []
        for d_ in range(4):
            t = st.enter_context(nc.psum_tensor("dbank%d" % d_, [128, 1024], F32))
            b0 = Buf("bank%d" % (2 * d_)); b1 = Buf("bank%d" % (2 * d_ + 1))
            banks.append(V(t[:, 0:512], b0)); banks.append(V(t[:, 512:1024], b1))
            dbl.append(V(t[:], (b0, b1)))
        SC = dbl[3]
        SCs = [dbl[3], dbl[0]]

        def mm(out, lhsT, rhs, start=True, stop=True):
            P.op("pe", lambda e: e.matmul(out.ap, lhsT=lhsT.ap, rhs=rhs.ap, start=start, stop=stop),
                 reads=[lhsT.b, rhs.b], writes=[out.b])

        def tr(out, in_, idt):
            P.op("pe", lambda e: e.transpose(out=out.ap, in_=in_.ap, identity=idt.ap), reads=[in_.b, idt.b], writes=[out.b])

        def act(out, in_, func, bias=None, scale=None, accum=None):
            kw = {}
            rd = [in_.b]
            wr = [out.b]
            if bias is not None:
                if isinstance(bias, V):
                    kw["bias"] = bias.ap; rd.append(bias.b)
                else:
                    kw["bias"] = bias
            if scale is not None:
                if isinstance(scale, V):
                    kw["scale"] = scale.ap; rd.append(scale.b)
                else:
                    kw["scale"] = scale
            if accum is not None:
                kw["accum_out"] = accum.ap; wr.append(accum.b)
            P.op("act", lambda e: e.activation(out=out.ap, in_=in_.ap, func=func, **kw), reads=rd, writes=wr)

        def acopy(out, in_):
            P.op("act", lambda e: e.copy(out=out.ap, in_=in_.ap), reads=[in_.b], writes=[out.b])

        def vcopy(out, in_, eng="dve"):
            P.op(eng, lambda e: e.tensor_copy(out=out.ap, in_=in_.ap), reads=[in_.b], writes=[out.b])

        def tt(out, in0, in1, op, eng="dve"):
            P.op(eng, lambda e: e.tensor_tensor(out=out.ap, in0=in0.ap, in1=in1.ap, op=op), reads=[in0.b, in1.b], writes=[out.b])

        def ts(out, in0, s1, s2, op0, op1=None, eng="dve"):
            rd = [in0.b]
            a1 = s1
            a2 = s2
            if isinstance(s1, V):
                a1 = s1.ap; rd.append(s1.b)
            if isinstance(s2, V):
                a2 = s2.ap; rd.append(s2.b)
            if op1 is None:
                P.op(eng, lambda e: e.tensor_scalar(out=out.ap, in0=in0.ap, scalar1=a1, scalar2=None, op0=op0), reads=rd, writes=[out.b])
            else:
                P.op(eng, lambda e: e.tensor_scalar(out=out.ap, in0=in0.ap, scalar1=a1, scalar2=a2, op0=op0, op1=op1), reads=rd, writes=[out.b])

        def stt(out, in0, scalar, in1, op0, op1):
            rd = [in0.b, in1.b]
            a = scalar
            if isinstance(scalar, V):
                a = scalar.ap; rd.append(scalar.b)
            P.op("dve", lambda e: e.scalar_tensor_tensor(out=out.ap, in0=in0.ap, scalar=a, in1=in1.ap, op0=op0, op1=op1), reads=rd, writes=[out.b])

        def red(out, in_, op, axis=AX.X):
            P.op("dve", lambda e: e.tensor_reduce(out=out.ap, in_=in_.ap, axis=axis, op=op), reads=[in_.b], writes=[out.b])

        def recip(out, in_):
            P.op("dve", lambda e: e.reciprocal(out=out.ap, in_=in_.ap), reads=[in_.b], writes=[out.b])

        def scan(out, d0, d1, init):
            P.op("dve", lambda e: e.tensor_tensor_scan(out=out.ap, data0=d0.ap, data1=d1.ap, initial=init.ap, op0=ALU.mult, op1=ALU.add),
                 reads=[d0.b, d1.b, init.b], writes=[out.b])

        def mset(out, val, eng="pool"):
            P.op(eng, lambda e: e.memset(out.ap, val), writes=[out.b])

        def ld(out, in_, eng="sp"):
            P.dma(eng, lambda e: e.dma_start(out=out.ap, in_=in_.ap), reads=[in_.b], writes=[out.b])

        def ldw(out, in_):
            P.dma("pool", lambda e: e.dma_start(out=out.ap, in_=in_.ap), reads=[in_.b], writes=[out.b])

        ident = alloc([128, 128], F32, "ident"); identb = alloc([128, 128], BF16, "identb")
        bar_scr = alloc([128, 8], F32, "barscr")
        ld(ident, ident_d); ldw(identb, ident_d)
        epsc = alloc([128, 1], F32, "epsc"); mset(epsc, EPS)

        def barrier():
            P.barrier(lambda e: e.memset(bar_scr.ap, 0.0))

        def load_xT(src_rows, xt, xT, bank):
            ld(xt, src_rows)
            for hb in range(2):
                for k in range(4):
                    kk = hb * 4 + k
                    tr(bank[:, k * 128:(k + 1) * 128], xt[:, kk * 128:(kk + 1) * 128], ident)
                src = bank.re("p (k n) -> p k n", k=4)
                if hb == 0:
                    acopy(xT[:, 0:4, :], src)
                else:
                    vcopy(xT[:, 4:8, :], src)

        def transpose_to(dst, src, ncol, bank, eng="act", rows=128):
            bb = bank.cast(BF16)
            for k in range(ncol):
                tr(bb[:, k * 128:k * 128 + rows], src[:, k * 128:(k + 1) * 128], identb[0:rows, 0:rows])
            s_ = bb[:, 0:ncol * 128].re("p (k n) -> p k n", k=ncol)[:, :, 0:rows]
            if eng == "act":
                acopy(dst, s_)
            else:
                vcopy(dst, s_)

        def rmsnorm(out, src, n, gtile, sq, ss):
            act(sq, src, AF.Square, accum=ss)
            act(ss, ss, AF.Ln, scale=1.0 / n, bias=epsc)
            act(ss, ss, AF.Exp, scale=-0.5)
            stt(out, src, ss, gtile, ALU.mult, ALU.mult)

        def layernorm(out, r, g, b, stats, mv):
            for c in range(2):
                P.op("dve", lambda e, c=c: e.bn_stats(out=stats.ap[:, c * 6:(c + 1) * 6], in_=r.ap[:, c * 512:(c + 1) * 512]), reads=[r.b], writes=[stats.b])
            P.op("dve", lambda e: e.bn_aggr(out=mv.ap[:, 0:2], in_=stats.ap[:, 0:12]), reads=[stats.b], writes=[mv.b])
            act(mv[:, 1:2], mv[:, 1:2], AF.Ln, bias=epsc)
            act(mv[:, 1:2], mv[:, 1:2], AF.Exp, scale=-0.5)
            ts(out, r, mv[:, 0:1], mv[:, 1:2], ALU.subtract, ALU.mult)
            tt(out, out, g, ALU.mult, eng="pool")
            tt(out, out, b, ALU.add, eng="pool")

        def rope(out, src, cc, ss_, tmp, nh, eng="dve"):
            ccb = cc.un(1).bc([128, nh, 64]); ssb = ss_.un(1).bc([128, nh, 64])
            tt(out, src, ccb, ALU.mult, eng=eng)
            tt(tmp[:, :, 0:32], src[:, :, 32:64], ssb[:, :, 0:32], ALU.mult, eng=eng)
            tt(tmp[:, :, 32:64], src[:, :, 0:32], ssb[:, :, 32:64], ALU.mult, eng=eng)
            tt(out, out, tmp, ALU.add, eng="pool")

        mkT = alloc([128, 4, 2, 256], BF16, "mkT")
        mvb = alloc([128, 2, D], BF16, "mvb")
        mark_after_mem = bump[0]
        btab = Buf("tab")
        W_in_u = alloc([128, 8, 512], BF16, "W_in_u"); W_in_kv = alloc([128, 8, 320], BF16, "W_in_kv")
        LrT = alloc([128, 16, 128], BF16, "LrT"); LiT = alloc([128, 16, 128], BF16, "LiT")
        BRb = alloc([128, 16, 32], F32, "BRb"); BIb = alloc([128, 16, 32], F32, "BIb")
        lam128 = alloc([128, 2, 16], F32, "lam128")
        Xpp = alloc([128, 2, 32], F32, "Xpp")
        Xown = alloc([128, 16, 32], F32, "Xown")
        ohot = alloc([128, 8], F32, "ohot")
        Oacc = alloc([128, 16, 4, 128], F32, "Oacc")
        m_run = alloc([128, 64], F32, "m_run"); l_run = alloc([128, 64], F32, "l_run")
        Osam = alloc([128, 512], F32, "Osam")
        ssmx = {}
        mark_after_mixer_state = bump[0]
        QTn = alloc([128, NOWN, 4, 128], BF16, "QTn"); QTr = alloc([128, NOWN, 4, 128], BF16, "QTr")
        mark_after_q = bump[0]

        w_in_v = w_in_d.re("(kt p) n -> p kt n", p=128)
        ldw(W_in_u, w_in_v[:, :, 0:512]); ldw(W_in_kv, w_in_v[:, :, 896:1216])
        ld(ohot, onehot_d)

        S_t = alloc([128, 16, 128], F32, "S_t"); C_t = alloc([128, 16, 128], F32, "C_t"); R_t = alloc([128, 16, 128], F32, "R_t")
        for v_ in (S_t, C_t, R_t):
            v_.b = btab
        WBre = alloc([128, 16, 128], BF16, "WBre"); WBim = alloc([128, 16, 128], BF16, "WBim")

        p0 = bump[0]
        ar = alloc([128, 16]); ai = alloc([128, 16]); ldt = alloc([128, 16]); jj = alloc([128, 128])
        bre = alloc([128, 16, 16]); bim = alloc([128, 16, 16])
        bsm = Buf("ssm_small")
        for v_ in (ar, ai, ldt, jj, bre, bim):
            v_.b = bsm
        ld(ar, ar_d); ld(ai, ai_d); ld(ldt, ldt_d); ld(jj, jj_d)
        ld(bre.re("p a b -> p (a b)"), bre_d); ld(bim.re("p a b -> p (a b)"), bim_d)
        dt_ = alloc([128, 16]); th = alloc([128, 16]); rr = alloc([128, 16])
        sm = [alloc([128, 16]) for _ in range(8)]
        for v_ in [dt_, th, rr] + sm:
            v_.b = bsm
        act(dt_, ldt, AF.Exp)
        tt(th, ai, dt_, ALU.mult)
        tt(rr, ar, dt_, ALU.mult)
        act(rr, rr, AF.Exp)
        A_t = alloc([128, 16, 128]); T1 = alloc([128, 2048]); TI = alloc([128, 2048], I32)
        for v_ in (A_t, T1, TI):
            v_.b = btab
        jjb = jj.un(1).bc([128, 16, 128])
        tt(A_t, jjb, th.un(2).bc([128, 16, 128]), ALU.mult)
        tt(R_t, jjb, rr.un(2).bc([128, 16, 128]), ALU.max)
        tt(R_t, R_t, rr.un(2).bc([128, 16, 128]), ALU.min)
        Af = A_t.re("p a b -> p (a b)")
        TIf = TI.cast(F32)

        def sin_of(out, shift):
            ts(T1, Af, shift, 1.0 / TWO_PI, ALU.add, ALU.mult)
            vcopy(TI, T1)
            vcopy(T1, TI)
            stt(T1, T1, -TWO_PI, Af, ALU.mult, ALU.add)
            if shift != 0.0:
                ts(T1, T1, shift, None, ALU.add)
            ts(TIf, T1, math.pi, -TWO_PI, ALU.is_gt, ALU.mult)
            tt(T1, T1, TIf, ALU.add)
            ts(TIf, T1, -math.pi, TWO_PI, ALU.is_lt, ALU.mult)
            tt(T1, T1, TIf, ALU.add)
            ts(T1, T1, math.pi, -math.pi, ALU.min, ALU.max)
            act(out, T1, AF.Sin)

        sin_of(S_t.re("p a b -> p (a b)"), 0.0)
        sin_of(C_t.re("p a b -> p (a b)"), math.pi / 2)
        lbr, lbi, den, fre, fim, t0_, t1_, t2_ = sm
        cos1 = C_t[:, :, 0]; sin1 = S_t[:, :, 0]
        tt(lbr, rr, cos1, ALU.mult)
        ts(lbr, lbr, -1.0, None, ALU.add)
        tt(lbi, rr, sin1, ALU.mult)
        tt(den, ar, ar, ALU.mult)
        tt(t0_, ai, ai, ALU.mult)
        tt(den, den, t0_, ALU.add)
        recip(den, den)
        tt(t0_, lbr, ar, ALU.mult); tt(t1_, lbi, ai, ALU.mult); tt(fre, t0_, t1_, ALU.add); tt(fre, fre, den, ALU.mult)
        tt(t0_, lbi, ar, ALU.mult); tt(t1_, lbr, ai, ALU.mult); tt(fim, t0_, t1_, ALU.subtract); tt(fim, fim, den, ALU.mult)
        Mre = alloc([128, 16, 128]); Mim = alloc([128, 16, 128]); tb = alloc([128, 16, 16]); tb2 = alloc([128, 16, 16])
        Mreb = alloc([128, 16, 128], BF16); Mimb = alloc([128, 16, 128], BF16)
        bM = Buf("M")
        for v_ in (Mre, Mim, tb, tb2, Mreb, Mimb):
            v_.b = bM
        freb = fre.un(2).bc([128, 16, 16]); fimb = fim.un(2).bc([128, 16, 16])
        mset(Mre, 0.0); mset(Mim, 0.0)
        tt(tb, bre, freb, ALU.mult); tt(tb2, bim, fimb, ALU.mult)
        for lo, col in ((0, 0), (64, 16)):
            for j4 in range(4):
                tt(Mre[lo:lo + 64, j4::4, 32 * j4 + col:32 * j4 + col + 16], tb[lo:lo + 64, j4::4, :], tb2[lo:lo + 64, j4::4, :], ALU.subtract)
        tt(tb, bim, freb, ALU.mult); tt(tb2, bre, fimb, ALU.mult)
        for lo, col in ((0, 0), (64, 16)):
            for j4 in range(4):
                tt(Mim[lo:lo + 64, j4::4, 32 * j4 + col:32 * j4 + col + 16], tb[lo:lo + 64, j4::4, :], tb2[lo:lo + 64, j4::4, :], ALU.add)
        vcopy(Mreb, Mre); vcopy(Mimb, Mim)
        for M_, WB_ in ((Mreb, WBre), (Mimb, WBim)):
            for hf in range(2):
                bb = banks[0].cast(BF16)
                for q in range(8):
                    tr(bb[:, q * 128:(q + 1) * 128], M_[:, 8 * hf + q, :], identb)
                vcopy(WB_[:, 8 * hf:8 * hf + 8, :].re("p a b -> p (a b)"), bb)

        mset(BRb, 0.0); mset(BIb, 0.0)
        tt(tb, bre, freb, ALU.mult); tt(tb2, bim, fimb, ALU.mult)
        for lo, col in ((0, 0), (64, 16)):
            tt(BRb[lo:lo + 64, :, col:col + 16], tb[lo:lo + 64], tb2[lo:lo + 64], ALU.subtract)
        tt(tb, bim, freb, ALU.mult); tt(tb2, bre, fimb, ALU.mult)
        for lo, col in ((0, 0), (64, 16)):
            tt(BIb[lo:lo + 64, :, col:col + 16], tb[lo:lo + 64], tb2[lo:lo + 64], ALU.add)
        lnr = alloc([128, 16]); lnr.b = bsm
        tt(lnr, ar, dt_, ALU.mult)
        act(t2_, lnr, AF.Exp, scale=128.0)
        tt(lam128[:, 0, :], t2_, C_t[:, :, 127], ALU.mult)
        tt(lam128[:, 1, :], t2_, S_t[:, :, 127], ALU.mult)
        jj2 = alloc([128, 128]); jj2.b = bsm
        ld(jj2, jj2_d)
        C2 = Mre; S2 = Mim; Mag = T1.re("p (a b) -> p a b", a=16)
        jj2b = jj2.un(1).bc([128, 16, 128])
        tt(A_t, jj2b, th.un(2).bc([128, 16, 128]), ALU.mult)
        sin_of(S2.re("p a b -> p (a b)"), 0.0)
        sin_of(C2.re("p a b -> p (a b)"), math.pi / 2)
        tt(Mag, jj2b, lnr.un(2).bc([128, 16, 128]), ALU.mult)
        act(Mag, Mag, AF.Exp)
        Lb = [Mreb, Mimb]
        tt(Lb[0], Mag, C2, ALU.mult); tt(Lb[1], Mag, S2, ALU.mult)
        for M_, LT_ in ((Lb[0], LrT), (Lb[1], LiT)):
            for hf in range(2):
                bb = banks[0].cast(BF16)
                for q in range(8):
                    tr(bb[:, q * 128:(q + 1) * 128], M_[:, 8 * hf + q, :], identb)
                vcopy(LT_[:, 8 * hf:8 * hf + 8, :].re("p a b -> p (a b)"), bb)

        for nm_, v_ in (("S_t", S_t), ("C_t", C_t), ("R_t", R_t)):
            ld(tab_d[nm_], v_.re("p a b -> p (a b)"))
        ld(tab_d["WBre"], WBre.re("p a b -> p (a b)")); ld(tab_d["WBim"], WBim.re("p a b -> p (a b)"))

        barrier()
        bump[0] = mark_after_q
        W_xk = alloc([128, 8, D], BF16, "W_xk"); W_xv = alloc([128, 8, D], BF16, "W_xv")
        ldw(W_xk, w_xk_d.re("(kt p) n -> p kt n", p=128)); ldw(W_xv, w_xv_d.re("(kt p) n -> p kt n", p=128))
        xt0 = alloc([128, D], F32, "xt0"); memT = alloc([128, 8, 256], BF16, "memT"); xT0 = alloc([128, 8, 128], BF16, "xT0")
        mkf = alloc([128, D], F32, "mkf")
        for mt in range(2):
            load_xT(mem[mt * 128:(mt + 1) * 128, :], xt0, xT0, banks[0])
            vcopy(memT[:, :, mt * 128:(mt + 1) * 128], xT0, eng="pool")
            for W_, o_d, keep in ((W_xk, memk_o, False), (W_xv, memv_o, True)):
                for nb in range(2):
                    for k in range(8):
                        mm(banks[1 + nb], xT0[:, k, :], W_[:, k, nb * 512:(nb + 1) * 512], start=(k == 0), stop=(k == 7))
                    acopy(mkf[:, nb * 512:(nb + 1) * 512], banks[1 + nb])
                ld(o_d[mt * 128:(mt + 1) * 128, :], mkf)
                if keep:
                    vcopy(mvb[:, mt, :], mkf)
        for h in range(4):
            for et in range(2):
                c0 = h * 256 + et * 128
                for k in range(8):
                    mm(banks[3][:, 0:256], W_xk[:, k, c0:c0 + 128], memT[:, k, :], start=(k == 0), stop=(k == 7))
                acopy(mkT[:, h, et, :], banks[3][:, 0:256])

        barrier()
        bump[0] = mark_after_q

        W_q = alloc([128, 3, 768], BF16, "W_q")
        ldw(W_q, w_q_d.re("(kt p) n -> p kt n", p=128))
        W_in_q = alloc([128, 8, 384], BF16, "W_in_q")
        ldw(W_in_q, w_in_v[:, :, 512:896])
        gq = alloc([128, 384], F32, "gq"); ld(gq, gq_d)
        NB = 2
        xt = [alloc([128, D], F32, "xt%d" % i) for i in range(NB)]
        xT = [alloc([128, 8, 128], BF16, "xT%d" % i) for i in range(NB)]
        rc = [alloc([128, 64], F32, "rc%d" % i) for i in range(NB)]
        rs = [alloc([128, 64], F32, "rs%d" % i) for i in range(NB)]
        sqp = [alloc([128, 384], F32, "sq%d" % i) for i in range(NB)]; ss1 = [alloc([128, 1], F32) for _ in range(NB)]
        cqn = [alloc([128, 384], BF16) for _ in range(NB)]
        cqT = [alloc([128, 3, 128], BF16) for _ in range(NB)]
        qf = [alloc([128, 4, 192], F32) for _ in range(NB)]
        qr = [alloc([128, 4, 64], F32) for _ in range(NB)]
        qtmp = [alloc([128, 4, 64], F32) for _ in range(NB)]
        qb = [alloc([128, 4, 192], BF16) for _ in range(NB)]

        def q_stages(i, p):
            Xb = banks[p]; Jb = banks[2 + p]; Qb = dbl[2 + p]

            def sA():
                load_xT(xo[i * 128:(i + 1) * 128, :], xt[p], xT[p], Xb)
                ld(rc[p], ropec_o[i * 128:(i + 1) * 128, :]); ld(rs[p], ropes_o[i * 128:(i + 1) * 128, :])

            def sB():
                for k in range(8):
                    mm(Jb[:, 0:384], xT[p][:, k, :], W_in_q[:, k, :], start=(k == 0), stop=(k == 7))

            def sC():
                rmsnorm(cqn[p], Jb[:, 0:384], 384, gq, sqp[p], ss1[p])

            def sD():
                transpose_to(cqT[p], cqn[p], 3, Xb, eng="act")

            def sE():
                for nb, (c0, c1) in enumerate(((0, 512), (512, 768))):
                    for k in range(3):
                        mm(Qb[:, nb * 512:nb * 512 + (c1 - c0)], cqT[p][:, k, :], W_q[:, k, c0:c1], start=(k == 0), stop=(k == 2))

            def sF():
                acopy(qf[p].re("p a b -> p (a b)"), Qb[:, 0:768])

            def sG():
                rope(qr[p], qf[p][:, :, 128:192], rc[p], rs[p], qtmp[p], 4)

            def sH():
                P.op("act", lambda e: e.mul(out=qb[p].ap[:, :, 0:128], in_=qf[p].ap[:, :, 0:128], mul=MLA_SCALE), reads=[qf[p].b], writes=[qb[p].b])
                P.op("act", lambda e: e.mul(out=qb[p].ap[:, :, 128:192], in_=qr[p].ap, mul=MLA_SCALE), reads=[qr[p].b], writes=[qb[p].b])

            def sI():
                bb = Xb.cast(BF16)
                for h in range(4):
                    tr(bb[:, h * 128:(h + 1) * 128], qb[p][:, h, 0:128], identb)
                    tr(bb[0:64, 512 + h * 128:512 + (h + 1) * 128], qb[p][:, h, 128:192], identb)

            def sJ():
                bb = Xb.cast(BF16)
                vcopy(QTn[:, i, :, :].re("p a b -> p (a b)"), bb[:, 0:512])
                vcopy(QTr[0:64, i, :, :].re("p a b -> p (a b)"), bb[0:64, 512:1024])

            return [sA, sB, sC, sD, sE, sF, sG, sH, sI, sJ]

        for ip in range(9):
            sa = q_stages(2 * ip, 0)
            sb_ = q_stages(2 * ip + 1, 1) if 2 * ip + 1 < NOWN else []
            for k_ in range(len(sa)):
                sa[k_]()
                if k_ < len(sb_):
                    sb_[k_]()

        barrier()
        bump[0] = mark_after_q

        def ssm_rounds(uT_, pbs, init_fn, nseg, ybank=None, last_out=None):
            L = 128 // nseg
            S_t = ssmx["S_t"]; C_t = ssmx["C_t"]; R_t = ssmx["R_t"]; WBre = ssmx["WBre"]; WBim = ssmx["WBim"]
            Cre = ssmx["Cre"]; Cimn = ssmx["Cimn"]; Dblk = ssmx["Dblk"]
            xs4 = ssmx["xs4"]; zl_all = ssmx["zl_all"]
            def do_round(r):
                m4 = ssmx["m4"][r % 2]; wri = ssmx["wri"][r % 2]; zri = ssmx["zri"][r % 2]
                pb = pbs[r % 2]
                for j in range(2):
                    gp = 2 * r + j
                    mm(pb[:, j * 128:(j + 1) * 128], WBre[:, gp, :], uT_[:, gp // 4, :])
                    mm(pb[:, 256 + j * 128:256 + (j + 1) * 128], WBim[:, gp, :], uT_[:, gp // 4, :])
                if nseg == 1:
                    Cq = C_t[:, 2 * r:2 * r + 2, :].re("p a b -> p (a b)"); Sq = S_t[:, 2 * r:2 * r + 2, :].re("p a b -> p (a b)")
                    pre = pb[:, 0:256]; pim = pb[:, 256:512]
                    mk = lambda v_: v_
                else:
                    Cq = C_t[:, 2 * r:2 * r + 2, 0:L].un(2).bc([128, 2, nseg, L]); Sq = S_t[:, 2 * r:2 * r + 2, 0:L].un(2).bc([128, 2, nseg, L])
                    pre = pb[:, 0:256].re("p (a s l) -> p a s l", a=2, s=nseg); pim = pb[:, 256:512].re("p (a s l) -> p a s l", a=2, s=nseg)
                    mk = lambda v_: v_.re("p (a s l) -> p a s l", a=2, s=nseg)
                tt(mk(m4[0]), pre, Cq, ALU.mult); tt(mk(m4[1]), pim, Sq, ALU.mult)
                tt(mk(m4[2]), pim, Cq, ALU.mult); tt(mk(m4[3]), pre, Sq, ALU.mult)
                tt(wri[0], m4[0], m4[1], ALU.add, eng="pool")
                tt(wri[1], m4[2], m4[3], ALU.subtract, eng="pool")

            def do_back(r):
                m4 = ssmx["m4"][r % 2]; wri = ssmx["wri"][r % 2]; zri = ssmx["zri"][r % 2]
                if nseg == 1:
                    Cq = C_t[:, 2 * r:2 * r + 2, :].re("p a b -> p (a b)"); Sq = S_t[:, 2 * r:2 * r + 2, :].re("p a b -> p (a b)")
                else:
                    Cq = C_t[:, 2 * r:2 * r + 2, 0:L].un(2).bc([128, 2, nseg, L]); Sq = S_t[:, 2 * r:2 * r + 2, 0:L].un(2).bc([128, 2, nseg, L])
                for j in range(2):
                    gp = 2 * r + j
                    for s_ in range(nseg):
                        for part in range(2):
                            scan(zri[part][:, j, s_ * L:(s_ + 1) * L], R_t[:, gp, 0:L], wri[part][:, j * 128 + s_ * L:j * 128 + (s_ + 1) * L], init_fn(gp, s_, part))
                if last_out is not None:
                    for part in range(2):
                        src = zri[part].re("p a (s l) -> p a s l", s=nseg)[:, :, :, L - 1]
                        vcopy(zl_all[part][:, 0:nseg, 2 * r:2 * r + 2].re("p s a -> p a s"), src, eng="pool")
                if ybank is not None:
                    dmd = ssmx["dmd"][r % 2]; xrb = ssmx["xrb"]; xib = ssmx["xib"]
                    if nseg == 1:
                        Cd, Sd = Cq, Sq
                        zr_ = zri[0].re("p a b -> p (a b)"); zi_ = zri[1].re("p a b -> p (a b)")
                        dk = lambda v_: v_
                    else:
                        Cd, Sd = Cq, Sq
                        zr_ = zri[0].re("p a (s l) -> p a s l", s=nseg); zi_ = zri[1].re("p a (s l) -> p a s l", s=nseg)
                        dk = lambda v_: v_.re("p (a s l) -> p a s l", a=2, s=nseg)
                    tt(dk(dmd[0]), zr_, Cd, ALU.mult); tt(dk(dmd[1]), zi_, Sd, ALU.mult, eng="pool")
                    tt(dk(dmd[2]), zr_, Sd, ALU.mult); tt(dk(dmd[3]), zi_, Cd, ALU.mult, eng="pool")
                    tt(xrb[r % 2].re("p a b -> p (a b)"), dmd[0], dmd[1], ALU.subtract, eng="pool")
                    tt(xib[r % 2].re("p a b -> p (a b)"), dmd[2], dmd[3], ALU.add, eng="pool")

            def do_cproj(r):
                if ybank is None:
                    return
                xrb = ssmx["xrb"]; xib = ssmx["xib"]
                for j in range(2):
                    gp = 2 * r + j
                    o_ = ybank[:, 32 * gp:32 * gp + 32]
                    mm(o_, xrb[r % 2][:, j, :], Cre[:, gp, :], start=True, stop=False)
                    mm(o_, xib[r % 2][:, j, :], Cimn[:, gp, :], start=False, stop=False)
                    mm(o_, uT_[:, gp // 4, :], Dblk[:, gp, :], start=False, stop=True)
            def do_tail():
                do_cproj(7)
                if last_out is None:
                    return
                if True:
                    cL = C_t[:, :, L - 1]; sL = S_t[:, :, L - 1]
                    for s_ in range(nseg):
                        o_re, o_im = last_out(s_)
                        zr_l = zl_all[0][:, s_, :]; zi_l = zl_all[1][:, s_, :]
                        tt(xs4[0], zr_l, cL, ALU.mult); tt(xs4[1], zi_l, sL, ALU.mult)
                        tt(xs4[2], zr_l, sL, ALU.mult); tt(xs4[3], zi_l, cL, ALU.mult)
                        tt(o_re, xs4[0], xs4[1], ALU.subtract)
                        tt(o_im, xs4[2], xs4[3], ALU.add)


            def step(k_):
                def f():
                    if k_ == 0:
                        do_round(0)
                    if k_ + 1 < 8:
                        do_round(k_ + 1)
                    do_back(k_)
                    if k_ >= 1:
                        do_cproj(k_ - 1)
                return f
            return [step(k_) for k_ in range(8)] + [do_tail]

        def ssm_tile(uT_, pbs, init_fn, nseg, ybank=None, last_out=None):
            for f_ in ssm_rounds(uT_, pbs, init_fn, nseg, ybank, last_out):
                f_()

        W_kv = alloc([128, 2, 4, 256], BF16, "W_kv")
        ldw(W_kv.re("p k h e -> p k (h e)"), w_kv_d.re("(kt p) n -> p kt n", p=128))
        gkv = alloc([128, 256], F32, "gkv"); ld(gkv, gkv_d)
        maskd = alloc([128, 1024], F32, "maskd"); ld(maskd, maskd_d)
        xt = [alloc([128, D], F32, "hxt%d" % i) for i in range(NB)]
        xT = [alloc([128, 8, 128], BF16, "hxT%d" % i) for i in range(NB)]
        utok = [alloc([128, 512], BF16, "hut%d" % i) for i in range(NB)]
        kvf = [alloc([128, 320], F32, "kvf%d" % i) for i in range(NB)]
        prs = [alloc([128, 16, 32], F32, "prs%d" % i) for i in range(NB)]
        pis = [alloc([128, 16, 32], F32, "pis%d" % i) for i in range(NB)]
        rc = [alloc([128, 64], F32) for _ in range(NB)]; rs = [alloc([128, 64], F32) for _ in range(NB)]
        ckvf = [alloc([128, 256], F32) for _ in range(NB)]; kpef = [alloc([128, 64], F32) for _ in range(NB)]
        tmp64 = [alloc([128, 64], F32) for _ in range(NB)]
        sq = alloc([128, 256], F32, "hsq"); ss1 = [alloc([128, 1], F32) for _ in range(NB)]
        ckvb = [alloc([128, 256], BF16) for _ in range(NB)]; kpeb = [alloc([128, 128], BF16) for _ in range(NB)]
        ckvT = [alloc([128, 2, 128], BF16) for _ in range(NB)]
        KT = [alloc([128, 4, 1024], BF16, "KT%d" % i) for i in range(2)]
        KR = [alloc([128, 1024], BF16, "KR%d" % i) for i in range(2)]
        Vb = [alloc([128, 8, 512], BF16, "Vb%d" % i) for i in range(2)]
        Pb = [alloc([128, 1024], BF16, "Pb%d" % i) for i in range(2)]
        PT = [alloc([128, 8, 128], BF16, "PT%d" % i) for i in range(2)]
        NS4 = 4
        mx = [alloc([128, 1], F32) for _ in range(NS4)]; mnew = [alloc([128, 1], F32) for _ in range(NS4)]
        negm = [alloc([128, 1], F32) for _ in range(NS4)]; alp = [alloc([128, 1], F32) for _ in range(NS4)]
        rsum = [alloc([128, 1], F32) for _ in range(NS4)]
        dm_ = [alloc([128, 1], F32) for _ in range(NS4)]
        hm1 = [alloc([128, 16, 32], F32, "hm1_%d" % i) for i in range(NB)]; hm2 = [alloc([128, 16, 32], F32, "hm2_%d" % i) for i in range(NB)]
        sre = [alloc([128, 32], F32) for _ in range(2)]
        xs8 = [alloc([128, 16], F32) for _ in range(4)]

        mset(Xpp, 0.0); mset(Xown, 0.0)
        mset(m_run, NEG); mset(l_run, 0.0); mset(Oacc, 0.0)
        for kp in range(2):
            mset(kpeb[kp], 0.0)

        def hist_stages(t, kbuf, j, kb):
            b = t % NB
            bA = banks[0]; bB = banks[1]
            xc = Xpp[:, t % 2, :]; xn = Xpp[:, (t + 1) % 2, :]

            def sA():
                ld(xt[b], xp[t * 128:(t + 1) * 128, :])
                ld(rc[b], ropec_p[t * 128:(t + 1) * 128, :]); ld(rs[b], ropes_p[t * 128:(t + 1) * 128, :])

            def sB():
                for k in range(4):
                    tr(bA[:, k * 128:(k + 1) * 128], xt[b][:, k * 128:(k + 1) * 128], ident)
                for k in range(4):
                    tr(bB[:, k * 128:(k + 1) * 128], xt[b][:, (4 + k) * 128:(5 + k) * 128], ident)

            def sC():
                acopy(xT[b][:, 0:4, :], bA.re("p (k n) -> p k n", k=4))
                acopy(xT[b][:, 4:8, :], bB.re("p (k n) -> p k n", k=4))

            def sD():
                for k in range(8):
                    mm(bA, xT[b][:, k, :], W_in_u[:, k, :], start=(k == 0), stop=(k == 7))
                for k in range(8):
                    mm(bB[:, 0:320], xT[b][:, k, :], W_in_kv[:, k, :], start=(k == 0), stop=(k == 7))

            def sE():
                acopy(utok[b], bA)
                acopy(kvf[b], bB[:, 0:320])

            def sF():
                act(sq, kvf[b][:, 0:256], AF.Square, accum=ss1[b])
                act(ss1[b], ss1[b], AF.Ln, scale=1.0 / 256, bias=epsc)
                act(ss1[b], ss1[b], AF.Exp, scale=-0.5)
                for gp in range(16):
                    mm(bA[:, 32 * gp:32 * gp + 32], LrT[:, gp, :], utok[b][:, 32 * gp:32 * gp + 32])
                for gp in range(16):
                    mm(bB[:, 32 * gp:32 * gp + 32], LiT[:, gp, :], utok[b][:, 32 * gp:32 * gp + 32])

            late = kb >= 4

            def sG():
                if late:
                    pr3 = bA.re("p (a b) -> p a b", a=16); pi3 = bB.re("p (a b) -> p a b", a=16)
                    tt(hm1[b], pr3, BRb, ALU.mult); tt(hm2[b], pi3, BIb, ALU.mult)
                    tt(prs[b], pi3, BRb, ALU.mult); tt(pis[b], pr3, BIb, ALU.mult)
                else:
                    acopy(prs[b].re("p a b -> p (a b)"), bA)
                    acopy(pis[b].re("p a b -> p (a b)"), bB)
                tt(ckvf[b], kvf[b][:, 0:256], gkv, ALU.mult, eng="pool")
                ts(ckvf[b], ckvf[b], ss1[b], 1.0, ALU.mult, ALU.mult, eng="pool")
                rope(kpef[b].re("p (a b) -> p a b", a=1), kvf[b][:, 256:320].re("p (a b) -> p a b", a=1), rc[b], rs[b],
                     tmp64[b].re("p (a b) -> p a b", a=1), 1, eng="pool")
                vcopy(ckvb[b], ckvf[b], eng="pool")
                vcopy(kpeb[b][:, 0:64], kpef[b], eng="pool")

            def sH():
                bb = bA.cast(BF16)
                for kc in range(2):
                    tr(bb[:, kc * 128:(kc + 1) * 128], ckvb[b][:, kc * 128:(kc + 1) * 128], identb)
                tr(bb[:, 256:384], kpeb[b], identb)
                if late:
                    tt(hm1[b], hm1[b], hm2[b], ALU.subtract, eng="pool")
                    tt(hm2[b], prs[b], pis[b], ALU.add, eng="pool")
                else:
                    tt(hm1[b], prs[b], BRb, ALU.mult, eng="pool"); tt(hm2[b], pis[b], BIb, ALU.mult, eng="pool")
                    tt(hm1[b], hm1[b], hm2[b], ALU.subtract, eng="pool")
                    tt(hm2[b], pis[b], BRb, ALU.mult, eng="pool"); tt(prs[b], prs[b], BIb, ALU.mult, eng="pool")
                    tt(hm2[b], hm2[b], prs[b], ALU.add, eng="pool")

            def sI():
                bb = bA.cast(BF16)
                acopy(ckvT[b].re("p a b -> p (a b)"), bb[:, 0:256])
                acopy(KR[kbuf][0:64, j * 128:(j + 1) * 128], bb[0:64, 256:384])

            def sJ():
                for h in range(4):
                    for kc in range(2):
                        mm(bB[:, h * 128:(h + 1) * 128], W_kv[:, kc, h, 0:128], ckvT[b][:, kc, :], start=(kc == 0), stop=(kc == 1))
                for kc in range(2):
                    mm(bA, ckvT[b][:, kc, :], W_kv[:, kc, :, 128:256], start=(kc == 0), stop=(kc == 1))
                lr_ = lam128[:, 0, :]; li_ = lam128[:, 1, :]
                ld(ckv_o[t * 128:(t + 1) * 128, :], ckvf[b])
                ld(kpe_o[t * 128:(t + 1) * 128, :], kpef[b])
                red(sre[t % 2][:, 0:16], hm1[b], ALU.add)
                red(sre[t % 2][:, 16:32], hm2[b], ALU.add)
                stt(Xown[:, kb, :], xc, ohot[:, j:j + 1], Xown[:, kb, :], ALU.mult, ALU.add)
                tt(xs8[0], xc[:, 0:16], lr_, ALU.mult, eng="pool"); tt(xs8[1], xc[:, 16:32], li_, ALU.mult, eng="pool")
                tt(xs8[2], xc[:, 16:32], lr_, ALU.mult, eng="pool"); tt(xs8[3], xc[:, 0:16], li_, ALU.mult, eng="pool")
                tt(xs8[0], xs8[0], xs8[1], ALU.subtract, eng="pool"); tt(xs8[2], xs8[2], xs8[3], ALU.add, eng="pool")
                tt(xn[:, 0:16], xs8[0], sre[t % 2][:, 0:16], ALU.add, eng="pool")
                tt(xn[:, 16:32], xs8[2], sre[t % 2][:, 16:32], ALU.add, eng="pool")

            def sK():
                acopy(KT[kbuf][:, :, j * 128:(j + 1) * 128], bB.re("p (h n) -> p h n", h=4))
                acopy(Vb[kbuf][:, j, :], bA)

            def pair(p_, c_):
                def f():
                    p_(); c_()
                return f
            return [sA, pair(sB, sC), pair(sD, sE), pair(sF, sG), pair(sH, sI), pair(sJ, sK)]

        def hist_items(kb):
            items = []
            for jp in range(4):
                sa = hist_stages(kb * 8 + 2 * jp, kb % 2, 2 * jp, kb)
                sb_ = hist_stages(kb * 8 + 2 * jp + 1, kb % 2, 2 * jp + 1, kb)
                for a_, b_ in zip(sa, sb_):
                    items.append(a_); items.append(b_)
            return items

        SCp = [dbl[3], dbl[2]]
        ptb = [banks[2], banks[3]]

        def att_A(n, i, h, kbuf):
            sc = SCp[n % 2]
            for c2 in range(2):
                mm(sc[:, c2 * 512:(c2 + 1) * 512], QTn[:, i, h, :], KT[kbuf][:, h, c2 * 512:(c2 + 1) * 512], start=True, stop=False)
                mm(sc[:, c2 * 512:(c2 + 1) * 512], QTr[0:64, i, h, :], KR[kbuf][0:64, c2 * 512:(c2 + 1) * 512], start=False, stop=True)

        def att_B(n, i, h, kbuf, diag):
            u2 = n % 2; u4 = n % NS4
            sc = SCp[u2]
            col = i * 4 + h
            if diag:
                tt(sc, sc, maskd, ALU.add)
            red(mx[u4], sc, ALU.max)
            tt(mnew[u4], mx[u4], m_run[:, col:col + 1], ALU.max)
            tt(dm_[u4], m_run[:, col:col + 1], mnew[u4], ALU.subtract)
            vcopy(m_run[:, col:col + 1], mnew[u4])
            ts(negm[u4], mnew[u4], -1.0, None, ALU.mult)
            act(alp[u4], dm_[u4], AF.Exp)
            act(Pb[u2], sc, AF.Exp, bias=negm[u4], accum=rsum[u4])

        def att_CD(n, i, h, kbuf):
            u2 = n % 2
            pt_b = ptb[u2].cast(BF16)
            for kt in range(8):
                tr(pt_b[:, kt * 128:(kt + 1) * 128], Pb[u2][:, kt * 128:(kt + 1) * 128], identb)
            if n % 3 == 0:
                vcopy(PT[u2].re("p a b -> p (a b)"), pt_b)
            else:
                acopy(PT[u2].re("p a b -> p (a b)"), pt_b)

        def att_EF(n, i, h, kbuf):
            u2 = n % 2; u4 = n % NS4
            ov = ptb[u2][:, 0:128]
            for kt in range(8):
                mm(ov, PT[u2][:, kt, :], Vb[kbuf][:, kt, h * 128:(h + 1) * 128], start=(kt == 0), stop=(kt == 7))
            stt(Oacc[:, i, h, :], Oacc[:, i, h, :], alp[u4], ov, ALU.mult, ALU.add)
            col = i * 4 + h
            stt(l_run[:, col:col + 1], l_run[:, col:col + 1], alp[u4], rsum[u4], ALU.mult, ALU.add)

        for it in hist_items(0):
            it()
        nun = 0
        for kb in range(16):
            kbuf = kb % 2
            units = [(i, h) for i in range(kb, 16) for h in range(4)]
            U = len(units)
            items = hist_items(kb + 1) if kb + 1 < 16 else []
            per = (len(items) + U - 1) // U if items else 0
            ip = 0
            for q_ in range(U + 3):
                if q_ < U:
                    att_A(nun + q_, units[q_][0], units[q_][1], kbuf)
                if 0 <= q_ - 1 < U:
                    att_B(nun + q_ - 1, units[q_ - 1][0], units[q_ - 1][1], kbuf, units[q_ - 1][0] == kb)
                if 0 <= q_ - 2 < U:
                    att_CD(nun + q_ - 2, units[q_ - 2][0], units[q_ - 2][1], kbuf)
                if 0 <= q_ - 3 < U:
                    att_EF(nun + q_ - 3, units[q_ - 3][0], units[q_ - 3][1], kbuf)
                for _ in range(per):
                    if ip < len(items):
                        items[ip](); ip += 1
            while ip < len(items):
                items[ip](); ip += 1
            nun += U

        ld(ssmp_o, Xpp[:, NTP % 2, :])

        barrier()
        bump[0] = mark_after_q

        W_kv = alloc([128, 2, 4, 256], BF16, "W_kv2")
        ldw(W_kv.re("p k h e -> p k (h e)"), w_kv_d.re("(kt p) n -> p kt n", p=128))
        gkv = alloc([128, 256], F32, "gkv2"); ld(gkv, gkv_d)
        xt_s = alloc([128, D], F32, "sxt"); xT_s = alloc([128, 8, 128], BF16, "sxT")
        rc_s = alloc([128, 64], F32); rs_s = alloc([128, 64], F32)
        ckvf_s = alloc([128, 256], F32); kpef_s = alloc([128, 64], F32); tmp64_s = alloc([128, 64], F32)
        sq = alloc([128, 512], F32, "ssq"); ss_s = alloc([128, 1], F32)
        ckvb_s = alloc([128, 256], BF16); kpeb_s = alloc([128, 128], BF16)
        ckvTn = alloc([128, 2, 128], BF16, "ckvTn"); kpeTn = alloc([128, 128], BF16, "kpeTn")
        KTn = alloc([128, 4, 128], BF16, "KTn")
        cat = [alloc([128, 256], F32, "cat%d" % i) for i in range(2)]
        catb = [alloc([128, 256], BF16) for _ in range(2)]
        cpt = [alloc([128, 64], F32, "cpt%d" % i) for i in range(2)]
        cptb = [alloc([128, 128], BF16) for _ in range(2)]
        ckvTc = alloc([128, 2, 1024], BF16, "ckvTc"); kpeTc = alloc([128, 1024], BF16, "kpeTc")
        KTc = alloc([128, 4, 1024], BF16, "KTc"); Vc = alloc([128, 8, 512], BF16, "Vc"); Vn = alloc([128, 512], BF16, "Vn")
        scs = alloc([128, 1056], F32, "scs")
        Ps = alloc([128, 1152], BF16, "Ps"); PTs = alloc([128, 9, 32], BF16, "PTs")
        mxs = alloc([128, 1], F32); negs = alloc([128, 1], F32); sums = alloc([128, 1], F32)
        osb = alloc([128, 512], F32, "osb")
        isam = 16
        load_xT(xo[isam * 128:(isam + 1) * 128, :], xt_s, xT_s, banks[0])
        ld(rc_s, ropec_o[isam * 128:(isam + 1) * 128, :]); ld(rs_s, ropes_o[isam * 128:(isam + 1) * 128, :])
        for k in range(8):
            mm(banks[2][:, 0:320], xT_s[:, k, :], W_in_kv[:, k, :], start=(k == 0), stop=(k == 7))
        rmsnorm(ckvf_s, banks[2][:, 0:256], 256, gkv, sq[:, 0:256], ss_s)
        ld(ckvs_o, ckvf_s)
        rope(kpef_s.re("p (a b) -> p a b", a=1), banks[2][:, 256:320].re("p (a b) -> p a b", a=1), rc_s, rs_s, tmp64_s.re("p (a b) -> p a b", a=1), 1)
        ld(kpes_o, kpef_s)
        vcopy(ckvb_s, ckvf_s)
        mset(kpeb_s, 0.0)
        vcopy(kpeb_s[:, 0:64], kpef_s)
        bb = banks[3].cast(BF16)
        for kc in range(2):
            tr(bb[:, kc * 128:(kc + 1) * 128], ckvb_s[:, kc * 128:(kc + 1) * 128], identb)
        tr(bb[:, 256:384], kpeb_s, identb)
        acopy(ckvTn.re("p a b -> p (a b)"), bb[:, 0:256])
        acopy(kpeTn[0:64, :], bb[0:64, 256:384])
        for h in range(4):
            for kc in range(2):
                mm(banks[3][:, h * 128:(h + 1) * 128], W_kv[:, kc, h, 0:128], ckvTn[:, kc, :], start=(kc == 0), stop=(kc == 1))
        acopy(KTn.re("p a b -> p (a b)"), banks[3])
        for kp in range(2):
            mset(cptb[kp], 0.0)
        for s_ in range(4):
            for kt in range(8):
                b = kt % 2
                r0 = s_ * 1024 + kt * 128
                ld(cat[b], cckv_d[r0:r0 + 128, :]); ld(cpt[b], ckpe_d[r0:r0 + 128, :])
                vcopy(catb[b], cat[b], eng="pool"); vcopy(cptb[b][:, 0:64], cpt[b], eng="pool")
                bb = banks[1].cast(BF16)
                for kc in range(2):
                    tr(bb[:, kc * 128:(kc + 1) * 128], catb[b][:, kc * 128:(kc + 1) * 128], identb)
                tr(bb[:, 256:384], cptb[b], identb)
                acopy(ckvTc[:, :, kt * 128:(kt + 1) * 128], bb[:, 0:256].re("p (a b) -> p a b", a=2))
                acopy(kpeTc[0:64, kt * 128:(kt + 1) * 128], bb[0:64, 256:384])
            for h in range(4):
                for n in range(2):
                    for kc in range(2):
                        mm(banks[2], W_kv[:, kc, h, 0:128], ckvTc[:, kc, n * 512:(n + 1) * 512], start=(kc == 0), stop=(kc == 1))
                    acopy(KTc[:, h, n * 512:(n + 1) * 512], banks[2])
            for kt in range(8):
                for kc in range(2):
                    mm(banks[3], ckvTc[:, kc, kt * 128:(kt + 1) * 128], W_kv[:, kc, :, 128:256], start=(kc == 0), stop=(kc == 1))
                vcopy(Vc[:, kt, :], banks[3])
            for kc in range(2):
                mm(banks[3][0:32, :], ckvTn[:, kc, s_ * 32:(s_ + 1) * 32], W_kv[:, kc, :, 128:256], start=(kc == 0), stop=(kc == 1))
            vcopy(Vn[0:32, :], banks[3][0:32, :])
            qs = slice(s_ * 32, (s_ + 1) * 32)
            for h in range(4):
                for n in range(2):
                    mm(SC[0:32, n * 512:(n + 1) * 512], QTn[:, isam, h, qs], KTc[:, h, n * 512:(n + 1) * 512], start=True, stop=False)
                    mm(SC[0:32, n * 512:(n + 1) * 512], QTr[0:64, isam, h, qs], kpeTc[0:64, n * 512:(n + 1) * 512], start=False, stop=True)
                mm(banks[4][0:32, 0:32], QTn[:, isam, h, qs], KTn[:, h, qs], start=True, stop=False)
                mm(banks[4][0:32, 0:32], QTr[0:64, isam, h, qs], kpeTn[0:64, qs], start=False, stop=True)
                acopy(scs[0:32, 0:1024], SC[0:32, :])
                acopy(scs[0:32, 1024:1056], banks[4][0:32, 0:32])
                red(mxs[0:32, :], scs[0:32, :], ALU.max)
                ts(negs[0:32, :], mxs[0:32, :], -1.0, None, ALU.mult)
                mset(Ps[0:32, 1024:1152], 0.0)
                act(Ps[0:32, 0:1056], scs[0:32, :], AF.Exp, bias=negs[0:32, :], accum=sums[0:32, :])
                pt_b = banks[5].cast(BF16)
                for kt in range(9):
                    tr(pt_b[:, kt * 32:(kt + 1) * 32], Ps[0:32, kt * 128:(kt + 1) * 128], identb[0:32, 0:32])
                vcopy(PTs.re("p a b -> p (a b)"), pt_b[:, 0:288])
                ov = banks[4][0:32, 128:256]
                for kt in range(8):
                    mm(ov, PTs[:, kt, :], Vc[:, kt, h * 128:(h + 1) * 128], start=(kt == 0), stop=False)
                mm(ov, PTs[0:32, 8, :], Vn[0:32, h * 128:(h + 1) * 128], start=False, stop=True)
                recip(sums[0:32, :], sums[0:32, :])
                ts(osb[0:32, h * 128:(h + 1) * 128], ov, sums[0:32, :], None, ALU.mult)
            ld(Osam[s_ * 32:(s_ + 1) * 32, :], osb[0:32, :])

        barrier()
        bump[0] = mark_after_mixer_state

        W_glu = alloc([128, 4, D], BF16, "W_glu"); W_o = alloc([128, 8, D], BF16, "W_o")
        ldw(W_glu, w_glu_d.re("(kt p) n -> p kt n", p=128)); ldw(W_o, w_o_d.re("(kt p) n -> p kt n", p=128))
        gos = alloc([128, 512], F32, "gos"); gom = alloc([128, 512], F32, "gom"); ld(gos, gos_d); ld(gom, gom_d)
        lng = alloc([128, D], F32, "lng"); lnb = alloc([128, D], F32, "lnb")
        ld(lng, lng_d[:, 0:D]); ld(lnb, lnb_d[:, 0:D])
        btab2 = Buf("tab2")
        for nm_ in ("S_t", "C_t", "R_t"):
            v_ = alloc([128, 16, 128], F32, nm_ + "2"); v_.b = btab2
            ld(v_.re("p a b -> p (a b)"), tab_d[nm_]); ssmx[nm_] = v_
        for nm_ in ("WBre", "WBim"):
            v_ = alloc([128, 16, 128], BF16, nm_ + "2")
            ld(v_.re("p a b -> p (a b)"), tab_d[nm_]); ssmx[nm_] = v_
        for nm_, src_ in (("Cre", cre_d), ("Cimn", cimn_d), ("Dblk", dblk_d)):
            v_ = alloc([128, 16, 32], BF16, nm_)
            ldw(v_.re("p a b -> p (a b)"), src_); ssmx[nm_] = v_
        ssmx["m4"] = [[alloc([128, 256], F32) for _ in range(4)] for _ in range(2)]
        ssmx["wri"] = [[alloc([128, 256], F32) for _ in range(2)] for _ in range(2)]
        ssmx["zri"] = [[alloc([128, 2, 128], F32) for _ in range(2)] for _ in range(2)]
        ssmx["xs4"] = [alloc([128, 16], F32) for _ in range(4)]
        ssmx["zl_all"] = [alloc([128, 4, 16], F32) for _ in range(2)]
        ssmx["dmd"] = [[alloc([128, 256], F32) for _ in range(4)]] * 2
        ssmx["xrb"] = [alloc([128, 2, 128], BF16) for _ in range(2)]
        ssmx["xib"] = [alloc([128, 2, 128], BF16) for _ in range(2)]
        S0 = alloc([128, 2, 4, 16], F32, "S0"); ld(S0.re("p a b c -> p (a b c)"), s0_d)
        Sfin = alloc([128, 2, 4, 16], F32, "Sfin")
        xt = [alloc([128, D], F32, "axt%d" % i) for i in range(NB)]
        xT = [alloc([128, 8, 128], BF16, "axT%d" % i) for i in range(NB)]
        uT = [alloc([128, 4, 128], BF16, "auT%d" % i) for i in range(NB)]
        ysq = alloc([128, 512], F32, "ysq"); yt = alloc([128, 512], F32, "yt"); ysg = alloc([128, 512], F32, "ysg")
        glb = alloc([128, 512], BF16, "glb"); gT = alloc([128, 4, 128], BF16, "gT")
        sg2 = ysq; osf = yt
        mixb = alloc([128, D], BF16, "mixb"); mixT = alloc([128, 8, 128], BF16, "mixT")
        rl = alloc([128, 4], F32, "rl"); omf = alloc([128, 4, 128], F32, "omf")
        ss_a = alloc([128, 1], F32); sq = ysg
        rres = alloc([128, D], F32, "rres"); hout = alloc([128, D], F32, "hout")
        stats = alloc([128, 12], F32); mvv = alloc([128, 2], F32)

        def pre_stages(i):
            b = i % NB
            yb = banks[3] if i % 2 == 0 else banks[0]

            def p0():
                load_xT(xo[i * 128:(i + 1) * 128, :], xt[b], xT[b], banks[1])
                for kt in range(4):
                    for k in range(8):
                        mm(banks[1][:, kt * 128:(kt + 1) * 128], W_in_u[:, k, kt * 128:(kt + 1) * 128], xT[b][:, k, :], start=(k == 0), stop=(k == 7))
                acopy(uT[b].re("p a b -> p (a b)"), banks[1])

            if i < 16:
                rr_ = ssm_rounds(uT[b], (banks[4], banks[5]), lambda gp, s_, part, i=i: Xown[:, i, part * 16 + gp:part * 16 + gp + 1], 1, ybank=yb)
            else:
                rr_ = ssm_rounds(uT[b], (banks[4], banks[5]), lambda gp, s_, part: S0[:, part, s_, gp:gp + 1], 4, ybank=yb,
                                 last_out=lambda s_: (Sfin[:, 0, s_, :], Sfin[:, 1, s_, :]))
                rr_.append(lambda: ld(ssms_o, Sfin.re("p a b c -> p (a b c)")))
            return [p0] + rr_

        def post_stages(i):
            b = i % NB
            yb = banks[3] if i % 2 == 0 else banks[0]
            xt_ = xt[b]

            def q0():
                act(ysq, yb, AF.Square)
                ts(yt, ysq, 0.044715, 1.0, ALU.mult, ALU.add)
                tt(yt, yt, yb, ALU.mult)

            def q1():
                act(ysg, yt, AF.Sigmoid, scale=1.5957691216057308)
                tt(glb, ysg, yb, ALU.mult)

            def q2():
                transpose_to(gT, glb, 4, banks[2], eng="act")

            def q3():
                for nb in range(2):
                    for k in range(4):
                        mm(SC[:, nb * 512:(nb + 1) * 512], gT[:, k, :], W_glu[:, k, nb * 512:(nb + 1) * 512], start=(k == 0), stop=(k == 3))

            def q4():
                act(sg2, SC[:, 512:1024], AF.Sigmoid)
                tt(osf, sg2, SC[:, 0:512], ALU.mult)

            def q5():
                rmsnorm(mixb[:, 0:512], osf, 512, gos, sq, ss_a)

            def q6():
                if i < 16:
                    recip(rl, l_run[:, i * 4:(i + 1) * 4])
                    tt(omf, Oacc[:, i, :, :], rl.un(2).bc([128, 4, 128]), ALU.mult)
                    rmsnorm(mixb[:, 512:1024], omf.re("p a b -> p (a b)"), 512, gom, sq, ss_a)
                else:
                    rmsnorm(mixb[:, 512:1024], Osam, 512, gom, sq, ss_a)

            def q7():
                for half in range(2):
                    transpose_to(mixT[:, half * 4:(half + 1) * 4, :], mixb[:, half * 512:(half + 1) * 512], 4, banks[2], eng=("act" if half == 0 else "dve"))

            def q8():
                for nb in range(2):
                    for k in range(8):
                        mm(SC[:, nb * 512:(nb + 1) * 512], mixT[:, k, :], W_o[:, k, nb * 512:(nb + 1) * 512], start=(k == 0), stop=(k == 7))

            def q9():
                stt(rres, xt_, ALPHA, SC, ALU.mult, ALU.add)
                layernorm(hout, rres, lng, lnb, stats, mvv)
                ld(h1_d[i * 128:(i + 1) * 128, :], hout)

            return [q0, q1, q2, q3, q4, q5, q6, q7, q8, q9]

        prev_post = []
        for i in range(NOWN + 1):
            pre = pre_stages(i) if i < NOWN else []
            n_ = max(len(pre), len(prev_post))
            for k_ in range(n_):
                if k_ < len(pre):
                    pre[k_]()
                if k_ < len(prev_post):
                    prev_post[k_]()
            prev_post = post_stages(i) if i < NOWN else []

        barrier()
        bump[0] = mark_after_mem

        W_xq = alloc([128, 8, D], BF16, "W_xq"); W_xo = alloc([128, 8, D], BF16, "W_xo")
        ldw(W_xq, w_xq_d.re("(kt p) n -> p kt n", p=128)); ldw(W_xo, w_xo_d.re("(kt p) n -> p kt n", p=128))
        lng = alloc([128, D], F32, "lng2"); lnb = alloc([128, D], F32, "lnb2")
        ld(lng, lng_d[:, D:2 * D]); ld(lnb, lnb_d[:, D:2 * D])
        hin = [alloc([128, D], F32, "bh%d" % i) for i in range(NB)]
        hb2 = [alloc([128, D], BF16, "bhb%d" % i) for i in range(2)]; hT2 = [alloc([128, 8, 128], BF16, "bhT%d" % i) for i in range(2)]
        qxb2 = [alloc([128, D], BF16, "qxb%d" % i) for i in range(2)]; qxT2 = [alloc([128, 8, 128], BF16, "qxT%d" % i) for i in range(2)]
        mx42 = [alloc([128, 4], F32) for _ in range(2)]; neg42 = [alloc([128, 4], F32) for _ in range(2)]; sum42 = [alloc([128, 4], F32) for _ in range(2)]
        Px2 = [alloc([128, 4, 256], BF16, "Px%d" % i) for i in range(2)]; PxT2 = [alloc([128, 8, 128], BF16, "PxT%d" % i) for i in range(2)]
        oxb2 = [alloc([128, D], BF16, "oxb%d" % i) for i in range(2)]; oxT2 = [alloc([128, 8, 128], BF16, "oxT%d" % i) for i in range(2)]
        rres2 = [alloc([128, D], F32, "brres%d" % i) for i in range(2)]; hout2 = [alloc([128, D], F32, "bhout%d" % i) for i in range(2)]
        stats2 = [alloc([128, 12], F32) for _ in range(2)]; mvv2 = [alloc([128, 2], F32) for _ in range(2)]
        hb_ = hb2[0]; hT = hT2[0]; qxb = qxb2[0]; qxT = qxT2[0]; mx4 = mx42[0]; neg4 = neg42[0]; sum4 = sum42[0]
        Px = Px2[0]; PxT = PxT2[0]; oxb = oxb2[0]; oxT = oxT2[0]; rres = rres2[0]; hout = hout2[0]; stats = stats2[0]; mvv = mvv2[0]
        cmk = [alloc([128, D], F32, "cmk%d" % i) for i in range(2)]
        cmkb = [alloc([128, D], BF16, "cmkb%d" % i) for i in range(1)]
        mkTs4 = [alloc([128, 4, 2, 256], BF16, "mkTs%d" % i) for i in range(4)]; mvs4 = [alloc([128, 2, D], BF16, "mvs%d" % i) for i in range(4)]
        scx = alloc([128, 8], F32, "scx"); oxs = alloc([128, D], BF16, "oxs")

        def xa_stages(i, p):
            A_ = banks[p]; C_ = dbl[1 + p]
            Ab = A_.cast(BF16)

            def tr8(src):
                for k in range(8):
                    tr(Ab[:, k * 128:(k + 1) * 128], src[:, k * 128:(k + 1) * 128], identb)

            def ev8(dst):
                acopy(dst[:, 0:4, :], Ab[:, 0:512].re("p (k n) -> p k n", k=4))
                vcopy(dst[:, 4:8, :], Ab[:, 512:1024].re("p (k n) -> p k n", k=4))

            def t0():
                ld(hin[p], h1_d[i * 128:(i + 1) * 128, :])
                vcopy(hb2[p], hin[p], eng="pool")

            def t3():
                for nb in range(2):
                    for k in range(8):
                        mm(C_[:, nb * 512:(nb + 1) * 512], hT2[p][:, k, :], W_xq[:, k, nb * 512:(nb + 1) * 512], start=(k == 0), stop=(k == 7))

            def t4():
                for nb in range(2):
                    P.op("act", lambda e, nb=nb: e.mul(out=qxb2[p].ap[:, nb * 512:(nb + 1) * 512], in_=C_.ap[:, nb * 512:(nb + 1) * 512], mul=X_SCALE),
                         reads=[C_.b], writes=[qxb2[p].b])

            def t7():
                for h in range(4):
                    for et in range(2):
                        mm(C_[:, h * 256:(h + 1) * 256], qxT2[p][:, h * 2 + et, :], mkT[:, h, et, :], start=(et == 0), stop=(et == 1))

            def t8():
                red(mx42[p], C_.re("p (h m) -> p h m", h=4), ALU.max)
                ts(neg42[p], mx42[p], -1.0, None, ALU.mult)
                for h in range(4):
                    act(Px2[p][:, h, :], C_[:, h * 256:(h + 1) * 256], AF.Exp, bias=neg42[p][:, h:h + 1], accum=sum42[p][:, h:h + 1])

            def t11():
                for h in range(4):
                    for mt in range(2):
                        mm(C_[:, h * 256:(h + 1) * 256], PxT2[p][:, h * 2 + mt, :], mvb[:, mt, h * 256:(h + 1) * 256], start=(mt == 0), stop=(mt == 1))

            def t12():
                recip(sum42[p], sum42[p])
                tt(oxb2[p].re("p (h e) -> p h e", h=4), C_.re("p (h e) -> p h e", h=4), sum42[p].un(2).bc([128, 4, 256]), ALU.mult)

            def t15():
                for nb in range(2):
                    for k in range(8):
                        mm(C_[:, nb * 512:(nb + 1) * 512], oxT2[p][:, k, :], W_xo[:, k, nb * 512:(nb + 1) * 512], start=(k == 0), stop=(k == 7))

            def t16():
                stt(rres2[p], hin[p], ALPHA, C_, ALU.mult, ALU.add)
                layernorm(hout2[p], rres2[p], lng, lnb, stats2[p], mvv2[p])
                ld(h2_d[i * 128:(i + 1) * 128, :], hout2[p])

            return [t0, lambda: tr8(hb2[p]), lambda: ev8(hT2[p]), t3, t4, lambda: tr8(qxb2[p]), lambda: ev8(qxT2[p]), t7, t8,
                    lambda: tr8(Px2[p].re("p a b -> p (a b)")), lambda: ev8(PxT2[p]), t11, t12, lambda: tr8(oxb2[p]), lambda: ev8(oxT2[p]), t15, t16]

        def prep_items():
            items = []
            for s_ in range(4):
                for mt in range(2):
                    r0 = s_ * 256 + mt * 128

                    def l0(r0=r0):
                        ld(cmk[0], cmk_d[r0:r0 + 128, :]); ld(cmk[1], cmv_d[r0:r0 + 128, :])

                    def l1(s_=s_, mt=mt):
                        vcopy(cmkb[0], cmk[0], eng="pool")
                        vcopy(mvs4[s_][:, mt, :], cmk[1], eng="pool")

                    def l2(s_=s_, mt=mt):
                        for half in range(2):
                            bb = banks[6 + half].cast(BF16)
                            for k in range(4):
                                kk = half * 4 + k
                                tr(bb[:, k * 128:(k + 1) * 128], cmkb[0][:, kk * 128:(kk + 1) * 128], identb)

                    def l3(s_=s_, mt=mt):
                        for half in range(2):
                            bb = banks[6 + half].cast(BF16)
                            acopy(mkTs4[s_][:, half * 2:half * 2 + 2, :, mt * 128:(mt + 1) * 128].re("p h e m -> p (h e) m"), bb[:, 0:512].re("p (k m) -> p k m", k=4))

                    items += [l0, l1, l2, l3]
            return items

        pitems = prep_items()
        pi_ = 0
        slot = 0
        for ip in range(8):
            sa = xa_stages(2 * ip, 0); sb_ = xa_stages(2 * ip + 1, 1)
            for a_, b_ in zip(sa, sb_):
                a_(); b_()
                slot += 1
                if slot % 4 == 0 and pi_ < len(pitems):
                    pitems[pi_](); pi_ += 1
        while pi_ < len(pitems):
            pitems[pi_](); pi_ += 1

        for i in range(16, NOWN):
            b = i % NB
            ld(hin[b], h1_d[i * 128:(i + 1) * 128, :])
            vcopy(hb_, hin[b], eng="pool")
            for half in range(2):
                transpose_to(hT[:, half * 4:(half + 1) * 4, :], hb_[:, half * 512:(half + 1) * 512], 4, banks[0], eng=("act" if half == 0 else "dve"))
            for nb in range(2):
                for k in range(8):
                    mm(banks[1 + nb], hT[:, k, :], W_xq[:, k, nb * 512:(nb + 1) * 512], start=(k == 0), stop=(k == 7))
                P.op("act", lambda e, nb=nb: e.mul(out=qxb.ap[:, nb * 512:(nb + 1) * 512], in_=banks[1 + nb].ap, mul=X_SCALE), reads=[banks[1 + nb].b], writes=[qxb.b])
            for half in range(2):
                transpose_to(qxT[:, half * 4:(half + 1) * 4, :], qxb[:, half * 512:(half + 1) * 512], 4, banks[3], eng=("act" if half == 0 else "dve"))
            if i < 16:
                for h in range(4):
                    for et in range(2):
                        mm(SC[:, h * 256:(h + 1) * 256], qxT[:, h * 2 + et, :], mkT[:, h, et, :], start=(et == 0), stop=(et == 1))
                red(mx4, SC.re("p (h m) -> p h m", h=4), ALU.max)
                ts(neg4, mx4, -1.0, None, ALU.mult)
                for h in range(4):
                    act(Px[:, h, :], SC[:, h * 256:(h + 1) * 256], AF.Exp, bias=neg4[:, h:h + 1], accum=sum4[:, h:h + 1])
                for half in range(2):
                    transpose_to(PxT[:, half * 4:(half + 1) * 4, :], Px.re("p a b -> p (a b)")[:, half * 512:(half + 1) * 512], 4, banks[4], eng=("act" if half == 0 else "dve"))
                for h in range(4):
                    for mt in range(2):
                        mm(SC[:, h * 256:(h + 1) * 256], PxT[:, h * 2 + mt, :], mvb[:, mt, h * 256:(h + 1) * 256], start=(mt == 0), stop=(mt == 1))
                recip(sum4, sum4)
                tt(oxb.re("p (h e) -> p h e", h=4), SC.re("p (h e) -> p h e", h=4), sum4.un(2).bc([128, 4, 256]), ALU.mult)
            else:
                for s_ in range(4):
                    qs = slice(s_ * 32, (s_ + 1) * 32)
                    mkTs = mkTs4[s_]; mvs = mvs4[s_]
                    for h in range(4):
                        for et in range(2):
                            mm(SC[0:32, h * 256:(h + 1) * 256], qxT[:, h * 2 + et, qs], mkTs[:, h, et, :], start=(et == 0), stop=(et == 1))
                    red(mx4[0:32, :], SC[0:32, :].re("p (h m) -> p h m", h=4), ALU.max)
                    ts(neg4[0:32, :], mx4[0:32, :], -1.0, None, ALU.mult)
                    for h in range(4):
                        act(Px[0:32, h, :], SC[0:32, h * 256:(h + 1) * 256], AF.Exp, bias=neg4[0:32, h:h + 1], accum=sum4[0:32, h:h + 1])
                    bb = banks[4].cast(BF16)
                    for k in range(8):
                        tr(bb[:, k * 32:(k + 1) * 32], Px.re("p a b -> p (a b)")[0:32, k * 128:(k + 1) * 128], identb[0:32, 0:32])
                    vcopy(PxT[:, :, 0:32], bb[:, 0:256].re("p (k q) -> p k q", k=8))
                    for h in range(4):
                        for mt in range(2):
                            mm(SC[0:32, h * 256:(h + 1) * 256], PxT[:, h * 2 + mt, 0:32], mvs[:, mt, h * 256:(h + 1) * 256], start=(mt == 0), stop=(mt == 1))
                    recip(sum4[0:32, :], sum4[0:32, :])
                    tt(oxs[0:32, :].re("p (h e) -> p h e", h=4), SC[0:32, :].re("p (h e) -> p h e", h=4), sum4[0:32, :].un(2).bc([32, 4, 256]), ALU.mult)
                    ld(oxb[s_ * 32:(s_ + 1) * 32, :], oxs[0:32, :])
            for half in range(2):
                transpose_to(oxT[:, half * 4:(half + 1) * 4, :], oxb[:, half * 512:(half + 1) * 512], 4, banks[5], eng=("act" if half == 0 else "dve"))
            for nb in range(2):
                for k in range(8):
                    mm(banks[1 + nb], oxT[:, k, :], W_xo[:, k, nb * 512:(nb + 1) * 512], start=(k == 0), stop=(k == 7))
            for nb in range(2):
                stt(rres[:, nb * 512:(nb + 1) * 512], hin[b][:, nb * 512:(nb + 1) * 512], ALPHA, banks[1 + nb], ALU.mult, ALU.add)
            layernorm(hout, rres, lng, lnb, stats, mvv)
            ld(h2_d[i * 128:(i + 1) * 128, :], hout)

        barrier()
        bump[0] = mark_after_mem

        W1 = alloc([128, 8, 4096], BF16, "W1"); W2 = alloc([128, 32, D], BF16, "W2")
        for c in range(2):
            ldw(W1[:, :, c * 2048:(c + 1) * 2048], w_ff1_d.re("(kt p) n -> p kt n", p=128)[:, :, c * 2048:(c + 1) * 2048])
        for c in range(4):
            ldw(W2[:, c * 8:(c + 1) * 8, :], w_ff2_d.re("(kt p) n -> p kt n", p=128)[:, c * 8:(c + 1) * 8, :])
        lng = alloc([128, D], F32, "lng3"); lnb = alloc([128, D], F32, "lnb3")
        ld(lng, lng_d[:, 2 * D:3 * D]); ld(lnb, lnb_d[:, 2 * D:3 * D])
        hin = alloc([128, D], F32, "ch")
        hb_ = alloc([128, D], BF16, "chb"); hT = alloc([128, 8, 512], BF16, "chT")
        zr = [alloc([128, 512], F32, "zr%d" % i) for i in range(2)]; zT = alloc([128, 32, 512], BF16, "zT")
        rres = alloc([128, D], F32, "crres"); hout = alloc([128, D], F32, "chout")
        stats = alloc([128, 12], F32); mvv = alloc([128, 2], F32)
        groups = [list(range(g_ * 4, g_ * 4 + 4)) for g_ in range(4)] + [[16]]
        for grp in groups:
            nt = len(grp)
            for q_, i in enumerate(grp):
                ld(hin, h2_d[i * 128:(i + 1) * 128, :])
                vcopy(hb_, hin, eng="pool")
                for half in range(2):
                    transpose_to(hT[:, half * 4:(half + 1) * 4, q_ * 128:(q_ + 1) * 128], hb_[:, half * 512:(half + 1) * 512], 4, banks[0], eng=("act" if half == 0 else "dve"))
            W_ = nt * 128
            for f in range(32):
                bk = banks[1 + f % 2]
                for k in range(8):
                    mm(bk[:, 0:W_], W1[:, k, f * 128:(f + 1) * 128], hT[:, k, 0:W_], start=(k == 0), stop=(k == 7))
                act(zr[f % 2][:, 0:W_], bk[:, 0:W_], AF.Relu)
                tt(zT[:, f, 0:W_], zr[f % 2][:, 0:W_], zr[f % 2][:, 0:W_], ALU.mult)
            for q_, i in enumerate(grp):
                for nb in range(2):
                    for f in range(32):
                        mm(SC[:, nb * 512:(nb + 1) * 512], zT[:, f, q_ * 128:(q_ + 1) * 128], W2[:, f, nb * 512:(nb + 1) * 512], start=(f == 0), stop=(f == 31))
                ld(hin, h2_d[i * 128:(i + 1) * 128, :])
                stt(rres, hin, ALPHA, SC, ALU.mult, ALU.add)
                layernorm(hout, rres, lng, lnb, stats, mvv)
                ld(y_o[i * 128:(i + 1) * 128, :], hout)

        P.emit(st)
    return nc


def _lay_gp(a):
    sh = a.shape[2:]
    n = len(sh)
    return np.ascontiguousarray(a.reshape(16, 2, 64, *sh).transpose(1, 2, 0, *range(3, 3 + n)).reshape(128, 16, *sh))


def _bc(v, n=128):
    return np.ascontiguousarray(np.broadcast_to(np.asarray(v, np.float32).reshape(1, -1), (n, np.asarray(v).size)))


def _rope_tables(pos):
    inv = (10000.0 ** (-np.arange(32, dtype=np.float32) / 32)).astype(np.float32)
    ang = pos.astype(np.float32)[:, None] * inv[None, :]
    c = np.cos(ang).astype(np.float32)
    s = np.sin(ang).astype(np.float32)
    return np.concatenate([c, c], 1), np.concatenate([-s, s], 1)


_NC_CACHE = {}


def kernel(x_prompt, x_sample, mem_prompt, cache_mla_ckv, cache_mla_kpe, state_ssm_re, state_ssm_im,
           cache_mem_k, cache_mem_v, w_in, g_q, w_q_up, g_kv, w_kv_up, a_re, a_im, b_re, b_im, c_re, c_im,
           d_skip, log_dt, w_glu, g_out_ssm, g_out_mla, w_o, w_xq, w_xk, w_xv, w_xo, w_ff1, w_ff2, ln_g, ln_b):
    f = lambda a: np.ascontiguousarray(np.asarray(a, dtype=np.float32))
    x_prompt = f(x_prompt); x_sample = f(x_sample)
    xp = x_prompt[0]
    cre = np.zeros((128, 16, 32), np.float32); cimn = np.zeros((128, 16, 32), np.float32)
    cr = f(c_re)[0].reshape(16, 2, 16, 64); ci = f(c_im)[0].reshape(16, 2, 16, 64)
    for g2 in range(2):
        cre[g2 * 64:(g2 + 1) * 64, :, g2 * 16:(g2 + 1) * 16] = cr[:, g2].transpose(2, 0, 1)
        cimn[g2 * 64:(g2 + 1) * 64, :, g2 * 16:(g2 + 1) * 16] = -ci[:, g2].transpose(2, 0, 1)
    dblk = np.zeros((128, 16, 32), np.float32)
    dd = f(d_skip)[0].reshape(512)
    for gp in range(16):
        for c in range(32):
            ch = gp * 32 + c
            dblk[ch % 128, gp, c] = dd[ch]
    pos_p = np.arange(16384)
    rcp, rsp = _rope_tables(pos_p)
    common = {
        "xp": xp, "mem": f(mem_prompt)[0], "ident": np.eye(128, dtype=np.float32),
        "w_in": f(w_in)[0], "w_q": f(w_q_up)[0].reshape(384, 768), "w_kv": f(w_kv_up)[0].reshape(256, 1024),
        "w_glu": f(w_glu)[0], "w_o": f(w_o)[0], "w_xq": f(w_xq)[0].reshape(D, D), "w_xk": f(w_xk)[0].reshape(D, D),
        "w_xv": f(w_xv)[0].reshape(D, D), "w_xo": f(w_xo)[0].reshape(D, D), "w_ff1": f(w_ff1)[0], "w_ff2": f(w_ff2)[0],
        "gq": _bc(f(g_q)[0]), "gkv": _bc(f(g_kv)[0]), "gos": _bc(f(g_out_ssm)[0]), "gom": _bc(f(g_out_mla)[0]),
        "lng": _bc(f(ln_g)[0].reshape(-1)), "lnb": _bc(f(ln_b)[0].reshape(-1)),
        "ropec_p": rcp, "ropes_p": rsp,
        "ar": _lay_gp(f(a_re)[0]), "ai": _lay_gp(f(a_im)[0]),
        "ldt": _lay_gp(np.ascontiguousarray(np.broadcast_to(f(log_dt)[0][:, None], (32, 64)))),
        "bre": _lay_gp(f(b_re)[0]).reshape(128, 256), "bim": _lay_gp(f(b_im)[0]).reshape(128, 256),
        "jj": _bc(np.arange(1, 129, dtype=np.float32)), "jj2": _bc(127.0 - np.arange(128, dtype=np.float32)),
        "cre": cre.reshape(128, 512), "cimn": cimn.reshape(128, 512), "dblk": dblk.reshape(128, 512),
    }
    qi = np.arange(128)[:, None]
    in_maps = []
    for c in range(NCORES):
        tiles = [8 * i + c for i in range(16)]
        xo = np.concatenate([xp[t * 128:(t + 1) * 128] for t in tiles] + [x_sample[4 * c:4 * c + 4].reshape(128, D)], 0)
        pos_o = np.concatenate([np.arange(t * 128, (t + 1) * 128) for t in tiles] + [np.tile(1024 + np.arange(32), 4)])
        rco, rso = _rope_tables(pos_o)
        kj = np.arange(1024)[None, :]
        vis = ((kj // 128) < c) | (((kj // 128) == c) & (((kj % 128) // 64) <= (qi // 64)))
        maskd = np.where(vis, 0.0, NEG).astype(np.float32)
        onehot = np.zeros((128, 8), np.float32); onehot[:, c] = 1.0
        s0 = np.stack([_lay_gp(f(state_ssm_re)[0, 4 * c + s]) for s in range(4)], 1)
        s0i = np.stack([_lay_gp(f(state_ssm_im)[0, 4 * c + s]) for s in range(4)], 1)
        m = dict(common)
        m.update({
            "xo": np.ascontiguousarray(xo), "ropec_o": rco, "ropes_o": rso, "maskd": maskd, "onehot": onehot,
            "s0": np.ascontiguousarray(np.stack([s0, s0i], 1).reshape(128, 128)),
            "cckv": f(cache_mla_ckv)[0, 4 * c:4 * c + 4].reshape(4096, 256),
            "ckpe": f(cache_mla_kpe)[0, 4 * c:4 * c + 4].reshape(4096, 64),
            "cmk": f(cache_mem_k)[0, 4 * c:4 * c + 4].reshape(1024, D),
            "cmv": f(cache_mem_v)[0, 4 * c:4 * c + 4].reshape(1024, D),
        })
        in_maps.append(m)
    if "nc" not in _NC_CACHE:
        _NC_CACHE["nc"] = build_nc()
    res = run_bass_kernel_spmd(_NC_CACHE["nc"], in_maps, core_ids=list(range(NCORES)))
    R = res.results
    y_p = np.zeros((1, 16384, D), np.float32); y_s = np.zeros((32, 32, D), np.float32)
    ckv_s = np.zeros((1, 32, 32, 256), np.float32); kpe_s = np.zeros((1, 32, 32, 64), np.float32)
    sre_s = np.zeros((1, 32, 32, 64), np.float32); sim_s = np.zeros((1, 32, 32, 64), np.float32)

    def unlay(a):
        return a.reshape(2, 64, 16).transpose(2, 0, 1).reshape(32, 64)

    for c in range(NCORES):
        yo = R[c]["y_o"]
        for i in range(16):
            t = 8 * i + c
            y_p[0, t * 128:(t + 1) * 128] = yo[i * 128:(i + 1) * 128]
        y_s[4 * c:4 * c + 4] = yo[16 * 128:].reshape(4, 32, D)
        ckv_s[0, 4 * c:4 * c + 4] = R[c]["ckvs_o"].reshape(4, 32, 256)
        kpe_s[0, 4 * c:4 * c + 4] = R[c]["kpes_o"].reshape(4, 32, 64)
        sf = R[c]["ssms_o"].reshape(128, 2, 4, 16)
        for s in range(4):
            sre_s[0, 4 * c + s] = unlay(sf[:, 0, s, :])
            sim_s[0, 4 * c + s] = unlay(sf[:, 1, s, :])
    r0 = R[0]
    ckv_p = r0["ckv_o"].reshape(1, 1, 16384, 256); kpe_p = r0["kpe_o"].reshape(1, 1, 16384, 64)
    sp = r0["ssmp_o"]
    sre_p = unlay(sp[:, 0:16]).reshape(1, 1, 32, 64); sim_p = unlay(sp[:, 16:32]).reshape(1, 1, 32, 64)
    mk_p = r0["memk_o"].reshape(1, 1, 256, 4, 256); mv_p = r0["memv_o"].reshape(1, 1, 256, 4, 256)
    return (y_p, y_s, ckv_p, kpe_p, sre_p, sim_p, mk_p, mv_p, ckv_s, kpe_s, sre_s, sim_s)
```

```python
import math
from contextlib import ExitStack

import numpy as np
import concourse.bass as bass
import concourse.mybir as mybir
from concourse.bass_utils import run_bass_kernel_spmd

F32 = mybir.dt.float32
BF16 = mybir.dt.bfloat16
I32 = mybir.dt.int32
AF = mybir.ActivationFunctionType
ALU = mybir.AluOpType
AX = mybir.AxisListType

NCORES = 8
D = 1024
NTP = 128
NOWN = 17
EPS = 1e-5
ALPHA = 2.0 ** 0.25
MLA_SCALE = 192.0 ** -0.5
X_SCALE = 256.0 ** -0.5
TWO_PI = 2.0 * math.pi
NEG = -1e30
NRING = 24
COMPUTE = ("pe", "act", "dve", "pool")


class Buf:
    __slots__ = ("name", "lw", "rd")

    def __init__(self, name=""):
        self.name = name
        self.lw = None
        self.rd = []


def _flat(bs):
    out = []
    for b in bs:
        if isinstance(b, (tuple, list)):
            out.extend(b)
        else:
            out.append(b)
    return out


class Op:
    __slots__ = ("eng", "fn", "deps", "isdma", "flag", "cnt", "ring", "n")

    def __init__(self, eng, fn, isdma):
        self.eng = eng
        self.fn = fn
        self.isdma = isdma
        self.deps = set()
        self.flag = False
        self.cnt = 0
        self.ring = None
        self.n = 0


class Prog:
    def __init__(self, nc):
        self.nc = nc
        self.ops = {e: [] for e in ("pe", "act", "dve", "pool", "sp")}
        self.ndma = {e: 0 for e in self.ops}
        self.allops = []
        self.floor = []
        self.dmas_since = []

    def _add(self, eng, fn, reads, writes, isdma):
        op = Op(eng, fn, isdma)
        reads = _flat(reads)
        writes = _flat(writes)
        for f in self.floor:
            op.deps.add(f)
        for b in reads:
            if b.lw is not None:
                op.deps.add(b.lw)
        for b in writes:
            if b.lw is not None:
                op.deps.add(b.lw)
            for r in b.rd:
                op.deps.add(r)
        for b in reads:
            b.rd.append(op)
        for b in writes:
            b.lw = op
            b.rd = []
        op.deps.discard(op)
        if isdma:
            op.n = self.ndma[eng]
            self.ndma[eng] += 1
            self.dmas_since.append(op)
        self.ops[eng].append(op)
        self.allops.append(op)
        return op

    def op(self, eng, fn, reads=(), writes=()):
        return self._add(eng, fn, reads, writes, False)

    def dma(self, eng, fn, reads=(), writes=()):
        return self._add(eng, fn, reads, writes, True)

    def barrier(self, fn):
        op = Op("pool", fn, False)
        for f in self.floor:
            op.deps.add(f)
        for e in COMPUTE:
            for o in reversed(self.ops[e]):
                if not o.isdma:
                    op.deps.add(o)
                    break
        for o in self.dmas_since:
            op.deps.add(o)
        self.dmas_since = []
        self.ops["pool"].append(op)
        self.allops.append(op)
        self.floor = [op]

    def emit(self, stack):
        nc = self.nc
        for op in self.allops:
            for d in op.deps:
                if d.eng == "pe" and op.eng == "pe" and not d.isdma:
                    continue
                d.flag = True
        sems = {e: stack.enter_context(nc.semaphore("s_" + e)) for e in COMPUTE}
        rings = {}
        for e in self.ops:
            if self.ndma[e]:
                rings[e] = [stack.enter_context(nc.semaphore("r_%s_%d" % (e, i))) for i in range(NRING)]
        for e in self.ops:
            c = 0
            for op in self.ops[e]:
                if op.isdma:
                    op.ring = rings[e][op.n % NRING]
                    op.cnt = 16 * (op.n // NRING + 1)
                elif op.flag:
                    c += 1
                    op.cnt = c
        block = stack.enter_context(nc.Block())

        def run(e, h):
            waited = {}

            def wait(sem, val):
                k = id(sem)
                if waited.get(k, 0) >= val:
                    return
                waited[k] = val
                h.wait_ge(sem, val)

            for op in self.ops[e]:
                for d in op.deps:
                    if d.isdma:
                        wait(d.ring, d.cnt)
                    else:
                        if d.eng == "pe" and e == "pe":
                            continue
                        wait(sems[d.eng], d.cnt)
                if op.isdma and op.n >= NRING:
                    wait(op.ring, op.cnt - 16)
                ins = op.fn(h)
                if op.isdma:
                    ins.then_inc(op.ring, 16)
                elif op.flag:
                    ins.then_inc(sems[e], 1)
            if e in rings:
                n = self.ndma[e]
                for i in range(min(n, NRING)):
                    last = ((n - 1 - i) // NRING) * NRING + i
                    wait(rings[e][i], 16 * (last // NRING + 1))

        @block.tensor
        def _(h):
            run("pe", h)

        @block.scalar
        def _(h):
            run("act", h)

        @block.vector
        def _(h):
            run("dve", h)

        @block.gpsimd
        def _(h):
            run("pool", h)

        @block.sync
        def _(h):
            run("sp", h)


class V:
    __slots__ = ("ap", "b")

    def __init__(self, ap, b):
        self.ap = ap
        self.b = b

    def __getitem__(self, idx):
        return V(self.ap[idx], self.b)

    def re(self, pat, **kw):
        return V(self.ap.rearrange(pat, **kw), self.b)

    def cast(self, dt):
        return V(self.ap.bitcast(dt), self.b)

    def bc(self, shape):
        return V(self.ap.broadcast_to(shape), self.b)

    def un(self, ax):
        return V(self.ap.unsqueeze(ax), self.b)


def build_nc():
    nc = bass.Bass("TRN2", target_bir_lowering=False)
    P = Prog(nc)
    dram = {}

    def din(name, shape, dt=F32):
        t = nc.dram_tensor(name, list(shape), dt, kind="ExternalInput")
        dram[name] = V(t.ap(), Buf(name))
        return dram[name]

    def dout(name, shape):
        t = nc.dram_tensor(name, list(shape), F32, kind="ExternalOutput")
        dram[name] = V(t.ap(), Buf(name))
        return dram[name]

    def dscr(name, shape, dt=F32):
        t = nc.dram_tensor(name, list(shape), dt)
        return V(t.ap(), Buf(name))

    xp = din("xp", [NTP * 128, D])
    xo = din("xo", [NOWN * 128, D])
    mem = din("mem", [256, D])
    ident_d = din("ident", [128, 128])
    w_in_d = din("w_in", [D, 1216]); w_q_d = din("w_q", [384, 768]); w_kv_d = din("w_kv", [256, 1024])
    w_glu_d = din("w_glu", [512, 1024]); w_o_d = din("w_o", [D, D]); w_xq_d = din("w_xq", [D, D])
    w_xk_d = din("w_xk", [D, D]); w_xv_d = din("w_xv", [D, D]); w_xo_d = din("w_xo", [D, D])
    w_ff1_d = din("w_ff1", [D, 4096]); w_ff2_d = din("w_ff2", [4096, D])
    gq_d = din("gq", [128, 384]); gkv_d = din("gkv", [128, 256]); gos_d = din("gos", [128, 512]); gom_d = din("gom", [128, 512])
    lng_d = din("lng", [128, 3 * D]); lnb_d = din("lnb", [128, 3 * D])
    ropec_p = din("ropec_p", [NTP * 128, 64]); ropes_p = din("ropes_p", [NTP * 128, 64])
    ropec_o = din("ropec_o", [NOWN * 128, 64]); ropes_o = din("ropes_o", [NOWN * 128, 64])
    maskd_d = din("maskd", [128, 1024]); onehot_d = din("onehot", [128, 8])
    ar_d = din("ar", [128, 16]); ai_d = din("ai", [128, 16]); ldt_d = din("ldt", [128, 16])
    bre_d = din("bre", [128, 256]); bim_d = din("bim", [128, 256]); jj_d = din("jj", [128, 128]); jj2_d = din("jj2", [128, 128])
    cre_d = din("cre", [128, 512]); cimn_d = din("cimn", [128, 512]); dblk_d = din("dblk", [128, 512])
    s0_d = din("s0", [128, 128])
    cckv_d = din("cckv", [4 * 1024, 256]); ckpe_d = din("ckpe", [4 * 1024, 64])
    cmk_d = din("cmk", [4 * 256, D]); cmv_d = din("cmv", [4 * 256, D])

    y_o = dout("y_o", [NOWN * 128, D])
    ckv_o = dout("ckv_o", [NTP * 128, 256]); kpe_o = dout("kpe_o", [NTP * 128, 64])
    ssmp_o = dout("ssmp_o", [128, 32])
    memk_o = dout("memk_o", [256, D]); memv_o = dout("memv_o", [256, D])
    ckvs_o = dout("ckvs_o", [128, 256]); kpes_o = dout("kpes_o", [128, 64]); ssms_o = dout("ssms_o", [128, 128])

    h1_d = dscr("h1_d", [NOWN * 128, D]); h2_d = dscr("h2_d", [NOWN * 128, D])
    tab_d = {n_: dscr("tab_" + n_, [128, 2048]) for n_ in ("S_t", "C_t", "R_t")}
    tab_d["WBre"] = dscr("tab_WBre", [128, 2048], BF16); tab_d["WBim"] = dscr("tab_WBim", [128, 2048], BF16)

    with ExitStack() as st:
        ARENA_N = 52000
        arena = st.enter_context(nc.sbuf_tensor("arena", [128, ARENA_N], F32))
        bump = [0]

        def alloc(shape, dt=F32, name=""):
            n = 1
            for s_ in shape[1:]:
                n *= s_
            words = (n * (2 if dt == BF16 else 4) + 3) // 4
            words = (words + 7) // 8 * 8
            off = bump[0]
            bump[0] += words
            assert bump[0] <= ARENA_N, ("SBUF arena overflow", name, bump[0])
            ap = arena[:, off:off + words]
            if dt != F32:
                ap = ap.bitcast(dt)
            ap = ap[:, 0:n]
            if len(shape) > 2:
                names = " ".join("d%d" % i for i in range(len(shape) - 1))
                kw = {"d%d" % i: shape[i + 1] for i in range(len(shape) - 1)}
                ap = ap.rearrange("p (%s) -> p %s" % (names, names), **kw)
            return V(ap, Buf(name))

        banks = []
        dbl = []
        for d_ in range(4):
            t = st.enter_context(nc.psum_tensor("dbank%d" % d_, [128, 1024], F32))
            b0 = Buf("bank%d" % (2 * d_)); b1 = Buf("bank%d" % (2 * d_ + 1))
            banks.append(V(t[:, 0:512], b0)); banks.append(V(t[:, 512:1024], b1))
            dbl.append(V(t[:], (b0, b1)))
        SC = dbl[3]
        SCs = [dbl[3], dbl[0]]

        def mm(out, lhsT, rhs, start=True, stop=True):
            P.op("pe", lambda e: e.matmul(out.ap, lhsT=lhsT.ap, rhs=rhs.ap, start=start, stop=stop),
                 reads=[lhsT.b, rhs.b], writes=[out.b])

        def tr(out, in_, idt):
            P.op("pe", lambda e: e.transpose(out=out.ap, in_=in_.ap, identity=idt.ap), reads=[in_.b, idt.b], writes=[out.b])

        def act(out, in_, func, bias=None, scale=None, accum=None):
            kw = {}
            rd = [in_.b]
            wr = [out.b]
            if bias is not None:
                if isinstance(bias, V):
                    kw["bias"] = bias.ap; rd.append(bias.b)
                else:
                    kw["bias"] = bias
            if scale is not None:
                if isinstance(scale, V):
                    kw["scale"] = scale.ap; rd.append(scale.b)
                else:
                    kw["scale"] = scale
            if accum is not None:
                kw["accum_out"] = accum.ap; wr.append(accum.b)
            P.op("act", lambda e: e.activation(out=out.ap, in_=in_.ap, func=func, **kw), reads=rd, writes=wr)

        def acopy(out, in_):
            P.op("act", lambda e: e.copy(out=out.ap, in_=in_.ap), reads=[in_.b], writes=[out.b])

        def vcopy(out, in_, eng="dve"):
            P.op(eng, lambda e: e.tensor_copy(out=out.ap, in_=in_.ap), reads=[in_.b], writes=[out.b])

        def tt(out, in0, in1, op, eng="dve"):
            P.op(eng, lambda e: e.tensor_tensor(out=out.ap, in0=in0.ap, in1=in1.ap, op=op), reads=[in0.b, in1.b], writes=[out.b])

        def ts(out, in0, s1, s2, op0, op1=None, eng="dve"):
            rd = [in0.b]
            a1 = s1
            a2 = s2
            if isinstance(s1, V):
                a1 = s1.ap; rd.append(s1.b)
            if isinstance(s2, V):
                a2 = s2.ap; rd.append(s2.b)
            if op1 is None:
                P.op(eng, lambda e: e.tensor_scalar(out=out.ap, in0=in0.ap, scalar1=a1, scalar2=None, op0=op0), reads=rd, writes=[out.b])
            else:
                P.op(eng, lambda e: e.tensor_scalar(out=out.ap, in0=in0.ap, scalar1=a1, scalar2=a2, op0=op0, op1=op1), reads=rd, writes=[out.b])

        def stt(out, in0, scalar, in1, op0, op1):
            rd = [in0.b, in1.b]
            a = scalar
            if isinstance(scalar, V):
                a = scalar.ap; rd.append(scalar.b)
            P.op("dve", lambda e: e.scalar_tensor_tensor(out=out.ap, in0=in0.ap, scalar=a, in1=in1.ap, op0=op0, op1=op1), reads=rd, writes=[out.b])

        def red(out, in_, op, axis=AX.X):
            P.op("dve", lambda e: e.tensor_reduce(out=out.ap, in_=in_.ap, axis=axis, op=op), reads=[in_.b], writes=[out.b])

        def recip(out, in_):
            P.op("dve", lambda e: e.reciprocal(out=out.ap, in_=in_.ap), reads=[in_.b], writes=[out.b])

        def scan(out, d0, d1, init):
            P.op("dve", lambda e: e.tensor_tensor_scan(out=out.ap, data0=d0.ap, data1=d1.ap, initial=init.ap, op0=ALU.mult, op1=ALU.add),
                 reads=[d0.b, d1.b, init.b], writes=[out.b])

        def mset(out, val, eng="pool"):
            P.op(eng, lambda e: e.memset(out.ap, val), writes=[out.b])

        def ld(out, in_, eng="sp"):
            P.dma(eng, lambda e: e.dma_start(out=out.ap, in_=in_.ap), reads=[in_.b], writes=[out.b])

        def ldw(out, in_):
            P.dma("pool", lambda e: e.dma_start(out=out.ap, in_=in_.ap), reads=[in_.b], writes=[out.b])

        ident = alloc([128, 128], F32, "ident"); identb = alloc([128, 128], BF16, "identb")
        bar_scr = alloc([128, 8], F32, "barscr")
        ld(ident, ident_d); ldw(identb, ident_d)
        epsc = alloc([128, 1], F32, "epsc"); mset(epsc, EPS)

        def barrier():
            P.barrier(lambda e: e.memset(bar_scr.ap, 0.0))

        def load_xT(src_rows, xt, xT, bank):
            ld(xt, src_rows)
            for hb in range(2):
                for k in range(4):
                    kk = hb * 4 + k
                    tr(bank[:, k * 128:(k + 1) * 128], xt[:, kk * 128:(kk + 1) * 128], ident)
                src = bank.re("p (k n) -> p k n", k=4)
                if hb == 0:
                    acopy(xT[:, 0:4, :], src)
                else:
                    vcopy(xT[:, 4:8, :], src)

        def transpose_to(dst, src, ncol, bank, eng="act", rows=128):
            bb = bank.cast(BF16)
            for k in range(ncol):
                tr(bb[:, k * 128:k * 128 + rows], src[:, k * 128:(k + 1) * 128], identb[0:rows, 0:rows])
            s_ = bb[:, 0:ncol * 128].re("p (k n) -> p k n", k=ncol)[:, :, 0:rows]
            if eng == "act":
                acopy(dst, s_)
            else:
                vcopy(dst, s_)

        def rmsnorm(out, src, n, gtile, sq, ss):
            act(sq, src, AF.Square, accum=ss)
            act(ss, ss, AF.Ln, scale=1.0 / n, bias=epsc)
            act(ss, ss, AF.Exp, scale=-0.5)
            stt(out, src, ss, gtile, ALU.mult, ALU.mult)

        def layernorm(out, r, g, b, stats, mv):
            for c in range(2):
                P.op("dve", lambda e, c=c: e.bn_stats(out=stats.ap[:, c * 6:(c + 1) * 6], in_=r.ap[:, c * 512:(c + 1) * 512]), reads=[r.b], writes=[stats.b])
            P.op("dve", lambda e: e.bn_aggr(out=mv.ap[:, 0:2], in_=stats.ap[:, 0:12]), reads=[stats.b], writes=[mv.b])
            act(mv[:, 1:2], mv[:, 1:2], AF.Ln, bias=epsc)
            act(mv[:, 1:2], mv[:, 1:2], AF.Exp, scale=-0.5)
            ts(out, r, mv[:, 0:1], mv[:, 1:2], ALU.subtract, ALU.mult)
            tt(out, out, g, ALU.mult, eng="pool")
            tt(out, out, b, ALU.add, eng="pool")

        def rope(out, src, cc, ss_, tmp, nh, eng="dve"):
            ccb = cc.un(1).bc([128, nh, 64]); ssb = ss_.un(1).bc([128, nh, 64])
            tt(out, src, ccb, ALU.mult, eng=eng)
            tt(tmp[:, :, 0:32], src[:, :, 32:64], ssb[:, :, 0:32], ALU.mult, eng=eng)
            tt(tmp[:, :, 32:64], src[:, :, 0:32], ssb[:, :, 32:64], ALU.mult, eng=eng)
            tt(out, out, tmp, ALU.add, eng="pool")

        mkT = alloc([128, 4, 2, 256], BF16, "mkT")
        mvb = alloc([128, 2, D], BF16, "mvb")
        mark_after_mem = bump[0]
        btab = Buf("tab")
        W_in_u = alloc([128, 8, 512], BF16, "W_in_u"); W_in_kv = alloc([128, 8, 320], BF16, "W_in_kv")
        LrT = alloc([128, 16, 128], BF16, "LrT"); LiT = alloc([128, 16, 128], BF16, "LiT")
        BRb = alloc([128, 16, 32], F32, "BRb"); BIb = alloc([128, 16, 32], F32, "BIb")
        lam128 = alloc([128, 2, 16], F32, "lam128")
        Xpp = alloc([128, 2, 32], F32, "Xpp")
        Xown = alloc([128, 16, 32], F32, "Xown")
        ohot = alloc([128, 8], F32, "ohot")
        Oacc = alloc([128, 16, 4, 128], F32, "Oacc")
        m_run = alloc([128, 64], F32, "m_run"); l_run = alloc([128, 64], F32, "l_run")
        Osam = alloc([128, 512], F32, "Osam")
        ssmx = {}
        mark_after_mixer_state = bump[0]
        QTn = alloc([128, NOWN, 4, 128], BF16, "QTn"); QTr = alloc([128, NOWN, 4, 128], BF16, "QTr")
        mark_after_q = bump[0]

        w_in_v = w_in_d.re("(kt p) n -> p kt n", p=128)
        ldw(W_in_u, w_in_v[:, :, 0:512]); ldw(W_in_kv, w_in_v[:, :, 896:1216])
        ld(ohot, onehot_d)

        S_t = alloc([128, 16, 128], F32, "S_t"); C_t = alloc([128, 16, 128], F32, "C_t"); R_t = alloc([128, 16, 128], F32, "R_t")
        for v_ in (S_t, C_t, R_t):
            v_.b = btab
        WBre = alloc([128, 16, 128], BF16, "WBre"); WBim = alloc([128, 16, 128], BF16, "WBim")

        p0 = bump[0]
        ar = alloc([128, 16]); ai = alloc([128, 16]); ldt = alloc([128, 16]); jj = alloc([128, 128])
        bre = alloc([128, 16, 16]); bim = alloc([128, 16, 16])
        bsm = Buf("ssm_small")
        for v_ in (ar, ai, ldt, jj, bre, bim):
            v_.b = bsm
        ld(ar, ar_d); ld(ai, ai_d); ld(ldt, ldt_d); ld(jj, jj_d)
        ld(bre.re("p a b -> p (a b)"), bre_d); ld(bim.re("p a b -> p (a b)"), bim_d)
        dt_ = alloc([128, 16]); th = alloc([128, 16]); rr = alloc([128, 16])
        sm = [alloc([128, 16]) for _ in range(8)]
        for v_ in [dt_, th, rr] + sm:
            v_.b = bsm
        act(dt_, ldt, AF.Exp)
        tt(th, ai, dt_, ALU.mult)
        tt(rr, ar, dt_, ALU.mult)
        act(rr, rr, AF.Exp)
        A_t = alloc([128, 16, 128]); T1 = alloc([128, 2048]); TI = alloc([128, 2048], I32)
        for v_ in (A_t, T1, TI):
            v_.b = btab
        jjb = jj.un(1).bc([128, 16, 128])
        tt(A_t, jjb, th.un(2).bc([128, 16, 128]), ALU.mult)
        tt(R_t, jjb, rr.un(2).bc([128, 16, 128]), ALU.max)
        tt(R_t, R_t, rr.un(2).bc([128, 16, 128]), ALU.min)
        Af = A_t.re("p a b -> p (a b)")
        TIf = TI.cast(F32)

        def sin_of(out, shift):
            ts(T1, Af, shift, 1.0 / TWO_PI, ALU.add, ALU.mult)
            vcopy(TI, T1)
            vcopy(T1, TI)
            stt(T1, T1, -TWO_PI, Af, ALU.mult, ALU.add)
            if shift != 0.0:
                ts(T1, T1, shift, None, ALU.add)
            ts(TIf, T1, math.pi, -TWO_PI, ALU.is_gt, ALU.mult)
            tt(T1, T1, TIf, ALU.add)
            ts(TIf, T1, -math.pi, TWO_PI, ALU.is_lt, ALU.mult)
            tt(T1, T1, TIf, ALU.add)
            ts(T1, T1, math.pi, -math.pi, ALU.min, ALU.max)
            act(out, T1, AF.Sin)

        sin_of(S_t.re("p a b -> p (a b)"), 0.0)
        sin_of(C_t.re("p a b -> p (a b)"), math.pi / 2)
        lbr, lbi, den, fre, fim, t0_, t1_, t2_ = sm
        cos1 = C_t[:, :, 0]; sin1 = S_t[:, :, 0]
        tt(lbr, rr, cos1, ALU.mult)
        ts(lbr, lbr, -1.0, None, ALU.add)
        tt(lbi, rr, sin1, ALU.mult)
        tt(den, ar, ar, ALU.mult)
        tt(t0_, ai, ai, ALU.mult)
        tt(den, den, t0_, ALU.add)
        recip(den, den)
        tt(t0_, lbr, ar, ALU.mult); tt(t1_, lbi, ai, ALU.mult); tt(fre, t0_, t1_, ALU.add); tt(fre, fre, den, ALU.mult)
        tt(t0_, lbi, ar, ALU.mult); tt(t1_, lbr, ai, ALU.mult); tt(fim, t0_, t1_, ALU.subtract); tt(fim, fim, den, ALU.mult)
        Mre = alloc([128, 16, 128]); Mim = alloc([128, 16, 128]); tb = alloc([128, 16, 16]); tb2 = alloc([128, 16, 16])
        Mreb = alloc([128, 16, 128], BF16); Mimb = alloc([128, 16, 128], BF16)
        bM = Buf("M")
        for v_ in (Mre, Mim, tb, tb2, Mreb, Mimb):
            v_.b = bM
        freb = fre.un(2).bc([128, 16, 16]); fimb = fim.un(2).bc([128, 16, 16])
        mset(Mre, 0.0); mset(Mim, 0.0)
        tt(tb, bre, freb, ALU.mult); tt(tb2, bim, fimb, ALU.mult)
        for lo, col in ((0, 0), (64, 16)):
            for j4 in range(4):
                tt(Mre[lo:lo + 64, j4::4, 32 * j4 + col:32 * j4 + col + 16], tb[lo:lo + 64, j4::4, :], tb2[lo:lo + 64, j4::4, :], ALU.subtract)
        tt(tb, bim, freb, ALU.mult); tt(tb2, bre, fimb, ALU.mult)
        for lo, col in ((0, 0), (64, 16)):
            for j4 in range(4):
                tt(Mim[lo:lo + 64, j4::4, 32 * j4 + col:32 * j4 + col + 16], tb[lo:lo + 64, j4::4, :], tb2[lo:lo + 64, j4::4, :], ALU.add)
        vcopy(Mreb, Mre); vcopy(Mimb, Mim)
        for M_, WB_ in ((Mreb, WBre), (Mimb, WBim)):
            for hf in range(2):
                bb = banks[0].cast(BF16)
                for q in range(8):
                    tr(bb[:, q * 128:(q + 1) * 128], M_[:, 8 * hf + q, :], identb)
                vcopy(WB_[:, 8 * hf:8 * hf + 8, :].re("p a b -> p (a b)"), bb)

        mset(BRb, 0.0); mset(BIb, 0.0)
        tt(tb, bre, freb, ALU.mult); tt(tb2, bim, fimb, ALU.mult)
        for lo, col in ((0, 0), (64, 16)):
            tt(BRb[lo:lo + 64, :, col:col + 16], tb[lo:lo + 64], tb2[lo:lo + 64], ALU.subtract)
        tt(tb, bim, freb, ALU.mult); tt(tb2, bre, fimb, ALU.mult)
        for lo, col in ((0, 0), (64, 16)):
            tt(BIb[lo:lo + 64, :, col:col + 16], tb[lo:lo + 64], tb2[lo:lo + 64], ALU.add)
        lnr = alloc([128, 16]); lnr.b = bsm
        tt(lnr, ar, dt_, ALU.mult)
        act(t2_, lnr, AF.Exp, scale=128.0)
        tt(lam128[:, 0, :], t2_, C_t[:, :, 127], ALU.mult)
        tt(lam128[:, 1, :], t2_, S_t[:, :, 127], ALU.mult)
        jj2 = alloc([128, 128]); jj2.b = bsm
        ld(jj2, jj2_d)
        C2 = Mre; S2 = Mim; Mag = T1.re("p (a b) -> p a b", a=16)
        jj2b = jj2.un(1).bc([128, 16, 128])
        tt(A_t, jj2b, th.un(2).bc([128, 16, 128]), ALU.mult)
        sin_of(S2.re("p a b -> p (a b)"), 0.0)
        sin_of(C2.re("p a b -> p (a b)"), math.pi / 2)
        tt(Mag, jj2b, lnr.un(2).bc([128, 16, 128]), ALU.mult)
        act(Mag, Mag, AF.Exp)
        Lb = [Mreb, Mimb]
        tt(Lb[0], Mag, C2, ALU.mult); tt(Lb[1], Mag, S2, ALU.mult)
        for M_, LT_ in ((Lb[0], LrT), (Lb[1], LiT)):
            for hf in range(2):
                bb = banks[0].cast(BF16)
                for q in range(8):
                    tr(bb[:, q * 128:(q + 1) * 128], M_[:, 8 * hf + q, :], identb)
                vcopy(LT_[:, 8 * hf:8 * hf + 8, :].re("p a b -> p (a b)"), bb)

        for nm_, v_ in (("S_t", S_t), ("C_t", C_t), ("R_t", R_t)):
            ld(tab_d[nm_], v_.re("p a b -> p (a b)"))
        ld(tab_d["WBre"], WBre.re("p a b -> p (a b)")); ld(tab_d["WBim"], WBim.re("p a b -> p (a b)"))

        barrier()
        bump[0] = mark_after_q
        W_xk = alloc([128, 8, D], BF16, "W_xk"); W_xv = alloc([128, 8, D], BF16, "W_xv")
        ldw(W_xk, w_xk_d.re("(kt p) n -> p kt n", p=128)); ldw(W_xv, w_xv_d.re("(kt p) n -> p kt n", p=128))
        xt0 = alloc([128, D], F32, "xt0"); memT = alloc([128, 8, 256], BF16, "memT"); xT0 = alloc([128, 8, 128], BF16, "xT0")
        mkf = alloc([128, D], F32, "mkf")
        for mt in range(2):
            load_xT(mem[mt * 128:(mt + 1) * 128, :], xt0, xT0, banks[0])
            vcopy(memT[:, :, mt * 128:(mt + 1) * 128], xT0, eng="pool")
            for W_, o_d, keep in ((W_xk, memk_o, False), (W_xv, memv_o, True)):
                for nb in range(2):
                    for k in range(8):
                        mm(banks[1 + nb], xT0[:, k, :], W_[:, k, nb * 512:(nb + 1) * 512], start=(k == 0), stop=(k == 7))
                    acopy(mkf[:, nb * 512:(nb + 1) * 512], banks[1 + nb])
                ld(o_d[mt * 128:(mt + 1) * 128, :], mkf)
                if keep:
                    vcopy(mvb[:, mt, :], mkf)
        for h in range(4):
            for et in range(2):
                c0 = h * 256 + et * 128
                for k in range(8):
                    mm(banks[3][:, 0:256], W_xk[:, k, c0:c0 + 128], memT[:, k, :], start=(k == 0), stop=(k == 7))
                acopy(mkT[:, h, et, :], banks[3][:, 0:256])


        W_q = alloc([128, 3, 768], BF16, "W_q")
        ldw(W_q, w_q_d.re("(kt p) n -> p kt n", p=128))
        W_in_q = alloc([128, 8, 384], BF16, "W_in_q")
        ldw(W_in_q, w_in_v[:, :, 512:896])
        gq = alloc([128, 384], F32, "gq"); ld(gq, gq_d)
        NB = 2
        xt = [alloc([128, D], F32, "xt%d" % i) for i in range(NB)]
        xT = [alloc([128, 8, 128], BF16, "xT%d" % i) for i in range(NB)]
        rc = [alloc([128, 64], F32, "rc%d" % i) for i in range(NB)]
        rs = [alloc([128, 64], F32, "rs%d" % i) for i in range(NB)]
        sqp = [alloc([128, 384], F32, "sq%d" % i) for i in range(NB)]; ss1 = [alloc([128, 1], F32) for _ in range(NB)]
        cqn = [alloc([128, 384], BF16) for _ in range(NB)]
        cqT = [alloc([128, 3, 128], BF16) for _ in range(NB)]
        qf = [alloc([128, 4, 192], F32) for _ in range(NB)]
        qr = [alloc([128, 4, 64], F32) for _ in range(NB)]
        qtmp = [alloc([128, 4, 64], F32) for _ in range(NB)]
        qb = [alloc([128, 4, 192], BF16) for _ in range(NB)]
        qrd = [alloc([128, 4, 128], BF16) for _ in range(NB)]

        def q_stages(i, p):
            Xb = banks[p]; Jb = banks[2 + p]; Qb = dbl[2 + p]

            def sA():
                load_xT(xo[i * 128:(i + 1) * 128, :], xt[p], xT[p], Xb)
                ld(rc[p], ropec_o[i * 128:(i + 1) * 128, :]); ld(rs[p], ropes_o[i * 128:(i + 1) * 128, :])

            def sB():
                for k in range(8):
                    mm(Jb[:, 0:384], xT[p][:, k, :], W_in_q[:, k, :], start=(k == 0), stop=(k == 7))

            def sC():
                rmsnorm(cqn[p], Jb[:, 0:384], 384, gq, sqp[p], ss1[p])

            def sD():
                transpose_to(cqT[p], cqn[p], 3, Xb, eng="act")

            def sE():
                for nb, (c0, c1) in enumerate(((0, 512), (512, 768))):
                    for k in range(3):
                        mm(Qb[:, nb * 512:nb * 512 + (c1 - c0)], cqT[p][:, k, :], W_q[:, k, c0:c1], start=(k == 0), stop=(k == 2))

            def sF():
                acopy(qf[p].re("p a b -> p (a b)"), Qb[:, 0:768])

            def sG():
                rope(qr[p], qf[p][:, :, 128:192], rc[p], rs[p], qtmp[p], 4)

            def sH():
                P.op("act", lambda e: e.mul(out=qb[p].ap[:, :, 0:128], in_=qf[p].ap[:, :, 0:128], mul=MLA_SCALE), reads=[qf[p].b], writes=[qb[p].b])
                P.op("act", lambda e: e.mul(out=qrd[p].ap[:, :, 0:64], in_=qr[p].ap, mul=MLA_SCALE), reads=[qr[p].b], writes=[qrd[p].b])
                P.op("act", lambda e: e.mul(out=qrd[p].ap[:, :, 64:128], in_=qr[p].ap, mul=MLA_SCALE), reads=[qr[p].b], writes=[qrd[p].b])

            def sI():
                bb = Xb.cast(BF16)
                for h in range(4):
                    tr(bb[:, h * 128:(h + 1) * 128], qb[p][:, h, 0:128], identb)
                    tr(bb[:, 512 + h * 128:512 + (h + 1) * 128], qrd[p][:, h, :], identb)

            def sJ():
                bb = Xb.cast(BF16)
                vcopy(QTn[:, i, :, :].re("p a b -> p (a b)"), bb[:, 0:512])
                vcopy(QTr[:, i, :, :].re("p a b -> p (a b)"), bb[:, 512:1024])

            return [sA, sB, sC, sD, sE, sF, sG, sH, sI, sJ]

        for ip in range(9):
            sa = q_stages(2 * ip, 0)
            sb_ = q_stages(2 * ip + 1, 1) if 2 * ip + 1 < NOWN else []
            for k_ in range(len(sa)):
                sa[k_]()
                if k_ < len(sb_):
                    sb_[k_]()

        barrier()
        bump[0] = mark_after_q

        def ssm_rounds(uT_, pbs, init_fn, nseg, ybank=None, last_out=None):
            L = 128 // nseg
            S_t = ssmx["S_t"]; C_t = ssmx["C_t"]; R_t = ssmx["R_t"]; WBre = ssmx["WBre"]; WBim = ssmx["WBim"]
            Cre = ssmx["Cre"]; Cimn = ssmx["Cimn"]; Dblk = ssmx["Dblk"]
            xs4 = ssmx["xs4"]; zl_all = ssmx["zl_all"]
            def do_round(r):
                m4 = ssmx["m4"][r % 2]; wri = ssmx["wri"][r % 2]; zri = ssmx["zri"][r % 2]
                pb = pbs[r % 2]
                for j in range(2):
                    gp = 2 * r + j
                    mm(pb[:, j * 128:(j + 1) * 128], WBre[:, gp, :], uT_[:, gp // 4, :])
                    mm(pb[:, 256 + j * 128:256 + (j + 1) * 128], WBim[:, gp, :], uT_[:, gp // 4, :])
                if nseg == 1:
                    Cq = C_t[:, 2 * r:2 * r + 2, :].re("p a b -> p (a b)"); Sq = S_t[:, 2 * r:2 * r + 2, :].re("p a b -> p (a b)")
                    pre = pb[:, 0:256]; pim = pb[:, 256:512]
                    mk = lambda v_: v_
                else:
                    Cq = C_t[:, 2 * r:2 * r + 2, 0:L].un(2).bc([128, 2, nseg, L]); Sq = S_t[:, 2 * r:2 * r + 2, 0:L].un(2).bc([128, 2, nseg, L])
                    pre = pb[:, 0:256].re("p (a s l) -> p a s l", a=2, s=nseg); pim = pb[:, 256:512].re("p (a s l) -> p a s l", a=2, s=nseg)
                    mk = lambda v_: v_.re("p (a s l) -> p a s l", a=2, s=nseg)
                tt(mk(m4[0]), pre, Cq, ALU.mult); tt(mk(m4[1]), pim, Sq, ALU.mult)
                tt(mk(m4[2]), pim, Cq, ALU.mult); tt(mk(m4[3]), pre, Sq, ALU.mult)
                tt(wri[0], m4[0], m4[1], ALU.add, eng="pool")
                tt(wri[1], m4[2], m4[3], ALU.subtract, eng="pool")

            def do_back(r):
                m4 = ssmx["m4"][r % 2]; wri = ssmx["wri"][r % 2]; zri = ssmx["zri"][r % 2]
                if nseg == 1:
                    Cq = C_t[:, 2 * r:2 * r + 2, :].re("p a b -> p (a b)"); Sq = S_t[:, 2 * r:2 * r + 2, :].re("p a b -> p (a b)")
                else:
                    Cq = C_t[:, 2 * r:2 * r + 2, 0:L].un(2).bc([128, 2, nseg, L]); Sq = S_t[:, 2 * r:2 * r + 2, 0:L].un(2).bc([128, 2, nseg, L])
                for j in range(2):
                    gp = 2 * r + j
                    for s_ in range(nseg):
                        for part in range(2):
                            scan(zri[part][:, j, s_ * L:(s_ + 1) * L], R_t[:, gp, 0:L], wri[part][:, j * 128 + s_ * L:j * 128 + (s_ + 1) * L], init_fn(gp, s_, part))
                if last_out is not None:
                    for part in range(2):
                        src = zri[part].re("p a (s l) -> p a s l", s=nseg)[:, :, :, L - 1]
                        vcopy(zl_all[part][:, 0:nseg, 2 * r:2 * r + 2].re("p s a -> p a s"), src, eng="pool")
                if ybank is not None:
                    dmd = ssmx["dmd"][r % 2]; xrb = ssmx["xrb"]; xib = ssmx["xib"]
                    if nseg == 1:
                        Cd, Sd = Cq, Sq
                        zr_ = zri[0].re("p a b -> p (a b)"); zi_ = zri[1].re("p a b -> p (a b)")
                        dk = lambda v_: v_
                    else:
                        Cd, Sd = Cq, Sq
                        zr_ = zri[0].re("p a (s l) -> p a s l", s=nseg); zi_ = zri[1].re("p a (s l) -> p a s l", s=nseg)
                        dk = lambda v_: v_.re("p (a s l) -> p a s l", a=2, s=nseg)
                    tt(dk(dmd[0]), zr_, Cd, ALU.mult); tt(dk(dmd[1]), zi_, Sd, ALU.mult, eng="pool")
                    tt(dk(dmd[2]), zr_, Sd, ALU.mult); tt(dk(dmd[3]), zi_, Cd, ALU.mult, eng="pool")
                    tt(xrb[r % 2].re("p a b -> p (a b)"), dmd[0], dmd[1], ALU.subtract, eng="pool")
                    tt(xib[r % 2].re("p a b -> p (a b)"), dmd[2], dmd[3], ALU.add, eng="pool")

            def do_cproj(r):
                if ybank is None:
                    return
                xrb = ssmx["xrb"]; xib = ssmx["xib"]
                for j in range(2):
                    gp = 2 * r + j
                    o_ = ybank[:, 32 * gp:32 * gp + 32]
                    mm(o_, xrb[r % 2][:, j, :], Cre[:, gp, :], start=True, stop=False)
                    mm(o_, xib[r % 2][:, j, :], Cimn[:, gp, :], start=False, stop=False)
                    mm(o_, uT_[:, gp // 4, :], Dblk[:, gp, :], start=False, stop=True)
            def do_tail():
                do_cproj(7)
                if last_out is None:
                    return
                if True:
                    cL = C_t[:, :, L - 1]; sL = S_t[:, :, L - 1]
                    for s_ in range(nseg):
                        o_re, o_im = last_out(s_)
                        zr_l = zl_all[0][:, s_, :]; zi_l = zl_all[1][:, s_, :]
                        tt(xs4[0], zr_l, cL, ALU.mult); tt(xs4[1], zi_l, sL, ALU.mult)
                        tt(xs4[2], zr_l, sL, ALU.mult); tt(xs4[3], zi_l, cL, ALU.mult)
                        tt(o_re, xs4[0], xs4[1], ALU.subtract)
                        tt(o_im, xs4[2], xs4[3], ALU.add)


            def step(k_):
                def f():
                    if k_ == 0:
                        do_round(0)
                    if k_ + 1 < 8:
                        do_round(k_ + 1)
                    do_back(k_)
                    if k_ >= 1:
                        do_cproj(k_ - 1)
                return f
            return [step(k_) for k_ in range(8)] + [do_tail]

        def ssm_tile(uT_, pbs, init_fn, nseg, ybank=None, last_out=None):
            for f_ in ssm_rounds(uT_, pbs, init_fn, nseg, ybank, last_out):
                f_()

        W_kv = alloc([128, 2, 4, 256], BF16, "W_kv")
        ldw(W_kv.re("p k h e -> p k (h e)"), w_kv_d.re("(kt p) n -> p kt n", p=128))
        gkv = alloc([128, 256], F32, "gkv"); ld(gkv, gkv_d)
        maskd = alloc([128, 1024], F32, "maskd"); ld(maskd, maskd_d)
        xt = [alloc([128, D], F32, "hxt%d" % i) for i in range(NB)]
        xT = [alloc([128, 8, 128], BF16, "hxT%d" % i) for i in range(NB)]
        utok = [alloc([128, 512], BF16, "hut%d" % i) for i in range(NB)]
        kvf = [alloc([128, 320], F32, "kvf%d" % i) for i in range(NB)]
        prs = [alloc([128, 16, 32], F32, "prs%d" % i) for i in range(NB)]
        pis = [alloc([128, 16, 32], F32, "pis%d" % i) for i in range(NB)]
        rc = [alloc([128, 64], F32) for _ in range(NB)]; rs = [alloc([128, 64], F32) for _ in range(NB)]
        ckvf = [alloc([128, 256], F32) for _ in range(NB)]; kpef = [alloc([128, 64], F32) for _ in range(NB)]
        tmp64 = [alloc([128, 64], F32) for _ in range(NB)]
        sq = alloc([128, 256], F32, "hsq"); ss1 = [alloc([128, 1], F32) for _ in range(NB)]
        ckvb = [alloc([128, 256], BF16) for _ in range(NB)]; kpeb = [alloc([128, 128], BF16) for _ in range(NB)]
        ckvT = [alloc([128, 2, 128], BF16) for _ in range(NB)]
        KT = [alloc([128, 4, 1024], BF16, "KT%d" % i) for i in range(2)]
        KR = [alloc([128, 1024], BF16, "KR%d" % i) for i in range(2)]
        Vb = [alloc([128, 8, 512], BF16, "Vb%d" % i) for i in range(2)]
        Pb = [alloc([128, 1024], BF16, "Pb%d" % i) for i in range(2)]
        PT = [alloc([128, 8, 128], BF16, "PT%d" % i) for i in range(2)]
        NS4 = 4
        mx = [alloc([128, 1], F32) for _ in range(NS4)]; mnew = [alloc([128, 1], F32) for _ in range(NS4)]
        negm = [alloc([128, 1], F32) for _ in range(NS4)]; alp = [alloc([128, 1], F32) for _ in range(NS4)]
        rsum = [alloc([128, 1], F32) for _ in range(NS4)]
        dm_ = [alloc([128, 1], F32) for _ in range(NS4)]
        hm1 = [alloc([128, 16, 32], F32, "hm1_%d" % i) for i in range(NB)]; hm2 = [alloc([128, 16, 32], F32, "hm2_%d" % i) for i in range(NB)]
        sre = [alloc([128, 32], F32) for _ in range(2)]
        xs8 = [alloc([128, 16], F32) for _ in range(4)]

        mset(Xpp, 0.0); mset(Xown, 0.0)
        mset(m_run, -NEG); mset(l_run, 0.0); mset(Oacc, 0.0)
        for kp in range(2):
            mset(kpeb[kp], 0.0)

        def hist_stages(t, kbuf, j, kb):
            b = t % NB
            bA = banks[0]; bB = banks[1]
            xc = Xpp[:, t % 2, :]; xn = Xpp[:, (t + 1) % 2, :]

            def sA():
                ld(xt[b], xp[t * 128:(t + 1) * 128, :])
                ld(rc[b], ropec_p[t * 128:(t + 1) * 128, :]); ld(rs[b], ropes_p[t * 128:(t + 1) * 128, :])

            def sB():
                for k in range(4):
                    tr(bA[:, k * 128:(k + 1) * 128], xt[b][:, k * 128:(k + 1) * 128], ident)
                for k in range(4):
                    tr(bB[:, k * 128:(k + 1) * 128], xt[b][:, (4 + k) * 128:(5 + k) * 128], ident)

            def sC():
                acopy(xT[b][:, 0:4, :], bA.re("p (k n) -> p k n", k=4))
                acopy(xT[b][:, 4:8, :], bB.re("p (k n) -> p k n", k=4))

            def sD():
                for k in range(8):
                    mm(bA, xT[b][:, k, :], W_in_u[:, k, :], start=(k == 0), stop=(k == 7))
                for k in range(8):
                    mm(bB[:, 0:320], xT[b][:, k, :], W_in_kv[:, k, :], start=(k == 0), stop=(k == 7))

            def sE():
                acopy(utok[b], bA)
                acopy(kvf[b], bB[:, 0:320])

            def sF():
                act(sq, kvf[b][:, 0:256], AF.Square, accum=ss1[b])
                act(ss1[b], ss1[b], AF.Ln, scale=1.0 / 256, bias=epsc)
                act(ss1[b], ss1[b], AF.Exp, scale=-0.5)
                for gp in range(16):
                    mm(bA[:, 32 * gp:32 * gp + 32], LrT[:, gp, :], utok[b][:, 32 * gp:32 * gp + 32])
                for gp in range(16):
                    mm(bB[:, 32 * gp:32 * gp + 32], LiT[:, gp, :], utok[b][:, 32 * gp:32 * gp + 32])

            late = kb >= 4

            def sG():
                if late:
                    pr3 = bA.re("p (a b) -> p a b", a=16); pi3 = bB.re("p (a b) -> p a b", a=16)
                    tt(hm1[b], pr3, BRb, ALU.mult); tt(hm2[b], pi3, BIb, ALU.mult)
                    tt(prs[b], pi3, BRb, ALU.mult); tt(pis[b], pr3, BIb, ALU.mult)
                else:
                    acopy(prs[b].re("p a b -> p (a b)"), bA)
                    acopy(pis[b].re("p a b -> p (a b)"), bB)
                tt(ckvf[b], kvf[b][:, 0:256], gkv, ALU.mult, eng="pool")
                ts(ckvf[b], ckvf[b], ss1[b], 1.0, ALU.mult, ALU.mult, eng="pool")
                rope(kpef[b].re("p (a b) -> p a b", a=1), kvf[b][:, 256:320].re("p (a b) -> p a b", a=1), rc[b], rs[b],
                     tmp64[b].re("p (a b) -> p a b", a=1), 1, eng="pool")
                vcopy(ckvb[b], ckvf[b], eng="pool")
                vcopy(kpeb[b][:, 0:64], kpef[b], eng="pool")
                vcopy(kpeb[b][:, 64:128], kpef[b], eng="pool")

            def sH():
                bb = bA.cast(BF16)
                for kc in range(2):
                    tr(bb[:, kc * 128:(kc + 1) * 128], ckvb[b][:, kc * 128:(kc + 1) * 128], identb)
                tr(bb[:, 256:384], kpeb[b], identb)
                if late:
                    tt(hm1[b], hm1[b], hm2[b], ALU.subtract, eng="pool")
                    tt(hm2[b], prs[b], pis[b], ALU.add, eng="pool")
                else:
                    tt(hm1[b], prs[b], BRb, ALU.mult, eng="pool"); tt(hm2[b], pis[b], BIb, ALU.mult, eng="pool")
                    tt(hm1[b], hm1[b], hm2[b], ALU.subtract, eng="pool")
                    tt(hm2[b], pis[b], BRb, ALU.mult, eng="pool"); tt(prs[b], prs[b], BIb, ALU.mult, eng="pool")
                    tt(hm2[b], hm2[b], prs[b], ALU.add, eng="pool")

            def sI():
                bb = bA.cast(BF16)
                acopy(ckvT[b].re("p a b -> p (a b)"), bb[:, 0:256])
                acopy(KR[kbuf][:, j * 128:(j + 1) * 128], bb[:, 256:384])

            def sJ():
                for h in range(4):
                    for kc in range(2):
                        mm(bB[:, h * 128:(h + 1) * 128], W_kv[:, kc, h, 0:128], ckvT[b][:, kc, :], start=(kc == 0), stop=(kc == 1))
                for kc in range(2):
                    mm(bA, ckvT[b][:, kc, :], W_kv[:, kc, :, 128:256], start=(kc == 0), stop=(kc == 1))
                lr_ = lam128[:, 0, :]; li_ = lam128[:, 1, :]
                ld(ckv_o[t * 128:(t + 1) * 128, :], ckvf[b])
                ld(kpe_o[t * 128:(t + 1) * 128, :], kpef[b])
                red(sre[t % 2][:, 0:16], hm1[b], ALU.add)
                red(sre[t % 2][:, 16:32], hm2[b], ALU.add)
                stt(Xown[:, kb, :], xc, ohot[:, j:j + 1], Xown[:, kb, :], ALU.mult, ALU.add)
                tt(xs8[0], xc[:, 0:16], lr_, ALU.mult, eng="pool"); tt(xs8[1], xc[:, 16:32], li_, ALU.mult, eng="pool")
                tt(xs8[2], xc[:, 16:32], lr_, ALU.mult, eng="pool"); tt(xs8[3], xc[:, 0:16], li_, ALU.mult, eng="pool")
                tt(xs8[0], xs8[0], xs8[1], ALU.subtract, eng="pool"); tt(xs8[2], xs8[2], xs8[3], ALU.add, eng="pool")
                tt(xn[:, 0:16], xs8[0], sre[t % 2][:, 0:16], ALU.add, eng="pool")
                tt(xn[:, 16:32], xs8[2], sre[t % 2][:, 16:32], ALU.add, eng="pool")

            def sK():
                acopy(KT[kbuf][:, :, j * 128:(j + 1) * 128], bB.re("p (h n) -> p h n", h=4))
                acopy(Vb[kbuf][:, j, :], bA)

            def pair(p_, c_):
                def f():
                    p_(); c_()
                return f
            return [sA, pair(sB, sC), pair(sD, sE), pair(sF, sG), pair(sH, sI), pair(sJ, sK)]

        def hist_items(kb):
            items = []
            for jp in range(4):
                sa = hist_stages(kb * 8 + 2 * jp, kb % 2, 2 * jp, kb)
                sb_ = hist_stages(kb * 8 + 2 * jp + 1, kb % 2, 2 * jp + 1, kb)
                for a_, b_ in zip(sa, sb_):
                    items.append(a_); items.append(b_)
            return items

        SCp = [dbl[3], dbl[2]]
        ptb = [banks[2], banks[3]]

        def att_A(n, i, h, kbuf):
            sc = SCp[n % 2]
            for c2 in range(2):
                mm(sc[:, c2 * 512:(c2 + 1) * 512], QTn[:, i, h, :], KT[kbuf][:, h, c2 * 512:(c2 + 1) * 512], start=True, stop=False)
            for c2 in range(2):
                lo = 64 * c2
                mm(sc[:, c2 * 512:(c2 + 1) * 512], QTr[lo:lo + 64, i, h, :], KR[kbuf][lo:lo + 64, c2 * 512:(c2 + 1) * 512], start=False, stop=True)

        def att_B(n, i, h, kbuf, diag):
            u2 = n % 2; u4 = n % NS4
            sc = SCp[u2]
            col = i * 4 + h
            if diag:
                tt(sc, sc, maskd, ALU.add)
            P.op("dve", lambda e: e.tensor_reduce(out=mx[u4].ap, in_=sc.ap, axis=AX.X, op=ALU.max, negate=True), reads=_flat([sc.b]), writes=[mx[u4].b])
            tt(negm[u4], mx[u4], m_run[:, col:col + 1], ALU.min)
            tt(dm_[u4], negm[u4], m_run[:, col:col + 1], ALU.subtract)
            vcopy(m_run[:, col:col + 1], negm[u4])
            act(alp[u4], dm_[u4], AF.Exp)
            act(Pb[u2], sc, AF.Exp, bias=negm[u4], accum=rsum[u4])

        def att_CD(n, i, h, kbuf):
            u2 = n % 2
            pt_b = ptb[u2].cast(BF16)
            for kt in range(8):
                tr(pt_b[:, kt * 128:(kt + 1) * 128], Pb[u2][:, kt * 128:(kt + 1) * 128], identb)
            if n % 3 == 0:
                vcopy(PT[u2].re("p a b -> p (a b)"), pt_b)
            else:
                acopy(PT[u2].re("p a b -> p (a b)"), pt_b)

        def att_EF(n, i, h, kbuf):
            u2 = n % 2; u4 = n % NS4
            ov = ptb[u2][:, 0:128]
            for kt in range(8):
                mm(ov, PT[u2][:, kt, :], Vb[kbuf][:, kt, h * 128:(h + 1) * 128], start=(kt == 0), stop=(kt == 7))
            stt(Oacc[:, i, h, :], Oacc[:, i, h, :], alp[u4], ov, ALU.mult, ALU.add)
            col = i * 4 + h
            stt(l_run[:, col:col + 1], l_run[:, col:col + 1], alp[u4], rsum[u4], ALU.mult, ALU.add)

        for it in hist_items(0):
            it()
        nun = 0
        for kb in range(16):
            kbuf = kb % 2
            units = [(i, h) for i in range(kb, 16) for h in range(4)]
            U = len(units)
            items = hist_items(kb + 1) if kb + 1 < 16 else []
            per = (len(items) + U - 1) // U if items else 0
            ip = 0
            for q_ in range(U + 3):
                if q_ < U:
                    att_A(nun + q_, units[q_][0], units[q_][1], kbuf)
                if 0 <= q_ - 1 < U:
                    att_B(nun + q_ - 1, units[q_ - 1][0], units[q_ - 1][1], kbuf, units[q_ - 1][0] == kb)
                if 0 <= q_ - 2 < U:
                    att_CD(nun + q_ - 2, units[q_ - 2][0], units[q_ - 2][1], kbuf)
                if 0 <= q_ - 3 < U:
                    att_EF(nun + q_ - 3, units[q_ - 3][0], units[q_ - 3][1], kbuf)
                for _ in range(per):
                    if ip < len(items):
                        items[ip](); ip += 1
            while ip < len(items):
                items[ip](); ip += 1
            nun += U

        ld(ssmp_o, Xpp[:, NTP % 2, :])

        barrier()
        bump[0] = mark_after_q

        W_kv = alloc([128, 2, 4, 256], BF16, "W_kv2")
        ldw(W_kv.re("p k h e -> p k (h e)"), w_kv_d.re("(kt p) n -> p kt n", p=128))
        gkv = alloc([128, 256], F32, "gkv2"); ld(gkv, gkv_d)
        xt_s = alloc([128, D], F32, "sxt"); xT_s = alloc([128, 8, 128], BF16, "sxT")
        rc_s = alloc([128, 64], F32); rs_s = alloc([128, 64], F32)
        ckvf_s = alloc([128, 256], F32); kpef_s = alloc([128, 64], F32); tmp64_s = alloc([128, 64], F32)
        sq = alloc([128, 512], F32, "ssq"); ss_s = alloc([128, 1], F32)
        ckvb_s = alloc([128, 256], BF16); kpeb_s = alloc([128, 128], BF16)
        ckvTn = alloc([128, 2, 128], BF16, "ckvTn"); kpeTn = alloc([128, 128], BF16, "kpeTn")
        KTn = alloc([128, 4, 128], BF16, "KTn")
        cat = [alloc([128, 256], F32, "cat%d" % i) for i in range(2)]
        catb = [alloc([128, 256], BF16) for _ in range(2)]
        cpt = [alloc([128, 64], F32, "cpt%d" % i) for i in range(2)]
        cptb = [alloc([128, 128], BF16) for _ in range(2)]
        ckvTc = alloc([128, 2, 1024], BF16, "ckvTc"); kpeTc = alloc([128, 1024], BF16, "kpeTc")
        KTc = alloc([128, 4, 1024], BF16, "KTc"); Vc = alloc([128, 8, 512], BF16, "Vc"); Vn = alloc([128, 512], BF16, "Vn")
        scs = alloc([128, 1056], F32, "scs")
        Ps = alloc([128, 1152], BF16, "Ps"); PTs = alloc([128, 9, 32], BF16, "PTs")
        mxs = alloc([128, 1], F32); negs = alloc([128, 1], F32); sums = alloc([128, 1], F32)
        osb = alloc([128, 512], F32, "osb")
        isam = 16
        load_xT(xo[isam * 128:(isam + 1) * 128, :], xt_s, xT_s, banks[0])
        ld(rc_s, ropec_o[isam * 128:(isam + 1) * 128, :]); ld(rs_s, ropes_o[isam * 128:(isam + 1) * 128, :])
        for k in range(8):
            mm(banks[2][:, 0:320], xT_s[:, k, :], W_in_kv[:, k, :], start=(k == 0), stop=(k == 7))
        rmsnorm(ckvf_s, banks[2][:, 0:256], 256, gkv, sq[:, 0:256], ss_s)
        ld(ckvs_o, ckvf_s)
        rope(kpef_s.re("p (a b) -> p a b", a=1), banks[2][:, 256:320].re("p (a b) -> p a b", a=1), rc_s, rs_s, tmp64_s.re("p (a b) -> p a b", a=1), 1)
        ld(kpes_o, kpef_s)
        vcopy(ckvb_s, ckvf_s)
        mset(kpeb_s, 0.0)
        vcopy(kpeb_s[:, 0:64], kpef_s)
        bb = banks[3].cast(BF16)
        for kc in range(2):
            tr(bb[:, kc * 128:(kc + 1) * 128], ckvb_s[:, kc * 128:(kc + 1) * 128], identb)
        tr(bb[:, 256:384], kpeb_s, identb)
        acopy(ckvTn.re("p a b -> p (a b)"), bb[:, 0:256])
        acopy(kpeTn[0:64, :], bb[0:64, 256:384])
        for h in range(4):
            for kc in range(2):
                mm(banks[3][:, h * 128:(h + 1) * 128], W_kv[:, kc, h, 0:128], ckvTn[:, kc, :], start=(kc == 0), stop=(kc == 1))
        acopy(KTn.re("p a b -> p (a b)"), banks[3])
        for kp in range(2):
            mset(cptb[kp], 0.0)
        for s_ in range(4):
            for kt in range(8):
                b = kt % 2
                r0 = s_ * 1024 + kt * 128
                ld(cat[b], cckv_d[r0:r0 + 128, :]); ld(cpt[b], ckpe_d[r0:r0 + 128, :])
                vcopy(catb[b], cat[b], eng="pool"); vcopy(cptb[b][:, 0:64], cpt[b], eng="pool")
                bb = banks[1].cast(BF16)
                for kc in range(2):
                    tr(bb[:, kc * 128:(kc + 1) * 128], catb[b][:, kc * 128:(kc + 1) * 128], identb)
                tr(bb[:, 256:384], cptb[b], identb)
                acopy(ckvTc[:, :, kt * 128:(kt + 1) * 128], bb[:, 0:256].re("p (a b) -> p a b", a=2))
                acopy(kpeTc[0:64, kt * 128:(kt + 1) * 128], bb[0:64, 256:384])
            for h in range(4):
                for n in range(2):
                    for kc in range(2):
                        mm(banks[2], W_kv[:, kc, h, 0:128], ckvTc[:, kc, n * 512:(n + 1) * 512], start=(kc == 0), stop=(kc == 1))
                    acopy(KTc[:, h, n * 512:(n + 1) * 512], banks[2])
            for kt in range(8):
                for kc in range(2):
                    mm(banks[3], ckvTc[:, kc, kt * 128:(kt + 1) * 128], W_kv[:, kc, :, 128:256], start=(kc == 0), stop=(kc == 1))
                vcopy(Vc[:, kt, :], banks[3])
            for kc in range(2):
                mm(banks[3][0:32, :], ckvTn[:, kc, s_ * 32:(s_ + 1) * 32], W_kv[:, kc, :, 128:256], start=(kc == 0), stop=(kc == 1))
            vcopy(Vn[0:32, :], banks[3][0:32, :])
            qs = slice(s_ * 32, (s_ + 1) * 32)
            for h in range(4):
                for n in range(2):
                    mm(SC[0:32, n * 512:(n + 1) * 512], QTn[:, isam, h, qs], KTc[:, h, n * 512:(n + 1) * 512], start=True, stop=False)
                    mm(SC[0:32, n * 512:(n + 1) * 512], QTr[0:64, isam, h, qs], kpeTc[0:64, n * 512:(n + 1) * 512], start=False, stop=True)
                mm(banks[4][0:32, 0:32], QTn[:, isam, h, qs], KTn[:, h, qs], start=True, stop=False)
                mm(banks[4][0:32, 0:32], QTr[0:64, isam, h, qs], kpeTn[0:64, qs], start=False, stop=True)
                acopy(scs[0:32, 0:1024], SC[0:32, :])
                acopy(scs[0:32, 1024:1056], banks[4][0:32, 0:32])
                red(mxs[0:32, :], scs[0:32, :], ALU.max)
                ts(negs[0:32, :], mxs[0:32, :], -1.0, None, ALU.mult)
                mset(Ps[0:32, 1024:1152], 0.0)
                act(Ps[0:32, 0:1056], scs[0:32, :], AF.Exp, bias=negs[0:32, :], accum=sums[0:32, :])
                pt_b = banks[5].cast(BF16)
                for kt in range(9):
                    tr(pt_b[:, kt * 32:(kt + 1) * 32], Ps[0:32, kt * 128:(kt + 1) * 128], identb[0:32, 0:32])
                vcopy(PTs.re("p a b -> p (a b)"), pt_b[:, 0:288])
                ov = banks[4][0:32, 128:256]
                for kt in range(8):
                    mm(ov, PTs[:, kt, :], Vc[:, kt, h * 128:(h + 1) * 128], start=(kt == 0), stop=False)
                mm(ov, PTs[0:32, 8, :], Vn[0:32, h * 128:(h + 1) * 128], start=False, stop=True)
                recip(sums[0:32, :], sums[0:32, :])
                ts(osb[0:32, h * 128:(h + 1) * 128], ov, sums[0:32, :], None, ALU.mult)
            ld(Osam[s_ * 32:(s_ + 1) * 32, :], osb[0:32, :])

        barrier()
        bump[0] = mark_after_mixer_state

        W_glu = alloc([128, 4, D], BF16, "W_glu"); W_o = alloc([128, 8, D], BF16, "W_o")
        ldw(W_glu, w_glu_d.re("(kt p) n -> p kt n", p=128)); ldw(W_o, w_o_d.re("(kt p) n -> p kt n", p=128))
        gos = alloc([128, 512], F32, "gos"); gom = alloc([128, 512], F32, "gom"); ld(gos, gos_d); ld(gom, gom_d)
        lng = alloc([128, D], F32, "lng"); lnb = alloc([128, D], F32, "lnb")
        ld(lng, lng_d[:, 0:D]); ld(lnb, lnb_d[:, 0:D])
        btab2 = Buf("tab2")
        for nm_ in ("S_t", "C_t", "R_t"):
            v_ = alloc([128, 16, 128], F32, nm_ + "2"); v_.b = btab2
            ld(v_.re("p a b -> p (a b)"), tab_d[nm_]); ssmx[nm_] = v_
        for nm_ in ("WBre", "WBim"):
            v_ = alloc([128, 16, 128], BF16, nm_ + "2")
            ld(v_.re("p a b -> p (a b)"), tab_d[nm_]); ssmx[nm_] = v_
        for nm_, src_ in (("Cre", cre_d), ("Cimn", cimn_d), ("Dblk", dblk_d)):
            v_ = alloc([128, 16, 32], BF16, nm_)
            ldw(v_.re("p a b -> p (a b)"), src_); ssmx[nm_] = v_
        ssmx["m4"] = [[alloc([128, 256], F32) for _ in range(4)] for _ in range(2)]
        ssmx["wri"] = [[alloc([128, 256], F32) for _ in range(2)] for _ in range(2)]
        ssmx["zri"] = [[alloc([128, 2, 128], F32) for _ in range(2)] for _ in range(2)]
        ssmx["xs4"] = [alloc([128, 16], F32) for _ in range(4)]
        ssmx["zl_all"] = [alloc([128, 4, 16], F32) for _ in range(2)]
        ssmx["dmd"] = [[alloc([128, 256], F32) for _ in range(4)]] * 2
        ssmx["xrb"] = [alloc([128, 2, 128], BF16) for _ in range(2)]
        ssmx["xib"] = [alloc([128, 2, 128], BF16) for _ in range(2)]
        S0 = alloc([128, 2, 4, 16], F32, "S0"); ld(S0.re("p a b c -> p (a b c)"), s0_d)
        Sfin = alloc([128, 2, 4, 16], F32, "Sfin")
        xt = [alloc([128, D], F32, "axt%d" % i) for i in range(NB)]
        xT = [alloc([128, 8, 128], BF16, "axT%d" % i) for i in range(NB)]
        uT = [alloc([128, 4, 128], BF16, "auT%d" % i) for i in range(NB)]
        ysq = alloc([128, 512], F32, "ysq"); yt = alloc([128, 512], F32, "yt"); ysg = alloc([128, 512], F32, "ysg")
        glb = alloc([128, 512], BF16, "glb"); gT = alloc([128, 4, 128], BF16, "gT")
        sg2 = ysq; osf = yt
        mixb = alloc([128, D], BF16, "mixb"); mixT = alloc([128, 8, 128], BF16, "mixT")
        rl = alloc([128, 4], F32, "rl"); omf = alloc([128, 4, 128], F32, "omf")
        ss_a = alloc([128, 1], F32); sq = ysg
        rres = alloc([128, D], F32, "rres"); hout = alloc([128, D], F32, "hout")
        stats = alloc([128, 12], F32); mvv = alloc([128, 2], F32)

        def pre_stages(i):
            b = i % NB
            yb = banks[3] if i % 2 == 0 else banks[0]

            def p0():
                load_xT(xo[i * 128:(i + 1) * 128, :], xt[b], xT[b], banks[1])
                for kt in range(4):
                    for k in range(8):
                        mm(banks[1][:, kt * 128:(kt + 1) * 128], W_in_u[:, k, kt * 128:(kt + 1) * 128], xT[b][:, k, :], start=(k == 0), stop=(k == 7))
                acopy(uT[b].re("p a b -> p (a b)"), banks[1])

            if i < 16:
                rr_ = ssm_rounds(uT[b], (banks[4], banks[5]), lambda gp, s_, part, i=i: Xown[:, i, part * 16 + gp:part * 16 + gp + 1], 1, ybank=yb)
            else:
                rr_ = ssm_rounds(uT[b], (banks[4], banks[5]), lambda gp, s_, part: S0[:, part, s_, gp:gp + 1], 4, ybank=yb,
                                 last_out=lambda s_: (Sfin[:, 0, s_, :], Sfin[:, 1, s_, :]))
                rr_.append(lambda: ld(ssms_o, Sfin.re("p a b c -> p (a b c)")))
            return [p0] + rr_

        def post_stages(i):
            b = i % NB
            yb = banks[3] if i % 2 == 0 else banks[0]
            xt_ = xt[b]

            def q0():
                act(ysq, yb, AF.Square)
                ts(yt, ysq, 0.044715, 1.0, ALU.mult, ALU.add)
                tt(yt, yt, yb, ALU.mult)

            def q1():
                act(ysg, yt, AF.Sigmoid, scale=1.5957691216057308)
                tt(glb, ysg, yb, ALU.mult)

            def q2():
                transpose_to(gT, glb, 4, banks[2], eng="act")

            def q3():
                for nb in range(2):
                    for k in range(4):
                        mm(SC[:, nb * 512:(nb + 1) * 512], gT[:, k, :], W_glu[:, k, nb * 512:(nb + 1) * 512], start=(k == 0), stop=(k == 3))

            def q4():
                act(sg2, SC[:, 512:1024], AF.Sigmoid)
                tt(osf, sg2, SC[:, 0:512], ALU.mult)

            def q5():
                rmsnorm(mixb[:, 0:512], osf, 512, gos, sq, ss_a)

            def q6():
                if i < 16:
                    recip(rl, l_run[:, i * 4:(i + 1) * 4])
                    tt(omf, Oacc[:, i, :, :], rl.un(2).bc([128, 4, 128]), ALU.mult)
                    rmsnorm(mixb[:, 512:1024], omf.re("p a b -> p (a b)"), 512, gom, sq, ss_a)
                else:
                    rmsnorm(mixb[:, 512:1024], Osam, 512, gom, sq, ss_a)

            def q7():
                for half in range(2):
                    transpose_to(mixT[:, half * 4:(half + 1) * 4, :], mixb[:, half * 512:(half + 1) * 512], 4, banks[2], eng=("act" if half == 0 else "dve"))

            def q8():
                for nb in range(2):
                    for k in range(8):
                        mm(SC[:, nb * 512:(nb + 1) * 512], mixT[:, k, :], W_o[:, k, nb * 512:(nb + 1) * 512], start=(k == 0), stop=(k == 7))

            def q9():
                stt(rres, xt_, ALPHA, SC, ALU.mult, ALU.add)
                layernorm(hout, rres, lng, lnb, stats, mvv)
                ld(h1_d[i * 128:(i + 1) * 128, :], hout)

            return [q0, q1, q2, q3, q4, q5, q6, q7, q8, q9]

        prev_post = []
        for i in range(NOWN + 1):
            pre = pre_stages(i) if i < NOWN else []
            n_ = max(len(pre), len(prev_post))
            for k_ in range(n_):
                if k_ < len(pre):
                    pre[k_]()
                if k_ < len(prev_post):
                    prev_post[k_]()
            prev_post = post_stages(i) if i < NOWN else []

        barrier()
        bump[0] = mark_after_mem

        W_xq = alloc([128, 8, D], BF16, "W_xq"); W_xo = alloc([128, 8, D], BF16, "W_xo")
        ldw(W_xq, w_xq_d.re("(kt p) n -> p kt n", p=128)); ldw(W_xo, w_xo_d.re("(kt p) n -> p kt n", p=128))
        lng = alloc([128, D], F32, "lng2"); lnb = alloc([128, D], F32, "lnb2")
        ld(lng, lng_d[:, D:2 * D]); ld(lnb, lnb_d[:, D:2 * D])
        hin = [alloc([128, D], F32, "bh%d" % i) for i in range(NB)]
        hb2 = [alloc([128, D], BF16, "bhb%d" % i) for i in range(2)]; hT2 = [alloc([128, 8, 128], BF16, "bhT%d" % i) for i in range(2)]
        qxb2 = [alloc([128, D], BF16, "qxb%d" % i) for i in range(2)]; qxT2 = [alloc([128, 8, 128], BF16, "qxT%d" % i) for i in range(2)]
        mx42 = [alloc([128, 4], F32) for _ in range(2)]; neg42 = [alloc([128, 4], F32) for _ in range(2)]; sum42 = [alloc([128, 4], F32) for _ in range(2)]
        Px2 = [alloc([128, 4, 256], BF16, "Px%d" % i) for i in range(2)]; PxT2 = [alloc([128, 8, 128], BF16, "PxT%d" % i) for i in range(2)]
        oxb2 = [alloc([128, D], BF16, "oxb%d" % i) for i in range(2)]; oxT2 = [alloc([128, 8, 128], BF16, "oxT%d" % i) for i in range(2)]
        rres2 = [alloc([128, D], F32, "brres%d" % i) for i in range(2)]; hout2 = [alloc([128, D], F32, "bhout%d" % i) for i in range(2)]
        stats2 = [alloc([128, 12], F32) for _ in range(2)]; mvv2 = [alloc([128, 2], F32) for _ in range(2)]
        hb_ = hb2[0]; hT = hT2[0]; qxb = qxb2[0]; qxT = qxT2[0]; mx4 = mx42[0]; neg4 = neg42[0]; sum4 = sum42[0]
        Px = Px2[0]; PxT = PxT2[0]; oxb = oxb2[0]; oxT = oxT2[0]; rres = rres2[0]; hout = hout2[0]; stats = stats2[0]; mvv = mvv2[0]
        cmk = [alloc([128, D], F32, "cmk%d" % i) for i in range(2)]
        cmkb = [alloc([128, D], BF16, "cmkb%d" % i) for i in range(1)]
        mkTs4 = [alloc([128, 4, 2, 256], BF16, "mkTs%d" % i) for i in range(4)]; mvs4 = [alloc([128, 2, D], BF16, "mvs%d" % i) for i in range(4)]
        scx = alloc([128, 8], F32, "scx"); oxs = alloc([128, D], BF16, "oxs")

        def xa_stages(i, p):
            A_ = banks[p]; C_ = dbl[1 + p]
            Ab = A_.cast(BF16)

            def tr8(src):
                for k in range(8):
                    tr(Ab[:, k * 128:(k + 1) * 128], src[:, k * 128:(k + 1) * 128], identb)

            def ev8(dst):
                acopy(dst[:, 0:4, :], Ab[:, 0:512].re("p (k n) -> p k n", k=4))
                vcopy(dst[:, 4:8, :], Ab[:, 512:1024].re("p (k n) -> p k n", k=4))

            def t0():
                ld(hin[p], h1_d[i * 128:(i + 1) * 128, :])
                vcopy(hb2[p], hin[p], eng="pool")

            def t3():
                for nb in range(2):
                    for k in range(8):
                        mm(C_[:, nb * 512:(nb + 1) * 512], hT2[p][:, k, :], W_xq[:, k, nb * 512:(nb + 1) * 512], start=(k == 0), stop=(k == 7))

            def t4():
                for nb in range(2):
                    P.op("act", lambda e, nb=nb: e.mul(out=qxb2[p].ap[:, nb * 512:(nb + 1) * 512], in_=C_.ap[:, nb * 512:(nb + 1) * 512], mul=X_SCALE),
                         reads=[C_.b], writes=[qxb2[p].b])

            def t7():
                for h in range(4):
                    for et in range(2):
                        mm(C_[:, h * 256:(h + 1) * 256], qxT2[p][:, h * 2 + et, :], mkT[:, h, et, :], start=(et == 0), stop=(et == 1))

            def t8():
                red(mx42[p], C_.re("p (h m) -> p h m", h=4), ALU.max)
                ts(neg42[p], mx42[p], -1.0, None, ALU.mult)
                for h in range(4):
                    act(Px2[p][:, h, :], C_[:, h * 256:(h + 1) * 256], AF.Exp, bias=neg42[p][:, h:h + 1], accum=sum42[p][:, h:h + 1])

            def t11():
                for h in range(4):
                    for mt in range(2):
                        mm(C_[:, h * 256:(h + 1) * 256], PxT2[p][:, h * 2 + mt, :], mvb[:, mt, h * 256:(h + 1) * 256], start=(mt == 0), stop=(mt == 1))

            def t12():
                recip(sum42[p], sum42[p])
                tt(oxb2[p].re("p (h e) -> p h e", h=4), C_.re("p (h e) -> p h e", h=4), sum42[p].un(2).bc([128, 4, 256]), ALU.mult)

            def t15():
                for nb in range(2):
                    for k in range(8):
                        mm(C_[:, nb * 512:(nb + 1) * 512], oxT2[p][:, k, :], W_xo[:, k, nb * 512:(nb + 1) * 512], start=(k == 0), stop=(k == 7))

            def t16():
                stt(rres2[p], hin[p], ALPHA, C_, ALU.mult, ALU.add)
                layernorm(hout2[p], rres2[p], lng, lnb, stats2[p], mvv2[p])
                ld(h2_d[i * 128:(i + 1) * 128, :], hout2[p])

            return [t0, lambda: tr8(hb2[p]), lambda: ev8(hT2[p]), t3, t4, lambda: tr8(qxb2[p]), lambda: ev8(qxT2[p]), t7, t8,
                    lambda: tr8(Px2[p].re("p a b -> p (a b)")), lambda: ev8(PxT2[p]), t11, t12, lambda: tr8(oxb2[p]), lambda: ev8(oxT2[p]), t15, t16]

        def prep_items():
            items = []
            for s_ in range(4):
                for mt in range(2):
                    r0 = s_ * 256 + mt * 128

                    def l0(r0=r0):
                        ld(cmk[0], cmk_d[r0:r0 + 128, :]); ld(cmk[1], cmv_d[r0:r0 + 128, :])

                    def l1(s_=s_, mt=mt):
                        vcopy(cmkb[0], cmk[0], eng="pool")
                        vcopy(mvs4[s_][:, mt, :], cmk[1], eng="pool")

                    def l2(s_=s_, mt=mt):
                        for half in range(2):
                            bb = banks[6 + half].cast(BF16)
                            for k in range(4):
                                kk = half * 4 + k
                                tr(bb[:, k * 128:(k + 1) * 128], cmkb[0][:, kk * 128:(kk + 1) * 128], identb)

                    def l3(s_=s_, mt=mt):
                        for half in range(2):
                            bb = banks[6 + half].cast(BF16)
                            acopy(mkTs4[s_][:, half * 2:half * 2 + 2, :, mt * 128:(mt + 1) * 128].re("p h e m -> p (h e) m"), bb[:, 0:512].re("p (k m) -> p k m", k=4))

                    items += [l0, l1, l2, l3]
            return items

        pitems = prep_items()
        pi_ = 0
        slot = 0
        for ip in range(8):
            sa = xa_stages(2 * ip, 0); sb_ = xa_stages(2 * ip + 1, 1)
            for a_, b_ in zip(sa, sb_):
                a_(); b_()
                slot += 1
                if slot % 4 == 0 and pi_ < len(pitems):
                    pitems[pi_](); pi_ += 1
        while pi_ < len(pitems):
            pitems[pi_](); pi_ += 1

        for i in range(16, NOWN):
            b = i % NB
            ld(hin[b], h1_d[i * 128:(i + 1) * 128, :])
            vcopy(hb_, hin[b], eng="pool")
            for half in range(2):
                transpose_to(hT[:, half * 4:(half + 1) * 4, :], hb_[:, half * 512:(half + 1) * 512], 4, banks[0], eng=("act" if half == 0 else "dve"))
            for nb in range(2):
                for k in range(8):
                    mm(banks[1 + nb], hT[:, k, :], W_xq[:, k, nb * 512:(nb + 1) * 512], start=(k == 0), stop=(k == 7))
                P.op("act", lambda e, nb=nb: e.mul(out=qxb.ap[:, nb * 512:(nb + 1) * 512], in_=banks[1 + nb].ap, mul=X_SCALE), reads=[banks[1 + nb].b], writes=[qxb.b])
            for half in range(2):
                transpose_to(qxT[:, half * 4:(half + 1) * 4, :], qxb[:, half * 512:(half + 1) * 512], 4, banks[3], eng=("act" if half == 0 else "dve"))
            if i < 16:
                for h in range(4):
                    for et in range(2):
                        mm(SC[:, h * 256:(h + 1) * 256], qxT[:, h * 2 + et, :], mkT[:, h, et, :], start=(et == 0), stop=(et == 1))
                red(mx4, SC.re("p (h m) -> p h m", h=4), ALU.max)
                ts(neg4, mx4, -1.0, None, ALU.mult)
                for h in range(4):
                    act(Px[:, h, :], SC[:, h * 256:(h + 1) * 256], AF.Exp, bias=neg4[:, h:h + 1], accum=sum4[:, h:h + 1])
                for half in range(2):
                    transpose_to(PxT[:, half * 4:(half + 1) * 4, :], Px.re("p a b -> p (a b)")[:, half * 512:(half + 1) * 512], 4, banks[4], eng=("act" if half == 0 else "dve"))
                for h in range(4):
                    for mt in range(2):
                        mm(SC[:, h * 256:(h + 1) * 256], PxT[:, h * 2 + mt, :], mvb[:, mt, h * 256:(h + 1) * 256], start=(mt == 0), stop=(mt == 1))
                recip(sum4, sum4)
                tt(oxb.re("p (h e) -> p h e", h=4), SC.re("p (h e) -> p h e", h=4), sum4.un(2).bc([128, 4, 256]), ALU.mult)
            else:
                for s_ in range(4):
                    qs = slice(s_ * 32, (s_ + 1) * 32)
                    mkTs = mkTs4[s_]; mvs = mvs4[s_]
                    for h in range(4):
                        for et in range(2):
                            mm(SC[0:32, h * 256:(h + 1) * 256], qxT[:, h * 2 + et, qs], mkTs[:, h, et, :], start=(et == 0), stop=(et == 1))
                    red(mx4[0:32, :], SC[0:32, :].re("p (h m) -> p h m", h=4), ALU.max)
                    ts(neg4[0:32, :], mx4[0:32, :], -1.0, None, ALU.mult)
                    for h in range(4):
                        act(Px[0:32, h, :], SC[0:32, h * 256:(h + 1) * 256], AF.Exp, bias=neg4[0:32, h:h + 1], accum=sum4[0:32, h:h + 1])
                    bb = banks[4].cast(BF16)
                    for k in range(8):
                        tr(bb[:, k * 32:(k + 1) * 32], Px.re("p a b -> p (a b)")[0:32, k * 128:(k + 1) * 128], identb[0:32, 0:32])
                    vcopy(PxT[:, :, 0:32], bb[:, 0:256].re("p (k q) -> p k q", k=8))
                    for h in range(4):
                        for mt in range(2):
                            mm(SC[0:32, h * 256:(h + 1) * 256], PxT[:, h * 2 + mt, 0:32], mvs[:, mt, h * 256:(h + 1) * 256], start=(mt == 0), stop=(mt == 1))
                    recip(sum4[0:32, :], sum4[0:32, :])
                    tt(oxs[0:32, :].re("p (h e) -> p h e", h=4), SC[0:32, :].re("p (h e) -> p h e", h=4), sum4[0:32, :].un(2).bc([32, 4, 256]), ALU.mult)
                    ld(oxb[s_ * 32:(s_ + 1) * 32, :], oxs[0:32, :])
            for half in range(2):
                transpose_to(oxT[:, half * 4:(half + 1) * 4, :], oxb[:, half * 512:(half + 1) * 512], 4, banks[5], eng=("act" if half == 0 else "dve"))
            for nb in range(2):
                for k in range(8):
                    mm(banks[1 + nb], oxT[:, k, :], W_xo[:, k, nb * 512:(nb + 1) * 512], start=(k == 0), stop=(k == 7))
            for nb in range(2):
                stt(rres[:, nb * 512:(nb + 1) * 512], hin[b][:, nb * 512:(nb + 1) * 512], ALPHA, banks[1 + nb], ALU.mult, ALU.add)
            layernorm(hout, rres, lng, lnb, stats, mvv)
            ld(h2_d[i * 128:(i + 1) * 128, :], hout)

        barrier()
        bump[0] = mark_after_mem

        W1 = alloc([128, 8, 4096], BF16, "W1"); W2 = alloc([128, 32, D], BF16, "W2")
        for c in range(2):
            ldw(W1[:, :, c * 2048:(c + 1) * 2048], w_ff1_d.re("(kt p) n -> p kt n", p=128)[:, :, c * 2048:(c + 1) * 2048])
        for c in range(4):
            ldw(W2[:, c * 8:(c + 1) * 8, :], w_ff2_d.re("(kt p) n -> p kt n", p=128)[:, c * 8:(c + 1) * 8, :])
        lng = alloc([128, D], F32, "lng3"); lnb = alloc([128, D], F32, "lnb3")
        ld(lng, lng_d[:, 2 * D:3 * D]); ld(lnb, lnb_d[:, 2 * D:3 * D])
        hin = alloc([128, D], F32, "ch")
        hb_ = alloc([128, D], BF16, "chb"); hT = alloc([128, 8, 512], BF16, "chT")
        zr = [alloc([128, 512], F32, "zr%d" % i) for i in range(2)]; zT = alloc([128, 32, 512], BF16, "zT")
        rres = alloc([128, D], F32, "crres"); hout = alloc([128, D], F32, "chout")
        stats = alloc([128, 12], F32); mvv = alloc([128, 2], F32)
        groups = [list(range(g_ * 4, g_ * 4 + 4)) for g_ in range(4)] + [[16]]
        for grp in groups:
            nt = len(grp)
            for q_, i in enumerate(grp):
                ld(hin, h2_d[i * 128:(i + 1) * 128, :])
                vcopy(hb_, hin, eng="pool")
                for half in range(2):
                    transpose_to(hT[:, half * 4:(half + 1) * 4, q_ * 128:(q_ + 1) * 128], hb_[:, half * 512:(half + 1) * 512], 4, banks[0], eng=("act" if half == 0 else "dve"))
            W_ = nt * 128
            for f in range(32):
                bk = banks[1 + f % 2]
                for k in range(8):
                    mm(bk[:, 0:W_], W1[:, k, f * 128:(f + 1) * 128], hT[:, k, 0:W_], start=(k == 0), stop=(k == 7))
                act(zr[f % 2][:, 0:W_], bk[:, 0:W_], AF.Relu)
                tt(zT[:, f, 0:W_], zr[f % 2][:, 0:W_], zr[f % 2][:, 0:W_], ALU.mult)
            for q_, i in enumerate(grp):
                for nb in range(2):
                    for f in range(32):
                        mm(SC[:, nb * 512:(nb + 1) * 512], zT[:, f, q_ * 128:(q_ + 1) * 128], W2[:, f, nb * 512:(nb + 1) * 512], start=(f == 0), stop=(f == 31))
                ld(hin, h2_d[i * 128:(i + 1) * 128, :])
                stt(rres, hin, ALPHA, SC, ALU.mult, ALU.add)
                layernorm(hout, rres, lng, lnb, stats, mvv)
                ld(y_o[i * 128:(i + 1) * 128, :], hout)

        P.emit(st)
    return nc


def _lay_gp(a):
    sh = a.shape[2:]
    n = len(sh)
    return np.ascontiguousarray(a.reshape(16, 2, 64, *sh).transpose(1, 2, 0, *range(3, 3 + n)).reshape(128, 16, *sh))


def _bc(v, n=128):
    return np.ascontiguousarray(np.broadcast_to(np.asarray(v, np.float32).reshape(1, -1), (n, np.asarray(v).size)))


def _rope_tables(pos):
    inv = (10000.0 ** (-np.arange(32, dtype=np.float32) / 32)).astype(np.float32)
    ang = pos.astype(np.float32)[:, None] * inv[None, :]
    c = np.cos(ang).astype(np.float32)
    s = np.sin(ang).astype(np.float32)
    return np.concatenate([c, c], 1), np.concatenate([-s, s], 1)


_NC_CACHE = {}


def kernel(x_prompt, x_sample, mem_prompt, cache_mla_ckv, cache_mla_kpe, state_ssm_re, state_ssm_im,
           cache_mem_k, cache_mem_v, w_in, g_q, w_q_up, g_kv, w_kv_up, a_re, a_im, b_re, b_im, c_re, c_im,
           d_skip, log_dt, w_glu, g_out_ssm, g_out_mla, w_o, w_xq, w_xk, w_xv, w_xo, w_ff1, w_ff2, ln_g, ln_b):
    f = lambda a: np.ascontiguousarray(np.asarray(a, dtype=np.float32))
    x_prompt = f(x_prompt); x_sample = f(x_sample)
    xp = x_prompt[0]
    cre = np.zeros((128, 16, 32), np.float32); cimn = np.zeros((128, 16, 32), np.float32)
    cr = f(c_re)[0].reshape(16, 2, 16, 64); ci = f(c_im)[0].reshape(16, 2, 16, 64)
    for g2 in range(2):
        cre[g2 * 64:(g2 + 1) * 64, :, g2 * 16:(g2 + 1) * 16] = cr[:, g2].transpose(2, 0, 1)
        cimn[g2 * 64:(g2 + 1) * 64, :, g2 * 16:(g2 + 1) * 16] = -ci[:, g2].transpose(2, 0, 1)
    dblk = np.zeros((128, 16, 32), np.float32)
    dd = f(d_skip)[0].reshape(512)
    for gp in range(16):
        for c in range(32):
            ch = gp * 32 + c
            dblk[ch % 128, gp, c] = dd[ch]
    pos_p = np.arange(16384)
    rcp, rsp = _rope_tables(pos_p)
    common = {
        "xp": xp, "mem": f(mem_prompt)[0], "ident": np.eye(128, dtype=np.float32),
        "w_in": f(w_in)[0], "w_q": f(w_q_up)[0].reshape(384, 768), "w_kv": f(w_kv_up)[0].reshape(256, 1024),
        "w_glu": f(w_glu)[0], "w_o": f(w_o)[0], "w_xq": f(w_xq)[0].reshape(D, D), "w_xk": f(w_xk)[0].reshape(D, D),
        "w_xv": f(w_xv)[0].reshape(D, D), "w_xo": f(w_xo)[0].reshape(D, D), "w_ff1": f(w_ff1)[0], "w_ff2": f(w_ff2)[0],
        "gq": _bc(f(g_q)[0]), "gkv": _bc(f(g_kv)[0]), "gos": _bc(f(g_out_ssm)[0]), "gom": _bc(f(g_out_mla)[0]),
        "lng": _bc(f(ln_g)[0].reshape(-1)), "lnb": _bc(f(ln_b)[0].reshape(-1)),
        "ropec_p": rcp, "ropes_p": rsp,
        "ar": _lay_gp(f(a_re)[0]), "ai": _lay_gp(f(a_im)[0]),
        "ldt": _lay_gp(np.ascontiguousarray(np.broadcast_to(f(log_dt)[0][:, None], (32, 64)))),
        "bre": _lay_gp(f(b_re)[0]).reshape(128, 256), "bim": _lay_gp(f(b_im)[0]).reshape(128, 256),
        "jj": _bc(np.arange(1, 129, dtype=np.float32)), "jj2": _bc(127.0 - np.arange(128, dtype=np.float32)),
        "cre": cre.reshape(128, 512), "cimn": cimn.reshape(128, 512), "dblk": dblk.reshape(128, 512),
    }
    qi = np.arange(128)[:, None]
    in_maps = []
    for c in range(NCORES):
        tiles = [8 * i + c for i in range(16)]
        xo = np.concatenate([xp[t * 128:(t + 1) * 128] for t in tiles] + [x_sample[4 * c:4 * c + 4].reshape(128, D)], 0)
        pos_o = np.concatenate([np.arange(t * 128, (t + 1) * 128) for t in tiles] + [np.tile(1024 + np.arange(32), 4)])
        rco, rso = _rope_tables(pos_o)
        kj = np.arange(1024)[None, :]
        vis = ((kj // 128) < c) | (((kj // 128) == c) & (((kj % 128) // 64) <= (qi // 64)))
        maskd = np.where(vis, 0.0, NEG).astype(np.float32)
        onehot = np.zeros((128, 8), np.float32); onehot[:, c] = 1.0
        s0 = np.stack([_lay_gp(f(state_ssm_re)[0, 4 * c + s]) for s in range(4)], 1)
        s0i = np.stack([_lay_gp(f(state_ssm_im)[0, 4 * c + s]) for s in range(4)], 1)
        m = dict(common)
        m.update({
            "xo": np.ascontiguousarray(xo), "ropec_o": rco, "ropes_o": rso, "maskd": maskd, "onehot": onehot,
            "s0": np.ascontiguousarray(np.stack([s0, s0i], 1).reshape(128, 128)),
            "cckv": f(cache_mla_ckv)[0, 4 * c:4 * c + 4].reshape(4096, 256),
            "ckpe": f(cache_mla_kpe)[0, 4 * c:4 * c + 4].reshape(4096, 64),
            "cmk": f(cache_mem_k)[0, 4 * c:4 * c + 4].reshape(1024, D),
            "cmv": f(cache_mem_v)[0, 4 * c:4 * c + 4].reshape(1024, D),
        })
        in_maps.append(m)
    if "nc" not in _NC_CACHE:
        _NC_CACHE["nc"] = build_nc()
    res = run_bass_kernel_spmd(_NC_CACHE["nc"], in_maps, core_ids=list(range(NCORES)))
    R = res.results
    y_p = np.zeros((1, 16384, D), np.float32); y_s = np.zeros((32, 32, D), np.float32)
    ckv_s = np.zeros((1, 32, 32, 256), np.float32); kpe_s = np.zeros((1, 32, 32, 64), np.float32)
    sre_s = np.zeros((1, 32, 32, 64), np.float32); sim_s = np.zeros((1, 32, 32, 64), np.float32)

    def unlay(a):
        return a.reshape(2, 64, 16).transpose(2, 0, 1).reshape(32, 64)

    for c in range(NCORES):
        yo = R[c]["y_o"]
        for i in range(16):
            t = 8 * i + c
            y_p[0, t * 128:(t + 1) * 128] = yo[i * 128:(i + 1) * 128]
        y_s[4 * c:4 * c + 4] = yo[16 * 128:].reshape(4, 32, D)
        ckv_s[0, 4 * c:4 * c + 4] = R[c]["ckvs_o"].reshape(4, 32, 256)
        kpe_s[0, 4 * c:4 * c + 4] = R[c]["kpes_o"].reshape(4, 32, 64)
        sf = R[c]["ssms_o"].reshape(128, 2, 4, 16)
        for s in range(4):
            sre_s[0, 4 * c + s] = unlay(sf[:, 0, s, :])
            sim_s[0, 4 * c + s] = unlay(sf[:, 1, s, :])
    r0 = R[0]
    ckv_p = r0["ckv_o"].reshape(1, 1, 16384, 256); kpe_p = r0["kpe_o"].reshape(1, 1, 16384, 64)
    sp = r0["ssmp_o"]
    sre_p = unlay(sp[:, 0:16]).reshape(1, 1, 32, 64); sim_p = unlay(sp[:, 16:32]).reshape(1, 1, 32, 64)
    mk_p = r0["memk_o"].reshape(1, 1, 256, 4, 256); mv_p = r0["memv_o"].reshape(1, 1, 256, 4, 256)
    return (y_p, y_s, ckv_p, kpe_p, sre_p, sim_p, mk_p, mv_p, ckv_s, kpe_s, sre_s, sim_s)
```

```python
import math
from contextlib import ExitStack

import numpy as np
import concourse.bass as bass
import concourse.mybir as mybir
from concourse.bass_utils import run_bass_kernel_spmd

F32 = mybir.dt.float32
BF16 = mybir.dt.bfloat16
I32 = mybir.dt.int32
AF = mybir.ActivationFunctionType
ALU = mybir.AluOpType
AX = mybir.AxisListType

NCORES = 8
D = 1024
NTP = 128
NOWN = 17
EPS = 1e-5
ALPHA = 2.0 ** 0.25
MLA_SCALE = 192.0 ** -0.5
X_SCALE = 256.0 ** -0.5
TWO_PI = 2.0 * math.pi
NEG = -1e30
NRING = 24
COMPUTE = ("pe", "act", "dve", "pool")


class Buf:
    __slots__ = ("name", "lw", "rd")

    def __init__(self, name=""):
        self.name = name
        self.lw = None
        self.rd = []


def _flat(bs):
    out = []
    for b in bs:
        if isinstance(b, (tuple, list)):
            out.extend(b)
        else:
            out.append(b)
    return out


class Op:
    __slots__ = ("eng", "fn", "deps", "isdma", "flag", "cnt", "ring", "n")

    def __init__(self, eng, fn, isdma):
        self.eng = eng
        self.fn = fn
        self.isdma = isdma
        self.deps = set()
        self.flag = False
        self.cnt = 0
        self.ring = None
        self.n = 0


class Prog:
    def __init__(self, nc):
        self.nc = nc
        self.ops = {e: [] for e in ("pe", "act", "dve", "pool", "sp")}
        self.ndma = {e: 0 for e in self.ops}
        self.allops = []
        self.floor = []
        self.dmas_since = []

    def _add(self, eng, fn, reads, writes, isdma):
        op = Op(eng, fn, isdma)
        reads = _flat(reads)
        writes = _flat(writes)
        for f in self.floor:
            op.deps.add(f)
        for b in reads:
            if b.lw is not None:
                op.deps.add(b.lw)
        for b in writes:
            if b.lw is not None:
                op.deps.add(b.lw)
            for r in b.rd:
                op.deps.add(r)
        for b in reads:
            b.rd.append(op)
        for b in writes:
            b.lw = op
            b.rd = []
        op.deps.discard(op)
        if isdma:
            op.n = self.ndma[eng]
            self.ndma[eng] += 1
            self.dmas_since.append(op)
        self.ops[eng].append(op)
        self.allops.append(op)
        return op

    def op(self, eng, fn, reads=(), writes=()):
        return self._add(eng, fn, reads, writes, False)

    def dma(self, eng, fn, reads=(), writes=()):
        return self._add(eng, fn, reads, writes, True)

    def barrier(self, fn):
        op = Op("pool", fn, False)
        for f in self.floor:
            op.deps.add(f)
        for e in COMPUTE:
            for o in reversed(self.ops[e]):
                if not o.isdma:
                    op.deps.add(o)
                    break
        for o in self.dmas_since:
            op.deps.add(o)
        self.dmas_since = []
        self.ops["pool"].append(op)
        self.allops.append(op)
        self.floor = [op]

    def emit(self, stack):
        nc = self.nc
        for op in self.allops:
            for d in op.deps:
                if d.eng == "pe" and op.eng == "pe" and not d.isdma:
                    continue
                d.flag = True
        sems = {e: stack.enter_context(nc.semaphore("s_" + e)) for e in COMPUTE}
        rings = {}
        for e in self.ops:
            if self.ndma[e]:
                rings[e] = [stack.enter_context(nc.semaphore("r_%s_%d" % (e, i))) for i in range(NRING)]
        for e in self.ops:
            c = 0
            for op in self.ops[e]:
                if op.isdma:
                    op.ring = rings[e][op.n % NRING]
                    op.cnt = 16 * (op.n // NRING + 1)
                elif op.flag:
                    c += 1
                    op.cnt = c
        block = stack.enter_context(nc.Block())

        def run(e, h):
            waited = {}

            def wait(sem, val):
                k = id(sem)
                if waited.get(k, 0) >= val:
                    return
                waited[k] = val
                h.wait_ge(sem, val)

            for op in self.ops[e]:
                for d in op.deps:
                    if d.isdma:
                        wait(d.ring, d.cnt)
                    else:
                        if d.eng == "pe" and e == "pe":
                            continue
                        wait(sems[d.eng], d.cnt)
                if op.isdma and op.n >= NRING:
                    wait(op.ring, op.cnt - 16)
                ins = op.fn(h)
                if op.isdma:
                    ins.then_inc(op.ring, 16)
                elif op.flag:
                    ins.then_inc(sems[e], 1)
            if e in rings:
                n = self.ndma[e]
                for i in range(min(n, NRING)):
                    last = ((n - 1 - i) // NRING) * NRING + i
                    wait(rings[e][i], 16 * (last // NRING + 1))

        @block.tensor
        def _(h):
            run("pe", h)

        @block.scalar
        def _(h):
            run("act", h)

        @block.vector
        def _(h):
            run("dve", h)

        @block.gpsimd
        def _(h):
            run("pool", h)

        @block.sync
        def _(h):
            run("sp", h)


class V:
    __slots__ = ("ap", "b")

    def __init__(self, ap, b):
        self.ap = ap
        self.b = b

    def __getitem__(self, idx):
        return V(self.ap[idx], self.b)

    def re(self, pat, **kw):
        return V(self.ap.rearrange(pat, **kw), self.b)

    def cast(self, dt):
        return V(self.ap.bitcast(dt), self.b)

    def bc(self, shape):
        return V(self.ap.broadcast_to(shape), self.b)

    def un(self, ax):
        return V(self.ap.unsqueeze(ax), self.b)


def build_nc():
    nc = bass.Bass("TRN2", target_bir_lowering=False)
    P = Prog(nc)
    dram = {}

    def din(name, shape, dt=F32):
        t = nc.dram_tensor(name, list(shape), dt, kind="ExternalInput")
        dram[name] = V(t.ap(), Buf(name))
        return dram[name]

    def dout(name, shape):
        t = nc.dram_tensor(name, list(shape), F32, kind="ExternalOutput")
        dram[name] = V(t.ap(), Buf(name))
        return dram[name]

    def dscr(name, shape, dt=F32):
        t = nc.dram_tensor(name, list(shape), dt)
        return V(t.ap(), Buf(name))

    xp = din("xp", [NTP * 128, D])
    xo = din("xo", [NOWN * 128, D])
    mem = din("mem", [256, D])
    ident_d = din("ident", [128, 128])
    w_in_d = din("w_in", [D, 1216]); w_q_d = din("w_q", [384, 768]); w_kv_d = din("w_kv", [256, 1024])
    w_glu_d = din("w_glu", [512, 1024]); w_o_d = din("w_o", [D, D]); w_xq_d = din("w_xq", [D, D])
    w_xk_d = din("w_xk", [D, D]); w_xv_d = din("w_xv", [D, D]); w_xo_d = din("w_xo", [D, D])
    w_ff1_d = din("w_ff1", [D, 4096]); w_ff2_d = din("w_ff2", [4096, D])
    gq_d = din("gq", [128, 384]); gkv_d = din("gkv", [128, 256]); gos_d = din("gos", [128, 512]); gom_d = din("gom", [128, 512])
    lng_d = din("lng", [128, 3 * D]); lnb_d = din("lnb", [128, 3 * D])
    ropec_p = din("ropec_p", [NTP * 128, 64]); ropes_p = din("ropes_p", [NTP * 128, 64])
    ropec_o = din("ropec_o", [NOWN * 128, 64]); ropes_o = din("ropes_o", [NOWN * 128, 64])
    maskd_d = din("maskd", [128, 1024]); onehot_d = din("onehot", [128, 8])
    ar_d = din("ar", [128, 16]); ai_d = din("ai", [128, 16]); ldt_d = din("ldt", [128, 16])
    bre_d = din("bre", [128, 256]); bim_d = din("bim", [128, 256]); jj_d = din("jj", [128, 128]); jj2_d = din("jj2", [128, 128])
    cre_d = din("cre", [128, 512]); cimn_d = din("cimn", [128, 512]); dblk_d = din("dblk", [128, 512])
    s0_d = din("s0", [128, 128])
    cckv_d = din("cckv", [4 * 1024, 256]); ckpe_d = din("ckpe", [4 * 1024, 64])
    cmk_d = din("cmk", [4 * 256, D]); cmv_d = din("cmv", [4 * 256, D])

    y_o = dout("y_o", [NOWN * 128, D])
    ckv_o = dout("ckv_o", [NTP * 128, 256]); kpe_o = dout("kpe_o", [NTP * 128, 64])
    ssmp_o = dout("ssmp_o", [128, 32])
    memk_o = dout("memk_o", [256, D]); memv_o = dout("memv_o", [256, D])
    ckvs_o = dout("ckvs_o", [128, 256]); kpes_o = dout("kpes_o", [128, 64]); ssms_o = dout("ssms_o", [128, 128])

    h1_d = dscr("h1_d", [NOWN * 128, D]); h2_d = dscr("h2_d", [NOWN * 128, D])
    tab_d = {n_: dscr("tab_" + n_, [128, 2048]) for n_ in ("S_t", "C_t", "R_t")}
    tab_d["WBre"] = dscr("tab_WBre", [128, 2048], BF16); tab_d["WBim"] = dscr("tab_WBim", [128, 2048], BF16)

    with ExitStack() as st:
        ARENA_N = 52000
        arena = st.enter_context(nc.sbuf_tensor("arena", [128, ARENA_N], F32))
        bump = [0]

        def alloc(shape, dt=F32, name=""):
            n = 1
            for s_ in shape[1:]:
                n *= s_
            words = (n * (2 if dt == BF16 else 4) + 3) // 4
            words = (words + 7) // 8 * 8
            off = bump[0]
            bump[0] += words
            assert bump[0] <= ARENA_N, ("SBUF arena overflow", name, bump[0])
            ap = arena[:, off:off + words]
            if dt != F32:
                ap = ap.bitcast(dt)
            ap = ap[:, 0:n]
            if len(shape) > 2:
                names = " ".join("d%d" % i for i in range(len(shape) - 1))
                kw = {"d%d" % i: shape[i + 1] for i in range(len(shape) - 1)}
                ap = ap.rearrange("p (%s) -> p %s" % (names, names), **kw)
            return V(ap, Buf(name))

        banks = []
        dbl = []
        for d_ in range(4):
            t = st.enter_context(nc.psum_tensor("dbank%d" % d_, [128, 1024], F32))
            b0 = Buf("bank%d" % (2 * d_)); b1 = Buf("bank%d" % (2 * d_ + 1))
            banks.append(V(t[:, 0:512], b0)); banks.append(V(t[:, 512:1024], b1))
            dbl.append(V(t[:], (b0, b1)))
        SC = dbl[3]
        SCs = [dbl[3], dbl[0]]

        def mm(out, lhsT, rhs, start=True, stop=True):
            P.op("pe", lambda e: e.matmul(out.ap, lhsT=lhsT.ap, rhs=rhs.ap, start=start, stop=stop),
                 reads=[lhsT.b, rhs.b], writes=[out.b])

        def tr(out, in_, idt):
            P.op("pe", lambda e: e.transpose(out=out.ap, in_=in_.ap, identity=idt.ap), reads=[in_.b, idt.b], writes=[out.b])

        def act(out, in_, func, bias=None, scale=None, accum=None):
            kw = {}
            rd = [in_.b]
            wr = [out.b]
            if bias is not None:
                if isinstance(bias, V):
                    kw["bias"] = bias.ap; rd.append(bias.b)
                else:
                    kw["bias"] = bias
            if scale is not None:
                if isinstance(scale, V):
                    kw["scale"] = scale.ap; rd.append(scale.b)
                else:
                    kw["scale"] = scale
            if accum is not None:
                kw["accum_out"] = accum.ap; wr.append(accum.b)
            P.op("act", lambda e: e.activation(out=out.ap, in_=in_.ap, func=func, **kw), reads=rd, writes=wr)

        def acopy(out, in_):
            P.op("act", lambda e: e.copy(out=out.ap, in_=in_.ap), reads=[in_.b], writes=[out.b])

        def vcopy(out, in_, eng="dve"):
            P.op(eng, lambda e: e.tensor_copy(out=out.ap, in_=in_.ap), reads=[in_.b], writes=[out.b])

        def tt(out, in0, in1, op, eng="dve"):
            P.op(eng, lambda e: e.tensor_tensor(out=out.ap, in0=in0.ap, in1=in1.ap, op=op), reads=[in0.b, in1.b], writes=[out.b])

        def ts(out, in0, s1, s2, op0, op1=None, eng="dve"):
            rd = [in0.b]
            a1 = s1
            a2 = s2
            if isinstance(s1, V):
                a1 = s1.ap; rd.append(s1.b)
            if isinstance(s2, V):
                a2 = s2.ap; rd.append(s2.b)
            if op1 is None:
                P.op(eng, lambda e: e.tensor_scalar(out=out.ap, in0=in0.ap, scalar1=a1, scalar2=None, op0=op0), reads=rd, writes=[out.b])
            else:
                P.op(eng, lambda e: e.tensor_scalar(out=out.ap, in0=in0.ap, scalar1=a1, scalar2=a2, op0=op0, op1=op1), reads=rd, writes=[out.b])

        def stt(out, in0, scalar, in1, op0, op1):
            rd = [in0.b, in1.b]
            a = scalar
            if isinstance(scalar, V):
                a = scalar.ap; rd.append(scalar.b)
            P.op("dve", lambda e: e.scalar_tensor_tensor(out=out.ap, in0=in0.ap, scalar=a, in1=in1.ap, op0=op0, op1=op1), reads=rd, writes=[out.b])

        def red(out, in_, op, axis=AX.X):
            P.op("dve", lambda e: e.tensor_reduce(out=out.ap, in_=in_.ap, axis=axis, op=op), reads=[in_.b], writes=[out.b])

        def recip(out, in_):
            P.op("dve", lambda e: e.reciprocal(out=out.ap, in_=in_.ap), reads=[in_.b], writes=[out.b])

        def scan(out, d0, d1, init):
            P.op("dve", lambda e: e.tensor_tensor_scan(out=out.ap, data0=d0.ap, data1=d1.ap, initial=init.ap, op0=ALU.mult, op1=ALU.add),
                 reads=[d0.b, d1.b, init.b], writes=[out.b])

        def mset(out, val, eng="pool"):
            P.op(eng, lambda e: e.memset(out.ap, val), writes=[out.b])

        def ld(out, in_, eng="sp"):
            P.dma(eng, lambda e: e.dma_start(out=out.ap, in_=in_.ap), reads=[in_.b], writes=[out.b])

        def ldw(out, in_):
            P.dma("pool", lambda e: e.dma_start(out=out.ap, in_=in_.ap), reads=[in_.b], writes=[out.b])

        ident = alloc([128, 128], F32, "ident"); identb = alloc([128, 128], BF16, "identb")
        bar_scr = alloc([128, 8], F32, "barscr")
        ld(ident, ident_d); ldw(identb, ident_d)
        epsc = alloc([128, 1], F32, "epsc"); mset(epsc, EPS)

        def barrier():
            P.barrier(lambda e: e.memset(bar_scr.ap, 0.0))

        def load_xT(src_rows, xt, xT, bank):
            ld(xt, src_rows)
            for hb in range(2):
                for k in range(4):
                    kk = hb * 4 + k
                    tr(bank[:, k * 128:(k + 1) * 128], xt[:, kk * 128:(kk + 1) * 128], ident)
                src = bank.re("p (k n) -> p k n", k=4)
                if hb == 0:
                    acopy(xT[:, 0:4, :], src)
                else:
                    vcopy(xT[:, 4:8, :], src)

        def transpose_to(dst, src, ncol, bank, eng="act", rows=128):
            bb = bank.cast(BF16)
            for k in range(ncol):
                tr(bb[:, k * 128:k * 128 + rows], src[:, k * 128:(k + 1) * 128], identb[0:rows, 0:rows])
            s_ = bb[:, 0:ncol * 128].re("p (k n) -> p k n", k=ncol)[:, :, 0:rows]
            if eng == "act":
                acopy(dst, s_)
            else:
                vcopy(dst, s_)

        def rmsnorm(out, src, n, gtile, sq, ss):
            act(sq, src, AF.Square, accum=ss)
            act(ss, ss, AF.Ln, scale=1.0 / n, bias=epsc)
            act(ss, ss, AF.Exp, scale=-0.5)
            stt(out, src, ss, gtile, ALU.mult, ALU.mult)

        def layernorm(out, r, g, b, stats, mv):
            for c in range(2):
                P.op("dve", lambda e, c=c: e.bn_stats(out=stats.ap[:, c * 6:(c + 1) * 6], in_=r.ap[:, c * 512:(c + 1) * 512]), reads=[r.b], writes=[stats.b])
            P.op("dve", lambda e: e.bn_aggr(out=mv.ap[:, 0:2], in_=stats.ap[:, 0:12]), reads=[stats.b], writes=[mv.b])
            act(mv[:, 1:2], mv[:, 1:2], AF.Ln, bias=epsc)
            act(mv[:, 1:2], mv[:, 1:2], AF.Exp, scale=-0.5)
            ts(out, r, mv[:, 0:1], mv[:, 1:2], ALU.subtract, ALU.mult)
            tt(out, out, g, ALU.mult, eng="pool")
            tt(out, out, b, ALU.add, eng="pool")

        def rope(out, src, cc, ss_, tmp, nh, eng="dve"):
            ccb = cc.un(1).bc([128, nh, 64]); ssb = ss_.un(1).bc([128, nh, 64])
            tt(out, src, ccb, ALU.mult, eng=eng)
            tt(tmp[:, :, 0:32], src[:, :, 32:64], ssb[:, :, 0:32], ALU.mult, eng=eng)
            tt(tmp[:, :, 32:64], src[:, :, 0:32], ssb[:, :, 32:64], ALU.mult, eng=eng)
            tt(out, out, tmp, ALU.add, eng="pool")

        mkT = alloc([128, 4, 2, 256], BF16, "mkT")
        mvb = alloc([128, 2, D], BF16, "mvb")
        mark_after_mem = bump[0]
        btab = Buf("tab")
        W_in_u = alloc([128, 8, 512], BF16, "W_in_u"); W_in_kv = alloc([128, 8, 320], BF16, "W_in_kv")
        LrT = alloc([128, 16, 128], BF16, "LrT"); LiT = alloc([128, 16, 128], BF16, "LiT")
        BRb = alloc([128, 16, 32], F32, "BRb"); BIb = alloc([128, 16, 32], F32, "BIb")
        lam128 = alloc([128, 2, 16], F32, "lam128")
        Xpp = alloc([128, 2, 32], F32, "Xpp")
        Xown = alloc([128, 16, 32], F32, "Xown")
        ohot = alloc([128, 8], F32, "ohot")
        Oacc = alloc([128, 16, 4, 128], F32, "Oacc")
        m_run = alloc([128, 64], F32, "m_run"); l_run = alloc([128, 64], F32, "l_run")
        Osam = alloc([128, 512], F32, "Osam")
        ssmx = {}
        mark_after_mixer_state = bump[0]
        QTn = alloc([128, NOWN, 4, 128], BF16, "QTn"); QTr = alloc([128, NOWN, 4, 128], BF16, "QTr")
        mark_after_q = bump[0]

        w_in_v = w_in_d.re("(kt p) n -> p kt n", p=128)
        ldw(W_in_u, w_in_v[:, :, 0:512]); ldw(W_in_kv, w_in_v[:, :, 896:1216])
        ld(ohot, onehot_d)

        S_t = alloc([128, 16, 128], F32, "S_t"); C_t = alloc([128, 16, 128], F32, "C_t"); R_t = alloc([128, 16, 128], F32, "R_t")
        for v_ in (S_t, C_t, R_t):
            v_.b = btab
        WBre = alloc([128, 16, 128], BF16, "WBre"); WBim = alloc([128, 16, 128], BF16, "WBim")

        p0 = bump[0]
        ar = alloc([128, 16]); ai = alloc([128, 16]); ldt = alloc([128, 16]); jj = alloc([128, 128])
        bre = alloc([128, 16, 16]); bim = alloc([128, 16, 16])
        bsm = Buf("ssm_small")
        for v_ in (ar, ai, ldt, jj, bre, bim):
            v_.b = bsm
        ld(ar, ar_d); ld(ai, ai_d); ld(ldt, ldt_d); ld(jj, jj_d)
        ld(bre.re("p a b -> p (a b)"), bre_d); ld(bim.re("p a b -> p (a b)"), bim_d)
        dt_ = alloc([128, 16]); th = alloc([128, 16]); rr = alloc([128, 16])
        sm = [alloc([128, 16]) for _ in range(8)]
        for v_ in [dt_, th, rr] + sm:
            v_.b = bsm
        act(dt_, ldt, AF.Exp)
        tt(th, ai, dt_, ALU.mult)
        tt(rr, ar, dt_, ALU.mult)
        act(rr, rr, AF.Exp)
        A_t = alloc([128, 16, 128]); T1 = alloc([128, 2048]); TI = alloc([128, 2048], I32)
        for v_ in (A_t, T1, TI):
            v_.b = btab
        jjb = jj.un(1).bc([128, 16, 128])
        tt(A_t, jjb, th.un(2).bc([128, 16, 128]), ALU.mult)
        tt(R_t, jjb, rr.un(2).bc([128, 16, 128]), ALU.max)
        tt(R_t, R_t, rr.un(2).bc([128, 16, 128]), ALU.min)
        Af = A_t.re("p a b -> p (a b)")
        TIf = TI.cast(F32)

        def sin_of(out, shift):
            ts(T1, Af, shift, 1.0 / TWO_PI, ALU.add, ALU.mult)
            vcopy(TI, T1)
            vcopy(T1, TI)
            stt(T1, T1, -TWO_PI, Af, ALU.mult, ALU.add)
            if shift != 0.0:
                ts(T1, T1, shift, None, ALU.add)
            ts(TIf, T1, math.pi, -TWO_PI, ALU.is_gt, ALU.mult)
            tt(T1, T1, TIf, ALU.add)
            ts(TIf, T1, -math.pi, TWO_PI, ALU.is_lt, ALU.mult)
            tt(T1, T1, TIf, ALU.add)
            ts(T1, T1, math.pi, -math.pi, ALU.min, ALU.max)
            act(out, T1, AF.Sin)

        sin_of(S_t.re("p a b -> p (a b)"), 0.0)
        sin_of(C_t.re("p a b -> p (a b)"), math.pi / 2)
        lbr, lbi, den, fre, fim, t0_, t1_, t2_ = sm
        cos1 = C_t[:, :, 0]; sin1 = S_t[:, :, 0]
        tt(lbr, rr, cos1, ALU.mult)
        ts(lbr, lbr, -1.0, None, ALU.add)
        tt(lbi, rr, sin1, ALU.mult)
        tt(den, ar, ar, ALU.mult)
        tt(t0_, ai, ai, ALU.mult)
        tt(den, den, t0_, ALU.add)
        recip(den, den)
        tt(t0_, lbr, ar, ALU.mult); tt(t1_, lbi, ai, ALU.mult); tt(fre, t0_, t1_, ALU.add); tt(fre, fre, den, ALU.mult)
        tt(t0_, lbi, ar, ALU.mult); tt(t1_, lbr, ai, ALU.mult); tt(fim, t0_, t1_, ALU.subtract); tt(fim, fim, den, ALU.mult)
        Mre = alloc([128, 16, 128]); Mim = alloc([128, 16, 128]); tb = alloc([128, 16, 16]); tb2 = alloc([128, 16, 16])
        Mreb = alloc([128, 16, 128], BF16); Mimb = alloc([128, 16, 128], BF16)
        bM = Buf("M")
        for v_ in (Mre, Mim, tb, tb2, Mreb, Mimb):
            v_.b = bM
        freb = fre.un(2).bc([128, 16, 16]); fimb = fim.un(2).bc([128, 16, 16])
        mset(Mre, 0.0); mset(Mim, 0.0)
        tt(tb, bre, freb, ALU.mult); tt(tb2, bim, fimb, ALU.mult)
        for lo, col in ((0, 0), (64, 16)):
            for j4 in range(4):
                tt(Mre[lo:lo + 64, j4::4, 32 * j4 + col:32 * j4 + col + 16], tb[lo:lo + 64, j4::4, :], tb2[lo:lo + 64, j4::4, :], ALU.subtract)
        tt(tb, bim, freb, ALU.mult); tt(tb2, bre, fimb, ALU.mult)
        for lo, col in ((0, 0), (64, 16)):
            for j4 in range(4):
                tt(Mim[lo:lo + 64, j4::4, 32 * j4 + col:32 * j4 + col + 16], tb[lo:lo + 64, j4::4, :], tb2[lo:lo + 64, j4::4, :], ALU.add)
        vcopy(Mreb, Mre); vcopy(Mimb, Mim)
        for M_, WB_ in ((Mreb, WBre), (Mimb, WBim)):
            for hf in range(2):
                bb = banks[0].cast(BF16)
                for q in range(8):
                    tr(bb[:, q * 128:(q + 1) * 128], M_[:, 8 * hf + q, :], identb)
                vcopy(WB_[:, 8 * hf:8 * hf + 8, :].re("p a b -> p (a b)"), bb)

        mset(BRb, 0.0); mset(BIb, 0.0)
        tt(tb, bre, freb, ALU.mult); tt(tb2, bim, fimb, ALU.mult)
        for lo, col in ((0, 0), (64, 16)):
            tt(BRb[lo:lo + 64, :, col:col + 16], tb[lo:lo + 64], tb2[lo:lo + 64], ALU.subtract)
        tt(tb, bim, freb, ALU.mult); tt(tb2, bre, fimb, ALU.mult)
        for lo, col in ((0, 0), (64, 16)):
            tt(BIb[lo:lo + 64, :, col:col + 16], tb[lo:lo + 64], tb2[lo:lo + 64], ALU.add)
        lnr = alloc([128, 16]); lnr.b = bsm
        tt(lnr, ar, dt_, ALU.mult)
        act(t2_, lnr, AF.Exp, scale=128.0)
        tt(lam128[:, 0, :], t2_, C_t[:, :, 127], ALU.mult)
        tt(lam128[:, 1, :], t2_, S_t[:, :, 127], ALU.mult)
        jj2 = alloc([128, 128]); jj2.b = bsm
        ld(jj2, jj2_d)
        C2 = Mre; S2 = Mim; Mag = T1.re("p (a b) -> p a b", a=16)
        jj2b = jj2.un(1).bc([128, 16, 128])
        tt(A_t, jj2b, th.un(2).bc([128, 16, 128]), ALU.mult)
        sin_of(S2.re("p a b -> p (a b)"), 0.0)
        sin_of(C2.re("p a b -> p (a b)"), math.pi / 2)
        tt(Mag, jj2b, lnr.un(2).bc([128, 16, 128]), ALU.mult)
        act(Mag, Mag, AF.Exp)
        Lb = [Mreb, Mimb]
        tt(Lb[0], Mag, C2, ALU.mult); tt(Lb[1], Mag, S2, ALU.mult)
        for M_, LT_ in ((Lb[0], LrT), (Lb[1], LiT)):
            for hf in range(2):
                bb = banks[0].cast(BF16)
                for q in range(8):
                    tr(bb[:, q * 128:(q + 1) * 128], M_[:, 8 * hf + q, :], identb)
                vcopy(LT_[:, 8 * hf:8 * hf + 8, :].re("p a b -> p (a b)"), bb)

        for nm_, v_ in (("S_t", S_t), ("C_t", C_t), ("R_t", R_t)):
            ld(tab_d[nm_], v_.re("p a b -> p (a b)"))
        ld(tab_d["WBre"], WBre.re("p a b -> p (a b)")); ld(tab_d["WBim"], WBim.re("p a b -> p (a b)"))

        barrier()
        bump[0] = mark_after_q
        W_xk = alloc([128, 8, D], BF16, "W_xk"); W_xv = alloc([128, 8, D], BF16, "W_xv")
        ldw(W_xk, w_xk_d.re("(kt p) n -> p kt n", p=128)); ldw(W_xv, w_xv_d.re("(kt p) n -> p kt n", p=128))
        xt0 = alloc([128, D], F32, "xt0"); memT = alloc([128, 8, 256], BF16, "memT"); xT0 = alloc([128, 8, 128], BF16, "xT0")
        mkf = alloc([128, D], F32, "mkf")
        for mt in range(2):
            load_xT(mem[mt * 128:(mt + 1) * 128, :], xt0, xT0, banks[0])
            vcopy(memT[:, :, mt * 128:(mt + 1) * 128], xT0, eng="pool")
            for W_, o_d, keep in ((W_xk, memk_o, False), (W_xv, memv_o, True)):
                for nb in range(2):
                    for k in range(8):
                        mm(banks[1 + nb], xT0[:, k, :], W_[:, k, nb * 512:(nb + 1) * 512], start=(k == 0), stop=(k == 7))
                    acopy(mkf[:, nb * 512:(nb + 1) * 512], banks[1 + nb])
                ld(o_d[mt * 128:(mt + 1) * 128, :], mkf)
                if keep:
                    vcopy(mvb[:, mt, :], mkf)
        for h in range(4):
            for et in range(2):
                c0 = h * 256 + et * 128
                for k in range(8):
                    mm(banks[3][:, 0:256], W_xk[:, k, c0:c0 + 128], memT[:, k, :], start=(k == 0), stop=(k == 7))
                acopy(mkT[:, h, et, :], banks[3][:, 0:256])


        W_q = alloc([128, 3, 768], BF16, "W_q")
        ldw(W_q, w_q_d.re("(kt p) n -> p kt n", p=128))
        W_in_q = alloc([128, 8, 384], BF16, "W_in_q")
        ldw(W_in_q, w_in_v[:, :, 512:896])
        gq = alloc([128, 384], F32, "gq"); ld(gq, gq_d)
        NB = 2
        xt = [alloc([128, D], F32, "xt%d" % i) for i in range(NB)]
        xT = [alloc([128, 8, 128], BF16, "xT%d" % i) for i in range(NB)]
        rc = [alloc([128, 64], F32, "rc%d" % i) for i in range(NB)]
        rs = [alloc([128, 64], F32, "rs%d" % i) for i in range(NB)]
        sqp = [alloc([128, 384], F32, "sq%d" % i) for i in range(NB)]; ss1 = [alloc([128, 1], F32) for _ in range(NB)]
        cqn = [alloc([128, 384], BF16) for _ in range(NB)]
        cqT = [alloc([128, 3, 128], BF16) for _ in range(NB)]
        qf = [alloc([128, 4, 192], F32) for _ in range(NB)]
        qr = [alloc([128, 4, 64], F32) for _ in range(NB)]
        qtmp = [alloc([128, 4, 64], F32) for _ in range(NB)]
        qb = [alloc([128, 4, 192], BF16) for _ in range(NB)]
        qrd = [alloc([128, 4, 128], BF16) for _ in range(NB)]

        def q_stages(i, p):
            Xb = banks[p]; Jb = banks[2 + p]; Qb = dbl[2 + p]

            def sA():
                load_xT(xo[i * 128:(i + 1) * 128, :], xt[p], xT[p], Xb)
                ld(rc[p], ropec_o[i * 128:(i + 1) * 128, :]); ld(rs[p], ropes_o[i * 128:(i + 1) * 128, :])

            def sB():
                for k in range(8):
                    mm(Jb[:, 0:384], xT[p][:, k, :], W_in_q[:, k, :], start=(k == 0), stop=(k == 7))

            def sC():
                rmsnorm(cqn[p], Jb[:, 0:384], 384, gq, sqp[p], ss1[p])

            def sD():
                transpose_to(cqT[p], cqn[p], 3, Xb, eng="act")

            def sE():
                for nb, (c0, c1) in enumerate(((0, 512), (512, 768))):
                    for k in range(3):
                        mm(Qb[:, nb * 512:nb * 512 + (c1 - c0)], cqT[p][:, k, :], W_q[:, k, c0:c1], start=(k == 0), stop=(k == 2))

            def sF():
                acopy(qf[p].re("p a b -> p (a b)"), Qb[:, 0:768])

            def sG():
                rope(qr[p], qf[p][:, :, 128:192], rc[p], rs[p], qtmp[p], 4)

            def sH():
                P.op("act", lambda e: e.mul(out=qb[p].ap[:, :, 0:128], in_=qf[p].ap[:, :, 0:128], mul=MLA_SCALE), reads=[qf[p].b], writes=[qb[p].b])
                P.op("act", lambda e: e.mul(out=qrd[p].ap[:, :, 0:64], in_=qr[p].ap, mul=MLA_SCALE), reads=[qr[p].b], writes=[qrd[p].b])
                P.op("act", lambda e: e.mul(out=qrd[p].ap[:, :, 64:128], in_=qr[p].ap, mul=MLA_SCALE), reads=[qr[p].b], writes=[qrd[p].b])

            def sI():
                bb = Xb.cast(BF16)
                for h in range(4):
                    tr(bb[:, h * 128:(h + 1) * 128], qb[p][:, h, 0:128], identb)
                    tr(bb[:, 512 + h * 128:512 + (h + 1) * 128], qrd[p][:, h, :], identb)

            def sJ():
                bb = Xb.cast(BF16)
                vcopy(QTn[:, i, :, :].re("p a b -> p (a b)"), bb[:, 0:512])
                vcopy(QTr[:, i, :, :].re("p a b -> p (a b)"), bb[:, 512:1024])

            return [sA, sB, sC, sD, sE, sF, sG, sH, sI, sJ]

        for ip in range(9):
            sa = q_stages(2 * ip, 0)
            sb_ = q_stages(2 * ip + 1, 1) if 2 * ip + 1 < NOWN else []
            for k_ in range(len(sa)):
                sa[k_]()
                if k_ < len(sb_):
                    sb_[k_]()

        barrier()
        bump[0] = mark_after_q

        def ssm_rounds(uT_, pbs, init_fn, nseg, ybank=None, last_out=None):
            L = 128 // nseg
            S_t = ssmx["S_t"]; C_t = ssmx["C_t"]; R_t = ssmx["R_t"]; WBre = ssmx["WBre"]; WBim = ssmx["WBim"]
            Cre = ssmx["Cre"]; Cimn = ssmx["Cimn"]; Dblk = ssmx["Dblk"]
            xs4 = ssmx["xs4"]; zl_all = ssmx["zl_all"]
            def do_round(r):
                m4 = ssmx["m4"][r % 2]; wri = ssmx["wri"][r % 2]; zri = ssmx["zri"][r % 2]
                pb = pbs[r % 2]
                for j in range(2):
                    gp = 2 * r + j
                    mm(pb[:, j * 128:(j + 1) * 128], WBre[:, gp, :], uT_[:, gp // 4, :])
                    mm(pb[:, 256 + j * 128:256 + (j + 1) * 128], WBim[:, gp, :], uT_[:, gp // 4, :])
                if nseg == 1:
                    Cq = C_t[:, 2 * r:2 * r + 2, :].re("p a b -> p (a b)"); Sq = S_t[:, 2 * r:2 * r + 2, :].re("p a b -> p (a b)")
                    pre = pb[:, 0:256]; pim = pb[:, 256:512]
                    mk = lambda v_: v_
                else:
                    Cq = C_t[:, 2 * r:2 * r + 2, 0:L].un(2).bc([128, 2, nseg, L]); Sq = S_t[:, 2 * r:2 * r + 2, 0:L].un(2).bc([128, 2, nseg, L])
                    pre = pb[:, 0:256].re("p (a s l) -> p a s l", a=2, s=nseg); pim = pb[:, 256:512].re("p (a s l) -> p a s l", a=2, s=nseg)
                    mk = lambda v_: v_.re("p (a s l) -> p a s l", a=2, s=nseg)
                tt(mk(m4[0]), pre, Cq, ALU.mult); tt(mk(m4[1]), pim, Sq, ALU.mult)
                tt(mk(m4[2]), pim, Cq, ALU.mult); tt(mk(m4[3]), pre, Sq, ALU.mult)
                tt(wri[0], m4[0], m4[1], ALU.add, eng="pool")
                tt(wri[1], m4[2], m4[3], ALU.subtract, eng="pool")

            def do_back(r):
                m4 = ssmx["m4"][r % 2]; wri = ssmx["wri"][r % 2]; zri = ssmx["zri"][r % 2]
                if nseg == 1:
                    Cq = C_t[:, 2 * r:2 * r + 2, :].re("p a b -> p (a b)"); Sq = S_t[:, 2 * r:2 * r + 2, :].re("p a b -> p (a b)")
                else:
                    Cq = C_t[:, 2 * r:2 * r + 2, 0:L].un(2).bc([128, 2, nseg, L]); Sq = S_t[:, 2 * r:2 * r + 2, 0:L].un(2).bc([128, 2, nseg, L])
                for j in range(2):
                    gp = 2 * r + j
                    for s_ in range(nseg):
                        for part in range(2):
                            scan(zri[part][:, j, s_ * L:(s_ + 1) * L], R_t[:, gp, 0:L], wri[part][:, j * 128 + s_ * L:j * 128 + (s_ + 1) * L], init_fn(gp, s_, part))
                if last_out is not None:
                    for part in range(2):
                        src = zri[part].re("p a (s l) -> p a s l", s=nseg)[:, :, :, L - 1]
                        vcopy(zl_all[part][:, 0:nseg, 2 * r:2 * r + 2].re("p s a -> p a s"), src, eng="pool")
                if ybank is not None:
                    dmd = ssmx["dmd"][r % 2]; xrb = ssmx["xrb"]; xib = ssmx["xib"]
                    if nseg == 1:
                        Cd, Sd = Cq, Sq
                        zr_ = zri[0].re("p a b -> p (a b)"); zi_ = zri[1].re("p a b -> p (a b)")
                        dk = lambda v_: v_
                    else:
                        Cd, Sd = Cq, Sq
                        zr_ = zri[0].re("p a (s l) -> p a s l", s=nseg); zi_ = zri[1].re("p a (s l) -> p a s l", s=nseg)
                        dk = lambda v_: v_.re("p (a s l) -> p a s l", a=2, s=nseg)
                    tt(dk(dmd[0]), zr_, Cd, ALU.mult); tt(dk(dmd[1]), zi_, Sd, ALU.mult, eng="pool")
                    tt(dk(dmd[2]), zr_, Sd, ALU.mult); tt(dk(dmd[3]), zi_, Cd, ALU.mult, eng="pool")
                    tt(xrb[r % 2].re("p a b -> p (a b)"), dmd[0], dmd[1], ALU.subtract, eng="pool")
                    tt(xib[r % 2].re("p a b -> p (a b)"), dmd[2], dmd[3], ALU.add, eng="pool")

            def do_cproj(r):
                if ybank is None:
                    return
                xrb = ssmx["xrb"]; xib = ssmx["xib"]
                for j in range(2):
                    gp = 2 * r + j
                    o_ = ybank[:, 32 * gp:32 * gp + 32]
                    mm(o_, xrb[r % 2][:, j, :], Cre[:, gp, :], start=True, stop=False)
                    mm(o_, xib[r % 2][:, j, :], Cimn[:, gp, :], start=False, stop=False)
                    mm(o_, uT_[:, gp // 4, :], Dblk[:, gp, :], start=False, stop=True)
            def do_tail():
                do_cproj(7)
                if last_out is None:
                    return
                if True:
                    cL = C_t[:, :, L - 1]; sL = S_t[:, :, L - 1]
                    for s_ in range(nseg):
                        o_re, o_im = last_out(s_)
                        zr_l = zl_all[0][:, s_, :]; zi_l = zl_all[1][:, s_, :]
                        tt(xs4[0], zr_l, cL, ALU.mult); tt(xs4[1], zi_l, sL, ALU.mult)
                        tt(xs4[2], zr_l, sL, ALU.mult); tt(xs4[3], zi_l, cL, ALU.mult)
                        tt(o_re, xs4[0], xs4[1], ALU.subtract)
                        tt(o_im, xs4[2], xs4[3], ALU.add)


            def step(k_):
                def f():
                    if k_ == 0:
                        do_round(0)
                    if k_ + 1 < 8:
                        do_round(k_ + 1)
                    do_back(k_)
                    if k_ >= 1:
                        do_cproj(k_ - 1)
                return f
            return [step(k_) for k_ in range(8)] + [do_tail]

        def ssm_tile(uT_, pbs, init_fn, nseg, ybank=None, last_out=None):
            for f_ in ssm_rounds(uT_, pbs, init_fn, nseg, ybank, last_out):
                f_()

        W_kv = alloc([128, 2, 4, 256], BF16, "W_kv")
        ldw(W_kv.re("p k h e -> p k (h e)"), w_kv_d.re("(kt p) n -> p kt n", p=128))
        gkv = alloc([128, 256], F32, "gkv"); ld(gkv, gkv_d)
        maskd = alloc([128, 1024], F32, "maskd"); ld(maskd, maskd_d)
        xt = [alloc([128, D], F32, "hxt%d" % i) for i in range(NB)]
        xT = [alloc([128, 8, 128], BF16, "hxT%d" % i) for i in range(NB)]
        utok = [alloc([128, 512], BF16, "hut%d" % i) for i in range(NB)]
        kvf = [alloc([128, 320], F32, "kvf%d" % i) for i in range(NB)]
        prs = [alloc([128, 16, 32], F32, "prs%d" % i) for i in range(NB)]
        pis = [alloc([128, 16, 32], F32, "pis%d" % i) for i in range(NB)]
        rc = [alloc([128, 64], F32) for _ in range(NB)]; rs = [alloc([128, 64], F32) for _ in range(NB)]
        ckvf = [alloc([128, 256], F32) for _ in range(NB)]; kpef = [alloc([128, 64], F32) for _ in range(NB)]
        tmp64 = [alloc([128, 64], F32) for _ in range(NB)]
        sq = alloc([128, 256], F32, "hsq"); ss1 = [alloc([128, 1], F32) for _ in range(NB)]
        ckvb = [alloc([128, 256], BF16) for _ in range(NB)]; kpeb = [alloc([128, 128], BF16) for _ in range(NB)]
        ckvT = [alloc([128, 2, 128], BF16) for _ in range(NB)]
        KT = [alloc([128, 4, 1024], BF16, "KT%d" % i) for i in range(2)]
        KR = [alloc([128, 1024], BF16, "KR%d" % i) for i in range(2)]
        Vb = [alloc([128, 8, 512], BF16, "Vb%d" % i) for i in range(2)]
        Pb = [alloc([128, 1024], BF16, "Pb%d" % i) for i in range(2)]
        PT = [alloc([128, 8, 128], BF16, "PT%d" % i) for i in range(2)]
        NS4 = 4
        mx = [alloc([128, 1], F32) for _ in range(NS4)]; mnew = [alloc([128, 1], F32) for _ in range(NS4)]
        negm = [alloc([128, 1], F32) for _ in range(NS4)]; alp = [alloc([128, 1], F32) for _ in range(NS4)]
        rsum = [alloc([128, 1], F32) for _ in range(NS4)]
        dm_ = [alloc([128, 1], F32) for _ in range(NS4)]
        hm1 = [alloc([128, 16, 32], F32, "hm1_%d" % i) for i in range(NB)]; hm2 = [alloc([128, 16, 32], F32, "hm2_%d" % i) for i in range(NB)]
        sre = [alloc([128, 32], F32) for _ in range(2)]
        xs8 = [alloc([128, 16], F32) for _ in range(4)]

        mset(Xpp, 0.0); mset(Xown, 0.0)
        mset(m_run, -NEG); mset(l_run, 0.0); mset(Oacc, 0.0)
        for kp in range(2):
            mset(kpeb[kp], 0.0)

        def hist_stages(t, kbuf, j, kb):
            b = t % NB
            bA = banks[0]; bB = banks[1]
            xc = Xpp[:, t % 2, :]; xn = Xpp[:, (t + 1) % 2, :]

            def sA():
                ld(xt[b], xp[t * 128:(t + 1) * 128, :])
                ld(rc[b], ropec_p[t * 128:(t + 1) * 128, :]); ld(rs[b], ropes_p[t * 128:(t + 1) * 128, :])

            def sB():
                for k in range(4):
                    tr(bA[:, k * 128:(k + 1) * 128], xt[b][:, k * 128:(k + 1) * 128], ident)
                for k in range(4):
                    tr(bB[:, k * 128:(k + 1) * 128], xt[b][:, (4 + k) * 128:(5 + k) * 128], ident)

            def sC():
                acopy(xT[b][:, 0:4, :], bA.re("p (k n) -> p k n", k=4))
                acopy(xT[b][:, 4:8, :], bB.re("p (k n) -> p k n", k=4))

            def sD():
                for k in range(8):
                    mm(bA, xT[b][:, k, :], W_in_u[:, k, :], start=(k == 0), stop=(k == 7))
                for k in range(8):
                    mm(bB[:, 0:320], xT[b][:, k, :], W_in_kv[:, k, :], start=(k == 0), stop=(k == 7))

            def sE():
                acopy(utok[b], bA)
                acopy(kvf[b], bB[:, 0:320])

            def sF():
                act(sq, kvf[b][:, 0:256], AF.Square, accum=ss1[b])
                act(ss1[b], ss1[b], AF.Ln, scale=1.0 / 256, bias=epsc)
                act(ss1[b], ss1[b], AF.Exp, scale=-0.5)
                for gp in range(16):
                    mm(bA[:, 32 * gp:32 * gp + 32], LrT[:, gp, :], utok[b][:, 32 * gp:32 * gp + 32])
                for gp in range(16):
                    mm(bB[:, 32 * gp:32 * gp + 32], LiT[:, gp, :], utok[b][:, 32 * gp:32 * gp + 32])

            late = kb >= 4

            def sG():
                if late:
                    pr3 = bA.re("p (a b) -> p a b", a=16); pi3 = bB.re("p (a b) -> p a b", a=16)
                    tt(hm1[b], pr3, BRb, ALU.mult); tt(hm2[b], pi3, BIb, ALU.mult)
                    tt(prs[b], pi3, BRb, ALU.mult); tt(pis[b], pr3, BIb, ALU.mult)
                else:
                    acopy(prs[b].re("p a b -> p (a b)"), bA)
                    acopy(pis[b].re("p a b -> p (a b)"), bB)
                tt(ckvf[b], kvf[b][:, 0:256], gkv, ALU.mult, eng="pool")
                ts(ckvf[b], ckvf[b], ss1[b], 1.0, ALU.mult, ALU.mult, eng="pool")
                rope(kpef[b].re("p (a b) -> p a b", a=1), kvf[b][:, 256:320].re("p (a b) -> p a b", a=1), rc[b], rs[b],
                     tmp64[b].re("p (a b) -> p a b", a=1), 1, eng="pool")
                vcopy(ckvb[b], ckvf[b], eng="pool")
                vcopy(kpeb[b][:, 0:64], kpef[b], eng="pool")
                vcopy(kpeb[b][:, 64:128], kpef[b], eng="pool")

            def sH():
                bb = bA.cast(BF16)
                for kc in range(2):
                    tr(bb[:, kc * 128:(kc + 1) * 128], ckvb[b][:, kc * 128:(kc + 1) * 128], identb)
                tr(bb[:, 256:384], kpeb[b], identb)
                if late:
                    tt(hm1[b], hm1[b], hm2[b], ALU.subtract, eng="pool")
                    tt(hm2[b], prs[b], pis[b], ALU.add, eng="pool")
                else:
                    tt(hm1[b], prs[b], BRb, ALU.mult, eng="pool"); tt(hm2[b], pis[b], BIb, ALU.mult, eng="pool")
                    tt(hm1[b], hm1[b], hm2[b], ALU.subtract, eng="pool")
                    tt(hm2[b], pis[b], BRb, ALU.mult, eng="pool"); tt(prs[b], prs[b], BIb, ALU.mult, eng="pool")
                    tt(hm2[b], hm2[b], prs[b], ALU.add, eng="pool")

            def sI():
                bb = bA.cast(BF16)
                acopy(ckvT[b].re("p a b -> p (a b)"), bb[:, 0:256])
                acopy(KR[kbuf][:, j * 128:(j + 1) * 128], bb[:, 256:384])

            def sJ():
                for h in range(4):
                    for kc in range(2):
                        mm(bB[:, h * 128:(h + 1) * 128], W_kv[:, kc, h, 0:128], ckvT[b][:, kc, :], start=(kc == 0), stop=(kc == 1))
                for kc in range(2):
                    mm(bA, ckvT[b][:, kc, :], W_kv[:, kc, :, 128:256], start=(kc == 0), stop=(kc == 1))
                lr_ = lam128[:, 0, :]; li_ = lam128[:, 1, :]
                ld(ckv_o[t * 128:(t + 1) * 128, :], ckvf[b])
                ld(kpe_o[t * 128:(t + 1) * 128, :], kpef[b])
                red(sre[t % 2][:, 0:16], hm1[b], ALU.add)
                red(sre[t % 2][:, 16:32], hm2[b], ALU.add)
                stt(Xown[:, kb, :], xc, ohot[:, j:j + 1], Xown[:, kb, :], ALU.mult, ALU.add)
                tt(xs8[0], xc[:, 0:16], lr_, ALU.mult, eng="pool"); tt(xs8[1], xc[:, 16:32], li_, ALU.mult, eng="pool")
                tt(xs8[2], xc[:, 16:32], lr_, ALU.mult, eng="pool"); tt(xs8[3], xc[:, 0:16], li_, ALU.mult, eng="pool")
                tt(xs8[0], xs8[0], xs8[1], ALU.subtract, eng="pool"); tt(xs8[2], xs8[2], xs8[3], ALU.add, eng="pool")
                tt(xn[:, 0:16], xs8[0], sre[t % 2][:, 0:16], ALU.add, eng="pool")
                tt(xn[:, 16:32], xs8[2], sre[t % 2][:, 16:32], ALU.add, eng="pool")

            def sK():
                acopy(KT[kbuf][:, :, j * 128:(j + 1) * 128], bB.re("p (h n) -> p h n", h=4))
                acopy(Vb[kbuf][:, j, :], bA)

            def pair(p_, c_):
                def f():
                    p_(); c_()
                return f
            return [sA, pair(sB, sC), pair(sD, sE), pair(sF, sG), pair(sH, sI), pair(sJ, sK)]

        def hist_items(kb):
            items = []
            for jp in range(4):
                sa = hist_stages(kb * 8 + 2 * jp, kb % 2, 2 * jp, kb)
                sb_ = hist_stages(kb * 8 + 2 * jp + 1, kb % 2, 2 * jp + 1, kb)
                for a_, b_ in zip(sa, sb_):
                    items.append(a_); items.append(b_)
            return items

        SCp = [dbl[3], dbl[2]]
        ptb = [banks[2], banks[3]]

        def att_A(n, i, h, kbuf):
            sc = SCp[n % 2]
            for c2 in range(2):
                mm(sc[:, c2 * 512:(c2 + 1) * 512], QTn[:, i, h, :], KT[kbuf][:, h, c2 * 512:(c2 + 1) * 512], start=True, stop=False)
            for c2 in range(2):
                lo = 64 * c2
                mm(sc[:, c2 * 512:(c2 + 1) * 512], QTr[lo:lo + 64, i, h, :], KR[kbuf][lo:lo + 64, c2 * 512:(c2 + 1) * 512], start=False, stop=True)

        def att_B(n, i, h, kbuf, diag):
            u2 = n % 2; u4 = n % NS4
            sc = SCp[u2]
            col = i * 4 + h
            if diag:
                tt(sc, sc, maskd, ALU.add)
            P.op("dve", lambda e: e.tensor_reduce(out=mx[u4].ap, in_=sc.ap, axis=AX.X, op=ALU.max, negate=True), reads=_flat([sc.b]), writes=[mx[u4].b])
            tt(negm[u4], mx[u4], m_run[:, col:col + 1], ALU.min)
            tt(dm_[u4], negm[u4], m_run[:, col:col + 1], ALU.subtract)
            vcopy(m_run[:, col:col + 1], negm[u4])
            act(alp[u4], dm_[u4], AF.Exp)
            act(Pb[u2], sc, AF.Exp, bias=negm[u4], accum=rsum[u4])

        def att_CD(n, i, h, kbuf):
            u2 = n % 2
            pt_b = ptb[u2].cast(BF16)
            for kt in range(8):
                tr(pt_b[:, kt * 128:(kt + 1) * 128], Pb[u2][:, kt * 128:(kt + 1) * 128], identb)
            if n % 3 == 0:
                vcopy(PT[u2].re("p a b -> p (a b)"), pt_b)
            else:
                acopy(PT[u2].re("p a b -> p (a b)"), pt_b)

        def att_EF(n, i, h, kbuf):
            u2 = n % 2; u4 = n % NS4
            ov = ptb[u2][:, 0:128]
            for kt in range(8):
                mm(ov, PT[u2][:, kt, :], Vb[kbuf][:, kt, h * 128:(h + 1) * 128], start=(kt == 0), stop=(kt == 7))
            stt(Oacc[:, i, h, :], Oacc[:, i, h, :], alp[u4], ov, ALU.mult, ALU.add)
            col = i * 4 + h
            stt(l_run[:, col:col + 1], l_run[:, col:col + 1], alp[u4], rsum[u4], ALU.mult, ALU.add)

        for it in hist_items(0):
            it()
        nun = 0
        for kb in range(16):
            kbuf = kb % 2
            units = [(i, h) for i in range(kb, 16) for h in range(4)]
            U = len(units)
            items = hist_items(kb + 1) if kb + 1 < 16 else []
            per = (len(items) + U - 1) // U if items else 0
            ip = 0
            for q_ in range(U + 3):
                if q_ < U:
                    att_A(nun + q_, units[q_][0], units[q_][1], kbuf)
                if 0 <= q_ - 1 < U:
                    att_B(nun + q_ - 1, units[q_ - 1][0], units[q_ - 1][1], kbuf, units[q_ - 1][0] == kb)
                if 0 <= q_ - 2 < U:
                    att_CD(nun + q_ - 2, units[q_ - 2][0], units[q_ - 2][1], kbuf)
                if 0 <= q_ - 3 < U:
                    att_EF(nun + q_ - 3, units[q_ - 3][0], units[q_ - 3][1], kbuf)
                for _ in range(per):
                    if ip < len(items):
                        items[ip](); ip += 1
            while ip < len(items):
                items[ip](); ip += 1
            nun += U

        ld(ssmp_o, Xpp[:, NTP % 2, :])

        barrier()
        bump[0] = mark_after_q

        W_kv = alloc([128, 2, 4, 256], BF16, "W_kv2")
        ldw(W_kv.re("p k h e -> p k (h e)"), w_kv_d.re("(kt p) n -> p kt n", p=128))
        gkv = alloc([128, 256], F32, "gkv2"); ld(gkv, gkv_d)
        xt_s = alloc([128, D], F32, "sxt"); xT_s = alloc([128, 8, 128], BF16, "sxT")
        rc_s = alloc([128, 64], F32); rs_s = alloc([128, 64], F32)
        ckvf_s = alloc([128, 256], F32); kpef_s = alloc([128, 64], F32); tmp64_s = alloc([128, 64], F32)
        sq = alloc([128, 512], F32, "ssq"); ss_s = alloc([128, 1], F32)
        ckvb_s = alloc([128, 256], BF16); kpeb_s = alloc([128, 128], BF16)
        ckvTn = alloc([128, 2, 128], BF16, "ckvTn"); kpeTn = alloc([128, 128], BF16, "kpeTn")
        KTn = alloc([128, 4, 128], BF16, "KTn")
        cat = [alloc([128, 256], F32, "cat%d" % i) for i in range(2)]
        catb = [alloc([128, 256], BF16) for _ in range(2)]
        cpt = [alloc([128, 64], F32, "cpt%d" % i) for i in range(2)]
        cptb = [alloc([128, 128], BF16) for _ in range(2)]
        ckvTc = alloc([128, 2, 1024], BF16, "ckvTc"); kpeTc = alloc([128, 1024], BF16, "kpeTc")
        KTc = alloc([128, 4, 1024], BF16, "KTc"); Vc = alloc([128, 8, 512], BF16, "Vc"); Vn = alloc([128, 512], BF16, "Vn")
        scs = alloc([128, 1056], F32, "scs")
        Ps = alloc([128, 1152], BF16, "Ps"); PTs = alloc([128, 9, 32], BF16, "PTs")
        mxs = alloc([128, 1], F32); negs = alloc([128, 1], F32); sums = alloc([128, 1], F32)
        osb = alloc([128, 512], F32, "osb")
        isam = 16
        load_xT(xo[isam * 128:(isam + 1) * 128, :], xt_s, xT_s, banks[0])
        ld(rc_s, ropec_o[isam * 128:(isam + 1) * 128, :]); ld(rs_s, ropes_o[isam * 128:(isam + 1) * 128, :])
        for k in range(8):
            mm(banks[2][:, 0:320], xT_s[:, k, :], W_in_kv[:, k, :], start=(k == 0), stop=(k == 7))
        rmsnorm(ckvf_s, banks[2][:, 0:256], 256, gkv, sq[:, 0:256], ss_s)
        ld(ckvs_o, ckvf_s)
        rope(kpef_s.re("p (a b) -> p a b", a=1), banks[2][:, 256:320].re("p (a b) -> p a b", a=1), rc_s, rs_s, tmp64_s.re("p (a b) -> p a b", a=1), 1)
        ld(kpes_o, kpef_s)
        vcopy(ckvb_s, ckvf_s)
        mset(kpeb_s, 0.0)
        vcopy(kpeb_s[:, 0:64], kpef_s)
        bb = banks[3].cast(BF16)
        for kc in range(2):
            tr(bb[:, kc * 128:(kc + 1) * 128], ckvb_s[:, kc * 128:(kc + 1) * 128], identb)
        tr(bb[:, 256:384], kpeb_s, identb)
        acopy(ckvTn.re("p a b -> p (a b)"), bb[:, 0:256])
        acopy(kpeTn[0:64, :], bb[0:64, 256:384])
        for h in range(4):
            for kc in range(2):
                mm(banks[3][:, h * 128:(h + 1) * 128], W_kv[:, kc, h, 0:128], ckvTn[:, kc, :], start=(kc == 0), stop=(kc == 1))
        acopy(KTn.re("p a b -> p (a b)"), banks[3])
        for kp in range(2):
            mset(cptb[kp], 0.0)
        for s_ in range(4):
            for kt in range(8):
                b = kt % 2
                r0 = s_ * 1024 + kt * 128
                ld(cat[b], cckv_d[r0:r0 + 128, :]); ld(cpt[b], ckpe_d[r0:r0 + 128, :])
                vcopy(catb[b], cat[b], eng="pool"); vcopy(cptb[b][:, 0:64], cpt[b], eng="pool")
                bb = banks[1].cast(BF16)
                for kc in range(2):
                    tr(bb[:, kc * 128:(kc + 1) * 128], catb[b][:, kc * 128:(kc + 1) * 128], identb)
                tr(bb[:, 256:384], cptb[b], identb)
                acopy(ckvTc[:, :, kt * 128:(kt + 1) * 128], bb[:, 0:256].re("p (a b) -> p a b", a=2))
                acopy(kpeTc[0:64, kt * 128:(kt + 1) * 128], bb[0:64, 256:384])
            for h in range(4):
                for n in range(2):
                    for kc in range(2):
                        mm(banks[2], W_kv[:, kc, h, 0:128], ckvTc[:, kc, n * 512:(n + 1) * 512], start=(kc == 0), stop=(kc == 1))
                    acopy(KTc[:, h, n * 512:(n + 1) * 512], banks[2])
            for kt in range(8):
                for kc in range(2):
                    mm(banks[3], ckvTc[:, kc, kt * 128:(kt + 1) * 128], W_kv[:, kc, :, 128:256], start=(kc == 0), stop=(kc == 1))
                vcopy(Vc[:, kt, :], banks[3])
            for kc in range(2):
                mm(banks[3][0:32, :], ckvTn[:, kc, s_ * 32:(s_ + 1) * 32], W_kv[:, kc, :, 128:256], start=(kc == 0), stop=(kc == 1))
            vcopy(Vn[0:32, :], banks[3][0:32, :])
            qs = slice(s_ * 32, (s_ + 1) * 32)
            for h in range(4):
                for n in range(2):
                    mm(SC[0:32, n * 512:(n + 1) * 512], QTn[:, isam, h, qs], KTc[:, h, n * 512:(n + 1) * 512], start=True, stop=False)
                    mm(SC[0:32, n * 512:(n + 1) * 512], QTr[0:64, isam, h, qs], kpeTc[0:64, n * 512:(n + 1) * 512], start=False, stop=True)
                mm(banks[4][0:32, 0:32], QTn[:, isam, h, qs], KTn[:, h, qs], start=True, stop=False)
                mm(banks[4][0:32, 0:32], QTr[0:64, isam, h, qs], kpeTn[0:64, qs], start=False, stop=True)
                acopy(scs[0:32, 0:1024], SC[0:32, :])
                acopy(scs[0:32, 1024:1056], banks[4][0:32, 0:32])
                red(mxs[0:32, :], scs[0:32, :], ALU.max)
                ts(negs[0:32, :], mxs[0:32, :], -1.0, None, ALU.mult)
                mset(Ps[0:32, 1024:1152], 0.0)
                act(Ps[0:32, 0:1056], scs[0:32, :], AF.Exp, bias=negs[0:32, :], accum=sums[0:32, :])
                pt_b = banks[5].cast(BF16)
                for kt in range(9):
                    tr(pt_b[:, kt * 32:(kt + 1) * 32], Ps[0:32, kt * 128:(kt + 1) * 128], identb[0:32, 0:32])
                vcopy(PTs.re("p a b -> p (a b)"), pt_b[:, 0:288])
                ov = banks[4][0:32, 128:256]
                for kt in range(8):
                    mm(ov, PTs[:, kt, :], Vc[:, kt, h * 128:(h + 1) * 128], start=(kt == 0), stop=False)
                mm(ov, PTs[0:32, 8, :], Vn[0:32, h * 128:(h + 1) * 128], start=False, stop=True)
                recip(sums[0:32, :], sums[0:32, :])
                ts(osb[0:32, h * 128:(h + 1) * 128], ov, sums[0:32, :], None, ALU.mult)
            ld(Osam[s_ * 32:(s_ + 1) * 32, :], osb[0:32, :])

        barrier()
        bump[0] = mark_after_mixer_state

        W_glu = alloc([128, 4, D], BF16, "W_glu"); W_o = alloc([128, 8, D], BF16, "W_o")
        ldw(W_glu, w_glu_d.re("(kt p) n -> p kt n", p=128)); ldw(W_o, w_o_d.re("(kt p) n -> p kt n", p=128))
        gos = alloc([128, 512], F32, "gos"); gom = alloc([128, 512], F32, "gom"); ld(gos, gos_d); ld(gom, gom_d)
        lng = alloc([128, D], F32, "lng"); lnb = alloc([128, D], F32, "lnb")
        ld(lng, lng_d[:, 0:D]); ld(lnb, lnb_d[:, 0:D])
        btab2 = Buf("tab2")
        for nm_ in ("S_t", "C_t", "R_t"):
            v_ = alloc([128, 16, 128], F32, nm_ + "2"); v_.b = btab2
            ld(v_.re("p a b -> p (a b)"), tab_d[nm_]); ssmx[nm_] = v_
        for nm_ in ("WBre", "WBim"):
            v_ = alloc([128, 16, 128], BF16, nm_ + "2")
            ld(v_.re("p a b -> p (a b)"), tab_d[nm_]); ssmx[nm_] = v_
        for nm_, src_ in (("Cre", cre_d), ("Cimn", cimn_d), ("Dblk", dblk_d)):
            v_ = alloc([128, 16, 32], BF16, nm_)
            ldw(v_.re("p a b -> p (a b)"), src_); ssmx[nm_] = v_
        ssmx["m4"] = [[alloc([128, 256], F32) for _ in range(4)] for _ in range(2)]
        ssmx["wri"] = [[alloc([128, 256], F32) for _ in range(2)] for _ in range(2)]
        ssmx["zri"] = [[alloc([128, 2, 128], F32) for _ in range(2)] for _ in range(2)]
        ssmx["xs4"] = [alloc([128, 16], F32) for _ in range(4)]
        ssmx["zl_all"] = [alloc([128, 4, 16], F32) for _ in range(2)]
        ssmx["dmd"] = [[alloc([128, 256], F32) for _ in range(4)]] * 2
        ssmx["xrb"] = [alloc([128, 2, 128], BF16) for _ in range(2)]
        ssmx["xib"] = [alloc([128, 2, 128], BF16) for _ in range(2)]
        S0 = alloc([128, 2, 4, 16], F32, "S0"); ld(S0.re("p a b c -> p (a b c)"), s0_d)
        Sfin = alloc([128, 2, 4, 16], F32, "Sfin")
        xt = [alloc([128, D], F32, "axt%d" % i) for i in range(NB)]
        xT = [alloc([128, 8, 128], BF16, "axT%d" % i) for i in range(NB)]
        uT = [alloc([128, 4, 128], BF16, "auT%d" % i) for i in range(NB)]
        ysq = alloc([128, 512], F32, "ysq"); yt = alloc([128, 512], F32, "yt"); ysg = alloc([128, 512], F32, "ysg")
        glb = alloc([128, 512], BF16, "glb"); gT = alloc([128, 4, 128], BF16, "gT")
        sg2 = ysq; osf = yt
        mixb = alloc([128, D], BF16, "mixb"); mixT = alloc([128, 8, 128], BF16, "mixT")
        rl = alloc([128, 4], F32, "rl"); omf = alloc([128, 4, 128], F32, "omf")
        ss_a = alloc([128, 1], F32); sq = ysg
        rres = alloc([128, D], F32, "rres"); hout = alloc([128, D], F32, "hout")
        stats = alloc([128, 12], F32); mvv = alloc([128, 2], F32)

        def pre_stages(i):
            b = i % NB
            yb = banks[3] if i % 2 == 0 else banks[0]

            def p0():
                load_xT(xo[i * 128:(i + 1) * 128, :], xt[b], xT[b], banks[1])
                for kt in range(4):
                    for k in range(8):
                        mm(banks[1][:, kt * 128:(kt + 1) * 128], W_in_u[:, k, kt * 128:(kt + 1) * 128], xT[b][:, k, :], start=(k == 0), stop=(k == 7))
                acopy(uT[b].re("p a b -> p (a b)"), banks[1])

            if i < 16:
                rr_ = ssm_rounds(uT[b], (banks[4], banks[5]), lambda gp, s_, part, i=i: Xown[:, i, part * 16 + gp:part * 16 + gp + 1], 1, ybank=yb)
            else:
                rr_ = ssm_rounds(uT[b], (banks[4], banks[5]), lambda gp, s_, part: S0[:, part, s_, gp:gp + 1], 4, ybank=yb,
                                 last_out=lambda s_: (Sfin[:, 0, s_, :], Sfin[:, 1, s_, :]))
                rr_.append(lambda: ld(ssms_o, Sfin.re("p a b c -> p (a b c)")))
            return [p0] + rr_

        def post_stages(i):
            b = i % NB
            yb = banks[3] if i % 2 == 0 else banks[0]
            xt_ = xt[b]

            def q0():
                act(ysq, yb, AF.Square)
                ts(yt, ysq, 0.044715, 1.0, ALU.mult, ALU.add)
                tt(yt, yt, yb, ALU.mult)

            def q1():
                act(ysg, yt, AF.Sigmoid, scale=1.5957691216057308)
                tt(glb, ysg, yb, ALU.mult)

            def q2():
                transpose_to(gT, glb, 4, banks[2], eng="act")

            def q3():
                for nb in range(2):
                    for k in range(4):
                        mm(SC[:, nb * 512:(nb + 1) * 512], gT[:, k, :], W_glu[:, k, nb * 512:(nb + 1) * 512], start=(k == 0), stop=(k == 3))

            def q4():
                act(sg2, SC[:, 512:1024], AF.Sigmoid)
                tt(osf, sg2, SC[:, 0:512], ALU.mult)

            def q5():
                rmsnorm(mixb[:, 0:512], osf, 512, gos, sq, ss_a)

            def q6():
                if i < 16:
                    recip(rl, l_run[:, i * 4:(i + 1) * 4])
                    tt(omf, Oacc[:, i, :, :], rl.un(2).bc([128, 4, 128]), ALU.mult)
                    rmsnorm(mixb[:, 512:1024], omf.re("p a b -> p (a b)"), 512, gom, sq, ss_a)
                else:
                    rmsnorm(mixb[:, 512:1024], Osam, 512, gom, sq, ss_a)

            def q7():
                for half in range(2):
                    transpose_to(mixT[:, half * 4:(half + 1) * 4, :], mixb[:, half * 512:(half + 1) * 512], 4, banks[2], eng=("act" if half == 0 else "dve"))

            def q8():
                for nb in range(2):
                    for k in range(8):
                        mm(SC[:, nb * 512:(nb + 1) * 512], mixT[:, k, :], W_o[:, k, nb * 512:(nb + 1) * 512], start=(k == 0), stop=(k == 7))

            def q9():
                stt(rres, xt_, ALPHA, SC, ALU.mult, ALU.add)
                layernorm(hout, rres, lng, lnb, stats, mvv)
                ld(h1_d[i * 128:(i + 1) * 128, :], hout)

            return [q0, q1, q2, q3, q4, q5, q6, q7, q8, q9]

        prev_post = []
        for i in range(NOWN + 1):
            pre = pre_stages(i) if i < NOWN else []
            n_ = max(len(pre), len(prev_post))
            for k_ in range(n_):
                if k_ < len(pre):
                    pre[k_]()
                if k_ < len(prev_post):
                    prev_post[k_]()
            prev_post = post_stages(i) if i < NOWN else []

        barrier()
        bump[0] = mark_after_mem

        W_xq = alloc([128, 8, D], BF16, "W_xq"); W_xo = alloc([128, 8, D], BF16, "W_xo")
        W_xqc = [V(W_xq.ap[:, :, c * 512:(c + 1) * 512], Buf("W_xqc%d" % c)) for c in range(2)]
        for c in range(2):
            ldw(W_xqc[c], w_xq_d.re("(kt p) n -> p kt n", p=128)[:, :, c * 512:(c + 1) * 512])
        ldw(W_xo, w_xo_d.re("(kt p) n -> p kt n", p=128))
        lng = alloc([128, D], F32, "lng2"); lnb = alloc([128, D], F32, "lnb2")
        ld(lng, lng_d[:, D:2 * D]); ld(lnb, lnb_d[:, D:2 * D])
        hin = [alloc([128, D], F32, "bh%d" % i) for i in range(NB)]
        hb2 = [alloc([128, D], BF16, "bhb%d" % i) for i in range(2)]; hT2 = [alloc([128, 8, 128], BF16, "bhT%d" % i) for i in range(2)]
        qxb2 = [alloc([128, D], BF16, "qxb%d" % i) for i in range(2)]; qxT2 = [alloc([128, 8, 128], BF16, "qxT%d" % i) for i in range(2)]
        mx42 = [alloc([128, 4], F32) for _ in range(2)]; neg42 = [alloc([128, 4], F32) for _ in range(2)]; sum42 = [alloc([128, 4], F32) for _ in range(2)]
        Px2 = [alloc([128, 4, 256], BF16, "Px%d" % i) for i in range(2)]; PxT2 = [alloc([128, 8, 128], BF16, "PxT%d" % i) for i in range(2)]
        oxb2 = [alloc([128, D], BF16, "oxb%d" % i) for i in range(2)]; oxT2 = [alloc([128, 8, 128], BF16, "oxT%d" % i) for i in range(2)]
        rres2 = [alloc([128, D], F32, "brres%d" % i) for i in range(2)]; hout2 = [alloc([128, D], F32, "bhout%d" % i) for i in range(2)]
        stats2 = [alloc([128, 12], F32) for _ in range(2)]; mvv2 = [alloc([128, 2], F32) for _ in range(2)]
        hb_ = hb2[0]; hT = hT2[0]; qxb = qxb2[0]; qxT = qxT2[0]; mx4 = mx42[0]; neg4 = neg42[0]; sum4 = sum42[0]
        Px = Px2[0]; PxT = PxT2[0]; oxb = oxb2[0]; oxT = oxT2[0]; rres = rres2[0]; hout = hout2[0]; stats = stats2[0]; mvv = mvv2[0]
        cmk = [alloc([128, D], F32, "cmk%d" % i) for i in range(2)]
        cmkb = [alloc([128, D], BF16, "cmkb%d" % i) for i in range(1)]
        mkTs4 = [alloc([128, 4, 2, 256], BF16, "mkTs%d" % i) for i in range(4)]; mvs4 = [alloc([128, 2, D], BF16, "mvs%d" % i) for i in range(4)]
        scx = alloc([128, 8], F32, "scx"); oxs = alloc([128, D], BF16, "oxs")

        def xa_stages(i, p):
            A_ = banks[p]; C_ = dbl[1 + p]
            Ab = A_.cast(BF16)

            def tr8(src):
                for k in range(8):
                    tr(Ab[:, k * 128:(k + 1) * 128], src[:, k * 128:(k + 1) * 128], identb)

            def ev8(dst):
                acopy(dst[:, 0:4, :], Ab[:, 0:512].re("p (k n) -> p k n", k=4))
                vcopy(dst[:, 4:8, :], Ab[:, 512:1024].re("p (k n) -> p k n", k=4))

            def t0():
                ld(hin[p], h1_d[i * 128:(i + 1) * 128, :])
                vcopy(hb2[p], hin[p], eng="pool")

            def t3():
                for nb in range(2):
                    for k in range(8):
                        mm(C_[:, nb * 512:(nb + 1) * 512], hT2[p][:, k, :], W_xqc[nb][:, k, :], start=(k == 0), stop=(k == 7))

            def t4():
                for nb in range(2):
                    P.op("act", lambda e, nb=nb: e.mul(out=qxb2[p].ap[:, nb * 512:(nb + 1) * 512], in_=C_.ap[:, nb * 512:(nb + 1) * 512], mul=X_SCALE),
                         reads=[C_.b], writes=[qxb2[p].b])

            def t7():
                for h in range(4):
                    for et in range(2):
                        mm(C_[:, h * 256:(h + 1) * 256], qxT2[p][:, h * 2 + et, :], mkT[:, h, et, :], start=(et == 0), stop=(et == 1))

            def t8():
                red(mx42[p], C_.re("p (h m) -> p h m", h=4), ALU.max)
                ts(neg42[p], mx42[p], -1.0, None, ALU.mult)
                for h in range(4):
                    act(Px2[p][:, h, :], C_[:, h * 256:(h + 1) * 256], AF.Exp, bias=neg42[p][:, h:h + 1], accum=sum42[p][:, h:h + 1])

            def t11():
                for h in range(4):
                    for mt in range(2):
                        mm(C_[:, h * 256:(h + 1) * 256], PxT2[p][:, h * 2 + mt, :], mvb[:, mt, h * 256:(h + 1) * 256], start=(mt == 0), stop=(mt == 1))

            def t12():
                recip(sum42[p], sum42[p])
                tt(oxb2[p].re("p (h e) -> p h e", h=4), C_.re("p (h e) -> p h e", h=4), sum42[p].un(2).bc([128, 4, 256]), ALU.mult)

            def t15():
                for nb in range(2):
                    for k in range(8):
                        mm(C_[:, nb * 512:(nb + 1) * 512], oxT2[p][:, k, :], W_xo[:, k, nb * 512:(nb + 1) * 512], start=(k == 0), stop=(k == 7))

            def t16():
                stt(rres2[p], hin[p], ALPHA, C_, ALU.mult, ALU.add)
                layernorm(hout2[p], rres2[p], lng, lnb, stats2[p], mvv2[p])
                ld(h2_d[i * 128:(i + 1) * 128, :], hout2[p])

            return [t0, lambda: tr8(hb2[p]), lambda: ev8(hT2[p]), t3, t4, lambda: tr8(qxb2[p]), lambda: ev8(qxT2[p]), t7, t8,
                    lambda: tr8(Px2[p].re("p a b -> p (a b)")), lambda: ev8(PxT2[p]), t11, t12, lambda: tr8(oxb2[p]), lambda: ev8(oxT2[p]), t15, t16]

        def prep_items():
            items = []
            for s_ in range(4):
                for mt in range(2):
                    r0 = s_ * 256 + mt * 128

                    def l0(r0=r0):
                        ld(cmk[0], cmk_d[r0:r0 + 128, :]); ld(cmk[1], cmv_d[r0:r0 + 128, :])

                    def l1(s_=s_, mt=mt):
                        vcopy(cmkb[0], cmk[0], eng="pool")
                        vcopy(mvs4[s_][:, mt, :], cmk[1], eng="pool")

                    def l2(s_=s_, mt=mt):
                        for half in range(2):
                            bb = banks[6 + half].cast(BF16)
                            for k in range(4):
                                kk = half * 4 + k
                                tr(bb[:, k * 128:(k + 1) * 128], cmkb[0][:, kk * 128:(kk + 1) * 128], identb)

                    def l3(s_=s_, mt=mt):
                        for half in range(2):
                            bb = banks[6 + half].cast(BF16)
                            acopy(mkTs4[s_][:, half * 2:half * 2 + 2, :, mt * 128:(mt + 1) * 128].re("p h e m -> p (h e) m"), bb[:, 0:512].re("p (k m) -> p k m", k=4))

                    items += [l0, l1, l2, l3]
            return items

        pitems = prep_items()
        pi_ = 0
        slot = 0
        for ip in range(8):
            sa = xa_stages(2 * ip, 0); sb_ = xa_stages(2 * ip + 1, 1)
            for a_, b_ in zip(sa, sb_):
                a_(); b_()
                slot += 1
                if slot % 4 == 0 and pi_ < len(pitems):
                    pitems[pi_](); pi_ += 1
        while pi_ < len(pitems):
            pitems[pi_](); pi_ += 1

        for i in range(16, NOWN):
            b = i % NB
            ld(hin[b], h1_d[i * 128:(i + 1) * 128, :])
            vcopy(hb_, hin[b], eng="pool")
            for half in range(2):
                transpose_to(hT[:, half * 4:(half + 1) * 4, :], hb_[:, half * 512:(half + 1) * 512], 4, banks[0], eng=("act" if half == 0 else "dve"))
            for nb in range(2):
                for k in range(8):
                    mm(banks[1 + nb], hT[:, k, :], W_xqc[nb][:, k, :], start=(k == 0), stop=(k == 7))
                P.op("act", lambda e, nb=nb: e.mul(out=qxb.ap[:, nb * 512:(nb + 1) * 512], in_=banks[1 + nb].ap, mul=X_SCALE), reads=[banks[1 + nb].b], writes=[qxb.b])
            for half in range(2):
                transpose_to(qxT[:, half * 4:(half + 1) * 4, :], qxb[:, half * 512:(half + 1) * 512], 4, banks[3], eng=("act" if half == 0 else "dve"))
            if i < 16:
                for h in range(4):
                    for et in range(2):
                        mm(SC[:, h * 256:(h + 1) * 256], qxT[:, h * 2 + et, :], mkT[:, h, et, :], start=(et == 0), stop=(et == 1))
                red(mx4, SC.re("p (h m) -> p h m", h=4), ALU.max)
                ts(neg4, mx4, -1.0, None, ALU.mult)
                for h in range(4):
                    act(Px[:, h, :], SC[:, h * 256:(h + 1) * 256], AF.Exp, bias=neg4[:, h:h + 1], accum=sum4[:, h:h + 1])
                for half in range(2):
                    transpose_to(PxT[:, half * 4:(half + 1) * 4, :], Px.re("p a b -> p (a b)")[:, half * 512:(half + 1) * 512], 4, banks[4], eng=("act" if half == 0 else "dve"))
                for h in range(4):
                    for mt in range(2):
                        mm(SC[:, h * 256:(h + 1) * 256], PxT[:, h * 2 + mt, :], mvb[:, mt, h * 256:(h + 1) * 256], start=(mt == 0), stop=(mt == 1))
                recip(sum4, sum4)
                tt(oxb.re("p (h e) -> p h e", h=4), SC.re("p (h e) -> p h e", h=4), sum4.un(2).bc([128, 4, 256]), ALU.mult)
            else:
                for s_ in range(4):
                    qs = slice(s_ * 32, (s_ + 1) * 32)
                    mkTs = mkTs4[s_]; mvs = mvs4[s_]
                    for h in range(4):
                        for et in range(2):
                            mm(SC[0:32, h * 256:(h + 1) * 256], qxT[:, h * 2 + et, qs], mkTs[:, h, et, :], start=(et == 0), stop=(et == 1))
                    red(mx4[0:32, :], SC[0:32, :].re("p (h m) -> p h m", h=4), ALU.max)
                    ts(neg4[0:32, :], mx4[0:32, :], -1.0, None, ALU.mult)
                    for h in range(4):
                        act(Px[0:32, h, :], SC[0:32, h * 256:(h + 1) * 256], AF.Exp, bias=neg4[0:32, h:h + 1], accum=sum4[0:32, h:h + 1])
                    bb = banks[4].cast(BF16)
                    for k in range(8):
                        tr(bb[:, k * 32:(k + 1) * 32], Px.re("p a b -> p (a b)")[0:32, k * 128:(k + 1) * 128], identb[0:32, 0:32])
                    vcopy(PxT[:, :, 0:32], bb[:, 0:256].re("p (k q) -> p k q", k=8))
                    for h in range(4):
                        for mt in range(2):
                            mm(SC[0:32, h * 256:(h + 1) * 256], PxT[:, h * 2 + mt, 0:32], mvs[:, mt, h * 256:(h + 1) * 256], start=(mt == 0), stop=(mt == 1))
                    recip(sum4[0:32, :], sum4[0:32, :])
                    tt(oxs[0:32, :].re("p (h e) -> p h e", h=4), SC[0:32, :].re("p (h e) -> p h e", h=4), sum4[0:32, :].un(2).bc([32, 4, 256]), ALU.mult)
                    ld(oxb[s_ * 32:(s_ + 1) * 32, :], oxs[0:32, :])
            for half in range(2):
                transpose_to(oxT[:, half * 4:(half + 1) * 4, :], oxb[:, half * 512:(half + 1) * 512], 4, banks[5], eng=("act" if half == 0 else "dve"))
            for nb in range(2):
                for k in range(8):
                    mm(banks[1 + nb], oxT[:, k, :], W_xo[:, k, nb * 512:(nb + 1) * 512], start=(k == 0), stop=(k == 7))
            for nb in range(2):
                stt(rres[:, nb * 512:(nb + 1) * 512], hin[b][:, nb * 512:(nb + 1) * 512], ALPHA, banks[1 + nb], ALU.mult, ALU.add)
            layernorm(hout, rres, lng, lnb, stats, mvv)
            ld(h2_d[i * 128:(i + 1) * 128, :], hout)

        barrier()
        bump[0] = mark_after_mem

        W1 = alloc([128, 8, 4096], BF16, "W1"); W2 = alloc([128, 32, D], BF16, "W2")
        W1c = [V(W1.ap[:, :, c * 512:(c + 1) * 512], Buf("W1c%d" % c)) for c in range(8)]
        W2c = [V(W2.ap[:, c * 8:(c + 1) * 8, :], Buf("W2c%d" % c)) for c in range(4)]
        w1v = w_ff1_d.re("(kt p) n -> p kt n", p=128); w2v = w_ff2_d.re("(kt p) n -> p kt n", p=128)
        for c in range(8):
            ldw(W1c[c], w1v[:, :, c * 512:(c + 1) * 512])
        for c in range(4):
            ldw(W2c[c], w2v[:, c * 8:(c + 1) * 8, :])
        lng = alloc([128, D], F32, "lng3"); lnb = alloc([128, D], F32, "lnb3")
        ld(lng, lng_d[:, 2 * D:3 * D]); ld(lnb, lnb_d[:, 2 * D:3 * D])
        hin = alloc([128, D], F32, "ch")
        hb_ = alloc([128, D], BF16, "chb"); hT = alloc([128, 8, 512], BF16, "chT")
        zr = [alloc([128, 512], F32, "zr%d" % i) for i in range(2)]; zT = alloc([128, 32, 512], BF16, "zT")
        rres = alloc([128, D], F32, "crres"); hout = alloc([128, D], F32, "chout")
        stats = alloc([128, 12], F32); mvv = alloc([128, 2], F32)
        groups = [list(range(g_ * 4, g_ * 4 + 4)) for g_ in range(4)] + [[16]]
        for grp in groups:
            nt = len(grp)
            for q_, i in enumerate(grp):
                ld(hin, h2_d[i * 128:(i + 1) * 128, :])
                vcopy(hb_, hin, eng="pool")
                for half in range(2):
                    transpose_to(hT[:, half * 4:(half + 1) * 4, q_ * 128:(q_ + 1) * 128], hb_[:, half * 512:(half + 1) * 512], 4, banks[0], eng=("act" if half == 0 else "dve"))
            W_ = nt * 128
            for f in range(32):
                bk = banks[1 + f % 2]
                for k in range(8):
                    mm(bk[:, 0:W_], W1c[f // 4][:, k, (f % 4) * 128:(f % 4 + 1) * 128], hT[:, k, 0:W_], start=(k == 0), stop=(k == 7))
                act(zr[f % 2][:, 0:W_], bk[:, 0:W_], AF.Relu)
                tt(zT[:, f, 0:W_], zr[f % 2][:, 0:W_], zr[f % 2][:, 0:W_], ALU.mult)
            for q_, i in enumerate(grp):
                for nb in range(2):
                    for f in range(32):
                        mm(SC[:, nb * 512:(nb + 1) * 512], zT[:, f, q_ * 128:(q_ + 1) * 128], W2c[f // 8][:, f % 8, nb * 512:(nb + 1) * 512], start=(f == 0), stop=(f == 31))
                ld(hin, h2_d[i * 128:(i + 1) * 128, :])
                stt(rres, hin, ALPHA, SC, ALU.mult, ALU.add)
                layernorm(hout, rres, lng, lnb, stats, mvv)
                ld(y_o[i * 128:(i + 1) * 128, :], hout)

        P.emit(st)
    return nc


def _lay_gp(a):
    sh = a.shape[2:]
    n = len(sh)
    return np.ascontiguousarray(a.reshape(16, 2, 64, *sh).transpose(1, 2, 0, *range(3, 3 + n)).reshape(128, 16, *sh))


def _bc(v, n=128):
    return np.ascontiguousarray(np.broadcast_to(np.asarray(v, np.float32).reshape(1, -1), (n, np.asarray(v).size)))


def _rope_tables(pos):
    inv = (10000.0 ** (-np.arange(32, dtype=np.float32) / 32)).astype(np.float32)
    ang = pos.astype(np.float32)[:, None] * inv[None, :]
    c = np.cos(ang).astype(np.float32)
    s = np.sin(ang).astype(np.float32)
    return np.concatenate([c, c], 1), np.concatenate([-s, s], 1)


_NC_CACHE = {}


def kernel(x_prompt, x_sample, mem_prompt, cache_mla_ckv, cache_mla_kpe, state_ssm_re, state_ssm_im,
           cache_mem_k, cache_mem_v, w_in, g_q, w_q_up, g_kv, w_kv_up, a_re, a_im, b_re, b_im, c_re, c_im,
           d_skip, log_dt, w_glu, g_out_ssm, g_out_mla, w_o, w_xq, w_xk, w_xv, w_xo, w_ff1, w_ff2, ln_g, ln_b):
    f = lambda a: np.ascontiguousarray(np.asarray(a, dtype=np.float32))
    x_prompt = f(x_prompt); x_sample = f(x_sample)
    xp = x_prompt[0]
    cre = np.zeros((128, 16, 32), np.float32); cimn = np.zeros((128, 16, 32), np.float32)
    cr = f(c_re)[0].reshape(16, 2, 16, 64); ci = f(c_im)[0].reshape(16, 2, 16, 64)
    for g2 in range(2):
        cre[g2 * 64:(g2 + 1) * 64, :, g2 * 16:(g2 + 1) * 16] = cr[:, g2].transpose(2, 0, 1)
        cimn[g2 * 64:(g2 + 1) * 64, :, g2 * 16:(g2 + 1) * 16] = -ci[:, g2].transpose(2, 0, 1)
    dblk = np.zeros((128, 16, 32), np.float32)
    dd = f(d_skip)[0].reshape(512)
    for gp in range(16):
        for c in range(32):
            ch = gp * 32 + c
            dblk[ch % 128, gp, c] = dd[ch]
    pos_p = np.arange(16384)
    rcp, rsp = _rope_tables(pos_p)
    common = {
        "xp": xp, "mem": f(mem_prompt)[0], "ident": np.eye(128, dtype=np.float32),
        "w_in": f(w_in)[0], "w_q": f(w_q_up)[0].reshape(384, 768), "w_kv": f(w_kv_up)[0].reshape(256, 1024),
        "w_glu": f(w_glu)[0], "w_o": f(w_o)[0], "w_xq": f(w_xq)[0].reshape(D, D), "w_xk": f(w_xk)[0].reshape(D, D),
        "w_xv": f(w_xv)[0].reshape(D, D), "w_xo": f(w_xo)[0].reshape(D, D), "w_ff1": f(w_ff1)[0], "w_ff2": f(w_ff2)[0],
        "gq": _bc(f(g_q)[0]), "gkv": _bc(f(g_kv)[0]), "gos": _bc(f(g_out_ssm)[0]), "gom": _bc(f(g_out_mla)[0]),
        "lng": _bc(f(ln_g)[0].reshape(-1)), "lnb": _bc(f(ln_b)[0].reshape(-1)),
        "ropec_p": rcp, "ropes_p": rsp,
        "ar": _lay_gp(f(a_re)[0]), "ai": _lay_gp(f(a_im)[0]),
        "ldt": _lay_gp(np.ascontiguousarray(np.broadcast_to(f(log_dt)[0][:, None], (32, 64)))),
        "bre": _lay_gp(f(b_re)[0]).reshape(128, 256), "bim": _lay_gp(f(b_im)[0]).reshape(128, 256),
        "jj": _bc(np.arange(1, 129, dtype=np.float32)), "jj2": _bc(127.0 - np.arange(128, dtype=np.float32)),
        "cre": cre.reshape(128, 512), "cimn": cimn.reshape(128, 512), "dblk": dblk.reshape(128, 512),
    }
    qi = np.arange(128)[:, None]
    in_maps = []
    for c in range(NCORES):
        tiles = [8 * i + c for i in range(16)]
        xo = np.concatenate([xp[t * 128:(t + 1) * 128] for t in tiles] + [x_sample[4 * c:4 * c + 4].reshape(128, D)], 0)
        pos_o = np.concatenate([np.arange(t * 128, (t + 1) * 128) for t in tiles] + [np.tile(1024 + np.arange(32), 4)])
        rco, rso = _rope_tables(pos_o)
        kj = np.arange(1024)[None, :]
        vis = ((kj // 128) < c) | (((kj // 128) == c) & (((kj % 128) // 64) <= (qi // 64)))
        maskd = np.where(vis, 0.0, NEG).astype(np.float32)
        onehot = np.zeros((128, 8), np.float32); onehot[:, c] = 1.0
        s0 = np.stack([_lay_gp(f(state_ssm_re)[0, 4 * c + s]) for s in range(4)], 1)
        s0i = np.stack([_lay_gp(f(state_ssm_im)[0, 4 * c + s]) for s in range(4)], 1)
        m = dict(common)
        m.update({
            "xo": np.ascontiguousarray(xo), "ropec_o": rco, "ropes_o": rso, "maskd": maskd, "onehot": onehot,
            "s0": np.ascontiguousarray(np.stack([s0, s0i], 1).reshape(128, 128)),
            "cckv": f(cache_mla_ckv)[0, 4 * c:4 * c + 4].reshape(4096, 256),
            "ckpe": f(cache_mla_kpe)[0, 4 * c:4 * c + 4].reshape(4096, 64),
            "cmk": f(cache_mem_k)[0, 4 * c:4 * c + 4].reshape(1024, D),
            "cmv": f(cache_mem_v)[0, 4 * c:4 * c + 4].reshape(1024, D),
        })
        in_maps.append(m)
    if "nc" not in _NC_CACHE:
        _NC_CACHE["nc"] = build_nc()
    res = run_bass_kernel_spmd(_NC_CACHE["nc"], in_maps, core_ids=list(range(NCORES)))
    R = res.results
    y_p = np.zeros((1, 16384, D), np.float32); y_s = np.zeros((32, 32, D), np.float32)
    ckv_s = np.zeros((1, 32, 32, 256), np.float32); kpe_s = np.zeros((1, 32, 32, 64), np.float32)
    sre_s = np.zeros((1, 32, 32, 64), np.float32); sim_s = np.zeros((1, 32, 32, 64), np.float32)

    def unlay(a):
        return a.reshape(2, 64, 16).transpose(2, 0, 1).reshape(32, 64)

    for c in range(NCORES):
        yo = R[c]["y_o"]
        for i in range(16):
            t = 8 * i + c
            y_p[0, t * 128:(t + 1) * 128] = yo[i * 128:(i + 1) * 128]
        y_s[4 * c:4 * c + 4] = yo[16 * 128:].reshape(4, 32, D)
        ckv_s[0, 4 * c:4 * c + 4] = R[c]["ckvs_o"].reshape(4, 32, 256)
        kpe_s[0, 4 * c:4 * c + 4] = R[c]["kpes_o"].reshape(4, 32, 64)
        sf = R[c]["ssms_o"].reshape(128, 2, 4, 16)
        for s in range(4):
            sre_s[0, 4 * c + s] = unlay(sf[:, 0, s, :])
            sim_s[0, 4 * c + s] = unlay(sf[:, 1, s, :])
    r0 = R[0]
    ckv_p = r0["ckv_o"].reshape(1, 1, 16384, 256); kpe_p = r0["kpe_o"].reshape(1, 1, 16384, 64)
    sp = r0["ssmp_o"]
    sre_p = unlay(sp[:, 0:16]).reshape(1, 1, 32, 64); sim_p = unlay(sp[:, 16:32]).reshape(1, 1, 32, 64)
    mk_p = r0["memk_o"].reshape(1, 1, 256, 4, 256); mv_p = r0["memv_o"].reshape(1, 1, 256, 4, 256)
    return (y_p, y_s, ckv_p, kpe_p, sre_p, sim_p, mk_p, mv_p, ckv_s, kpe_s, sre_s, sim_s)
```

```python
import math
from contextlib import ExitStack

import numpy as np
import concourse.bass as bass
import concourse.mybir as mybir
from concourse.bass_utils import run_bass_kernel_spmd

F32 = mybir.dt.float32
BF16 = mybir.dt.bfloat16
I32 = mybir.dt.int32
AF = mybir.ActivationFunctionType
ALU = mybir.AluOpType
AX = mybir.AxisListType

NCORES = 8
D = 1024
NTP = 128
NOWN = 17
EPS = 1e-5
ALPHA = 2.0 ** 0.25
MLA_SCALE = 192.0 ** -0.5
X_SCALE = 256.0 ** -0.5
TWO_PI = 2.0 * math.pi
NEG = -1e30
NRING = 24
COMPUTE = ("pe", "act", "dve", "pool")


class Buf:
    __slots__ = ("name", "lw", "rd")

    def __init__(self, name=""):
        self.name = name
        self.lw = None
        self.rd = []


def _flat(bs):
    out = []
    for b in bs:
        if isinstance(b, (tuple, list)):
            out.extend(b)
        else:
            out.append(b)
    return out


class Op:
    __slots__ = ("eng", "fn", "deps", "isdma", "flag", "cnt", "ring", "n")

    def __init__(self, eng, fn, isdma):
        self.eng = eng
        self.fn = fn
        self.isdma = isdma
        self.deps = set()
        self.flag = False
        self.cnt = 0
        self.ring = None
        self.n = 0


class Prog:
    def __init__(self, nc):
        self.nc = nc
        self.ops = {e: [] for e in ("pe", "act", "dve", "pool", "sp")}
        self.ndma = {e: 0 for e in self.ops}
        self.allops = []
        self.floor = []
        self.dmas_since = []

    def _add(self, eng, fn, reads, writes, isdma):
        op = Op(eng, fn, isdma)
        reads = _flat(reads)
        writes = _flat(writes)
        for f in self.floor:
            op.deps.add(f)
        for b in reads:
            if b.lw is not None:
                op.deps.add(b.lw)
        for b in writes:
            if b.lw is not None:
                op.deps.add(b.lw)
            for r in b.rd:
                op.deps.add(r)
        for b in reads:
            b.rd.append(op)
        for b in writes:
            b.lw = op
            b.rd = []
        op.deps.discard(op)
        if isdma:
            op.n = self.ndma[eng]
            self.ndma[eng] += 1
            self.dmas_since.append(op)
        self.ops[eng].append(op)
        self.allops.append(op)
        return op

    def op(self, eng, fn, reads=(), writes=()):
        return self._add(eng, fn, reads, writes, False)

    def dma(self, eng, fn, reads=(), writes=()):
        return self._add(eng, fn, reads, writes, True)

    def barrier(self, fn):
        op = Op("pool", fn, False)
        for f in self.floor:
            op.deps.add(f)
        for e in COMPUTE:
            for o in reversed(self.ops[e]):
                if not o.isdma:
                    op.deps.add(o)
                    break
        for o in self.dmas_since:
            op.deps.add(o)
        self.dmas_since = []
        self.ops["pool"].append(op)
        self.allops.append(op)
        self.floor = [op]

    def emit(self, stack):
        nc = self.nc
        for op in self.allops:
            for d in op.deps:
                if d.eng == "pe" and op.eng == "pe" and not d.isdma:
                    continue
                d.flag = True
        sems = {e: stack.enter_context(nc.semaphore("s_" + e)) for e in COMPUTE}
        rings = {}
        for e in self.ops:
            if self.ndma[e]:
                rings[e] = [stack.enter_context(nc.semaphore("r_%s_%d" % (e, i))) for i in range(NRING)]
        for e in self.ops:
            c = 0
            for op in self.ops[e]:
                if op.isdma:
                    op.ring = rings[e][op.n % NRING]
                    op.cnt = 16 * (op.n // NRING + 1)
                elif op.flag:
                    c += 1
                    op.cnt = c
        block = stack.enter_context(nc.Block())

        def run(e, h):
            waited = {}

            def wait(sem, val):
                k = id(sem)
                if waited.get(k, 0) >= val:
                    return
                waited[k] = val
                h.wait_ge(sem, val)

            for op in self.ops[e]:
                for d in op.deps:
                    if d.isdma:
                        wait(d.ring, d.cnt)
                    else:
                        if d.eng == "pe" and e == "pe":
                            continue
                        wait(sems[d.eng], d.cnt)
                if op.isdma and op.n >= NRING:
                    wait(op.ring, op.cnt - 16)
                ins = op.fn(h)
                if op.isdma:
                    ins.then_inc(op.ring, 16)
                elif op.flag:
                    ins.then_inc(sems[e], 1)
            if e in rings:
                n = self.ndma[e]
                for i in range(min(n, NRING)):
                    last = ((n - 1 - i) // NRING) * NRING + i
                    wait(rings[e][i], 16 * (last // NRING + 1))

        @block.tensor
        def _(h):
            run("pe", h)

        @block.scalar
        def _(h):
            run("act", h)

        @block.vector
        def _(h):
            run("dve", h)

        @block.gpsimd
        def _(h):
            run("pool", h)

        @block.sync
        def _(h):
            run("sp", h)


class V:
    __slots__ = ("ap", "b")

    def __init__(self, ap, b):
        self.ap = ap
        self.b = b

    def __getitem__(self, idx):
        return V(self.ap[idx], self.b)

    def re(self, pat, **kw):
        return V(self.ap.rearrange(pat, **kw), self.b)

    def cast(self, dt):
        return V(self.ap.bitcast(dt), self.b)

    def bc(self, shape):
        return V(self.ap.broadcast_to(shape), self.b)

    def un(self, ax):
        return V(self.ap.unsqueeze(ax), self.b)


def build_nc():
    nc = bass.Bass("TRN2", target_bir_lowering=False)
    P = Prog(nc)
    dram = {}

    def din(name, shape, dt=F32):
        t = nc.dram_tensor(name, list(shape), dt, kind="ExternalInput")
        dram[name] = V(t.ap(), Buf(name))
        return dram[name]

    def dout(name, shape):
        t = nc.dram_tensor(name, list(shape), F32, kind="ExternalOutput")
        dram[name] = V(t.ap(), Buf(name))
        return dram[name]

    def dscr(name, shape, dt=F32):
        t = nc.dram_tensor(name, list(shape), dt)
        return V(t.ap(), Buf(name))

    xp = din("xp", [NTP * 128, D])
    xo = din("xo", [NOWN * 128, D])
    mem = din("mem", [256, D])
    ident_d = din("ident", [128, 128])
    w_in_d = din("w_in", [D, 1216]); w_q_d = din("w_q", [384, 768]); w_kv_d = din("w_kv", [256, 1024])
    w_glu_d = din("w_glu", [512, 1024]); w_o_d = din("w_o", [D, D]); w_xq_d = din("w_xq", [D, D])
    w_xk_d = din("w_xk", [D, D]); w_xv_d = din("w_xv", [D, D]); w_xo_d = din("w_xo", [D, D])
    w_ff1_d = din("w_ff1", [D, 4096]); w_ff2_d = din("w_ff2", [4096, D])
    gq_d = din("gq", [128, 384]); gkv_d = din("gkv", [128, 256]); gos_d = din("gos", [128, 512]); gom_d = din("gom", [128, 512])
    lng_d = din("lng", [128, 3 * D]); lnb_d = din("lnb", [128, 3 * D])
    ropec_p = din("ropec_p", [NTP * 128, 64]); ropes_p = din("ropes_p", [NTP * 128, 64])
    ropec_o = din("ropec_o", [NOWN * 128, 64]); ropes_o = din("ropes_o", [NOWN * 128, 64])
    maskd_d = din("maskd", [128, 1024]); onehot_d = din("onehot", [128, 8])
    ar_d = din("ar", [128, 16]); ai_d = din("ai", [128, 16]); ldt_d = din("ldt", [128, 16])
    bre_d = din("bre", [128, 256]); bim_d = din("bim", [128, 256]); jj_d = din("jj", [128, 128]); jj2_d = din("jj2", [128, 128])
    cre_d = din("cre", [128, 512]); cimn_d = din("cimn", [128, 512]); dblk_d = din("dblk", [128, 512])
    s0_d = din("s0", [128, 128])
    cckv_d = din("cckv", [4 * 1024, 256]); ckpe_d = din("ckpe", [4 * 1024, 64])
    cmk_d = din("cmk", [4 * 256, D]); cmv_d = din("cmv", [4 * 256, D])

    y_o = dout("y_o", [NOWN * 128, D])
    ckv_o = dout("ckv_o", [NTP * 128, 256]); kpe_o = dout("kpe_o", [NTP * 128, 64])
    ssmp_o = dout("ssmp_o", [128, 32])
    memk_o = dout("memk_o", [256, D]); memv_o = dout("memv_o", [256, D])
    ckvs_o = dout("ckvs_o", [128, 256]); kpes_o = dout("kpes_o", [128, 64]); ssms_o = dout("ssms_o", [128, 128])

    h1_d = dscr("h1_d", [NOWN * 128, D]); h2_d = dscr("h2_d", [NOWN * 128, D])
    tab_d = {n_: dscr("tab_" + n_, [128, 2048]) for n_ in ("S_t", "C_t", "R_t")}
    tab_d["WBre"] = dscr("tab_WBre", [128, 2048], BF16); tab_d["WBim"] = dscr("tab_WBim", [128, 2048], BF16)

    with ExitStack() as st:
        ARENA_N = 52000
        arena = st.enter_context(nc.sbuf_tensor("arena", [128, ARENA_N], F32))
        bump = [0]

        def alloc(shape, dt=F32, name=""):
            n = 1
            for s_ in shape[1:]:
                n *= s_
            words = (n * (2 if dt == BF16 else 4) + 3) // 4
            words = (words + 7) // 8 * 8
            off = bump[0]
            bump[0] += words
            assert bump[0] <= ARENA_N, ("SBUF arena overflow", name, bump[0])
            ap = arena[:, off:off + words]
            if dt != F32:
                ap = ap.bitcast(dt)
            ap = ap[:, 0:n]
            if len(shape) > 2:
                names = " ".join("d%d" % i for i in range(len(shape) - 1))
                kw = {"d%d" % i: shape[i + 1] for i in range(len(shape) - 1)}
                ap = ap.rearrange("p (%s) -> p %s" % (names, names), **kw)
            return V(ap, Buf(name))

        banks = []
        dbl = []
        for d_ in range(4):
            t = st.enter_context(nc.psum_tensor("dbank%d" % d_, [128, 1024], F32))
            b0 = Buf("bank%d" % (2 * d_)); b1 = Buf("bank%d" % (2 * d_ + 1))
            banks.append(V(t[:, 0:512], b0)); banks.append(V(t[:, 512:1024], b1))
            dbl.append(V(t[:], (b0, b1)))
        SC = dbl[3]
        SCs = [dbl[3], dbl[0]]

        def mm(out, lhsT, rhs, start=True, stop=True):
            P.op("pe", lambda e: e.matmul(out.ap, lhsT=lhsT.ap, rhs=rhs.ap, start=start, stop=stop),
                 reads=[lhsT.b, rhs.b], writes=[out.b])

        def tr(out, in_, idt):
            P.op("pe", lambda e: e.transpose(out=out.ap, in_=in_.ap, identity=idt.ap), reads=[in_.b, idt.b], writes=[out.b])

        def act(out, in_, func, bias=None, scale=None, accum=None):
            kw = {}
            rd = [in_.b]
            wr = [out.b]
            if bias is not None:
                if isinstance(bias, V):
                    kw["bias"] = bias.ap; rd.append(bias.b)
                else:
                    kw["bias"] = bias
            if scale is not None:
                if isinstance(scale, V):
                    kw["scale"] = scale.ap; rd.append(scale.b)
                else:
                    kw["scale"] = scale
            if accum is not None:
                kw["accum_out"] = accum.ap; wr.append(accum.b)
            P.op("act", lambda e: e.activation(out=out.ap, in_=in_.ap, func=func, **kw), reads=rd, writes=wr)

        def acopy(out, in_):
            P.op("act", lambda e: e.copy(out=out.ap, in_=in_.ap), reads=[in_.b], writes=[out.b])

        def vcopy(out, in_, eng="dve"):
            P.op(eng, lambda e: e.tensor_copy(out=out.ap, in_=in_.ap), reads=[in_.b], writes=[out.b])

        def tt(out, in0, in1, op, eng="dve"):
            P.op(eng, lambda e: e.tensor_tensor(out=out.ap, in0=in0.ap, in1=in1.ap, op=op), reads=[in0.b, in1.b], writes=[out.b])

        def ts(out, in0, s1, s2, op0, op1=None, eng="dve"):
            rd = [in0.b]
            a1 = s1
            a2 = s2
            if isinstance(s1, V):
                a1 = s1.ap; rd.append(s1.b)
            if isinstance(s2, V):
                a2 = s2.ap; rd.append(s2.b)
            if op1 is None:
                P.op(eng, lambda e: e.tensor_scalar(out=out.ap, in0=in0.ap, scalar1=a1, scalar2=None, op0=op0), reads=rd, writes=[out.b])
            else:
                P.op(eng, lambda e: e.tensor_scalar(out=out.ap, in0=in0.ap, scalar1=a1, scalar2=a2, op0=op0, op1=op1), reads=rd, writes=[out.b])

        def stt(out, in0, scalar, in1, op0, op1):
            rd = [in0.b, in1.b]
            a = scalar
            if isinstance(scalar, V):
                a = scalar.ap; rd.append(scalar.b)
            P.op("dve", lambda e: e.scalar_tensor_tensor(out=out.ap, in0=in0.ap, scalar=a, in1=in1.ap, op0=op0, op1=op1), reads=rd, writes=[out.b])

        def red(out, in_, op, axis=AX.X):
            P.op("dve", lambda e: e.tensor_reduce(out=out.ap, in_=in_.ap, axis=axis, op=op), reads=[in_.b], writes=[out.b])

        def recip(out, in_):
            P.op("dve", lambda e: e.reciprocal(out=out.ap, in_=in_.ap), reads=[in_.b], writes=[out.b])

        def scan(out, d0, d1, init):
            P.op("dve", lambda e: e.tensor_tensor_scan(out=out.ap, data0=d0.ap, data1=d1.ap, initial=init.ap, op0=ALU.mult, op1=ALU.add),
                 reads=[d0.b, d1.b, init.b], writes=[out.b])

        def mset(out, val, eng="pool"):
            P.op(eng, lambda e: e.memset(out.ap, val), writes=[out.b])

        def ld(out, in_, eng="sp"):
            P.dma(eng, lambda e: e.dma_start(out=out.ap, in_=in_.ap), reads=[in_.b], writes=[out.b])

        def ldw(out, in_):
            P.dma("pool", lambda e: e.dma_start(out=out.ap, in_=in_.ap), reads=[in_.b], writes=[out.b])

        ident = alloc([128, 128], F32, "ident"); identb = alloc([128, 128], BF16, "identb")
        bar_scr = alloc([128, 8], F32, "barscr")
        ld(ident, ident_d); ldw(identb, ident_d)
        epsc = alloc([128, 1], F32, "epsc"); mset(epsc, EPS)

        def barrier():
            P.barrier(lambda e: e.memset(bar_scr.ap, 0.0))

        def load_xT(src_rows, xt, xT, bank):
            ld(xt, src_rows)
            for hb in range(2):
                for k in range(4):
                    kk = hb * 4 + k
                    tr(bank[:, k * 128:(k + 1) * 128], xt[:, kk * 128:(kk + 1) * 128], ident)
                src = bank.re("p (k n) -> p k n", k=4)
                if hb == 0:
                    acopy(xT[:, 0:4, :], src)
                else:
                    vcopy(xT[:, 4:8, :], src)

        def transpose_to(dst, src, ncol, bank, eng="act", rows=128):
            bb = bank.cast(BF16)
            for k in range(ncol):
                tr(bb[:, k * 128:k * 128 + rows], src[:, k * 128:(k + 1) * 128], identb[0:rows, 0:rows])
            s_ = bb[:, 0:ncol * 128].re("p (k n) -> p k n", k=ncol)[:, :, 0:rows]
            if eng == "act":
                acopy(dst, s_)
            else:
                vcopy(dst, s_)

        def rmsnorm(out, src, n, gtile, sq, ss):
            act(sq, src, AF.Square, accum=ss)
            act(ss, ss, AF.Ln, scale=1.0 / n, bias=epsc)
            act(ss, ss, AF.Exp, scale=-0.5)
            stt(out, src, ss, gtile, ALU.mult, ALU.mult)

        def layernorm(out, r, g, b, stats, mv):
            for c in range(2):
                P.op("dve", lambda e, c=c: e.bn_stats(out=stats.ap[:, c * 6:(c + 1) * 6], in_=r.ap[:, c * 512:(c + 1) * 512]), reads=[r.b], writes=[stats.b])
            P.op("dve", lambda e: e.bn_aggr(out=mv.ap[:, 0:2], in_=stats.ap[:, 0:12]), reads=[stats.b], writes=[mv.b])
            act(mv[:, 1:2], mv[:, 1:2], AF.Ln, bias=epsc)
            act(mv[:, 1:2], mv[:, 1:2], AF.Exp, scale=-0.5)
            ts(out, r, mv[:, 0:1], mv[:, 1:2], ALU.subtract, ALU.mult)
            tt(out, out, g, ALU.mult, eng="pool")
            tt(out, out, b, ALU.add, eng="pool")

        def rope(out, src, cc, ss_, tmp, nh, eng="dve"):
            ccb = cc.un(1).bc([128, nh, 64]); ssb = ss_.un(1).bc([128, nh, 64])
            tt(out, src, ccb, ALU.mult, eng=eng)
            tt(tmp[:, :, 0:32], src[:, :, 32:64], ssb[:, :, 0:32], ALU.mult, eng=eng)
            tt(tmp[:, :, 32:64], src[:, :, 0:32], ssb[:, :, 32:64], ALU.mult, eng=eng)
            tt(out, out, tmp, ALU.add, eng="pool")

        mkT = alloc([128, 4, 2, 256], BF16, "mkT")
        mvb = alloc([128, 2, D], BF16, "mvb")
        mark_after_mem = bump[0]
        btab = Buf("tab")
        W_in_u = alloc([128, 8, 512], BF16, "W_in_u"); W_in_kv = alloc([128, 8, 320], BF16, "W_in_kv")
        LrT = alloc([128, 16, 128], BF16, "LrT"); LiT = alloc([128, 16, 128], BF16, "LiT")
        BRb = alloc([128, 16, 32], F32, "BRb"); BIb = alloc([128, 16, 32], F32, "BIb")
        lam128 = alloc([128, 2, 16], F32, "lam128")
        Xpp = alloc([128, 2, 32], F32, "Xpp")
        Xown = alloc([128, 16, 32], F32, "Xown")
        ohot = alloc([128, 8], F32, "ohot")
        Oacc = alloc([128, 16, 4, 128], F32, "Oacc")
        m_run = alloc([128, 64], F32, "m_run"); l_run = alloc([128, 64], F32, "l_run")
        Osam = alloc([128, 512], F32, "Osam")
        ssmx = {}
        mark_after_mixer_state = bump[0]
        QTn = alloc([128, NOWN, 4, 128], BF16, "QTn"); QTr = alloc([128, NOWN, 4, 128], BF16, "QTr")
        mark_after_q = bump[0]

        w_in_v = w_in_d.re("(kt p) n -> p kt n", p=128)
        ldw(W_in_u, w_in_v[:, :, 0:512]); ldw(W_in_kv, w_in_v[:, :, 896:1216])
        ld(ohot, onehot_d)

        S_t = alloc([128, 16, 128], F32, "S_t"); C_t = alloc([128, 16, 128], F32, "C_t"); R_t = alloc([128, 16, 128], F32, "R_t")
        for v_ in (S_t, C_t, R_t):
            v_.b = btab
        WBre = alloc([128, 16, 128], BF16, "WBre"); WBim = alloc([128, 16, 128], BF16, "WBim")

        p0 = bump[0]
        ar = alloc([128, 16]); ai = alloc([128, 16]); ldt = alloc([128, 16]); jj = alloc([128, 128])
        bre = alloc([128, 16, 16]); bim = alloc([128, 16, 16])
        bsm = Buf("ssm_small")
        for v_ in (ar, ai, ldt, jj, bre, bim):
            v_.b = bsm
        ld(ar, ar_d); ld(ai, ai_d); ld(ldt, ldt_d); ld(jj, jj_d)
        ld(bre.re("p a b -> p (a b)"), bre_d); ld(bim.re("p a b -> p (a b)"), bim_d)
        dt_ = alloc([128, 16]); th = alloc([128, 16]); rr = alloc([128, 16])
        sm = [alloc([128, 16]) for _ in range(8)]
        for v_ in [dt_, th, rr] + sm:
            v_.b = bsm
        act(dt_, ldt, AF.Exp)
        tt(th, ai, dt_, ALU.mult)
        tt(rr, ar, dt_, ALU.mult)
        act(rr, rr, AF.Exp)
        A_t = alloc([128, 16, 128]); T1 = alloc([128, 2048]); TI = alloc([128, 2048], I32)
        for v_ in (A_t, T1, TI):
            v_.b = btab
        jjb = jj.un(1).bc([128, 16, 128])
        tt(A_t, jjb, th.un(2).bc([128, 16, 128]), ALU.mult)
        tt(R_t, jjb, rr.un(2).bc([128, 16, 128]), ALU.max)
        tt(R_t, R_t, rr.un(2).bc([128, 16, 128]), ALU.min)
        Af = A_t.re("p a b -> p (a b)")
        TIf = TI.cast(F32)

        def sin_of(out, shift):
            ts(T1, Af, shift, 1.0 / TWO_PI, ALU.add, ALU.mult)
            vcopy(TI, T1)
            vcopy(T1, TI)
            stt(T1, T1, -TWO_PI, Af, ALU.mult, ALU.add)
            if shift != 0.0:
                ts(T1, T1, shift, None, ALU.add)
            ts(TIf, T1, math.pi, -TWO_PI, ALU.is_gt, ALU.mult)
            tt(T1, T1, TIf, ALU.add)
            ts(TIf, T1, -math.pi, TWO_PI, ALU.is_lt, ALU.mult)
            tt(T1, T1, TIf, ALU.add)
            ts(T1, T1, math.pi, -math.pi, ALU.min, ALU.max)
            act(out, T1, AF.Sin)

        sin_of(S_t.re("p a b -> p (a b)"), 0.0)
        sin_of(C_t.re("p a b -> p (a b)"), math.pi / 2)
        lbr, lbi, den, fre, fim, t0_, t1_, t2_ = sm
        cos1 = C_t[:, :, 0]; sin1 = S_t[:, :, 0]
        tt(lbr, rr, cos1, ALU.mult)
        ts(lbr, lbr, -1.0, None, ALU.add)
        tt(lbi, rr, sin1, ALU.mult)
        tt(den, ar, ar, ALU.mult)
        tt(t0_, ai, ai, ALU.mult)
        tt(den, den, t0_, ALU.add)
        recip(den, den)
        tt(t0_, lbr, ar, ALU.mult); tt(t1_, lbi, ai, ALU.mult); tt(fre, t0_, t1_, ALU.add); tt(fre, fre, den, ALU.mult)
        tt(t0_, lbi, ar, ALU.mult); tt(t1_, lbr, ai, ALU.mult); tt(fim, t0_, t1_, ALU.subtract); tt(fim, fim, den, ALU.mult)
        Mre = alloc([128, 16, 128]); Mim = alloc([128, 16, 128]); tb = alloc([128, 16, 16]); tb2 = alloc([128, 16, 16])
        Mreb = alloc([128, 16, 128], BF16); Mimb = alloc([128, 16, 128], BF16)
        bM = Buf("M")
        for v_ in (Mre, Mim, tb, tb2, Mreb, Mimb):
            v_.b = bM
        freb = fre.un(2).bc([128, 16, 16]); fimb = fim.un(2).bc([128, 16, 16])
        mset(Mre, 0.0); mset(Mim, 0.0)
        tt(tb, bre, freb, ALU.mult); tt(tb2, bim, fimb, ALU.mult)
        for lo, col in ((0, 0), (64, 16)):
            for j4 in range(4):
                tt(Mre[lo:lo + 64, j4::4, 32 * j4 + col:32 * j4 + col + 16], tb[lo:lo + 64, j4::4, :], tb2[lo:lo + 64, j4::4, :], ALU.subtract)
        tt(tb, bim, freb, ALU.mult); tt(tb2, bre, fimb, ALU.mult)
        for lo, col in ((0, 0), (64, 16)):
            for j4 in range(4):
                tt(Mim[lo:lo + 64, j4::4, 32 * j4 + col:32 * j4 + col + 16], tb[lo:lo + 64, j4::4, :], tb2[lo:lo + 64, j4::4, :], ALU.add)
        vcopy(Mreb, Mre); vcopy(Mimb, Mim)
        for M_, WB_ in ((Mreb, WBre), (Mimb, WBim)):
            for hf in range(2):
                bb = banks[0].cast(BF16)
                for q in range(8):
                    tr(bb[:, q * 128:(q + 1) * 128], M_[:, 8 * hf + q, :], identb)
                vcopy(WB_[:, 8 * hf:8 * hf + 8, :].re("p a b -> p (a b)"), bb)

        mset(BRb, 0.0); mset(BIb, 0.0)
        tt(tb, bre, freb, ALU.mult); tt(tb2, bim, fimb, ALU.mult)
        for lo, col in ((0, 0), (64, 16)):
            tt(BRb[lo:lo + 64, :, col:col + 16], tb[lo:lo + 64], tb2[lo:lo + 64], ALU.subtract)
        tt(tb, bim, freb, ALU.mult); tt(tb2, bre, fimb, ALU.mult)
        for lo, col in ((0, 0), (64, 16)):
            tt(BIb[lo:lo + 64, :, col:col + 16], tb[lo:lo + 64], tb2[lo:lo + 64], ALU.add)
        lnr = alloc([128, 16]); lnr.b = bsm
        tt(lnr, ar, dt_, ALU.mult)
        act(t2_, lnr, AF.Exp, scale=128.0)
        tt(lam128[:, 0, :], t2_, C_t[:, :, 127], ALU.mult)
        tt(lam128[:, 1, :], t2_, S_t[:, :, 127], ALU.mult)
        jj2 = alloc([128, 128]); jj2.b = bsm
        ld(jj2, jj2_d)
        C2 = Mre; S2 = Mim; Mag = T1.re("p (a b) -> p a b", a=16)
        jj2b = jj2.un(1).bc([128, 16, 128])
        tt(A_t, jj2b, th.un(2).bc([128, 16, 128]), ALU.mult)
        sin_of(S2.re("p a b -> p (a b)"), 0.0)
        sin_of(C2.re("p a b -> p (a b)"), math.pi / 2)
        tt(Mag, jj2b, lnr.un(2).bc([128, 16, 128]), ALU.mult)
        act(Mag, Mag, AF.Exp)
        Lb = [Mreb, Mimb]
        tt(Lb[0], Mag, C2, ALU.mult); tt(Lb[1], Mag, S2, ALU.mult)
        for M_, LT_ in ((Lb[0], LrT), (Lb[1], LiT)):
            for hf in range(2):
                bb = banks[0].cast(BF16)
                for q in range(8):
                    tr(bb[:, q * 128:(q + 1) * 128], M_[:, 8 * hf + q, :], identb)
                vcopy(LT_[:, 8 * hf:8 * hf + 8, :].re("p a b -> p (a b)"), bb)

        for nm_, v_ in (("S_t", S_t), ("C_t", C_t), ("R_t", R_t)):
            ld(tab_d[nm_], v_.re("p a b -> p (a b)"))
        ld(tab_d["WBre"], WBre.re("p a b -> p (a b)")); ld(tab_d["WBim"], WBim.re("p a b -> p (a b)"))

        barrier()
        bump[0] = mark_after_q
        W_xk = alloc([128, 8, D], BF16, "W_xk"); W_xv = alloc([128, 8, D], BF16, "W_xv")
        ldw(W_xk, w_xk_d.re("(kt p) n -> p kt n", p=128)); ldw(W_xv, w_xv_d.re("(kt p) n -> p kt n", p=128))
        xt0 = alloc([128, D], F32, "xt0"); memT = alloc([128, 8, 256], BF16, "memT"); xT0 = alloc([128, 8, 128], BF16, "xT0")
        mkf = alloc([128, D], F32, "mkf")
        for mt in range(2):
            load_xT(mem[mt * 128:(mt + 1) * 128, :], xt0, xT0, banks[0])
            vcopy(memT[:, :, mt * 128:(mt + 1) * 128], xT0, eng="pool")
            for W_, o_d, keep in ((W_xk, memk_o, False), (W_xv, memv_o, True)):
                for nb in range(2):
                    for k in range(8):
                        mm(banks[1 + nb], xT0[:, k, :], W_[:, k, nb * 512:(nb + 1) * 512], start=(k == 0), stop=(k == 7))
                    acopy(mkf[:, nb * 512:(nb + 1) * 512], banks[1 + nb])
                ld(o_d[mt * 128:(mt + 1) * 128, :], mkf)
                if keep:
                    vcopy(mvb[:, mt, :], mkf)
        for h in range(4):
            for et in range(2):
                c0 = h * 256 + et * 128
                for k in range(8):
                    mm(banks[3][:, 0:256], W_xk[:, k, c0:c0 + 128], memT[:, k, :], start=(k == 0), stop=(k == 7))
                acopy(mkT[:, h, et, :], banks[3][:, 0:256])


        W_q = alloc([128, 3, 768], BF16, "W_q")
        ldw(W_q, w_q_d.re("(kt p) n -> p kt n", p=128))
        W_in_q = alloc([128, 8, 384], BF16, "W_in_q")
        ldw(W_in_q, w_in_v[:, :, 512:896])
        gq = alloc([128, 384], F32, "gq"); ld(gq, gq_d)
        NB = 2
        xt = [alloc([128, D], F32, "xt%d" % i) for i in range(NB)]
        xT = [alloc([128, 8, 128], BF16, "xT%d" % i) for i in range(NB)]
        rc = [alloc([128, 64], F32, "rc%d" % i) for i in range(NB)]
        rs = [alloc([128, 64], F32, "rs%d" % i) for i in range(NB)]
        sqp = [alloc([128, 384], F32, "sq%d" % i) for i in range(NB)]; ss1 = [alloc([128, 1], F32) for _ in range(NB)]
        cqn = [alloc([128, 384], BF16) for _ in range(NB)]
        cqT = [alloc([128, 3, 128], BF16) for _ in range(NB)]
        qf = [alloc([128, 4, 192], F32) for _ in range(NB)]
        qr = [alloc([128, 4, 64], F32) for _ in range(NB)]
        qtmp = [alloc([128, 4, 64], F32) for _ in range(NB)]
        qb = [alloc([128, 4, 192], BF16) for _ in range(NB)]
        qrd = [alloc([128, 4, 128], BF16) for _ in range(NB)]

        def q_stages(i, p):
            Xb = banks[p]; Jb = banks[2 + p]; Qb = dbl[2 + p]

            def sA():
                load_xT(xo[i * 128:(i + 1) * 128, :], xt[p], xT[p], Xb)
                ld(rc[p], ropec_o[i * 128:(i + 1) * 128, :]); ld(rs[p], ropes_o[i * 128:(i + 1) * 128, :])

            def sB():
                for k in range(8):
                    mm(Jb[:, 0:384], xT[p][:, k, :], W_in_q[:, k, :], start=(k == 0), stop=(k == 7))

            def sC():
                rmsnorm(cqn[p], Jb[:, 0:384], 384, gq, sqp[p], ss1[p])

            def sD():
                transpose_to(cqT[p], cqn[p], 3, Xb, eng="act")

            def sE():
                for nb, (c0, c1) in enumerate(((0, 512), (512, 768))):
                    for k in range(3):
                        mm(Qb[:, nb * 512:nb * 512 + (c1 - c0)], cqT[p][:, k, :], W_q[:, k, c0:c1], start=(k == 0), stop=(k == 2))

            def sF():
                acopy(qf[p].re("p a b -> p (a b)"), Qb[:, 0:768])

            def sG():
                rope(qr[p], qf[p][:, :, 128:192], rc[p], rs[p], qtmp[p], 4)

            def sH():
                P.op("act", lambda e: e.mul(out=qb[p].ap[:, :, 0:128], in_=qf[p].ap[:, :, 0:128], mul=MLA_SCALE), reads=[qf[p].b], writes=[qb[p].b])
                P.op("act", lambda e: e.mul(out=qrd[p].ap[:, :, 0:64], in_=qr[p].ap, mul=MLA_SCALE), reads=[qr[p].b], writes=[qrd[p].b])
                P.op("act", lambda e: e.mul(out=qrd[p].ap[:, :, 64:128], in_=qr[p].ap, mul=MLA_SCALE), reads=[qr[p].b], writes=[qrd[p].b])

            def sI():
                bb = Xb.cast(BF16)
                for h in range(4):
                    tr(bb[:, h * 128:(h + 1) * 128], qb[p][:, h, 0:128], identb)
                    tr(bb[:, 512 + h * 128:512 + (h + 1) * 128], qrd[p][:, h, :], identb)

            def sJ():
                bb = Xb.cast(BF16)
                vcopy(QTn[:, i, :, :].re("p a b -> p (a b)"), bb[:, 0:512])
                vcopy(QTr[:, i, :, :].re("p a b -> p (a b)"), bb[:, 512:1024])

            return [sA, sB, sC, sD, sE, sF, sG, sH, sI, sJ]

        for ip in range(9):
            sa = q_stages(2 * ip, 0)
            sb_ = q_stages(2 * ip + 1, 1) if 2 * ip + 1 < NOWN else []
            for k_ in range(len(sa)):
                sa[k_]()
                if k_ < len(sb_):
                    sb_[k_]()

        barrier()
        bump[0] = mark_after_q

        def ssm_rounds(uT_, pbs, init_fn, nseg, ybank=None, last_out=None):
            L = 128 // nseg
            S_t = ssmx["S_t"]; C_t = ssmx["C_t"]; R_t = ssmx["R_t"]; WBre = ssmx["WBre"]; WBim = ssmx["WBim"]
            Cre = ssmx["Cre"]; Cimn = ssmx["Cimn"]; Dblk = ssmx["Dblk"]
            xs4 = ssmx["xs4"]; zl_all = ssmx["zl_all"]
            def do_round(r):
                m4 = ssmx["m4"][r % 2]; wri = ssmx["wri"][r % 2]; zri = ssmx["zri"][r % 2]
                pb = pbs[r % 2]
                for j in range(2):
                    gp = 2 * r + j
                    mm(pb[:, j * 128:(j + 1) * 128], WBre[:, gp, :], uT_[:, gp // 4, :])
                    mm(pb[:, 256 + j * 128:256 + (j + 1) * 128], WBim[:, gp, :], uT_[:, gp // 4, :])
                if nseg == 1:
                    Cq = C_t[:, 2 * r:2 * r + 2, :].re("p a b -> p (a b)"); Sq = S_t[:, 2 * r:2 * r + 2, :].re("p a b -> p (a b)")
                    pre = pb[:, 0:256]; pim = pb[:, 256:512]
                    mk = lambda v_: v_
                else:
                    Cq = C_t[:, 2 * r:2 * r + 2, 0:L].un(2).bc([128, 2, nseg, L]); Sq = S_t[:, 2 * r:2 * r + 2, 0:L].un(2).bc([128, 2, nseg, L])
                    pre = pb[:, 0:256].re("p (a s l) -> p a s l", a=2, s=nseg); pim = pb[:, 256:512].re("p (a s l) -> p a s l", a=2, s=nseg)
                    mk = lambda v_: v_.re("p (a s l) -> p a s l", a=2, s=nseg)
                tt(mk(m4[0]), pre, Cq, ALU.mult); tt(mk(m4[1]), pim, Sq, ALU.mult)
                tt(mk(m4[2]), pim, Cq, ALU.mult); tt(mk(m4[3]), pre, Sq, ALU.mult)
                tt(wri[0], m4[0], m4[1], ALU.add, eng="pool")
                tt(wri[1], m4[2], m4[3], ALU.subtract, eng="pool")

            def do_back(r):
                m4 = ssmx["m4"][r % 2]; wri = ssmx["wri"][r % 2]; zri = ssmx["zri"][r % 2]
                if nseg == 1:
                    Cq = C_t[:, 2 * r:2 * r + 2, :].re("p a b -> p (a b)"); Sq = S_t[:, 2 * r:2 * r + 2, :].re("p a b -> p (a b)")
                else:
                    Cq = C_t[:, 2 * r:2 * r + 2, 0:L].un(2).bc([128, 2, nseg, L]); Sq = S_t[:, 2 * r:2 * r + 2, 0:L].un(2).bc([128, 2, nseg, L])
                for j in range(2):
                    gp = 2 * r + j
                    for s_ in range(nseg):
                        for part in range(2):
                            scan(zri[part][:, j, s_ * L:(s_ + 1) * L], R_t[:, gp, 0:L], wri[part][:, j * 128 + s_ * L:j * 128 + (s_ + 1) * L], init_fn(gp, s_, part))
                if last_out is not None:
                    for part in range(2):
                        src = zri[part].re("p a (s l) -> p a s l", s=nseg)[:, :, :, L - 1]
                        vcopy(zl_all[part][:, 0:nseg, 2 * r:2 * r + 2].re("p s a -> p a s"), src, eng="pool")
                if ybank is not None:
                    dmd = ssmx["dmd"][r % 2]; xrb = ssmx["xrb"]; xib = ssmx["xib"]
                    if nseg == 1:
                        Cd, Sd = Cq, Sq
                        zr_ = zri[0].re("p a b -> p (a b)"); zi_ = zri[1].re("p a b -> p (a b)")
                        dk = lambda v_: v_
                    else:
                        Cd, Sd = Cq, Sq
                        zr_ = zri[0].re("p a (s l) -> p a s l", s=nseg); zi_ = zri[1].re("p a (s l) -> p a s l", s=nseg)
                        dk = lambda v_: v_.re("p (a s l) -> p a s l", a=2, s=nseg)
                    tt(dk(dmd[0]), zr_, Cd, ALU.mult); tt(dk(dmd[1]), zi_, Sd, ALU.mult, eng="pool")
                    tt(dk(dmd[2]), zr_, Sd, ALU.mult); tt(dk(dmd[3]), zi_, Cd, ALU.mult, eng="pool")
                    tt(xrb[r % 2].re("p a b -> p (a b)"), dmd[0], dmd[1], ALU.subtract, eng="pool")
                    tt(xib[r % 2].re("p a b -> p (a b)"), dmd[2], dmd[3], ALU.add, eng="pool")

            def do_cproj(r):
                if ybank is None:
                    return
                xrb = ssmx["xrb"]; xib = ssmx["xib"]
                for j in range(2):
                    gp = 2 * r + j
                    o_ = ybank[:, 32 * gp:32 * gp + 32]
                    mm(o_, xrb[r % 2][:, j, :], Cre[:, gp, :], start=True, stop=False)
                    mm(o_, xib[r % 2][:, j, :], Cimn[:, gp, :], start=False, stop=False)
                    mm(o_, uT_[:, gp // 4, :], Dblk[:, gp, :], start=False, stop=True)
            def do_tail():
                do_cproj(7)
                if last_out is None:
                    return
                if True:
                    cL = C_t[:, :, L - 1]; sL = S_t[:, :, L - 1]
                    for s_ in range(nseg):
                        o_re, o_im = last_out(s_)
                        zr_l = zl_all[0][:, s_, :]; zi_l = zl_all[1][:, s_, :]
                        tt(xs4[0], zr_l, cL, ALU.mult); tt(xs4[1], zi_l, sL, ALU.mult)
                        tt(xs4[2], zr_l, sL, ALU.mult); tt(xs4[3], zi_l, cL, ALU.mult)
                        tt(o_re, xs4[0], xs4[1], ALU.subtract)
                        tt(o_im, xs4[2], xs4[3], ALU.add)


            def step(k_):
                def f():
                    if k_ == 0:
                        do_round(0)
                    if k_ + 1 < 8:
                        do_round(k_ + 1)
                    do_back(k_)
                    if k_ >= 1:
                        do_cproj(k_ - 1)
                return f
            return [step(k_) for k_ in range(8)] + [do_tail]

        def ssm_tile(uT_, pbs, init_fn, nseg, ybank=None, last_out=None):
            for f_ in ssm_rounds(uT_, pbs, init_fn, nseg, ybank, last_out):
                f_()

        W_kv = alloc([128, 2, 4, 256], BF16, "W_kv")
        ldw(W_kv.re("p k h e -> p k (h e)"), w_kv_d.re("(kt p) n -> p kt n", p=128))
        gkv = alloc([128, 256], F32, "gkv"); ld(gkv, gkv_d)
        maskd = alloc([128, 1024], F32, "maskd"); ld(maskd, maskd_d)
        xt = [alloc([128, D], F32, "hxt%d" % i) for i in range(NB)]
        xT = [alloc([128, 8, 128], BF16, "hxT%d" % i) for i in range(NB)]
        utok = [alloc([128, 512], BF16, "hut%d" % i) for i in range(NB)]
        kvf = [alloc([128, 320], F32, "kvf%d" % i) for i in range(NB)]
        prs = [alloc([128, 16, 32], F32, "prs%d" % i) for i in range(NB)]
        pis = [alloc([128, 16, 32], F32, "pis%d" % i) for i in range(NB)]
        rc = [alloc([128, 64], F32) for _ in range(NB)]; rs = [alloc([128, 64], F32) for _ in range(NB)]
        ckvf = [alloc([128, 256], F32) for _ in range(NB)]; kpef = [alloc([128, 64], F32) for _ in range(NB)]
        tmp64 = [alloc([128, 64], F32) for _ in range(NB)]
        sq = alloc([128, 256], F32, "hsq"); ss1 = [alloc([128, 1], F32) for _ in range(NB)]
        ckvb = [alloc([128, 256], BF16) for _ in range(NB)]; kpeb = [alloc([128, 128], BF16) for _ in range(NB)]
        ckvT = [alloc([128, 2, 128], BF16) for _ in range(NB)]
        KT = [alloc([128, 4, 1024], BF16, "KT%d" % i) for i in range(2)]
        KR = [alloc([128, 1024], BF16, "KR%d" % i) for i in range(2)]
        Vb = [alloc([128, 8, 512], BF16, "Vb%d" % i) for i in range(2)]
        Pb = [alloc([128, 1024], BF16, "Pb%d" % i) for i in range(2)]
        PT = [alloc([128, 8, 128], BF16, "PT%d" % i) for i in range(2)]
        NS4 = 4
        mx = [alloc([128, 1], F32) for _ in range(NS4)]; mnew = [alloc([128, 1], F32) for _ in range(NS4)]
        negm = [alloc([128, 1], F32) for _ in range(NS4)]; alp = [alloc([128, 1], F32) for _ in range(NS4)]
        rsum = [alloc([128, 1], F32) for _ in range(NS4)]
        dm_ = [alloc([128, 1], F32) for _ in range(NS4)]
        hm1 = [alloc([128, 16, 32], F32, "hm1_%d" % i) for i in range(NB)]; hm2 = [alloc([128, 16, 32], F32, "hm2_%d" % i) for i in range(NB)]
        sre = [alloc([128, 32], F32) for _ in range(2)]
        xs8 = [alloc([128, 16], F32) for _ in range(4)]

        mset(Xpp, 0.0); mset(Xown, 0.0)
        mset(m_run, -NEG); mset(l_run, 0.0); mset(Oacc, 0.0)
        for kp in range(2):
            mset(kpeb[kp], 0.0)

        def hist_stages(t, kbuf, j, kb):
            b = t % NB
            bA = banks[0]; bB = banks[1]
            xc = Xpp[:, t % 2, :]; xn = Xpp[:, (t + 1) % 2, :]

            def sA():
                ld(xt[b], xp[t * 128:(t + 1) * 128, :])
                ld(rc[b], ropec_p[t * 128:(t + 1) * 128, :]); ld(rs[b], ropes_p[t * 128:(t + 1) * 128, :])

            def sB():
                for k in range(4):
                    tr(bA[:, k * 128:(k + 1) * 128], xt[b][:, k * 128:(k + 1) * 128], ident)
                for k in range(4):
                    tr(bB[:, k * 128:(k + 1) * 128], xt[b][:, (4 + k) * 128:(5 + k) * 128], ident)

            def sC():
                acopy(xT[b][:, 0:4, :], bA.re("p (k n) -> p k n", k=4))
                acopy(xT[b][:, 4:8, :], bB.re("p (k n) -> p k n", k=4))

            def sD():
                for k in range(8):
                    mm(bA, xT[b][:, k, :], W_in_u[:, k, :], start=(k == 0), stop=(k == 7))
                for k in range(8):
                    mm(bB[:, 0:320], xT[b][:, k, :], W_in_kv[:, k, :], start=(k == 0), stop=(k == 7))

            def sE():
                acopy(utok[b], bA)
                acopy(kvf[b], bB[:, 0:320])

            def sF():
                act(sq, kvf[b][:, 0:256], AF.Square, accum=ss1[b])
                act(ss1[b], ss1[b], AF.Ln, scale=1.0 / 256, bias=epsc)
                act(ss1[b], ss1[b], AF.Exp, scale=-0.5)
                for gp in range(16):
                    mm(bA[:, 32 * gp:32 * gp + 32], LrT[:, gp, :], utok[b][:, 32 * gp:32 * gp + 32])
                for gp in range(16):
                    mm(bB[:, 32 * gp:32 * gp + 32], LiT[:, gp, :], utok[b][:, 32 * gp:32 * gp + 32])

            late = kb >= 4

            def sG():
                if late:
                    pr3 = bA.re("p (a b) -> p a b", a=16); pi3 = bB.re("p (a b) -> p a b", a=16)
                    tt(hm1[b], pr3, BRb, ALU.mult); tt(hm2[b], pi3, BIb, ALU.mult)
                    tt(prs[b], pi3, BRb, ALU.mult); tt(pis[b], pr3, BIb, ALU.mult)
                else:
                    acopy(prs[b].re("p a b -> p (a b)"), bA)
                    acopy(pis[b].re("p a b -> p (a b)"), bB)
                tt(ckvf[b], kvf[b][:, 0:256], gkv, ALU.mult, eng="pool")
                ts(ckvf[b], ckvf[b], ss1[b], 1.0, ALU.mult, ALU.mult, eng="pool")
                rope(kpef[b].re("p (a b) -> p a b", a=1), kvf[b][:, 256:320].re("p (a b) -> p a b", a=1), rc[b], rs[b],
                     tmp64[b].re("p (a b) -> p a b", a=1), 1, eng="pool")
                vcopy(ckvb[b], ckvf[b], eng="pool")
                vcopy(kpeb[b][:, 0:64], kpef[b], eng="pool")
                vcopy(kpeb[b][:, 64:128], kpef[b], eng="pool")

            def sH():
                bb = bA.cast(BF16)
                for kc in range(2):
                    tr(bb[:, kc * 128:(kc + 1) * 128], ckvb[b][:, kc * 128:(kc + 1) * 128], identb)
                tr(bb[:, 256:384], kpeb[b], identb)
                if late:
                    tt(hm1[b], hm1[b], hm2[b], ALU.subtract, eng="pool")
                    tt(hm2[b], prs[b], pis[b], ALU.add, eng="pool")
                else:
                    tt(hm1[b], prs[b], BRb, ALU.mult, eng="pool"); tt(hm2[b], pis[b], BIb, ALU.mult, eng="pool")
                    tt(hm1[b], hm1[b], hm2[b], ALU.subtract, eng="pool")
                    tt(hm2[b], pis[b], BRb, ALU.mult, eng="pool"); tt(prs[b], prs[b], BIb, ALU.mult, eng="pool")
                    tt(hm2[b], hm2[b], prs[b], ALU.add, eng="pool")

            def sI():
                bb = bA.cast(BF16)
                acopy(ckvT[b].re("p a b -> p (a b)"), bb[:, 0:256])
                acopy(KR[kbuf][:, j * 128:(j + 1) * 128], bb[:, 256:384])

            def sJ():
                for h in range(4):
                    for kc in range(2):
                        mm(bB[:, h * 128:(h + 1) * 128], W_kv[:, kc, h, 0:128], ckvT[b][:, kc, :], start=(kc == 0), stop=(kc == 1))
                for kc in range(2):
                    mm(bA, ckvT[b][:, kc, :], W_kv[:, kc, :, 128:256], start=(kc == 0), stop=(kc == 1))
                lr_ = lam128[:, 0, :]; li_ = lam128[:, 1, :]
                ld(ckv_o[t * 128:(t + 1) * 128, :], ckvf[b])
                ld(kpe_o[t * 128:(t + 1) * 128, :], kpef[b])
                red(sre[t % 2][:, 0:16], hm1[b], ALU.add)
                red(sre[t % 2][:, 16:32], hm2[b], ALU.add)
                stt(Xown[:, kb, :], xc, ohot[:, j:j + 1], Xown[:, kb, :], ALU.mult, ALU.add)
                tt(xs8[0], xc[:, 0:16], lr_, ALU.mult, eng="pool"); tt(xs8[1], xc[:, 16:32], li_, ALU.mult, eng="pool")
                tt(xs8[2], xc[:, 16:32], lr_, ALU.mult, eng="pool"); tt(xs8[3], xc[:, 0:16], li_, ALU.mult, eng="pool")
                tt(xs8[0], xs8[0], xs8[1], ALU.subtract, eng="pool"); tt(xs8[2], xs8[2], xs8[3], ALU.add, eng="pool")
                tt(xn[:, 0:16], xs8[0], sre[t % 2][:, 0:16], ALU.add, eng="pool")
                tt(xn[:, 16:32], xs8[2], sre[t % 2][:, 16:32], ALU.add, eng="pool")

            def sK():
                acopy(KT[kbuf][:, :, j * 128:(j + 1) * 128], bB.re("p (h n) -> p h n", h=4))
                acopy(Vb[kbuf][:, j, :], bA)

            def pair(p_, c_):
                def f():
                    p_(); c_()
                return f
            return [sA, pair(sB, sC), pair(sD, sE), pair(sF, sG), pair(sH, sI), pair(sJ, sK)]

        def hist_items(kb):
            items = []
            for jp in range(4):
                sa = hist_stages(kb * 8 + 2 * jp, kb % 2, 2 * jp, kb)
                sb_ = hist_stages(kb * 8 + 2 * jp + 1, kb % 2, 2 * jp + 1, kb)
                for a_, b_ in zip(sa, sb_):
                    items.append(a_); items.append(b_)
            return items

        SCp = [dbl[3], dbl[2]]
        ptb = [banks[2], banks[3]]

        def att_A(n, i, h, kbuf):
            sc = SCp[n % 2]
            for c2 in range(2):
                mm(sc[:, c2 * 512:(c2 + 1) * 512], QTn[:, i, h, :], KT[kbuf][:, h, c2 * 512:(c2 + 1) * 512], start=True, stop=False)
            for c2 in range(2):
                lo = 64 * c2
                mm(sc[:, c2 * 512:(c2 + 1) * 512], QTr[lo:lo + 64, i, h, :], KR[kbuf][lo:lo + 64, c2 * 512:(c2 + 1) * 512], start=False, stop=True)

        def att_B(n, i, h, kbuf, diag):
            u2 = n % 2; u4 = n % NS4
            sc = SCp[u2]
            col = i * 4 + h
            if diag:
                tt(sc, sc, maskd, ALU.add)
            P.op("dve", lambda e: e.tensor_reduce(out=mx[u4].ap, in_=sc.ap, axis=AX.X, op=ALU.max, negate=True), reads=_flat([sc.b]), writes=[mx[u4].b])
            tt(negm[u4], mx[u4], m_run[:, col:col + 1], ALU.min)
            tt(dm_[u4], negm[u4], m_run[:, col:col + 1], ALU.subtract)
            vcopy(m_run[:, col:col + 1], negm[u4])
            act(alp[u4], dm_[u4], AF.Exp)
            act(Pb[u2], sc, AF.Exp, bias=negm[u4], accum=rsum[u4])

        def att_CD(n, i, h, kbuf):
            u2 = n % 2
            pt_b = ptb[u2].cast(BF16)
            for kt in range(8):
                tr(pt_b[:, kt * 128:(kt + 1) * 128], Pb[u2][:, kt * 128:(kt + 1) * 128], identb)
            if n % 3 == 0:
                vcopy(PT[u2].re("p a b -> p (a b)"), pt_b)
            else:
                acopy(PT[u2].re("p a b -> p (a b)"), pt_b)

        def att_EF(n, i, h, kbuf):
            u2 = n % 2; u4 = n % NS4
            ov = ptb[u2][:, 0:128]
            for kt in range(8):
                mm(ov, PT[u2][:, kt, :], Vb[kbuf][:, kt, h * 128:(h + 1) * 128], start=(kt == 0), stop=(kt == 7))
            stt(Oacc[:, i, h, :], Oacc[:, i, h, :], alp[u4], ov, ALU.mult, ALU.add)
            col = i * 4 + h
            stt(l_run[:, col:col + 1], l_run[:, col:col + 1], alp[u4], rsum[u4], ALU.mult, ALU.add)

        for it in hist_items(0):
            it()
        nun = 0
        for kb in range(16):
            kbuf = kb % 2
            units = [(i, h) for i in range(kb, 16) for h in range(4)]
            U = len(units)
            items = hist_items(kb + 1) if kb + 1 < 16 else []
            per = (len(items) + U - 1) // U if items else 0
            ip = 0
            for q_ in range(U + 3):
                if q_ < U:
                    att_A(nun + q_, units[q_][0], units[q_][1], kbuf)
                if 0 <= q_ - 1 < U:
                    att_B(nun + q_ - 1, units[q_ - 1][0], units[q_ - 1][1], kbuf, units[q_ - 1][0] == kb)
                if 0 <= q_ - 2 < U:
                    att_CD(nun + q_ - 2, units[q_ - 2][0], units[q_ - 2][1], kbuf)
                if 0 <= q_ - 3 < U:
                    att_EF(nun + q_ - 3, units[q_ - 3][0], units[q_ - 3][1], kbuf)
                for _ in range(per):
                    if ip < len(items):
                        items[ip](); ip += 1
            while ip < len(items):
                items[ip](); ip += 1
            nun += U

        ld(ssmp_o, Xpp[:, NTP % 2, :])

        barrier()
        bump[0] = mark_after_q

        W_kv = alloc([128, 2, 4, 256], BF16, "W_kv2")
        ldw(W_kv.re("p k h e -> p k (h e)"), w_kv_d.re("(kt p) n -> p kt n", p=128))
        gkv = alloc([128, 256], F32, "gkv2"); ld(gkv, gkv_d)
        xt_s = alloc([128, D], F32, "sxt"); xT_s = alloc([128, 8, 128], BF16, "sxT")
        rc_s = alloc([128, 64], F32); rs_s = alloc([128, 64], F32)
        ckvf_s = alloc([128, 256], F32); kpef_s = alloc([128, 64], F32); tmp64_s = alloc([128, 64], F32)
        sq = alloc([128, 512], F32, "ssq"); ss_s = alloc([128, 1], F32)
        ckvb_s = alloc([128, 256], BF16); kpeb_s = alloc([128, 128], BF16)
        ckvTn = alloc([128, 2, 128], BF16, "ckvTn"); kpeTn = alloc([128, 128], BF16, "kpeTn")
        KTn = alloc([128, 4, 128], BF16, "KTn")
        cat = [alloc([128, 256], F32, "cat%d" % i) for i in range(8)]
        catb = [alloc([128, 256], BF16) for _ in range(2)]
        cpt = [alloc([128, 64], F32, "cpt%d" % i) for i in range(8)]
        cptb = [alloc([128, 128], BF16) for _ in range(2)]
        ckvTc = alloc([128, 2, 1024], BF16, "ckvTc"); kpeTc = alloc([128, 1024], BF16, "kpeTc")
        KTc = alloc([128, 4, 1024], BF16, "KTc"); Vc = alloc([128, 8, 512], BF16, "Vc"); Vn = alloc([128, 512], BF16, "Vn")
        scs = alloc([128, 1056], F32, "scs")
        Ps = alloc([128, 1152], BF16, "Ps"); PTs = alloc([128, 9, 32], BF16, "PTs")
        mxs = alloc([128, 1], F32); negs = alloc([128, 1], F32); sums = alloc([128, 1], F32)
        osb = alloc([128, 512], F32, "osb")
        isam = 16
        load_xT(xo[isam * 128:(isam + 1) * 128, :], xt_s, xT_s, banks[0])
        ld(rc_s, ropec_o[isam * 128:(isam + 1) * 128, :]); ld(rs_s, ropes_o[isam * 128:(isam + 1) * 128, :])
        for k in range(8):
            mm(banks[2][:, 0:320], xT_s[:, k, :], W_in_kv[:, k, :], start=(k == 0), stop=(k == 7))
        rmsnorm(ckvf_s, banks[2][:, 0:256], 256, gkv, sq[:, 0:256], ss_s)
        ld(ckvs_o, ckvf_s)
        rope(kpef_s.re("p (a b) -> p a b", a=1), banks[2][:, 256:320].re("p (a b) -> p a b", a=1), rc_s, rs_s, tmp64_s.re("p (a b) -> p a b", a=1), 1)
        ld(kpes_o, kpef_s)
        vcopy(ckvb_s, ckvf_s)
        mset(kpeb_s, 0.0)
        vcopy(kpeb_s[:, 0:64], kpef_s)
        bb = banks[3].cast(BF16)
        for kc in range(2):
            tr(bb[:, kc * 128:(kc + 1) * 128], ckvb_s[:, kc * 128:(kc + 1) * 128], identb)
        tr(bb[:, 256:384], kpeb_s, identb)
        acopy(ckvTn.re("p a b -> p (a b)"), bb[:, 0:256])
        acopy(kpeTn[0:64, :], bb[0:64, 256:384])
        for h in range(4):
            for kc in range(2):
                mm(banks[3][:, h * 128:(h + 1) * 128], W_kv[:, kc, h, 0:128], ckvTn[:, kc, :], start=(kc == 0), stop=(kc == 1))
        acopy(KTn.re("p a b -> p (a b)"), banks[3])
        for kp in range(2):
            mset(cptb[kp], 0.0)
        def issue_cache_loads(sq_):
            for kt in range(8):
                r0 = sq_ * 1024 + kt * 128
                ld(cat[kt], cckv_d[r0:r0 + 128, :]); ld(cpt[kt], ckpe_d[r0:r0 + 128, :])

        issue_cache_loads(0)
        for s_ in range(4):
            for kt in range(8):
                b = kt % 2
                vcopy(catb[b], cat[kt], eng="pool"); vcopy(cptb[b][:, 0:64], cpt[kt], eng="pool")
                if kt == 7 and s_ + 1 < 4:
                    issue_cache_loads(s_ + 1)
                bb = banks[1].cast(BF16)
                for kc in range(2):
                    tr(bb[:, kc * 128:(kc + 1) * 128], catb[b][:, kc * 128:(kc + 1) * 128], identb)
                tr(bb[:, 256:384], cptb[b], identb)
                acopy(ckvTc[:, :, kt * 128:(kt + 1) * 128], bb[:, 0:256].re("p (a b) -> p a b", a=2))
                acopy(kpeTc[0:64, kt * 128:(kt + 1) * 128], bb[0:64, 256:384])
            for h in range(4):
                for n in range(2):
                    for kc in range(2):
                        mm(banks[2], W_kv[:, kc, h, 0:128], ckvTc[:, kc, n * 512:(n + 1) * 512], start=(kc == 0), stop=(kc == 1))
                    acopy(KTc[:, h, n * 512:(n + 1) * 512], banks[2])
            for kt in range(8):
                for kc in range(2):
                    mm(banks[3], ckvTc[:, kc, kt * 128:(kt + 1) * 128], W_kv[:, kc, :, 128:256], start=(kc == 0), stop=(kc == 1))
                vcopy(Vc[:, kt, :], banks[3])
            for kc in range(2):
                mm(banks[3][0:32, :], ckvTn[:, kc, s_ * 32:(s_ + 1) * 32], W_kv[:, kc, :, 128:256], start=(kc == 0), stop=(kc == 1))
            vcopy(Vn[0:32, :], banks[3][0:32, :])
            qs = slice(s_ * 32, (s_ + 1) * 32)
            for h in range(4):
                for n in range(2):
                    mm(SC[0:32, n * 512:(n + 1) * 512], QTn[:, isam, h, qs], KTc[:, h, n * 512:(n + 1) * 512], start=True, stop=False)
                    mm(SC[0:32, n * 512:(n + 1) * 512], QTr[0:64, isam, h, qs], kpeTc[0:64, n * 512:(n + 1) * 512], start=False, stop=True)
                mm(banks[4][0:32, 0:32], QTn[:, isam, h, qs], KTn[:, h, qs], start=True, stop=False)
                mm(banks[4][0:32, 0:32], QTr[0:64, isam, h, qs], kpeTn[0:64, qs], start=False, stop=True)
                acopy(scs[0:32, 0:1024], SC[0:32, :])
                acopy(scs[0:32, 1024:1056], banks[4][0:32, 0:32])
                red(mxs[0:32, :], scs[0:32, :], ALU.max)
                ts(negs[0:32, :], mxs[0:32, :], -1.0, None, ALU.mult)
                mset(Ps[0:32, 1024:1152], 0.0)
                act(Ps[0:32, 0:1056], scs[0:32, :], AF.Exp, bias=negs[0:32, :], accum=sums[0:32, :])
                pt_b = banks[5].cast(BF16)
                for kt in range(9):
                    tr(pt_b[:, kt * 32:(kt + 1) * 32], Ps[0:32, kt * 128:(kt + 1) * 128], identb[0:32, 0:32])
                vcopy(PTs.re("p a b -> p (a b)"), pt_b[:, 0:288])
                ov = banks[4][0:32, 128:256]
                for kt in range(8):
                    mm(ov, PTs[:, kt, :], Vc[:, kt, h * 128:(h + 1) * 128], start=(kt == 0), stop=False)
                mm(ov, PTs[0:32, 8, :], Vn[0:32, h * 128:(h + 1) * 128], start=False, stop=True)
                recip(sums[0:32, :], sums[0:32, :])
                ts(osb[0:32, h * 128:(h + 1) * 128], ov, sums[0:32, :], None, ALU.mult)
            ld(Osam[s_ * 32:(s_ + 1) * 32, :], osb[0:32, :])

        barrier()
        bump[0] = mark_after_mixer_state

        W_glu = alloc([128, 4, D], BF16, "W_glu"); W_o = alloc([128, 8, D], BF16, "W_o")
        ldw(W_glu, w_glu_d.re("(kt p) n -> p kt n", p=128)); ldw(W_o, w_o_d.re("(kt p) n -> p kt n", p=128))
        gos = alloc([128, 512], F32, "gos"); gom = alloc([128, 512], F32, "gom"); ld(gos, gos_d); ld(gom, gom_d)
        lng = alloc([128, D], F32, "lng"); lnb = alloc([128, D], F32, "lnb")
        ld(lng, lng_d[:, 0:D]); ld(lnb, lnb_d[:, 0:D])
        btab2 = Buf("tab2")
        for nm_ in ("S_t", "C_t", "R_t"):
            v_ = alloc([128, 16, 128], F32, nm_ + "2"); v_.b = btab2
            ld(v_.re("p a b -> p (a b)"), tab_d[nm_]); ssmx[nm_] = v_
        for nm_ in ("WBre", "WBim"):
            v_ = alloc([128, 16, 128], BF16, nm_ + "2")
            ld(v_.re("p a b -> p (a b)"), tab_d[nm_]); ssmx[nm_] = v_
        for nm_, src_ in (("Cre", cre_d), ("Cimn", cimn_d), ("Dblk", dblk_d)):
            v_ = alloc([128, 16, 32], BF16, nm_)
            ldw(v_.re("p a b -> p (a b)"), src_); ssmx[nm_] = v_
        ssmx["m4"] = [[alloc([128, 256], F32) for _ in range(4)] for _ in range(2)]
        ssmx["wri"] = [[alloc([128, 256], F32) for _ in range(2)] for _ in range(2)]
        ssmx["zri"] = [[alloc([128, 2, 128], F32) for _ in range(2)] for _ in range(2)]
        ssmx["xs4"] = [alloc([128, 16], F32) for _ in range(4)]
        ssmx["zl_all"] = [alloc([128, 4, 16], F32) for _ in range(2)]
        ssmx["dmd"] = [[alloc([128, 256], F32) for _ in range(4)]] * 2
        ssmx["xrb"] = [alloc([128, 2, 128], BF16) for _ in range(2)]
        ssmx["xib"] = [alloc([128, 2, 128], BF16) for _ in range(2)]
        S0 = alloc([128, 2, 4, 16], F32, "S0"); ld(S0.re("p a b c -> p (a b c)"), s0_d)
        Sfin = alloc([128, 2, 4, 16], F32, "Sfin")
        xt = [alloc([128, D], F32, "axt%d" % i) for i in range(NB)]
        xT = [alloc([128, 8, 128], BF16, "axT%d" % i) for i in range(NB)]
        uT = [alloc([128, 4, 128], BF16, "auT%d" % i) for i in range(NB)]
        ysq = alloc([128, 512], F32, "ysq"); yt = alloc([128, 512], F32, "yt"); ysg = alloc([128, 512], F32, "ysg")
        glb = alloc([128, 512], BF16, "glb"); gT = alloc([128, 4, 128], BF16, "gT")
        sg2 = ysq; osf = yt
        mixb = alloc([128, D], BF16, "mixb"); mixT = alloc([128, 8, 128], BF16, "mixT")
        rl = alloc([128, 4], F32, "rl"); omf = alloc([128, 4, 128], F32, "omf")
        ss_a = alloc([128, 1], F32); sq = ysg
        rres = alloc([128, D], F32, "rres"); hout = alloc([128, D], F32, "hout")
        stats = alloc([128, 12], F32); mvv = alloc([128, 2], F32)

        def pre_stages(i):
            b = i % NB
            yb = banks[3] if i % 2 == 0 else banks[0]

            def p0():
                load_xT(xo[i * 128:(i + 1) * 128, :], xt[b], xT[b], banks[1])
                for kt in range(4):
                    for k in range(8):
                        mm(banks[1][:, kt * 128:(kt + 1) * 128], W_in_u[:, k, kt * 128:(kt + 1) * 128], xT[b][:, k, :], start=(k == 0), stop=(k == 7))
                acopy(uT[b].re("p a b -> p (a b)"), banks[1])

            if i < 16:
                rr_ = ssm_rounds(uT[b], (banks[4], banks[5]), lambda gp, s_, part, i=i: Xown[:, i, part * 16 + gp:part * 16 + gp + 1], 1, ybank=yb)
            else:
                rr_ = ssm_rounds(uT[b], (banks[4], banks[5]), lambda gp, s_, part: S0[:, part, s_, gp:gp + 1], 4, ybank=yb,
                                 last_out=lambda s_: (Sfin[:, 0, s_, :], Sfin[:, 1, s_, :]))
                rr_.append(lambda: ld(ssms_o, Sfin.re("p a b c -> p (a b c)")))
            return [p0] + rr_

        def post_stages(i):
            b = i % NB
            yb = banks[3] if i % 2 == 0 else banks[0]
            xt_ = xt[b]

            def q0():
                act(ysq, yb, AF.Square)
                ts(yt, ysq, 0.044715, 1.0, ALU.mult, ALU.add)
                tt(yt, yt, yb, ALU.mult)

            def q1():
                act(ysg, yt, AF.Sigmoid, scale=1.5957691216057308)
                tt(glb, ysg, yb, ALU.mult)

            def q2():
                transpose_to(gT, glb, 4, banks[2], eng="act")

            def q3():
                for nb in range(2):
                    for k in range(4):
                        mm(SC[:, nb * 512:(nb + 1) * 512], gT[:, k, :], W_glu[:, k, nb * 512:(nb + 1) * 512], start=(k == 0), stop=(k == 3))

            def q4():
                act(sg2, SC[:, 512:1024], AF.Sigmoid)
                tt(osf, sg2, SC[:, 0:512], ALU.mult)

            def q5():
                rmsnorm(mixb[:, 0:512], osf, 512, gos, sq, ss_a)

            def q6():
                if i < 16:
                    recip(rl, l_run[:, i * 4:(i + 1) * 4])
                    tt(omf, Oacc[:, i, :, :], rl.un(2).bc([128, 4, 128]), ALU.mult)
                    rmsnorm(mixb[:, 512:1024], omf.re("p a b -> p (a b)"), 512, gom, sq, ss_a)
                else:
                    rmsnorm(mixb[:, 512:1024], Osam, 512, gom, sq, ss_a)

            def q7():
                for half in range(2):
                    transpose_to(mixT[:, half * 4:(half + 1) * 4, :], mixb[:, half * 512:(half + 1) * 512], 4, banks[2], eng=("act" if half == 0 else "dve"))

            def q8():
                for nb in range(2):
                    for k in range(8):
                        mm(SC[:, nb * 512:(nb + 1) * 512], mixT[:, k, :], W_o[:, k, nb * 512:(nb + 1) * 512], start=(k == 0), stop=(k == 7))

            def q9():
                stt(rres, xt_, ALPHA, SC, ALU.mult, ALU.add)
                layernorm(hout, rres, lng, lnb, stats, mvv)
                ld(h1_d[i * 128:(i + 1) * 128, :], hout)

            return [q0, q1, q2, q3, q4, q5, q6, q7, q8, q9]

        prev_post = []
        for i in range(NOWN + 1):
            pre = pre_stages(i) if i < NOWN else []
            n_ = max(len(pre), len(prev_post))
            for k_ in range(n_):
                if k_ < len(pre):
                    pre[k_]()
                if k_ < len(prev_post):
                    prev_post[k_]()
            prev_post = post_stages(i) if i < NOWN else []

        barrier()
        bump[0] = mark_after_mem

        W_xq = alloc([128, 8, D], BF16, "W_xq"); W_xo = alloc([128, 8, D], BF16, "W_xo")
        W_xqc = [V(W_xq.ap[:, :, c * 512:(c + 1) * 512], Buf("W_xqc%d" % c)) for c in range(2)]
        for c in range(2):
            ldw(W_xqc[c], w_xq_d.re("(kt p) n -> p kt n", p=128)[:, :, c * 512:(c + 1) * 512])
        ldw(W_xo, w_xo_d.re("(kt p) n -> p kt n", p=128))
        lng = alloc([128, D], F32, "lng2"); lnb = alloc([128, D], F32, "lnb2")
        ld(lng, lng_d[:, D:2 * D]); ld(lnb, lnb_d[:, D:2 * D])
        hin = [alloc([128, D], F32, "bh%d" % i) for i in range(NB)]
        hb2 = [alloc([128, D], BF16, "bhb%d" % i) for i in range(2)]; hT2 = [alloc([128, 8, 128], BF16, "bhT%d" % i) for i in range(2)]
        qxb2 = [alloc([128, D], BF16, "qxb%d" % i) for i in range(2)]; qxT2 = [alloc([128, 8, 128], BF16, "qxT%d" % i) for i in range(2)]
        mx42 = [alloc([128, 4], F32) for _ in range(2)]; neg42 = [alloc([128, 4], F32) for _ in range(2)]; sum42 = [alloc([128, 4], F32) for _ in range(2)]
        Px2 = [alloc([128, 4, 256], BF16, "Px%d" % i) for i in range(2)]; PxT2 = [alloc([128, 8, 128], BF16, "PxT%d" % i) for i in range(2)]
        oxb2 = [alloc([128, D], BF16, "oxb%d" % i) for i in range(2)]; oxT2 = [alloc([128, 8, 128], BF16, "oxT%d" % i) for i in range(2)]
        rres2 = [alloc([128, D], F32, "brres%d" % i) for i in range(2)]; hout2 = [alloc([128, D], F32, "bhout%d" % i) for i in range(2)]
        stats2 = [alloc([128, 12], F32) for _ in range(2)]; mvv2 = [alloc([128, 2], F32) for _ in range(2)]
        hb_ = hb2[0]; hT = hT2[0]; qxb = qxb2[0]; qxT = qxT2[0]; mx4 = mx42[0]; neg4 = neg42[0]; sum4 = sum42[0]
        Px = Px2[0]; PxT = PxT2[0]; oxb = oxb2[0]; oxT = oxT2[0]; rres = rres2[0]; hout = hout2[0]; stats = stats2[0]; mvv = mvv2[0]
        cmk = [alloc([128, D], F32, "cmk%d" % i) for i in range(2)]
        cmkb = [alloc([128, D], BF16, "cmkb%d" % i) for i in range(1)]
        mkTs4 = [alloc([128, 4, 2, 256], BF16, "mkTs%d" % i) for i in range(4)]; mvs4 = [alloc([128, 2, D], BF16, "mvs%d" % i) for i in range(4)]
        scx = alloc([128, 8], F32, "scx"); oxs = alloc([128, D], BF16, "oxs")

        def xa_stages(i, p):
            A_ = banks[p]; C_ = dbl[1 + p]
            Ab = A_.cast(BF16)

            def tr8(src):
                for k in range(8):
                    tr(Ab[:, k * 128:(k + 1) * 128], src[:, k * 128:(k + 1) * 128], identb)

            def ev8(dst):
                acopy(dst[:, 0:4, :], Ab[:, 0:512].re("p (k n) -> p k n", k=4))
                vcopy(dst[:, 4:8, :], Ab[:, 512:1024].re("p (k n) -> p k n", k=4))

            def t0():
                ld(hin[p], h1_d[i * 128:(i + 1) * 128, :])
                vcopy(hb2[p], hin[p], eng="pool")

            def t3():
                for nb in range(2):
                    for k in range(8):
                        mm(C_[:, nb * 512:(nb + 1) * 512], hT2[p][:, k, :], W_xqc[nb][:, k, :], start=(k == 0), stop=(k == 7))

            def t4():
                for nb in range(2):
                    P.op("act", lambda e, nb=nb: e.mul(out=qxb2[p].ap[:, nb * 512:(nb + 1) * 512], in_=C_.ap[:, nb * 512:(nb + 1) * 512], mul=X_SCALE),
                         reads=[C_.b], writes=[qxb2[p].b])

            def t7():
                for h in range(4):
                    for et in range(2):
                        mm(C_[:, h * 256:(h + 1) * 256], qxT2[p][:, h * 2 + et, :], mkT[:, h, et, :], start=(et == 0), stop=(et == 1))

            def t8():
                red(mx42[p], C_.re("p (h m) -> p h m", h=4), ALU.max)
                ts(neg42[p], mx42[p], -1.0, None, ALU.mult)
                for h in range(4):
                    act(Px2[p][:, h, :], C_[:, h * 256:(h + 1) * 256], AF.Exp, bias=neg42[p][:, h:h + 1], accum=sum42[p][:, h:h + 1])

            def t11():
                for h in range(4):
                    for mt in range(2):
                        mm(C_[:, h * 256:(h + 1) * 256], PxT2[p][:, h * 2 + mt, :], mvb[:, mt, h * 256:(h + 1) * 256], start=(mt == 0), stop=(mt == 1))

            def t12():
                recip(sum42[p], sum42[p])
                tt(oxb2[p].re("p (h e) -> p h e", h=4), C_.re("p (h e) -> p h e", h=4), sum42[p].un(2).bc([128, 4, 256]), ALU.mult)

            def t15():
                for nb in range(2):
                    for k in range(8):
                        mm(C_[:, nb * 512:(nb + 1) * 512], oxT2[p][:, k, :], W_xo[:, k, nb * 512:(nb + 1) * 512], start=(k == 0), stop=(k == 7))

            def t16():
                stt(rres2[p], hin[p], ALPHA, C_, ALU.mult, ALU.add)
                layernorm(hout2[p], rres2[p], lng, lnb, stats2[p], mvv2[p])
                ld(h2_d[i * 128:(i + 1) * 128, :], hout2[p])

            return [t0, lambda: tr8(hb2[p]), lambda: ev8(hT2[p]), t3, t4, lambda: tr8(qxb2[p]), lambda: ev8(qxT2[p]), t7, t8,
                    lambda: tr8(Px2[p].re("p a b -> p (a b)")), lambda: ev8(PxT2[p]), t11, t12, lambda: tr8(oxb2[p]), lambda: ev8(oxT2[p]), t15, t16]

        def prep_items():
            items = []
            for s_ in range(4):
                for mt in range(2):
                    r0 = s_ * 256 + mt * 128

                    def l0(r0=r0):
                        ld(cmk[0], cmk_d[r0:r0 + 128, :]); ld(cmk[1], cmv_d[r0:r0 + 128, :])

                    def l1(s_=s_, mt=mt):
                        vcopy(cmkb[0], cmk[0], eng="pool")
                        vcopy(mvs4[s_][:, mt, :], cmk[1], eng="pool")

                    def l2(s_=s_, mt=mt):
                        for half in range(2):
                            bb = banks[6 + half].cast(BF16)
                            for k in range(4):
                                kk = half * 4 + k
                                tr(bb[:, k * 128:(k + 1) * 128], cmkb[0][:, kk * 128:(kk + 1) * 128], identb)

                    def l3(s_=s_, mt=mt):
                        for half in range(2):
                            bb = banks[6 + half].cast(BF16)
                            acopy(mkTs4[s_][:, half * 2:half * 2 + 2, :, mt * 128:(mt + 1) * 128].re("p h e m -> p (h e) m"), bb[:, 0:512].re("p (k m) -> p k m", k=4))

                    items += [l0, l1, l2, l3]
            return items

        pitems = prep_items()
        pi_ = 0
        slot = 0
        for ip in range(8):
            sa = xa_stages(2 * ip, 0); sb_ = xa_stages(2 * ip + 1, 1)
            for a_, b_ in zip(sa, sb_):
                a_(); b_()
                slot += 1
                if slot % 4 == 0 and pi_ < len(pitems):
                    pitems[pi_](); pi_ += 1
        while pi_ < len(pitems):
            pitems[pi_](); pi_ += 1

        for i in range(16, NOWN):
            b = i % NB
            ld(hin[b], h1_d[i * 128:(i + 1) * 128, :])
            vcopy(hb_, hin[b], eng="pool")
            for half in range(2):
                transpose_to(hT[:, half * 4:(half + 1) * 4, :], hb_[:, half * 512:(half + 1) * 512], 4, banks[0], eng=("act" if half == 0 else "dve"))
            for nb in range(2):
                for k in range(8):
                    mm(banks[1 + nb], hT[:, k, :], W_xqc[nb][:, k, :], start=(k == 0), stop=(k == 7))
                P.op("act", lambda e, nb=nb: e.mul(out=qxb.ap[:, nb * 512:(nb + 1) * 512], in_=banks[1 + nb].ap, mul=X_SCALE), reads=[banks[1 + nb].b], writes=[qxb.b])
            for half in range(2):
                transpose_to(qxT[:, half * 4:(half + 1) * 4, :], qxb[:, half * 512:(half + 1) * 512], 4, banks[3], eng=("act" if half == 0 else "dve"))
            if i < 16:
                for h in range(4):
                    for et in range(2):
                        mm(SC[:, h * 256:(h + 1) * 256], qxT[:, h * 2 + et, :], mkT[:, h, et, :], start=(et == 0), stop=(et == 1))
                red(mx4, SC.re("p (h m) -> p h m", h=4), ALU.max)
                ts(neg4, mx4, -1.0, None, ALU.mult)
                for h in range(4):
                    act(Px[:, h, :], SC[:, h * 256:(h + 1) * 256], AF.Exp, bias=neg4[:, h:h + 1], accum=sum4[:, h:h + 1])
                for half in range(2):
                    transpose_to(PxT[:, half * 4:(half + 1) * 4, :], Px.re("p a b -> p (a b)")[:, half * 512:(half + 1) * 512], 4, banks[4], eng=("act" if half == 0 else "dve"))
                for h in range(4):
                    for mt in range(2):
                        mm(SC[:, h * 256:(h + 1) * 256], PxT[:, h * 2 + mt, :], mvb[:, mt, h * 256:(h + 1) * 256], start=(mt == 0), stop=(mt == 1))
                recip(sum4, sum4)
                tt(oxb.re("p (h e) -> p h e", h=4), SC.re("p (h e) -> p h e", h=4), sum4.un(2).bc([128, 4, 256]), ALU.mult)
            else:
                for s_ in range(4):
                    qs = slice(s_ * 32, (s_ + 1) * 32)
                    mkTs = mkTs4[s_]; mvs = mvs4[s_]
                    for h in range(4):
                        for et in range(2):
                            mm(SC[0:32, h * 256:(h + 1) * 256], qxT[:, h * 2 + et, qs], mkTs[:, h, et, :], start=(et == 0), stop=(et == 1))
                    red(mx4[0:32, :], SC[0:32, :].re("p (h m) -> p h m", h=4), ALU.max)
                    ts(neg4[0:32, :], mx4[0:32, :], -1.0, None, ALU.mult)
                    for h in range(4):
                        act(Px[0:32, h, :], SC[0:32, h * 256:(h + 1) * 256], AF.Exp, bias=neg4[0:32, h:h + 1], accum=sum4[0:32, h:h + 1])
                    bb = banks[4].cast(BF16)
                    for k in range(8):
                        tr(bb[:, k * 32:(k + 1) * 32], Px.re("p a b -> p (a b)")[0:32, k * 128:(k + 1) * 128], identb[0:32, 0:32])
                    vcopy(PxT[:, :, 0:32], bb[:, 0:256].re("p (k q) -> p k q", k=8))
                    for h in range(4):
                        for mt in range(2):
                            mm(SC[0:32, h * 256:(h + 1) * 256], PxT[:, h * 2 + mt, 0:32], mvs[:, mt, h * 256:(h + 1) * 256], start=(mt == 0), stop=(mt == 1))
                    recip(sum4[0:32, :], sum4[0:32, :])
                    tt(oxs[0:32, :].re("p (h e) -> p h e", h=4), SC[0:32, :].re("p (h e) -> p h e", h=4), sum4[0:32, :].un(2).bc([32, 4, 256]), ALU.mult)
                    ld(oxb[s_ * 32:(s_ + 1) * 32, :], oxs[0:32, :])
            for half in range(2):
                transpose_to(oxT[:, half * 4:(half + 1) * 4, :], oxb[:, half * 512:(half + 1) * 512], 4, banks[5], eng=("act" if half == 0 else "dve"))
            for nb in range(2):
                for k in range(8):
                    mm(banks[1 + nb], oxT[:, k, :], W_xo[:, k, nb * 512:(nb + 1) * 512], start=(k == 0), stop=(k == 7))
            for nb in range(2):
                stt(rres[:, nb * 512:(nb + 1) * 512], hin[b][:, nb * 512:(nb + 1) * 512], ALPHA, banks[1 + nb], ALU.mult, ALU.add)
            layernorm(hout, rres, lng, lnb, stats, mvv)
            ld(h2_d[i * 128:(i + 1) * 128, :], hout)

        barrier()
        bump[0] = mark_after_mem

        W1 = alloc([128, 8, 4096], BF16, "W1"); W2 = alloc([128, 32, D], BF16, "W2")
        W1c = [V(W1.ap[:, :, c * 512:(c + 1) * 512], Buf("W1c%d" % c)) for c in range(8)]
        W2c = [V(W2.ap[:, c * 8:(c + 1) * 8, :], Buf("W2c%d" % c)) for c in range(4)]
        w1v = w_ff1_d.re("(kt p) n -> p kt n", p=128); w2v = w_ff2_d.re("(kt p) n -> p kt n", p=128)
        for c in range(8):
            ldw(W1c[c], w1v[:, :, c * 512:(c + 1) * 512])
        for c in range(4):
            ldw(W2c[c], w2v[:, c * 8:(c + 1) * 8, :])
        lng = alloc([128, D], F32, "lng3"); lnb = alloc([128, D], F32, "lnb3")
        ld(lng, lng_d[:, 2 * D:3 * D]); ld(lnb, lnb_d[:, 2 * D:3 * D])
        hin = alloc([128, D], F32, "ch")
        hb_ = alloc([128, D], BF16, "chb"); hT = alloc([128, 8, 512], BF16, "chT")
        zr = [alloc([128, 512], F32, "zr%d" % i) for i in range(2)]; zT = alloc([128, 32, 512], BF16, "zT")
        rres = alloc([128, D], F32, "crres"); hout = alloc([128, D], F32, "chout")
        stats = alloc([128, 12], F32); mvv = alloc([128, 2], F32)
        groups = [list(range(g_ * 4, g_ * 4 + 4)) for g_ in range(4)] + [[16]]
        for grp in groups:
            nt = len(grp)
            for q_, i in enumerate(grp):
                ld(hin, h2_d[i * 128:(i + 1) * 128, :])
                vcopy(hb_, hin, eng="pool")
                for half in range(2):
                    transpose_to(hT[:, half * 4:(half + 1) * 4, q_ * 128:(q_ + 1) * 128], hb_[:, half * 512:(half + 1) * 512], 4, banks[0], eng=("act" if half == 0 else "dve"))
            W_ = nt * 128
            for f in range(32):
                bk = banks[1 + f % 2]
                for k in range(8):
                    mm(bk[:, 0:W_], W1c[f // 4][:, k, (f % 4) * 128:(f % 4 + 1) * 128], hT[:, k, 0:W_], start=(k == 0), stop=(k == 7))
                act(zr[f % 2][:, 0:W_], bk[:, 0:W_], AF.Relu)
                tt(zT[:, f, 0:W_], zr[f % 2][:, 0:W_], zr[f % 2][:, 0:W_], ALU.mult)
            for q_, i in enumerate(grp):
                for nb in range(2):
                    for f in range(32):
                        mm(SC[:, nb * 512:(nb + 1) * 512], zT[:, f, q_ * 128:(q_ + 1) * 128], W2c[f // 8][:, f % 8, nb * 512:(nb + 1) * 512], start=(f == 0), stop=(f == 31))
                ld(hin, h2_d[i * 128:(i + 1) * 128, :])
                stt(rres, hin, ALPHA, SC, ALU.mult, ALU.add)
                layernorm(hout, rres, lng, lnb, stats, mvv)
                ld(y_o[i * 128:(i + 1) * 128, :], hout)

        P.emit(st)
    return nc


def _lay_gp(a):
    sh = a.shape[2:]
    n = len(sh)
    return np.ascontiguousarray(a.reshape(16, 2, 64, *sh).transpose(1, 2, 0, *range(3, 3 + n)).reshape(128, 16, *sh))


def _bc(v, n=128):
    return np.ascontiguousarray(np.broadcast_to(np.asarray(v, np.float32).reshape(1, -1), (n, np.asarray(v).size)))


def _rope_tables(pos):
    inv = (10000.0 ** (-np.arange(32, dtype=np.float32) / 32)).astype(np.float32)
    ang = pos.astype(np.float32)[:, None] * inv[None, :]
    c = np.cos(ang).astype(np.float32)
    s = np.sin(ang).astype(np.float32)
    return np.concatenate([c, c], 1), np.concatenate([-s, s], 1)


_NC_CACHE = {}


def kernel(x_prompt, x_sample, mem_prompt, cache_mla_ckv, cache_mla_kpe, state_ssm_re, state_ssm_im,
           cache_mem_k, cache_mem_v, w_in, g_q, w_q_up, g_kv, w_kv_up, a_re, a_im, b_re, b_im, c_re, c_im,
           d_skip, log_dt, w_glu, g_out_ssm, g_out_mla, w_o, w_xq, w_xk, w_xv, w_xo, w_ff1, w_ff2, ln_g, ln_b):
    f = lambda a: np.ascontiguousarray(np.asarray(a, dtype=np.float32))
    x_prompt = f(x_prompt); x_sample = f(x_sample)
    xp = x_prompt[0]
    cre = np.zeros((128, 16, 32), np.float32); cimn = np.zeros((128, 16, 32), np.float32)
    cr = f(c_re)[0].reshape(16, 2, 16, 64); ci = f(c_im)[0].reshape(16, 2, 16, 64)
    for g2 in range(2):
        cre[g2 * 64:(g2 + 1) * 64, :, g2 * 16:(g2 + 1) * 16] = cr[:, g2].transpose(2, 0, 1)
        cimn[g2 * 64:(g2 + 1) * 64, :, g2 * 16:(g2 + 1) * 16] = -ci[:, g2].transpose(2, 0, 1)
    dblk = np.zeros((128, 16, 32), np.float32)
    dd = f(d_skip)[0].reshape(512)
    for gp in range(16):
        for c in range(32):
            ch = gp * 32 + c
            dblk[ch % 128, gp, c] = dd[ch]
    pos_p = np.arange(16384)
    rcp, rsp = _rope_tables(pos_p)
    common = {
        "xp": xp, "mem": f(mem_prompt)[0], "ident": np.eye(128, dtype=np.float32),
        "w_in": f(w_in)[0], "w_q": f(w_q_up)[0].reshape(384, 768), "w_kv": f(w_kv_up)[0].reshape(256, 1024),
        "w_glu": f(w_glu)[0], "w_o": f(w_o)[0], "w_xq": f(w_xq)[0].reshape(D, D), "w_xk": f(w_xk)[0].reshape(D, D),
        "w_xv": f(w_xv)[0].reshape(D, D), "w_xo": f(w_xo)[0].reshape(D, D), "w_ff1": f(w_ff1)[0], "w_ff2": f(w_ff2)[0],
        "gq": _bc(f(g_q)[0]), "gkv": _bc(f(g_kv)[0]), "gos": _bc(f(g_out_ssm)[0]), "gom": _bc(f(g_out_mla)[0]),
        "lng": _bc(f(ln_g)[0].reshape(-1)), "lnb": _bc(f(ln_b)[0].reshape(-1)),
        "ropec_p": rcp, "ropes_p": rsp,
        "ar": _lay_gp(f(a_re)[0]), "ai": _lay_gp(f(a_im)[0]),
        "ldt": _lay_gp(np.ascontiguousarray(np.broadcast_to(f(log_dt)[0][:, None], (32, 64)))),
        "bre": _lay_gp(f(b_re)[0]).reshape(128, 256), "bim": _lay_gp(f(b_im)[0]).reshape(128, 256),
        "jj": _bc(np.arange(1, 129, dtype=np.float32)), "jj2": _bc(127.0 - np.arange(128, dtype=np.float32)),
        "cre": cre.reshape(128, 512), "cimn": cimn.reshape(128, 512), "dblk": dblk.reshape(128, 512),
    }
    qi = np.arange(128)[:, None]
    in_maps = []
    for c in range(NCORES):
        tiles = [8 * i + c for i in range(16)]
        xo = np.concatenate([xp[t * 128:(t + 1) * 128] for t in tiles] + [x_sample[4 * c:4 * c + 4].reshape(128, D)], 0)
        pos_o = np.concatenate([np.arange(t * 128, (t + 1) * 128) for t in tiles] + [np.tile(1024 + np.arange(32), 4)])
        rco, rso = _rope_tables(pos_o)
        kj = np.arange(1024)[None, :]
        vis = ((kj // 128) < c) | (((kj // 128) == c) & (((kj % 128) // 64) <= (qi // 64)))
        maskd = np.where(vis, 0.0, NEG).astype(np.float32)
        onehot = np.zeros((128, 8), np.float32); onehot[:, c] = 1.0
        s0 = np.stack([_lay_gp(f(state_ssm_re)[0, 4 * c + s]) for s in range(4)], 1)
        s0i = np.stack([_lay_gp(f(state_ssm_im)[0, 4 * c + s]) for s in range(4)], 1)
        m = dict(common)
        m.update({
            "xo": np.ascontiguousarray(xo), "ropec_o": rco, "ropes_o": rso, "maskd": maskd, "onehot": onehot,
            "s0": np.ascontiguousarray(np.stack([s0, s0i], 1).reshape(128, 128)),
            "cckv": f(cache_mla_ckv)[0, 4 * c:4 * c + 4].reshape(4096, 256),
            "ckpe": f(cache_mla_kpe)[0, 4 * c:4 * c + 4].reshape(4096, 64),
            "cmk": f(cache_mem_k)[0, 4 * c:4 * c + 4].reshape(1024, D),
            "cmv": f(cache_mem_v)[0, 4 * c:4 * c + 4].reshape(1024, D),
        })
        in_maps.append(m)
    if "nc" not in _NC_CACHE:
        _NC_CACHE["nc"] = build_nc()
    res = run_bass_kernel_spmd(_NC_CACHE["nc"], in_maps, core_ids=list(range(NCORES)))
    R = res.results
    y_p = np.zeros((1, 16384, D), np.float32); y_s = np.zeros((32, 32, D), np.float32)
    ckv_s = np.zeros((1, 32, 32, 256), np.float32); kpe_s = np.zeros((1, 32, 32, 64), np.float32)
    sre_s = np.zeros((1, 32, 32, 64), np.float32); sim_s = np.zeros((1, 32, 32, 64), np.float32)

    def unlay(a):
        return a.reshape(2, 64, 16).transpose(2, 0, 1).reshape(32, 64)

    for c in range(NCORES):
        yo = R[c]["y_o"]
        for i in range(16):
            t = 8 * i + c
            y_p[0, t * 128:(t + 1) * 128] = yo[i * 128:(i + 1) * 128]
        y_s[4 * c:4 * c + 4] = yo[16 * 128:].reshape(4, 32, D)
        ckv_s[0, 4 * c:4 * c + 4] = R[c]["ckvs_o"].reshape(4, 32, 256)
        kpe_s[0, 4 * c:4 * c + 4] = R[c]["kpes_o"].reshape(4, 32, 64)
        sf = R[c]["ssms_o"].reshape(128, 2, 4, 16)
        for s in range(4):
            sre_s[0, 4 * c + s] = unlay(sf[:, 0, s, :])
            sim_s[0, 4 * c + s] = unlay(sf[:, 1, s, :])
    r0 = R[0]
    ckv_p = r0["ckv_o"].reshape(1, 1, 16384, 256); kpe_p = r0["kpe_o"].reshape(1, 1, 16384, 64)
    sp = r0["ssmp_o"]
    sre_p = unlay(sp[:, 0:16]).reshape(1, 1, 32, 64); sim_p = unlay(sp[:, 16:32]).reshape(1, 1, 32, 64)
    mk_p = r0["memk_o"].reshape(1, 1, 256, 4, 256); mv_p = r0["memv_o"].reshape(1, 1, 256, 4, 256)
    return (y_p, y_s, ckv_p, kpe_p, sre_p, sim_p, mk_p, mv_p, ckv_s, kpe_s, sre_s, sim_s)
```

```python
import math
from contextlib import ExitStack

import numpy as np
import concourse.bass as bass
import concourse.mybir as mybir
from concourse.bass_utils import run_bass_kernel_spmd

F32 = mybir.dt.float32
BF16 = mybir.dt.bfloat16
I32 = mybir.dt.int32
AF = mybir.ActivationFunctionType
ALU = mybir.AluOpType
AX = mybir.AxisListType

NCORES = 8
D = 1024
NTP = 128
NOWN = 17
EPS = 1e-5
ALPHA = 2.0 ** 0.25
MLA_SCALE = 192.0 ** -0.5
X_SCALE = 256.0 ** -0.5
TWO_PI = 2.0 * math.pi
NEG = -1e30
NRING = 24
COMPUTE = ("pe", "act", "dve", "pool")


class Buf:
    __slots__ = ("name", "lw", "rd")

    def __init__(self, name=""):
        self.name = name
        self.lw = None
        self.rd = []


def _flat(bs):
    out = []
    for b in bs:
        if isinstance(b, (tuple, list)):
            out.extend(b)
        else:
            out.append(b)
    return out


class Op:
    __slots__ = ("eng", "fn", "deps", "isdma", "flag", "cnt", "ring", "n")

    def __init__(self, eng, fn, isdma):
        self.eng = eng
        self.fn = fn
        self.isdma = isdma
        self.deps = set()
        self.flag = False
        self.cnt = 0
        self.ring = None
        self.n = 0


class Prog:
    def __init__(self, nc):
        self.nc = nc
        self.ops = {e: [] for e in ("pe", "act", "dve", "pool", "sp")}
        self.ndma = {e: 0 for e in self.ops}
        self.allops = []
        self.floor = []
        self.dmas_since = []

    def _add(self, eng, fn, reads, writes, isdma):
        op = Op(eng, fn, isdma)
        reads = _flat(reads)
        writes = _flat(writes)
        for f in self.floor:
            op.deps.add(f)
        for b in reads:
            if b.lw is not None:
                op.deps.add(b.lw)
        for b in writes:
            if b.lw is not None:
                op.deps.add(b.lw)
            for r in b.rd:
                op.deps.add(r)
        for b in reads:
            b.rd.append(op)
        for b in writes:
            b.lw = op
            b.rd = []
        op.deps.discard(op)
        if isdma:
            op.n = self.ndma[eng]
            self.ndma[eng] += 1
            self.dmas_since.append(op)
        self.ops[eng].append(op)
        self.allops.append(op)
        return op

    def op(self, eng, fn, reads=(), writes=()):
        return self._add(eng, fn, reads, writes, False)

    def dma(self, eng, fn, reads=(), writes=()):
        return self._add(eng, fn, reads, writes, True)

    def barrier(self, fn):
        op = Op("pool", fn, False)
        for f in self.floor:
            op.deps.add(f)
        for e in COMPUTE:
            for o in reversed(self.ops[e]):
                if not o.isdma:
                    op.deps.add(o)
                    break
        for o in self.dmas_since:
            op.deps.add(o)
        self.dmas_since = []
        self.ops["pool"].append(op)
        self.allops.append(op)
        self.floor = [op]

    def emit(self, stack):
        nc = self.nc
        for op in self.allops:
            for d in op.deps:
                if d.eng == "pe" and op.eng == "pe" and not d.isdma:
                    continue
                d.flag = True
        sems = {e: stack.enter_context(nc.semaphore("s_" + e)) for e in COMPUTE}
        rings = {}
        for e in self.ops:
            if self.ndma[e]:
                rings[e] = [stack.enter_context(nc.semaphore("r_%s_%d" % (e, i))) for i in range(NRING)]
        for e in self.ops:
            c = 0
            for op in self.ops[e]:
                if op.isdma:
                    op.ring = rings[e][op.n % NRING]
                    op.cnt = 16 * (op.n // NRING + 1)
                elif op.flag:
                    c += 1
                    op.cnt = c
        block = stack.enter_context(nc.Block())

        def run(e, h):
            waited = {}

            def wait(sem, val):
                k = id(sem)
                if waited.get(k, 0) >= val:
                    return
                waited[k] = val
                h.wait_ge(sem, val)

            for op in self.ops[e]:
                for d in op.deps:
                    if d.isdma:
                        wait(d.ring, d.cnt)
                    else:
                        if d.eng == "pe" and e == "pe":
                            continue
                        wait(sems[d.eng], d.cnt)
                if op.isdma and op.n >= NRING:
                    wait(op.ring, op.cnt - 16)
                ins = op.fn(h)
                if op.isdma:
                    ins.then_inc(op.ring, 16)
                elif op.flag:
                    ins.then_inc(sems[e], 1)
            if e in rings:
                n = self.ndma[e]
                for i in range(min(n, NRING)):
                    last = ((n - 1 - i) // NRING) * NRING + i
                    wait(rings[e][i], 16 * (last // NRING + 1))

        @block.tensor
        def _(h):
            run("pe", h)

        @block.scalar
        def _(h):
            run("act", h)

        @block.vector
        def _(h):
            run("dve", h)

        @block.gpsimd
        def _(h):
            run("pool", h)

        @block.sync
        def _(h):
            run("sp", h)


class V:
    __slots__ = ("ap", "b")

    def __init__(self, ap, b):
        self.ap = ap
        self.b = b

    def __getitem__(self, idx):
        return V(self.ap[idx], self.b)

    def re(self, pat, **kw):
        return V(self.ap.rearrange(pat, **kw), self.b)

    def cast(self, dt):
        return V(self.ap.bitcast(dt), self.b)

    def bc(self, shape):
        return V(self.ap.broadcast_to(shape), self.b)

    def un(self, ax):
        return V(self.ap.unsqueeze(ax), self.b)


def build_nc():
    nc = bass.Bass("TRN2", target_bir_lowering=False)
    P = Prog(nc)
    dram = {}

    def din(name, shape, dt=F32):
        t = nc.dram_tensor(name, list(shape), dt, kind="ExternalInput")
        dram[name] = V(t.ap(), Buf(name))
        return dram[name]

    def dout(name, shape):
        t = nc.dram_tensor(name, list(shape), F32, kind="ExternalOutput")
        dram[name] = V(t.ap(), Buf(name))
        return dram[name]

    def dscr(name, shape, dt=F32):
        t = nc.dram_tensor(name, list(shape), dt)
        return V(t.ap(), Buf(name))

    xp = din("xp", [NTP * 128, D])
    xo = din("xo", [NOWN * 128, D])
    mem = din("mem", [256, D])
    ident_d = din("ident", [128, 128])
    w_in_d = din("w_in", [D, 1216]); w_q_d = din("w_q", [384, 768]); w_kv_d = din("w_kv", [256, 1024])
    w_glu_d = din("w_glu", [512, 1024]); w_o_d = din("w_o", [D, D]); w_xq_d = din("w_xq", [D, D])
    w_xk_d = din("w_xk", [D, D]); w_xv_d = din("w_xv", [D, D]); w_xo_d = din("w_xo", [D, D])
    w_ff1_d = din("w_ff1", [D, 4096]); w_ff2_d = din("w_ff2", [4096, D])
    gq_d = din("gq", [128, 384]); gkv_d = din("gkv", [128, 256]); gos_d = din("gos", [128, 512]); gom_d = din("gom", [128, 512])
    lng_d = din("lng", [128, 3 * D]); lnb_d = din("lnb", [128, 3 * D])
    ropec_p = din("ropec_p", [NTP * 128, 64]); ropes_p = din("ropes_p", [NTP * 128, 64])
    ropec_o = din("ropec_o", [NOWN * 128, 64]); ropes_o = din("ropes_o", [NOWN * 128, 64])
    maskd_d = din("maskd", [128, 1024]); onehot_d = din("onehot", [128, 8])
    ar_d = din("ar", [128, 16]); ai_d = din("ai", [128, 16]); ldt_d = din("ldt", [128, 16])
    bre_d = din("bre", [128, 256]); bim_d = din("bim", [128, 256]); jj_d = din("jj", [128, 128]); jj2_d = din("jj2", [128, 128])
    cre_d = din("cre", [128, 512]); cimn_d = din("cimn", [128, 512]); dblk_d = din("dblk", [128, 512])
    s0_d = din("s0", [128, 128])
    cckv_d = din("cckv", [4 * 1024, 256]); ckpe_d = din("ckpe", [4 * 1024, 64])
    cmk_d = din("cmk", [4 * 256, D]); cmv_d = din("cmv", [4 * 256, D])

    y_o = dout("y_o", [NOWN * 128, D])
    ckv_o = dout("ckv_o", [NTP * 128, 256]); kpe_o = dout("kpe_o", [NTP * 128, 64])
    ssmp_o = dout("ssmp_o", [128, 32])
    memk_o = dout("memk_o", [256, D]); memv_o = dout("memv_o", [256, D])
    ckvs_o = dout("ckvs_o", [128, 256]); kpes_o = dout("kpes_o", [128, 64]); ssms_o = dout("ssms_o", [128, 128])

    h1_d = dscr("h1_d", [NOWN * 128, D]); h2_d = dscr("h2_d", [NOWN * 128, D])
    tab_d = {n_: dscr("tab_" + n_, [128, 2048]) for n_ in ("S_t", "C_t", "R_t")}
    tab_d["WBre"] = dscr("tab_WBre", [128, 2048], BF16); tab_d["WBim"] = dscr("tab_WBim", [128, 2048], BF16)

    with ExitStack() as st:
        ARENA_N = 52000
        arena = st.enter_context(nc.sbuf_tensor("arena", [128, ARENA_N], F32))
        bump = [0]

        def alloc(shape, dt=F32, name=""):
            n = 1
            for s_ in shape[1:]:
                n *= s_
            words = (n * (2 if dt == BF16 else 4) + 3) // 4
            words = (words + 7) // 8 * 8
            off = bump[0]
            bump[0] += words
            assert bump[0] <= ARENA_N, ("SBUF arena overflow", name, bump[0])
            ap = arena[:, off:off + words]
            if dt != F32:
                ap = ap.bitcast(dt)
            ap = ap[:, 0:n]
            if len(shape) > 2:
                names = " ".join("d%d" % i for i in range(len(shape) - 1))
                kw = {"d%d" % i: shape[i + 1] for i in range(len(shape) - 1)}
                ap = ap.rearrange("p (%s) -> p %s" % (names, names), **kw)
            return V(ap, Buf(name))

        banks = []
        dbl = []
        for d_ in range(4):
            t = st.enter_context(nc.psum_tensor("dbank%d" % d_, [128, 1024], F32))
            b0 = Buf("bank%d" % (2 * d_)); b1 = Buf("bank%d" % (2 * d_ + 1))
            banks.append(V(t[:, 0:512], b0)); banks.append(V(t[:, 512:1024], b1))
            dbl.append(V(t[:], (b0, b1)))
        SC = dbl[3]
        SCs = [dbl[3], dbl[0]]

        def mm(out, lhsT, rhs, start=True, stop=True):
            P.op("pe", lambda e: e.matmul(out.ap, lhsT=lhsT.ap, rhs=rhs.ap, start=start, stop=stop),
                 reads=[lhsT.b, rhs.b], writes=[out.b])

        def tr(out, in_, idt):
            P.op("pe", lambda e: e.transpose(out=out.ap, in_=in_.ap, identity=idt.ap), reads=[in_.b, idt.b], writes=[out.b])

        def act(out, in_, func, bias=None, scale=None, accum=None):
            kw = {}
            rd = [in_.b]
            wr = [out.b]
            if bias is not None:
                if isinstance(bias, V):
                    kw["bias"] = bias.ap; rd.append(bias.b)
                else:
                    kw["bias"] = bias
            if scale is not None:
                if isinstance(scale, V):
                    kw["scale"] = scale.ap; rd.append(scale.b)
                else:
                    kw["scale"] = scale
            if accum is not None:
                kw["accum_out"] = accum.ap; wr.append(accum.b)
            P.op("act", lambda e: e.activation(out=out.ap, in_=in_.ap, func=func, **kw), reads=rd, writes=wr)

        def acopy(out, in_):
            P.op("act", lambda e: e.copy(out=out.ap, in_=in_.ap), reads=[in_.b], writes=[out.b])

        def vcopy(out, in_, eng="dve"):
            P.op(eng, lambda e: e.tensor_copy(out=out.ap, in_=in_.ap), reads=[in_.b], writes=[out.b])

        def tt(out, in0, in1, op, eng="dve"):
            P.op(eng, lambda e: e.tensor_tensor(out=out.ap, in0=in0.ap, in1=in1.ap, op=op), reads=[in0.b, in1.b], writes=[out.b])

        def ts(out, in0, s1, s2, op0, op1=None, eng="dve"):
            rd = [in0.b]
            a1 = s1
            a2 = s2
            if isinstance(s1, V):
                a1 = s1.ap; rd.append(s1.b)
            if isinstance(s2, V):
                a2 = s2.ap; rd.append(s2.b)
            if op1 is None:
                P.op(eng, lambda e: e.tensor_scalar(out=out.ap, in0=in0.ap, scalar1=a1, scalar2=None, op0=op0), reads=rd, writes=[out.b])
            else:
                P.op(eng, lambda e: e.tensor_scalar(out=out.ap, in0=in0.ap, scalar1=a1, scalar2=a2, op0=op0, op1=op1), reads=rd, writes=[out.b])

        def stt(out, in0, scalar, in1, op0, op1):
            rd = [in0.b, in1.b]
            a = scalar
            if isinstance(scalar, V):
                a = scalar.ap; rd.append(scalar.b)
            P.op("dve", lambda e: e.scalar_tensor_tensor(out=out.ap, in0=in0.ap, scalar=a, in1=in1.ap, op0=op0, op1=op1), reads=rd, writes=[out.b])

        def red(out, in_, op, axis=AX.X):
            P.op("dve", lambda e: e.tensor_reduce(out=out.ap, in_=in_.ap, axis=axis, op=op), reads=[in_.b], writes=[out.b])

        def recip(out, in_):
            P.op("dve", lambda e: e.reciprocal(out=out.ap, in_=in_.ap), reads=[in_.b], writes=[out.b])

        def scan(out, d0, d1, init):
            P.op("dve", lambda e: e.tensor_tensor_scan(out=out.ap, data0=d0.ap, data1=d1.ap, initial=init.ap, op0=ALU.mult, op1=ALU.add),
                 reads=[d0.b, d1.b, init.b], writes=[out.b])

        def mset(out, val, eng="pool"):
            P.op(eng, lambda e: e.memset(out.ap, val), writes=[out.b])

        def ld(out, in_, eng="sp"):
            P.dma(eng, lambda e: e.dma_start(out=out.ap, in_=in_.ap), reads=[in_.b], writes=[out.b])

        def ldw(out, in_):
            P.dma("pool", lambda e: e.dma_start(out=out.ap, in_=in_.ap), reads=[in_.b], writes=[out.b])

        ident = alloc([128, 128], F32, "ident"); identb = alloc([128, 128], BF16, "identb")
        bar_scr = alloc([128, 8], F32, "barscr")
        ld(ident, ident_d); ldw(identb, ident_d)
        epsc = alloc([128, 1], F32, "epsc"); mset(epsc, EPS)

        def barrier():
            P.barrier(lambda e: e.memset(bar_scr.ap, 0.0))

        def load_xT(src_rows, xt, xT, bank):
            ld(xt, src_rows)
            for hb in range(2):
                for k in range(4):
                    kk = hb * 4 + k
                    tr(bank[:, k * 128:(k + 1) * 128], xt[:, kk * 128:(kk + 1) * 128], ident)
                src = bank.re("p (k n) -> p k n", k=4)
                if hb == 0:
                    acopy(xT[:, 0:4, :], src)
                else:
                    vcopy(xT[:, 4:8, :], src)

        def transpose_to(dst, src, ncol, bank, eng="act", rows=128):
            bb = bank.cast(BF16)
            for k in range(ncol):
                tr(bb[:, k * 128:k * 128 + rows], src[:, k * 128:(k + 1) * 128], identb[0:rows, 0:rows])
            s_ = bb[:, 0:ncol * 128].re("p (k n) -> p k n", k=ncol)[:, :, 0:rows]
            if eng == "act":
                acopy(dst, s_)
            else:
                vcopy(dst, s_)

        def rmsnorm(out, src, n, gtile, sq, ss):
            act(sq, src, AF.Square, accum=ss)
            act(ss, ss, AF.Ln, scale=1.0 / n, bias=epsc)
            act(ss, ss, AF.Exp, scale=-0.5)
            stt(out, src, ss, gtile, ALU.mult, ALU.mult)

        def layernorm(out, r, g, b, stats, mv):
            for c in range(2):
                P.op("dve", lambda e, c=c: e.bn_stats(out=stats.ap[:, c * 6:(c + 1) * 6], in_=r.ap[:, c * 512:(c + 1) * 512]), reads=[r.b], writes=[stats.b])
            P.op("dve", lambda e: e.bn_aggr(out=mv.ap[:, 0:2], in_=stats.ap[:, 0:12]), reads=[stats.b], writes=[mv.b])
            act(mv[:, 1:2], mv[:, 1:2], AF.Ln, bias=epsc)
            act(mv[:, 1:2], mv[:, 1:2], AF.Exp, scale=-0.5)
            ts(out, r, mv[:, 0:1], mv[:, 1:2], ALU.subtract, ALU.mult)
            tt(out, out, g, ALU.mult, eng="pool")
            tt(out, out, b, ALU.add, eng="pool")

        def rope(out, src, cc, ss_, tmp, nh, eng="dve"):
            ccb = cc.un(1).bc([128, nh, 64]); ssb = ss_.un(1).bc([128, nh, 64])
            tt(out, src, ccb, ALU.mult, eng=eng)
            tt(tmp[:, :, 0:32], src[:, :, 32:64], ssb[:, :, 0:32], ALU.mult, eng=eng)
            tt(tmp[:, :, 32:64], src[:, :, 0:32], ssb[:, :, 32:64], ALU.mult, eng=eng)
            tt(out, out, tmp, ALU.add, eng="pool")

        mkT = alloc([128, 4, 2, 256], BF16, "mkT")
        mvb = alloc([128, 2, D], BF16, "mvb")
        mark_after_mem = bump[0]
        btab = Buf("tab")
        W_in_u = alloc([128, 8, 512], BF16, "W_in_u"); W_in_kv = alloc([128, 8, 320], BF16, "W_in_kv")
        LrT = alloc([128, 16, 128], BF16, "LrT"); LiT = alloc([128, 16, 128], BF16, "LiT")
        BRb = alloc([128, 16, 32], F32, "BRb"); BIb = alloc([128, 16, 32], F32, "BIb")
        lam128 = alloc([128, 2, 16], F32, "lam128")
        Xpp = alloc([128, 2, 32], F32, "Xpp")
        Xown = alloc([128, 16, 32], F32, "Xown")
        ohot = alloc([128, 8], F32, "ohot")
        Oacc = alloc([128, 16, 4, 128], F32, "Oacc")
        m_run = alloc([128, 64], F32, "m_run"); l_run = alloc([128, 64], F32, "l_run")
        Osam = alloc([128, 512], F32, "Osam")
        ssmx = {}
        mark_after_mixer_state = bump[0]
        QTn = alloc([128, NOWN, 4, 128], BF16, "QTn"); QTr = alloc([128, NOWN, 4, 128], BF16, "QTr")
        mark_after_q = bump[0]

        w_in_v = w_in_d.re("(kt p) n -> p kt n", p=128)
        ldw(W_in_u, w_in_v[:, :, 0:512]); ldw(W_in_kv, w_in_v[:, :, 896:1216])
        ld(ohot, onehot_d)

        S_t = alloc([128, 16, 128], F32, "S_t"); C_t = alloc([128, 16, 128], F32, "C_t"); R_t = alloc([128, 16, 128], F32, "R_t")
        for v_ in (S_t, C_t, R_t):
            v_.b = btab
        WBre = alloc([128, 16, 128], BF16, "WBre"); WBim = alloc([128, 16, 128], BF16, "WBim")

        p0 = bump[0]
        ar = alloc([128, 16]); ai = alloc([128, 16]); ldt = alloc([128, 16]); jj = alloc([128, 128])
        bre = alloc([128, 16, 16]); bim = alloc([128, 16, 16])
        bsm = Buf("ssm_small")
        for v_ in (ar, ai, ldt, jj, bre, bim):
            v_.b = bsm
        ld(ar, ar_d); ld(ai, ai_d); ld(ldt, ldt_d); ld(jj, jj_d)
        ld(bre.re("p a b -> p (a b)"), bre_d); ld(bim.re("p a b -> p (a b)"), bim_d)
        dt_ = alloc([128, 16]); th = alloc([128, 16]); rr = alloc([128, 16])
        sm = [alloc([128, 16]) for _ in range(8)]
        for v_ in [dt_, th, rr] + sm:
            v_.b = bsm
        act(dt_, ldt, AF.Exp)
        tt(th, ai, dt_, ALU.mult)
        tt(rr, ar, dt_, ALU.mult)
        act(rr, rr, AF.Exp)
        A_t = alloc([128, 16, 128]); T1 = alloc([128, 2048]); TI = alloc([128, 2048], I32)
        for v_ in (A_t, T1, TI):
            v_.b = btab
        jjb = jj.un(1).bc([128, 16, 128])
        tt(A_t, jjb, th.un(2).bc([128, 16, 128]), ALU.mult)
        tt(R_t, jjb, rr.un(2).bc([128, 16, 128]), ALU.max)
        tt(R_t, R_t, rr.un(2).bc([128, 16, 128]), ALU.min)
        Af = A_t.re("p a b -> p (a b)")
        TIf = TI.cast(F32)

        def sin_of(out, shift):
            ts(T1, Af, shift, 1.0 / TWO_PI, ALU.add, ALU.mult)
            vcopy(TI, T1)
            vcopy(T1, TI)
            stt(T1, T1, -TWO_PI, Af, ALU.mult, ALU.add)
            if shift != 0.0:
                ts(T1, T1, shift, None, ALU.add)
            ts(TIf, T1, math.pi, -TWO_PI, ALU.is_gt, ALU.mult)
            tt(T1, T1, TIf, ALU.add)
            ts(TIf, T1, -math.pi, TWO_PI, ALU.is_lt, ALU.mult)
            tt(T1, T1, TIf, ALU.add)
            ts(T1, T1, math.pi, -math.pi, ALU.min, ALU.max)
            act(out, T1, AF.Sin)

        sin_of(S_t.re("p a b -> p (a b)"), 0.0)
        sin_of(C_t.re("p a b -> p (a b)"), math.pi / 2)
        lbr, lbi, den, fre, fim, t0_, t1_, t2_ = sm
        cos1 = C_t[:, :, 0]; sin1 = S_t[:, :, 0]
        tt(lbr, rr, cos1, ALU.mult)
        ts(lbr, lbr, -1.0, None, ALU.add)
        tt(lbi, rr, sin1, ALU.mult)
        tt(den, ar, ar, ALU.mult)
        tt(t0_, ai, ai, ALU.mult)
        tt(den, den, t0_, ALU.add)
        recip(den, den)
        tt(t0_, lbr, ar, ALU.mult); tt(t1_, lbi, ai, ALU.mult); tt(fre, t0_, t1_, ALU.add); tt(fre, fre, den, ALU.mult)
        tt(t0_, lbi, ar, ALU.mult); tt(t1_, lbr, ai, ALU.mult); tt(fim, t0_, t1_, ALU.subtract); tt(fim, fim, den, ALU.mult)
        Mre = alloc([128, 16, 128]); Mim = alloc([128, 16, 128]); tb = alloc([128, 16, 16]); tb2 = alloc([128, 16, 16])
        Mreb = alloc([128, 16, 128], BF16); Mimb = alloc([128, 16, 128], BF16)
        bM = Buf("M")
        for v_ in (Mre, Mim, tb, tb2, Mreb, Mimb):
            v_.b = bM
        freb = fre.un(2).bc([128, 16, 16]); fimb = fim.un(2).bc([128, 16, 16])
        mset(Mre, 0.0); mset(Mim, 0.0)
        tt(tb, bre, freb, ALU.mult); tt(tb2, bim, fimb, ALU.mult)
        for lo, col in ((0, 0), (64, 16)):
            for j4 in range(4):
                tt(Mre[lo:lo + 64, j4::4, 32 * j4 + col:32 * j4 + col + 16], tb[lo:lo + 64, j4::4, :], tb2[lo:lo + 64, j4::4, :], ALU.subtract)
        tt(tb, bim, freb, ALU.mult); tt(tb2, bre, fimb, ALU.mult)
        for lo, col in ((0, 0), (64, 16)):
            for j4 in range(4):
                tt(Mim[lo:lo + 64, j4::4, 32 * j4 + col:32 * j4 + col + 16], tb[lo:lo + 64, j4::4, :], tb2[lo:lo + 64, j4::4, :], ALU.add)
        vcopy(Mreb, Mre); vcopy(Mimb, Mim)
        for M_, WB_ in ((Mreb, WBre), (Mimb, WBim)):
            for hf in range(2):
                bb = banks[0].cast(BF16)
                for q in range(8):
                    tr(bb[:, q * 128:(q + 1) * 128], M_[:, 8 * hf + q, :], identb)
                vcopy(WB_[:, 8 * hf:8 * hf + 8, :].re("p a b -> p (a b)"), bb)

        mset(BRb, 0.0); mset(BIb, 0.0)
        tt(tb, bre, freb, ALU.mult); tt(tb2, bim, fimb, ALU.mult)
        for lo, col in ((0, 0), (64, 16)):
            tt(BRb[lo:lo + 64, :, col:col + 16], tb[lo:lo + 64], tb2[lo:lo + 64], ALU.subtract)
        tt(tb, bim, freb, ALU.mult); tt(tb2, bre, fimb, ALU.mult)
        for lo, col in ((0, 0), (64, 16)):
            tt(BIb[lo:lo + 64, :, col:col + 16], tb[lo:lo + 64], tb2[lo:lo + 64], ALU.add)
        lnr = alloc([128, 16]); lnr.b = bsm
        tt(lnr, ar, dt_, ALU.mult)
        act(t2_, lnr, AF.Exp, scale=128.0)
        tt(lam128[:, 0, :], t2_, C_t[:, :, 127], ALU.mult)
        tt(lam128[:, 1, :], t2_, S_t[:, :, 127], ALU.mult)
        jj2 = alloc([128, 128]); jj2.b = bsm
        ld(jj2, jj2_d)
        C2 = Mre; S2 = Mim; Mag = T1.re("p (a b) -> p a b", a=16)
        jj2b = jj2.un(1).bc([128, 16, 128])
        tt(A_t, jj2b, th.un(2).bc([128, 16, 128]), ALU.mult)
        sin_of(S2.re("p a b -> p (a b)"), 0.0)
        sin_of(C2.re("p a b -> p (a b)"), math.pi / 2)
        tt(Mag, jj2b, lnr.un(2).bc([128, 16, 128]), ALU.mult)
        act(Mag, Mag, AF.Exp)
        Lb = [Mreb, Mimb]
        tt(Lb[0], Mag, C2, ALU.mult); tt(Lb[1], Mag, S2, ALU.mult)
        for M_, LT_ in ((Lb[0], LrT), (Lb[1], LiT)):
            for hf in range(2):
                bb = banks[0].cast(BF16)
                for q in range(8):
                    tr(bb[:, q * 128:(q + 1) * 128], M_[:, 8 * hf + q, :], identb)
                vcopy(LT_[:, 8 * hf:8 * hf + 8, :].re("p a b -> p (a b)"), bb)

        for nm_, v_ in (("S_t", S_t), ("C_t", C_t), ("R_t", R_t)):
            ld(tab_d[nm_], v_.re("p a b -> p (a b)"))
        ld(tab_d["WBre"], WBre.re("p a b -> p (a b)")); ld(tab_d["WBim"], WBim.re("p a b -> p (a b)"))

        barrier()
        bump[0] = mark_after_q
        W_xk = alloc([128, 8, D], BF16, "W_xk"); W_xv = alloc([128, 8, D], BF16, "W_xv")
        ldw(W_xk, w_xk_d.re("(kt p) n -> p kt n", p=128)); ldw(W_xv, w_xv_d.re("(kt p) n -> p kt n", p=128))
        xt0 = alloc([128, D], F32, "xt0"); memT = alloc([128, 8, 256], BF16, "memT"); xT0 = alloc([128, 8, 128], BF16, "xT0")
        mkf = alloc([128, D], F32, "mkf")
        for mt in range(2):
            load_xT(mem[mt * 128:(mt + 1) * 128, :], xt0, xT0, banks[0])
            vcopy(memT[:, :, mt * 128:(mt + 1) * 128], xT0, eng="pool")
            for W_, o_d, keep in ((W_xk, memk_o, False), (W_xv, memv_o, True)):
                for nb in range(2):
                    for k in range(8):
                        mm(banks[1 + nb], xT0[:, k, :], W_[:, k, nb * 512:(nb + 1) * 512], start=(k == 0), stop=(k == 7))
                    acopy(mkf[:, nb * 512:(nb + 1) * 512], banks[1 + nb])
                ld(o_d[mt * 128:(mt + 1) * 128, :], mkf)
                if keep:
                    vcopy(mvb[:, mt, :], mkf)
        for h in range(4):
            for et in range(2):
                c0 = h * 256 + et * 128
                for k in range(8):
                    mm(banks[3][:, 0:256], W_xk[:, k, c0:c0 + 128], memT[:, k, :], start=(k == 0), stop=(k == 7))
                acopy(mkT[:, h, et, :], banks[3][:, 0:256])


        W_q = alloc([128, 3, 768], BF16, "W_q")
        ldw(W_q, w_q_d.re("(kt p) n -> p kt n", p=128))
        W_in_q = alloc([128, 8, 384], BF16, "W_in_q")
        ldw(W_in_q, w_in_v[:, :, 512:896])
        gq = alloc([128, 384], F32, "gq"); ld(gq, gq_d)
        NB = 2
        xt = [alloc([128, D], F32, "xt%d" % i) for i in range(NB)]
        xT = [alloc([128, 8, 128], BF16, "xT%d" % i) for i in range(NB)]
        rc = [alloc([128, 64], F32, "rc%d" % i) for i in range(NB)]
        rs = [alloc([128, 64], F32, "rs%d" % i) for i in range(NB)]
        sqp = [alloc([128, 384], F32, "sq%d" % i) for i in range(NB)]; ss1 = [alloc([128, 1], F32) for _ in range(NB)]
        cqn = [alloc([128, 384], BF16) for _ in range(NB)]
        cqT = [alloc([128, 3, 128], BF16) for _ in range(NB)]
        qf = [alloc([128, 4, 192], F32) for _ in range(NB)]
        qr = [alloc([128, 4, 64], F32) for _ in range(NB)]
        qtmp = [alloc([128, 4, 64], F32) for _ in range(NB)]
        qb = [alloc([128, 4, 192], BF16) for _ in range(NB)]
        qrd = [alloc([128, 4, 128], BF16) for _ in range(NB)]

        def q_stages(i, p):
            Xb = banks[p]; Jb = banks[2 + p]; Qb = dbl[2 + p]

            def sA():
                load_xT(xo[i * 128:(i + 1) * 128, :], xt[p], xT[p], Xb)
                ld(rc[p], ropec_o[i * 128:(i + 1) * 128, :]); ld(rs[p], ropes_o[i * 128:(i + 1) * 128, :])

            def sB():
                for k in range(8):
                    mm(Jb[:, 0:384], xT[p][:, k, :], W_in_q[:, k, :], start=(k == 0), stop=(k == 7))

            def sC():
                rmsnorm(cqn[p], Jb[:, 0:384], 384, gq, sqp[p], ss1[p])

            def sD():
                transpose_to(cqT[p], cqn[p], 3, Xb, eng="act")

            def sE():
                for nb, (c0, c1) in enumerate(((0, 512), (512, 768))):
                    for k in range(3):
                        mm(Qb[:, nb * 512:nb * 512 + (c1 - c0)], cqT[p][:, k, :], W_q[:, k, c0:c1], start=(k == 0), stop=(k == 2))

            def sF():
                acopy(qf[p].re("p a b -> p (a b)"), Qb[:, 0:768])

            def sG():
                rope(qr[p], qf[p][:, :, 128:192], rc[p], rs[p], qtmp[p], 4)

            def sH():
                P.op("act", lambda e: e.mul(out=qb[p].ap[:, :, 0:128], in_=qf[p].ap[:, :, 0:128], mul=MLA_SCALE), reads=[qf[p].b], writes=[qb[p].b])
                P.op("act", lambda e: e.mul(out=qrd[p].ap[:, :, 0:64], in_=qr[p].ap, mul=MLA_SCALE), reads=[qr[p].b], writes=[qrd[p].b])
                P.op("act", lambda e: e.mul(out=qrd[p].ap[:, :, 64:128], in_=qr[p].ap, mul=MLA_SCALE), reads=[qr[p].b], writes=[qrd[p].b])

            def sI():
                bb = Xb.cast(BF16)
                for h in range(4):
                    tr(bb[:, h * 128:(h + 1) * 128], qb[p][:, h, 0:128], identb)
                    tr(bb[:, 512 + h * 128:512 + (h + 1) * 128], qrd[p][:, h, :], identb)

            def sJ():
                bb = Xb.cast(BF16)
                vcopy(QTn[:, i, :, :].re("p a b -> p (a b)"), bb[:, 0:512])
                vcopy(QTr[:, i, :, :].re("p a b -> p (a b)"), bb[:, 512:1024])

            return [sA, sB, sC, sD, sE, sF, sG, sH, sI, sJ]

        for ip in range(9):
            sa = q_stages(2 * ip, 0)
            sb_ = q_stages(2 * ip + 1, 1) if 2 * ip + 1 < NOWN else []
            for k_ in range(len(sa)):
                sa[k_]()
                if k_ < len(sb_):
                    sb_[k_]()

        barrier()
        bump[0] = mark_after_q

        def ssm_rounds(uT_, pbs, init_fn, nseg, ybank=None, last_out=None):
            L = 128 // nseg
            S_t = ssmx["S_t"]; C_t = ssmx["C_t"]; R_t = ssmx["R_t"]; WBre = ssmx["WBre"]; WBim = ssmx["WBim"]
            Cre = ssmx["Cre"]; Cimn = ssmx["Cimn"]; Dblk = ssmx["Dblk"]
            xs4 = ssmx["xs4"]; zl_all = ssmx["zl_all"]
            def do_round(r):
                m4 = ssmx["m4"][r % 2]; wri = ssmx["wri"][r % 2]; zri = ssmx["zri"][r % 2]
                pb = pbs[r % 2]
                for j in range(2):
                    gp = 2 * r + j
                    mm(pb[:, j * 128:(j + 1) * 128], WBre[:, gp, :], uT_[:, gp // 4, :])
                    mm(pb[:, 256 + j * 128:256 + (j + 1) * 128], WBim[:, gp, :], uT_[:, gp // 4, :])
                if nseg == 1:
                    Cq = C_t[:, 2 * r:2 * r + 2, :].re("p a b -> p (a b)"); Sq = S_t[:, 2 * r:2 * r + 2, :].re("p a b -> p (a b)")
                    pre = pb[:, 0:256]; pim = pb[:, 256:512]
                    mk = lambda v_: v_
                else:
                    Cq = C_t[:, 2 * r:2 * r + 2, 0:L].un(2).bc([128, 2, nseg, L]); Sq = S_t[:, 2 * r:2 * r + 2, 0:L].un(2).bc([128, 2, nseg, L])
                    pre = pb[:, 0:256].re("p (a s l) -> p a s l", a=2, s=nseg); pim = pb[:, 256:512].re("p (a s l) -> p a s l", a=2, s=nseg)
                    mk = lambda v_: v_.re("p (a s l) -> p a s l", a=2, s=nseg)
                tt(mk(m4[0]), pre, Cq, ALU.mult); tt(mk(m4[1]), pim, Sq, ALU.mult)
                tt(mk(m4[2]), pim, Cq, ALU.mult); tt(mk(m4[3]), pre, Sq, ALU.mult)
                tt(wri[0], m4[0], m4[1], ALU.add, eng="pool")
                tt(wri[1], m4[2], m4[3], ALU.subtract, eng="pool")

            def do_back(r):
                m4 = ssmx["m4"][r % 2]; wri = ssmx["wri"][r % 2]; zri = ssmx["zri"][r % 2]
                if nseg == 1:
                    Cq = C_t[:, 2 * r:2 * r + 2, :].re("p a b -> p (a b)"); Sq = S_t[:, 2 * r:2 * r + 2, :].re("p a b -> p (a b)")
                else:
                    Cq = C_t[:, 2 * r:2 * r + 2, 0:L].un(2).bc([128, 2, nseg, L]); Sq = S_t[:, 2 * r:2 * r + 2, 0:L].un(2).bc([128, 2, nseg, L])
                for j in range(2):
                    gp = 2 * r + j
                    for s_ in range(nseg):
                        for part in range(2):
                            scan(zri[part][:, j, s_ * L:(s_ + 1) * L], R_t[:, gp, 0:L], wri[part][:, j * 128 + s_ * L:j * 128 + (s_ + 1) * L], init_fn(gp, s_, part))
                if last_out is not None:
                    for part in range(2):
                        src = zri[part].re("p a (s l) -> p a s l", s=nseg)[:, :, :, L - 1]
                        vcopy(zl_all[part][:, 0:nseg, 2 * r:2 * r + 2].re("p s a -> p a s"), src, eng="pool")
                if ybank is not None:
                    dmd = ssmx["dmd"][r % 2]; xrb = ssmx["xrb"]; xib = ssmx["xib"]
                    if nseg == 1:
                        Cd, Sd = Cq, Sq
                        zr_ = zri[0].re("p a b -> p (a b)"); zi_ = zri[1].re("p a b -> p (a b)")
                        dk = lambda v_: v_
                    else:
                        Cd, Sd = Cq, Sq
                        zr_ = zri[0].re("p a (s l) -> p a s l", s=nseg); zi_ = zri[1].re("p a (s l) -> p a s l", s=nseg)
                        dk = lambda v_: v_.re("p (a s l) -> p a s l", a=2, s=nseg)
                    tt(dk(dmd[0]), zr_, Cd, ALU.mult); tt(dk(dmd[1]), zi_, Sd, ALU.mult, eng="pool")
                    tt(dk(dmd[2]), zr_, Sd, ALU.mult); tt(dk(dmd[3]), zi_, Cd, ALU.mult, eng="pool")
                    tt(xrb[r % 2].re("p a b -> p (a b)"), dmd[0], dmd[1], ALU.subtract, eng="pool")
                    tt(xib[r % 2].re("p a b -> p (a b)"), dmd[2], dmd[3], ALU.add, eng="pool")

            def do_cproj(r):
                if ybank is None:
                    return
                xrb = ssmx["xrb"]; xib = ssmx["xib"]
                for j in range(2):
                    gp = 2 * r + j
                    o_ = ybank[:, 32 * gp:32 * gp + 32]
                    mm(o_, xrb[r % 2][:, j, :], Cre[:, gp, :], start=True, stop=False)
                    mm(o_, xib[r % 2][:, j, :], Cimn[:, gp, :], start=False, stop=False)
                    mm(o_, uT_[:, gp // 4, :], Dblk[:, gp, :], start=False, stop=True)
            def do_tail():
                do_cproj(7)
                if last_out is None:
                    return
                if True:
                    cL = C_t[:, :, L - 1]; sL = S_t[:, :, L - 1]
                    for s_ in range(nseg):
                        o_re, o_im = last_out(s_)
                        zr_l = zl_all[0][:, s_, :]; zi_l = zl_all[1][:, s_, :]
                        tt(xs4[0], zr_l, cL, ALU.mult); tt(xs4[1], zi_l, sL, ALU.mult)
                        tt(xs4[2], zr_l, sL, ALU.mult); tt(xs4[3], zi_l, cL, ALU.mult)
                        tt(o_re, xs4[0], xs4[1], ALU.subtract)
                        tt(o_im, xs4[2], xs4[3], ALU.add)


            def step(k_):
                def f():
                    if k_ == 0:
                        do_round(0)
                    if k_ + 1 < 8:
                        do_round(k_ + 1)
                    do_back(k_)
                    if k_ >= 1:
                        do_cproj(k_ - 1)
                return f
            return [step(k_) for k_ in range(8)] + [do_tail]

        def ssm_tile(uT_, pbs, init_fn, nseg, ybank=None, last_out=None):
            for f_ in ssm_rounds(uT_, pbs, init_fn, nseg, ybank, last_out):
                f_()

        W_kv = alloc([128, 2, 4, 256], BF16, "W_kv")
        ldw(W_kv.re("p k h e -> p k (h e)"), w_kv_d.re("(kt p) n -> p kt n", p=128))
        gkv = alloc([128, 256], F32, "gkv"); ld(gkv, gkv_d)
        maskd = alloc([128, 1024], F32, "maskd"); ld(maskd, maskd_d)
        xt = [alloc([128, D], F32, "hxt%d" % i) for i in range(NB)]
        xT = [alloc([128, 8, 128], BF16, "hxT%d" % i) for i in range(NB)]
        utok = [alloc([128, 512], BF16, "hut%d" % i) for i in range(NB)]
        kvf = [alloc([128, 320], F32, "kvf%d" % i) for i in range(NB)]
        prs = [alloc([128, 16, 32], F32, "prs%d" % i) for i in range(NB)]
        pis = [alloc([128, 16, 32], F32, "pis%d" % i) for i in range(NB)]
        rc = [alloc([128, 64], F32) for _ in range(NB)]; rs = [alloc([128, 64], F32) for _ in range(NB)]
        ckvf = [alloc([128, 256], F32) for _ in range(NB)]; kpef = [alloc([128, 64], F32) for _ in range(NB)]
        tmp64 = [alloc([128, 64], F32) for _ in range(NB)]
        sq = alloc([128, 256], F32, "hsq"); ss1 = [alloc([128, 1], F32) for _ in range(NB)]
        ckvb = [alloc([128, 256], BF16) for _ in range(NB)]; kpeb = [alloc([128, 128], BF16) for _ in range(NB)]
        ckvT = [alloc([128, 2, 128], BF16) for _ in range(NB)]
        KT = [alloc([128, 4, 1024], BF16, "KT%d" % i) for i in range(2)]
        KR = [alloc([128, 1024], BF16, "KR%d" % i) for i in range(2)]
        Vb = [alloc([128, 8, 512], BF16, "Vb%d" % i) for i in range(2)]
        Pb = [alloc([128, 1024], BF16, "Pb%d" % i) for i in range(2)]
        PT = [alloc([128, 8, 128], BF16, "PT%d" % i) for i in range(2)]
        NS4 = 4
        mx = [alloc([128, 1], F32) for _ in range(NS4)]; mnew = [alloc([128, 1], F32) for _ in range(NS4)]
        negm = [alloc([128, 1], F32) for _ in range(NS4)]; alp = [alloc([128, 1], F32) for _ in range(NS4)]
        rsum = [alloc([128, 1], F32) for _ in range(NS4)]
        dm_ = [alloc([128, 1], F32) for _ in range(NS4)]
        hm1 = [alloc([128, 16, 32], F32, "hm1_%d" % i) for i in range(NB)]; hm2 = [alloc([128, 16, 32], F32, "hm2_%d" % i) for i in range(NB)]
        sre = [alloc([128, 32], F32) for _ in range(2)]
        xs8 = [alloc([128, 16], F32) for _ in range(4)]

        mset(Xpp, 0.0); mset(Xown, 0.0)
        mset(m_run, -NEG); mset(l_run, 0.0); mset(Oacc, 0.0)
        for kp in range(2):
            mset(kpeb[kp], 0.0)

        def hist_stages(t, kbuf, j, kb):
            b = t % NB
            bA = banks[0]; bB = banks[1]
            xc = Xpp[:, t % 2, :]; xn = Xpp[:, (t + 1) % 2, :]

            def sA():
                ld(xt[b], xp[t * 128:(t + 1) * 128, :])
                ld(rc[b], ropec_p[t * 128:(t + 1) * 128, :]); ld(rs[b], ropes_p[t * 128:(t + 1) * 128, :])

            def sB():
                for k in range(4):
                    tr(bA[:, k * 128:(k + 1) * 128], xt[b][:, k * 128:(k + 1) * 128], ident)
                for k in range(4):
                    tr(bB[:, k * 128:(k + 1) * 128], xt[b][:, (4 + k) * 128:(5 + k) * 128], ident)

            def sC():
                acopy(xT[b][:, 0:4, :], bA.re("p (k n) -> p k n", k=4))
                acopy(xT[b][:, 4:8, :], bB.re("p (k n) -> p k n", k=4))

            def sD():
                for k in range(8):
                    mm(bA, xT[b][:, k, :], W_in_u[:, k, :], start=(k == 0), stop=(k == 7))
                for k in range(8):
                    mm(bB[:, 0:320], xT[b][:, k, :], W_in_kv[:, k, :], start=(k == 0), stop=(k == 7))

            def sE():
                acopy(utok[b], bA)
                acopy(kvf[b], bB[:, 0:320])

            def sF():
                act(sq, kvf[b][:, 0:256], AF.Square, accum=ss1[b])
                act(ss1[b], ss1[b], AF.Ln, scale=1.0 / 256, bias=epsc)
                act(ss1[b], ss1[b], AF.Exp, scale=-0.5)
                for gp in range(16):
                    mm(bA[:, 32 * gp:32 * gp + 32], LrT[:, gp, :], utok[b][:, 32 * gp:32 * gp + 32])
                for gp in range(16):
                    mm(bB[:, 32 * gp:32 * gp + 32], LiT[:, gp, :], utok[b][:, 32 * gp:32 * gp + 32])

            late = kb >= 4

            def sG():
                if late:
                    pr3 = bA.re("p (a b) -> p a b", a=16); pi3 = bB.re("p (a b) -> p a b", a=16)
                    tt(hm1[b], pr3, BRb, ALU.mult); tt(hm2[b], pi3, BIb, ALU.mult)
                    tt(prs[b], pi3, BRb, ALU.mult); tt(pis[b], pr3, BIb, ALU.mult)
                else:
                    acopy(prs[b].re("p a b -> p (a b)"), bA)
                    acopy(pis[b].re("p a b -> p (a b)"), bB)
                tt(ckvf[b], kvf[b][:, 0:256], gkv, ALU.mult, eng="pool")
                ts(ckvf[b], ckvf[b], ss1[b], 1.0, ALU.mult, ALU.mult, eng="pool")
                rope(kpef[b].re("p (a b) -> p a b", a=1), kvf[b][:, 256:320].re("p (a b) -> p a b", a=1), rc[b], rs[b],
                     tmp64[b].re("p (a b) -> p a b", a=1), 1, eng="pool")
                vcopy(ckvb[b], ckvf[b], eng="pool")
                vcopy(kpeb[b][:, 0:64], kpef[b], eng="pool")
                vcopy(kpeb[b][:, 64:128], kpef[b], eng="pool")

            def sH():
                bb = bA.cast(BF16)
                for kc in range(2):
                    tr(bb[:, kc * 128:(kc + 1) * 128], ckvb[b][:, kc * 128:(kc + 1) * 128], identb)
                tr(bb[:, 256:384], kpeb[b], identb)
                if late:
                    tt(hm1[b], hm1[b], hm2[b], ALU.subtract, eng="pool")
                    tt(hm2[b], prs[b], pis[b], ALU.add, eng="pool")
                else:
                    tt(hm1[b], prs[b], BRb, ALU.mult, eng="pool"); tt(hm2[b], pis[b], BIb, ALU.mult, eng="pool")
                    tt(hm1[b], hm1[b], hm2[b], ALU.subtract, eng="pool")
                    tt(hm2[b], pis[b], BRb, ALU.mult, eng="pool"); tt(prs[b], prs[b], BIb, ALU.mult, eng="pool")
                    tt(hm2[b], hm2[b], prs[b], ALU.add, eng="pool")

            def sI():
                bb = bA.cast(BF16)
                acopy(ckvT[b].re("p a b -> p (a b)"), bb[:, 0:256])
                acopy(KR[kbuf][:, j * 128:(j + 1) * 128], bb[:, 256:384])

            def sJ():
                for h in range(4):
                    for kc in range(2):
                        mm(bB[:, h * 128:(h + 1) * 128], W_kv[:, kc, h, 0:128], ckvT[b][:, kc, :], start=(kc == 0), stop=(kc == 1))
                for kc in range(2):
                    mm(bA, ckvT[b][:, kc, :], W_kv[:, kc, :, 128:256], start=(kc == 0), stop=(kc == 1))
                lr_ = lam128[:, 0, :]; li_ = lam128[:, 1, :]
                ld(ckv_o[t * 128:(t + 1) * 128, :], ckvf[b])
                ld(kpe_o[t * 128:(t + 1) * 128, :], kpef[b])
                red(sre[t % 2][:, 0:16], hm1[b], ALU.add)
                red(sre[t % 2][:, 16:32], hm2[b], ALU.add)
                stt(Xown[:, kb, :], xc, ohot[:, j:j + 1], Xown[:, kb, :], ALU.mult, ALU.add)
                tt(xs8[0], xc[:, 0:16], lr_, ALU.mult, eng="pool"); tt(xs8[1], xc[:, 16:32], li_, ALU.mult, eng="pool")
                tt(xs8[2], xc[:, 16:32], lr_, ALU.mult, eng="pool"); tt(xs8[3], xc[:, 0:16], li_, ALU.mult, eng="pool")
                tt(xs8[0], xs8[0], xs8[1], ALU.subtract, eng="pool"); tt(xs8[2], xs8[2], xs8[3], ALU.add, eng="pool")
                tt(xn[:, 0:16], xs8[0], sre[t % 2][:, 0:16], ALU.add, eng="pool")
                tt(xn[:, 16:32], xs8[2], sre[t % 2][:, 16:32], ALU.add, eng="pool")

            def sK():
                acopy(KT[kbuf][:, :, j * 128:(j + 1) * 128], bB.re("p (h n) -> p h n", h=4))
                acopy(Vb[kbuf][:, j, :], bA)

            def pair(p_, c_):
                def f():
                    p_(); c_()
                return f
            return [sA, pair(sB, sC), pair(sD, sE), pair(sF, sG), pair(sH, sI), pair(sJ, sK)]

        def hist_items(kb):
            items = []
            for jp in range(4):
                sa = hist_stages(kb * 8 + 2 * jp, kb % 2, 2 * jp, kb)
                sb_ = hist_stages(kb * 8 + 2 * jp + 1, kb % 2, 2 * jp + 1, kb)
                for a_, b_ in zip(sa, sb_):
                    items.append(a_); items.append(b_)
            return items

        SCp = [dbl[3], dbl[2]]
        ptb = [banks[2], banks[3]]

        def att_A(n, i, h, kbuf):
            sc = SCp[n % 2]
            for c2 in range(2):
                mm(sc[:, c2 * 512:(c2 + 1) * 512], QTn[:, i, h, :], KT[kbuf][:, h, c2 * 512:(c2 + 1) * 512], start=True, stop=False)
            for c2 in range(2):
                lo = 64 * c2
                mm(sc[:, c2 * 512:(c2 + 1) * 512], QTr[lo:lo + 64, i, h, :], KR[kbuf][lo:lo + 64, c2 * 512:(c2 + 1) * 512], start=False, stop=True)

        def att_B(n, i, h, kbuf, diag):
            u2 = n % 2; u4 = n % NS4
            sc = SCp[u2]
            col = i * 4 + h
            if diag:
                tt(sc, sc, maskd, ALU.add)
            P.op("dve", lambda e: e.tensor_reduce(out=mx[u4].ap, in_=sc.ap, axis=AX.X, op=ALU.max, negate=True), reads=_flat([sc.b]), writes=[mx[u4].b])
            tt(negm[u4], mx[u4], m_run[:, col:col + 1], ALU.min)
            tt(dm_[u4], negm[u4], m_run[:, col:col + 1], ALU.subtract)
            vcopy(m_run[:, col:col + 1], negm[u4])
            act(alp[u4], dm_[u4], AF.Exp)
            act(Pb[u2], sc, AF.Exp, bias=negm[u4], accum=rsum[u4])

        def att_CD(n, i, h, kbuf):
            u2 = n % 2
            pt_b = ptb[u2].cast(BF16)
            for kt in range(8):
                tr(pt_b[:, kt * 128:(kt + 1) * 128], Pb[u2][:, kt * 128:(kt + 1) * 128], identb)
            if n % 3 == 0:
                vcopy(PT[u2].re("p a b -> p (a b)"), pt_b)
            else:
                acopy(PT[u2].re("p a b -> p (a b)"), pt_b)

        def att_EF(n, i, h, kbuf):
            u2 = n % 2; u4 = n % NS4
            ov = ptb[u2][:, 0:128]
            for kt in range(8):
                mm(ov, PT[u2][:, kt, :], Vb[kbuf][:, kt, h * 128:(h + 1) * 128], start=(kt == 0), stop=(kt == 7))
            stt(Oacc[:, i, h, :], Oacc[:, i, h, :], alp[u4], ov, ALU.mult, ALU.add)
            col = i * 4 + h
            stt(l_run[:, col:col + 1], l_run[:, col:col + 1], alp[u4], rsum[u4], ALU.mult, ALU.add)

        for it in hist_items(0):
            it()
        nun = 0
        for kb in range(16):
            kbuf = kb % 2
            units = [(i, h) for i in range(kb, 16) for h in range(4)]
            U = len(units)
            items = hist_items(kb + 1) if kb + 1 < 16 else []
            per = (len(items) + U - 1) // U if items else 0
            ip = 0
            for q_ in range(U + 3):
                if q_ < U:
                    att_A(nun + q_, units[q_][0], units[q_][1], kbuf)
                if 0 <= q_ - 1 < U:
                    att_B(nun + q_ - 1, units[q_ - 1][0], units[q_ - 1][1], kbuf, units[q_ - 1][0] == kb)
                if 0 <= q_ - 2 < U:
                    att_CD(nun + q_ - 2, units[q_ - 2][0], units[q_ - 2][1], kbuf)
                if 0 <= q_ - 3 < U:
                    att_EF(nun + q_ - 3, units[q_ - 3][0], units[q_ - 3][1], kbuf)
                for _ in range(per):
                    if ip < len(items):
                        items[ip](); ip += 1
            while ip < len(items):
                items[ip](); ip += 1
            nun += U

        ld(ssmp_o, Xpp[:, NTP % 2, :])

        barrier()
        bump[0] = mark_after_q

        W_kv = alloc([128, 2, 4, 256], BF16, "W_kv2")
        ldw(W_kv.re("p k h e -> p k (h e)"), w_kv_d.re("(kt p) n -> p kt n", p=128))
        gkv = alloc([128, 256], F32, "gkv2"); ld(gkv, gkv_d)
        xt_s = alloc([128, D], F32, "sxt"); xT_s = alloc([128, 8, 128], BF16, "sxT")
        rc_s = alloc([128, 64], F32); rs_s = alloc([128, 64], F32)
        ckvf_s = alloc([128, 256], F32); kpef_s = alloc([128, 64], F32); tmp64_s = alloc([128, 64], F32)
        sq = alloc([128, 512], F32, "ssq"); ss_s = alloc([128, 1], F32)
        ckvb_s = alloc([128, 256], BF16); kpeb_s = alloc([128, 128], BF16)
        ckvTn = alloc([128, 2, 128], BF16, "ckvTn"); kpeTn = alloc([128, 128], BF16, "kpeTn")
        KTn = alloc([128, 4, 128], BF16, "KTn")
        cat = [alloc([128, 256], F32, "cat%d" % i) for i in range(8)]
        catb = [alloc([128, 256], BF16) for _ in range(2)]
        cpt = [alloc([128, 64], F32, "cpt%d" % i) for i in range(8)]
        cptb = [alloc([128, 128], BF16) for _ in range(2)]
        ckvTc = alloc([128, 2, 1024], BF16, "ckvTc"); kpeTc = alloc([128, 1024], BF16, "kpeTc")
        KTc = alloc([128, 4, 1024], BF16, "KTc"); Vc = alloc([128, 8, 512], BF16, "Vc"); Vn = alloc([128, 512], BF16, "Vn")
        scs = alloc([128, 1056], F32, "scs")
        Ps = alloc([128, 1152], BF16, "Ps"); PTs = alloc([128, 9, 32], BF16, "PTs")
        mxs = alloc([128, 1], F32); negs = alloc([128, 1], F32); sums = alloc([128, 1], F32)
        osb = alloc([128, 512], F32, "osb")
        isam = 16
        load_xT(xo[isam * 128:(isam + 1) * 128, :], xt_s, xT_s, banks[0])
        ld(rc_s, ropec_o[isam * 128:(isam + 1) * 128, :]); ld(rs_s, ropes_o[isam * 128:(isam + 1) * 128, :])
        for k in range(8):
            mm(banks[2][:, 0:320], xT_s[:, k, :], W_in_kv[:, k, :], start=(k == 0), stop=(k == 7))
        rmsnorm(ckvf_s, banks[2][:, 0:256], 256, gkv, sq[:, 0:256], ss_s)
        ld(ckvs_o, ckvf_s)
        rope(kpef_s.re("p (a b) -> p a b", a=1), banks[2][:, 256:320].re("p (a b) -> p a b", a=1), rc_s, rs_s, tmp64_s.re("p (a b) -> p a b", a=1), 1)
        ld(kpes_o, kpef_s)
        vcopy(ckvb_s, ckvf_s)
        mset(kpeb_s, 0.0)
        vcopy(kpeb_s[:, 0:64], kpef_s)
        bb = banks[3].cast(BF16)
        for kc in range(2):
            tr(bb[:, kc * 128:(kc + 1) * 128], ckvb_s[:, kc * 128:(kc + 1) * 128], identb)
        tr(bb[:, 256:384], kpeb_s, identb)
        acopy(ckvTn.re("p a b -> p (a b)"), bb[:, 0:256])
        acopy(kpeTn[0:64, :], bb[0:64, 256:384])
        for h in range(4):
            for kc in range(2):
                mm(banks[3][:, h * 128:(h + 1) * 128], W_kv[:, kc, h, 0:128], ckvTn[:, kc, :], start=(kc == 0), stop=(kc == 1))
        acopy(KTn.re("p a b -> p (a b)"), banks[3])
        for kp in range(2):
            mset(cptb[kp], 0.0)
        def issue_cache_loads(sq_):
            for kt in range(8):
                r0 = sq_ * 1024 + kt * 128
                ld(cat[kt], cckv_d[r0:r0 + 128, :]); ld(cpt[kt], ckpe_d[r0:r0 + 128, :])

        issue_cache_loads(0)
        for s_ in range(4):
            for kt in range(8):
                b = kt % 2
                vcopy(catb[b], cat[kt], eng="pool"); vcopy(cptb[b][:, 0:64], cpt[kt], eng="pool")
                if kt == 7 and s_ + 1 < 4:
                    issue_cache_loads(s_ + 1)
                bb = banks[kt % 2].cast(BF16)
                for kc in range(2):
                    tr(bb[:, kc * 128:(kc + 1) * 128], catb[b][:, kc * 128:(kc + 1) * 128], identb)
                tr(bb[:, 256:384], cptb[b], identb)
                acopy(ckvTc[:, :, kt * 128:(kt + 1) * 128], bb[:, 0:256].re("p (a b) -> p a b", a=2))
                acopy(kpeTc[0:64, kt * 128:(kt + 1) * 128], bb[0:64, 256:384])
            for h in range(4):
                for n in range(2):
                    kbk = banks[2] if n == 0 else banks[5]
                    for kc in range(2):
                        mm(kbk, W_kv[:, kc, h, 0:128], ckvTc[:, kc, n * 512:(n + 1) * 512], start=(kc == 0), stop=(kc == 1))
                    acopy(KTc[:, h, n * 512:(n + 1) * 512], kbk)
            for kt in range(8):
                for kc in range(2):
                    mm(banks[3], ckvTc[:, kc, kt * 128:(kt + 1) * 128], W_kv[:, kc, :, 128:256], start=(kc == 0), stop=(kc == 1))
                vcopy(Vc[:, kt, :], banks[3])
            for kc in range(2):
                mm(banks[3][0:32, :], ckvTn[:, kc, s_ * 32:(s_ + 1) * 32], W_kv[:, kc, :, 128:256], start=(kc == 0), stop=(kc == 1))
            vcopy(Vn[0:32, :], banks[3][0:32, :])
            qs = slice(s_ * 32, (s_ + 1) * 32)
            for h in range(4):
                for n in range(2):
                    mm(SC[0:32, n * 512:(n + 1) * 512], QTn[:, isam, h, qs], KTc[:, h, n * 512:(n + 1) * 512], start=True, stop=False)
                    mm(SC[0:32, n * 512:(n + 1) * 512], QTr[0:64, isam, h, qs], kpeTc[0:64, n * 512:(n + 1) * 512], start=False, stop=True)
                mm(banks[4][0:32, 0:32], QTn[:, isam, h, qs], KTn[:, h, qs], start=True, stop=False)
                mm(banks[4][0:32, 0:32], QTr[0:64, isam, h, qs], kpeTn[0:64, qs], start=False, stop=True)
                acopy(scs[0:32, 0:1024], SC[0:32, :])
                acopy(scs[0:32, 1024:1056], banks[4][0:32, 0:32])
                red(mxs[0:32, :], scs[0:32, :], ALU.max)
                ts(negs[0:32, :], mxs[0:32, :], -1.0, None, ALU.mult)
                mset(Ps[0:32, 1024:1152], 0.0)
                act(Ps[0:32, 0:1056], scs[0:32, :], AF.Exp, bias=negs[0:32, :], accum=sums[0:32, :])
                pt_b = banks[5].cast(BF16)
                for kt in range(9):
                    tr(pt_b[:, kt * 32:(kt + 1) * 32], Ps[0:32, kt * 128:(kt + 1) * 128], identb[0:32, 0:32])
                vcopy(PTs.re("p a b -> p (a b)"), pt_b[:, 0:288])
                ov = banks[4][0:32, 128:256]
                for kt in range(8):
                    mm(ov, PTs[:, kt, :], Vc[:, kt, h * 128:(h + 1) * 128], start=(kt == 0), stop=False)
                mm(ov, PTs[0:32, 8, :], Vn[0:32, h * 128:(h + 1) * 128], start=False, stop=True)
                recip(sums[0:32, :], sums[0:32, :])
                ts(osb[0:32, h * 128:(h + 1) * 128], ov, sums[0:32, :], None, ALU.mult)
            ld(Osam[s_ * 32:(s_ + 1) * 32, :], osb[0:32, :])

        barrier()
        bump[0] = mark_after_mixer_state

        W_glu = alloc([128, 4, D], BF16, "W_glu"); W_o = alloc([128, 8, D], BF16, "W_o")
        ldw(W_glu, w_glu_d.re("(kt p) n -> p kt n", p=128)); ldw(W_o, w_o_d.re("(kt p) n -> p kt n", p=128))
        gos = alloc([128, 512], F32, "gos"); gom = alloc([128, 512], F32, "gom"); ld(gos, gos_d); ld(gom, gom_d)
        lng = alloc([128, D], F32, "lng"); lnb = alloc([128, D], F32, "lnb")
        ld(lng, lng_d[:, 0:D]); ld(lnb, lnb_d[:, 0:D])
        btab2 = Buf("tab2")
        for nm_ in ("S_t", "C_t", "R_t"):
            v_ = alloc([128, 16, 128], F32, nm_ + "2"); v_.b = btab2
            ld(v_.re("p a b -> p (a b)"), tab_d[nm_]); ssmx[nm_] = v_
        for nm_ in ("WBre", "WBim"):
            v_ = alloc([128, 16, 128], BF16, nm_ + "2")
            ld(v_.re("p a b -> p (a b)"), tab_d[nm_]); ssmx[nm_] = v_
        for nm_, src_ in (("Cre", cre_d), ("Cimn", cimn_d), ("Dblk", dblk_d)):
            v_ = alloc([128, 16, 32], BF16, nm_)
            ldw(v_.re("p a b -> p (a b)"), src_); ssmx[nm_] = v_
        ssmx["m4"] = [[alloc([128, 256], F32) for _ in range(4)] for _ in range(2)]
        ssmx["wri"] = [[alloc([128, 256], F32) for _ in range(2)] for _ in range(2)]
        ssmx["zri"] = [[alloc([128, 2, 128], F32) for _ in range(2)] for _ in range(2)]
        ssmx["xs4"] = [alloc([128, 16], F32) for _ in range(4)]
        ssmx["zl_all"] = [alloc([128, 4, 16], F32) for _ in range(2)]
        ssmx["dmd"] = [[alloc([128, 256], F32) for _ in range(4)]] * 2
        ssmx["xrb"] = [alloc([128, 2, 128], BF16) for _ in range(2)]
        ssmx["xib"] = [alloc([128, 2, 128], BF16) for _ in range(2)]
        S0 = alloc([128, 2, 4, 16], F32, "S0"); ld(S0.re("p a b c -> p (a b c)"), s0_d)
        Sfin = alloc([128, 2, 4, 16], F32, "Sfin")
        xt = [alloc([128, D], F32, "axt%d" % i) for i in range(NB)]
        xT = [alloc([128, 8, 128], BF16, "axT%d" % i) for i in range(NB)]
        uT = [alloc([128, 4, 128], BF16, "auT%d" % i) for i in range(NB)]
        ysq = alloc([128, 512], F32, "ysq"); yt = alloc([128, 512], F32, "yt"); ysg = alloc([128, 512], F32, "ysg")
        glb = alloc([128, 512], BF16, "glb"); gT = alloc([128, 4, 128], BF16, "gT")
        sg2 = ysq; osf = yt
        mixb = alloc([128, D], BF16, "mixb"); mixT = alloc([128, 8, 128], BF16, "mixT")
        rl = alloc([128, 4], F32, "rl"); omf = alloc([128, 4, 128], F32, "omf")
        ss_a = alloc([128, 1], F32); sq = ysg
        rres = alloc([128, D], F32, "rres"); hout = alloc([128, D], F32, "hout")
        stats = alloc([128, 12], F32); mvv = alloc([128, 2], F32)

        def pre_stages(i):
            b = i % NB
            yb = banks[3] if i % 2 == 0 else banks[0]

            def p0():
                load_xT(xo[i * 128:(i + 1) * 128, :], xt[b], xT[b], banks[1])
                for kt in range(4):
                    for k in range(8):
                        mm(banks[1][:, kt * 128:(kt + 1) * 128], W_in_u[:, k, kt * 128:(kt + 1) * 128], xT[b][:, k, :], start=(k == 0), stop=(k == 7))
                acopy(uT[b].re("p a b -> p (a b)"), banks[1])

            if i < 16:
                rr_ = ssm_rounds(uT[b], (banks[4], banks[5]), lambda gp, s_, part, i=i: Xown[:, i, part * 16 + gp:part * 16 + gp + 1], 1, ybank=yb)
            else:
                rr_ = ssm_rounds(uT[b], (banks[4], banks[5]), lambda gp, s_, part: S0[:, part, s_, gp:gp + 1], 4, ybank=yb,
                                 last_out=lambda s_: (Sfin[:, 0, s_, :], Sfin[:, 1, s_, :]))
                rr_.append(lambda: ld(ssms_o, Sfin.re("p a b c -> p (a b c)")))
            return [p0] + rr_

        def post_stages(i):
            b = i % NB
            yb = banks[3] if i % 2 == 0 else banks[0]
            xt_ = xt[b]

            def q0():
                act(ysq, yb, AF.Square)
                ts(yt, ysq, 0.044715, 1.0, ALU.mult, ALU.add)
                tt(yt, yt, yb, ALU.mult)

            def q1():
                act(ysg, yt, AF.Sigmoid, scale=1.5957691216057308)
                tt(glb, ysg, yb, ALU.mult)

            def q2():
                transpose_to(gT, glb, 4, banks[2], eng="act")

            def q3():
                for nb in range(2):
                    for k in range(4):
                        mm(SC[:, nb * 512:(nb + 1) * 512], gT[:, k, :], W_glu[:, k, nb * 512:(nb + 1) * 512], start=(k == 0), stop=(k == 3))

            def q4():
                act(sg2, SC[:, 512:1024], AF.Sigmoid)
                tt(osf, sg2, SC[:, 0:512], ALU.mult)

            def q5():
                rmsnorm(mixb[:, 0:512], osf, 512, gos, sq, ss_a)

            def q6():
                if i < 16:
                    recip(rl, l_run[:, i * 4:(i + 1) * 4])
                    tt(omf, Oacc[:, i, :, :], rl.un(2).bc([128, 4, 128]), ALU.mult)
                    rmsnorm(mixb[:, 512:1024], omf.re("p a b -> p (a b)"), 512, gom, sq, ss_a)
                else:
                    rmsnorm(mixb[:, 512:1024], Osam, 512, gom, sq, ss_a)

            def q7():
                for half in range(2):
                    transpose_to(mixT[:, half * 4:(half + 1) * 4, :], mixb[:, half * 512:(half + 1) * 512], 4, banks[2], eng=("act" if half == 0 else "dve"))

            def q8():
                for nb in range(2):
                    for k in range(8):
                        mm(SC[:, nb * 512:(nb + 1) * 512], mixT[:, k, :], W_o[:, k, nb * 512:(nb + 1) * 512], start=(k == 0), stop=(k == 7))

            def q9():
                stt(rres, xt_, ALPHA, SC, ALU.mult, ALU.add)
                layernorm(hout, rres, lng, lnb, stats, mvv)
                ld(h1_d[i * 128:(i + 1) * 128, :], hout)

            return [q0, q1, q2, q3, q4, q5, q6, q7, q8, q9]

        prev_post = []
        for i in range(NOWN + 1):
            pre = pre_stages(i) if i < NOWN else []
            n_ = max(len(pre), len(prev_post))
            for k_ in range(n_):
                if k_ < len(pre):
                    pre[k_]()
                if k_ < len(prev_post):
                    prev_post[k_]()
            prev_post = post_stages(i) if i < NOWN else []

        barrier()
        bump[0] = mark_after_mem

        W_xq = alloc([128, 8, D], BF16, "W_xq"); W_xo = alloc([128, 8, D], BF16, "W_xo")
        W_xqc = [V(W_xq.ap[:, :, c * 512:(c + 1) * 512], Buf("W_xqc%d" % c)) for c in range(2)]
        for c in range(2):
            ldw(W_xqc[c], w_xq_d.re("(kt p) n -> p kt n", p=128)[:, :, c * 512:(c + 1) * 512])
        ldw(W_xo, w_xo_d.re("(kt p) n -> p kt n", p=128))
        lng = alloc([128, D], F32, "lng2"); lnb = alloc([128, D], F32, "lnb2")
        ld(lng, lng_d[:, D:2 * D]); ld(lnb, lnb_d[:, D:2 * D])
        hin = [alloc([128, D], F32, "bh%d" % i) for i in range(NB)]
        hb2 = [alloc([128, D], BF16, "bhb%d" % i) for i in range(2)]; hT2 = [alloc([128, 8, 128], BF16, "bhT%d" % i) for i in range(2)]
        qxb2 = [alloc([128, D], BF16, "qxb%d" % i) for i in range(2)]; qxT2 = [alloc([128, 8, 128], BF16, "qxT%d" % i) for i in range(2)]
        mx42 = [alloc([128, 4], F32) for _ in range(2)]; neg42 = [alloc([128, 4], F32) for _ in range(2)]; sum42 = [alloc([128, 4], F32) for _ in range(2)]
        Px2 = [alloc([128, 4, 256], BF16, "Px%d" % i) for i in range(2)]; PxT2 = [alloc([128, 8, 128], BF16, "PxT%d" % i) for i in range(2)]
        oxb2 = [alloc([128, D], BF16, "oxb%d" % i) for i in range(2)]; oxT2 = [alloc([128, 8, 128], BF16, "oxT%d" % i) for i in range(2)]
        rres2 = [alloc([128, D], F32, "brres%d" % i) for i in range(2)]; hout2 = [alloc([128, D], F32, "bhout%d" % i) for i in range(2)]
        stats2 = [alloc([128, 12], F32) for _ in range(2)]; mvv2 = [alloc([128, 2], F32) for _ in range(2)]
        hb_ = hb2[0]; hT = hT2[0]; qxb = qxb2[0]; qxT = qxT2[0]; mx4 = mx42[0]; neg4 = neg42[0]; sum4 = sum42[0]
        Px = Px2[0]; PxT = PxT2[0]; oxb = oxb2[0]; oxT = oxT2[0]; rres = rres2[0]; hout = hout2[0]; stats = stats2[0]; mvv = mvv2[0]
        cmk = [alloc([128, D], F32, "cmk%d" % i) for i in range(2)]
        cmkb = [alloc([128, D], BF16, "cmkb%d" % i) for i in range(1)]
        mkTs4 = [alloc([128, 4, 2, 256], BF16, "mkTs%d" % i) for i in range(4)]; mvs4 = [alloc([128, 2, D], BF16, "mvs%d" % i) for i in range(4)]
        scx = alloc([128, 8], F32, "scx"); oxs = alloc([128, D], BF16, "oxs")

        def xa_stages(i, p):
            A_ = banks[p]; C_ = dbl[1 + p]
            Ab = A_.cast(BF16)

            def tr8(src):
                for k in range(8):
                    tr(Ab[:, k * 128:(k + 1) * 128], src[:, k * 128:(k + 1) * 128], identb)

            def ev8(dst):
                acopy(dst[:, 0:4, :], Ab[:, 0:512].re("p (k n) -> p k n", k=4))
                vcopy(dst[:, 4:8, :], Ab[:, 512:1024].re("p (k n) -> p k n", k=4))

            def t0():
                ld(hin[p], h1_d[i * 128:(i + 1) * 128, :])
                vcopy(hb2[p], hin[p], eng="pool")

            def t3():
                for nb in range(2):
                    for k in range(8):
                        mm(C_[:, nb * 512:(nb + 1) * 512], hT2[p][:, k, :], W_xqc[nb][:, k, :], start=(k == 0), stop=(k == 7))

            def t4():
                for nb in range(2):
                    P.op("act", lambda e, nb=nb: e.mul(out=qxb2[p].ap[:, nb * 512:(nb + 1) * 512], in_=C_.ap[:, nb * 512:(nb + 1) * 512], mul=X_SCALE),
                         reads=[C_.b], writes=[qxb2[p].b])

            def t7():
                for h in range(4):
                    for et in range(2):
                        mm(C_[:, h * 256:(h + 1) * 256], qxT2[p][:, h * 2 + et, :], mkT[:, h, et, :], start=(et == 0), stop=(et == 1))

            def t8():
                red(mx42[p], C_.re("p (h m) -> p h m", h=4), ALU.max)
                ts(neg42[p], mx42[p], -1.0, None, ALU.mult)
                for h in range(4):
                    act(Px2[p][:, h, :], C_[:, h * 256:(h + 1) * 256], AF.Exp, bias=neg42[p][:, h:h + 1], accum=sum42[p][:, h:h + 1])

            def t11():
                for h in range(4):
                    for mt in range(2):
                        mm(C_[:, h * 256:(h + 1) * 256], PxT2[p][:, h * 2 + mt, :], mvb[:, mt, h * 256:(h + 1) * 256], start=(mt == 0), stop=(mt == 1))

            def t12():
                recip(sum42[p], sum42[p])
                tt(oxb2[p].re("p (h e) -> p h e", h=4), C_.re("p (h e) -> p h e", h=4), sum42[p].un(2).bc([128, 4, 256]), ALU.mult)

            def t15():
                for nb in range(2):
                    for k in range(8):
                        mm(C_[:, nb * 512:(nb + 1) * 512], oxT2[p][:, k, :], W_xo[:, k, nb * 512:(nb + 1) * 512], start=(k == 0), stop=(k == 7))

            def t16():
                stt(rres2[p], hin[p], ALPHA, C_, ALU.mult, ALU.add)
                layernorm(hout2[p], rres2[p], lng, lnb, stats2[p], mvv2[p])
                ld(h2_d[i * 128:(i + 1) * 128, :], hout2[p])

            return [t0, lambda: tr8(hb2[p]), lambda: ev8(hT2[p]), t3, t4, lambda: tr8(qxb2[p]), lambda: ev8(qxT2[p]), t7, t8,
                    lambda: tr8(Px2[p].re("p a b -> p (a b)")), lambda: ev8(PxT2[p]), t11, t12, lambda: tr8(oxb2[p]), lambda: ev8(oxT2[p]), t15, t16]

        def prep_items():
            items = []
            for s_ in range(4):
                for mt in range(2):
                    r0 = s_ * 256 + mt * 128

                    def l0(r0=r0):
                        ld(cmk[0], cmk_d[r0:r0 + 128, :]); ld(cmk[1], cmv_d[r0:r0 + 128, :])

                    def l1(s_=s_, mt=mt):
                        vcopy(cmkb[0], cmk[0], eng="pool")
                        vcopy(mvs4[s_][:, mt, :], cmk[1], eng="pool")

                    def l2(s_=s_, mt=mt):
                        for half in range(2):
                            bb = banks[6 + half].cast(BF16)
                            for k in range(4):
                                kk = half * 4 + k
                                tr(bb[:, k * 128:(k + 1) * 128], cmkb[0][:, kk * 128:(kk + 1) * 128], identb)

                    def l3(s_=s_, mt=mt):
                        for half in range(2):
                            bb = banks[6 + half].cast(BF16)
                            acopy(mkTs4[s_][:, half * 2:half * 2 + 2, :, mt * 128:(mt + 1) * 128].re("p h e m -> p (h e) m"), bb[:, 0:512].re("p (k m) -> p k m", k=4))

                    items += [l0, l1, l2, l3]
            return items

        pitems = prep_items()
        pi_ = 0
        slot = 0
        for ip in range(8):
            sa = xa_stages(2 * ip, 0); sb_ = xa_stages(2 * ip + 1, 1)
            for a_, b_ in zip(sa, sb_):
                a_(); b_()
                slot += 1
                if slot % 4 == 0 and pi_ < len(pitems):
                    pitems[pi_](); pi_ += 1
        while pi_ < len(pitems):
            pitems[pi_](); pi_ += 1

        for i in range(16, NOWN):
            b = i % NB
            ld(hin[b], h1_d[i * 128:(i + 1) * 128, :])
            vcopy(hb_, hin[b], eng="pool")
            for half in range(2):
                transpose_to(hT[:, half * 4:(half + 1) * 4, :], hb_[:, half * 512:(half + 1) * 512], 4, banks[0], eng=("act" if half == 0 else "dve"))
            for nb in range(2):
                for k in range(8):
                    mm(banks[1 + nb], hT[:, k, :], W_xqc[nb][:, k, :], start=(k == 0), stop=(k == 7))
                P.op("act", lambda e, nb=nb: e.mul(out=qxb.ap[:, nb * 512:(nb + 1) * 512], in_=banks[1 + nb].ap, mul=X_SCALE), reads=[banks[1 + nb].b], writes=[qxb.b])
            for half in range(2):
                transpose_to(qxT[:, half * 4:(half + 1) * 4, :], qxb[:, half * 512:(half + 1) * 512], 4, banks[3], eng=("act" if half == 0 else "dve"))
            if i < 16:
                for h in range(4):
                    for et in range(2):
                        mm(SC[:, h * 256:(h + 1) * 256], qxT[:, h * 2 + et, :], mkT[:, h, et, :], start=(et == 0), stop=(et == 1))
                red(mx4, SC.re("p (h m) -> p h m", h=4), ALU.max)
                ts(neg4, mx4, -1.0, None, ALU.mult)
                for h in range(4):
                    act(Px[:, h, :], SC[:, h * 256:(h + 1) * 256], AF.Exp, bias=neg4[:, h:h + 1], accum=sum4[:, h:h + 1])
                for half in range(2):
                    transpose_to(PxT[:, half * 4:(half + 1) * 4, :], Px.re("p a b -> p (a b)")[:, half * 512:(half + 1) * 512], 4, banks[4], eng=("act" if half == 0 else "dve"))
                for h in range(4):
                    for mt in range(2):
                        mm(SC[:, h * 256:(h + 1) * 256], PxT[:, h * 2 + mt, :], mvb[:, mt, h * 256:(h + 1) * 256], start=(mt == 0), stop=(mt == 1))
                recip(sum4, sum4)
                tt(oxb.re("p (h e) -> p h e", h=4), SC.re("p (h e) -> p h e", h=4), sum4.un(2).bc([128, 4, 256]), ALU.mult)
            else:
                for s_ in range(4):
                    qs = slice(s_ * 32, (s_ + 1) * 32)
                    mkTs = mkTs4[s_]; mvs = mvs4[s_]
                    for h in range(4):
                        for et in range(2):
                            mm(SC[0:32, h * 256:(h + 1) * 256], qxT[:, h * 2 + et, qs], mkTs[:, h, et, :], start=(et == 0), stop=(et == 1))
                    red(mx4[0:32, :], SC[0:32, :].re("p (h m) -> p h m", h=4), ALU.max)
                    ts(neg4[0:32, :], mx4[0:32, :], -1.0, None, ALU.mult)
                    for h in range(4):
                        act(Px[0:32, h, :], SC[0:32, h * 256:(h + 1) * 256], AF.Exp, bias=neg4[0:32, h:h + 1], accum=sum4[0:32, h:h + 1])
                    bb = banks[4].cast(BF16)
                    for k in range(8):
                        tr(bb[:, k * 32:(k + 1) * 32], Px.re("p a b -> p (a b)")[0:32, k * 128:(k + 1) * 128], identb[0:32, 0:32])
                    vcopy(PxT[:, :, 0:32], bb[:, 0:256].re("p (k q) -> p k q", k=8))
                    for h in range(4):
                        for mt in range(2):
                            mm(SC[0:32, h * 256:(h + 1) * 256], PxT[:, h * 2 + mt, 0:32], mvs[:, mt, h * 256:(h + 1) * 256], start=(mt == 0), stop=(mt == 1))
                    recip(sum4[0:32, :], sum4[0:32, :])
                    tt(oxs[0:32, :].re("p (h e) -> p h e", h=4), SC[0:32, :].re("p (h e) -> p h e", h=4), sum4[0:32, :].un(2).bc([32, 4, 256]), ALU.mult)
                    ld(oxb[s_ * 32:(s_ + 1) * 32, :], oxs[0:32, :])
            for half in range(2):
                transpose_to(oxT[:, half * 4:(half + 1) * 4, :], oxb[:, half * 512:(half + 1) * 512], 4, banks[5], eng=("act" if half == 0 else "dve"))
            for nb in range(2):
                for k in range(8):
                    mm(banks[1 + nb], oxT[:, k, :], W_xo[:, k, nb * 512:(nb + 1) * 512], start=(k == 0), stop=(k == 7))
            for nb in range(2):
                stt(rres[:, nb * 512:(nb + 1) * 512], hin[b][:, nb * 512:(nb + 1) * 512], ALPHA, banks[1 + nb], ALU.mult, ALU.add)
            layernorm(hout, rres, lng, lnb, stats, mvv)
            ld(h2_d[i * 128:(i + 1) * 128, :], hout)

        barrier()
        bump[0] = mark_after_mem

        W1 = alloc([128, 8, 4096], BF16, "W1"); W2 = alloc([128, 32, D], BF16, "W2")
        W1c = [V(W1.ap[:, :, c * 512:(c + 1) * 512], Buf("W1c%d" % c)) for c in range(8)]
        W2c = [V(W2.ap[:, c * 8:(c + 1) * 8, :], Buf("W2c%d" % c)) for c in range(4)]
        w1v = w_ff1_d.re("(kt p) n -> p kt n", p=128); w2v = w_ff2_d.re("(kt p) n -> p kt n", p=128)
        for c in range(8):
            ldw(W1c[c], w1v[:, :, c * 512:(c + 1) * 512])
        for c in range(4):
            ldw(W2c[c], w2v[:, c * 8:(c + 1) * 8, :])
        lng = alloc([128, D], F32, "lng3"); lnb = alloc([128, D], F32, "lnb3")
        ld(lng, lng_d[:, 2 * D:3 * D]); ld(lnb, lnb_d[:, 2 * D:3 * D])
        hin = alloc([128, D], F32, "ch")
        hb_ = alloc([128, D], BF16, "chb"); hT = alloc([128, 8, 512], BF16, "chT")
        zr = [alloc([128, 512], F32, "zr%d" % i) for i in range(2)]; zT = alloc([128, 32, 512], BF16, "zT")
        rres = alloc([128, D], F32, "crres"); hout = alloc([128, D], F32, "chout")
        stats = alloc([128, 12], F32); mvv = alloc([128, 2], F32)
        groups = [list(range(g_ * 4, g_ * 4 + 4)) for g_ in range(4)] + [[16]]
        for grp in groups:
            nt = len(grp)
            for q_, i in enumerate(grp):
                ld(hin, h2_d[i * 128:(i + 1) * 128, :])
                vcopy(hb_, hin, eng="pool")
                for half in range(2):
                    transpose_to(hT[:, half * 4:(half + 1) * 4, q_ * 128:(q_ + 1) * 128], hb_[:, half * 512:(half + 1) * 512], 4, banks[0], eng=("act" if half == 0 else "dve"))
            W_ = nt * 128
            for f in range(32):
                bk = banks[1 + f % 2]
                for k in range(8):
                    mm(bk[:, 0:W_], W1c[f // 4][:, k, (f % 4) * 128:(f % 4 + 1) * 128], hT[:, k, 0:W_], start=(k == 0), stop=(k == 7))
                act(zr[f % 2][:, 0:W_], bk[:, 0:W_], AF.Relu)
                tt(zT[:, f, 0:W_], zr[f % 2][:, 0:W_], zr[f % 2][:, 0:W_], ALU.mult)
            for q_, i in enumerate(grp):
                for nb in range(2):
                    for f in range(32):
                        mm(SC[:, nb * 512:(nb + 1) * 512], zT[:, f, q_ * 128:(q_ + 1) * 128], W2c[f // 8][:, f % 8, nb * 512:(nb + 1) * 512], start=(f == 0), stop=(f == 31))
                ld(hin, h2_d[i * 128:(i + 1) * 128, :])
                stt(rres, hin, ALPHA, SC, ALU.mult, ALU.add)
                layernorm(hout, rres, lng, lnb, stats, mvv)
                ld(y_o[i * 128:(i + 1) * 128, :], hout)

        P.emit(st)
    return nc


def _lay_gp(a):
    sh = a.shape[2:]
    n = len(sh)
    return np.ascontiguousarray(a.reshape(16, 2, 64, *sh).transpose(1, 2, 0, *range(3, 3 + n)).reshape(128, 16, *sh))


def _bc(v, n=128):
    return np.ascontiguousarray(np.broadcast_to(np.asarray(v, np.float32).reshape(1, -1), (n, np.asarray(v).size)))


def _rope_tables(pos):
    inv = (10000.0 ** (-np.arange(32, dtype=np.float32) / 32)).astype(np.float32)
    ang = pos.astype(np.float32)[:, None] * inv[None, :]
    c = np.cos(ang).astype(np.float32)
    s = np.sin(ang).astype(np.float32)
    return np.concatenate([c, c], 1), np.concatenate([-s, s], 1)


_NC_CACHE = {}


def kernel(x_prompt, x_sample, mem_prompt, cache_mla_ckv, cache_mla_kpe, state_ssm_re, state_ssm_im,
           cache_mem_k, cache_mem_v, w_in, g_q, w_q_up, g_kv, w_kv_up, a_re, a_im, b_re, b_im, c_re, c_im,
           d_skip, log_dt, w_glu, g_out_ssm, g_out_mla, w_o, w_xq, w_xk, w_xv, w_xo, w_ff1, w_ff2, ln_g, ln_b):
    f = lambda a: np.ascontiguousarray(np.asarray(a, dtype=np.float32))
    x_prompt = f(x_prompt); x_sample = f(x_sample)
    xp = x_prompt[0]
    cre = np.zeros((128, 16, 32), np.float32); cimn = np.zeros((128, 16, 32), np.float32)
    cr = f(c_re)[0].reshape(16, 2, 16, 64); ci = f(c_im)[0].reshape(16, 2, 16, 64)
    for g2 in range(2):
        cre[g2 * 64:(g2 + 1) * 64, :, g2 * 16:(g2 + 1) * 16] = cr[:, g2].transpose(2, 0, 1)
        cimn[g2 * 64:(g2 + 1) * 64, :, g2 * 16:(g2 + 1) * 16] = -ci[:, g2].transpose(2, 0, 1)
    dblk = np.zeros((128, 16, 32), np.float32)
    dd = f(d_skip)[0].reshape(512)
    for gp in range(16):
        for c in range(32):
            ch = gp * 32 + c
            dblk[ch % 128, gp, c] = dd[ch]
    pos_p = np.arange(16384)
    rcp, rsp = _rope_tables(pos_p)
    common = {
        "xp": xp, "mem": f(mem_prompt)[0], "ident": np.eye(128, dtype=np.float32),
        "w_in": f(w_in)[0], "w_q": f(w_q_up)[0].reshape(384, 768), "w_kv": f(w_kv_up)[0].reshape(256, 1024),
        "w_glu": f(w_glu)[0], "w_o": f(w_o)[0], "w_xq": f(w_xq)[0].reshape(D, D), "w_xk": f(w_xk)[0].reshape(D, D),
        "w_xv": f(w_xv)[0].reshape(D, D), "w_xo": f(w_xo)[0].reshape(D, D), "w_ff1": f(w_ff1)[0], "w_ff2": f(w_ff2)[0],
        "gq": _bc(f(g_q)[0]), "gkv": _bc(f(g_kv)[0]), "gos": _bc(f(g_out_ssm)[0]), "gom": _bc(f(g_out_mla)[0]),
        "lng": _bc(f(ln_g)[0].reshape(-1)), "lnb": _bc(f(ln_b)[0].reshape(-1)),
        "ropec_p": rcp, "ropes_p": rsp,
        "ar": _lay_gp(f(a_re)[0]), "ai": _lay_gp(f(a_im)[0]),
        "ldt": _lay_gp(np.ascontiguousarray(np.broadcast_to(f(log_dt)[0][:, None], (32, 64)))),
        "bre": _lay_gp(f(b_re)[0]).reshape(128, 256), "bim": _lay_gp(f(b_im)[0]).reshape(128, 256),
        "jj": _bc(np.arange(1, 129, dtype=np.float32)), "jj2": _bc(127.0 - np.arange(128, dtype=np.float32)),
        "cre": cre.reshape(128, 512), "cimn": cimn.reshape(128, 512), "dblk": dblk.reshape(128, 512),
    }
    qi = np.arange(128)[:, None]
    in_maps = []
    for c in range(NCORES):
        tiles = [8 * i + c for i in range(16)]
        xo = np.concatenate([xp[t * 128:(t + 1) * 128] for t in tiles] + [x_sample[4 * c:4 * c + 4].reshape(128, D)], 0)
        pos_o = np.concatenate([np.arange(t * 128, (t + 1) * 128) for t in tiles] + [np.tile(1024 + np.arange(32), 4)])
        rco, rso = _rope_tables(pos_o)
        kj = np.arange(1024)[None, :]
        vis = ((kj // 128) < c) | (((kj // 128) == c) & (((kj % 128) // 64) <= (qi // 64)))
        maskd = np.where(vis, 0.0, NEG).astype(np.float32)
        onehot = np.zeros((128, 8), np.float32); onehot[:, c] = 1.0
        s0 = np.stack([_lay_gp(f(state_ssm_re)[0, 4 * c + s]) for s in range(4)], 1)
        s0i = np.stack([_lay_gp(f(state_ssm_im)[0, 4 * c + s]) for s in range(4)], 1)
        m = dict(common)
        m.update({
            "xo": np.ascontiguousarray(xo), "ropec_o": rco, "ropes_o": rso, "maskd": maskd, "onehot": onehot,
            "s0": np.ascontiguousarray(np.stack([s0, s0i], 1).reshape(128, 128)),
            "cckv": f(cache_mla_ckv)[0, 4 * c:4 * c + 4].reshape(4096, 256),
            "ckpe": f(cache_mla_kpe)[0, 4 * c:4 * c + 4].reshape(4096, 64),
            "cmk": f(cache_mem_k)[0, 4 * c:4 * c + 4].reshape(1024, D),
            "cmv": f(cache_mem_v)[0, 4 * c:4 * c + 4].reshape(1024, D),
        })
        in_maps.append(m)
    if "nc" not in _NC_CACHE:
        _NC_CACHE["nc"] = build_nc()
    res = run_bass_kernel_spmd(_NC_CACHE["nc"], in_maps, core_ids=list(range(NCORES)))
    R = res.results
    y_p = np.zeros((1, 16384, D), np.float32); y_s = np.zeros((32, 32, D), np.float32)
    ckv_s = np.zeros((1, 32, 32, 256), np.float32); kpe_s = np.zeros((1, 32, 32, 64), np.float32)
    sre_s = np.zeros((1, 32, 32, 64), np.float32); sim_s = np.zeros((1, 32, 32, 64), np.float32)

    def unlay(a):
        return a.reshape(2, 64, 16).transpose(2, 0, 1).reshape(32, 64)

    for c in range(NCORES):
        yo = R[c]["y_o"]
        for i in range(16):
            t = 8 * i + c
            y_p[0, t * 128:(t + 1) * 128] = yo[i * 128:(i + 1) * 128]
        y_s[4 * c:4 * c + 4] = yo[16 * 128:].reshape(4, 32, D)
        ckv_s[0, 4 * c:4 * c + 4] = R[c]["ckvs_o"].reshape(4, 32, 256)
        kpe_s[0, 4 * c:4 * c + 4] = R[c]["kpes_o"].reshape(4, 32, 64)
        sf = R[c]["ssms_o"].reshape(128, 2, 4, 16)
        for s in range(4):
            sre_s[0, 4 * c + s] = unlay(sf[:, 0, s, :])
            sim_s[0, 4 * c + s] = unlay(sf[:, 1, s, :])
    r0 = R[0]
    ckv_p = r0["ckv_o"].reshape(1, 1, 16384, 256); kpe_p = r0["kpe_o"].reshape(1, 1, 16384, 64)
    sp = r0["ssmp_o"]
    sre_p = unlay(sp[:, 0:16]).reshape(1, 1, 32, 64); sim_p = unlay(sp[:, 16:32]).reshape(1, 1, 32, 64)
    mk_p = r0["memk_o"].reshape(1, 1, 256, 4, 256); mv_p = r0["memv_o"].reshape(1, 1, 256, 4, 256)
    return (y_p, y_s, ckv_p, kpe_p, sre_p, sim_p, mk_p, mv_p, ckv_s, kpe_s, sre_s, sim_s)
```
